# Optimizing a Trainium2 kernel written in Bass

```python
import jax, jax.numpy as jnp
from jax import lax
import numpy as np

D_MODEL = 1024
BATCH = 8
SEQ = 4096
DEPTH = 2

N_MIXERS = 2
N_CONV_LAYERS = (DEPTH + 1) // 2
N_ATTN_LAYERS = DEPTH // 2
CONV_WIDTH = 31
ATTN_HEADS = 16
HEAD_DIM = D_MODEL // ATTN_HEADS
Q_BLOCK = 128
PEER_HEADS = 8
N_KEYS = 128
N_EXPERTS = N_KEYS * N_KEYS
PEER_TOPK = 16
KEY_DIM = 256
HALF_KEY = KEY_DIM // 2
TOKEN_CHUNK = 128
LN_EPS = 1e-5
DN_ALPHA = (2 * DEPTH) ** 0.25
DN_BETA = (8 * DEPTH) ** -0.25

kernel_name = "hybrid_conv_fox_peer_deepnorm_adaln"


def _layernorm(x, g, b):
    x32 = x.astype(jnp.float32)
    mu = jnp.mean(x32, axis=-1, keepdims=True)
    var = jnp.mean(jnp.square(x32 - mu), axis=-1, keepdims=True)
    y = (x32 - mu) * lax.rsqrt(var + LN_EPS)
    return (y * g + b).astype(x.dtype)


def _adaln(c, w, b):
    mod = jax.nn.silu(c) @ w + b
    shift, scale, gate = jnp.split(mod, 3, axis=-1)
    return shift[:, None, :], scale[:, None, :], gate[:, None, :]


def _conv_module(h, w_in, b_in, w_dw, b_dw, ln_g, ln_b, w_out, b_out):
    a = jax.nn.glu(h @ w_in + b_in, axis=-1)
    a = lax.conv_general_dilated(
        a, w_dw[:, None, :],
        window_strides=(1,), padding=[(CONV_WIDTH - 1, 0)],
        dimension_numbers=("NWC", "WIO", "NWC"),
        feature_group_count=D_MODEL) + b_dw
    a = jax.nn.silu(_layernorm(a, ln_g, ln_b))
    return a @ w_out + b_out


def _fox_attention(h, w_in, b_in, w_out, b_out):
    B_, S_, _ = h.shape
    proj = h @ w_in + b_in
    q, k, v, f = jnp.split(proj, [D_MODEL, 2 * D_MODEL, 3 * D_MODEL], axis=-1)
    q = (q * HEAD_DIM ** -0.5).reshape(B_, S_, ATTN_HEADS, HEAD_DIM).transpose(0, 2, 1, 3)
    k = k.reshape(B_, S_, ATTN_HEADS, HEAD_DIM).transpose(0, 2, 1, 3)
    v = v.reshape(B_, S_, ATTN_HEADS, HEAD_DIM).transpose(0, 2, 1, 3)
    log_f = jax.nn.log_sigmoid(f.astype(jnp.float32))
    cum = lax.cumsum(log_f, axis=1).transpose(0, 2, 1)
    n_blk = S_ // Q_BLOCK
    q_blocks = q.reshape(B_, ATTN_HEADS, n_blk, Q_BLOCK, HEAD_DIM).transpose(2, 0, 1, 3, 4)
    cum_blocks = cum.reshape(B_, ATTN_HEADS, n_blk, Q_BLOCK).transpose(2, 0, 1, 3)
    key_pos = jnp.arange(S_)

    def one_block(args):
        qb, cb, blk = args
        logits = jnp.einsum("bhqd,bhkd->bhqk", qb, k, preferred_element_type=jnp.float32)
        logits = logits + cb[..., :, None] - cum[:, :, None, :]
        q_pos = blk * Q_BLOCK + jnp.arange(Q_BLOCK)
        causal = key_pos[None, :] <= q_pos[:, None]
        p = jax.nn.softmax(jnp.where(causal, logits, -jnp.inf), axis=-1)
        return jnp.einsum("bhqk,bhkd->bhqd", p.astype(v.dtype), v)

    out = lax.map(one_block, (q_blocks, cum_blocks, jnp.arange(n_blk)))
    out = out.transpose(1, 0, 3, 2, 4).reshape(B_, S_, D_MODEL)
    return out @ w_out + b_out


def _peer(h, w_query, sub_keys_1, sub_keys_2, expert_u, expert_v):
    B_, S_, _ = h.shape
    q = (h @ w_query).reshape(B_, S_, PEER_HEADS, 2, HALF_KEY)
    s1 = jnp.einsum("bshd,kd->bshk", q[..., 0, :], sub_keys_1, preferred_element_type=jnp.float32)
    s2 = jnp.einsum("bshd,kd->bshk", q[..., 1, :], sub_keys_2, preferred_element_type=jnp.float32)
    v1, i1 = lax.top_k(s1, PEER_TOPK)
    v2, i2 = lax.top_k(s2, PEER_TOPK)
    n_cand = PEER_TOPK * PEER_TOPK
    cand_s = (v1[..., :, None] + v2[..., None, :]).reshape(B_, S_, PEER_HEADS, n_cand)
    cand_i = (i1[..., :, None] * N_KEYS + i2[..., None, :]).reshape(B_, S_, PEER_HEADS, n_cand)
    top_s, pos = lax.top_k(cand_s, PEER_TOPK)
    experts = jnp.take_along_axis(cand_i, pos, axis=-1)
    gates = jax.nn.softmax(top_s, axis=-1)
    n_tok = B_ * S_
    n_chunk = n_tok // TOKEN_CHUNK
    n_sel = PEER_HEADS * PEER_TOPK
    xs = h.reshape(n_chunk, TOKEN_CHUNK, D_MODEL)
    es = experts.reshape(n_chunk, TOKEN_CHUNK, n_sel)
    gs = gates.reshape(n_chunk, TOKEN_CHUNK, n_sel).astype(h.dtype)

    def one_chunk(args):
        xc, ec, gc = args
        u = jnp.take(expert_u, ec, axis=0)
        vv = jnp.take(expert_v, ec, axis=0)
        act = jax.nn.gelu(jnp.einsum("tkd,td->tk", u, xc), approximate=False)
        return jnp.einsum("tk,tkd->td", gc * act, vv)

    out = lax.map(one_chunk, (xs, es, gs))
    return out.reshape(B_, S_, D_MODEL)


def setup_inputs(seed: int = 0) -> dict:
    key = jax.random.key(seed)
    ks = jax.random.split(key, 32)
    f32 = jnp.float32
    D = D_MODEL
    nrm = lambda k, shape, s: jax.random.normal(k, shape, f32) * s
    attn_cols = 3 * D + ATTN_HEADS
    attn_in_b = jnp.concatenate([
        nrm(ks[14], (N_ATTN_LAYERS, 3 * D), 0.02),
        jax.random.uniform(ks[15], (N_ATTN_LAYERS, ATTN_HEADS), f32, 1.0, 4.0)], axis=-1)
    return {
        "x": nrm(ks[0], (BATCH, SEQ, D), 1.0),
        "c": nrm(ks[1], (BATCH, D), 1.0),
        "ada_mix_w": nrm(ks[2], (DEPTH, D, 3 * D), 0.5 * D ** -0.5),
        "ada_mix_b": nrm(ks[3], (DEPTH, 3 * D), 0.02),
        "ln_mix_g": 1.0 + nrm(ks[4], (DEPTH, D), 0.02),
        "ln_mix_b": nrm(ks[5], (DEPTH, D), 0.02),
        "conv_in_w": nrm(ks[6], (N_CONV_LAYERS, D, 2 * D), D ** -0.5),
        "conv_in_b": nrm(ks[7], (N_CONV_LAYERS, 2 * D), 0.02),
        "conv_dw_w": nrm(ks[8], (N_CONV_LAYERS, CONV_WIDTH, D), CONV_WIDTH ** -0.5),
        "conv_dw_b": nrm(ks[9], (N_CONV_LAYERS, D), 0.02),
        "conv_ln_g": 1.0 + nrm(ks[10], (N_CONV_LAYERS, D), 0.02),
        "conv_ln_b": nrm(ks[11], (N_CONV_LAYERS, D), 0.02),
        "conv_out_w": nrm(ks[12], (N_CONV_LAYERS, D, D), DN_BETA * D ** -0.5),
        "conv_out_b": nrm(ks[13], (N_CONV_LAYERS, D), 0.02),
        "attn_in_w": nrm(ks[16], (N_ATTN_LAYERS, D, attn_cols), D ** -0.5),
        "attn_in_b": attn_in_b,
        "attn_out_w": nrm(ks[17], (N_ATTN_LAYERS, D, D), DN_BETA * D ** -0.5),
        "attn_out_b": nrm(ks[18], (N_ATTN_LAYERS, D), 0.02),
        "ada_ffn_w": nrm(ks[19], (DEPTH, D, 3 * D), 0.5 * D ** -0.5),
        "ada_ffn_b": nrm(ks[20], (DEPTH, 3 * D), 0.02),
        "ln_ffn_g": 1.0 + nrm(ks[21], (DEPTH, D), 0.02),
        "ln_ffn_b": nrm(ks[22], (DEPTH, D), 0.02),
        "peer_query_w": nrm(ks[23], (DEPTH, D, PEER_HEADS * KEY_DIM), D ** -0.5),
        "peer_sub_keys_1": nrm(ks[24], (DEPTH, N_KEYS, HALF_KEY), HALF_KEY ** -0.5),
        "peer_sub_keys_2": nrm(ks[25], (DEPTH, N_KEYS, HALF_KEY), HALF_KEY ** -0.5),
        "peer_expert_u": nrm(ks[26], (DEPTH, N_EXPERTS, D), D ** -0.5),
        "peer_expert_v": nrm(ks[27], (DEPTH, N_EXPERTS, D), DN_BETA * PEER_HEADS ** -0.5),
    }


def reference(x, c, ada_mix_w, ada_mix_b, ln_mix_g, ln_mix_b,
              conv_in_w, conv_in_b, conv_dw_w, conv_dw_b, conv_ln_g, conv_ln_b,
              conv_out_w, conv_out_b,
              attn_in_w, attn_in_b, attn_out_w, attn_out_b,
              ada_ffn_w, ada_ffn_b, ln_ffn_g, ln_ffn_b,
              peer_query_w, peer_sub_keys_1, peer_sub_keys_2, peer_expert_u, peer_expert_v):
    for i in range(DEPTH):
        j = i // N_MIXERS
        shift, scale, gate = _adaln(c, ada_mix_w[i], ada_mix_b[i])
        h = x * (1 + scale) + shift
        if i % N_MIXERS == 0:
            y = _conv_module(h, conv_in_w[j], conv_in_b[j], conv_dw_w[j], conv_dw_b[j],
                             conv_ln_g[j], conv_ln_b[j], conv_out_w[j], conv_out_b[j])
        else:
            y = _fox_attention(h, attn_in_w[j], attn_in_b[j], attn_out_w[j], attn_out_b[j])
        x = _layernorm(DN_ALPHA * x + gate * y, ln_mix_g[i], ln_mix_b[i])
        shift, scale, gate = _adaln(c, ada_ffn_w[i], ada_ffn_b[i])
        h = x * (1 + scale) + shift
        y = _peer(h, peer_query_w[i], peer_sub_keys_1[i], peer_sub_keys_2[i],
                  peer_expert_u[i], peer_expert_v[i])
        x = _layernorm(DN_ALPHA * x + gate * y, ln_ffn_g[i], ln_ffn_b[i])
    return x
```

```python
import numpy as np
from contextlib import ExitStack
import concourse.bass as bass
import concourse.mybir as mybir
from concourse.bass_utils import run_bass_kernel_spmd

F32 = mybir.dt.float32
I32 = mybir.dt.int32
U32 = mybir.dt.uint32
ALU = mybir.AluOpType
AF = mybir.ActivationFunctionType
AX = mybir.AxisListType

S = 4096
D = 1024
NT = S // 128
ALPHA = float((2 * 2) ** 0.25)
EPS = 1e-5
NEXP = 16384
MAXV = 30000
NEG = -30000.0


class Prog:
    def __init__(self, nc, es):
        self.nc = nc
        self.es = es
        self.eng = {'pe': nc.tensor, 'dve': nc.vector, 'act': nc.scalar,
                    'pool': nc.gpsimd, 'sp': nc.sync}
        self.seq = {e: 0 for e in self.eng}
        self.csem = {e: [] for e in self.eng}
        self.known = {e: {} for e in self.eng}
        self.snap = {}
        self.last_w = {}
        self.readers = {}
        self.semobj = {}
        self.dma_pool = {}
        self.nsem = 0
        self.nwaits = 0
        self.nops = 0
        for q, n in (('sp', 24), ('pool', 24), ('act', 8)):
            self.dma_pool[q] = {'sems': [self._newsem(f"d{q}{i}") for i in range(n)],
                                'cnt': [0] * n, 'next': 0}

    def _newsem(self, name):
        s = self.es.enter_context(self.nc.semaphore(name))
        self.semobj[name] = s
        self.nsem += 1
        return name

    def _need(self, e, tok, skip_self):
        if tok is None:
            return
        name, val, owner = tok
        if skip_self and owner == e:
            return
        if self.known[e].get(name, 0) >= val:
            return
        self.eng[e].wait_ge(self.semobj[name], val)
        self.nwaits += 1
        k = self.known[e]
        k[name] = val
        sn = self.snap.get((name, val))
        if sn:
            for n2, v2 in sn.items():
                if k.get(n2, 0) < v2:
                    k[n2] = v2

    def op(self, e, fn, reads=(), writes=(), dma=False, skip_self=None):
        if skip_self is None:
            skip_self = (e == 'pe')
        if dma:
            skip_self = False
        for r in reads:
            self._need(e, self.last_w.get(r), skip_self)
        for w in writes:
            self._need(e, self.last_w.get(w), skip_self)
            for t in self.readers.get(w, ()):
                self._need(e, t, skip_self)
        self.nops += 1
        if dma:
            pool = self.dma_pool[e]
            i = pool['next']
            pool['next'] = (i + 1) % len(pool['sems'])
            name = pool['sems'][i]
            if pool['cnt'][i] + 16 > MAXV:
                name = self._newsem(f"{name}r{self.nsem}")
                pool['sems'][i] = name
                pool['cnt'][i] = 0
            prev = pool['cnt'][i]
            if prev > 0:
                self._need(e, (name, prev, e + '_dma'), False)
            ins = fn(self.eng[e])
            pool['cnt'][i] = prev + 16
            ins.then_inc(self.semobj[name], 16)
            tok = (name, prev + 16, e + '_dma')
        else:
            n = self.seq[e]
            ep = n // MAXV
            while len(self.csem[e]) <= ep:
                self.csem[e].append(self._newsem(f"c{e}{len(self.csem[e])}"))
            name = self.csem[e][ep]
            ins = fn(self.eng[e])
            ins.then_inc(self.semobj[name], 1)
            self.seq[e] = n + 1
            tok = (name, n - ep * MAXV + 1, e)
        self.snap[(tok[0], tok[1])] = dict(self.known[e])
        for r in reads:
            self.readers.setdefault(r, []).append(tok)
        for w in writes:
            self.last_w[w] = tok
            self.readers[w] = []
        return tok

    def barrier(self):
        toks = []
        for e in self.eng:
            n = self.seq[e]
            if n > 0:
                ep = (n - 1) // MAXV
                toks.append((self.csem[e][ep], n - ep * MAXV, e))
        for q, pool in self.dma_pool.items():
            for name, c in zip(pool['sems'], pool['cnt']):
                if c > 0:
                    toks.append((name, c, q + '_dma'))
        for e in self.eng:
            for t in toks:
                self._need(e, t, False)
        self.last_w.clear()
        self.readers.clear()
        self.snap.clear()


class Stage:
    _n = 0

    def __init__(self, K, name):
        self.K = K
        Stage._n += 1
        self.name = f"{name}{Stage._n}"

    def __enter__(self):
        self.es = ExitStack()
        self.es.__enter__()
        return self

    def T(self, name, shape, dt=F32):
        return self.es.enter_context(self.K.nc.sbuf_tensor(f"{self.name}_{name}", shape, dt))

    def __exit__(self, *a):
        self.K.P.barrier()
        return self.es.__exit__(*a)


class Kern:
    def __init__(self, cfg):
        self.cfg = cfg

    def build(self):
        nc = bass.Bass("TRN2", target_bir_lowering=False)
        self.nc = nc
        dbg = self.cfg.get('debug', False)

        def din(name, shape, dt=F32):
            return nc.dram_tensor(name, list(shape), dt, kind="ExternalInput").ap()

        def dscr(name, shape, dt=F32):
            kind = "ExternalOutput" if (dbg and name in self.cfg.get('expose', ())) else "Internal"
            return nc.dram_tensor(name, list(shape), dt, kind=kind).ap()

        A = {}
        A['x'] = din('x', [S, D])
        A['c_l'] = din('c_l', [128, 8])
        A['ada_mix_w'] = din('ada_mix_w', [2, D, 3 * D])
        A['ada_ffn_w'] = din('ada_ffn_w', [2, D, 3 * D])
        A['ada_b'] = din('ada_b', [4, 3 * D])
        A['ln_g'] = din('ln_g', [4, D])
        A['ln_b'] = din('ln_b', [4, D])
        A['conv_in_w'] = din('conv_in_w', [D, 2 * D])
        A['conv_in_b_l'] = din('conv_in_b_l', [128, 16])
        A['conv_dw_w_l'] = din('conv_dw_w_l', [128, 8, 31])
        A['conv_vec_l'] = din('conv_vec_l', [128, 3, 8])
        A['conv_out_w'] = din('conv_out_w', [D, D])
        A['conv_out_b'] = din('conv_out_b', [1, D])
        A['attn_in_w'] = din('attn_in_w', [D, 3 * D + 16])
        A['attn_qkb_l'] = din('attn_qkb_l', [128, 16])
        A['attn_vb'] = din('attn_vb', [1, D])
        A['attn_fb'] = din('attn_fb', [16, 1])
        A['attn_out_w'] = din('attn_out_w', [D, D])
        A['attn_out_b'] = din('attn_out_b', [1, D])
        A['peer_query_w'] = din('peer_query_w', [2, D, 2 * D])
        A['peer_skT'] = din('peer_skT', [2, 2, 128, 128])
        A['peer_u'] = din('peer_u', [2, NEXP, D])
        A['peer_v'] = din('peer_v', [2, NEXP, D])
        A['out'] = nc.dram_tensor('out', [S, D], F32, kind="ExternalOutput").ap()
        A['X1'] = dscr('X1', [S, D])
        A['X2'] = dscr('X2', [S, D])
        A['X3'] = dscr('X3', [S, D])
        A['ST'] = dscr('ST', [8, 128, S])
        A['IDX'] = dscr('IDX', [S, 128], I32)
        A['GATE'] = dscr('GATE', [S, 128])
        A['QA'] = dscr('QA', [16, 66, S])
        A['KA'] = dscr('KA', [16, 66, S])
        A['V'] = dscr('V', [S, D])
        A['AO'] = dscr('AO', [S, D])
        self.A = A

        with ExitStack() as es:
            self.P = P = Prog(nc, es)
            G = lambda name, shape, dt=F32: es.enter_context(nc.sbuf_tensor(name, shape, dt))
            self.ps = [es.enter_context(nc.psum_tensor(f"ps{i}", [128, 512], F32)) for i in range(8)]
            self.ident = G('ident', [128, 128])
            self.ones = G('ones', [128, 128])
            self.SC = G('SC', [128, 8, 128])
            self.shift_bc = G('shift_bc', [128, D])
            self.scale_bc = G('scale_bc', [128, D])
            self.gate_bc = G('gate_bc', [128, D])
            self.g_bc = G('g_bc', [128, D])
            self.b_bc = G('b_bc', [128, D])
            self.bs = G('bs', [128, 2, 6])
            self.mv = G('mv', [128, 2])
            self.rs = G('rs', [128, 1])
            self.emit_globals()
            order = self.cfg.get('stages', ['conv', 'peer0', 'attn', 'peer1'])
            cur = A['x']
            nxt = {'conv': A['X1'], 'peer0': A['X2'], 'attn': A['X3'], 'peer1': A['out']}
            for i, st in enumerate(order):
                dst = A['out'] if i == len(order) - 1 else nxt[st]
                if st == 'conv':
                    self.emit_adaln(0)
                    self.emit_conv1(cur)
                    self.emit_proj_out(cur, dst, A['conv_out_w'], A['conv_out_b'], src_fm=A['ST'])
                elif st == 'attn':
                    self.emit_adaln(2)
                    self.emit_attn1(cur)
                    self.emit_attn2()
                    self.emit_proj_out(cur, dst, A['attn_out_w'], A['attn_out_b'], src_tm=A['AO'])
                else:
                    L = int(st[-1])
                    self.emit_adaln(1 + 2 * L)
                    self.emit_peer1(cur, L)
                    self.emit_peer2(cur, dst, L)
                cur = dst
            P.barrier()
            print(f"[kern] ops={P.nops} waits={P.nwaits} sems={P.nsem} seq={P.seq}")
        return nc

    def emit_globals(self):
        P, nc = self.P, self.nc
        ident, ones = self.ident, self.ones
        P.op('pool', lambda e: e.memset(ident[:], 1.0), writes=['ident'])
        P.op('pool', lambda e: e.affine_select(out=ident[:], in_=ident[:], pattern=[[-1, 128]],
                                               compare_op=ALU.is_equal, fill=0.0, base=0, channel_multiplier=1),
             reads=['ident'], writes=['ident'])
        P.op('pool', lambda e: e.memset(ones[:], 1.0), writes=['ones'])
        with Stage(self, 'gl') as st:
            ct = st.T('ct', [128, 8])
            P.op('sp', lambda e: e.dma_start(out=ct[:], in_=self.A['c_l']), writes=['ct'], dma=True)
            P.op('act', lambda e: e.activation(out=ct[:], in_=ct[:], func=AF.Silu), reads=['ct'], writes=['ct'])
            SC = self.SC
            P.op('dve', lambda e: e.tensor_copy(out=SC[:], in_=ct[:].unsqueeze(2).to_broadcast([128, 8, 128])),
                 reads=['ct'], writes=['SC'])

    def modulate(self, xt, ht, rx, rh):
        P = self.P
        P.op('dve', lambda e: e.tensor_tensor(out=ht, in0=xt, in1=self.scale_bc[:], op=ALU.mult),
             reads=[rx, 'scale_bc'], writes=[rh])
        P.op('pool', lambda e: e.tensor_tensor(out=ht, in0=ht, in1=self.shift_bc[:], op=ALU.add),
             reads=[rh, 'shift_bc'], writes=[rh])

    def transpose8(self, src, rsrc, dstT, rdst, col0, pb):
        P = self.P
        ps = self.ps
        for half in range(2):
            bank = ps[pb + half]
            rb = f'ps{pb + half}'
            for kk in range(4):
                k = half * 4 + kk
                P.op('pe', lambda e, k=k, kk=kk, bank=bank: e.transpose(out=bank[:, kk * 128:(kk + 1) * 128],
                                                                      in_=src[:, k * 128:(k + 1) * 128],
                                                                      identity=self.ident[:]),
                     reads=[rsrc, 'ident'], writes=[rb])
            dst = dstT[:, half * 4:half * 4 + 4, col0:col0 + 128]
            srcp = bank[:].rearrange("p (k n) -> p k n", k=4)
            if half == 0:
                P.op('act', lambda e, dst=dst, srcp=srcp: e.copy(out=dst, in_=srcp), reads=[rb], writes=[rdst])
            else:
                P.op('dve', lambda e, dst=dst, srcp=srcp: e.tensor_copy(out=dst, in_=srcp), reads=[rb], writes=[rdst])

    def layernorm_inplace(self, r, rr):
        P = self.P
        bs, mv, rs = self.bs, self.mv, self.rs
        for c in range(2):
            P.op('dve', lambda e, c=c: e.bn_stats(out=bs[:, c, :], in_=r[:, c * 512:(c + 1) * 512]),
                 reads=[rr], writes=['bs'])
        P.op('dve', lambda e: e.bn_aggr(out=mv[:], in_=bs[:].rearrange("p a b -> p (a b)")), reads=['bs'], writes=['mv'])
        P.op('dve', lambda e: e.tensor_scalar(out=rs[:], in0=mv[:, 1:2], scalar1=EPS, scalar2=None, op0=ALU.add),
             reads=['mv'], writes=['rs'])
        P.op('act', lambda e: e.activation(out=rs[:], in_=rs[:], func=AF.Sqrt), reads=['rs'], writes=['rs'])
        P.op('dve', lambda e: e.reciprocal(out=rs[:], in_=rs[:]), reads=['rs'], writes=['rs'])
        P.op('dve', lambda e: e.tensor_scalar(out=r, in0=r, scalar1=mv[:, 0:1], scalar2=rs[:, 0:1],
                                              op0=ALU.subtract, op1=ALU.mult), reads=[rr, 'mv', 'rs'], writes=[rr])
        P.op('pool', lambda e: e.tensor_tensor(out=r, in0=r, in1=self.g_bc[:], op=ALU.mult), reads=[rr, 'g_bc'], writes=[rr])
        P.op('pool', lambda e: e.tensor_tensor(out=r, in0=r, in1=self.b_bc[:], op=ALU.add), reads=[rr, 'b_bc'], writes=[rr])

    def emit_adaln(self, sub):
        P, nc, A, ps = self.P, self.nc, self.A, self.ps
        L = sub // 2
        wsrc = (A['ada_mix_w'] if sub % 2 == 0 else A['ada_ffn_w'])[L]
        ones = self.ones
        with Stage(self, f'ada{sub}') as st:
            brow = st.T('brow', [1, 3 * D])
            lrow = st.T('lrow', [1, 2 * D])
            wch = [st.T(f'wch{i}', [128, 8, 512]) for i in range(2)]
            P.op('sp', lambda e: e.dma_start(out=brow[:], in_=A['ada_b'][sub:sub + 1, :]), writes=['brow'], dma=True)
            P.op('sp', lambda e: e.dma_start(out=lrow[:, 0:D], in_=A['ln_g'][sub:sub + 1, :]), writes=['lrow'], dma=True)
            P.op('sp', lambda e: e.dma_start(out=lrow[:, D:2 * D], in_=A['ln_b'][sub:sub + 1, :]), writes=['lrow'], dma=True)
            dsts = [self.shift_bc, self.shift_bc, self.scale_bc, self.scale_bc, self.gate_bc, self.gate_bc]
            names = ['shift_bc', 'shift_bc', 'scale_bc', 'scale_bc', 'gate_bc', 'gate_bc']
            for n6 in range(6):
                wb = wch[n6 % 2]
                rw = f'wch{n6 % 2}'
                P.op('sp', lambda e, wb=wb, n6=n6: e.dma_start(
                    out=wb[:], in_=wsrc[:, n6 * 512:(n6 + 1) * 512].rearrange("(k p) n -> p k n", p=128)),
                    writes=[rw], dma=True)
                bank = ps[n6 % 2]
                rb = f'ps{n6 % 2}'
                for k in range(8):
                    P.op('pe', lambda e, k=k, wb=wb, bank=bank: e.matmul(out=bank[:], lhsT=self.SC[:, k, :], rhs=wb[:, k, :],
                                                                        start=(k == 0), stop=False),
                         reads=['SC', rw], writes=[rb])
                P.op('pe', lambda e, bank=bank, n6=n6: e.matmul(out=bank[:], lhsT=ones[0:1, :], rhs=brow[0:1, n6 * 512:(n6 + 1) * 512],
                                                                start=False, stop=True),
                     reads=['ones', 'brow'], writes=[rb])
                dst = dsts[n6][:, (n6 % 2) * 512:(n6 % 2 + 1) * 512]
                if n6 in (2, 3):
                    P.op('dve', lambda e, dst=dst, bank=bank: e.tensor_scalar(out=dst, in0=bank[:], scalar1=1.0, scalar2=None, op0=ALU.add),
                         reads=[rb], writes=[names[n6]])
                else:
                    P.op('dve', lambda e, dst=dst, bank=bank: e.tensor_copy(out=dst, in_=bank[:]), reads=[rb], writes=[names[n6]])
            for j in range(4):
                bank = ps[2 + j % 2]
                rb = f'ps{2 + j % 2}'
                P.op('pe', lambda e, bank=bank, j=j: e.matmul(out=bank[:], lhsT=ones[0:1, :], rhs=lrow[0:1, j * 512:(j + 1) * 512],
                                                              start=True, stop=True), reads=['ones', 'lrow'], writes=[rb])
                dstt = self.g_bc if j < 2 else self.b_bc
                dst = dstt[:, (j % 2) * 512:(j % 2 + 1) * 512]
                P.op('act', lambda e, dst=dst, bank=bank: e.copy(out=dst, in_=bank[:]), reads=[rb],
                     writes=['g_bc' if j < 2 else 'b_bc'])

    def emit_conv1(self, xin):
        P, nc, A, ps = self.P, self.nc, self.A, self.ps
        ones = self.ones
        with Stage(self, 'c1') as st:
            win = st.T('win', [128, 8, 2048])
            cib = st.T('cib', [128, 16])
            dw = st.T('dw', [128, 8, 31])
            cv = st.T('cv', [128, 3, 8])
            xts = [st.T(f'xt{i}', [128, D]) for i in range(2)]
            ht = st.T('ht', [128, D])
            hT = st.T('hT', [128, 8, 512])
            acc = st.T('acc', [128, 8, 512])
            aext = st.T('aext', [128, 8, 542])
            sig = [st.T(f'sig{i}', [128, 512]) for i in range(2)]
            sq = [st.T(f'sq{i}', [128, 512]) for i in range(2)]
            meant = st.T('meant', [128, 512])
            rstd = st.T('rstd', [128, 512])
            tmp = st.T('tmp', [128, 512])
            for q in range(4):
                P.op('sp', lambda e, q=q: e.dma_start(out=win[:, :, q * 512:(q + 1) * 512],
                                                      in_=A['conv_in_w'][:, q * 512:(q + 1) * 512].rearrange("(k p) n -> p k n", p=128)),
                     writes=[('win', q)], dma=True)
            P.op('sp', lambda e: e.dma_start(out=cib[:], in_=A['conv_in_b_l']), writes=['cib'], dma=True)
            P.op('sp', lambda e: e.dma_start(out=dw[:], in_=A['conv_dw_w_l']), writes=['dw'], dma=True)
            P.op('sp', lambda e: e.dma_start(out=cv[:], in_=A['conv_vec_l']), writes=['cv'], dma=True)
            for cc in range(8):
                P.op('pool', lambda e, cc=cc: e.memset(aext[:, cc, 0:30], 0.0), writes=[('aext', cc)])
            ti = 0
            for jb in range(8):
                for tl in range(4):
                    xt = xts[ti % 2]
                    rx = f'xt{ti % 2}'
                    P.op('sp', lambda e, xt=xt, ti=ti: e.dma_start(out=xt[:], in_=xin[ti * 128:(ti + 1) * 128, :]),
                         writes=[rx], dma=True)
                    self.modulate(xt[:], ht[:], rx, 'ht')
                    self.transpose8(ht, 'ht', hT, 'hT', tl * 128, 0)
                    ti += 1
                for cc in range(8):
                    pa, pb = ps[2 + (cc % 2) * 2], ps[3 + (cc % 2) * 2]
                    ra, rb = f'ps{2 + (cc % 2) * 2}', f'ps{3 + (cc % 2) * 2}'
                    for k in range(8):
                        P.op('pe', lambda e, k=k, cc=cc, pa=pa: e.matmul(out=pa[:], lhsT=win[:, k, cc * 128:(cc + 1) * 128], rhs=hT[:, k, :],
                                                                        start=(k == 0), stop=(k == 7)),
                             reads=[('win', cc // 4), 'hT'], writes=[ra])
                    for k in range(8):
                        P.op('pe', lambda e, k=k, cc=cc, pb=pb: e.matmul(out=pb[:], lhsT=win[:, k, D + cc * 128:D + (cc + 1) * 128], rhs=hT[:, k, :],
                                                                        start=(k == 0), stop=(k == 7)),
                             reads=[('win', 2 + cc // 4), 'hT'], writes=[rb])
                    sg = sig[cc % 2]
                    rsg = f'sig{cc % 2}'
                    P.op('act', lambda e, sg=sg, pb=pb, cc=cc: e.activation(out=sg[:], in_=pb[:], func=AF.Sigmoid, bias=cib[:, 8 + cc:9 + cc], scale=1.0),
                         reads=[rb, 'cib'], writes=[rsg])
                    P.op('dve', lambda e, sg=sg, pa=pa, cc=cc: e.scalar_tensor_tensor(out=aext[:, cc, 30:542], in0=pa[:], scalar=cib[:, cc:cc + 1], in1=sg[:],
                                                                                 op0=ALU.add, op1=ALU.mult),
                         reads=[ra, rsg, 'cib'], writes=[('aext', cc)])
                for cc in range(8):
                    P.op('dve', lambda e, cc=cc: e.tensor_scalar(out=acc[:, cc, :], in0=aext[:, cc, 0:512], scalar1=dw[:, cc, 0:1], scalar2=cv[:, 0, cc:cc + 1],
                                                                 op0=ALU.mult, op1=ALU.add),
                         reads=[('aext', cc), 'dw', 'cv'], writes=[('acc', cc)])
                    for w in range(1, 31):
                        P.op('dve', lambda e, cc=cc, w=w: e.scalar_tensor_tensor(out=acc[:, cc, :], in0=aext[:, cc, w:w + 512], scalar=dw[:, cc, w:w + 1],
                                                                                 in1=acc[:, cc, :], op0=ALU.mult, op1=ALU.add),
                             reads=[('aext', cc), 'dw', ('acc', cc)], writes=[('acc', cc)])
                    P.op('act', lambda e, cc=cc: e.copy(out=aext[:, cc, 0:30], in_=aext[:, cc, 512:542]),
                         reads=[('aext', cc)], writes=[('aext', cc)])
                for cc in range(8):
                    s2 = sq[cc % 2]
                    rs2 = f'sq{cc % 2}'
                    P.op('act', lambda e, cc=cc, s2=s2: e.activation(out=s2[:], in_=acc[:, cc, :], func=AF.Square),
                         reads=[('acc', cc)], writes=[rs2])
                    P.op('pe', lambda e, cc=cc: e.matmul(out=ps[6][:], lhsT=ones[:], rhs=acc[:, cc, :], start=(cc == 0), stop=(cc == 7)),
                         reads=['ones', ('acc', cc)], writes=['ps6'])
                    P.op('pe', lambda e, cc=cc, s2=s2: e.matmul(out=ps[7][:], lhsT=ones[:], rhs=s2[:], start=(cc == 0), stop=(cc == 7)),
                         reads=['ones', rs2], writes=['ps7'])
                P.op('act', lambda e: e.activation(out=meant[:], in_=ps[6][:], func=AF.Copy, scale=1.0 / D), reads=['ps6'], writes=['meant'])
                P.op('dve', lambda e: e.tensor_tensor(out=tmp[:], in0=meant[:], in1=meant[:], op=ALU.mult), reads=['meant'], writes=['tmp'])
                P.op('dve', lambda e: e.scalar_tensor_tensor(out=rstd[:], in0=ps[7][:], scalar=1.0 / D, in1=tmp[:], op0=ALU.mult, op1=ALU.subtract),
                     reads=['ps7', 'tmp'], writes=['rstd'])
                P.op('dve', lambda e: e.tensor_scalar(out=rstd[:], in0=rstd[:], scalar1=EPS, scalar2=None, op0=ALU.add), reads=['rstd'], writes=['rstd'])
                P.op('act', lambda e: e.activation(out=rstd[:], in_=rstd[:], func=AF.Sqrt), reads=['rstd'], writes=['rstd'])
                P.op('dve', lambda e: e.reciprocal(out=rstd[:], in_=rstd[:]), reads=['rstd'], writes=['rstd'])
                for cc in range(8):
                    P.op('dve', lambda e, cc=cc: e.tensor_tensor(out=acc[:, cc, :], in0=acc[:, cc, :], in1=meant[:], op=ALU.subtract),
                         reads=[('acc', cc), 'meant'], writes=[('acc', cc)])
                    P.op('pool', lambda e, cc=cc: e.tensor_tensor(out=acc[:, cc, :], in0=acc[:, cc, :], in1=rstd[:], op=ALU.mult),
                         reads=[('acc', cc), 'rstd'], writes=[('acc', cc)])
                    P.op('act', lambda e, cc=cc: e.activation(out=acc[:, cc, :], in_=acc[:, cc, :], func=AF.Silu,
                                                              bias=cv[:, 2, cc:cc + 1], scale=cv[:, 1, cc:cc + 1]),
                         reads=[('acc', cc), 'cv'], writes=[('acc', cc)])
                P.op('sp', lambda e, jb=jb: e.dma_start(out=A['ST'][:, :, jb * 512:(jb + 1) * 512].rearrange("c p t -> p c t"), in_=acc[:]),
                     reads=[('acc', cc) for cc in range(8)], writes=[('ST', jb)], dma=True)

    def emit_proj_out(self, xin, dst, w_ap, b_ap, src_fm=None, src_tm=None):
        P, nc, A, ps = self.P, self.nc, self.A, self.ps
        ones = self.ones
        with Stage(self, 'po') as st:
            wo = st.T('wo', [128, 8, D])
            bo = st.T('bo', [1, D])
            xts = [st.T(f'xt{i}', [128, D]) for i in range(2)]
            rts = [st.T(f'rt{i}', [128, D]) for i in range(2)]
            if src_fm is not None:
                sT = [st.T(f'sT{i}', [128, 8, 512]) for i in range(2)]
            else:
                ao = [st.T(f'ao{i}', [128, D]) for i in range(2)]
                aT = [st.T(f'aT{i}', [128, 8, 128]) for i in range(2)]
            for q in range(2):
                P.op('sp', lambda e, q=q: e.dma_start(out=wo[:, :, q * 512:(q + 1) * 512],
                                                      in_=w_ap[:, q * 512:(q + 1) * 512].rearrange("(k p) n -> p k n", p=128)),
                     writes=[('wo', q)], dma=True)
            P.op('sp', lambda e: e.dma_start(out=bo[:], in_=b_ap), writes=['bo'], dma=True)
            for ti in range(NT):
                xt = xts[ti % 2]
                rx = f'xt{ti % 2}'
                rt = rts[ti % 2]
                rr = f'rt{ti % 2}'
                P.op('sp', lambda e, xt=xt, ti=ti: e.dma_start(out=xt[:], in_=xin[ti * 128:(ti + 1) * 128, :]), writes=[rx], dma=True)
                if src_fm is not None:
                    jb, tl = ti // 4, ti % 4
                    sb = sT[jb % 2]
                    rsb = f'sT{jb % 2}'
                    if tl == 0:
                        P.op('sp', lambda e, sb=sb, jb=jb: e.dma_start(out=sb[:], in_=src_fm[:, :, jb * 512:(jb + 1) * 512].rearrange("c p t -> p c t")),
                             reads=[('ST', jb)], writes=[rsb], dma=True)
                    lhs = lambda k, sb=sb, tl=tl: sb[:, k, tl * 128:(tl + 1) * 128]
                    rl = rsb
                else:
                    a = ao[ti % 2]
                    ra = f'ao{ti % 2}'
                    at = aT[ti % 2]
                    rat = f'aT{ti % 2}'
                    P.op('sp', lambda e, a=a, ti=ti: e.dma_start(out=a[:], in_=src_tm[ti * 128:(ti + 1) * 128, :]), writes=[ra], dma=True)
                    self.transpose8(a, ra, at, rat, 0, 4)
                    lhs = lambda k, at=at: at[:, k, :]
                    rl = rat
                pb = (ti % 2) * 2
                for half in range(2):
                    bank = ps[pb + half]
                    rb = f'ps{pb + half}'
                    for k in range(8):
                        P.op('pe', lambda e, k=k, bank=bank, half=half, lhs=lhs: e.matmul(out=bank[:], lhsT=lhs(k), rhs=wo[:, k, half * 512:(half + 1) * 512],
                                                                                     start=(k == 0), stop=False),
                             reads=[rl, ('wo', half)], writes=[rb])
                    P.op('pe', lambda e, bank=bank, half=half: e.matmul(out=bank[:], lhsT=ones[0:1, :], rhs=bo[0:1, half * 512:(half + 1) * 512],
                                                                        start=False, stop=True), reads=['ones', 'bo'], writes=[rb])
                    P.op('dve', lambda e, bank=bank, half=half, rt=rt: e.tensor_tensor(out=rt[:, half * 512:(half + 1) * 512], in0=bank[:],
                                                                                  in1=self.gate_bc[:, half * 512:(half + 1) * 512], op=ALU.mult),
                         reads=[rb, 'gate_bc'], writes=[rr])
                P.op('dve', lambda e, rt=rt, xt=xt: e.scalar_tensor_tensor(out=rt[:], in0=xt[:], scalar=ALPHA, in1=rt[:], op0=ALU.mult, op1=ALU.add),
                     reads=[rx, rr], writes=[rr])
                self.layernorm_inplace(rt[:], rr)
                P.op('sp', lambda e, rt=rt, ti=ti: e.dma_start(out=dst[ti * 128:(ti + 1) * 128, :], in_=rt[:]), reads=[rr], dma=True)

    def emit_peer1(self, xin, L):
        P, nc, A, ps = self.P, self.nc, self.A, self.ps
        with Stage(self, f'p1{L}') as st:
            wq = st.T('wq', [128, 8, 2048])
            skT = st.T('skT', [128, 2, 128])
            xts = [st.T(f'xt{i}', [128, D]) for i in range(2)]
            ht = st.T('ht', [128, D])
            hT = st.T('hT', [128, 8, 256])
            qT = st.T('qT', [128, 16, 256])
            sc = st.T('sc', [128, 16, 128])
            m = st.T('m', [128, 16, 16])
            ix = st.T('ix', [128, 16, 16], U32)
            ixf = st.T('ixf', [128, 16, 16])
            wk = st.T('wk', [128, 128])
            cand = st.T('cand', [128, 8, 256])
            candi = st.T('candi', [128, 8, 256])
            wk2 = st.T('wk2', [128, 256])
            junk = st.T('junk', [128, 256])
            ts = st.T('ts', [128, 8, 16])
            ef = st.T('ef', [128, 128])
            ei = [st.T(f'ei{i}', [128, 128], I32) for i in range(2)]
            gt = [st.T(f'gt{i}', [128, 8, 16]) for i in range(2)]
            gsum = st.T('gsum', [128, 8])
            for q in range(4):
                P.op('sp', lambda e, q=q: e.dma_start(out=wq[:, :, q * 512:(q + 1) * 512],
                                                      in_=A['peer_query_w'][L][:, q * 512:(q + 1) * 512].rearrange("(k p) n -> p k n", p=128)),
                     writes=[('wq', q)], dma=True)
            P.op('sp', lambda e: e.dma_start(out=skT[:], in_=A['peer_skT'][L].rearrange("h d k -> d h k")), writes=['skT'], dma=True)
            ti = 0
            for jb in range(S // 256):
                for tl in range(2):
                    xt = xts[ti % 2]
                    rx = f'xt{ti % 2}'
                    P.op('sp', lambda e, xt=xt, ti=ti: e.dma_start(out=xt[:], in_=xin[ti * 128:(ti + 1) * 128, :]), writes=[rx], dma=True)
                    self.modulate(xt[:], ht[:], rx, 'ht')
                    self.transpose8(ht, 'ht', hT, 'hT', tl * 128, 0)
                    ti += 1
                for c in range(16):
                    bank = ps[2 + c % 2]
                    rb = f'ps{2 + c % 2}'
                    for k in range(8):
                        P.op('pe', lambda e, k=k, c=c, bank=bank: e.matmul(out=bank[:, 0:256], lhsT=wq[:, k, c * 128:(c + 1) * 128], rhs=hT[:, k, :],
                                                                          start=(k == 0), stop=(k == 7)),
                             reads=[('wq', c // 4), 'hT'], writes=[rb])
                    if c % 2 == 0:
                        P.op('act', lambda e, c=c, bank=bank: e.copy(out=qT[:, c, :], in_=bank[:, 0:256]), reads=[rb], writes=[('qT', c)])
                    else:
                        P.op('dve', lambda e, c=c, bank=bank: e.tensor_copy(out=qT[:, c, :], in_=bank[:, 0:256]), reads=[rb], writes=[('qT', c)])
                for tl in range(2):
                    tix = jb * 2 + tl
                    for c in range(16):
                        bank = ps[4 + c // 4]
                        rb = f'ps{4 + c // 4}'
                        P.op('pe', lambda e, c=c, bank=bank, tl=tl: e.matmul(out=bank[:, (c % 4) * 128:(c % 4 + 1) * 128],
                                                                            lhsT=qT[:, c, tl * 128:(tl + 1) * 128], rhs=skT[:, c % 2, :],
                                                                            start=True, stop=True),
                             reads=[('qT', c), 'skT'], writes=[rb])
                    for g4 in range(4):
                        P.op('act', lambda e, g4=g4: e.copy(out=sc[:, g4 * 4:(g4 + 1) * 4, :], in_=ps[4 + g4][:].rearrange("p (a k) -> p a k", a=4)),
                             reads=[f'ps{4 + g4}'], writes=['sc'])
                    for c in range(16):
                        P.op('dve', lambda e, c=c: e.max(out=m[:, c, 0:8], in_=sc[:, c, :]), reads=['sc'], writes=['m'])
                        P.op('dve', lambda e, c=c: e.max_index(out=ix[:, c, 0:8], in_max=m[:, c, 0:8], in_values=sc[:, c, :]),
                             reads=['sc', 'm'], writes=['ix'])
                        P.op('dve', lambda e, c=c: e.match_replace(out=wk[:], in_to_replace=m[:, c, 0:8], in_values=sc[:, c, :], imm_value=-1e30),
                             reads=['sc', 'm'], writes=['wk'])
                        P.op('dve', lambda e, c=c: e.max(out=m[:, c, 8:16], in_=wk[:]), reads=['wk'], writes=['m'])
                        P.op('dve', lambda e, c=c: e.max_index(out=ix[:, c, 8:16], in_max=m[:, c, 8:16], in_values=wk[:]),
                             reads=['wk', 'm'], writes=['ix'])
                    P.op('dve', lambda e: e.tensor_copy(out=ixf[:], in_=ix[:]), reads=['ix'], writes=['ixf'])
                    m4 = m[:].rearrange("p (h two) k -> p h two k", two=2)
                    i4 = ixf[:].rearrange("p (h two) k -> p h two k", two=2)
                    c4 = cand[:].rearrange("p h (a b) -> p h a b", a=16)
                    ci4 = candi[:].rearrange("p h (a b) -> p h a b", a=16)
                    for h in range(8):
                        P.op('dve', lambda e, h=h: e.tensor_tensor(out=c4[:, h], in0=m4[:, h, 0, :].unsqueeze(2).to_broadcast([128, 16, 16]),
                                                                   in1=m4[:, h, 1, :].unsqueeze(1).to_broadcast([128, 16, 16]), op=ALU.add),
                             reads=['m'], writes=['cand'])
                    P.op('dve', lambda e: e.tensor_scalar(out=i4[:, :, 0, :], in0=i4[:, :, 0, :], scalar1=128.0, scalar2=None, op0=ALU.mult),
                         reads=['ixf'], writes=['ixf'])
                    for h in range(8):
                        P.op('dve', lambda e, h=h: e.tensor_tensor(out=ci4[:, h], in0=i4[:, h, 0, :].unsqueeze(2).to_broadcast([128, 16, 16]),
                                                                   in1=i4[:, h, 1, :].unsqueeze(1).to_broadcast([128, 16, 16]), op=ALU.add),
                             reads=['ixf'], writes=['candi'])
                    for h in range(8):
                        P.op('dve', lambda e, h=h: e.max(out=ts[:, h, 0:8], in_=cand[:, h, :]), reads=['cand'], writes=['ts'])
                        P.op('dve', lambda e, h=h: e.match_replace(out=wk2[:], in_to_replace=ts[:, h, 0:8], in_values=cand[:, h, :], imm_value=-1e30),
                             reads=['cand', 'ts'], writes=['wk2'])
                        P.op('dve', lambda e, h=h: e.max(out=ts[:, h, 8:16], in_=wk2[:]), reads=['wk2'], writes=['ts'])
                    for h in range(8):
                        for k in range(16):
                            P.op('dve', lambda e, h=h, k=k: e.scalar_tensor_tensor(out=junk[:], in0=cand[:, h, :], scalar=ts[:, h, k:k + 1], in1=candi[:, h, :],
                                                                                   op0=ALU.is_equal, op1=ALU.mult, accum_out=ef[:, h * 16 + k:h * 16 + k + 1]),
                                 reads=['cand', 'candi', 'ts'], writes=['junk', 'ef'])
                    eib = ei[tix % 2]
                    rei = f'ei{tix % 2}'
                    gtb = gt[tix % 2]
                    rgt = f'gt{tix % 2}'
                    P.op('dve', lambda e: e.tensor_scalar(out=ef[:], in0=ef[:], scalar1=float(NEXP - 1), scalar2=float(L * NEXP), op0=ALU.min, op1=ALU.add),
                         reads=['ef'], writes=['ef'])
                    P.op('dve', lambda e, eib=eib: e.tensor_copy(out=eib[:], in_=ef[:]), reads=['ef'], writes=[rei])
                    P.op('dve', lambda e, gtb=gtb: e.tensor_tensor(out=gtb[:], in0=ts[:], in1=ts[:, :, 0:1].to_broadcast([128, 8, 16]), op=ALU.subtract),
                         reads=['ts'], writes=[rgt])
                    P.op('act', lambda e, gtb=gtb: e.activation(out=gtb[:], in_=gtb[:], func=AF.Exp), reads=[rgt], writes=[rgt])
                    P.op('dve', lambda e, gtb=gtb: e.tensor_reduce(out=gsum[:], in_=gtb[:], axis=AX.X, op=ALU.add), reads=[rgt], writes=['gsum'])
                    P.op('dve', lambda e: e.reciprocal(out=gsum[:], in_=gsum[:]), reads=['gsum'], writes=['gsum'])
                    P.op('dve', lambda e, gtb=gtb: e.tensor_tensor(out=gtb[:], in0=gtb[:], in1=gsum[:].unsqueeze(2).to_broadcast([128, 8, 16]), op=ALU.mult),
                         reads=[rgt, 'gsum'], writes=[rgt])
                    P.op('sp', lambda e, eib=eib, tix=tix: e.dma_start(out=A['IDX'][tix * 128:(tix + 1) * 128, :], in_=eib[:]),
                         reads=[rei], writes=[('IDX', tix)], dma=True)
                    P.op('sp', lambda e, gtb=gtb, tix=tix: e.dma_start(out=A['GATE'][tix * 128:(tix + 1) * 128, :], in_=gtb[:].rearrange("p h k -> p (h k)")),
                         reads=[rgt], writes=[('GATE', tix)], dma=True)

    def emit_peer2(self, xin, dst, L):
        P, nc, A, ps = self.P, self.nc, self.A, self.ps
        NB = self.cfg.get('nb', 14)
        U = A['peer_u'].rearrange("l e d -> (l e) d")
        V = A['peer_v'].rearrange("l e d -> (l e) d")
        with Stage(self, f'p2{L}') as st:
            xts = [st.T(f'xt{i}', [128, D]) for i in range(2)]
            hts = [st.T(f'ht{i}', [128, D]) for i in range(2)]
            eis = [st.T(f'ei{i}', [128, 128], I32) for i in range(2)]
            gts = [st.T(f'gt{i}', [128, 128]) for i in range(2)]
            ub = [st.T(f'ub{i}', [128, D]) for i in range(NB)]
            vb = [st.T(f'vb{i}', [128, D]) for i in range(NB)]
            junk = st.T('junk', [128, D])
            apre = st.T('apre', [128, 128])
            coef = st.T('coef', [128, 128])
            accs = [st.T(f'acc{i}', [128, D]) for i in range(2)]
            nu = nv = 0
            for ti in range(NT):
                b = ti % 2
                xt, ht, eib, gtb, acc = xts[b], hts[b], eis[b], gts[b], accs[b]
                rx, rh, rei, rgt, racc = f'xt{b}', f'ht{b}', f'ei{b}', f'gt{b}', f'acc{b}'
                P.op('sp', lambda e, xt=xt, ti=ti: e.dma_start(out=xt[:], in_=xin[ti * 128:(ti + 1) * 128, :]), writes=[rx], dma=True)
                P.op('sp', lambda e, eib=eib, ti=ti: e.dma_start(out=eib[:], in_=A['IDX'][ti * 128:(ti + 1) * 128, :]),
                     reads=[('IDX', ti)], writes=[rei], dma=True)
                P.op('sp', lambda e, gtb=gtb, ti=ti: e.dma_start(out=gtb[:], in_=A['GATE'][ti * 128:(ti + 1) * 128, :]),
                     reads=[('GATE', ti)], writes=[rgt], dma=True)
                self.modulate(xt[:], ht[:], rx, rh)
                for k in range(128):
                    s = nu % NB
                    nu += 1
                    P.op('pool', lambda e, s=s, k=k, eib=eib: e.indirect_dma_start(
                        out=ub[s][:], out_offset=None, in_=U,
                        in_offset=bass.IndirectOffsetOnAxis(ap=eib[:, k:k + 1], axis=0)),
                        reads=[rei], writes=[('ub', s)], dma=True)
                    P.op('dve', lambda e, s=s, k=k, ht=ht: e.scalar_tensor_tensor(out=junk[:], in0=ub[s][:], scalar=1.0, in1=ht[:], op0=ALU.mult, op1=ALU.mult,
                                                                               accum_out=apre[:, k:k + 1]),
                         reads=[('ub', s), rh], writes=['junk', 'apre'])
                P.op('act', lambda e: e.activation(out=coef[:], in_=apre[:], func=AF.Gelu), reads=['apre'], writes=['coef'])
                P.op('dve', lambda e, gtb=gtb: e.tensor_tensor(out=coef[:], in0=coef[:], in1=gtb[:], op=ALU.mult), reads=['coef', rgt], writes=['coef'])
                for k in range(128):
                    s = nv % NB
                    nv += 1
                    P.op('pool', lambda e, s=s, k=k, eib=eib: e.indirect_dma_start(
                        out=vb[s][:], out_offset=None, in_=V,
                        in_offset=bass.IndirectOffsetOnAxis(ap=eib[:, k:k + 1], axis=0)),
                        reads=[rei], writes=[('vb', s)], dma=True)
                    if k == 0:
                        P.op('dve', lambda e, s=s, acc=acc: e.tensor_scalar(out=acc[:], in0=vb[s][:], scalar1=coef[:, 0:1], scalar2=None, op0=ALU.mult),
                             reads=[('vb', s), 'coef'], writes=[racc])
                    else:
                        P.op('dve', lambda e, s=s, k=k, acc=acc: e.scalar_tensor_tensor(out=acc[:], in0=vb[s][:], scalar=coef[:, k:k + 1], in1=acc[:],
                                                                                      op0=ALU.mult, op1=ALU.add),
                             reads=[('vb', s), 'coef', racc], writes=[racc])
                P.op('pool', lambda e, acc=acc: e.tensor_tensor(out=acc[:], in0=acc[:], in1=self.gate_bc[:], op=ALU.mult), reads=[racc, 'gate_bc'], writes=[racc])
                P.op('dve', lambda e, acc=acc, xt=xt: e.scalar_tensor_tensor(out=acc[:], in0=xt[:], scalar=ALPHA, in1=acc[:], op0=ALU.mult, op1=ALU.add),
                     reads=[rx, racc], writes=[racc])
                self.layernorm_inplace(acc[:], racc)
                P.op('sp', lambda e, acc=acc, ti=ti: e.dma_start(out=dst[ti * 128:(ti + 1) * 128, :], in_=acc[:]), reads=[racc], dma=True)

    def emit_attn1(self, xin):
        P, nc, A, ps = self.P, self.nc, self.A, self.ps
        ones = self.ones
        NCOL = 3 * D + 16
        with Stage(self, 'a1') as st:
            win = st.T('win', [128, 8, NCOL])
            qkb = st.T('qkb', [128, 16])
            vbr = st.T('vbr', [1, D])
            fb = st.T('fb', [16, 1])
            xts = [st.T(f'xt{i}', [128, D]) for i in range(2)]
            ht = st.T('ht', [128, D])
            hT = st.T('hT', [128, 8, 512])
            qko = [st.T(f'qko{i}', [128, 512]) for i in range(2)]
            vo = [st.T(f'vo{i}', [128, D]) for i in range(2)]
            Fc = st.T('Fc', [16, S])
            spt = st.T('spt', [16, 512])
            o16 = st.T('o16', [16, S])
            for q in range(6):
                P.op('sp', lambda e, q=q: e.dma_start(out=win[:, :, q * 512:(q + 1) * 512],
                                                      in_=A['attn_in_w'][:, q * 512:(q + 1) * 512].rearrange("(k p) n -> p k n", p=128)),
                     writes=[('win', q)], dma=True)
            P.op('sp', lambda e: e.dma_start(out=win[:, :, 3 * D:NCOL], in_=A['attn_in_w'][:, 3 * D:NCOL].rearrange("(k p) n -> p k n", p=128)),
                 writes=[('win', 6)], dma=True)
            P.op('sp', lambda e: e.dma_start(out=qkb[:], in_=A['attn_qkb_l']), writes=['qkb'], dma=True)
            P.op('sp', lambda e: e.dma_start(out=vbr[:], in_=A['attn_vb']), writes=['vbr'], dma=True)
            P.op('sp', lambda e: e.dma_start(out=fb[:], in_=A['attn_fb']), writes=['fb'], dma=True)
            P.op('dve', lambda e: e.tensor_scalar(out=qkb[:, 0:8], in0=qkb[:, 0:8], scalar1=0.125, scalar2=None, op0=ALU.mult), reads=['qkb'], writes=['qkb'])
            P.op('dve', lambda e: e.tensor_scalar(out=fb[:], in0=fb[:], scalar1=-1.0, scalar2=None, op0=ALU.mult), reads=['fb'], writes=['fb'])
            P.op('pool', lambda e: e.memset(o16[:], 1.0), writes=['o16'])
            ti = 0
            for jb in range(8):
                cols = slice(jb * 512, (jb + 1) * 512)
                for tl in range(4):
                    xt = xts[ti % 2]
                    rx = f'xt{ti % 2}'
                    P.op('sp', lambda e, xt=xt, ti=ti: e.dma_start(out=xt[:], in_=xin[ti * 128:(ti + 1) * 128, :]), writes=[rx], dma=True)
                    self.modulate(xt[:], ht[:], rx, 'ht')
                    self.transpose8(ht, 'ht', hT, 'hT', tl * 128, 0)
                    ti += 1
                for c in range(16):
                    bank = ps[2 + c % 2]
                    rb = f'ps{2 + c % 2}'
                    for k in range(8):
                        P.op('pe', lambda e, k=k, c=c, bank=bank: e.matmul(out=bank[:], lhsT=win[:, k, c * 128:(c + 1) * 128], rhs=hT[:, k, :],
                                                                          start=(k == 0), stop=(k == 7)),
                             reads=[('win', c // 4), 'hT'], writes=[rb])
                    ob = qko[c % 2]
                    rob = f'qko{c % 2}'
                    P.op('act', lambda e, c=c, bank=bank, ob=ob: e.activation(out=ob[:], in_=bank[:], func=AF.Identity, bias=qkb[:, c:c + 1],
                                                                             scale=(0.125 if c < 8 else 1.0)),
                         reads=[rb, 'qkb'], writes=[rob])
                    dstt = A['QA'] if c < 8 else A['KA']
                    for hh in range(2):
                        head = (c % 8) * 2 + hh
                        P.op('sp', lambda e, ob=ob, hh=hh, head=head, dstt=dstt: e.dma_start(out=dstt[head, 0:64, cols], in_=ob[hh * 64:(hh + 1) * 64, :]),
                             reads=[rob], writes=[('QK', c, hh)], dma=True)
                for tl in range(4):
                    tix = jb * 4 + tl
                    vt = vo[tix % 2]
                    rv = f'vo{tix % 2}'
                    for half in range(2):
                        bank = ps[4 + half]
                        rb = f'ps{4 + half}'
                        for k in range(8):
                            P.op('pe', lambda e, k=k, bank=bank, half=half, tl=tl: e.matmul(out=bank[:], lhsT=hT[:, k, tl * 128:(tl + 1) * 128],
                                                                                        rhs=win[:, k, 2 * D + half * 512:2 * D + (half + 1) * 512],
                                                                                        start=(k == 0), stop=False),
                                 reads=['hT', ('win', 4 + half)], writes=[rb])
                        P.op('pe', lambda e, bank=bank, half=half: e.matmul(out=bank[:], lhsT=ones[0:1, :], rhs=vbr[0:1, half * 512:(half + 1) * 512],
                                                                            start=False, stop=True), reads=['ones', 'vbr'], writes=[rb])
                        if half == 0:
                            P.op('act', lambda e, vt=vt, bank=bank: e.copy(out=vt[:, 0:512], in_=bank[:]), reads=[rb], writes=[rv])
                        else:
                            P.op('dve', lambda e, vt=vt, bank=bank: e.tensor_copy(out=vt[:, 512:1024], in_=bank[:]), reads=[rb], writes=[rv])
                    P.op('sp', lambda e, vt=vt, tix=tix: e.dma_start(out=A['V'][tix * 128:(tix + 1) * 128, :], in_=vt[:]), reads=[rv], writes=[('V', tix)], dma=True)
                for k in range(8):
                    P.op('pe', lambda e, k=k: e.matmul(out=ps[6][0:16, :], lhsT=win[:, k, 3 * D:NCOL], rhs=hT[:, k, :], start=(k == 0), stop=(k == 7)),
                         reads=[('win', 6), 'hT'], writes=['ps6'])
                P.op('act', lambda e: e.activation(out=spt[:], in_=ps[6][0:16, :], func=AF.Exp, bias=fb[:, 0:1], scale=-1.0), reads=['ps6', 'fb'], writes=['spt'])
                P.op('act', lambda e: e.activation(out=spt[:], in_=spt[:], func=AF.Ln, bias=1.0, scale=1.0), reads=['spt'], writes=['spt'])
                P.op('dve', lambda e: e.tensor_scalar(out=spt[:], in0=spt[:], scalar1=-1.0, scalar2=None, op0=ALU.mult), reads=['spt'], writes=['spt'])
                init = 0.0 if jb == 0 else Fc[:, jb * 512 - 1:jb * 512]
                P.op('dve', lambda e, init=init: e.tensor_tensor_scan(out=Fc[:, cols], data0=o16[:, 0:512], data1=spt[:], initial=init,
                                                                      op0=ALU.mult, op1=ALU.add), reads=['o16', 'spt', 'Fc'], writes=['Fc'])
            P.op('sp', lambda e: e.dma_start(out=A['QA'][:, 64, :], in_=Fc[:]), reads=['Fc'], writes=['QAf'], dma=True)
            P.op('sp', lambda e: e.dma_start(out=A['QA'][:, 65, :], in_=o16[:]), reads=['o16'], writes=['QAo'], dma=True)
            P.op('sp', lambda e: e.dma_start(out=A['KA'][:, 64, :], in_=o16[:]), reads=['o16'], writes=['KAo'], dma=True)
            P.op('dve', lambda e: e.tensor_scalar(out=Fc[:], in0=Fc[:], scalar1=-1.0, scalar2=None, op0=ALU.mult), reads=['Fc'], writes=['Fc'])
            P.op('sp', lambda e: e.dma_start(out=A['KA'][:, 65, :], in_=Fc[:]), reads=['Fc'], writes=['KAf'], dma=True)

    def emit_attn2(self):
        P, nc, A, ps = self.P, self.nc, self.A, self.ps
        with Stage(self, 'a2') as st:
            QAh = [st.T(f'QAh{i}', [66, S]) for i in range(2)]
            KAh = [st.T(f'KAh{i}', [66, S]) for i in range(2)]
            Vh = [st.T(f'Vh{i}', [128, 32, 65]) for i in range(2)]
            Oh = [st.T(f'Oh{i}', [128, 32, 64]) for i in range(2)]
            pt = [st.T(f'pt{i}', [128, 512]) for i in range(3)]
            lm = [st.T(f'lm{i}', [128, 512]) for i in range(2)]
            mask = st.T('mask', [128, 4, 512])
            rec = st.T('rec', [128, 4])
            P.op('pool', lambda e: e.memset(mask[:], 0.0), writes=['mask'])
            for i4 in range(4):
                P.op('pool', lambda e, i4=i4: e.affine_select(out=mask[:, i4, :], in_=mask[:, i4, :], pattern=[[1, 512]], compare_op=ALU.is_ge,
                                                              fill=NEG, base=-128 * i4, channel_multiplier=-1), reads=['mask'], writes=['mask'])
            for i in range(2):
                P.op('pool', lambda e, i=i: e.memset(Vh[i][:, :, 64:65], 1.0), writes=[f'Vh{i}'])
            npt = 0
            nlm = 0
            nS = 0
            def loads(h):
                b = h % 2
                qa, ka, vh = QAh[b], KAh[b], Vh[b]
                rq, rk, rv = f'QAh{b}', f'KAh{b}', f'Vh{b}'
                for q4 in range(4):
                    cs = slice(q4 * 1024, (q4 + 1) * 1024)
                    P.op('sp', lambda e, qa=qa, h=h, cs=cs: e.dma_start(out=qa[:, cs], in_=A['QA'][h, :, cs]), writes=[rq], dma=True)
                    P.op('sp', lambda e, ka=ka, h=h, cs=cs: e.dma_start(out=ka[:, cs], in_=A['KA'][h, :, cs]), writes=[rk], dma=True)
                    P.op('sp', lambda e, vh=vh, h=h, q4=q4: e.dma_start(
                        out=vh[:, q4 * 8:(q4 + 1) * 8, 0:64],
                        in_=A['V'][q4 * 1024:(q4 + 1) * 1024, h * 64:(h + 1) * 64].rearrange("(i p) d -> p i d", p=128)),
                        writes=[rv], dma=True)

            loads(0)
            for h in range(self.cfg.get('nheads', 16)):
                b = h % 2
                qa, ka, vh, oh = QAh[b], KAh[b], Vh[b], Oh[b]
                rq, rk, rv, ro = f'QAh{b}', f'KAh{b}', f'Vh{b}', f'Oh{b}'
                if h + 1 < 16:
                    loads(h + 1)
                for j in range(8):
                    po = ps[4 + j % 2]
                    rpo = f'ps{4 + j % 2}'
                    for i in range(4 * j + 4):
                        sb = ps[nS % 3]
                        rsb = f'ps{nS % 3}'
                        nS += 1
                        P.op('pe', lambda e, sb=sb, ka=ka, qa=qa, i=i, j=j: e.matmul(out=sb[:], lhsT=ka[:, i * 128:(i + 1) * 128], rhs=qa[:, j * 512:(j + 1) * 512],
                                                                                 start=True, stop=True),
                             reads=[rk, rq], writes=[rsb])
                        p_ = pt[npt % 3]
                        rp = f'pt{npt % 3}'
                        npt += 1
                        if i >= 4 * j:
                            l_ = lm[nlm % 2]
                            rl = f'lm{nlm % 2}'
                            nlm += 1
                            P.op('dve', lambda e, l_=l_, sb=sb, i=i, j=j: e.tensor_tensor(out=l_[:], in0=sb[:], in1=mask[:, i - 4 * j, :], op=ALU.add),
                                 reads=[rsb, 'mask'], writes=[rl])
                            P.op('act', lambda e, p_=p_, l_=l_: e.activation(out=p_[:], in_=l_[:], func=AF.Exp), reads=[rl], writes=[rp])
                        else:
                            P.op('act', lambda e, p_=p_, sb=sb: e.activation(out=p_[:], in_=sb[:], func=AF.Exp), reads=[rsb], writes=[rp])
                        for c in range(4):
                            if i <= 4 * j + c:
                                P.op('pe', lambda e, p_=p_, c=c, i=i, j=j, po=po, vh=vh: e.matmul(out=po[:, c * 65:(c + 1) * 65], lhsT=p_[:, c * 128:(c + 1) * 128],
                                                                                              rhs=vh[:, i, :], start=(i == 0 and c == 0), stop=(i == 4 * j + c)),
                                     reads=[rp, rv], writes=[rpo])
                    pov = po[:, 0:260].rearrange("p (c d) -> p c d", c=4)
                    P.op('dve', lambda e, pov=pov: e.reciprocal(out=rec[:], in_=pov[:, :, 64]), reads=[rpo], writes=['rec'])
                    P.op('dve', lambda e, pov=pov, oh=oh, j=j: e.tensor_tensor(out=oh[:, 4 * j:4 * j + 4, :], in0=pov[:, :, 0:64],
                                                                               in1=rec[:].unsqueeze(2).to_broadcast([128, 4, 64]), op=ALU.mult),
                         reads=[rpo, 'rec'], writes=[ro])
                for q4 in range(4):
                    P.op('sp', lambda e, oh=oh, h=h, q4=q4: e.dma_start(
                        out=A['AO'][q4 * 1024:(q4 + 1) * 1024, h * 64:(h + 1) * 64].rearrange("(i p) d -> p i d", p=128),
                        in_=oh[:, q4 * 8:(q4 + 1) * 8, :]), reads=[ro], writes=[('AO', h, q4)], dma=True)


def make_in_maps(inputs, cores=range(8)):
    f = lambda a: np.ascontiguousarray(np.asarray(a, dtype=np.float32))
    sh = {}
    sh['ada_mix_w'] = f(inputs['ada_mix_w'])
    sh['ada_ffn_w'] = f(inputs['ada_ffn_w'])
    amb, afb = f(inputs['ada_mix_b']), f(inputs['ada_ffn_b'])
    sh['ada_b'] = f(np.stack([amb[0], afb[0], amb[1], afb[1]]))
    g1, g2 = f(inputs['ln_mix_g']), f(inputs['ln_ffn_g'])
    b1, b2 = f(inputs['ln_mix_b']), f(inputs['ln_ffn_b'])
    sh['ln_g'] = f(np.stack([g1[0], g2[0], g1[1], g2[1]]))
    sh['ln_b'] = f(np.stack([b1[0], b2[0], b1[1], b2[1]]))
    sh['conv_in_w'] = f(inputs['conv_in_w'][0])
    sh['conv_in_b_l'] = f(np.asarray(inputs['conv_in_b'][0]).reshape(16, 128).T)
    sh['conv_dw_w_l'] = f(np.asarray(inputs['conv_dw_w'][0]).reshape(31, 8, 128).transpose(2, 1, 0))
    sh['conv_vec_l'] = f(np.stack([np.asarray(inputs[k][0]).reshape(8, 128).T for k in ('conv_dw_b', 'conv_ln_g', 'conv_ln_b')], axis=1))
    sh['conv_out_w'] = f(inputs['conv_out_w'][0])
    sh['conv_out_b'] = f(np.asarray(inputs['conv_out_b'][0]).reshape(1, D))
    sh['attn_in_w'] = f(inputs['attn_in_w'][0])
    ab = np.asarray(inputs['attn_in_b'][0])
    sh['attn_qkb_l'] = f(ab[:2 * D].reshape(16, 128).T)
    sh['attn_vb'] = f(ab[2 * D:3 * D].reshape(1, D))
    sh['attn_fb'] = f(ab[3 * D:].reshape(16, 1))
    sh['attn_out_w'] = f(inputs['attn_out_w'][0])
    sh['attn_out_b'] = f(np.asarray(inputs['attn_out_b'][0]).reshape(1, D))
    sh['peer_query_w'] = f(inputs['peer_query_w'])
    k1, k2 = np.asarray(inputs['peer_sub_keys_1']), np.asarray(inputs['peer_sub_keys_2'])
    sh['peer_skT'] = f(np.stack([np.stack([k1[l].T, k2[l].T]) for l in range(2)]))
    sh['peer_u'] = f(inputs['peer_expert_u'])
    sh['peer_v'] = f(inputs['peer_expert_v'])
    x = np.asarray(inputs['x'])
    c = np.asarray(inputs['c'])
    maps = []
    for b in cores:
        m = dict(sh)
        m['x'] = f(x[b])
        m['c_l'] = f(c[b].reshape(8, 128).T)
        maps.append(m)
    return maps


_NC_CACHE = {}


def kernel(**inputs):
    if 'full' not in _NC_CACHE:
        _NC_CACHE['full'] = Kern({}).build()
    nc = _NC_CACHE['full']
    maps = make_in_maps(inputs)
    res = run_bass_kernel_spmd(nc, maps, core_ids=list(range(8)))
    return np.stack([np.asarray(r['out'], dtype=np.float32) for r in res.results], axis=0)
```

```python
import numpy as np
from contextlib import ExitStack
import concourse.bass as bass
import concourse.mybir as mybir
from concourse.bass_utils import run_bass_kernel_spmd

F32 = mybir.dt.float32
I32 = mybir.dt.int32
U32 = mybir.dt.uint32
F32R = mybir.dt.float32r
ALU = mybir.AluOpType
AF = mybir.ActivationFunctionType
AX = mybir.AxisListType

S = 4096
D = 1024
NT = S // 128
ALPHA = float((2 * 2) ** 0.25)
EPS = 1e-5
NEXP = 16384
MAXV = 30000
NEG = -30000.0


class Prog:
    def __init__(self, nc, es):
        self.nc = nc
        self.es = es
        self.eng = {'pe': nc.tensor, 'dve': nc.vector, 'act': nc.scalar,
                    'pool': nc.gpsimd, 'sp': nc.sync}
        self.seq = {e: 0 for e in self.eng}
        self.csem = {e: [] for e in self.eng}
        self.known = {e: {} for e in self.eng}
        self.snap = {}
        self.last_w = {}
        self.readers = {}
        self.semobj = {}
        self.dma_pool = {}
        self.nsem = 0
        self.nwaits = 0
        self.nops = 0
        for q, n in (('sp', 24), ('pool', 24), ('act', 8)):
            self.dma_pool[q] = {'sems': [self._newsem(f"d{q}{i}") for i in range(n)],
                                'cnt': [0] * n, 'next': 0}

    def _newsem(self, name):
        s = self.es.enter_context(self.nc.semaphore(name))
        self.semobj[name] = s
        self.nsem += 1
        return name

    def _need(self, e, tok, skip_self):
        if tok is None:
            return
        name, val, owner = tok
        if skip_self and owner == e:
            return
        if self.known[e].get(name, 0) >= val:
            return
        self.eng[e].wait_ge(self.semobj[name], val)
        self.nwaits += 1
        k = self.known[e]
        k[name] = val
        sn = self.snap.get((name, val))
        if sn:
            for n2, v2 in sn.items():
                if k.get(n2, 0) < v2:
                    k[n2] = v2

    def op(self, e, fn, reads=(), writes=(), dma=False, skip_self=None):
        if skip_self is None:
            skip_self = (e == 'pe')
        if dma:
            skip_self = False
        for r in reads:
            self._need(e, self.last_w.get(r), skip_self)
        for w in writes:
            self._need(e, self.last_w.get(w), skip_self)
            for t in self.readers.get(w, ()):
                self._need(e, t, skip_self)
        self.nops += 1
        if dma:
            pool = self.dma_pool[e]
            i = pool['next']
            pool['next'] = (i + 1) % len(pool['sems'])
            name = pool['sems'][i]
            if pool['cnt'][i] + 16 > MAXV:
                name = self._newsem(f"{name}r{self.nsem}")
                pool['sems'][i] = name
                pool['cnt'][i] = 0
            prev = pool['cnt'][i]
            if prev > 0:
                self._need(e, (name, prev, e + '_dma'), False)
            ins = fn(self.eng[e])
            pool['cnt'][i] = prev + 16
            ins.then_inc(self.semobj[name], 16)
            tok = (name, prev + 16, e + '_dma')
        else:
            n = self.seq[e]
            ep = n // MAXV
            while len(self.csem[e]) <= ep:
                self.csem[e].append(self._newsem(f"c{e}{len(self.csem[e])}"))
            name = self.csem[e][ep]
            ins = fn(self.eng[e])
            ins.then_inc(self.semobj[name], 1)
            self.seq[e] = n + 1
            tok = (name, n - ep * MAXV + 1, e)
        self.snap[(tok[0], tok[1])] = dict(self.known[e])
        for r in reads:
            self.readers.setdefault(r, []).append(tok)
        for w in writes:
            self.last_w[w] = tok
            self.readers[w] = []
        return tok

    def fence(self, e):
        n = self.seq[e]
        if n > 0:
            ep = (n - 1) // MAXV
            self._need(e, (self.csem[e][ep], n - ep * MAXV, e), False)

    def barrier(self):
        toks = []
        for e in self.eng:
            n = self.seq[e]
            if n > 0:
                ep = (n - 1) // MAXV
                toks.append((self.csem[e][ep], n - ep * MAXV, e))
        for q, pool in self.dma_pool.items():
            for name, c in zip(pool['sems'], pool['cnt']):
                if c > 0:
                    toks.append((name, c, q + '_dma'))
        for e in self.eng:
            for t in toks:
                self._need(e, t, False)
        self.last_w.clear()
        self.readers.clear()
        self.snap.clear()


class Stage:
    _n = 0

    def __init__(self, K, name):
        self.K = K
        Stage._n += 1
        self.name = f"{name}{Stage._n}"

    def __enter__(self):
        self.es = ExitStack()
        self.es.__enter__()
        return self

    def T(self, name, shape, dt=F32):
        return self.es.enter_context(self.K.nc.sbuf_tensor(f"{self.name}_{name}", shape, dt))

    def __exit__(self, *a):
        self.K.P.barrier()
        return self.es.__exit__(*a)


class Kern:
    def __init__(self, cfg):
        self.cfg = cfg

    def build(self):
        nc = bass.Bass("TRN2", target_bir_lowering=False)
        self.nc = nc
        dbg = self.cfg.get('debug', False)

        def din(name, shape, dt=F32):
            return nc.dram_tensor(name, list(shape), dt, kind="ExternalInput").ap()

        def dscr(name, shape, dt=F32):
            kind = "ExternalOutput" if (dbg and name in self.cfg.get('expose', ())) else "Internal"
            return nc.dram_tensor(name, list(shape), dt, kind=kind).ap()

        A = {}
        A['x'] = din('x', [S, D])
        A['c_l'] = din('c_l', [128, 8])
        A['ada_mix_w'] = din('ada_mix_w', [2, D, 3 * D])
        A['ada_ffn_w'] = din('ada_ffn_w', [2, D, 3 * D])
        A['ada_b'] = din('ada_b', [4, 3 * D])
        A['ln_g'] = din('ln_g', [4, D])
        A['ln_b'] = din('ln_b', [4, D])
        A['conv_in_w'] = din('conv_in_w', [D, 2 * D])
        A['conv_in_b_l'] = din('conv_in_b_l', [128, 16])
        A['conv_dw_w_l'] = din('conv_dw_w_l', [128, 8, 31])
        A['conv_vec_l'] = din('conv_vec_l', [128, 3, 8])
        A['conv_out_w'] = din('conv_out_w', [D, D])
        A['conv_out_b'] = din('conv_out_b', [1, D])
        A['attn_in_w'] = din('attn_in_w', [D, 3 * D + 16])
        A['attn_qkb_l'] = din('attn_qkb_l', [128, 16])
        A['attn_vb'] = din('attn_vb', [1, D])
        A['attn_fb'] = din('attn_fb', [16, 1])
        A['attn_out_w'] = din('attn_out_w', [D, D])
        A['attn_out_b'] = din('attn_out_b', [1, D])
        A['peer_query_w'] = din('peer_query_w', [2, D, 2 * D])
        A['peer_skT'] = din('peer_skT', [2, 2, 128, 128])
        A['peer_u'] = din('peer_u', [2, NEXP, D])
        A['peer_v'] = din('peer_v', [2, NEXP, D])
        A['out'] = nc.dram_tensor('out', [S, D], F32, kind="ExternalOutput").ap()
        A['X1'] = dscr('X1', [S, D])
        A['X2'] = dscr('X2', [S, D])
        A['X3'] = dscr('X3', [S, D])
        A['ST'] = dscr('ST', [8, 128, S])
        A['IDX'] = dscr('IDX', [S, 128], I32)
        A['GATE'] = dscr('GATE', [S, 128])
        A['QA'] = dscr('QA', [16, 68, S])
        A['KA'] = dscr('KA', [16, 68, S])
        A['V'] = dscr('V', [S, D])
        A['AO'] = dscr('AO', [S, D])
        self.A = A

        with ExitStack() as es:
            self.P = P = Prog(nc, es)
            G = lambda name, shape, dt=F32: es.enter_context(nc.sbuf_tensor(name, shape, dt))
            self.ps = [es.enter_context(nc.psum_tensor(f"ps{i}", [128, 512], F32)) for i in range(8)]
            self.ident = G('ident', [128, 128])
            self.ones = G('ones', [128, 128])
            self.SC = G('SC', [128, 8, 128])
            self.shift_bc = G('shift_bc', [128, D])
            self.scale_bc = G('scale_bc', [128, D])
            self.gate_bc = G('gate_bc', [128, D])
            self.g_bc = G('g_bc', [128, D])
            self.b_bc = G('b_bc', [128, D])
            self.bs = G('bs', [128, 2, 6])
            self.mv = G('mv', [128, 2])
            self.rs = G('rs', [128, 1])
            self.emit_globals()
            order = self.cfg.get('stages', ['conv', 'peer0', 'attn', 'peer1'])
            cur = A['x']
            nxt = {'conv': A['X1'], 'peer0': A['X2'], 'attn': A['X3'], 'peer1': A['out']}
            for i, st in enumerate(order):
                dst = A['out'] if i == len(order) - 1 else nxt[st]
                if st == 'conv':
                    self.emit_adaln(0)
                    self.emit_conv1(cur)
                    self.emit_proj_out(cur, dst, A['conv_out_w'], A['conv_out_b'], src_fm=A['ST'])
                elif st == 'attn':
                    self.emit_adaln(2)
                    self.emit_attn1(cur)
                    self.emit_attn2()
                    self.emit_proj_out(cur, dst, A['attn_out_w'], A['attn_out_b'], src_tm=A['AO'])
                else:
                    L = int(st[-1])
                    self.emit_adaln(1 + 2 * L)
                    self.emit_peer1(cur, L)
                    self.emit_peer2(cur, dst, L)
                cur = dst
            P.barrier()
            print(f"[kern] ops={P.nops} waits={P.nwaits} sems={P.nsem} seq={P.seq}")
        return nc

    def emit_globals(self):
        P, nc = self.P, self.nc
        ident, ones = self.ident, self.ones
        P.op('pool', lambda e: e.memset(ident[:], 1.0), writes=['ident'])
        P.op('pool', lambda e: e.affine_select(out=ident[:], in_=ident[:], pattern=[[-1, 128]],
                                               compare_op=ALU.is_equal, fill=0.0, base=0, channel_multiplier=1),
             reads=['ident'], writes=['ident'])
        P.op('pool', lambda e: e.memset(ones[:], 1.0), writes=['ones'])
        with Stage(self, 'gl') as st:
            ct = st.T('ct', [128, 8])
            P.op('sp', lambda e: e.dma_start(out=ct[:], in_=self.A['c_l']), writes=['ct'], dma=True)
            P.op('act', lambda e: e.activation(out=ct[:], in_=ct[:], func=AF.Silu), reads=['ct'], writes=['ct'])
            SC = self.SC
            P.op('dve', lambda e: e.tensor_copy(out=SC[:], in_=ct[:].unsqueeze(2).to_broadcast([128, 8, 128])),
                 reads=['ct'], writes=['SC'])

    def modulate(self, xt, ht, rx, rh):
        P = self.P
        P.op('dve', lambda e: e.tensor_tensor(out=ht, in0=xt, in1=self.scale_bc[:], op=ALU.mult),
             reads=[rx, 'scale_bc'], writes=[rh])
        P.op('pool', lambda e: e.tensor_tensor(out=ht, in0=ht, in1=self.shift_bc[:], op=ALU.add),
             reads=[rh, 'shift_bc'], writes=[rh])

    def transpose8(self, src, rsrc, dstT, rdst, col0, pb):
        P = self.P
        ps = self.ps
        for half in range(2):
            bank = ps[pb + half]
            rb = f'ps{pb + half}'
            for kk in range(4):
                k = half * 4 + kk
                P.op('pe', lambda e, k=k, kk=kk, bank=bank: e.transpose(out=bank[:, kk * 128:(kk + 1) * 128],
                                                                      in_=src[:, k * 128:(k + 1) * 128],
                                                                      identity=self.ident[:]),
                     reads=[rsrc, 'ident'], writes=[rb])
            dst = dstT[:, half * 4:half * 4 + 4, col0:col0 + 128]
            srcp = bank[:].rearrange("p (k n) -> p k n", k=4)
            if half == 0:
                P.op('act', lambda e, dst=dst, srcp=srcp: e.copy(out=dst, in_=srcp), reads=[rb], writes=[rdst])
            else:
                P.op('dve', lambda e, dst=dst, srcp=srcp: e.tensor_copy(out=dst, in_=srcp), reads=[rb], writes=[rdst])

    def layernorm_inplace(self, r, rr):
        P = self.P
        bs, mv, rs = self.bs, self.mv, self.rs
        for c in range(2):
            P.op('dve', lambda e, c=c: e.bn_stats(out=bs[:, c, :], in_=r[:, c * 512:(c + 1) * 512]),
                 reads=[rr], writes=['bs'])
        P.op('dve', lambda e: e.bn_aggr(out=mv[:], in_=bs[:].rearrange("p a b -> p (a b)")), reads=['bs'], writes=['mv'])
        P.op('dve', lambda e: e.tensor_scalar(out=rs[:], in0=mv[:, 1:2], scalar1=EPS, scalar2=None, op0=ALU.add),
             reads=['mv'], writes=['rs'])
        P.op('act', lambda e: e.activation(out=rs[:], in_=rs[:], func=AF.Sqrt), reads=['rs'], writes=['rs'])
        P.op('dve', lambda e: e.reciprocal(out=rs[:], in_=rs[:]), reads=['rs'], writes=['rs'])
        P.op('dve', lambda e: e.tensor_scalar(out=r, in0=r, scalar1=mv[:, 0:1], scalar2=rs[:, 0:1],
                                              op0=ALU.subtract, op1=ALU.mult), reads=[rr, 'mv', 'rs'], writes=[rr])
        P.op('pool', lambda e: e.tensor_tensor(out=r, in0=r, in1=self.g_bc[:], op=ALU.mult), reads=[rr, 'g_bc'], writes=[rr])
        P.op('pool', lambda e: e.tensor_tensor(out=r, in0=r, in1=self.b_bc[:], op=ALU.add), reads=[rr, 'b_bc'], writes=[rr])

    def emit_adaln(self, sub):
        P, nc, A, ps = self.P, self.nc, self.A, self.ps
        L = sub // 2
        wsrc = (A['ada_mix_w'] if sub % 2 == 0 else A['ada_ffn_w'])[L]
        ones = self.ones
        with Stage(self, f'ada{sub}') as st:
            brow = st.T('brow', [1, 3 * D])
            lrow = st.T('lrow', [1, 2 * D])
            wch = [st.T(f'wch{i}', [128, 8, 512]) for i in range(2)]
            P.op('sp', lambda e: e.dma_start(out=brow[:], in_=A['ada_b'][sub:sub + 1, :]), writes=['brow'], dma=True)
            P.op('sp', lambda e: e.dma_start(out=lrow[:, 0:D], in_=A['ln_g'][sub:sub + 1, :]), writes=['lrow'], dma=True)
            P.op('sp', lambda e: e.dma_start(out=lrow[:, D:2 * D], in_=A['ln_b'][sub:sub + 1, :]), writes=['lrow'], dma=True)
            dsts = [self.shift_bc, self.shift_bc, self.scale_bc, self.scale_bc, self.gate_bc, self.gate_bc]
            names = ['shift_bc', 'shift_bc', 'scale_bc', 'scale_bc', 'gate_bc', 'gate_bc']
            for n6 in range(6):
                wb = wch[n6 % 2]
                rw = f'wch{n6 % 2}'
                P.op('sp', lambda e, wb=wb, n6=n6: e.dma_start(
                    out=wb[:], in_=wsrc[:, n6 * 512:(n6 + 1) * 512].rearrange("(k p) n -> p k n", p=128)),
                    writes=[rw], dma=True)
                bank = ps[n6 % 2]
                rb = f'ps{n6 % 2}'
                for k in range(8):
                    P.op('pe', lambda e, k=k, wb=wb, bank=bank: e.matmul(out=bank[:], lhsT=self.SC[:, k, :], rhs=wb[:, k, :],
                                                                        start=(k == 0), stop=False),
                         reads=['SC', rw], writes=[rb])
                P.op('pe', lambda e, bank=bank, n6=n6: e.matmul(out=bank[:], lhsT=ones[0:1, :], rhs=brow[0:1, n6 * 512:(n6 + 1) * 512],
                                                                start=False, stop=True),
                     reads=['ones', 'brow'], writes=[rb])
                dst = dsts[n6][:, (n6 % 2) * 512:(n6 % 2 + 1) * 512]
                if n6 in (2, 3):
                    P.op('dve', lambda e, dst=dst, bank=bank: e.tensor_scalar(out=dst, in0=bank[:], scalar1=1.0, scalar2=None, op0=ALU.add),
                         reads=[rb], writes=[names[n6]])
                else:
                    P.op('dve', lambda e, dst=dst, bank=bank: e.tensor_copy(out=dst, in_=bank[:]), reads=[rb], writes=[names[n6]])
            for j in range(4):
                bank = ps[2 + j % 2]
                rb = f'ps{2 + j % 2}'
                P.op('pe', lambda e, bank=bank, j=j: e.matmul(out=bank[:], lhsT=ones[0:1, :], rhs=lrow[0:1, j * 512:(j + 1) * 512],
                                                              start=True, stop=True), reads=['ones', 'lrow'], writes=[rb])
                dstt = self.g_bc if j < 2 else self.b_bc
                dst = dstt[:, (j % 2) * 512:(j % 2 + 1) * 512]
                P.op('act', lambda e, dst=dst, bank=bank: e.copy(out=dst, in_=bank[:]), reads=[rb],
                     writes=['g_bc' if j < 2 else 'b_bc'])

    def emit_conv1(self, xin):
        P, nc, A, ps = self.P, self.nc, self.A, self.ps
        ones = self.ones
        with Stage(self, 'c1') as st:
            win = st.T('win', [128, 8, 2048])
            cib = st.T('cib', [128, 16])
            dw = st.T('dw', [128, 8, 31])
            cv = st.T('cv', [128, 3, 8])
            xts = [st.T(f'xt{i}', [128, D]) for i in range(2)]
            ht = st.T('ht', [128, D])
            hT = st.T('hT', [128, 8, 512])
            acc = st.T('acc', [128, 8, 512])
            aext = st.T('aext', [128, 8, 542])
            sig = [st.T(f'sig{i}', [128, 512]) for i in range(2)]
            sq = [st.T(f'sq{i}', [128, 512]) for i in range(2)]
            meant = st.T('meant', [128, 512])
            rstd = st.T('rstd', [128, 512])
            tmp = st.T('tmp', [128, 512])
            for q in range(4):
                P.op('sp', lambda e, q=q: e.dma_start(out=win[:, :, q * 512:(q + 1) * 512],
                                                      in_=A['conv_in_w'][:, q * 512:(q + 1) * 512].rearrange("(k p) n -> p k n", p=128)),
                     writes=[('win', q)], dma=True)
            P.op('sp', lambda e: e.dma_start(out=cib[:], in_=A['conv_in_b_l']), writes=['cib'], dma=True)
            P.op('sp', lambda e: e.dma_start(out=dw[:], in_=A['conv_dw_w_l']), writes=['dw'], dma=True)
            P.op('sp', lambda e: e.dma_start(out=cv[:], in_=A['conv_vec_l']), writes=['cv'], dma=True)
            for cc in range(8):
                P.op('pool', lambda e, cc=cc: e.memset(aext[:, cc, 0:30], 0.0), writes=[('aext', cc)])
            ti = 0
            for jb in range(8):
                for tl in range(4):
                    xt = xts[ti % 2]
                    rx = f'xt{ti % 2}'
                    P.op('sp', lambda e, xt=xt, ti=ti: e.dma_start(out=xt[:], in_=xin[ti * 128:(ti + 1) * 128, :]),
                         writes=[rx], dma=True)
                    self.modulate(xt[:], ht[:], rx, 'ht')
                    self.transpose8(ht, 'ht', hT, 'hT', tl * 128, 0)
                    ti += 1
                for cc in range(8):
                    pa, pb = ps[2 + (cc % 2) * 2], ps[3 + (cc % 2) * 2]
                    ra, rb = f'ps{2 + (cc % 2) * 2}', f'ps{3 + (cc % 2) * 2}'
                    for k in range(8):
                        P.op('pe', lambda e, k=k, cc=cc, pa=pa: e.matmul(out=pa[:], lhsT=win[:, k, cc * 128:(cc + 1) * 128], rhs=hT[:, k, :],
                                                                        start=(k == 0), stop=(k == 7)),
                             reads=[('win', cc // 4), 'hT'], writes=[ra])
                    for k in range(8):
                        P.op('pe', lambda e, k=k, cc=cc, pb=pb: e.matmul(out=pb[:], lhsT=win[:, k, D + cc * 128:D + (cc + 1) * 128], rhs=hT[:, k, :],
                                                                        start=(k == 0), stop=(k == 7)),
                             reads=[('win', 2 + cc // 4), 'hT'], writes=[rb])
                    sg = sig[cc % 2]
                    rsg = f'sig{cc % 2}'
                    P.op('act', lambda e, sg=sg, pb=pb, cc=cc: e.activation(out=sg[:], in_=pb[:], func=AF.Sigmoid, bias=cib[:, 8 + cc:9 + cc], scale=1.0),
                         reads=[rb, 'cib'], writes=[rsg])
                    P.op('dve', lambda e, sg=sg, pa=pa, cc=cc: e.scalar_tensor_tensor(out=aext[:, cc, 30:542], in0=pa[:], scalar=cib[:, cc:cc + 1], in1=sg[:],
                                                                                 op0=ALU.add, op1=ALU.mult),
                         reads=[ra, rsg, 'cib'], writes=[('aext', cc)])
                for cc in range(8):
                    P.op('dve', lambda e, cc=cc: e.tensor_scalar(out=acc[:, cc, :], in0=aext[:, cc, 0:512], scalar1=dw[:, cc, 0:1], scalar2=cv[:, 0, cc:cc + 1],
                                                                 op0=ALU.mult, op1=ALU.add),
                         reads=[('aext', cc), 'dw', 'cv'], writes=[('acc', cc)])
                    for w in range(1, 31):
                        P.op('dve', lambda e, cc=cc, w=w: e.scalar_tensor_tensor(out=acc[:, cc, :], in0=aext[:, cc, w:w + 512], scalar=dw[:, cc, w:w + 1],
                                                                                 in1=acc[:, cc, :], op0=ALU.mult, op1=ALU.add),
                             reads=[('aext', cc), 'dw', ('acc', cc)], writes=[('acc', cc)])
                    P.op('act', lambda e, cc=cc: e.copy(out=aext[:, cc, 0:30], in_=aext[:, cc, 512:542]),
                         reads=[('aext', cc)], writes=[('aext', cc)])
                for cc in range(8):
                    s2 = sq[cc % 2]
                    rs2 = f'sq{cc % 2}'
                    P.op('act', lambda e, cc=cc, s2=s2: e.activation(out=s2[:], in_=acc[:, cc, :], func=AF.Square),
                         reads=[('acc', cc)], writes=[rs2])
                    P.op('pe', lambda e, cc=cc: e.matmul(out=ps[6][:], lhsT=ones[:], rhs=acc[:, cc, :], start=(cc == 0), stop=(cc == 7)),
                         reads=['ones', ('acc', cc)], writes=['ps6'])
                    P.op('pe', lambda e, cc=cc, s2=s2: e.matmul(out=ps[7][:], lhsT=ones[:], rhs=s2[:], start=(cc == 0), stop=(cc == 7)),
                         reads=['ones', rs2], writes=['ps7'])
                P.op('act', lambda e: e.activation(out=meant[:], in_=ps[6][:], func=AF.Copy, scale=1.0 / D), reads=['ps6'], writes=['meant'])
                P.op('dve', lambda e: e.tensor_tensor(out=tmp[:], in0=meant[:], in1=meant[:], op=ALU.mult), reads=['meant'], writes=['tmp'])
                P.op('dve', lambda e: e.scalar_tensor_tensor(out=rstd[:], in0=ps[7][:], scalar=1.0 / D, in1=tmp[:], op0=ALU.mult, op1=ALU.subtract),
                     reads=['ps7', 'tmp'], writes=['rstd'])
                P.op('dve', lambda e: e.tensor_scalar(out=rstd[:], in0=rstd[:], scalar1=EPS, scalar2=None, op0=ALU.add), reads=['rstd'], writes=['rstd'])
                P.op('act', lambda e: e.activation(out=rstd[:], in_=rstd[:], func=AF.Sqrt), reads=['rstd'], writes=['rstd'])
                P.op('dve', lambda e: e.reciprocal(out=rstd[:], in_=rstd[:]), reads=['rstd'], writes=['rstd'])
                for cc in range(8):
                    P.op('dve', lambda e, cc=cc: e.tensor_tensor(out=acc[:, cc, :], in0=acc[:, cc, :], in1=meant[:], op=ALU.subtract),
                         reads=[('acc', cc), 'meant'], writes=[('acc', cc)])
                    P.op('pool', lambda e, cc=cc: e.tensor_tensor(out=acc[:, cc, :], in0=acc[:, cc, :], in1=rstd[:], op=ALU.mult),
                         reads=[('acc', cc), 'rstd'], writes=[('acc', cc)])
                    P.op('act', lambda e, cc=cc: e.activation(out=acc[:, cc, :], in_=acc[:, cc, :], func=AF.Silu,
                                                              bias=cv[:, 2, cc:cc + 1], scale=cv[:, 1, cc:cc + 1]),
                         reads=[('acc', cc), 'cv'], writes=[('acc', cc)])
                P.op('sp', lambda e, jb=jb: e.dma_start(out=A['ST'][:, :, jb * 512:(jb + 1) * 512].rearrange("c p t -> p c t"), in_=acc[:]),
                     reads=[('acc', cc) for cc in range(8)], writes=[('ST', jb)], dma=True)

    def emit_proj_out(self, xin, dst, w_ap, b_ap, src_fm=None, src_tm=None):
        P, nc, A, ps = self.P, self.nc, self.A, self.ps
        ones = self.ones
        with Stage(self, 'po') as st:
            wo = st.T('wo', [128, 8, D])
            bo = st.T('bo', [1, D])
            xts = [st.T(f'xt{i}', [128, D]) for i in range(2)]
            rts = [st.T(f'rt{i}', [128, D]) for i in range(2)]
            if src_fm is not None:
                sT = [st.T(f'sT{i}', [128, 8, 512]) for i in range(2)]
            else:
                ao = [st.T(f'ao{i}', [128, D]) for i in range(2)]
                aT = [st.T(f'aT{i}', [128, 8, 128]) for i in range(2)]
            for q in range(2):
                P.op('sp', lambda e, q=q: e.dma_start(out=wo[:, :, q * 512:(q + 1) * 512],
                                                      in_=w_ap[:, q * 512:(q + 1) * 512].rearrange("(k p) n -> p k n", p=128)),
                     writes=[('wo', q)], dma=True)
            P.op('sp', lambda e: e.dma_start(out=bo[:], in_=b_ap), writes=['bo'], dma=True)
            for ti in range(NT):
                xt = xts[ti % 2]
                rx = f'xt{ti % 2}'
                rt = rts[ti % 2]
                rr = f'rt{ti % 2}'
                P.op('sp', lambda e, xt=xt, ti=ti: e.dma_start(out=xt[:], in_=xin[ti * 128:(ti + 1) * 128, :]), writes=[rx], dma=True)
                if src_fm is not None:
                    jb, tl = ti // 4, ti % 4
                    sb = sT[jb % 2]
                    rsb = f'sT{jb % 2}'
                    if tl == 0:
                        P.op('sp', lambda e, sb=sb, jb=jb: e.dma_start(out=sb[:], in_=src_fm[:, :, jb * 512:(jb + 1) * 512].rearrange("c p t -> p c t")),
                             reads=[('ST', jb)], writes=[rsb], dma=True)
                    lhs = lambda k, sb=sb, tl=tl: sb[:, k, tl * 128:(tl + 1) * 128]
                    rl = rsb
                else:
                    a = ao[ti % 2]
                    ra = f'ao{ti % 2}'
                    at = aT[ti % 2]
                    rat = f'aT{ti % 2}'
                    P.op('sp', lambda e, a=a, ti=ti: e.dma_start(out=a[:], in_=src_tm[ti * 128:(ti + 1) * 128, :]), writes=[ra], dma=True)
                    self.transpose8(a, ra, at, rat, 0, 4)
                    lhs = lambda k, at=at: at[:, k, :]
                    rl = rat
                pb = (ti % 2) * 2
                for half in range(2):
                    bank = ps[pb + half]
                    rb = f'ps{pb + half}'
                    for k in range(8):
                        P.op('pe', lambda e, k=k, bank=bank, half=half, lhs=lhs: e.matmul(out=bank[:], lhsT=lhs(k), rhs=wo[:, k, half * 512:(half + 1) * 512],
                                                                                     start=(k == 0), stop=False),
                             reads=[rl, ('wo', half)], writes=[rb])
                    P.op('pe', lambda e, bank=bank, half=half: e.matmul(out=bank[:], lhsT=ones[0:1, :], rhs=bo[0:1, half * 512:(half + 1) * 512],
                                                                        start=False, stop=True), reads=['ones', 'bo'], writes=[rb])
                    P.op('dve', lambda e, bank=bank, half=half, rt=rt: e.tensor_tensor(out=rt[:, half * 512:(half + 1) * 512], in0=bank[:],
                                                                                  in1=self.gate_bc[:, half * 512:(half + 1) * 512], op=ALU.mult),
                         reads=[rb, 'gate_bc'], writes=[rr])
                P.op('dve', lambda e, rt=rt, xt=xt: e.scalar_tensor_tensor(out=rt[:], in0=xt[:], scalar=ALPHA, in1=rt[:], op0=ALU.mult, op1=ALU.add),
                     reads=[rx, rr], writes=[rr])
                self.layernorm_inplace(rt[:], rr)
                P.op('sp', lambda e, rt=rt, ti=ti: e.dma_start(out=dst[ti * 128:(ti + 1) * 128, :], in_=rt[:]), reads=[rr], dma=True)

    def emit_peer1(self, xin, L):
        P, nc, A, ps = self.P, self.nc, self.A, self.ps
        with Stage(self, f'p1{L}') as st:
            wq = st.T('wq', [128, 8, 2048])
            skT = st.T('skT', [128, 2, 128])
            xts = [st.T(f'xt{i}', [128, D]) for i in range(2)]
            ht = st.T('ht', [128, D])
            hT = st.T('hT', [128, 8, 256])
            qT = st.T('qT', [128, 16, 256])
            sc = st.T('sc', [128, 16, 128])
            m = st.T('m', [128, 16, 16])
            ix = st.T('ix', [128, 16, 16], U32)
            ixf = st.T('ixf', [128, 16, 16])
            wk = st.T('wk', [128, 16, 128])
            cand = st.T('cand', [128, 8, 256])
            candi = st.T('candi', [128, 8, 256])
            wk2 = st.T('wk2', [128, 8, 256])
            junk = [st.T(f'junk{i}', [128, 256]) for i in range(2)]
            ts = st.T('ts', [128, 8, 16])
            ef = st.T('ef', [128, 128])
            ei = [st.T(f'ei{i}', [128, 128], I32) for i in range(2)]
            gt = [st.T(f'gt{i}', [128, 8, 16]) for i in range(2)]
            gsum = st.T('gsum', [128, 8])
            for q in range(4):
                P.op('sp', lambda e, q=q: e.dma_start(out=wq[:, :, q * 512:(q + 1) * 512],
                                                      in_=A['peer_query_w'][L][:, q * 512:(q + 1) * 512].rearrange("(k p) n -> p k n", p=128)),
                     writes=[('wq', q)], dma=True)
            P.op('sp', lambda e: e.dma_start(out=skT[:], in_=A['peer_skT'][L].rearrange("h d k -> d h k")), writes=['skT'], dma=True)
            ti = 0
            for jb in range(S // 256):
                for tl in range(2):
                    xt = xts[ti % 2]
                    rx = f'xt{ti % 2}'
                    P.op('sp', lambda e, xt=xt, ti=ti: e.dma_start(out=xt[:], in_=xin[ti * 128:(ti + 1) * 128, :]), writes=[rx], dma=True)
                    self.modulate(xt[:], ht[:], rx, 'ht')
                    self.transpose8(ht, 'ht', hT, 'hT', tl * 128, 0)
                    ti += 1
                for c in range(16):
                    bank = ps[2 + c % 2]
                    rb = f'ps{2 + c % 2}'
                    for k in range(8):
                        P.op('pe', lambda e, k=k, c=c, bank=bank: e.matmul(out=bank[:, 0:256], lhsT=wq[:, k, c * 128:(c + 1) * 128], rhs=hT[:, k, :],
                                                                          start=(k == 0), stop=(k == 7)),
                             reads=[('wq', c // 4), 'hT'], writes=[rb])
                    if c % 2 == 0:
                        P.op('act', lambda e, c=c, bank=bank: e.copy(out=qT[:, c, :], in_=bank[:, 0:256]), reads=[rb], writes=[('qT', c)])
                    else:
                        P.op('dve', lambda e, c=c, bank=bank: e.tensor_copy(out=qT[:, c, :], in_=bank[:, 0:256]), reads=[rb], writes=[('qT', c)])
                for tl in range(2):
                    tix = jb * 2 + tl
                    for c in range(16):
                        bank = ps[4 + c // 4]
                        rb = f'ps{4 + c // 4}'
                        P.op('pe', lambda e, c=c, bank=bank, tl=tl: e.matmul(out=bank[:, (c % 4) * 128:(c % 4 + 1) * 128],
                                                                            lhsT=qT[:, c, tl * 128:(tl + 1) * 128], rhs=skT[:, c % 2, :],
                                                                            start=True, stop=True),
                             reads=[('qT', c), 'skT'], writes=[rb])
                    for g4 in range(4):
                        P.op('act', lambda e, g4=g4: e.copy(out=sc[:, g4 * 4:(g4 + 1) * 4, :], in_=ps[4 + g4][:].rearrange("p (a k) -> p a k", a=4)),
                             reads=[f'ps{4 + g4}'], writes=['sc'])
                    for c in range(16):
                        P.op('dve', lambda e, c=c: e.max(out=m[:, c, 0:8], in_=sc[:, c, :]), reads=['sc'], writes=[('m0', c)])
                    P.fence('dve')
                    for c in range(16):
                        P.op('dve', lambda e, c=c: e.max_index(out=ix[:, c, 0:8], in_max=m[:, c, 0:8], in_values=sc[:, c, :]),
                             reads=['sc', ('m0', c)], writes=[('ix0', c)])
                        P.op('dve', lambda e, c=c: e.match_replace(out=wk[:, c, :], in_to_replace=m[:, c, 0:8], in_values=sc[:, c, :], imm_value=-1e30),
                             reads=['sc', ('m0', c)], writes=[('wk', c)])
                    P.fence('dve')
                    for c in range(16):
                        P.op('dve', lambda e, c=c: e.max(out=m[:, c, 8:16], in_=wk[:, c, :]), reads=[('wk', c)], writes=[('m1', c)])
                    P.fence('dve')
                    for c in range(16):
                        P.op('dve', lambda e, c=c: e.max_index(out=ix[:, c, 8:16], in_max=m[:, c, 8:16], in_values=wk[:, c, :]),
                             reads=[('wk', c), ('m1', c)], writes=[('ix1', c)])
                    P.fence('dve')
                    mres = [('m0', c) for c in range(16)] + [('m1', c) for c in range(16)]
                    ixres = [('ix0', c) for c in range(16)] + [('ix1', c) for c in range(16)]
                    P.op('dve', lambda e: e.tensor_copy(out=ixf[:], in_=ix[:]), reads=ixres, writes=['ixf'])
                    m4 = m[:].rearrange("p (h two) k -> p h two k", two=2)
                    i4 = ixf[:].rearrange("p (h two) k -> p h two k", two=2)
                    c4 = cand[:].rearrange("p h (a b) -> p h a b", a=16)
                    ci4 = candi[:].rearrange("p h (a b) -> p h a b", a=16)
                    P.op('dve', lambda e: e.tensor_tensor(out=c4, in0=m4[:, :, 0, :].unsqueeze(3).to_broadcast([128, 8, 16, 16]),
                                                          in1=m4[:, :, 1, :].unsqueeze(2).to_broadcast([128, 8, 16, 16]), op=ALU.add),
                         reads=mres, writes=['cand'])
                    P.op('dve', lambda e: e.tensor_scalar(out=i4[:, :, 0, :], in0=i4[:, :, 0, :], scalar1=128.0, scalar2=None, op0=ALU.mult),
                         reads=['ixf'], writes=['ixf'])
                    P.op('dve', lambda e: e.tensor_tensor(out=ci4, in0=i4[:, :, 0, :].unsqueeze(3).to_broadcast([128, 8, 16, 16]),
                                                          in1=i4[:, :, 1, :].unsqueeze(2).to_broadcast([128, 8, 16, 16]), op=ALU.add),
                         reads=['ixf'], writes=['candi'])
                    for h in range(8):
                        P.op('dve', lambda e, h=h: e.max(out=ts[:, h, 0:8], in_=cand[:, h, :]), reads=['cand'], writes=[('ts0', h)])
                    P.fence('dve')
                    for h in range(8):
                        P.op('dve', lambda e, h=h: e.match_replace(out=wk2[:, h, :], in_to_replace=ts[:, h, 0:8], in_values=cand[:, h, :], imm_value=-1e30),
                             reads=['cand', ('ts0', h)], writes=[('wk2', h)])
                    P.fence('dve')
                    for h in range(8):
                        P.op('dve', lambda e, h=h: e.max(out=ts[:, h, 8:16], in_=wk2[:, h, :]), reads=[('wk2', h)], writes=[('ts1', h)])
                    P.fence('dve')
                    tsres = [('ts0', h) for h in range(8)] + [('ts1', h) for h in range(8)]
                    for h in range(8):
                        for k in range(16):
                            P.op('dve', lambda e, h=h, k=k: e.scalar_tensor_tensor(out=junk[(h * 16 + k) % 2][:], in0=cand[:, h, :], scalar=ts[:, h, k:k + 1], in1=candi[:, h, :],
                                                                                   op0=ALU.is_equal, op1=ALU.mult, accum_out=ef[:, h * 16 + k:h * 16 + k + 1]),
                                 reads=['cand', 'candi', ('ts0', h), ('ts1', h)], writes=[('ef', h * 16 + k)])
                    P.fence('dve')
                    eib = ei[tix % 2]
                    rei = f'ei{tix % 2}'
                    gtb = gt[tix % 2]
                    rgt = f'gt{tix % 2}'
                    P.op('dve', lambda e: e.tensor_scalar(out=ef[:], in0=ef[:], scalar1=float(NEXP - 1), scalar2=float(L * NEXP), op0=ALU.min, op1=ALU.add),
                         reads=[('ef', q) for q in range(128)], writes=['ef'])
                    P.op('dve', lambda e, eib=eib: e.tensor_copy(out=eib[:], in_=ef[:]), reads=['ef'], writes=[rei])
                    P.op('dve', lambda e, gtb=gtb: e.tensor_tensor(out=gtb[:], in0=ts[:], in1=ts[:, :, 0:1].to_broadcast([128, 8, 16]), op=ALU.subtract),
                         reads=tsres, writes=[rgt])
                    P.op('act', lambda e, gtb=gtb: e.activation(out=gtb[:], in_=gtb[:], func=AF.Exp), reads=[rgt], writes=[rgt])
                    P.op('dve', lambda e, gtb=gtb: e.tensor_reduce(out=gsum[:], in_=gtb[:], axis=AX.X, op=ALU.add), reads=[rgt], writes=['gsum'])
                    P.op('dve', lambda e: e.reciprocal(out=gsum[:], in_=gsum[:]), reads=['gsum'], writes=['gsum'])
                    P.op('dve', lambda e, gtb=gtb: e.tensor_tensor(out=gtb[:], in0=gtb[:], in1=gsum[:].unsqueeze(2).to_broadcast([128, 8, 16]), op=ALU.mult),
                         reads=[rgt, 'gsum'], writes=[rgt])
                    P.op('sp', lambda e, eib=eib, tix=tix: e.dma_start(out=A['IDX'][tix * 128:(tix + 1) * 128, :], in_=eib[:]),
                         reads=[rei], writes=[('IDX', tix)], dma=True)
                    P.op('sp', lambda e, gtb=gtb, tix=tix: e.dma_start(out=A['GATE'][tix * 128:(tix + 1) * 128, :], in_=gtb[:].rearrange("p h k -> p (h k)")),
                         reads=[rgt], writes=[('GATE', tix)], dma=True)

    def emit_peer2(self, xin, dst, L):
        P, nc, A, ps = self.P, self.nc, self.A, self.ps
        NB = self.cfg.get('nb', 14)
        U = A['peer_u'].rearrange("l e d -> (l e) d")
        V = A['peer_v'].rearrange("l e d -> (l e) d")
        with Stage(self, f'p2{L}') as st:
            xts = [st.T(f'xt{i}', [128, D]) for i in range(2)]
            hts = [st.T(f'ht{i}', [128, D]) for i in range(2)]
            eis = [st.T(f'ei{i}', [128, 128], I32) for i in range(2)]
            gts = [st.T(f'gt{i}', [128, 128]) for i in range(2)]
            ub = [st.T(f'ub{i}', [128, D]) for i in range(NB)]
            vb = [st.T(f'vb{i}', [128, D]) for i in range(NB)]
            junk = st.T('junk', [128, D])
            apre = st.T('apre', [128, 128])
            coef = st.T('coef', [128, 128])
            accs = [st.T(f'acc{i}', [128, D]) for i in range(2)]
            nu = nv = 0
            for ti in range(NT):
                b = ti % 2
                xt, ht, eib, gtb, acc = xts[b], hts[b], eis[b], gts[b], accs[b]
                rx, rh, rei, rgt, racc = f'xt{b}', f'ht{b}', f'ei{b}', f'gt{b}', f'acc{b}'
                P.op('sp', lambda e, xt=xt, ti=ti: e.dma_start(out=xt[:], in_=xin[ti * 128:(ti + 1) * 128, :]), writes=[rx], dma=True)
                P.op('sp', lambda e, eib=eib, ti=ti: e.dma_start(out=eib[:], in_=A['IDX'][ti * 128:(ti + 1) * 128, :]),
                     reads=[('IDX', ti)], writes=[rei], dma=True)
                P.op('sp', lambda e, gtb=gtb, ti=ti: e.dma_start(out=gtb[:], in_=A['GATE'][ti * 128:(ti + 1) * 128, :]),
                     reads=[('GATE', ti)], writes=[rgt], dma=True)
                self.modulate(xt[:], ht[:], rx, rh)
                for k in range(128):
                    s = nu % NB
                    nu += 1
                    P.op('pool', lambda e, s=s, k=k, eib=eib: e.indirect_dma_start(
                        out=ub[s][:], out_offset=None, in_=U,
                        in_offset=bass.IndirectOffsetOnAxis(ap=eib[:, k:k + 1], axis=0)),
                        reads=[rei], writes=[('ub', s)], dma=True)
                    P.op('dve', lambda e, s=s, k=k, ht=ht: e.scalar_tensor_tensor(out=junk[:], in0=ub[s][:], scalar=1.0, in1=ht[:], op0=ALU.mult, op1=ALU.mult,
                                                                               accum_out=apre[:, k:k + 1]),
                         reads=[('ub', s), rh], writes=['junk', 'apre'])
                P.op('act', lambda e: e.activation(out=coef[:], in_=apre[:], func=AF.Gelu), reads=['apre'], writes=['coef'])
                P.op('dve', lambda e, gtb=gtb: e.tensor_tensor(out=coef[:], in0=coef[:], in1=gtb[:], op=ALU.mult), reads=['coef', rgt], writes=['coef'])
                for k in range(128):
                    s = nv % NB
                    nv += 1
                    P.op('pool', lambda e, s=s, k=k, eib=eib: e.indirect_dma_start(
                        out=vb[s][:], out_offset=None, in_=V,
                        in_offset=bass.IndirectOffsetOnAxis(ap=eib[:, k:k + 1], axis=0)),
                        reads=[rei], writes=[('vb', s)], dma=True)
                    if k == 0:
                        P.op('dve', lambda e, s=s, acc=acc: e.tensor_scalar(out=acc[:], in0=vb[s][:], scalar1=coef[:, 0:1], scalar2=None, op0=ALU.mult),
                             reads=[('vb', s), 'coef'], writes=[racc])
                    else:
                        P.op('dve', lambda e, s=s, k=k, acc=acc: e.scalar_tensor_tensor(out=acc[:], in0=vb[s][:], scalar=coef[:, k:k + 1], in1=acc[:],
                                                                                      op0=ALU.mult, op1=ALU.add),
                             reads=[('vb', s), 'coef', racc], writes=[racc])
                P.op('pool', lambda e, acc=acc: e.tensor_tensor(out=acc[:], in0=acc[:], in1=self.gate_bc[:], op=ALU.mult), reads=[racc, 'gate_bc'], writes=[racc])
                P.op('dve', lambda e, acc=acc, xt=xt: e.scalar_tensor_tensor(out=acc[:], in0=xt[:], scalar=ALPHA, in1=acc[:], op0=ALU.mult, op1=ALU.add),
                     reads=[rx, racc], writes=[racc])
                self.layernorm_inplace(acc[:], racc)
                P.op('sp', lambda e, acc=acc, ti=ti: e.dma_start(out=dst[ti * 128:(ti + 1) * 128, :], in_=acc[:]), reads=[racc], dma=True)

    def emit_attn1(self, xin):
        P, nc, A, ps = self.P, self.nc, self.A, self.ps
        ones = self.ones
        NCOL = 3 * D + 16
        with Stage(self, 'a1') as st:
            winr = st.T('winr', [128, 8, 3 * D], F32R)
            wstg = [st.T('wstg0', [128, 8, 512])] * 2
            wf = st.T('wf', [128, 8, 16])
            qkb = st.T('qkb', [128, 16])
            vbr = st.T('vbr', [1, D])
            vb_bc = st.T('vb_bc', [128, D])
            fb = st.T('fb', [16, 1])
            xts = [st.T(f'xt{i}', [128, D]) for i in range(2)]
            ht = st.T('ht', [128, D])
            hT = st.T('hT', [128, 8, 512], F32R)
            qko = [st.T(f'qko{i}', [128, 512]) for i in range(2)]
            vo = [st.T(f'vo{i}', [128, D]) for i in range(2)]
            Fcb = [st.T(f'Fcb{i}', [16, 512]) for i in range(2)]
            Frb = st.T('Frb', [16, 512], F32R)
            Flb = st.T('Flb', [16, 512])
            nFr = st.T('nFr', [16, 512])
            nFl = st.T('nFl', [16, 512])
            spt = st.T('spt', [16, 512])
            o16 = st.T('o16', [16, 512])
            for q in range(6):
                wb = wstg[0]
                rw = 'wstg0'
                P.op('sp', lambda e, q=q, wb=wb: e.dma_start(out=wb[:], in_=A['attn_in_w'][:, q * 512:(q + 1) * 512].rearrange("(k p) n -> p k n", p=128)),
                     writes=[rw], dma=True)
                eng = ('dve', 'pool')[q % 2]
                P.op(eng, lambda e, q=q, wb=wb: e.tensor_copy(out=winr[:, :, q * 512:(q + 1) * 512], in_=wb[:]), reads=[rw], writes=[('win', q)])
            P.op('sp', lambda e: e.dma_start(out=wf[:], in_=A['attn_in_w'][:, 3 * D:NCOL].rearrange("(k p) n -> p k n", p=128)),
                 writes=['wf'], dma=True)
            P.op('sp', lambda e: e.dma_start(out=qkb[:], in_=A['attn_qkb_l']), writes=['qkb'], dma=True)
            P.op('sp', lambda e: e.dma_start(out=vbr[:], in_=A['attn_vb']), writes=['vbr'], dma=True)
            P.op('sp', lambda e: e.dma_start(out=fb[:], in_=A['attn_fb']), writes=['fb'], dma=True)
            P.op('dve', lambda e: e.tensor_scalar(out=qkb[:, 0:8], in0=qkb[:, 0:8], scalar1=0.125, scalar2=None, op0=ALU.mult), reads=['qkb'], writes=['qkb'])
            P.op('dve', lambda e: e.tensor_scalar(out=fb[:], in0=fb[:], scalar1=-1.0, scalar2=None, op0=ALU.mult), reads=['fb'], writes=['fb'])
            P.op('pool', lambda e: e.memset(o16[:], 1.0), writes=['o16'])
            for half in range(2):
                P.op('pe', lambda e, half=half: e.matmul(out=ps[4 + half][:], lhsT=ones[0:1, :], rhs=vbr[0:1, half * 512:(half + 1) * 512], start=True, stop=True),
                     reads=['ones', 'vbr'], writes=[f'ps{4 + half}'])
                P.op('act', lambda e, half=half: e.copy(out=vb_bc[:, half * 512:(half + 1) * 512], in_=ps[4 + half][:]), reads=[f'ps{4 + half}'], writes=['vb_bc'])
            ti = 0
            for jb in range(8):
                cols = slice(jb * 512, (jb + 1) * 512)
                for tl in range(4):
                    xt = xts[ti % 2]
                    rx = f'xt{ti % 2}'
                    P.op('sp', lambda e, xt=xt, ti=ti: e.dma_start(out=xt[:], in_=xin[ti * 128:(ti + 1) * 128, :]), writes=[rx], dma=True)
                    self.modulate(xt[:], ht[:], rx, 'ht')
                    self.transpose8(ht, 'ht', hT, 'hT', tl * 128, 0)
                    ti += 1
                for c in range(16):
                    bank = ps[2 + c % 2]
                    rb = f'ps{2 + c % 2}'
                    for k in range(8):
                        P.op('pe', lambda e, k=k, c=c, bank=bank: e.matmul(out=bank[:], lhsT=winr[:, k, c * 128:(c + 1) * 128], rhs=hT[:, k, :],
                                                                          start=(k == 0), stop=(k == 7)),
                             reads=[('win', c // 4), 'hT'], writes=[rb])
                    ob = qko[c % 2]
                    rob = f'qko{c % 2}'
                    P.op('act', lambda e, c=c, bank=bank, ob=ob: e.activation(out=ob[:], in_=bank[:], func=AF.Identity, bias=qkb[:, c:c + 1],
                                                                             scale=(0.125 if c < 8 else 1.0)),
                         reads=[rb, 'qkb'], writes=[rob])
                    dstt = A['QA'] if c < 8 else A['KA']
                    for hh in range(2):
                        head = (c % 8) * 2 + hh
                        P.op('sp', lambda e, ob=ob, hh=hh, head=head, dstt=dstt: e.dma_start(out=dstt[head, 0:64, cols], in_=ob[hh * 64:(hh + 1) * 64, :]),
                             reads=[rob], writes=[('QK', c, hh)], dma=True)
                for tl in range(4):
                    tix = jb * 4 + tl
                    vt = vo[tix % 2]
                    rv = f'vo{tix % 2}'
                    for half in range(2):
                        bank = ps[4 + half]
                        rb = f'ps{4 + half}'
                        for k in range(8):
                            P.op('pe', lambda e, k=k, bank=bank, half=half, tl=tl: e.matmul(out=bank[:], lhsT=hT[:, k, tl * 128:(tl + 1) * 128],
                                                                                        rhs=winr[:, k, 2 * D + half * 512:2 * D + (half + 1) * 512],
                                                                                        start=(k == 0), stop=(k == 7)),
                                 reads=['hT', ('win', 4 + half)], writes=[rb])
                        P.op('dve', lambda e, vt=vt, bank=bank, half=half: e.tensor_tensor(out=vt[:, half * 512:(half + 1) * 512], in0=bank[:],
                                                                                      in1=vb_bc[:, half * 512:(half + 1) * 512], op=ALU.add),
                             reads=[rb, 'vb_bc'], writes=[rv])
                    P.op('sp', lambda e, vt=vt, tix=tix: e.dma_start(out=A['V'][tix * 128:(tix + 1) * 128, :], in_=vt[:]), reads=[rv], writes=[('V', tix)], dma=True)
                for k in range(8):
                    P.op('pe', lambda e, k=k: e.matmul(out=ps[6][0:16, :], lhsT=wf[:, k, :], rhs=hT[:, k, :].bitcast(F32), start=(k == 0), stop=(k == 7)),
                         reads=['wf', 'hT'], writes=['ps6'])
                P.op('act', lambda e: e.activation(out=spt[:], in_=ps[6][0:16, :], func=AF.Exp, bias=fb[:, 0:1], scale=-1.0), reads=['ps6', 'fb'], writes=['spt'])
                P.op('act', lambda e: e.activation(out=spt[:], in_=spt[:], func=AF.Ln, bias=1.0, scale=1.0), reads=['spt'], writes=['spt'])
                P.op('dve', lambda e: e.tensor_scalar(out=spt[:], in0=spt[:], scalar1=-1.0, scalar2=None, op0=ALU.mult), reads=['spt'], writes=['spt'])
                Fc = Fcb[jb % 2]
                rF = f'Fcb{jb % 2}'
                init = 0.0 if jb == 0 else Fcb[(jb - 1) % 2][:, 511:512]
                P.op('dve', lambda e, init=init, Fc=Fc: e.tensor_tensor_scan(out=Fc[:], data0=o16[:], data1=spt[:], initial=init,
                                                                             op0=ALU.mult, op1=ALU.add),
                     reads=['o16', 'spt', f'Fcb{(jb - 1) % 2}'], writes=[rF])
                P.op('dve', lambda e, Fc=Fc: e.tensor_copy(out=Frb[:], in_=Fc[:]), reads=[rF], writes=['Frb'])
                P.op('dve', lambda e, Fc=Fc: e.tensor_tensor(out=Flb[:], in0=Fc[:], in1=Frb[:].bitcast(F32), op=ALU.subtract), reads=[rF, 'Frb'], writes=['Flb'])
                P.op('dve', lambda e: e.tensor_scalar(out=nFr[:], in0=Frb[:].bitcast(F32), scalar1=-1.0, scalar2=None, op0=ALU.mult), reads=['Frb'], writes=['nFr'])
                P.op('dve', lambda e: e.tensor_scalar(out=nFl[:], in0=Flb[:], scalar1=-1.0, scalar2=None, op0=ALU.mult), reads=['Flb'], writes=['nFl'])
                P.op('sp', lambda e: e.dma_start(out=A['QA'][:, 64, cols], in_=Frb[:].bitcast(F32)), reads=['Frb'], writes=[('QAf', jb)], dma=True)
                P.op('sp', lambda e: e.dma_start(out=A['QA'][:, 65, cols], in_=Flb[:]), reads=['Flb'], writes=[('QAl', jb)], dma=True)
                P.op('sp', lambda e: e.dma_start(out=A['KA'][:, 66, cols], in_=nFr[:]), reads=['nFr'], writes=[('KAf', jb)], dma=True)
                P.op('sp', lambda e: e.dma_start(out=A['KA'][:, 67, cols], in_=nFl[:]), reads=['nFl'], writes=[('KAl', jb)], dma=True)
                for r in (66, 67):
                    P.op('sp', lambda e, r=r: e.dma_start(out=A['QA'][:, r, cols], in_=o16[:]), reads=['o16'], writes=[('QAo', r, jb)], dma=True)
                for r in (64, 65):
                    P.op('sp', lambda e, r=r: e.dma_start(out=A['KA'][:, r, cols], in_=o16[:]), reads=['o16'], writes=[('KAo', r, jb)], dma=True)

    def emit_attn2(self):
        P, nc, A, ps = self.P, self.nc, self.A, self.ps
        NR = 68
        with Stage(self, 'a2') as st:
            qst = st.T('qst', [NR, S])
            kst = st.T('kst', [NR, S])
            vst = st.T('vst', [128, 32, 64])
            QAh = [st.T(f'QAh{i}', [NR, S], F32R) for i in range(2)]
            KAh = [st.T(f'KAh{i}', [NR, S], F32R) for i in range(2)]
            Vh = [st.T(f'Vh{i}', [128, 32, 66], F32R) for i in range(2)]
            Oh = [st.T(f'Oh{i}', [128, 32, 64]) for i in range(2)]
            pt = [st.T(f'pt{i}', [128, 512], F32R) for i in range(3)]
            lm = [st.T(f'lm{i}', [128, 512]) for i in range(2)]
            mask = st.T('mask', [128, 4, 512])
            rec = st.T('rec', [128, 4])
            P.op('pool', lambda e: e.memset(mask[:], 0.0), writes=['mask'])
            for i4 in range(4):
                P.op('pool', lambda e, i4=i4: e.affine_select(out=mask[:, i4, :], in_=mask[:, i4, :], pattern=[[1, 512]], compare_op=ALU.is_ge,
                                                              fill=NEG, base=-128 * i4, channel_multiplier=-1), reads=['mask'], writes=['mask'])
            for i in range(2):
                P.op('pool', lambda e, i=i: e.tensor_copy(out=Vh[i][:, :, 64:66], in_=self.ones[:, 0:64].rearrange("p (a b) -> p a b", b=2)),
                     reads=['ones'], writes=[f'Vh{i}'])
            npt = 0
            nlm = 0
            nS = 0

            def loads(h):
                b = h % 2
                qa, ka, vh = QAh[b], KAh[b], Vh[b]
                rq, rk, rv = f'QAh{b}', f'KAh{b}', f'Vh{b}'
                for q4 in range(4):
                    cs = slice(q4 * 1024, (q4 + 1) * 1024)
                    P.op('sp', lambda e, h=h, cs=cs: e.dma_start(out=qst[:, cs], in_=A['QA'][h, :, cs]), writes=[('qst', q4)], dma=True)
                    P.op('sp', lambda e, h=h, cs=cs: e.dma_start(out=kst[:, cs], in_=A['KA'][h, :, cs]), writes=[('kst', q4)], dma=True)
                    P.op('sp', lambda e, h=h, q4=q4: e.dma_start(
                        out=vst[:, q4 * 8:(q4 + 1) * 8, :],
                        in_=A['V'][q4 * 1024:(q4 + 1) * 1024, h * 64:(h + 1) * 64].rearrange("(i p) d -> p i d", p=128)),
                        writes=[('vst', q4)], dma=True)
                for q4 in range(4):
                    cs = slice(q4 * 1024, (q4 + 1) * 1024)
                    P.op('pool', lambda e, qa=qa, cs=cs: e.tensor_copy(out=qa[:, cs], in_=qst[:, cs]), reads=[('qst', q4)], writes=[rq])
                    P.op('pool', lambda e, ka=ka, cs=cs: e.tensor_copy(out=ka[:, cs], in_=kst[:, cs]), reads=[('kst', q4)], writes=[rk])
                    P.op('pool', lambda e, vh=vh, q4=q4: e.tensor_copy(out=vh[:, q4 * 8:(q4 + 1) * 8, 0:64], in_=vst[:, q4 * 8:(q4 + 1) * 8, :]),
                         reads=[('vst', q4)], writes=[rv])

            loads(0)
            NH = self.cfg.get('nheads', 16)
            for h in range(NH):
                b = h % 2
                qa, ka, vh, oh = QAh[b], KAh[b], Vh[b], Oh[b]
                rq, rk, rv, ro = f'QAh{b}', f'KAh{b}', f'Vh{b}', f'Oh{b}'
                if h + 1 < NH:
                    loads(h + 1)
                for j in range(8):
                    po = ps[4 + j % 2]
                    rpo = f'ps{4 + j % 2}'
                    for i in range(4 * j + 4):
                        sb = ps[nS % 3]
                        rsb = f'ps{nS % 3}'
                        nS += 1
                        P.op('pe', lambda e, sb=sb, ka=ka, qa=qa, i=i, j=j: e.matmul(out=sb[:], lhsT=ka[:, i * 128:(i + 1) * 128], rhs=qa[:, j * 512:(j + 1) * 512],
                                                                                 start=True, stop=True),
                             reads=[rk, rq], writes=[rsb])
                        p_ = pt[npt % 3]
                        rp = f'pt{npt % 3}'
                        npt += 1
                        if i >= 4 * j:
                            l_ = lm[nlm % 2]
                            rl = f'lm{nlm % 2}'
                            nlm += 1
                            P.op('dve', lambda e, l_=l_, sb=sb, i=i, j=j: e.tensor_tensor(out=l_[:], in0=sb[:], in1=mask[:, i - 4 * j, :], op=ALU.add),
                                 reads=[rsb, 'mask'], writes=[rl])
                            P.op('act', lambda e, p_=p_, l_=l_: e.activation(out=p_[:], in_=l_[:], func=AF.Exp), reads=[rl], writes=[rp])
                        else:
                            P.op('act', lambda e, p_=p_, sb=sb: e.activation(out=p_[:], in_=sb[:], func=AF.Exp), reads=[rsb], writes=[rp])
                        for c in range(4):
                            if i <= 4 * j + c:
                                P.op('pe', lambda e, p_=p_, c=c, i=i, j=j, po=po, vh=vh: e.matmul(out=po[:, c * 66:(c + 1) * 66], lhsT=p_[:, c * 128:(c + 1) * 128],
                                                                                              rhs=vh[:, i, :], start=(i == 0 and c == 0), stop=(i == 4 * j + c)),
                                     reads=[rp, rv], writes=[rpo])
                    pov = po[:, 0:264].rearrange("p (c d) -> p c d", c=4)
                    P.op('dve', lambda e, pov=pov: e.reciprocal(out=rec[:], in_=pov[:, :, 64]), reads=[rpo], writes=['rec'])
                    P.op('dve', lambda e, pov=pov, oh=oh, j=j: e.tensor_tensor(out=oh[:, 4 * j:4 * j + 4, :], in0=pov[:, :, 0:64],
                                                                               in1=rec[:].unsqueeze(2).to_broadcast([128, 4, 64]), op=ALU.mult),
                         reads=[rpo, 'rec'], writes=[ro])
                for q4 in range(4):
                    P.op('sp', lambda e, oh=oh, h=h, q4=q4: e.dma_start(
                        out=A['AO'][q4 * 1024:(q4 + 1) * 1024, h * 64:(h + 1) * 64].rearrange("(i p) d -> p i d", p=128),
                        in_=oh[:, q4 * 8:(q4 + 1) * 8, :]), reads=[ro], writes=[('AO', h, q4)], dma=True)


def make_in_maps(inputs, cores=range(8)):
    f = lambda a: np.ascontiguousarray(np.asarray(a, dtype=np.float32))
    sh = {}
    sh['ada_mix_w'] = f(inputs['ada_mix_w'])
    sh['ada_ffn_w'] = f(inputs['ada_ffn_w'])
    amb, afb = f(inputs['ada_mix_b']), f(inputs['ada_ffn_b'])
    sh['ada_b'] = f(np.stack([amb[0], afb[0], amb[1], afb[1]]))
    g1, g2 = f(inputs['ln_mix_g']), f(inputs['ln_ffn_g'])
    b1, b2 = f(inputs['ln_mix_b']), f(inputs['ln_ffn_b'])
    sh['ln_g'] = f(np.stack([g1[0], g2[0], g1[1], g2[1]]))
    sh['ln_b'] = f(np.stack([b1[0], b2[0], b1[1], b2[1]]))
    sh['conv_in_w'] = f(inputs['conv_in_w'][0])
    sh['conv_in_b_l'] = f(np.asarray(inputs['conv_in_b'][0]).reshape(16, 128).T)
    sh['conv_dw_w_l'] = f(np.asarray(inputs['conv_dw_w'][0]).reshape(31, 8, 128).transpose(2, 1, 0))
    sh['conv_vec_l'] = f(np.stack([np.asarray(inputs[k][0]).reshape(8, 128).T for k in ('conv_dw_b', 'conv_ln_g', 'conv_ln_b')], axis=1))
    sh['conv_out_w'] = f(inputs['conv_out_w'][0])
    sh['conv_out_b'] = f(np.asarray(inputs['conv_out_b'][0]).reshape(1, D))
    sh['attn_in_w'] = f(inputs['attn_in_w'][0])
    ab = np.asarray(inputs['attn_in_b'][0])
    sh['attn_qkb_l'] = f(ab[:2 * D].reshape(16, 128).T)
    sh['attn_vb'] = f(ab[2 * D:3 * D].reshape(1, D))
    sh['attn_fb'] = f(ab[3 * D:].reshape(16, 1))
    sh['attn_out_w'] = f(inputs['attn_out_w'][0])
    sh['attn_out_b'] = f(np.asarray(inputs['attn_out_b'][0]).reshape(1, D))
    sh['peer_query_w'] = f(inputs['peer_query_w'])
    k1, k2 = np.asarray(inputs['peer_sub_keys_1']), np.asarray(inputs['peer_sub_keys_2'])
    sh['peer_skT'] = f(np.stack([np.stack([k1[l].T, k2[l].T]) for l in range(2)]))
    sh['peer_u'] = f(inputs['peer_expert_u'])
    sh['peer_v'] = f(inputs['peer_expert_v'])
    x = np.asarray(inputs['x'])
    c = np.asarray(inputs['c'])
    maps = []
    for b in cores:
        m = dict(sh)
        m['x'] = f(x[b])
        m['c_l'] = f(c[b].reshape(8, 128).T)
        maps.append(m)
    return maps


_NC_CACHE = {}


def kernel(**inputs):
    if 'full' not in _NC_CACHE:
        _NC_CACHE['full'] = Kern({}).build()
    nc = _NC_CACHE['full']
    maps = make_in_maps(inputs)
    res = run_bass_kernel_spmd(nc, maps, core_ids=list(range(8)))
    return np.stack([np.asarray(r['out'], dtype=np.float32) for r in res.results], axis=0)
```

```python
import numpy as np
from contextlib import ExitStack
import concourse.bass as bass
import concourse.mybir as mybir
from concourse.bass_utils import run_bass_kernel_spmd

F32 = mybir.dt.float32
I32 = mybir.dt.int32
U32 = mybir.dt.uint32
F32R = mybir.dt.float32r
ALU = mybir.AluOpType
AF = mybir.ActivationFunctionType
AX = mybir.AxisListType

S = 4096
D = 1024
NT = S // 128
ALPHA = float((2 * 2) ** 0.25)
EPS = 1e-5
NEXP = 16384
MAXV = 30000
NEG = -30000.0


class Prog:
    def __init__(self, nc, es):
        self.nc = nc
        self.es = es
        self.eng = {'pe': nc.tensor, 'dve': nc.vector, 'act': nc.scalar,
                    'pool': nc.gpsimd, 'sp': nc.sync}
        self.seq = {e: 0 for e in self.eng}
        self.csem = {e: [] for e in self.eng}
        self.known = {e: {} for e in self.eng}
        self.snap = {}
        self.last_w = {}
        self.readers = {}
        self.semobj = {}
        self.dma_pool = {}
        self.nsem = 0
        self.nwaits = 0
        self.nops = 0
        for q, n in (('sp', 24), ('pool', 24), ('act', 8)):
            self.dma_pool[q] = {'sems': [self._newsem(f"d{q}{i}") for i in range(n)],
                                'cnt': [0] * n, 'next': 0}

    def _newsem(self, name):
        s = self.es.enter_context(self.nc.semaphore(name))
        self.semobj[name] = s
        self.nsem += 1
        return name

    def _need(self, e, tok, skip_self):
        if tok is None:
            return
        name, val, owner = tok
        if skip_self and owner == e:
            return
        if self.known[e].get(name, 0) >= val:
            return
        self.eng[e].wait_ge(self.semobj[name], val)
        self.nwaits += 1
        k = self.known[e]
        k[name] = val
        sn = self.snap.get((name, val))
        if sn:
            for n2, v2 in sn.items():
                if k.get(n2, 0) < v2:
                    k[n2] = v2

    def op(self, e, fn, reads=(), writes=(), dma=False, skip_self=None):
        if skip_self is None:
            skip_self = (e == 'pe')
        if dma:
            skip_self = False
        for r in reads:
            self._need(e, self.last_w.get(r), skip_self)
        for w in writes:
            self._need(e, self.last_w.get(w), skip_self)
            for t in self.readers.get(w, ()):
                self._need(e, t, skip_self)
        self.nops += 1
        if dma:
            pool = self.dma_pool[e]
            i = pool['next']
            pool['next'] = (i + 1) % len(pool['sems'])
            name = pool['sems'][i]
            if pool['cnt'][i] + 16 > MAXV:
                name = self._newsem(f"{name}r{self.nsem}")
                pool['sems'][i] = name
                pool['cnt'][i] = 0
            prev = pool['cnt'][i]
            if prev > 0:
                self._need(e, (name, prev, e + '_dma'), False)
            ins = fn(self.eng[e])
            pool['cnt'][i] = prev + 16
            ins.then_inc(self.semobj[name], 16)
            tok = (name, prev + 16, e + '_dma')
        else:
            n = self.seq[e]
            ep = n // MAXV
            while len(self.csem[e]) <= ep:
                self.csem[e].append(self._newsem(f"c{e}{len(self.csem[e])}"))
            name = self.csem[e][ep]
            ins = fn(self.eng[e])
            ins.then_inc(self.semobj[name], 1)
            self.seq[e] = n + 1
            tok = (name, n - ep * MAXV + 1, e)
        self.snap[(tok[0], tok[1])] = dict(self.known[e])
        for r in reads:
            self.readers.setdefault(r, []).append(tok)
        for w in writes:
            self.last_w[w] = tok
            self.readers[w] = []
        return tok

    def fence(self, e):
        n = self.seq[e]
        if n > 0:
            ep = (n - 1) // MAXV
            self._need(e, (self.csem[e][ep], n - ep * MAXV, e), False)

    def barrier(self):
        toks = []
        for e in self.eng:
            n = self.seq[e]
            if n > 0:
                ep = (n - 1) // MAXV
                toks.append((self.csem[e][ep], n - ep * MAXV, e))
        for q, pool in self.dma_pool.items():
            for name, c in zip(pool['sems'], pool['cnt']):
                if c > 0:
                    toks.append((name, c, q + '_dma'))
        for e in self.eng:
            for t in toks:
                self._need(e, t, False)
        self.last_w.clear()
        self.readers.clear()
        self.snap.clear()


class Stage:
    _n = 0

    def __init__(self, K, name):
        self.K = K
        Stage._n += 1
        self.name = f"{name}{Stage._n}"

    def __enter__(self):
        self.es = ExitStack()
        self.es.__enter__()
        return self

    def T(self, name, shape, dt=F32):
        return self.es.enter_context(self.K.nc.sbuf_tensor(f"{self.name}_{name}", shape, dt))

    def __exit__(self, *a):
        self.K.P.barrier()
        return self.es.__exit__(*a)


class Kern:
    def __init__(self, cfg):
        self.cfg = cfg

    def build(self):
        nc = bass.Bass("TRN2", target_bir_lowering=False)
        self.nc = nc
        dbg = self.cfg.get('debug', False)

        def din(name, shape, dt=F32):
            return nc.dram_tensor(name, list(shape), dt, kind="ExternalInput").ap()

        def dscr(name, shape, dt=F32):
            kind = "ExternalOutput" if (dbg and name in self.cfg.get('expose', ())) else "Internal"
            return nc.dram_tensor(name, list(shape), dt, kind=kind).ap()

        A = {}
        A['x'] = din('x', [S, D])
        A['c_l'] = din('c_l', [128, 8])
        A['ada_mix_w'] = din('ada_mix_w', [2, D, 3 * D])
        A['ada_ffn_w'] = din('ada_ffn_w', [2, D, 3 * D])
        A['ada_b'] = din('ada_b', [4, 3 * D])
        A['ln_g'] = din('ln_g', [4, D])
        A['ln_b'] = din('ln_b', [4, D])
        A['conv_in_w'] = din('conv_in_w', [D, 2 * D])
        A['conv_in_b_l'] = din('conv_in_b_l', [128, 16])
        A['conv_dw_w_l'] = din('conv_dw_w_l', [128, 8, 31])
        A['conv_vec_l'] = din('conv_vec_l', [128, 3, 8])
        A['conv_out_w'] = din('conv_out_w', [D, D])
        A['conv_out_b'] = din('conv_out_b', [1, D])
        A['attn_in_w'] = din('attn_in_w', [D, 3 * D + 16])
        A['attn_qkb_l'] = din('attn_qkb_l', [128, 16])
        A['attn_vb'] = din('attn_vb', [1, D])
        A['attn_fb'] = din('attn_fb', [16, 1])
        A['attn_out_w'] = din('attn_out_w', [D, D])
        A['attn_out_b'] = din('attn_out_b', [1, D])
        A['peer_query_w'] = din('peer_query_w', [2, D, 2 * D])
        A['peer_skT'] = din('peer_skT', [2, 2, 128, 128])
        A['peer_u'] = din('peer_u', [2, NEXP, D])
        A['peer_v'] = din('peer_v', [2, NEXP, D])
        A['out'] = nc.dram_tensor('out', [S, D], F32, kind="ExternalOutput").ap()
        A['X1'] = dscr('X1', [S, D])
        A['X2'] = dscr('X2', [S, D])
        A['X3'] = dscr('X3', [S, D])
        A['ST'] = dscr('ST', [8, 128, S])
        A['IDX'] = dscr('IDX', [S, 128], I32)
        A['SCR'] = dscr('SCR', [S, 2048])
        A['H'] = dscr('H', [S, D])
        A['GATE'] = dscr('GATE', [S, 128])
        A['QA'] = dscr('QA', [16, 68, S])
        A['KA'] = dscr('KA', [16, 68, S])
        A['V'] = dscr('V', [S, D])
        A['AOT'] = dscr('AOT', [8, 128, S])
        self.A = A

        with ExitStack() as es:
            self.P = P = Prog(nc, es)
            G = lambda name, shape, dt=F32: es.enter_context(nc.sbuf_tensor(name, shape, dt))
            self.ps = [es.enter_context(nc.psum_tensor(f"ps{i}", [128, 512], F32)) for i in range(8)]
            self.ident = G('ident', [128, 128])
            self.ones = G('ones', [128, 128])
            self.SC = G('SC', [128, 8, 128])
            self.shift_bc = G('shift_bc', [128, D])
            self.scale_bc = G('scale_bc', [128, D])
            self.gate_bc = G('gate_bc', [128, D])
            self.g_bc = G('g_bc', [128, D])
            self.b_bc = G('b_bc', [128, D])
            self.bs = G('bs', [128, 2, 6])
            self.mv = G('mv', [128, 2])
            self.rs = G('rs', [128, 1])
            self.emit_globals()
            order = self.cfg.get('stages', ['conv', 'peer0', 'attn', 'peer1'])
            cur = A['x']
            nxt = {'conv': A['X1'], 'peer0': A['X2'], 'attn': A['X3'], 'peer1': A['out']}
            for i, st in enumerate(order):
                dst = A['out'] if i == len(order) - 1 else nxt[st]
                if st == 'conv':
                    self.emit_adaln(0)
                    self.emit_conv1(cur)
                    self.emit_proj_out(cur, dst, A['conv_out_w'], A['conv_out_b'], src_fm=A['ST'])
                elif st == 'attn':
                    self.emit_adaln(2)
                    self.emit_attn1(cur)
                    self.emit_attn2()
                    self.emit_proj_out(cur, dst, A['attn_out_w'], A['attn_out_b'], src_fm=A['AOT'])
                else:
                    L = int(st[-1])
                    self.emit_adaln(1 + 2 * L)
                    if self.cfg.get('peer_fused', True):
                        self.emit_peer1a(cur, L)
                        self.emit_peer2f(cur, dst, L)
                    else:
                        self.emit_peer1(cur, L)
                        self.emit_peer2(cur, dst, L)
                cur = dst
            P.barrier()
            print(f"[kern] ops={P.nops} waits={P.nwaits} sems={P.nsem} seq={P.seq}")
        return nc

    def emit_globals(self):
        P, nc = self.P, self.nc
        ident, ones = self.ident, self.ones
        P.op('pool', lambda e: e.memset(ident[:], 1.0), writes=['ident'])
        P.op('pool', lambda e: e.affine_select(out=ident[:], in_=ident[:], pattern=[[-1, 128]],
                                               compare_op=ALU.is_equal, fill=0.0, base=0, channel_multiplier=1),
             reads=['ident'], writes=['ident'])
        P.op('pool', lambda e: e.memset(ones[:], 1.0), writes=['ones'])
        with Stage(self, 'gl') as st:
            ct = st.T('ct', [128, 8])
            P.op('sp', lambda e: e.dma_start(out=ct[:], in_=self.A['c_l']), writes=['ct'], dma=True)
            P.op('act', lambda e: e.activation(out=ct[:], in_=ct[:], func=AF.Silu), reads=['ct'], writes=['ct'])
            SC = self.SC
            P.op('dve', lambda e: e.tensor_copy(out=SC[:], in_=ct[:].unsqueeze(2).to_broadcast([128, 8, 128])),
                 reads=['ct'], writes=['SC'])

    def modulate(self, xt, ht, rx, rh):
        P = self.P
        P.op('dve', lambda e: e.tensor_tensor(out=ht, in0=xt, in1=self.scale_bc[:], op=ALU.mult),
             reads=[rx, 'scale_bc'], writes=[rh])
        P.op('pool', lambda e: e.tensor_tensor(out=ht, in0=ht, in1=self.shift_bc[:], op=ALU.add),
             reads=[rh, 'shift_bc'], writes=[rh])

    def transpose8(self, src, rsrc, dstT, rdst, col0, pb):
        P = self.P
        ps = self.ps
        for half in range(2):
            bank = ps[pb + half]
            rb = f'ps{pb + half}'
            for kk in range(4):
                k = half * 4 + kk
                P.op('pe', lambda e, k=k, kk=kk, bank=bank: e.transpose(out=bank[:, kk * 128:(kk + 1) * 128],
                                                                      in_=src[:, k * 128:(k + 1) * 128],
                                                                      identity=self.ident[:]),
                     reads=[rsrc, 'ident'], writes=[rb])
            dst = dstT[:, half * 4:half * 4 + 4, col0:col0 + 128]
            srcp = bank[:].rearrange("p (k n) -> p k n", k=4)
            if half == 0:
                P.op('act', lambda e, dst=dst, srcp=srcp: e.copy(out=dst, in_=srcp), reads=[rb], writes=[rdst])
            else:
                P.op('dve', lambda e, dst=dst, srcp=srcp: e.tensor_copy(out=dst, in_=srcp), reads=[rb], writes=[rdst])

    def layernorm_inplace(self, r, rr, gb='pool'):
        P = self.P
        bs, mv, rs = self.bs, self.mv, self.rs
        for c in range(2):
            P.op('dve', lambda e, c=c: e.bn_stats(out=bs[:, c, :], in_=r[:, c * 512:(c + 1) * 512]),
                 reads=[rr], writes=['bs'])
        P.op('dve', lambda e: e.bn_aggr(out=mv[:], in_=bs[:].rearrange("p a b -> p (a b)")), reads=['bs'], writes=['mv'])
        P.op('dve', lambda e: e.tensor_scalar(out=rs[:], in0=mv[:, 1:2], scalar1=EPS, scalar2=None, op0=ALU.add),
             reads=['mv'], writes=['rs'])
        P.op('act', lambda e: e.activation(out=rs[:], in_=rs[:], func=AF.Sqrt), reads=['rs'], writes=['rs'])
        P.op('dve', lambda e: e.reciprocal(out=rs[:], in_=rs[:]), reads=['rs'], writes=['rs'])
        P.op('dve', lambda e: e.tensor_scalar(out=r, in0=r, scalar1=mv[:, 0:1], scalar2=rs[:, 0:1],
                                              op0=ALU.subtract, op1=ALU.mult), reads=[rr, 'mv', 'rs'], writes=[rr])
        P.op(gb, lambda e: e.tensor_tensor(out=r, in0=r, in1=self.g_bc[:], op=ALU.mult), reads=[rr, 'g_bc'], writes=[rr])
        P.op(gb, lambda e: e.tensor_tensor(out=r, in0=r, in1=self.b_bc[:], op=ALU.add), reads=[rr, 'b_bc'], writes=[rr])

    def emit_adaln(self, sub):
        P, nc, A, ps = self.P, self.nc, self.A, self.ps
        L = sub // 2
        wsrc = (A['ada_mix_w'] if sub % 2 == 0 else A['ada_ffn_w'])[L]
        ones = self.ones
        with Stage(self, f'ada{sub}') as st:
            brow = st.T('brow', [1, 3 * D])
            lrow = st.T('lrow', [1, 2 * D])
            wch = [st.T(f'wch{i}', [128, 8, 512]) for i in range(2)]
            P.op('sp', lambda e: e.dma_start(out=brow[:], in_=A['ada_b'][sub:sub + 1, :]), writes=['brow'], dma=True)
            P.op('sp', lambda e: e.dma_start(out=lrow[:, 0:D], in_=A['ln_g'][sub:sub + 1, :]), writes=['lrow'], dma=True)
            P.op('sp', lambda e: e.dma_start(out=lrow[:, D:2 * D], in_=A['ln_b'][sub:sub + 1, :]), writes=['lrow'], dma=True)
            dsts = [self.shift_bc, self.shift_bc, self.scale_bc, self.scale_bc, self.gate_bc, self.gate_bc]
            names = ['shift_bc', 'shift_bc', 'scale_bc', 'scale_bc', 'gate_bc', 'gate_bc']
            for n6 in range(6):
                wb = wch[n6 % 2]
                rw = f'wch{n6 % 2}'
                P.op('sp', lambda e, wb=wb, n6=n6: e.dma_start(
                    out=wb[:], in_=wsrc[:, n6 * 512:(n6 + 1) * 512].rearrange("(k p) n -> p k n", p=128)),
                    writes=[rw], dma=True)
                bank = ps[n6 % 2]
                rb = f'ps{n6 % 2}'
                for k in range(8):
                    P.op('pe', lambda e, k=k, wb=wb, bank=bank: e.matmul(out=bank[:], lhsT=self.SC[:, k, :], rhs=wb[:, k, :],
                                                                        start=(k == 0), stop=False),
                         reads=['SC', rw], writes=[rb])
                P.op('pe', lambda e, bank=bank, n6=n6: e.matmul(out=bank[:], lhsT=ones[0:1, :], rhs=brow[0:1, n6 * 512:(n6 + 1) * 512],
                                                                start=False, stop=True),
                     reads=['ones', 'brow'], writes=[rb])
                dst = dsts[n6][:, (n6 % 2) * 512:(n6 % 2 + 1) * 512]
                if n6 in (2, 3):
                    P.op('dve', lambda e, dst=dst, bank=bank: e.tensor_scalar(out=dst, in0=bank[:], scalar1=1.0, scalar2=None, op0=ALU.add),
                         reads=[rb], writes=[names[n6]])
                else:
                    P.op('dve', lambda e, dst=dst, bank=bank: e.tensor_copy(out=dst, in_=bank[:]), reads=[rb], writes=[names[n6]])
            for j in range(4):
                bank = ps[2 + j % 2]
                rb = f'ps{2 + j % 2}'
                P.op('pe', lambda e, bank=bank, j=j: e.matmul(out=bank[:], lhsT=ones[0:1, :], rhs=lrow[0:1, j * 512:(j + 1) * 512],
                                                              start=True, stop=True), reads=['ones', 'lrow'], writes=[rb])
                dstt = self.g_bc if j < 2 else self.b_bc
                dst = dstt[:, (j % 2) * 512:(j % 2 + 1) * 512]
                P.op('act', lambda e, dst=dst, bank=bank: e.copy(out=dst, in_=bank[:]), reads=[rb],
                     writes=['g_bc' if j < 2 else 'b_bc'])

    def emit_conv1(self, xin):
        P, nc, A, ps = self.P, self.nc, self.A, self.ps
        ones = self.ones
        with Stage(self, 'c1') as st:
            win = st.T('win', [128, 8, 2048])
            cib = st.T('cib', [128, 16])
            dw = st.T('dw', [128, 8, 31])
            cv = st.T('cv', [128, 3, 8])
            xts = [st.T(f'xt{i}', [128, D]) for i in range(2)]
            ht = st.T('ht', [128, D])
            hT = st.T('hT', [128, 8, 512])
            acc = st.T('acc', [128, 8, 512])
            aext = st.T('aext', [128, 8, 542])
            sig = [st.T(f'sig{i}', [128, 512]) for i in range(2)]
            sq = [st.T(f'sq{i}', [128, 512]) for i in range(2)]
            meant = st.T('meant', [128, 512])
            rstd = st.T('rstd', [128, 512])
            tmp = st.T('tmp', [128, 512])
            for q in range(4):
                P.op('sp', lambda e, q=q: e.dma_start(out=win[:, :, q * 512:(q + 1) * 512],
                                                      in_=A['conv_in_w'][:, q * 512:(q + 1) * 512].rearrange("(k p) n -> p k n", p=128)),
                     writes=[('win', q)], dma=True)
            P.op('sp', lambda e: e.dma_start(out=cib[:], in_=A['conv_in_b_l']), writes=['cib'], dma=True)
            P.op('sp', lambda e: e.dma_start(out=dw[:], in_=A['conv_dw_w_l']), writes=['dw'], dma=True)
            P.op('sp', lambda e: e.dma_start(out=cv[:], in_=A['conv_vec_l']), writes=['cv'], dma=True)
            for cc in range(8):
                P.op('pool', lambda e, cc=cc: e.memset(aext[:, cc, 0:30], 0.0), writes=[('aext', cc)])
            ti = 0
            for jb in range(8):
                for tl in range(4):
                    xt = xts[ti % 2]
                    rx = f'xt{ti % 2}'
                    P.op('sp', lambda e, xt=xt, ti=ti: e.dma_start(out=xt[:], in_=xin[ti * 128:(ti + 1) * 128, :]),
                         writes=[rx], dma=True)
                    self.modulate(xt[:], ht[:], rx, 'ht')
                    self.transpose8(ht, 'ht', hT, 'hT', tl * 128, 0)
                    ti += 1
                for cc in range(8):
                    pa, pb = ps[2 + (cc % 2) * 2], ps[3 + (cc % 2) * 2]
                    ra, rb = f'ps{2 + (cc % 2) * 2}', f'ps{3 + (cc % 2) * 2}'
                    for k in range(8):
                        P.op('pe', lambda e, k=k, cc=cc, pa=pa: e.matmul(out=pa[:], lhsT=win[:, k, cc * 128:(cc + 1) * 128], rhs=hT[:, k, :],
                                                                        start=(k == 0), stop=(k == 7)),
                             reads=[('win', cc // 4), 'hT'], writes=[ra])
                    for k in range(8):
                        P.op('pe', lambda e, k=k, cc=cc, pb=pb: e.matmul(out=pb[:], lhsT=win[:, k, D + cc * 128:D + (cc + 1) * 128], rhs=hT[:, k, :],
                                                                        start=(k == 0), stop=(k == 7)),
                             reads=[('win', 2 + cc // 4), 'hT'], writes=[rb])
                    sg = sig[cc % 2]
                    rsg = f'sig{cc % 2}'
                    P.op('act', lambda e, sg=sg, pb=pb, cc=cc: e.activation(out=sg[:], in_=pb[:], func=AF.Sigmoid, bias=cib[:, 8 + cc:9 + cc], scale=1.0),
                         reads=[rb, 'cib'], writes=[rsg])
                    P.op('dve', lambda e, sg=sg, pa=pa, cc=cc: e.scalar_tensor_tensor(out=aext[:, cc, 30:542], in0=pa[:], scalar=cib[:, cc:cc + 1], in1=sg[:],
                                                                                 op0=ALU.add, op1=ALU.mult),
                         reads=[ra, rsg, 'cib'], writes=[('aext', cc)])
                for cc in range(8):
                    P.op('dve', lambda e, cc=cc: e.tensor_scalar(out=acc[:, cc, :], in0=aext[:, cc, 0:512], scalar1=dw[:, cc, 0:1], scalar2=cv[:, 0, cc:cc + 1],
                                                                 op0=ALU.mult, op1=ALU.add),
                         reads=[('aext', cc), 'dw', 'cv'], writes=[('acc', cc)])
                    for w in range(1, 31):
                        P.op('dve', lambda e, cc=cc, w=w: e.scalar_tensor_tensor(out=acc[:, cc, :], in0=aext[:, cc, w:w + 512], scalar=dw[:, cc, w:w + 1],
                                                                                 in1=acc[:, cc, :], op0=ALU.mult, op1=ALU.add),
                             reads=[('aext', cc), 'dw', ('acc', cc)], writes=[('acc', cc)])
                    P.op('act', lambda e, cc=cc: e.copy(out=aext[:, cc, 0:30], in_=aext[:, cc, 512:542]),
                         reads=[('aext', cc)], writes=[('aext', cc)])
                for cc in range(8):
                    s2 = sq[cc % 2]
                    rs2 = f'sq{cc % 2}'
                    P.op('act', lambda e, cc=cc, s2=s2: e.activation(out=s2[:], in_=acc[:, cc, :], func=AF.Square),
                         reads=[('acc', cc)], writes=[rs2])
                    P.op('pe', lambda e, cc=cc: e.matmul(out=ps[6][:], lhsT=ones[:], rhs=acc[:, cc, :], start=(cc == 0), stop=(cc == 7)),
                         reads=['ones', ('acc', cc)], writes=['ps6'])
                    P.op('pe', lambda e, cc=cc, s2=s2: e.matmul(out=ps[7][:], lhsT=ones[:], rhs=s2[:], start=(cc == 0), stop=(cc == 7)),
                         reads=['ones', rs2], writes=['ps7'])
                P.op('act', lambda e: e.activation(out=meant[:], in_=ps[6][:], func=AF.Copy, scale=1.0 / D), reads=['ps6'], writes=['meant'])
                P.op('dve', lambda e: e.tensor_tensor(out=tmp[:], in0=meant[:], in1=meant[:], op=ALU.mult), reads=['meant'], writes=['tmp'])
                P.op('dve', lambda e: e.scalar_tensor_tensor(out=rstd[:], in0=ps[7][:], scalar=1.0 / D, in1=tmp[:], op0=ALU.mult, op1=ALU.subtract),
                     reads=['ps7', 'tmp'], writes=['rstd'])
                P.op('dve', lambda e: e.tensor_scalar(out=rstd[:], in0=rstd[:], scalar1=EPS, scalar2=None, op0=ALU.add), reads=['rstd'], writes=['rstd'])
                P.op('act', lambda e: e.activation(out=rstd[:], in_=rstd[:], func=AF.Sqrt), reads=['rstd'], writes=['rstd'])
                P.op('dve', lambda e: e.reciprocal(out=rstd[:], in_=rstd[:]), reads=['rstd'], writes=['rstd'])
                for cc in range(8):
                    P.op('dve', lambda e, cc=cc: e.tensor_tensor(out=acc[:, cc, :], in0=acc[:, cc, :], in1=meant[:], op=ALU.subtract),
                         reads=[('acc', cc), 'meant'], writes=[('acc', cc)])
                    P.op('pool', lambda e, cc=cc: e.tensor_tensor(out=acc[:, cc, :], in0=acc[:, cc, :], in1=rstd[:], op=ALU.mult),
                         reads=[('acc', cc), 'rstd'], writes=[('acc', cc)])
                    P.op('act', lambda e, cc=cc: e.activation(out=acc[:, cc, :], in_=acc[:, cc, :], func=AF.Silu,
                                                              bias=cv[:, 2, cc:cc + 1], scale=cv[:, 1, cc:cc + 1]),
                         reads=[('acc', cc), 'cv'], writes=[('acc', cc)])
                P.op('sp', lambda e, jb=jb: e.dma_start(out=A['ST'][:, :, jb * 512:(jb + 1) * 512].rearrange("c p t -> p c t"), in_=acc[:]),
                     reads=[('acc', cc) for cc in range(8)], writes=[('ST', jb)], dma=True)

    def emit_proj_out(self, xin, dst, w_ap, b_ap, src_fm=None, src_tm=None):
        P, nc, A, ps = self.P, self.nc, self.A, self.ps
        ones = self.ones
        with Stage(self, 'po') as st:
            wo = st.T('wo', [128, 8, D])
            bo = st.T('bo', [1, D])
            xts = [st.T(f'xt{i}', [128, D]) for i in range(2)]
            rts = [st.T(f'rt{i}', [128, D]) for i in range(2)]
            if src_fm is not None:
                sT = [st.T(f'sT{i}', [128, 8, 512]) for i in range(2)]
            else:
                ao = [st.T(f'ao{i}', [128, D]) for i in range(2)]
                aT = [st.T(f'aT{i}', [128, 8, 128]) for i in range(2)]
            for q in range(2):
                P.op('sp', lambda e, q=q: e.dma_start(out=wo[:, :, q * 512:(q + 1) * 512],
                                                      in_=w_ap[:, q * 512:(q + 1) * 512].rearrange("(k p) n -> p k n", p=128)),
                     writes=[('wo', q)], dma=True)
            P.op('sp', lambda e: e.dma_start(out=bo[:], in_=b_ap), writes=['bo'], dma=True)
            for ti in range(NT):
                xt = xts[ti % 2]
                rx = f'xt{ti % 2}'
                rt = rts[ti % 2]
                rr = f'rt{ti % 2}'
                P.op('sp', lambda e, xt=xt, ti=ti: e.dma_start(out=xt[:], in_=xin[ti * 128:(ti + 1) * 128, :]), writes=[rx], dma=True)
                if src_fm is not None:
                    jb, tl = ti // 4, ti % 4
                    sb = sT[jb % 2]
                    rsb = f'sT{jb % 2}'
                    if tl == 0:
                        P.op('sp', lambda e, sb=sb, jb=jb: e.dma_start(out=sb[:], in_=src_fm[:, :, jb * 512:(jb + 1) * 512].rearrange("c p t -> p c t")),
                             reads=[('ST', jb)], writes=[rsb], dma=True)
                    lhs = lambda k, sb=sb, tl=tl: sb[:, k, tl * 128:(tl + 1) * 128]
                    rl = rsb
                else:
                    a = ao[ti % 2]
                    ra = f'ao{ti % 2}'
                    at = aT[ti % 2]
                    rat = f'aT{ti % 2}'
                    P.op('sp', lambda e, a=a, ti=ti: e.dma_start(out=a[:], in_=src_tm[ti * 128:(ti + 1) * 128, :]), writes=[ra], dma=True)
                    self.transpose8(a, ra, at, rat, 0, 4)
                    lhs = lambda k, at=at: at[:, k, :]
                    rl = rat
                pb = (ti % 2) * 2
                for half in range(2):
                    bank = ps[pb + half]
                    rb = f'ps{pb + half}'
                    for k in range(8):
                        P.op('pe', lambda e, k=k, bank=bank, half=half, lhs=lhs: e.matmul(out=bank[:], lhsT=lhs(k), rhs=wo[:, k, half * 512:(half + 1) * 512],
                                                                                     start=(k == 0), stop=False),
                             reads=[rl, ('wo', half)], writes=[rb])
                    P.op('pe', lambda e, bank=bank, half=half: e.matmul(out=bank[:], lhsT=ones[0:1, :], rhs=bo[0:1, half * 512:(half + 1) * 512],
                                                                        start=False, stop=True), reads=['ones', 'bo'], writes=[rb])
                    P.op('dve', lambda e, bank=bank, half=half, rt=rt: e.tensor_tensor(out=rt[:, half * 512:(half + 1) * 512], in0=bank[:],
                                                                                  in1=self.gate_bc[:, half * 512:(half + 1) * 512], op=ALU.mult),
                         reads=[rb, 'gate_bc'], writes=[rr])
                P.op('dve', lambda e, rt=rt, xt=xt: e.scalar_tensor_tensor(out=rt[:], in0=xt[:], scalar=ALPHA, in1=rt[:], op0=ALU.mult, op1=ALU.add),
                     reads=[rx, rr], writes=[rr])
                self.layernorm_inplace(rt[:], rr)
                P.op('sp', lambda e, rt=rt, ti=ti: e.dma_start(out=dst[ti * 128:(ti + 1) * 128, :], in_=rt[:]), reads=[rr], dma=True)

    def emit_peer1(self, xin, L):
        P, nc, A, ps = self.P, self.nc, self.A, self.ps
        with Stage(self, f'p1{L}') as st:
            wq = st.T('wq', [128, 8, 2048])
            skT = st.T('skT', [128, 2, 128])
            xts = [st.T(f'xt{i}', [128, D]) for i in range(2)]
            ht = st.T('ht', [128, D])
            hT = st.T('hT', [128, 8, 256])
            qT = st.T('qT', [128, 16, 256])
            sc = st.T('sc', [128, 16, 128])
            m = st.T('m', [128, 16, 16])
            ix = st.T('ix', [128, 16, 16], U32)
            ixf = st.T('ixf', [128, 16, 16])
            wk = st.T('wk', [128, 16, 128])
            cand = st.T('cand', [128, 8, 256])
            candi = st.T('candi', [128, 8, 256])
            wk2 = st.T('wk2', [128, 8, 256])
            junk = [st.T(f'junk{i}', [128, 256]) for i in range(2)]
            ts = st.T('ts', [128, 8, 16])
            ef = st.T('ef', [128, 128])
            ei = [st.T(f'ei{i}', [128, 128], I32) for i in range(2)]
            gt = [st.T(f'gt{i}', [128, 8, 16]) for i in range(2)]
            gsum = st.T('gsum', [128, 8])
            for q in range(4):
                P.op('sp', lambda e, q=q: e.dma_start(out=wq[:, :, q * 512:(q + 1) * 512],
                                                      in_=A['peer_query_w'][L][:, q * 512:(q + 1) * 512].rearrange("(k p) n -> p k n", p=128)),
                     writes=[('wq', q)], dma=True)
            P.op('sp', lambda e: e.dma_start(out=skT[:], in_=A['peer_skT'][L].rearrange("h d k -> d h k")), writes=['skT'], dma=True)
            ti = 0
            for jb in range(S // 256):
                for tl in range(2):
                    xt = xts[ti % 2]
                    rx = f'xt{ti % 2}'
                    P.op('sp', lambda e, xt=xt, ti=ti: e.dma_start(out=xt[:], in_=xin[ti * 128:(ti + 1) * 128, :]), writes=[rx], dma=True)
                    self.modulate(xt[:], ht[:], rx, 'ht')
                    self.transpose8(ht, 'ht', hT, 'hT', tl * 128, 0)
                    ti += 1
                for c in range(16):
                    bank = ps[2 + c % 2]
                    rb = f'ps{2 + c % 2}'
                    for k in range(8):
                        P.op('pe', lambda e, k=k, c=c, bank=bank: e.matmul(out=bank[:, 0:256], lhsT=wq[:, k, c * 128:(c + 1) * 128], rhs=hT[:, k, :],
                                                                          start=(k == 0), stop=(k == 7)),
                             reads=[('wq', c // 4), 'hT'], writes=[rb])
                    if c % 2 == 0:
                        P.op('act', lambda e, c=c, bank=bank: e.copy(out=qT[:, c, :], in_=bank[:, 0:256]), reads=[rb], writes=[('qT', c)])
                    else:
                        P.op('dve', lambda e, c=c, bank=bank: e.tensor_copy(out=qT[:, c, :], in_=bank[:, 0:256]), reads=[rb], writes=[('qT', c)])
                for tl in range(2):
                    tix = jb * 2 + tl
                    for c in range(16):
                        bank = ps[4 + c // 4]
                        rb = f'ps{4 + c // 4}'
                        P.op('pe', lambda e, c=c, bank=bank, tl=tl: e.matmul(out=bank[:, (c % 4) * 128:(c % 4 + 1) * 128],
                                                                            lhsT=qT[:, c, tl * 128:(tl + 1) * 128], rhs=skT[:, c % 2, :],
                                                                            start=True, stop=True),
                             reads=[('qT', c), 'skT'], writes=[rb])
                    for g4 in range(4):
                        P.op('act', lambda e, g4=g4: e.copy(out=sc[:, g4 * 4:(g4 + 1) * 4, :], in_=ps[4 + g4][:].rearrange("p (a k) -> p a k", a=4)),
                             reads=[f'ps{4 + g4}'], writes=['sc'])
                    for c in range(16):
                        P.op('dve', lambda e, c=c: e.max(out=m[:, c, 0:8], in_=sc[:, c, :]), reads=['sc'], writes=[('m0', c)])
                    P.fence('dve')
                    for c in range(16):
                        P.op('dve', lambda e, c=c: e.max_index(out=ix[:, c, 0:8], in_max=m[:, c, 0:8], in_values=sc[:, c, :]),
                             reads=['sc', ('m0', c)], writes=[('ix0', c)])
                        P.op('dve', lambda e, c=c: e.match_replace(out=wk[:, c, :], in_to_replace=m[:, c, 0:8], in_values=sc[:, c, :], imm_value=-1e30),
                             reads=['sc', ('m0', c)], writes=[('wk', c)])
                    P.fence('dve')
                    for c in range(16):
                        P.op('dve', lambda e, c=c: e.max(out=m[:, c, 8:16], in_=wk[:, c, :]), reads=[('wk', c)], writes=[('m1', c)])
                    P.fence('dve')
                    for c in range(16):
                        P.op('dve', lambda e, c=c: e.max_index(out=ix[:, c, 8:16], in_max=m[:, c, 8:16], in_values=wk[:, c, :]),
                             reads=[('wk', c), ('m1', c)], writes=[('ix1', c)])
                    P.fence('dve')
                    mres = [('m0', c) for c in range(16)] + [('m1', c) for c in range(16)]
                    ixres = [('ix0', c) for c in range(16)] + [('ix1', c) for c in range(16)]
                    P.op('dve', lambda e: e.tensor_copy(out=ixf[:], in_=ix[:]), reads=ixres, writes=['ixf'])
                    m4 = m[:].rearrange("p (h two) k -> p h two k", two=2)
                    i4 = ixf[:].rearrange("p (h two) k -> p h two k", two=2)
                    c4 = cand[:].rearrange("p h (a b) -> p h a b", a=16)
                    ci4 = candi[:].rearrange("p h (a b) -> p h a b", a=16)
                    P.op('dve', lambda e: e.tensor_tensor(out=c4, in0=m4[:, :, 0, :].unsqueeze(3).to_broadcast([128, 8, 16, 16]),
                                                          in1=m4[:, :, 1, :].unsqueeze(2).to_broadcast([128, 8, 16, 16]), op=ALU.add),
                         reads=mres, writes=['cand'])
                    P.op('dve', lambda e: e.tensor_scalar(out=i4[:, :, 0, :], in0=i4[:, :, 0, :], scalar1=128.0, scalar2=None, op0=ALU.mult),
                         reads=['ixf'], writes=['ixf'])
                    P.op('dve', lambda e: e.tensor_tensor(out=ci4, in0=i4[:, :, 0, :].unsqueeze(3).to_broadcast([128, 8, 16, 16]),
                                                          in1=i4[:, :, 1, :].unsqueeze(2).to_broadcast([128, 8, 16, 16]), op=ALU.add),
                         reads=['ixf'], writes=['candi'])
                    for h in range(8):
                        P.op('dve', lambda e, h=h: e.max(out=ts[:, h, 0:8], in_=cand[:, h, :]), reads=['cand'], writes=[('ts0', h)])
                    P.fence('dve')
                    for h in range(8):
                        P.op('dve', lambda e, h=h: e.match_replace(out=wk2[:, h, :], in_to_replace=ts[:, h, 0:8], in_values=cand[:, h, :], imm_value=-1e30),
                             reads=['cand', ('ts0', h)], writes=[('wk2', h)])
                    P.fence('dve')
                    for h in range(8):
                        P.op('dve', lambda e, h=h: e.max(out=ts[:, h, 8:16], in_=wk2[:, h, :]), reads=[('wk2', h)], writes=[('ts1', h)])
                    P.fence('dve')
                    tsres = [('ts0', h) for h in range(8)] + [('ts1', h) for h in range(8)]
                    for h in range(8):
                        for k in range(16):
                            P.op('dve', lambda e, h=h, k=k: e.scalar_tensor_tensor(out=junk[(h * 16 + k) % 2][:], in0=cand[:, h, :], scalar=ts[:, h, k:k + 1], in1=candi[:, h, :],
                                                                                   op0=ALU.is_equal, op1=ALU.mult, accum_out=ef[:, h * 16 + k:h * 16 + k + 1]),
                                 reads=['cand', 'candi', ('ts0', h), ('ts1', h)], writes=[('ef', h * 16 + k)])
                    P.fence('dve')
                    eib = ei[tix % 2]
                    rei = f'ei{tix % 2}'
                    gtb = gt[tix % 2]
                    rgt = f'gt{tix % 2}'
                    P.op('dve', lambda e: e.tensor_scalar(out=ef[:], in0=ef[:], scalar1=float(NEXP - 1), scalar2=float(L * NEXP), op0=ALU.min, op1=ALU.add),
                         reads=[('ef', q) for q in range(128)], writes=['ef'])
                    P.op('dve', lambda e, eib=eib: e.tensor_copy(out=eib[:], in_=ef[:]), reads=['ef'], writes=[rei])
                    P.op('dve', lambda e, gtb=gtb: e.tensor_tensor(out=gtb[:], in0=ts[:], in1=ts[:, :, 0:1].to_broadcast([128, 8, 16]), op=ALU.subtract),
                         reads=tsres, writes=[rgt])
                    P.op('act', lambda e, gtb=gtb: e.activation(out=gtb[:], in_=gtb[:], func=AF.Exp), reads=[rgt], writes=[rgt])
                    P.op('dve', lambda e, gtb=gtb: e.tensor_reduce(out=gsum[:], in_=gtb[:], axis=AX.X, op=ALU.add), reads=[rgt], writes=['gsum'])
                    P.op('dve', lambda e: e.reciprocal(out=gsum[:], in_=gsum[:]), reads=['gsum'], writes=['gsum'])
                    P.op('dve', lambda e, gtb=gtb: e.tensor_tensor(out=gtb[:], in0=gtb[:], in1=gsum[:].unsqueeze(2).to_broadcast([128, 8, 16]), op=ALU.mult),
                         reads=[rgt, 'gsum'], writes=[rgt])
                    P.op('sp', lambda e, eib=eib, tix=tix: e.dma_start(out=A['IDX'][tix * 128:(tix + 1) * 128, :], in_=eib[:]),
                         reads=[rei], writes=[('IDX', tix)], dma=True)
                    P.op('sp', lambda e, gtb=gtb, tix=tix: e.dma_start(out=A['GATE'][tix * 128:(tix + 1) * 128, :], in_=gtb[:].rearrange("p h k -> p (h k)")),
                         reads=[rgt], writes=[('GATE', tix)], dma=True)

    def emit_peer2(self, xin, dst, L):
        P, nc, A, ps = self.P, self.nc, self.A, self.ps
        NB = self.cfg.get('nb', 14)
        U = A['peer_u'].rearrange("l e d -> (l e) d")
        V = A['peer_v'].rearrange("l e d -> (l e) d")
        with Stage(self, f'p2{L}') as st:
            xts = [st.T(f'xt{i}', [128, D]) for i in range(2)]
            hts = [st.T(f'ht{i}', [128, D]) for i in range(2)]
            eis = [st.T(f'ei{i}', [128, 128], I32) for i in range(2)]
            gts = [st.T(f'gt{i}', [128, 128]) for i in range(2)]
            ub = [st.T(f'ub{i}', [128, D]) for i in range(NB)]
            vb = [st.T(f'vb{i}', [128, D]) for i in range(NB)]
            junk = st.T('junk', [128, D])
            apre = st.T('apre', [128, 128])
            coef = st.T('coef', [128, 128])
            accs = [st.T(f'acc{i}', [128, D]) for i in range(2)]
            nu = nv = 0
            for ti in range(NT):
                b = ti % 2
                xt, ht, eib, gtb, acc = xts[b], hts[b], eis[b], gts[b], accs[b]
                rx, rh, rei, rgt, racc = f'xt{b}', f'ht{b}', f'ei{b}', f'gt{b}', f'acc{b}'
                P.op('sp', lambda e, xt=xt, ti=ti: e.dma_start(out=xt[:], in_=xin[ti * 128:(ti + 1) * 128, :]), writes=[rx], dma=True)
                P.op('sp', lambda e, eib=eib, ti=ti: e.dma_start(out=eib[:], in_=A['IDX'][ti * 128:(ti + 1) * 128, :]),
                     reads=[('IDX', ti)], writes=[rei], dma=True)
                P.op('sp', lambda e, gtb=gtb, ti=ti: e.dma_start(out=gtb[:], in_=A['GATE'][ti * 128:(ti + 1) * 128, :]),
                     reads=[('GATE', ti)], writes=[rgt], dma=True)
                self.modulate(xt[:], ht[:], rx, rh)
                for k in range(128):
                    s = nu % NB
                    nu += 1
                    P.op('pool', lambda e, s=s, k=k, eib=eib: e.indirect_dma_start(
                        out=ub[s][:], out_offset=None, in_=U,
                        in_offset=bass.IndirectOffsetOnAxis(ap=eib[:, k:k + 1], axis=0)),
                        reads=[rei], writes=[('ub', s)], dma=True)
                    P.op('dve', lambda e, s=s, k=k, ht=ht: e.scalar_tensor_tensor(out=junk[:], in0=ub[s][:], scalar=1.0, in1=ht[:], op0=ALU.mult, op1=ALU.mult,
                                                                               accum_out=apre[:, k:k + 1]),
                         reads=[('ub', s), rh], writes=['junk', 'apre'])
                P.op('act', lambda e: e.activation(out=coef[:], in_=apre[:], func=AF.Gelu), reads=['apre'], writes=['coef'])
                P.op('dve', lambda e, gtb=gtb: e.tensor_tensor(out=coef[:], in0=coef[:], in1=gtb[:], op=ALU.mult), reads=['coef', rgt], writes=['coef'])
                for k in range(128):
                    s = nv % NB
                    nv += 1
                    P.op('pool', lambda e, s=s, k=k, eib=eib: e.indirect_dma_start(
                        out=vb[s][:], out_offset=None, in_=V,
                        in_offset=bass.IndirectOffsetOnAxis(ap=eib[:, k:k + 1], axis=0)),
                        reads=[rei], writes=[('vb', s)], dma=True)
                    if k == 0:
                        P.op('dve', lambda e, s=s, acc=acc: e.tensor_scalar(out=acc[:], in0=vb[s][:], scalar1=coef[:, 0:1], scalar2=None, op0=ALU.mult),
                             reads=[('vb', s), 'coef'], writes=[racc])
                    else:
                        P.op('dve', lambda e, s=s, k=k, acc=acc: e.scalar_tensor_tensor(out=acc[:], in0=vb[s][:], scalar=coef[:, k:k + 1], in1=acc[:],
                                                                                      op0=ALU.mult, op1=ALU.add),
                             reads=[('vb', s), 'coef', racc], writes=[racc])
                P.op('pool', lambda e, acc=acc: e.tensor_tensor(out=acc[:], in0=acc[:], in1=self.gate_bc[:], op=ALU.mult), reads=[racc, 'gate_bc'], writes=[racc])
                P.op('dve', lambda e, acc=acc, xt=xt: e.scalar_tensor_tensor(out=acc[:], in0=xt[:], scalar=ALPHA, in1=acc[:], op0=ALU.mult, op1=ALU.add),
                     reads=[rx, racc], writes=[racc])
                self.layernorm_inplace(acc[:], racc)
                P.op('sp', lambda e, acc=acc, ti=ti: e.dma_start(out=dst[ti * 128:(ti + 1) * 128, :], in_=acc[:]), reads=[racc], dma=True)

    def emit_peer1a(self, xin, L):
        P, nc, A, ps = self.P, self.nc, self.A, self.ps
        with Stage(self, f'pa{L}') as st:
            wq = st.T('wq', [128, 8, 2048])
            skT = st.T('skT', [128, 2, 128])
            xts = [st.T(f'xt{i}', [128, D]) for i in range(2)]
            hts = [st.T(f'ht{i}', [128, D]) for i in range(2)]
            hT = st.T('hT', [128, 8, 256])
            qT = st.T('qT', [128, 16, 256])
            sco = [st.T(f'sco{i}', [128, 2048]) for i in range(2)]
            for q in range(4):
                P.op('sp', lambda e, q=q: e.dma_start(out=wq[:, :, q * 512:(q + 1) * 512],
                                                      in_=A['peer_query_w'][L][:, q * 512:(q + 1) * 512].rearrange("(k p) n -> p k n", p=128)),
                     writes=[('wq', q)], dma=True)
            P.op('sp', lambda e: e.dma_start(out=skT[:], in_=A['peer_skT'][L].rearrange("h d k -> d h k")), writes=['skT'], dma=True)
            ti = 0
            for jb in range(S // 256):
                for tl in range(2):
                    xt, ht = xts[ti % 2], hts[ti % 2]
                    rx, rh = f'xt{ti % 2}', f'ht{ti % 2}'
                    P.op('sp', lambda e, xt=xt, ti=ti: e.dma_start(out=xt[:], in_=xin[ti * 128:(ti + 1) * 128, :]), writes=[rx], dma=True)
                    self.modulate(xt[:], ht[:], rx, rh)
                    P.op('sp', lambda e, ht=ht, ti=ti: e.dma_start(out=A['H'][ti * 128:(ti + 1) * 128, :], in_=ht[:]), reads=[rh], writes=[('H', ti)], dma=True)
                    self.transpose8(ht, rh, hT, 'hT', tl * 128, 0)
                    ti += 1
                for c in range(16):
                    bank = ps[2 + c % 2]
                    rb = f'ps{2 + c % 2}'
                    for k in range(8):
                        P.op('pe', lambda e, k=k, c=c, bank=bank: e.matmul(out=bank[:, 0:256], lhsT=wq[:, k, c * 128:(c + 1) * 128], rhs=hT[:, k, :],
                                                                          start=(k == 0), stop=(k == 7)),
                             reads=[('wq', c // 4), 'hT'], writes=[rb])
                    if c % 2 == 0:
                        P.op('act', lambda e, c=c, bank=bank: e.copy(out=qT[:, c, :], in_=bank[:, 0:256]), reads=[rb], writes=[('qT', c)])
                    else:
                        P.op('dve', lambda e, c=c, bank=bank: e.tensor_copy(out=qT[:, c, :], in_=bank[:, 0:256]), reads=[rb], writes=[('qT', c)])
                for tl in range(2):
                    tix = jb * 2 + tl
                    so = sco[tix % 2]
                    rso = f'sco{tix % 2}'
                    for c in range(16):
                        bank = ps[4 + c // 4]
                        rb = f'ps{4 + c // 4}'
                        P.op('pe', lambda e, c=c, bank=bank, tl=tl: e.matmul(out=bank[:, (c % 4) * 128:(c % 4 + 1) * 128],
                                                                            lhsT=qT[:, c, tl * 128:(tl + 1) * 128], rhs=skT[:, c % 2, :],
                                                                            start=True, stop=True),
                             reads=[('qT', c), 'skT'], writes=[rb])
                    for g4 in range(4):
                        eng = ('act', 'dve')[g4 % 2]
                        if eng == 'act':
                            P.op('act', lambda e, g4=g4, so=so: e.copy(out=so[:, g4 * 512:(g4 + 1) * 512], in_=ps[4 + g4][:]), reads=[f'ps{4 + g4}'], writes=[rso])
                        else:
                            P.op('dve', lambda e, g4=g4, so=so: e.tensor_copy(out=so[:, g4 * 512:(g4 + 1) * 512], in_=ps[4 + g4][:]), reads=[f'ps{4 + g4}'], writes=[rso])
                    P.op('sp', lambda e, so=so, tix=tix: e.dma_start(out=A['SCR'][tix * 128:(tix + 1) * 128, :], in_=so[:]), reads=[rso], writes=[('SCR', tix)], dma=True)

    def emit_peer2f(self, xin, dst, L):
        P, nc, A, ps = self.P, self.nc, self.A, self.ps
        NB = self.cfg.get('nb', 11)
        U = A['peer_u'].rearrange("l e d -> (l e) d")
        V = A['peer_v'].rearrange("l e d -> (l e) d")
        with Stage(self, f'pf{L}') as st:
            scs = [st.T(f'sc{i}', [128, 16, 128]) for i in range(2)]
            m = st.T('m', [128, 16, 16])
            ix = st.T('ix', [128, 16, 16], U32)
            ixf = st.T('ixf', [128, 16, 16])
            wk = st.T('wk', [128, 16, 128])
            cand = st.T('cand', [128, 8, 256])
            candi = st.T('candi', [128, 8, 256])
            wk2 = st.T('wk2', [128, 8, 256])
            junk2 = [st.T(f'junk2{i}', [128, 256]) for i in range(2)]
            ts = st.T('ts', [128, 8, 16])
            ef = st.T('ef', [128, 128])
            eis = [st.T(f'ei{i}', [128, 128], I32) for i in range(2)]
            gts = [st.T(f'gt{i}', [128, 8, 16]) for i in range(2)]
            gsum = st.T('gsum', [128, 8])
            xts = [st.T(f'xt{i}', [128, D]) for i in range(2)]
            hts = [st.T(f'ht{i}', [128, D]) for i in range(2)]
            ub = [st.T(f'ub{i}', [128, D]) for i in range(NB)]
            vb = [st.T(f'vb{i}', [128, D]) for i in range(NB)]
            junk = st.T('junk', [128, D])
            apre = st.T('apre', [128, 128])
            coef = st.T('coef', [128, 128])
            accs = [st.T(f'acc{i}', [128, D]) for i in range(2)]

            def load_sc(t):
                P.op('sp', lambda e, t=t: e.dma_start(out=scs[t % 2][:].rearrange("p a k -> p (a k)"), in_=A['SCR'][t * 128:(t + 1) * 128, :]),
                     writes=[f'sc{t % 2}'], dma=True)

            def load_xh(t):
                P.op('sp', lambda e, t=t: e.dma_start(out=xts[t % 2][:], in_=xin[t * 128:(t + 1) * 128, :]), writes=[f'xt{t % 2}'], dma=True)
                P.op('sp', lambda e, t=t: e.dma_start(out=hts[t % 2][:], in_=A['H'][t * 128:(t + 1) * 128, :]), writes=[f'ht{t % 2}'], dma=True)

            def topk_gen(t):
                sc = scs[t % 2]
                rsc = f'sc{t % 2}'
                eib, gtb = eis[t % 2], gts[t % 2]
                rei, rgt = f'ei{t % 2}', f'gt{t % 2}'
                for c in range(16):
                    P.op('dve', lambda e, c=c: e.max(out=m[:, c, 0:8], in_=sc[:, c, :]), reads=[rsc], writes=[('m0', c)])
                    yield
                P.fence('dve')
                for c in range(16):
                    P.op('dve', lambda e, c=c: e.max_index(out=ix[:, c, 0:8], in_max=m[:, c, 0:8], in_values=sc[:, c, :]),
                         reads=[rsc, ('m0', c)], writes=[('ix0', c)])
                    yield
                    P.op('dve', lambda e, c=c: e.match_replace(out=wk[:, c, :], in_to_replace=m[:, c, 0:8], in_values=sc[:, c, :], imm_value=-1e30),
                         reads=[rsc, ('m0', c)], writes=[('wk', c)])
                    yield
                P.fence('dve')
                for c in range(16):
                    P.op('dve', lambda e, c=c: e.max(out=m[:, c, 8:16], in_=wk[:, c, :]), reads=[('wk', c)], writes=[('m1', c)])
                    yield
                P.fence('dve')
                for c in range(16):
                    P.op('dve', lambda e, c=c: e.max_index(out=ix[:, c, 8:16], in_max=m[:, c, 8:16], in_values=wk[:, c, :]),
                         reads=[('wk', c), ('m1', c)], writes=[('ix1', c)])
                    yield
                P.fence('dve')
                mres = [('m0', c) for c in range(16)] + [('m1', c) for c in range(16)]
                ixres = [('ix0', c) for c in range(16)] + [('ix1', c) for c in range(16)]
                P.op('dve', lambda e: e.tensor_copy(out=ixf[:], in_=ix[:]), reads=ixres, writes=['ixf'])
                yield
                m4 = m[:].rearrange("p (h two) k -> p h two k", two=2)
                i4 = ixf[:].rearrange("p (h two) k -> p h two k", two=2)
                c4 = cand[:].rearrange("p h (a b) -> p h a b", a=16)
                ci4 = candi[:].rearrange("p h (a b) -> p h a b", a=16)
                P.op('dve', lambda e: e.tensor_tensor(out=c4, in0=m4[:, :, 0, :].unsqueeze(3).to_broadcast([128, 8, 16, 16]),
                                                      in1=m4[:, :, 1, :].unsqueeze(2).to_broadcast([128, 8, 16, 16]), op=ALU.add),
                     reads=mres, writes=['cand'])
                yield
                P.op('dve', lambda e: e.tensor_scalar(out=i4[:, :, 0, :], in0=i4[:, :, 0, :], scalar1=128.0, scalar2=None, op0=ALU.mult),
                     reads=['ixf'], writes=['ixf'])
                yield
                P.op('dve', lambda e: e.tensor_tensor(out=ci4, in0=i4[:, :, 0, :].unsqueeze(3).to_broadcast([128, 8, 16, 16]),
                                                      in1=i4[:, :, 1, :].unsqueeze(2).to_broadcast([128, 8, 16, 16]), op=ALU.add),
                     reads=['ixf'], writes=['candi'])
                yield
                for h in range(8):
                    P.op('dve', lambda e, h=h: e.max(out=ts[:, h, 0:8], in_=cand[:, h, :]), reads=['cand'], writes=[('ts0', h)])
                    yield
                P.fence('dve')
                for h in range(8):
                    P.op('dve', lambda e, h=h: e.match_replace(out=wk2[:, h, :], in_to_replace=ts[:, h, 0:8], in_values=cand[:, h, :], imm_value=-1e30),
                         reads=['cand', ('ts0', h)], writes=[('wk2', h)])
                    yield
                P.fence('dve')
                for h in range(8):
                    P.op('dve', lambda e, h=h: e.max(out=ts[:, h, 8:16], in_=wk2[:, h, :]), reads=[('wk2', h)], writes=[('ts1', h)])
                    yield
                P.fence('dve')
                tsres = [('ts0', h) for h in range(8)] + [('ts1', h) for h in range(8)]
                for h in range(8):
                    for k in range(16):
                        P.op('dve', lambda e, h=h, k=k: e.scalar_tensor_tensor(out=junk2[(h * 16 + k) % 2][:], in0=cand[:, h, :], scalar=ts[:, h, k:k + 1], in1=candi[:, h, :],
                                                                               op0=ALU.is_equal, op1=ALU.mult, accum_out=ef[:, h * 16 + k:h * 16 + k + 1]),
                             reads=['cand', 'candi', ('ts0', h), ('ts1', h)], writes=[('ef', h * 16 + k)])
                        yield
                P.fence('dve')
                P.op('dve', lambda e: e.tensor_scalar(out=ef[:], in0=ef[:], scalar1=float(NEXP - 1), scalar2=float(L * NEXP), op0=ALU.min, op1=ALU.add),
                     reads=[('ef', q) for q in range(128)], writes=['ef'])
                yield
                P.op('dve', lambda e: e.tensor_copy(out=eib[:], in_=ef[:]), reads=['ef'], writes=[rei])
                yield
                P.op('dve', lambda e: e.tensor_tensor(out=gtb[:], in0=ts[:], in1=ts[:, :, 0:1].to_broadcast([128, 8, 16]), op=ALU.subtract),
                     reads=tsres, writes=[rgt])
                yield
                P.op('act', lambda e: e.activation(out=gtb[:], in_=gtb[:], func=AF.Exp), reads=[rgt], writes=[rgt])
                P.op('dve', lambda e: e.tensor_reduce(out=gsum[:], in_=gtb[:], axis=AX.X, op=ALU.add), reads=[rgt], writes=['gsum'])
                yield
                P.op('dve', lambda e: e.reciprocal(out=gsum[:], in_=gsum[:]), reads=['gsum'], writes=['gsum'])
                yield
                P.op('dve', lambda e: e.tensor_tensor(out=gtb[:], in0=gtb[:], in1=gsum[:].unsqueeze(2).to_broadcast([128, 8, 16]), op=ALU.mult),
                     reads=[rgt, 'gsum'], writes=[rgt])
                yield

            def step(gen, n=1):
                if gen is None:
                    return None
                try:
                    for _ in range(n):
                        next(gen)
                except StopIteration:
                    return None
                return gen

            load_sc(0)
            load_sc(1)
            load_xh(0)
            g0 = topk_gen(0)
            while g0 is not None:
                g0 = step(g0, 64)
            nu = nv = 0
            for ti in range(NT):
                b = ti % 2
                xt, ht, eib, gtb, acc = xts[b], hts[b], eis[b], gts[b], accs[b]
                rx, rh, rei, rgt, racc = f'xt{b}', f'ht{b}', f'ei{b}', f'gt{b}', f'acc{b}'
                gt2 = gtb[:].rearrange("p h k -> p (h k)")
                if ti + 1 < NT:
                    load_xh(ti + 1)
                gen = topk_gen(ti + 1) if ti + 1 < NT else None
                for k in range(128):
                    s_ = nu % NB
                    nu += 1
                    P.op('pool', lambda e, s_=s_, k=k, eib=eib: e.indirect_dma_start(
                        out=ub[s_][:], out_offset=None, in_=U,
                        in_offset=bass.IndirectOffsetOnAxis(ap=eib[:, k:k + 1], axis=0)),
                        reads=[rei], writes=[('ub', s_)], dma=True)
                    P.op('dve', lambda e, s_=s_, k=k, ht=ht: e.scalar_tensor_tensor(out=junk[:], in0=ub[s_][:], scalar=1.0, in1=ht[:], op0=ALU.mult, op1=ALU.mult,
                                                                                 accum_out=apre[:, k:k + 1]),
                         reads=[('ub', s_), rh], writes=[('apre', k)])
                    gen = step(gen)
                P.op('act', lambda e: e.activation(out=coef[:], in_=apre[:], func=AF.Gelu), reads=[('apre', k) for k in range(128)], writes=['coef'])
                P.op('dve', lambda e, gt2=gt2: e.tensor_tensor(out=coef[:], in0=coef[:], in1=gt2, op=ALU.mult), reads=['coef', rgt], writes=['coef'])
                for k in range(128):
                    s_ = nv % NB
                    nv += 1
                    P.op('pool', lambda e, s_=s_, k=k, eib=eib: e.indirect_dma_start(
                        out=vb[s_][:], out_offset=None, in_=V,
                        in_offset=bass.IndirectOffsetOnAxis(ap=eib[:, k:k + 1], axis=0)),
                        reads=[rei], writes=[('vb', s_)], dma=True)
                    if k == 0:
                        P.op('dve', lambda e, s_=s_, acc=acc: e.tensor_scalar(out=acc[:], in0=vb[s_][:], scalar1=coef[:, 0:1], scalar2=None, op0=ALU.mult),
                             reads=[('vb', s_), 'coef'], writes=[racc])
                    else:
                        P.op('dve', lambda e, s_=s_, k=k, acc=acc: e.scalar_tensor_tensor(out=acc[:], in0=vb[s_][:], scalar=coef[:, k:k + 1], in1=acc[:],
                                                                                       op0=ALU.mult, op1=ALU.add),
                             reads=[('vb', s_), 'coef', racc], writes=[racc])
                    gen = step(gen)
                while gen is not None:
                    gen = step(gen, 64)
                if ti + 2 < NT:
                    load_sc(ti + 2)
                P.op('dve', lambda e, acc=acc: e.tensor_tensor(out=acc[:], in0=acc[:], in1=self.gate_bc[:], op=ALU.mult), reads=[racc, 'gate_bc'], writes=[racc])
                P.op('dve', lambda e, acc=acc, xt=xt: e.scalar_tensor_tensor(out=acc[:], in0=xt[:], scalar=ALPHA, in1=acc[:], op0=ALU.mult, op1=ALU.add),
                     reads=[rx, racc], writes=[racc])
                self.layernorm_inplace(acc[:], racc, gb='dve')
                P.op('sp', lambda e, acc=acc, ti=ti: e.dma_start(out=dst[ti * 128:(ti + 1) * 128, :], in_=acc[:]), reads=[racc], dma=True)

    def emit_attn1(self, xin):
        P, nc, A, ps = self.P, self.nc, self.A, self.ps
        ones = self.ones
        NCOL = 3 * D + 16
        with Stage(self, 'a1') as st:
            winr = st.T('winr', [128, 8, 3 * D], F32R)
            wstg = [st.T('wstg0', [128, 8, 512])] * 2
            wf = st.T('wf', [128, 8, 16])
            qkb = st.T('qkb', [128, 16])
            vbr = st.T('vbr', [1, D])
            vb_bc = st.T('vb_bc', [128, D])
            fb = st.T('fb', [16, 1])
            xts = [st.T(f'xt{i}', [128, D]) for i in range(2)]
            ht = st.T('ht', [128, D])
            hT = st.T('hT', [128, 8, 512], F32R)
            qko = [st.T(f'qko{i}', [128, 512]) for i in range(2)]
            vo = [st.T(f'vo{i}', [128, D]) for i in range(2)]
            Fcb = [st.T(f'Fcb{i}', [16, 512]) for i in range(2)]
            Frb = st.T('Frb', [16, 512], F32R)
            Flb = st.T('Flb', [16, 512])
            nFr = st.T('nFr', [16, 512])
            nFl = st.T('nFl', [16, 512])
            spt = st.T('spt', [16, 512])
            o16 = st.T('o16', [16, 512])
            for q in range(6):
                wb = wstg[0]
                rw = 'wstg0'
                P.op('sp', lambda e, q=q, wb=wb: e.dma_start(out=wb[:], in_=A['attn_in_w'][:, q * 512:(q + 1) * 512].rearrange("(k p) n -> p k n", p=128)),
                     writes=[rw], dma=True)
                eng = ('dve', 'pool')[q % 2]
                P.op(eng, lambda e, q=q, wb=wb: e.tensor_copy(out=winr[:, :, q * 512:(q + 1) * 512], in_=wb[:]), reads=[rw], writes=[('win', q)])
            P.op('sp', lambda e: e.dma_start(out=wf[:], in_=A['attn_in_w'][:, 3 * D:NCOL].rearrange("(k p) n -> p k n", p=128)),
                 writes=['wf'], dma=True)
            P.op('sp', lambda e: e.dma_start(out=qkb[:], in_=A['attn_qkb_l']), writes=['qkb'], dma=True)
            P.op('sp', lambda e: e.dma_start(out=vbr[:], in_=A['attn_vb']), writes=['vbr'], dma=True)
            P.op('sp', lambda e: e.dma_start(out=fb[:], in_=A['attn_fb']), writes=['fb'], dma=True)
            P.op('dve', lambda e: e.tensor_scalar(out=qkb[:, 0:8], in0=qkb[:, 0:8], scalar1=0.125, scalar2=None, op0=ALU.mult), reads=['qkb'], writes=['qkb'])
            P.op('dve', lambda e: e.tensor_scalar(out=fb[:], in0=fb[:], scalar1=-1.0, scalar2=None, op0=ALU.mult), reads=['fb'], writes=['fb'])
            P.op('pool', lambda e: e.memset(o16[:], 1.0), writes=['o16'])
            for half in range(2):
                P.op('pe', lambda e, half=half: e.matmul(out=ps[4 + half][:], lhsT=ones[0:1, :], rhs=vbr[0:1, half * 512:(half + 1) * 512], start=True, stop=True),
                     reads=['ones', 'vbr'], writes=[f'ps{4 + half}'])
                P.op('act', lambda e, half=half: e.copy(out=vb_bc[:, half * 512:(half + 1) * 512], in_=ps[4 + half][:]), reads=[f'ps{4 + half}'], writes=['vb_bc'])
            ti = 0
            for jb in range(8):
                cols = slice(jb * 512, (jb + 1) * 512)
                for tl in range(4):
                    xt = xts[ti % 2]
                    rx = f'xt{ti % 2}'
                    P.op('sp', lambda e, xt=xt, ti=ti: e.dma_start(out=xt[:], in_=xin[ti * 128:(ti + 1) * 128, :]), writes=[rx], dma=True)
                    self.modulate(xt[:], ht[:], rx, 'ht')
                    self.transpose8(ht, 'ht', hT, 'hT', tl * 128, 0)
                    ti += 1
                for c in range(16):
                    bank = ps[2 + c % 2]
                    rb = f'ps{2 + c % 2}'
                    for k in range(8):
                        P.op('pe', lambda e, k=k, c=c, bank=bank: e.matmul(out=bank[:], lhsT=winr[:, k, c * 128:(c + 1) * 128], rhs=hT[:, k, :],
                                                                          start=(k == 0), stop=(k == 7)),
                             reads=[('win', c // 4), 'hT'], writes=[rb])
                    ob = qko[c % 2]
                    rob = f'qko{c % 2}'
                    P.op('act', lambda e, c=c, bank=bank, ob=ob: e.activation(out=ob[:], in_=bank[:], func=AF.Identity, bias=qkb[:, c:c + 1],
                                                                             scale=(0.125 if c < 8 else 1.0)),
                         reads=[rb, 'qkb'], writes=[rob])
                    dstt = A['QA'] if c < 8 else A['KA']
                    for hh in range(2):
                        head = (c % 8) * 2 + hh
                        P.op('sp', lambda e, ob=ob, hh=hh, head=head, dstt=dstt: e.dma_start(out=dstt[head, 0:64, cols], in_=ob[hh * 64:(hh + 1) * 64, :]),
                             reads=[rob], writes=[('QK', c, hh)], dma=True)
                for tl in range(4):
                    tix = jb * 4 + tl
                    vt = vo[tix % 2]
                    rv = f'vo{tix % 2}'
                    for half in range(2):
                        bank = ps[4 + half]
                        rb = f'ps{4 + half}'
                        for k in range(8):
                            P.op('pe', lambda e, k=k, bank=bank, half=half, tl=tl: e.matmul(out=bank[:], lhsT=hT[:, k, tl * 128:(tl + 1) * 128],
                                                                                        rhs=winr[:, k, 2 * D + half * 512:2 * D + (half + 1) * 512],
                                                                                        start=(k == 0), stop=(k == 7)),
                                 reads=['hT', ('win', 4 + half)], writes=[rb])
                        P.op('dve', lambda e, vt=vt, bank=bank, half=half: e.tensor_tensor(out=vt[:, half * 512:(half + 1) * 512], in0=bank[:],
                                                                                      in1=vb_bc[:, half * 512:(half + 1) * 512], op=ALU.add),
                             reads=[rb, 'vb_bc'], writes=[rv])
                    P.op('sp', lambda e, vt=vt, tix=tix: e.dma_start(out=A['V'][tix * 128:(tix + 1) * 128, :], in_=vt[:]), reads=[rv], writes=[('V', tix)], dma=True)
                for k in range(8):
                    P.op('pe', lambda e, k=k: e.matmul(out=ps[6][0:16, :], lhsT=wf[:, k, :], rhs=hT[:, k, :].bitcast(F32), start=(k == 0), stop=(k == 7)),
                         reads=['wf', 'hT'], writes=['ps6'])
                P.op('act', lambda e: e.activation(out=spt[:], in_=ps[6][0:16, :], func=AF.Exp, bias=fb[:, 0:1], scale=-1.0), reads=['ps6', 'fb'], writes=['spt'])
                P.op('act', lambda e: e.activation(out=spt[:], in_=spt[:], func=AF.Ln, bias=1.0, scale=1.0), reads=['spt'], writes=['spt'])
                P.op('dve', lambda e: e.tensor_scalar(out=spt[:], in0=spt[:], scalar1=-1.0, scalar2=None, op0=ALU.mult), reads=['spt'], writes=['spt'])
                Fc = Fcb[jb % 2]
                rF = f'Fcb{jb % 2}'
                init = 0.0 if jb == 0 else Fcb[(jb - 1) % 2][:, 511:512]
                P.op('dve', lambda e, init=init, Fc=Fc: e.tensor_tensor_scan(out=Fc[:], data0=o16[:], data1=spt[:], initial=init,
                                                                             op0=ALU.mult, op1=ALU.add),
                     reads=['o16', 'spt', f'Fcb{(jb - 1) % 2}'], writes=[rF])
                P.op('dve', lambda e, Fc=Fc: e.tensor_copy(out=Frb[:], in_=Fc[:]), reads=[rF], writes=['Frb'])
                P.op('dve', lambda e, Fc=Fc: e.tensor_tensor(out=Flb[:], in0=Fc[:], in1=Frb[:].bitcast(F32), op=ALU.subtract), reads=[rF, 'Frb'], writes=['Flb'])
                P.op('dve', lambda e: e.tensor_scalar(out=nFr[:], in0=Frb[:].bitcast(F32), scalar1=-1.0, scalar2=None, op0=ALU.mult), reads=['Frb'], writes=['nFr'])
                P.op('dve', lambda e: e.tensor_scalar(out=nFl[:], in0=Flb[:], scalar1=-1.0, scalar2=None, op0=ALU.mult), reads=['Flb'], writes=['nFl'])
                P.op('sp', lambda e: e.dma_start(out=A['QA'][:, 64, cols], in_=Frb[:].bitcast(F32)), reads=['Frb'], writes=[('QAf', jb)], dma=True)
                P.op('sp', lambda e: e.dma_start(out=A['QA'][:, 65, cols], in_=Flb[:]), reads=['Flb'], writes=[('QAl', jb)], dma=True)
                P.op('sp', lambda e: e.dma_start(out=A['KA'][:, 66, cols], in_=nFr[:]), reads=['nFr'], writes=[('KAf', jb)], dma=True)
                P.op('sp', lambda e: e.dma_start(out=A['KA'][:, 67, cols], in_=nFl[:]), reads=['nFl'], writes=[('KAl', jb)], dma=True)
                for r in (66, 67):
                    P.op('sp', lambda e, r=r: e.dma_start(out=A['QA'][:, r, cols], in_=o16[:]), reads=['o16'], writes=[('QAo', r, jb)], dma=True)
                for r in (64, 65):
                    P.op('sp', lambda e, r=r: e.dma_start(out=A['KA'][:, r, cols], in_=o16[:]), reads=['o16'], writes=[('KAo', r, jb)], dma=True)

    def emit_attn2(self):
        P, nc, A, ps = self.P, self.nc, self.A, self.ps
        NR = 68
        with Stage(self, 'a2') as st:
            qst = st.T('qst', [NR, S])
            kst = st.T('kst', [NR, S])
            vst = st.T('vst', [128, 32, 64])
            QAh = [st.T(f'QAh{i}', [NR, S], F32R) for i in range(2)]
            KAh = [st.T(f'KAh{i}', [NR, S], F32R) for i in range(2)]
            Vh = [st.T(f'Vh{i}', [128, 32, 128], F32R) for i in range(2)]
            ones_r = st.T('ones_r', [128, 128], F32R)
            pt = [st.T(f'pt{i}', [128, 512], F32R) for i in range(3)]
            lm = [st.T(f'lm{i}', [128, 512]) for i in range(2)]
            mask = st.T('mask', [128, 4, 512])
            rzt = st.T('rzt', [64, 512])
            oT = [st.T(f'oT{i}', [64, 512]) for i in range(2)]
            P.op('pool', lambda e: e.memset(mask[:], 0.0), writes=['mask'])
            for i4 in range(4):
                P.op('pool', lambda e, i4=i4: e.affine_select(out=mask[:, i4, :], in_=mask[:, i4, :], pattern=[[1, 512]], compare_op=ALU.is_ge,
                                                              fill=NEG, base=-128 * i4, channel_multiplier=-1), reads=['mask'], writes=['mask'])
            P.op('pool', lambda e: e.tensor_copy(out=ones_r[:], in_=self.ones[:]), reads=['ones'], writes=['ones_r'])
            npt = 0
            nlm = 0
            nS = 0
            nO = 0

            def loads(h):
                b = h % 2
                qa, ka, vh = QAh[b], KAh[b], Vh[b]
                rq, rk, rv = f'QAh{b}', f'KAh{b}', f'Vh{b}'
                for q4 in range(4):
                    cs = slice(q4 * 1024, (q4 + 1) * 1024)
                    P.op('sp', lambda e, h=h, cs=cs: e.dma_start(out=qst[:, cs], in_=A['QA'][h, :, cs]), writes=[('qst', q4)], dma=True)
                    P.op('sp', lambda e, h=h, cs=cs: e.dma_start(out=kst[:, cs], in_=A['KA'][h, :, cs]), writes=[('kst', q4)], dma=True)
                    P.op('sp', lambda e, h=h, q4=q4: e.dma_start(
                        out=vst[:, q4 * 8:(q4 + 1) * 8, :],
                        in_=A['V'][q4 * 1024:(q4 + 1) * 1024, h * 64:(h + 1) * 64].rearrange("(i p) d -> p i d", p=128)),
                        writes=[('vst', q4)], dma=True)
                for q4 in range(4):
                    cs = slice(q4 * 1024, (q4 + 1) * 1024)
                    P.op('pool', lambda e, qa=qa, cs=cs: e.tensor_copy(out=qa[:, cs], in_=qst[:, cs]), reads=[('qst', q4)], writes=[rq])
                    P.op('pool', lambda e, ka=ka, cs=cs: e.tensor_copy(out=ka[:, cs], in_=kst[:, cs]), reads=[('kst', q4)], writes=[rk])
                    for dup in range(2):
                        P.op('pool', lambda e, vh=vh, q4=q4, dup=dup: e.tensor_copy(out=vh[:, q4 * 8:(q4 + 1) * 8, dup * 64:(dup + 1) * 64],
                                                                                  in_=vst[:, q4 * 8:(q4 + 1) * 8, :]),
                             reads=[('vst', q4)], writes=[rv])

            loads(0)
            NH = self.cfg.get('nheads', 16)
            for h in range(NH):
                b = h % 2
                qa, ka, vh = QAh[b], KAh[b], Vh[b]
                rq, rk, rv = f'QAh{b}', f'KAh{b}', f'Vh{b}'
                if h + 1 < NH:
                    loads(h + 1)
                for j in range(8):
                    poA, poB = ps[3 + 2 * (nO % 2)], ps[4 + 2 * (nO % 2)]
                    rA, rB = f'ps{3 + 2 * (nO % 2)}', f'ps{4 + 2 * (nO % 2)}'
                    ot = oT[nO % 2]
                    rot = f'oT{nO % 2}'
                    nO += 1
                    last = 4 * j + 3
                    for i in range(4 * j + 4):
                        sb = ps[nS % 3]
                        rsb = f'ps{nS % 3}'
                        nS += 1
                        P.op('pe', lambda e, sb=sb, ka=ka, qa=qa, i=i, j=j: e.matmul(out=sb[:], lhsT=ka[:, i * 128:(i + 1) * 128], rhs=qa[:, j * 512:(j + 1) * 512],
                                                                                 start=True, stop=True),
                             reads=[rk, rq], writes=[rsb])
                        p_ = pt[npt % 3]
                        rp = f'pt{npt % 3}'
                        npt += 1
                        if i >= 4 * j:
                            l_ = lm[nlm % 2]
                            rl = f'lm{nlm % 2}'
                            nlm += 1
                            P.op('dve', lambda e, l_=l_, sb=sb, i=i, j=j: e.tensor_tensor(out=l_[:], in0=sb[:], in1=mask[:, i - 4 * j, :], op=ALU.add),
                                 reads=[rsb, 'mask'], writes=[rl])
                            P.op('act', lambda e, p_=p_, l_=l_: e.activation(out=p_[:], in_=l_[:], func=AF.Exp), reads=[rl], writes=[rp])
                        else:
                            P.op('act', lambda e, p_=p_, sb=sb: e.activation(out=p_[:], in_=sb[:], func=AF.Exp), reads=[rsb], writes=[rp])
                        P.op('pe', lambda e, p_=p_, i=i, poA=poA, vh=vh, last=last: e.matmul(out=poA[:], lhsT=vh[:, i, :], rhs=p_[:], start=(i == 0), stop=(i == last)),
                             reads=[rp, rv], writes=[rA])
                        P.op('pe', lambda e, p_=p_, i=i, poB=poB, last=last: e.matmul(out=poB[:], lhsT=ones_r[:], rhs=p_[:], start=(i == 0), stop=(i == last)),
                             reads=[rp, 'ones_r'], writes=[rB])
                    P.op('dve', lambda e, poB=poB: e.reciprocal(out=rzt[:], in_=poB[0:64, :]), reads=[rB], writes=['rzt'])
                    P.op('dve', lambda e, poA=poA, ot=ot: e.tensor_tensor(out=ot[:], in0=poA[0:64, :], in1=rzt[:], op=ALU.mult), reads=[rA, 'rzt'], writes=[rot])
                    P.op('sp', lambda e, ot=ot, h=h, j=j: e.dma_start(out=A['AOT'][h // 2, (h % 2) * 64:(h % 2) * 64 + 64, j * 512:(j + 1) * 512], in_=ot[:]),
                         reads=[rot], writes=[('AOT', h, j)], dma=True)


def make_in_maps(inputs, cores=range(8)):
    f = lambda a: np.ascontiguousarray(np.asarray(a, dtype=np.float32))
    sh = {}
    sh['ada_mix_w'] = f(inputs['ada_mix_w'])
    sh['ada_ffn_w'] = f(inputs['ada_ffn_w'])
    amb, afb = f(inputs['ada_mix_b']), f(inputs['ada_ffn_b'])
    sh['ada_b'] = f(np.stack([amb[0], afb[0], amb[1], afb[1]]))
    g1, g2 = f(inputs['ln_mix_g']), f(inputs['ln_ffn_g'])
    b1, b2 = f(inputs['ln_mix_b']), f(inputs['ln_ffn_b'])
    sh['ln_g'] = f(np.stack([g1[0], g2[0], g1[1], g2[1]]))
    sh['ln_b'] = f(np.stack([b1[0], b2[0], b1[1], b2[1]]))
    sh['conv_in_w'] = f(inputs['conv_in_w'][0])
    sh['conv_in_b_l'] = f(np.asarray(inputs['conv_in_b'][0]).reshape(16, 128).T)
    sh['conv_dw_w_l'] = f(np.asarray(inputs['conv_dw_w'][0]).reshape(31, 8, 128).transpose(2, 1, 0))
    sh['conv_vec_l'] = f(np.stack([np.asarray(inputs[k][0]).reshape(8, 128).T for k in ('conv_dw_b', 'conv_ln_g', 'conv_ln_b')], axis=1))
    sh['conv_out_w'] = f(inputs['conv_out_w'][0])
    sh['conv_out_b'] = f(np.asarray(inputs['conv_out_b'][0]).reshape(1, D))
    sh['attn_in_w'] = f(inputs['attn_in_w'][0])
    ab = np.asarray(inputs['attn_in_b'][0])
    sh['attn_qkb_l'] = f(ab[:2 * D].reshape(16, 128).T)
    sh['attn_vb'] = f(ab[2 * D:3 * D].reshape(1, D))
    sh['attn_fb'] = f(ab[3 * D:].reshape(16, 1))
    sh['attn_out_w'] = f(inputs['attn_out_w'][0])
    sh['attn_out_b'] = f(np.asarray(inputs['attn_out_b'][0]).reshape(1, D))
    sh['peer_query_w'] = f(inputs['peer_query_w'])
    k1, k2 = np.asarray(inputs['peer_sub_keys_1']), np.asarray(inputs['peer_sub_keys_2'])
    sh['peer_skT'] = f(np.stack([np.stack([k1[l].T, k2[l].T]) for l in range(2)]))
    sh['peer_u'] = f(inputs['peer_expert_u'])
    sh['peer_v'] = f(inputs['peer_expert_v'])
    x = np.asarray(inputs['x'])
    c = np.asarray(inputs['c'])
    maps = []
    for b in cores:
        m = dict(sh)
        m['x'] = f(x[b])
        m['c_l'] = f(c[b].reshape(8, 128).T)
        maps.append(m)
    return maps


_NC_CACHE = {}


def kernel(**inputs):
    if 'full' not in _NC_CACHE:
        _NC_CACHE['full'] = Kern({}).build()
    nc = _NC_CACHE['full']
    maps = make_in_maps(inputs)
    res = run_bass_kernel_spmd(nc, maps, core_ids=list(range(8)))
    return np.stack([np.asarray(r['out'], dtype=np.float32) for r in res.results], axis=0)
```

```python
import numpy as np
from contextlib import ExitStack
import concourse.bass as bass
import concourse.mybir as mybir
from concourse.bass_utils import run_bass_kernel_spmd

F32 = mybir.dt.float32
I32 = mybir.dt.int32
U32 = mybir.dt.uint32
F32R = mybir.dt.float32r
ALU = mybir.AluOpType
AF = mybir.ActivationFunctionType
AX = mybir.AxisListType

S = 4096
D = 1024
NT = S // 128
ALPHA = float((2 * 2) ** 0.25)
EPS = 1e-5
NEXP = 16384
MAXV = 30000
NEG = -30000.0


class Prog:
    def __init__(self, nc, es):
        self.nc = nc
        self.es = es
        self.eng = {'pe': nc.tensor, 'dve': nc.vector, 'act': nc.scalar,
                    'pool': nc.gpsimd, 'sp': nc.sync}
        self.seq = {e: 0 for e in self.eng}
        self.csem = {e: [] for e in self.eng}
        self.known = {e: {} for e in self.eng}
        self.snap = {}
        self.last_w = {}
        self.readers = {}
        self.semobj = {}
        self.dma_pool = {}
        self.nsem = 0
        self.nwaits = 0
        self.nops = 0
        for q, n in (('sp', 24), ('pool', 24), ('act', 8)):
            self.dma_pool[q] = {'sems': [self._newsem(f"d{q}{i}") for i in range(n)],
                                'cnt': [0] * n, 'next': 0}

    def _newsem(self, name):
        s = self.es.enter_context(self.nc.semaphore(name))
        self.semobj[name] = s
        self.nsem += 1
        return name

    def _need(self, e, tok, skip_self):
        if tok is None:
            return
        name, val, owner = tok
        if skip_self and owner == e:
            return
        if self.known[e].get(name, 0) >= val:
            return
        self.eng[e].wait_ge(self.semobj[name], val)
        self.nwaits += 1
        k = self.known[e]
        k[name] = val
        sn = self.snap.get((name, val))
        if sn:
            for n2, v2 in sn.items():
                if k.get(n2, 0) < v2:
                    k[n2] = v2

    def op(self, e, fn, reads=(), writes=(), dma=False, skip_self=None):
        if skip_self is None:
            skip_self = (e == 'pe')
        if dma:
            skip_self = False
        for r in reads:
            self._need(e, self.last_w.get(r), skip_self)
        for w in writes:
            self._need(e, self.last_w.get(w), skip_self)
            for t in self.readers.get(w, ()):
                self._need(e, t, skip_self)
        self.nops += 1
        if dma:
            pool = self.dma_pool[e]
            i = pool['next']
            pool['next'] = (i + 1) % len(pool['sems'])
            name = pool['sems'][i]
            if pool['cnt'][i] + 16 > MAXV:
                name = self._newsem(f"{name}r{self.nsem}")
                pool['sems'][i] = name
                pool['cnt'][i] = 0
            prev = pool['cnt'][i]
            if prev > 0:
                self._need(e, (name, prev, e + '_dma'), False)
            ins = fn(self.eng[e])
            pool['cnt'][i] = prev + 16
            ins.then_inc(self.semobj[name], 16)
            tok = (name, prev + 16, e + '_dma')
        else:
            n = self.seq[e]
            ep = n // MAXV
            while len(self.csem[e]) <= ep:
                self.csem[e].append(self._newsem(f"c{e}{len(self.csem[e])}"))
            name = self.csem[e][ep]
            ins = fn(self.eng[e])
            ins.then_inc(self.semobj[name], 1)
            self.seq[e] = n + 1
            tok = (name, n - ep * MAXV + 1, e)
        self.snap[(tok[0], tok[1])] = dict(self.known[e])
        for r in reads:
            self.readers.setdefault(r, []).append(tok)
        for w in writes:
            self.last_w[w] = tok
            self.readers[w] = []
        return tok

    def fence(self, e):
        n = self.seq[e]
        if n > 0:
            ep = (n - 1) // MAXV
            self._need(e, (self.csem[e][ep], n - ep * MAXV, e), False)

    def barrier(self):
        toks = []
        for e in self.eng:
            n = self.seq[e]
            if n > 0:
                ep = (n - 1) // MAXV
                toks.append((self.csem[e][ep], n - ep * MAXV, e))
        for q, pool in self.dma_pool.items():
            for name, c in zip(pool['sems'], pool['cnt']):
                if c > 0:
                    toks.append((name, c, q + '_dma'))
        for e in self.eng:
            for t in toks:
                self._need(e, t, False)
        self.last_w.clear()
        self.readers.clear()
        self.snap.clear()


class Stage:
    _n = 0

    def __init__(self, K, name):
        self.K = K
        Stage._n += 1
        self.name = f"{name}{Stage._n}"

    def __enter__(self):
        self.es = ExitStack()
        self.es.__enter__()
        return self

    def T(self, name, shape, dt=F32):
        return self.es.enter_context(self.K.nc.sbuf_tensor(f"{self.name}_{name}", shape, dt))

    def __exit__(self, *a):
        self.K.P.barrier()
        return self.es.__exit__(*a)


class Kern:
    def __init__(self, cfg):
        self.cfg = cfg

    def build(self):
        nc = bass.Bass("TRN2", target_bir_lowering=False)
        self.nc = nc
        dbg = self.cfg.get('debug', False)

        def din(name, shape, dt=F32):
            return nc.dram_tensor(name, list(shape), dt, kind="ExternalInput").ap()

        def dscr(name, shape, dt=F32):
            kind = "ExternalOutput" if (dbg and name in self.cfg.get('expose', ())) else "Internal"
            return nc.dram_tensor(name, list(shape), dt, kind=kind).ap()

        A = {}
        A['x'] = din('x', [S, D])
        A['c_l'] = din('c_l', [128, 8])
        A['ada_mix_w'] = din('ada_mix_w', [2, D, 3 * D])
        A['ada_ffn_w'] = din('ada_ffn_w', [2, D, 3 * D])
        A['ada_b'] = din('ada_b', [4, 3 * D])
        A['ln_g'] = din('ln_g', [4, D])
        A['ln_b'] = din('ln_b', [4, D])
        A['conv_in_w'] = din('conv_in_w', [D, 2 * D])
        A['conv_in_b_l'] = din('conv_in_b_l', [128, 16])
        A['conv_dw_w_l'] = din('conv_dw_w_l', [128, 8, 31])
        A['conv_vec_l'] = din('conv_vec_l', [128, 3, 8])
        A['conv_out_w'] = din('conv_out_w', [D, D])
        A['conv_out_b'] = din('conv_out_b', [1, D])
        A['attn_in_w'] = din('attn_in_w', [D, 3 * D + 16])
        A['attn_qkb_l'] = din('attn_qkb_l', [128, 16])
        A['attn_vb'] = din('attn_vb', [1, D])
        A['attn_fb'] = din('attn_fb', [16, 1])
        A['attn_out_w'] = din('attn_out_w', [D, D])
        A['attn_out_b'] = din('attn_out_b', [1, D])
        A['peer_query_w'] = din('peer_query_w', [2, D, 2 * D])
        A['peer_skT'] = din('peer_skT', [2, 2, 128, 128])
        A['peer_u'] = din('peer_u', [2, NEXP, D])
        A['peer_v'] = din('peer_v', [2, NEXP, D])
        A['out'] = nc.dram_tensor('out', [S, D], F32, kind="ExternalOutput").ap()
        A['X1'] = dscr('X1', [S, D])
        A['X2'] = dscr('X2', [S, D])
        A['X3'] = dscr('X3', [S, D])
        A['ST'] = dscr('ST', [8, 128, S])
        A['IDX'] = dscr('IDX', [S, 128], I32)
        A['SCR'] = dscr('SCR', [S, 2048])
        A['H'] = dscr('H', [S, D])
        A['GATE'] = dscr('GATE', [S, 128])
        A['QA'] = dscr('QA', [16, 68, S])
        A['KA'] = dscr('KA', [16, 68, S])
        A['V'] = dscr('V', [S, D])
        A['AOT'] = dscr('AOT', [8, 128, S])
        self.A = A

        with ExitStack() as es:
            self.P = P = Prog(nc, es)
            G = lambda name, shape, dt=F32: es.enter_context(nc.sbuf_tensor(name, shape, dt))
            self.ps = [es.enter_context(nc.psum_tensor(f"ps{i}", [128, 512], F32)) for i in range(8)]
            self.ident = G('ident', [128, 128])
            self.ones = G('ones', [128, 128])
            self.SC = G('SC', [128, 8, 128])
            self.shift_bc = G('shift_bc', [128, D])
            self.scale_bc = G('scale_bc', [128, D])
            self.gate_bc = G('gate_bc', [128, D])
            self.g_bc = G('g_bc', [128, D])
            self.b_bc = G('b_bc', [128, D])
            self.bs = G('bs', [128, 2, 6])
            self.mv = G('mv', [128, 2])
            self.rs = G('rs', [128, 1])
            self.emit_globals()
            order = self.cfg.get('stages', ['conv', 'peer0', 'attn', 'peer1'])
            cur = A['x']
            nxt = {'conv': A['X1'], 'peer0': A['X2'], 'attn': A['X3'], 'peer1': A['out']}
            for i, st in enumerate(order):
                dst = A['out'] if i == len(order) - 1 else nxt[st]
                if st == 'conv':
                    self.emit_adaln(0)
                    self.emit_conv1(cur)
                    self.emit_proj_out(cur, dst, A['conv_out_w'], A['conv_out_b'], src_fm=A['ST'])
                elif st == 'attn':
                    self.emit_adaln(2)
                    self.emit_attn1(cur)
                    self.emit_attn2()
                    self.emit_proj_out(cur, dst, A['attn_out_w'], A['attn_out_b'], src_fm=A['AOT'])
                else:
                    L = int(st[-1])
                    self.emit_adaln(1 + 2 * L)
                    if self.cfg.get('peer_fused', True):
                        self.emit_peer1a(cur, L)
                        self.emit_peer2f(cur, dst, L)
                    else:
                        self.emit_peer1(cur, L)
                        self.emit_peer2(cur, dst, L)
                cur = dst
            P.barrier()
            print(f"[kern] ops={P.nops} waits={P.nwaits} sems={P.nsem} seq={P.seq}")
        return nc

    def emit_globals(self):
        P, nc = self.P, self.nc
        ident, ones = self.ident, self.ones
        P.op('pool', lambda e: e.memset(ident[:], 1.0), writes=['ident'])
        P.op('pool', lambda e: e.affine_select(out=ident[:], in_=ident[:], pattern=[[-1, 128]],
                                               compare_op=ALU.is_equal, fill=0.0, base=0, channel_multiplier=1),
             reads=['ident'], writes=['ident'])
        P.op('pool', lambda e: e.memset(ones[:], 1.0), writes=['ones'])
        with Stage(self, 'gl') as st:
            ct = st.T('ct', [128, 8])
            P.op('sp', lambda e: e.dma_start(out=ct[:], in_=self.A['c_l']), writes=['ct'], dma=True)
            P.op('act', lambda e: e.activation(out=ct[:], in_=ct[:], func=AF.Silu), reads=['ct'], writes=['ct'])
            SC = self.SC
            P.op('dve', lambda e: e.tensor_copy(out=SC[:], in_=ct[:].unsqueeze(2).to_broadcast([128, 8, 128])),
                 reads=['ct'], writes=['SC'])

    def modulate(self, xt, ht, rx, rh):
        P = self.P
        P.op('dve', lambda e: e.tensor_tensor(out=ht, in0=xt, in1=self.scale_bc[:], op=ALU.mult),
             reads=[rx, 'scale_bc'], writes=[rh])
        P.op('pool', lambda e: e.tensor_tensor(out=ht, in0=ht, in1=self.shift_bc[:], op=ALU.add),
             reads=[rh, 'shift_bc'], writes=[rh])

    def transpose8(self, src, rsrc, dstT, rdst, col0, pb):
        P = self.P
        ps = self.ps
        for half in range(2):
            bank = ps[pb + half]
            rb = f'ps{pb + half}'
            for kk in range(4):
                k = half * 4 + kk
                P.op('pe', lambda e, k=k, kk=kk, bank=bank: e.transpose(out=bank[:, kk * 128:(kk + 1) * 128],
                                                                      in_=src[:, k * 128:(k + 1) * 128],
                                                                      identity=self.ident[:]),
                     reads=[rsrc, 'ident'], writes=[rb])
            dst = dstT[:, half * 4:half * 4 + 4, col0:col0 + 128]
            srcp = bank[:].rearrange("p (k n) -> p k n", k=4)
            if half == 0:
                P.op('act', lambda e, dst=dst, srcp=srcp: e.copy(out=dst, in_=srcp), reads=[rb], writes=[rdst])
            else:
                P.op('dve', lambda e, dst=dst, srcp=srcp: e.tensor_copy(out=dst, in_=srcp), reads=[rb], writes=[rdst])

    def layernorm_inplace(self, r, rr, gb='pool'):
        P = self.P
        bs, mv, rs = self.bs, self.mv, self.rs
        for c in range(2):
            P.op('dve', lambda e, c=c: e.bn_stats(out=bs[:, c, :], in_=r[:, c * 512:(c + 1) * 512]),
                 reads=[rr], writes=['bs'])
        P.op('dve', lambda e: e.bn_aggr(out=mv[:], in_=bs[:].rearrange("p a b -> p (a b)")), reads=['bs'], writes=['mv'])
        P.op('dve', lambda e: e.tensor_scalar(out=rs[:], in0=mv[:, 1:2], scalar1=EPS, scalar2=None, op0=ALU.add),
             reads=['mv'], writes=['rs'])
        P.op('act', lambda e: e.activation(out=rs[:], in_=rs[:], func=AF.Sqrt), reads=['rs'], writes=['rs'])
        P.op('dve', lambda e: e.reciprocal(out=rs[:], in_=rs[:]), reads=['rs'], writes=['rs'])
        P.op('dve', lambda e: e.tensor_scalar(out=r, in0=r, scalar1=mv[:, 0:1], scalar2=rs[:, 0:1],
                                              op0=ALU.subtract, op1=ALU.mult), reads=[rr, 'mv', 'rs'], writes=[rr])
        P.op(gb, lambda e: e.tensor_tensor(out=r, in0=r, in1=self.g_bc[:], op=ALU.mult), reads=[rr, 'g_bc'], writes=[rr])
        P.op(gb, lambda e: e.tensor_tensor(out=r, in0=r, in1=self.b_bc[:], op=ALU.add), reads=[rr, 'b_bc'], writes=[rr])

    def emit_adaln(self, sub):
        P, nc, A, ps = self.P, self.nc, self.A, self.ps
        L = sub // 2
        wsrc = (A['ada_mix_w'] if sub % 2 == 0 else A['ada_ffn_w'])[L]
        ones = self.ones
        with Stage(self, f'ada{sub}') as st:
            brow = st.T('brow', [1, 3 * D])
            lrow = st.T('lrow', [1, 2 * D])
            wch = [st.T(f'wch{i}', [128, 8, 512]) for i in range(2)]
            P.op('sp', lambda e: e.dma_start(out=brow[:], in_=A['ada_b'][sub:sub + 1, :]), writes=['brow'], dma=True)
            P.op('sp', lambda e: e.dma_start(out=lrow[:, 0:D], in_=A['ln_g'][sub:sub + 1, :]), writes=['lrow'], dma=True)
            P.op('sp', lambda e: e.dma_start(out=lrow[:, D:2 * D], in_=A['ln_b'][sub:sub + 1, :]), writes=['lrow'], dma=True)
            dsts = [self.shift_bc, self.shift_bc, self.scale_bc, self.scale_bc, self.gate_bc, self.gate_bc]
            names = ['shift_bc', 'shift_bc', 'scale_bc', 'scale_bc', 'gate_bc', 'gate_bc']
            for n6 in range(6):
                wb = wch[n6 % 2]
                rw = f'wch{n6 % 2}'
                P.op('sp', lambda e, wb=wb, n6=n6: e.dma_start(
                    out=wb[:], in_=wsrc[:, n6 * 512:(n6 + 1) * 512].rearrange("(k p) n -> p k n", p=128)),
                    writes=[rw], dma=True)
                bank = ps[n6 % 2]
                rb = f'ps{n6 % 2}'
                for k in range(8):
                    P.op('pe', lambda e, k=k, wb=wb, bank=bank: e.matmul(out=bank[:], lhsT=self.SC[:, k, :], rhs=wb[:, k, :],
                                                                        start=(k == 0), stop=False),
                         reads=['SC', rw], writes=[rb])
                P.op('pe', lambda e, bank=bank, n6=n6: e.matmul(out=bank[:], lhsT=ones[0:1, :], rhs=brow[0:1, n6 * 512:(n6 + 1) * 512],
                                                                start=False, stop=True),
                     reads=['ones', 'brow'], writes=[rb])
                dst = dsts[n6][:, (n6 % 2) * 512:(n6 % 2 + 1) * 512]
                if n6 in (2, 3):
                    P.op('dve', lambda e, dst=dst, bank=bank: e.tensor_scalar(out=dst, in0=bank[:], scalar1=1.0, scalar2=None, op0=ALU.add),
                         reads=[rb], writes=[names[n6]])
                else:
                    P.op('dve', lambda e, dst=dst, bank=bank: e.tensor_copy(out=dst, in_=bank[:]), reads=[rb], writes=[names[n6]])
            for j in range(4):
                bank = ps[2 + j % 2]
                rb = f'ps{2 + j % 2}'
                P.op('pe', lambda e, bank=bank, j=j: e.matmul(out=bank[:], lhsT=ones[0:1, :], rhs=lrow[0:1, j * 512:(j + 1) * 512],
                                                              start=True, stop=True), reads=['ones', 'lrow'], writes=[rb])
                dstt = self.g_bc if j < 2 else self.b_bc
                dst = dstt[:, (j % 2) * 512:(j % 2 + 1) * 512]
                P.op('act', lambda e, dst=dst, bank=bank: e.copy(out=dst, in_=bank[:]), reads=[rb],
                     writes=['g_bc' if j < 2 else 'b_bc'])

    def emit_conv1(self, xin):
        P, nc, A, ps = self.P, self.nc, self.A, self.ps
        ones = self.ones
        with Stage(self, 'c1') as st:
            win = st.T('win', [128, 8, 2048])
            cib = st.T('cib', [128, 16])
            dw = st.T('dw', [128, 8, 31])
            cv = st.T('cv', [128, 3, 8])
            xts = [st.T(f'xt{i}', [128, D]) for i in range(2)]
            ht = st.T('ht', [128, D])
            hT = st.T('hT', [128, 8, 512])
            acc = st.T('acc', [128, 8, 512])
            aext = st.T('aext', [128, 8, 542])
            sig = [st.T(f'sig{i}', [128, 512]) for i in range(2)]
            sq = [st.T(f'sq{i}', [128, 512]) for i in range(2)]
            meant = st.T('meant', [128, 512])
            rstd = st.T('rstd', [128, 512])
            tmp = st.T('tmp', [128, 512])
            for q in range(4):
                P.op('sp', lambda e, q=q: e.dma_start(out=win[:, :, q * 512:(q + 1) * 512],
                                                      in_=A['conv_in_w'][:, q * 512:(q + 1) * 512].rearrange("(k p) n -> p k n", p=128)),
                     writes=[('win', q)], dma=True)
            P.op('sp', lambda e: e.dma_start(out=cib[:], in_=A['conv_in_b_l']), writes=['cib'], dma=True)
            P.op('sp', lambda e: e.dma_start(out=dw[:], in_=A['conv_dw_w_l']), writes=['dw'], dma=True)
            P.op('sp', lambda e: e.dma_start(out=cv[:], in_=A['conv_vec_l']), writes=['cv'], dma=True)
            for cc in range(8):
                P.op('pool', lambda e, cc=cc: e.memset(aext[:, cc, 0:30], 0.0), writes=[('aext', cc)])
            ti = 0
            for jb in range(8):
                for tl in range(4):
                    xt = xts[ti % 2]
                    rx = f'xt{ti % 2}'
                    P.op('sp', lambda e, xt=xt, ti=ti: e.dma_start(out=xt[:], in_=xin[ti * 128:(ti + 1) * 128, :]),
                         writes=[rx], dma=True)
                    self.modulate(xt[:], ht[:], rx, 'ht')
                    self.transpose8(ht, 'ht', hT, 'hT', tl * 128, 0)
                    ti += 1
                for cc in range(8):
                    pa, pb = ps[2 + (cc % 2) * 2], ps[3 + (cc % 2) * 2]
                    ra, rb = f'ps{2 + (cc % 2) * 2}', f'ps{3 + (cc % 2) * 2}'
                    for k in range(8):
                        P.op('pe', lambda e, k=k, cc=cc, pa=pa: e.matmul(out=pa[:], lhsT=win[:, k, cc * 128:(cc + 1) * 128], rhs=hT[:, k, :],
                                                                        start=(k == 0), stop=(k == 7)),
                             reads=[('win', cc // 4), 'hT'], writes=[ra])
                    for k in range(8):
                        P.op('pe', lambda e, k=k, cc=cc, pb=pb: e.matmul(out=pb[:], lhsT=win[:, k, D + cc * 128:D + (cc + 1) * 128], rhs=hT[:, k, :],
                                                                        start=(k == 0), stop=(k == 7)),
                             reads=[('win', 2 + cc // 4), 'hT'], writes=[rb])
                    sg = sig[cc % 2]
                    rsg = f'sig{cc % 2}'
                    P.op('act', lambda e, sg=sg, pb=pb, cc=cc: e.activation(out=sg[:], in_=pb[:], func=AF.Sigmoid, bias=cib[:, 8 + cc:9 + cc], scale=1.0),
                         reads=[rb, 'cib'], writes=[rsg])
                    P.op('dve', lambda e, sg=sg, pa=pa, cc=cc: e.scalar_tensor_tensor(out=aext[:, cc, 30:542], in0=pa[:], scalar=cib[:, cc:cc + 1], in1=sg[:],
                                                                                 op0=ALU.add, op1=ALU.mult),
                         reads=[ra, rsg, 'cib'], writes=[('aext', cc)])
                for cc in range(8):
                    P.op('dve', lambda e, cc=cc: e.tensor_scalar(out=acc[:, cc, :], in0=aext[:, cc, 0:512], scalar1=dw[:, cc, 0:1], scalar2=cv[:, 0, cc:cc + 1],
                                                                 op0=ALU.mult, op1=ALU.add),
                         reads=[('aext', cc), 'dw', 'cv'], writes=[('acc', cc)])
                    for w in range(1, 31):
                        P.op('dve', lambda e, cc=cc, w=w: e.scalar_tensor_tensor(out=acc[:, cc, :], in0=aext[:, cc, w:w + 512], scalar=dw[:, cc, w:w + 1],
                                                                                 in1=acc[:, cc, :], op0=ALU.mult, op1=ALU.add),
                             reads=[('aext', cc), 'dw', ('acc', cc)], writes=[('acc', cc)])
                    P.op('act', lambda e, cc=cc: e.copy(out=aext[:, cc, 0:30], in_=aext[:, cc, 512:542]),
                         reads=[('aext', cc)], writes=[('aext', cc)])
                for cc in range(8):
                    s2 = sq[cc % 2]
                    rs2 = f'sq{cc % 2}'
                    P.op('act', lambda e, cc=cc, s2=s2: e.activation(out=s2[:], in_=acc[:, cc, :], func=AF.Square),
                         reads=[('acc', cc)], writes=[rs2])
                    P.op('pe', lambda e, cc=cc: e.matmul(out=ps[6][:], lhsT=ones[:], rhs=acc[:, cc, :], start=(cc == 0), stop=(cc == 7)),
                         reads=['ones', ('acc', cc)], writes=['ps6'])
                    P.op('pe', lambda e, cc=cc, s2=s2: e.matmul(out=ps[7][:], lhsT=ones[:], rhs=s2[:], start=(cc == 0), stop=(cc == 7)),
                         reads=['ones', rs2], writes=['ps7'])
                P.op('act', lambda e: e.activation(out=meant[:], in_=ps[6][:], func=AF.Copy, scale=1.0 / D), reads=['ps6'], writes=['meant'])
                P.op('dve', lambda e: e.tensor_tensor(out=tmp[:], in0=meant[:], in1=meant[:], op=ALU.mult), reads=['meant'], writes=['tmp'])
                P.op('dve', lambda e: e.scalar_tensor_tensor(out=rstd[:], in0=ps[7][:], scalar=1.0 / D, in1=tmp[:], op0=ALU.mult, op1=ALU.subtract),
                     reads=['ps7', 'tmp'], writes=['rstd'])
                P.op('dve', lambda e: e.tensor_scalar(out=rstd[:], in0=rstd[:], scalar1=EPS, scalar2=None, op0=ALU.add), reads=['rstd'], writes=['rstd'])
                P.op('act', lambda e: e.activation(out=rstd[:], in_=rstd[:], func=AF.Sqrt), reads=['rstd'], writes=['rstd'])
                P.op('dve', lambda e: e.reciprocal(out=rstd[:], in_=rstd[:]), reads=['rstd'], writes=['rstd'])
                for cc in range(8):
                    P.op('dve', lambda e, cc=cc: e.tensor_tensor(out=acc[:, cc, :], in0=acc[:, cc, :], in1=meant[:], op=ALU.subtract),
                         reads=[('acc', cc), 'meant'], writes=[('acc', cc)])
                    P.op('pool', lambda e, cc=cc: e.tensor_tensor(out=acc[:, cc, :], in0=acc[:, cc, :], in1=rstd[:], op=ALU.mult),
                         reads=[('acc', cc), 'rstd'], writes=[('acc', cc)])
                    P.op('act', lambda e, cc=cc: e.activation(out=acc[:, cc, :], in_=acc[:, cc, :], func=AF.Silu,
                                                              bias=cv[:, 2, cc:cc + 1], scale=cv[:, 1, cc:cc + 1]),
                         reads=[('acc', cc), 'cv'], writes=[('acc', cc)])
                P.op('sp', lambda e, jb=jb: e.dma_start(out=A['ST'][:, :, jb * 512:(jb + 1) * 512].rearrange("c p t -> p c t"), in_=acc[:]),
                     reads=[('acc', cc) for cc in range(8)], writes=[('ST', jb)], dma=True)

    def emit_proj_out(self, xin, dst, w_ap, b_ap, src_fm=None, src_tm=None):
        P, nc, A, ps = self.P, self.nc, self.A, self.ps
        ones = self.ones
        with Stage(self, 'po') as st:
            wo = st.T('wo', [128, 8, D])
            bo = st.T('bo', [1, D])
            xts = [st.T(f'xt{i}', [128, D]) for i in range(2)]
            rts = [st.T(f'rt{i}', [128, D]) for i in range(2)]
            if src_fm is not None:
                sT = [st.T(f'sT{i}', [128, 8, 512]) for i in range(2)]
            else:
                ao = [st.T(f'ao{i}', [128, D]) for i in range(2)]
                aT = [st.T(f'aT{i}', [128, 8, 128]) for i in range(2)]
            for q in range(2):
                P.op('sp', lambda e, q=q: e.dma_start(out=wo[:, :, q * 512:(q + 1) * 512],
                                                      in_=w_ap[:, q * 512:(q + 1) * 512].rearrange("(k p) n -> p k n", p=128)),
                     writes=[('wo', q)], dma=True)
            P.op('sp', lambda e: e.dma_start(out=bo[:], in_=b_ap), writes=['bo'], dma=True)
            for ti in range(NT):
                xt = xts[ti % 2]
                rx = f'xt{ti % 2}'
                rt = rts[ti % 2]
                rr = f'rt{ti % 2}'
                P.op('sp', lambda e, xt=xt, ti=ti: e.dma_start(out=xt[:], in_=xin[ti * 128:(ti + 1) * 128, :]), writes=[rx], dma=True)
                if src_fm is not None:
                    jb, tl = ti // 4, ti % 4
                    sb = sT[jb % 2]
                    rsb = f'sT{jb % 2}'
                    if tl == 0:
                        P.op('sp', lambda e, sb=sb, jb=jb: e.dma_start(out=sb[:], in_=src_fm[:, :, jb * 512:(jb + 1) * 512].rearrange("c p t -> p c t")),
                             reads=[('ST', jb)], writes=[rsb], dma=True)
                    lhs = lambda k, sb=sb, tl=tl: sb[:, k, tl * 128:(tl + 1) * 128]
                    rl = rsb
                else:
                    a = ao[ti % 2]
                    ra = f'ao{ti % 2}'
                    at = aT[ti % 2]
                    rat = f'aT{ti % 2}'
                    P.op('sp', lambda e, a=a, ti=ti: e.dma_start(out=a[:], in_=src_tm[ti * 128:(ti + 1) * 128, :]), writes=[ra], dma=True)
                    self.transpose8(a, ra, at, rat, 0, 4)
                    lhs = lambda k, at=at: at[:, k, :]
                    rl = rat
                pb = (ti % 2) * 2
                for half in range(2):
                    bank = ps[pb + half]
                    rb = f'ps{pb + half}'
                    for k in range(8):
                        P.op('pe', lambda e, k=k, bank=bank, half=half, lhs=lhs: e.matmul(out=bank[:], lhsT=lhs(k), rhs=wo[:, k, half * 512:(half + 1) * 512],
                                                                                     start=(k == 0), stop=False),
                             reads=[rl, ('wo', half)], writes=[rb])
                    P.op('pe', lambda e, bank=bank, half=half: e.matmul(out=bank[:], lhsT=ones[0:1, :], rhs=bo[0:1, half * 512:(half + 1) * 512],
                                                                        start=False, stop=True), reads=['ones', 'bo'], writes=[rb])
                    P.op('dve', lambda e, bank=bank, half=half, rt=rt: e.tensor_tensor(out=rt[:, half * 512:(half + 1) * 512], in0=bank[:],
                                                                                  in1=self.gate_bc[:, half * 512:(half + 1) * 512], op=ALU.mult),
                         reads=[rb, 'gate_bc'], writes=[rr])
                P.op('dve', lambda e, rt=rt, xt=xt: e.scalar_tensor_tensor(out=rt[:], in0=xt[:], scalar=ALPHA, in1=rt[:], op0=ALU.mult, op1=ALU.add),
                     reads=[rx, rr], writes=[rr])
                self.layernorm_inplace(rt[:], rr)
                P.op('sp', lambda e, rt=rt, ti=ti: e.dma_start(out=dst[ti * 128:(ti + 1) * 128, :], in_=rt[:]), reads=[rr], dma=True)

    def emit_peer1(self, xin, L):
        P, nc, A, ps = self.P, self.nc, self.A, self.ps
        with Stage(self, f'p1{L}') as st:
            wq = st.T('wq', [128, 8, 2048])
            skT = st.T('skT', [128, 2, 128])
            xts = [st.T(f'xt{i}', [128, D]) for i in range(2)]
            ht = st.T('ht', [128, D])
            hT = st.T('hT', [128, 8, 256])
            qT = st.T('qT', [128, 16, 256])
            sc = st.T('sc', [128, 16, 128])
            m = st.T('m', [128, 16, 16])
            ix = st.T('ix', [128, 16, 16], U32)
            ixf = st.T('ixf', [128, 16, 16])
            wk = st.T('wk', [128, 16, 128])
            cand = st.T('cand', [128, 8, 256])
            candi = st.T('candi', [128, 8, 256])
            wk2 = st.T('wk2', [128, 8, 256])
            junk = [st.T(f'junk{i}', [128, 256]) for i in range(2)]
            ts = st.T('ts', [128, 8, 16])
            ef = st.T('ef', [128, 128])
            ei = [st.T(f'ei{i}', [128, 128], I32) for i in range(2)]
            gt = [st.T(f'gt{i}', [128, 8, 16]) for i in range(2)]
            gsum = st.T('gsum', [128, 8])
            for q in range(4):
                P.op('sp', lambda e, q=q: e.dma_start(out=wq[:, :, q * 512:(q + 1) * 512],
                                                      in_=A['peer_query_w'][L][:, q * 512:(q + 1) * 512].rearrange("(k p) n -> p k n", p=128)),
                     writes=[('wq', q)], dma=True)
            P.op('sp', lambda e: e.dma_start(out=skT[:], in_=A['peer_skT'][L].rearrange("h d k -> d h k")), writes=['skT'], dma=True)
            ti = 0
            for jb in range(S // 256):
                for tl in range(2):
                    xt = xts[ti % 2]
                    rx = f'xt{ti % 2}'
                    P.op('sp', lambda e, xt=xt, ti=ti: e.dma_start(out=xt[:], in_=xin[ti * 128:(ti + 1) * 128, :]), writes=[rx], dma=True)
                    self.modulate(xt[:], ht[:], rx, 'ht')
                    self.transpose8(ht, 'ht', hT, 'hT', tl * 128, 0)
                    ti += 1
                for c in range(16):
                    bank = ps[2 + c % 2]
                    rb = f'ps{2 + c % 2}'
                    for k in range(8):
                        P.op('pe', lambda e, k=k, c=c, bank=bank: e.matmul(out=bank[:, 0:256], lhsT=wq[:, k, c * 128:(c + 1) * 128], rhs=hT[:, k, :],
                                                                          start=(k == 0), stop=(k == 7)),
                             reads=[('wq', c // 4), 'hT'], writes=[rb])
                    if c % 2 == 0:
                        P.op('act', lambda e, c=c, bank=bank: e.copy(out=qT[:, c, :], in_=bank[:, 0:256]), reads=[rb], writes=[('qT', c)])
                    else:
                        P.op('dve', lambda e, c=c, bank=bank: e.tensor_copy(out=qT[:, c, :], in_=bank[:, 0:256]), reads=[rb], writes=[('qT', c)])
                for tl in range(2):
                    tix = jb * 2 + tl
                    for c in range(16):
                        bank = ps[4 + c // 4]
                        rb = f'ps{4 + c // 4}'
                        P.op('pe', lambda e, c=c, bank=bank, tl=tl: e.matmul(out=bank[:, (c % 4) * 128:(c % 4 + 1) * 128],
                                                                            lhsT=qT[:, c, tl * 128:(tl + 1) * 128], rhs=skT[:, c % 2, :],
                                                                            start=True, stop=True),
                             reads=[('qT', c), 'skT'], writes=[rb])
                    for g4 in range(4):
                        P.op('act', lambda e, g4=g4: e.copy(out=sc[:, g4 * 4:(g4 + 1) * 4, :], in_=ps[4 + g4][:].rearrange("p (a k) -> p a k", a=4)),
                             reads=[f'ps{4 + g4}'], writes=['sc'])
                    for c in range(16):
                        P.op('dve', lambda e, c=c: e.max(out=m[:, c, 0:8], in_=sc[:, c, :]), reads=['sc'], writes=[('m0', c)])
                    P.fence('dve')
                    for c in range(16):
                        P.op('dve', lambda e, c=c: e.max_index(out=ix[:, c, 0:8], in_max=m[:, c, 0:8], in_values=sc[:, c, :]),
                             reads=['sc', ('m0', c)], writes=[('ix0', c)])
                        P.op('dve', lambda e, c=c: e.match_replace(out=wk[:, c, :], in_to_replace=m[:, c, 0:8], in_values=sc[:, c, :], imm_value=-1e30),
                             reads=['sc', ('m0', c)], writes=[('wk', c)])
                    P.fence('dve')
                    for c in range(16):
                        P.op('dve', lambda e, c=c: e.max(out=m[:, c, 8:16], in_=wk[:, c, :]), reads=[('wk', c)], writes=[('m1', c)])
                    P.fence('dve')
                    for c in range(16):
                        P.op('dve', lambda e, c=c: e.max_index(out=ix[:, c, 8:16], in_max=m[:, c, 8:16], in_values=wk[:, c, :]),
                             reads=[('wk', c), ('m1', c)], writes=[('ix1', c)])
                    P.fence('dve')
                    mres = [('m0', c) for c in range(16)] + [('m1', c) for c in range(16)]
                    ixres = [('ix0', c) for c in range(16)] + [('ix1', c) for c in range(16)]
                    P.op('dve', lambda e: e.tensor_copy(out=ixf[:], in_=ix[:]), reads=ixres, writes=['ixf'])
                    m4 = m[:].rearrange("p (h two) k -> p h two k", two=2)
                    i4 = ixf[:].rearrange("p (h two) k -> p h two k", two=2)
                    c4 = cand[:].rearrange("p h (a b) -> p h a b", a=16)
                    ci4 = candi[:].rearrange("p h (a b) -> p h a b", a=16)
                    P.op('dve', lambda e: e.tensor_tensor(out=c4, in0=m4[:, :, 0, :].unsqueeze(3).to_broadcast([128, 8, 16, 16]),
                                                          in1=m4[:, :, 1, :].unsqueeze(2).to_broadcast([128, 8, 16, 16]), op=ALU.add),
                         reads=mres, writes=['cand'])
                    P.op('dve', lambda e: e.tensor_scalar(out=i4[:, :, 0, :], in0=i4[:, :, 0, :], scalar1=128.0, scalar2=None, op0=ALU.mult),
                         reads=['ixf'], writes=['ixf'])
                    P.op('dve', lambda e: e.tensor_tensor(out=ci4, in0=i4[:, :, 0, :].unsqueeze(3).to_broadcast([128, 8, 16, 16]),
                                                          in1=i4[:, :, 1, :].unsqueeze(2).to_broadcast([128, 8, 16, 16]), op=ALU.add),
                         reads=['ixf'], writes=['candi'])
                    for h in range(8):
                        P.op('dve', lambda e, h=h: e.max(out=ts[:, h, 0:8], in_=cand[:, h, :]), reads=['cand'], writes=[('ts0', h)])
                    P.fence('dve')
                    for h in range(8):
                        P.op('dve', lambda e, h=h: e.match_replace(out=wk2[:, h, :], in_to_replace=ts[:, h, 0:8], in_values=cand[:, h, :], imm_value=-1e30),
                             reads=['cand', ('ts0', h)], writes=[('wk2', h)])
                    P.fence('dve')
                    for h in range(8):
                        P.op('dve', lambda e, h=h: e.max(out=ts[:, h, 8:16], in_=wk2[:, h, :]), reads=[('wk2', h)], writes=[('ts1', h)])
                    P.fence('dve')
                    tsres = [('ts0', h) for h in range(8)] + [('ts1', h) for h in range(8)]
                    for h in range(8):
                        for k in range(16):
                            P.op('dve', lambda e, h=h, k=k: e.scalar_tensor_tensor(out=junk[(h * 16 + k) % 2][:], in0=cand[:, h, :], scalar=ts[:, h, k:k + 1], in1=candi[:, h, :],
                                                                                   op0=ALU.is_equal, op1=ALU.mult, accum_out=ef[:, h * 16 + k:h * 16 + k + 1]),
                                 reads=['cand', 'candi', ('ts0', h), ('ts1', h)], writes=[('ef', h * 16 + k)])
                    P.fence('dve')
                    eib = ei[tix % 2]
                    rei = f'ei{tix % 2}'
                    gtb = gt[tix % 2]
                    rgt = f'gt{tix % 2}'
                    P.op('dve', lambda e: e.tensor_scalar(out=ef[:], in0=ef[:], scalar1=float(NEXP - 1), scalar2=float(L * NEXP), op0=ALU.min, op1=ALU.add),
                         reads=[('ef', q) for q in range(128)], writes=['ef'])
                    P.op('dve', lambda e, eib=eib: e.tensor_copy(out=eib[:], in_=ef[:]), reads=['ef'], writes=[rei])
                    P.op('dve', lambda e, gtb=gtb: e.tensor_tensor(out=gtb[:], in0=ts[:], in1=ts[:, :, 0:1].to_broadcast([128, 8, 16]), op=ALU.subtract),
                         reads=tsres, writes=[rgt])
                    P.op('act', lambda e, gtb=gtb: e.activation(out=gtb[:], in_=gtb[:], func=AF.Exp), reads=[rgt], writes=[rgt])
                    P.op('dve', lambda e, gtb=gtb: e.tensor_reduce(out=gsum[:], in_=gtb[:], axis=AX.X, op=ALU.add), reads=[rgt], writes=['gsum'])
                    P.op('dve', lambda e: e.reciprocal(out=gsum[:], in_=gsum[:]), reads=['gsum'], writes=['gsum'])
                    P.op('dve', lambda e, gtb=gtb: e.tensor_tensor(out=gtb[:], in0=gtb[:], in1=gsum[:].unsqueeze(2).to_broadcast([128, 8, 16]), op=ALU.mult),
                         reads=[rgt, 'gsum'], writes=[rgt])
                    P.op('sp', lambda e, eib=eib, tix=tix: e.dma_start(out=A['IDX'][tix * 128:(tix + 1) * 128, :], in_=eib[:]),
                         reads=[rei], writes=[('IDX', tix)], dma=True)
                    P.op('sp', lambda e, gtb=gtb, tix=tix: e.dma_start(out=A['GATE'][tix * 128:(tix + 1) * 128, :], in_=gtb[:].rearrange("p h k -> p (h k)")),
                         reads=[rgt], writes=[('GATE', tix)], dma=True)

    def emit_peer2(self, xin, dst, L):
        P, nc, A, ps = self.P, self.nc, self.A, self.ps
        NB = self.cfg.get('nb', 14)
        U = A['peer_u'].rearrange("l e d -> (l e) d")
        V = A['peer_v'].rearrange("l e d -> (l e) d")
        with Stage(self, f'p2{L}') as st:
            xts = [st.T(f'xt{i}', [128, D]) for i in range(2)]
            hts = [st.T(f'ht{i}', [128, D]) for i in range(2)]
            eis = [st.T(f'ei{i}', [128, 128], I32) for i in range(2)]
            gts = [st.T(f'gt{i}', [128, 128]) for i in range(2)]
            ub = [st.T(f'ub{i}', [128, D]) for i in range(NB)]
            vb = [st.T(f'vb{i}', [128, D]) for i in range(NB)]
            junk = st.T('junk', [128, D])
            apre = st.T('apre', [128, 128])
            coef = st.T('coef', [128, 128])
            accs = [st.T(f'acc{i}', [128, D]) for i in range(2)]
            nu = nv = 0
            for ti in range(NT):
                b = ti % 2
                xt, ht, eib, gtb, acc = xts[b], hts[b], eis[b], gts[b], accs[b]
                rx, rh, rei, rgt, racc = f'xt{b}', f'ht{b}', f'ei{b}', f'gt{b}', f'acc{b}'
                P.op('sp', lambda e, xt=xt, ti=ti: e.dma_start(out=xt[:], in_=xin[ti * 128:(ti + 1) * 128, :]), writes=[rx], dma=True)
                P.op('sp', lambda e, eib=eib, ti=ti: e.dma_start(out=eib[:], in_=A['IDX'][ti * 128:(ti + 1) * 128, :]),
                     reads=[('IDX', ti)], writes=[rei], dma=True)
                P.op('sp', lambda e, gtb=gtb, ti=ti: e.dma_start(out=gtb[:], in_=A['GATE'][ti * 128:(ti + 1) * 128, :]),
                     reads=[('GATE', ti)], writes=[rgt], dma=True)
                self.modulate(xt[:], ht[:], rx, rh)
                for k in range(128):
                    s = nu % NB
                    nu += 1
                    P.op('pool', lambda e, s=s, k=k, eib=eib: e.indirect_dma_start(
                        out=ub[s][:], out_offset=None, in_=U,
                        in_offset=bass.IndirectOffsetOnAxis(ap=eib[:, k:k + 1], axis=0)),
                        reads=[rei], writes=[('ub', s)], dma=True)
                    P.op('dve', lambda e, s=s, k=k, ht=ht: e.scalar_tensor_tensor(out=junk[:], in0=ub[s][:], scalar=1.0, in1=ht[:], op0=ALU.mult, op1=ALU.mult,
                                                                               accum_out=apre[:, k:k + 1]),
                         reads=[('ub', s), rh], writes=['junk', 'apre'])
                P.op('act', lambda e: e.activation(out=coef[:], in_=apre[:], func=AF.Gelu), reads=['apre'], writes=['coef'])
                P.op('dve', lambda e, gtb=gtb: e.tensor_tensor(out=coef[:], in0=coef[:], in1=gtb[:], op=ALU.mult), reads=['coef', rgt], writes=['coef'])
                for k in range(128):
                    s = nv % NB
                    nv += 1
                    P.op('pool', lambda e, s=s, k=k, eib=eib: e.indirect_dma_start(
                        out=vb[s][:], out_offset=None, in_=V,
                        in_offset=bass.IndirectOffsetOnAxis(ap=eib[:, k:k + 1], axis=0)),
                        reads=[rei], writes=[('vb', s)], dma=True)
                    if k == 0:
                        P.op('dve', lambda e, s=s, acc=acc: e.tensor_scalar(out=acc[:], in0=vb[s][:], scalar1=coef[:, 0:1], scalar2=None, op0=ALU.mult),
                             reads=[('vb', s), 'coef'], writes=[racc])
                    else:
                        P.op('dve', lambda e, s=s, k=k, acc=acc: e.scalar_tensor_tensor(out=acc[:], in0=vb[s][:], scalar=coef[:, k:k + 1], in1=acc[:],
                                                                                      op0=ALU.mult, op1=ALU.add),
                             reads=[('vb', s), 'coef', racc], writes=[racc])
                P.op('pool', lambda e, acc=acc: e.tensor_tensor(out=acc[:], in0=acc[:], in1=self.gate_bc[:], op=ALU.mult), reads=[racc, 'gate_bc'], writes=[racc])
                P.op('dve', lambda e, acc=acc, xt=xt: e.scalar_tensor_tensor(out=acc[:], in0=xt[:], scalar=ALPHA, in1=acc[:], op0=ALU.mult, op1=ALU.add),
                     reads=[rx, racc], writes=[racc])
                self.layernorm_inplace(acc[:], racc)
                P.op('sp', lambda e, acc=acc, ti=ti: e.dma_start(out=dst[ti * 128:(ti + 1) * 128, :], in_=acc[:]), reads=[racc], dma=True)

    def emit_peer1a(self, xin, L):
        P, nc, A, ps = self.P, self.nc, self.A, self.ps
        with Stage(self, f'pa{L}') as st:
            wq = st.T('wq', [128, 8, 2048])
            skT = st.T('skT', [128, 2, 128])
            xts = [st.T(f'xt{i}', [128, D]) for i in range(2)]
            hts = [st.T(f'ht{i}', [128, D]) for i in range(2)]
            hT = st.T('hT', [128, 8, 256])
            qT = st.T('qT', [128, 16, 256])
            sco = [st.T(f'sco{i}', [128, 2048]) for i in range(2)]
            for q in range(4):
                P.op('sp', lambda e, q=q: e.dma_start(out=wq[:, :, q * 512:(q + 1) * 512],
                                                      in_=A['peer_query_w'][L][:, q * 512:(q + 1) * 512].rearrange("(k p) n -> p k n", p=128)),
                     writes=[('wq', q)], dma=True)
            P.op('sp', lambda e: e.dma_start(out=skT[:], in_=A['peer_skT'][L].rearrange("h d k -> d h k")), writes=['skT'], dma=True)
            ti = 0
            for jb in range(S // 256):
                for tl in range(2):
                    xt, ht = xts[ti % 2], hts[ti % 2]
                    rx, rh = f'xt{ti % 2}', f'ht{ti % 2}'
                    P.op('sp', lambda e, xt=xt, ti=ti: e.dma_start(out=xt[:], in_=xin[ti * 128:(ti + 1) * 128, :]), writes=[rx], dma=True)
                    self.modulate(xt[:], ht[:], rx, rh)
                    P.op('sp', lambda e, ht=ht, ti=ti: e.dma_start(out=A['H'][ti * 128:(ti + 1) * 128, :], in_=ht[:]), reads=[rh], writes=[('H', ti)], dma=True)
                    self.transpose8(ht, rh, hT, 'hT', tl * 128, 0)
                    ti += 1
                for c in range(16):
                    bank = ps[2 + c % 2]
                    rb = f'ps{2 + c % 2}'
                    for k in range(8):
                        P.op('pe', lambda e, k=k, c=c, bank=bank: e.matmul(out=bank[:, 0:256], lhsT=wq[:, k, c * 128:(c + 1) * 128], rhs=hT[:, k, :],
                                                                          start=(k == 0), stop=(k == 7)),
                             reads=[('wq', c // 4), 'hT'], writes=[rb])
                    if c % 2 == 0:
                        P.op('act', lambda e, c=c, bank=bank: e.copy(out=qT[:, c, :], in_=bank[:, 0:256]), reads=[rb], writes=[('qT', c)])
                    else:
                        P.op('dve', lambda e, c=c, bank=bank: e.tensor_copy(out=qT[:, c, :], in_=bank[:, 0:256]), reads=[rb], writes=[('qT', c)])
                for tl in range(2):
                    tix = jb * 2 + tl
                    so = sco[tix % 2]
                    rso = f'sco{tix % 2}'
                    for c in range(16):
                        bank = ps[4 + c // 4]
                        rb = f'ps{4 + c // 4}'
                        P.op('pe', lambda e, c=c, bank=bank, tl=tl: e.matmul(out=bank[:, (c % 4) * 128:(c % 4 + 1) * 128],
                                                                            lhsT=qT[:, c, tl * 128:(tl + 1) * 128], rhs=skT[:, c % 2, :],
                                                                            start=True, stop=True),
                             reads=[('qT', c), 'skT'], writes=[rb])
                    for g4 in range(4):
                        eng = ('act', 'dve')[g4 % 2]
                        if eng == 'act':
                            P.op('act', lambda e, g4=g4, so=so: e.copy(out=so[:, g4 * 512:(g4 + 1) * 512], in_=ps[4 + g4][:]), reads=[f'ps{4 + g4}'], writes=[rso])
                        else:
                            P.op('dve', lambda e, g4=g4, so=so: e.tensor_copy(out=so[:, g4 * 512:(g4 + 1) * 512], in_=ps[4 + g4][:]), reads=[f'ps{4 + g4}'], writes=[rso])
                    P.op('sp', lambda e, so=so, tix=tix: e.dma_start(out=A['SCR'][tix * 128:(tix + 1) * 128, :], in_=so[:]), reads=[rso], writes=[('SCR', tix)], dma=True)

    def emit_peer2f(self, xin, dst, L):
        P, nc, A, ps = self.P, self.nc, self.A, self.ps
        NB = self.cfg.get('nb', 11)
        U = A['peer_u'].rearrange("l e d -> (l e) d")
        V = A['peer_v'].rearrange("l e d -> (l e) d")
        with Stage(self, f'pf{L}') as st:
            scs = [st.T(f'sc{i}', [128, 16, 128]) for i in range(2)]
            m = st.T('m', [128, 16, 16])
            ix = st.T('ix', [128, 16, 16], U32)
            ixf = st.T('ixf', [128, 16, 16])
            wk = st.T('wk', [128, 16, 128])
            cand = st.T('cand', [128, 8, 256])
            candi = st.T('candi', [128, 8, 256])
            wk2 = st.T('wk2', [128, 8, 256])
            junk2 = [st.T(f'junk2{i}', [128, 256]) for i in range(2)]
            ts = st.T('ts', [128, 8, 16])
            ef = st.T('ef', [128, 128])
            eis = [st.T(f'ei{i}', [128, 128], I32) for i in range(2)]
            gts = [st.T(f'gt{i}', [128, 8, 16]) for i in range(2)]
            gsum = st.T('gsum', [128, 8])
            xts = [st.T(f'xt{i}', [128, D]) for i in range(2)]
            hts = [st.T(f'ht{i}', [128, D]) for i in range(2)]
            ub = [st.T(f'ub{i}', [128, D]) for i in range(NB)]
            vb = [st.T(f'vb{i}', [128, D]) for i in range(NB)]
            junk = st.T('junk', [128, D])
            apre = st.T('apre', [128, 128])
            coef = st.T('coef', [128, 128])
            accs = [st.T(f'acc{i}', [128, D]) for i in range(2)]

            def load_sc(t):
                P.op('sp', lambda e, t=t: e.dma_start(out=scs[t % 2][:].rearrange("p a k -> p (a k)"), in_=A['SCR'][t * 128:(t + 1) * 128, :]),
                     writes=[f'sc{t % 2}'], dma=True)

            def load_xh(t):
                P.op('sp', lambda e, t=t: e.dma_start(out=xts[t % 2][:], in_=xin[t * 128:(t + 1) * 128, :]), writes=[f'xt{t % 2}'], dma=True)
                P.op('sp', lambda e, t=t: e.dma_start(out=hts[t % 2][:], in_=A['H'][t * 128:(t + 1) * 128, :]), writes=[f'ht{t % 2}'], dma=True)

            def topk_gen(t):
                sc = scs[t % 2]
                rsc = f'sc{t % 2}'
                eib, gtb = eis[t % 2], gts[t % 2]
                rei, rgt = f'ei{t % 2}', f'gt{t % 2}'
                for c in range(16):
                    P.op('dve', lambda e, c=c: e.max(out=m[:, c, 0:8], in_=sc[:, c, :]), reads=[rsc], writes=[('m0', c)])
                    yield
                P.fence('dve')
                for c in range(16):
                    P.op('dve', lambda e, c=c: e.max_index(out=ix[:, c, 0:8], in_max=m[:, c, 0:8], in_values=sc[:, c, :]),
                         reads=[rsc, ('m0', c)], writes=[('ix0', c)])
                    yield
                    P.op('dve', lambda e, c=c: e.match_replace(out=wk[:, c, :], in_to_replace=m[:, c, 0:8], in_values=sc[:, c, :], imm_value=-1e30),
                         reads=[rsc, ('m0', c)], writes=[('wk', c)])
                    yield
                P.fence('dve')
                for c in range(16):
                    P.op('dve', lambda e, c=c: e.max(out=m[:, c, 8:16], in_=wk[:, c, :]), reads=[('wk', c)], writes=[('m1', c)])
                    yield
                P.fence('dve')
                for c in range(16):
                    P.op('dve', lambda e, c=c: e.max_index(out=ix[:, c, 8:16], in_max=m[:, c, 8:16], in_values=wk[:, c, :]),
                         reads=[('wk', c), ('m1', c)], writes=[('ix1', c)])
                    yield
                P.fence('dve')
                mres = [('m0', c) for c in range(16)] + [('m1', c) for c in range(16)]
                ixres = [('ix0', c) for c in range(16)] + [('ix1', c) for c in range(16)]
                P.op('dve', lambda e: e.tensor_copy(out=ixf[:], in_=ix[:]), reads=ixres, writes=['ixf'])
                yield
                m4 = m[:].rearrange("p (h two) k -> p h two k", two=2)
                i4 = ixf[:].rearrange("p (h two) k -> p h two k", two=2)
                c4 = cand[:].rearrange("p h (a b) -> p h a b", a=16)
                ci4 = candi[:].rearrange("p h (a b) -> p h a b", a=16)
                P.op('dve', lambda e: e.tensor_tensor(out=c4, in0=m4[:, :, 0, :].unsqueeze(3).to_broadcast([128, 8, 16, 16]),
                                                      in1=m4[:, :, 1, :].unsqueeze(2).to_broadcast([128, 8, 16, 16]), op=ALU.add),
                     reads=mres, writes=['cand'])
                yield
                P.op('dve', lambda e: e.tensor_scalar(out=i4[:, :, 0, :], in0=i4[:, :, 0, :], scalar1=128.0, scalar2=None, op0=ALU.mult),
                     reads=['ixf'], writes=['ixf'])
                yield
                P.op('dve', lambda e: e.tensor_tensor(out=ci4, in0=i4[:, :, 0, :].unsqueeze(3).to_broadcast([128, 8, 16, 16]),
                                                      in1=i4[:, :, 1, :].unsqueeze(2).to_broadcast([128, 8, 16, 16]), op=ALU.add),
                     reads=['ixf'], writes=['candi'])
                yield
                for h in range(8):
                    P.op('dve', lambda e, h=h: e.max(out=ts[:, h, 0:8], in_=cand[:, h, :]), reads=['cand'], writes=[('ts0', h)])
                    yield
                P.fence('dve')
                for h in range(8):
                    P.op('dve', lambda e, h=h: e.match_replace(out=wk2[:, h, :], in_to_replace=ts[:, h, 0:8], in_values=cand[:, h, :], imm_value=-1e30),
                         reads=['cand', ('ts0', h)], writes=[('wk2', h)])
                    yield
                P.fence('dve')
                for h in range(8):
                    P.op('dve', lambda e, h=h: e.max(out=ts[:, h, 8:16], in_=wk2[:, h, :]), reads=[('wk2', h)], writes=[('ts1', h)])
                    yield
                P.fence('dve')
                tsres = [('ts0', h) for h in range(8)] + [('ts1', h) for h in range(8)]
                for h in range(8):
                    for k in range(16):
                        P.op('dve', lambda e, h=h, k=k: e.scalar_tensor_tensor(out=junk2[(h * 16 + k) % 2][:], in0=cand[:, h, :], scalar=ts[:, h, k:k + 1], in1=candi[:, h, :],
                                                                               op0=ALU.is_equal, op1=ALU.mult, accum_out=ef[:, h * 16 + k:h * 16 + k + 1]),
                             reads=['cand', 'candi', ('ts0', h), ('ts1', h)], writes=[('ef', h * 16 + k)])
                        yield
                P.fence('dve')
                P.op('dve', lambda e: e.tensor_scalar(out=ef[:], in0=ef[:], scalar1=float(NEXP - 1), scalar2=float(L * NEXP), op0=ALU.min, op1=ALU.add),
                     reads=[('ef', q) for q in range(128)], writes=['ef'])
                yield
                P.op('dve', lambda e: e.tensor_copy(out=eib[:], in_=ef[:]), reads=['ef'], writes=[rei])
                yield
                P.op('dve', lambda e: e.tensor_tensor(out=gtb[:], in0=ts[:], in1=ts[:, :, 0:1].to_broadcast([128, 8, 16]), op=ALU.subtract),
                     reads=tsres, writes=[rgt])
                yield
                P.op('act', lambda e: e.activation(out=gtb[:], in_=gtb[:], func=AF.Exp), reads=[rgt], writes=[rgt])
                P.op('dve', lambda e: e.tensor_reduce(out=gsum[:], in_=gtb[:], axis=AX.X, op=ALU.add), reads=[rgt], writes=['gsum'])
                yield
                P.op('dve', lambda e: e.reciprocal(out=gsum[:], in_=gsum[:]), reads=['gsum'], writes=['gsum'])
                yield
                P.op('dve', lambda e: e.tensor_tensor(out=gtb[:], in0=gtb[:], in1=gsum[:].unsqueeze(2).to_broadcast([128, 8, 16]), op=ALU.mult),
                     reads=[rgt, 'gsum'], writes=[rgt])
                yield

            def step(gen, n=1):
                if gen is None:
                    return None
                try:
                    for _ in range(n):
                        next(gen)
                except StopIteration:
                    return None
                return gen

            load_sc(0)
            load_sc(1)
            load_xh(0)
            g0 = topk_gen(0)
            while g0 is not None:
                g0 = step(g0, 64)
            nu = nv = 0
            for ti in range(NT):
                b = ti % 2
                xt, ht, eib, gtb, acc = xts[b], hts[b], eis[b], gts[b], accs[b]
                rx, rh, rei, rgt, racc = f'xt{b}', f'ht{b}', f'ei{b}', f'gt{b}', f'acc{b}'
                gt2 = gtb[:].rearrange("p h k -> p (h k)")
                if ti + 1 < NT:
                    load_xh(ti + 1)
                gen = topk_gen(ti + 1) if ti + 1 < NT else None
                for k in range(128):
                    s_ = nu % NB
                    nu += 1
                    P.op('pool', lambda e, s_=s_, k=k, eib=eib: e.indirect_dma_start(
                        out=ub[s_][:], out_offset=None, in_=U,
                        in_offset=bass.IndirectOffsetOnAxis(ap=eib[:, k:k + 1], axis=0)),
                        reads=[rei], writes=[('ub', s_)], dma=True)
                    P.op('dve', lambda e, s_=s_, k=k, ht=ht: e.scalar_tensor_tensor(out=junk[:], in0=ub[s_][:], scalar=1.0, in1=ht[:], op0=ALU.mult, op1=ALU.mult,
                                                                                 accum_out=apre[:, k:k + 1]),
                         reads=[('ub', s_), rh], writes=[('apre', k)])
                    gen = step(gen)
                P.op('act', lambda e: e.activation(out=coef[:], in_=apre[:], func=AF.Gelu), reads=[('apre', k) for k in range(128)], writes=['coef'])
                P.op('dve', lambda e, gt2=gt2: e.tensor_tensor(out=coef[:], in0=coef[:], in1=gt2, op=ALU.mult), reads=['coef', rgt], writes=['coef'])
                for k in range(128):
                    s_ = nv % NB
                    nv += 1
                    P.op('pool', lambda e, s_=s_, k=k, eib=eib: e.indirect_dma_start(
                        out=vb[s_][:], out_offset=None, in_=V,
                        in_offset=bass.IndirectOffsetOnAxis(ap=eib[:, k:k + 1], axis=0)),
                        reads=[rei], writes=[('vb', s_)], dma=True)
                    if k == 0:
                        P.op('dve', lambda e, s_=s_, acc=acc: e.tensor_scalar(out=acc[:], in0=vb[s_][:], scalar1=coef[:, 0:1], scalar2=None, op0=ALU.mult),
                             reads=[('vb', s_), 'coef'], writes=[racc])
                    else:
                        P.op('dve', lambda e, s_=s_, k=k, acc=acc: e.scalar_tensor_tensor(out=acc[:], in0=vb[s_][:], scalar=coef[:, k:k + 1], in1=acc[:],
                                                                                       op0=ALU.mult, op1=ALU.add),
                             reads=[('vb', s_), 'coef', racc], writes=[racc])
                    gen = step(gen)
                while gen is not None:
                    gen = step(gen, 64)
                if ti + 2 < NT:
                    load_sc(ti + 2)
                P.op('dve', lambda e, acc=acc: e.tensor_tensor(out=acc[:], in0=acc[:], in1=self.gate_bc[:], op=ALU.mult), reads=[racc, 'gate_bc'], writes=[racc])
                P.op('dve', lambda e, acc=acc, xt=xt: e.scalar_tensor_tensor(out=acc[:], in0=xt[:], scalar=ALPHA, in1=acc[:], op0=ALU.mult, op1=ALU.add),
                     reads=[rx, racc], writes=[racc])
                self.layernorm_inplace(acc[:], racc, gb='dve')
                P.op('sp', lambda e, acc=acc, ti=ti: e.dma_start(out=dst[ti * 128:(ti + 1) * 128, :], in_=acc[:]), reads=[racc], dma=True)

    def emit_attn1(self, xin):
        P, nc, A, ps = self.P, self.nc, self.A, self.ps
        ones = self.ones
        NCOL = 3 * D + 16
        with Stage(self, 'a1') as st:
            winr = st.T('winr', [128, 8, 3 * D], F32R)
            wstg = [st.T('wstg0', [128, 8, 512])] * 2
            wf = st.T('wf', [128, 8, 16])
            qkb = st.T('qkb', [128, 16])
            vbr = st.T('vbr', [1, D])
            vb_bc = st.T('vb_bc', [128, D])
            fb = st.T('fb', [16, 1])
            xts = [st.T(f'xt{i}', [128, D]) for i in range(2)]
            ht = st.T('ht', [128, D])
            hT = st.T('hT', [128, 8, 512], F32R)
            qko = [st.T(f'qko{i}', [128, 512]) for i in range(2)]
            vo = [st.T(f'vo{i}', [128, D]) for i in range(2)]
            Fcb = [st.T(f'Fcb{i}', [16, 512]) for i in range(2)]
            Frb = st.T('Frb', [16, 512], F32R)
            Flb = st.T('Flb', [16, 512])
            nFr = st.T('nFr', [16, 512])
            nFl = st.T('nFl', [16, 512])
            spt = st.T('spt', [16, 512])
            o16 = st.T('o16', [16, 512])
            for q in range(6):
                wb = wstg[0]
                rw = 'wstg0'
                P.op('sp', lambda e, q=q, wb=wb: e.dma_start(out=wb[:], in_=A['attn_in_w'][:, q * 512:(q + 1) * 512].rearrange("(k p) n -> p k n", p=128)),
                     writes=[rw], dma=True)
                eng = ('dve', 'pool')[q % 2]
                P.op(eng, lambda e, q=q, wb=wb: e.tensor_copy(out=winr[:, :, q * 512:(q + 1) * 512], in_=wb[:]), reads=[rw], writes=[('win', q)])
            P.op('sp', lambda e: e.dma_start(out=wf[:], in_=A['attn_in_w'][:, 3 * D:NCOL].rearrange("(k p) n -> p k n", p=128)),
                 writes=['wf'], dma=True)
            P.op('sp', lambda e: e.dma_start(out=qkb[:], in_=A['attn_qkb_l']), writes=['qkb'], dma=True)
            P.op('sp', lambda e: e.dma_start(out=vbr[:], in_=A['attn_vb']), writes=['vbr'], dma=True)
            P.op('sp', lambda e: e.dma_start(out=fb[:], in_=A['attn_fb']), writes=['fb'], dma=True)
            P.op('dve', lambda e: e.tensor_scalar(out=qkb[:, 0:8], in0=qkb[:, 0:8], scalar1=0.125, scalar2=None, op0=ALU.mult), reads=['qkb'], writes=['qkb'])
            P.op('dve', lambda e: e.tensor_scalar(out=fb[:], in0=fb[:], scalar1=-1.0, scalar2=None, op0=ALU.mult), reads=['fb'], writes=['fb'])
            P.op('pool', lambda e: e.memset(o16[:], 1.0), writes=['o16'])
            for half in range(2):
                P.op('pe', lambda e, half=half: e.matmul(out=ps[4 + half][:], lhsT=ones[0:1, :], rhs=vbr[0:1, half * 512:(half + 1) * 512], start=True, stop=True),
                     reads=['ones', 'vbr'], writes=[f'ps{4 + half}'])
                P.op('act', lambda e, half=half: e.copy(out=vb_bc[:, half * 512:(half + 1) * 512], in_=ps[4 + half][:]), reads=[f'ps{4 + half}'], writes=['vb_bc'])
            ti = 0
            for jb in range(8):
                cols = slice(jb * 512, (jb + 1) * 512)
                for tl in range(4):
                    xt = xts[ti % 2]
                    rx = f'xt{ti % 2}'
                    P.op('sp', lambda e, xt=xt, ti=ti: e.dma_start(out=xt[:], in_=xin[ti * 128:(ti + 1) * 128, :]), writes=[rx], dma=True)
                    self.modulate(xt[:], ht[:], rx, 'ht')
                    self.transpose8(ht, 'ht', hT, 'hT', tl * 128, 0)
                    ti += 1
                for c in range(16):
                    bank = ps[2 + c % 2]
                    rb = f'ps{2 + c % 2}'
                    for k in range(8):
                        P.op('pe', lambda e, k=k, c=c, bank=bank: e.matmul(out=bank[:], lhsT=winr[:, k, c * 128:(c + 1) * 128], rhs=hT[:, k, :],
                                                                          start=(k == 0), stop=(k == 7)),
                             reads=[('win', c // 4), 'hT'], writes=[rb])
                    ob = qko[c % 2]
                    rob = f'qko{c % 2}'
                    P.op('act', lambda e, c=c, bank=bank, ob=ob: e.activation(out=ob[:], in_=bank[:], func=AF.Identity, bias=qkb[:, c:c + 1],
                                                                             scale=(0.125 if c < 8 else 1.0)),
                         reads=[rb, 'qkb'], writes=[rob])
                    dstt = A['QA'] if c < 8 else A['KA']
                    for hh in range(2):
                        head = (c % 8) * 2 + hh
                        P.op('sp', lambda e, ob=ob, hh=hh, head=head, dstt=dstt: e.dma_start(out=dstt[head, 0:64, cols], in_=ob[hh * 64:(hh + 1) * 64, :]),
                             reads=[rob], writes=[('QK', c, hh)], dma=True)
                for tl in range(4):
                    tix = jb * 4 + tl
                    vt = vo[tix % 2]
                    rv = f'vo{tix % 2}'
                    for half in range(2):
                        bank = ps[4 + half]
                        rb = f'ps{4 + half}'
                        for k in range(8):
                            P.op('pe', lambda e, k=k, bank=bank, half=half, tl=tl: e.matmul(out=bank[:], lhsT=hT[:, k, tl * 128:(tl + 1) * 128],
                                                                                        rhs=winr[:, k, 2 * D + half * 512:2 * D + (half + 1) * 512],
                                                                                        start=(k == 0), stop=(k == 7)),
                                 reads=['hT', ('win', 4 + half)], writes=[rb])
                        P.op('dve', lambda e, vt=vt, bank=bank, half=half: e.tensor_tensor(out=vt[:, half * 512:(half + 1) * 512], in0=bank[:],
                                                                                      in1=vb_bc[:, half * 512:(half + 1) * 512], op=ALU.add),
                             reads=[rb, 'vb_bc'], writes=[rv])
                    P.op('sp', lambda e, vt=vt, tix=tix: e.dma_start(out=A['V'][tix * 128:(tix + 1) * 128, :], in_=vt[:]), reads=[rv], writes=[('V', tix)], dma=True)
                for k in range(8):
                    P.op('pe', lambda e, k=k: e.matmul(out=ps[6][0:16, :], lhsT=wf[:, k, :], rhs=hT[:, k, :].bitcast(F32), start=(k == 0), stop=(k == 7)),
                         reads=['wf', 'hT'], writes=['ps6'])
                P.op('act', lambda e: e.activation(out=spt[:], in_=ps[6][0:16, :], func=AF.Exp, bias=fb[:, 0:1], scale=-1.0), reads=['ps6', 'fb'], writes=['spt'])
                P.op('act', lambda e: e.activation(out=spt[:], in_=spt[:], func=AF.Ln, bias=1.0, scale=1.0), reads=['spt'], writes=['spt'])
                P.op('dve', lambda e: e.tensor_scalar(out=spt[:], in0=spt[:], scalar1=-1.0, scalar2=None, op0=ALU.mult), reads=['spt'], writes=['spt'])
                Fc = Fcb[jb % 2]
                rF = f'Fcb{jb % 2}'
                init = 0.0 if jb == 0 else Fcb[(jb - 1) % 2][:, 511:512]
                P.op('dve', lambda e, init=init, Fc=Fc: e.tensor_tensor_scan(out=Fc[:], data0=o16[:], data1=spt[:], initial=init,
                                                                             op0=ALU.mult, op1=ALU.add),
                     reads=['o16', 'spt', f'Fcb{(jb - 1) % 2}'], writes=[rF])
                P.op('dve', lambda e, Fc=Fc: e.tensor_copy(out=Frb[:], in_=Fc[:]), reads=[rF], writes=['Frb'])
                P.op('dve', lambda e, Fc=Fc: e.tensor_tensor(out=Flb[:], in0=Fc[:], in1=Frb[:].bitcast(F32), op=ALU.subtract), reads=[rF, 'Frb'], writes=['Flb'])
                P.op('dve', lambda e: e.tensor_scalar(out=nFr[:], in0=Frb[:].bitcast(F32), scalar1=-1.0, scalar2=None, op0=ALU.mult), reads=['Frb'], writes=['nFr'])
                P.op('dve', lambda e: e.tensor_scalar(out=nFl[:], in0=Flb[:], scalar1=-1.0, scalar2=None, op0=ALU.mult), reads=['Flb'], writes=['nFl'])
                P.op('sp', lambda e: e.dma_start(out=A['QA'][:, 64, cols], in_=Frb[:].bitcast(F32)), reads=['Frb'], writes=[('QAf', jb)], dma=True)
                P.op('sp', lambda e: e.dma_start(out=A['QA'][:, 65, cols], in_=Flb[:]), reads=['Flb'], writes=[('QAl', jb)], dma=True)
                P.op('sp', lambda e: e.dma_start(out=A['KA'][:, 66, cols], in_=nFr[:]), reads=['nFr'], writes=[('KAf', jb)], dma=True)
                P.op('sp', lambda e: e.dma_start(out=A['KA'][:, 67, cols], in_=nFl[:]), reads=['nFl'], writes=[('KAl', jb)], dma=True)
                for r in (66, 67):
                    P.op('sp', lambda e, r=r: e.dma_start(out=A['QA'][:, r, cols], in_=o16[:]), reads=['o16'], writes=[('QAo', r, jb)], dma=True)
                for r in (64, 65):
                    P.op('sp', lambda e, r=r: e.dma_start(out=A['KA'][:, r, cols], in_=o16[:]), reads=['o16'], writes=[('KAo', r, jb)], dma=True)

    def emit_attn2(self):
        P, nc, A, ps = self.P, self.nc, self.A, self.ps
        NR = 68
        with Stage(self, 'a2') as st:
            qst = st.T('qst', [NR, S])
            kst = st.T('kst', [NR, S])
            vst = st.T('vst', [128, 32, 64])
            QAh = [st.T(f'QAh{i}', [NR, S], F32R) for i in range(2)]
            KAh = [st.T(f'KAh{i}', [NR, S], F32R) for i in range(2)]
            Vh = [st.T(f'Vh{i}', [128, 32, 128], F32R) for i in range(2)]
            ones_r = st.T('ones_r', [128, 128], F32R)
            pt = [st.T(f'pt{i}', [128, 512], F32R) for i in range(3)]
            lm = [st.T(f'lm{i}', [128, 512]) for i in range(2)]
            mask = st.T('mask', [128, 4, 512])
            rzt = st.T('rzt', [64, 512])
            oT = [st.T(f'oT{i}', [64, 512]) for i in range(2)]
            P.op('pool', lambda e: e.memset(mask[:], 0.0), writes=['mask'])
            for i4 in range(4):
                P.op('pool', lambda e, i4=i4: e.affine_select(out=mask[:, i4, :], in_=mask[:, i4, :], pattern=[[1, 512]], compare_op=ALU.is_ge,
                                                              fill=NEG, base=-128 * i4, channel_multiplier=-1), reads=['mask'], writes=['mask'])
            P.op('pool', lambda e: e.tensor_copy(out=ones_r[:], in_=self.ones[:]), reads=['ones'], writes=['ones_r'])
            npt = 0
            nlm = 0
            nS = 0
            nO = 0

            def loads(h):
                b = h % 2
                qa, ka, vh = QAh[b], KAh[b], Vh[b]
                rq, rk, rv = f'QAh{b}', f'KAh{b}', f'Vh{b}'
                for q4 in range(4):
                    cs = slice(q4 * 1024, (q4 + 1) * 1024)
                    P.op('sp', lambda e, h=h, cs=cs: e.dma_start(out=qst[:, cs], in_=A['QA'][h, :, cs]), writes=[('qst', q4)], dma=True)
                    P.op('sp', lambda e, h=h, cs=cs: e.dma_start(out=kst[:, cs], in_=A['KA'][h, :, cs]), writes=[('kst', q4)], dma=True)
                    P.op('sp', lambda e, h=h, q4=q4: e.dma_start(
                        out=vst[:, q4 * 8:(q4 + 1) * 8, :],
                        in_=A['V'][q4 * 1024:(q4 + 1) * 1024, h * 64:(h + 1) * 64].rearrange("(i p) d -> p i d", p=128)),
                        writes=[('vst', q4)], dma=True)
                for q4 in range(4):
                    cs = slice(q4 * 1024, (q4 + 1) * 1024)
                    P.op('pool', lambda e, qa=qa, cs=cs: e.tensor_copy(out=qa[:, cs], in_=qst[:, cs]), reads=[('qst', q4)], writes=[rq])
                    P.op('pool', lambda e, ka=ka, cs=cs: e.tensor_copy(out=ka[:, cs], in_=kst[:, cs]), reads=[('kst', q4)], writes=[rk])
                    for dup in range(2):
                        P.op('pool', lambda e, vh=vh, q4=q4, dup=dup: e.tensor_copy(out=vh[:, q4 * 8:(q4 + 1) * 8, dup * 64:(dup + 1) * 64],
                                                                                  in_=vst[:, q4 * 8:(q4 + 1) * 8, :]),
                             reads=[('vst', q4)], writes=[rv])

            loads(0)
            NH = self.cfg.get('nheads', 16)
            steps = [(h, j, i) for h in range(NH) for j in range(8) for i in range(4 * j + 4)]

            def emit_qk(n):
                h, j, i = steps[n]
                b = h % 2
                sb = ps[n % 3]
                P.op('pe', lambda e, sb=sb, ka=KAh[b], qa=QAh[b], i=i, j=j: e.matmul(out=sb[:], lhsT=ka[:, i * 128:(i + 1) * 128], rhs=qa[:, j * 512:(j + 1) * 512],
                                                                                 start=True, stop=True),
                     reads=[f'KAh{b}', f'QAh{b}'], writes=[f'ps{n % 3}'])

            emit_qk(0)
            for n, (h, j, i) in enumerate(steps):
                b = h % 2
                vh, rv = Vh[b], f'Vh{b}'
                if j == 0 and i == 0 and h + 1 < NH:
                    loads(h + 1)
                if n + 1 < len(steps):
                    emit_qk(n + 1)
                if i == 0:
                    oset = nO % 2
                    nO += 1
                poA, poB = ps[3 + 2 * oset], ps[4 + 2 * oset]
                rA, rB = f'ps{3 + 2 * oset}', f'ps{4 + 2 * oset}'
                ot, rot = oT[oset], f'oT{oset}'
                last = 4 * j + 3
                sb, rsb = ps[n % 3], f'ps{n % 3}'
                p_, rp = pt[n % 3], f'pt{n % 3}'
                if i >= 4 * j:
                    l_ = lm[nlm % 2]
                    rl = f'lm{nlm % 2}'
                    nlm += 1
                    P.op('dve', lambda e, l_=l_, sb=sb, i=i, j=j: e.tensor_tensor(out=l_[:], in0=sb[:], in1=mask[:, i - 4 * j, :], op=ALU.add),
                         reads=[rsb, 'mask'], writes=[rl])
                    P.op('act', lambda e, p_=p_, l_=l_: e.activation(out=p_[:], in_=l_[:], func=AF.Exp), reads=[rl], writes=[rp])
                else:
                    P.op('act', lambda e, p_=p_, sb=sb: e.activation(out=p_[:], in_=sb[:], func=AF.Exp), reads=[rsb], writes=[rp])
                P.op('pe', lambda e, p_=p_, i=i, poA=poA, vh=vh, last=last: e.matmul(out=poA[:], lhsT=vh[:, i, :], rhs=p_[:], start=(i == 0), stop=(i == last)),
                     reads=[rp, rv], writes=[rA])
                P.op('pe', lambda e, p_=p_, i=i, poB=poB, last=last: e.matmul(out=poB[:], lhsT=ones_r[:], rhs=p_[:], start=(i == 0), stop=(i == last)),
                     reads=[rp, 'ones_r'], writes=[rB])
                if i == last:
                    P.op('dve', lambda e, poB=poB: e.reciprocal(out=rzt[:], in_=poB[0:64, :]), reads=[rB], writes=['rzt'])
                    P.op('dve', lambda e, poA=poA, ot=ot: e.tensor_tensor(out=ot[:], in0=poA[0:64, :], in1=rzt[:], op=ALU.mult), reads=[rA, 'rzt'], writes=[rot])
                    P.op('sp', lambda e, ot=ot, h=h, j=j: e.dma_start(out=A['AOT'][h // 2, (h % 2) * 64:(h % 2) * 64 + 64, j * 512:(j + 1) * 512], in_=ot[:]),
                         reads=[rot], writes=[('AOT', h, j)], dma=True)


def make_in_maps(inputs, cores=range(8)):
    f = lambda a: np.ascontiguousarray(np.asarray(a, dtype=np.float32))
    sh = {}
    sh['ada_mix_w'] = f(inputs['ada_mix_w'])
    sh['ada_ffn_w'] = f(inputs['ada_ffn_w'])
    amb, afb = f(inputs['ada_mix_b']), f(inputs['ada_ffn_b'])
    sh['ada_b'] = f(np.stack([amb[0], afb[0], amb[1], afb[1]]))
    g1, g2 = f(inputs['ln_mix_g']), f(inputs['ln_ffn_g'])
    b1, b2 = f(inputs['ln_mix_b']), f(inputs['ln_ffn_b'])
    sh['ln_g'] = f(np.stack([g1[0], g2[0], g1[1], g2[1]]))
    sh['ln_b'] = f(np.stack([b1[0], b2[0], b1[1], b2[1]]))
    sh['conv_in_w'] = f(inputs['conv_in_w'][0])
    sh['conv_in_b_l'] = f(np.asarray(inputs['conv_in_b'][0]).reshape(16, 128).T)
    sh['conv_dw_w_l'] = f(np.asarray(inputs['conv_dw_w'][0]).reshape(31, 8, 128).transpose(2, 1, 0))
    sh['conv_vec_l'] = f(np.stack([np.asarray(inputs[k][0]).reshape(8, 128).T for k in ('conv_dw_b', 'conv_ln_g', 'conv_ln_b')], axis=1))
    sh['conv_out_w'] = f(inputs['conv_out_w'][0])
    sh['conv_out_b'] = f(np.asarray(inputs['conv_out_b'][0]).reshape(1, D))
    sh['attn_in_w'] = f(inputs['attn_in_w'][0])
    ab = np.asarray(inputs['attn_in_b'][0])
    sh['attn_qkb_l'] = f(ab[:2 * D].reshape(16, 128).T)
    sh['attn_vb'] = f(ab[2 * D:3 * D].reshape(1, D))
    sh['attn_fb'] = f(ab[3 * D:].reshape(16, 1))
    sh['attn_out_w'] = f(inputs['attn_out_w'][0])
    sh['attn_out_b'] = f(np.asarray(inputs['attn_out_b'][0]).reshape(1, D))
    sh['peer_query_w'] = f(inputs['peer_query_w'])
    k1, k2 = np.asarray(inputs['peer_sub_keys_1']), np.asarray(inputs['peer_sub_keys_2'])
    sh['peer_skT'] = f(np.stack([np.stack([k1[l].T, k2[l].T]) for l in range(2)]))
    sh['peer_u'] = f(inputs['peer_expert_u'])
    sh['peer_v'] = f(inputs['peer_expert_v'])
    x = np.asarray(inputs['x'])
    c = np.asarray(inputs['c'])
    maps = []
    for b in cores:
        m = dict(sh)
        m['x'] = f(x[b])
        m['c_l'] = f(c[b].reshape(8, 128).T)
        maps.append(m)
    return maps


_NC_CACHE = {}


def kernel(**inputs):
    if 'full' not in _NC_CACHE:
        _NC_CACHE['full'] = Kern({}).build()
    nc = _NC_CACHE['full']
    maps = make_in_maps(inputs)
    res = run_bass_kernel_spmd(nc, maps, core_ids=list(range(8)))
    return np.stack([np.asarray(r['out'], dtype=np.float32) for r in res.results], axis=0)
```

```python
import numpy as np
from contextlib import ExitStack
import concourse.bass as bass
import concourse.mybir as mybir
from concourse.bass_utils import run_bass_kernel_spmd

F32 = mybir.dt.float32
I32 = mybir.dt.int32
U32 = mybir.dt.uint32
F32R = mybir.dt.float32r
BF16 = mybir.dt.bfloat16
ALU = mybir.AluOpType
AF = mybir.ActivationFunctionType
AX = mybir.AxisListType

S = 4096
D = 1024
NT = S // 128
ALPHA = float((2 * 2) ** 0.25)
EPS = 1e-5
NEXP = 16384
MAXV = 30000
NEG = -30000.0


class Prog:
    def __init__(self, nc, es):
        self.nc = nc
        self.es = es
        self.eng = {'pe': nc.tensor, 'dve': nc.vector, 'act': nc.scalar,
                    'pool': nc.gpsimd, 'sp': nc.sync}
        self.seq = {e: 0 for e in self.eng}
        self.csem = {e: [] for e in self.eng}
        self.known = {e: {} for e in self.eng}
        self.snap = {}
        self.last_w = {}
        self.readers = {}
        self.semobj = {}
        self.dma_pool = {}
        self.nsem = 0
        self.nwaits = 0
        self.nops = 0
        for q, n in (('sp', 24), ('pool', 24), ('act', 8)):
            self.dma_pool[q] = {'sems': [self._newsem(f"d{q}{i}") for i in range(n)],
                                'cnt': [0] * n, 'next': 0}

    def _newsem(self, name):
        s = self.es.enter_context(self.nc.semaphore(name))
        self.semobj[name] = s
        self.nsem += 1
        return name

    def _need(self, e, tok, skip_self):
        if tok is None:
            return
        name, val, owner = tok
        if skip_self and owner == e:
            return
        if self.known[e].get(name, 0) >= val:
            return
        self.eng[e].wait_ge(self.semobj[name], val)
        self.nwaits += 1
        k = self.known[e]
        k[name] = val
        sn = self.snap.get((name, val))
        if sn:
            for n2, v2 in sn.items():
                if k.get(n2, 0) < v2:
                    k[n2] = v2

    def op(self, e, fn, reads=(), writes=(), dma=False, skip_self=None):
        if skip_self is None:
            skip_self = (e == 'pe')
        if dma:
            skip_self = False
        for r in reads:
            self._need(e, self.last_w.get(r), skip_self)
        for w in writes:
            self._need(e, self.last_w.get(w), skip_self)
            for t in self.readers.get(w, ()):
                self._need(e, t, skip_self)
        self.nops += 1
        if dma:
            pool = self.dma_pool[e]
            i = pool['next']
            pool['next'] = (i + 1) % len(pool['sems'])
            name = pool['sems'][i]
            if pool['cnt'][i] + 16 > MAXV:
                name = self._newsem(f"{name}r{self.nsem}")
                pool['sems'][i] = name
                pool['cnt'][i] = 0
            prev = pool['cnt'][i]
            if prev > 0:
                self._need(e, (name, prev, e + '_dma'), False)
            ins = fn(self.eng[e])
            pool['cnt'][i] = prev + 16
            ins.then_inc(self.semobj[name], 16)
            tok = (name, prev + 16, e + '_dma')
        else:
            n = self.seq[e]
            ep = n // MAXV
            while len(self.csem[e]) <= ep:
                self.csem[e].append(self._newsem(f"c{e}{len(self.csem[e])}"))
            name = self.csem[e][ep]
            ins = fn(self.eng[e])
            ins.then_inc(self.semobj[name], 1)
            self.seq[e] = n + 1
            tok = (name, n - ep * MAXV + 1, e)
        self.snap[(tok[0], tok[1])] = dict(self.known[e])
        for r in reads:
            self.readers.setdefault(r, []).append(tok)
        for w in writes:
            self.last_w[w] = tok
            self.readers[w] = []
        return tok

    def fence(self, e):
        n = self.seq[e]
        if n > 0:
            ep = (n - 1) // MAXV
            self._need(e, (self.csem[e][ep], n - ep * MAXV, e), False)

    def barrier(self):
        toks = []
        for e in self.eng:
            n = self.seq[e]
            if n > 0:
                ep = (n - 1) // MAXV
                toks.append((self.csem[e][ep], n - ep * MAXV, e))
        for q, pool in self.dma_pool.items():
            for name, c in zip(pool['sems'], pool['cnt']):
                if c > 0:
                    toks.append((name, c, q + '_dma'))
        for e in self.eng:
            for t in toks:
                self._need(e, t, False)
        self.last_w.clear()
        self.readers.clear()
        self.snap.clear()


class Stage:
    _n = 0

    def __init__(self, K, name):
        self.K = K
        Stage._n += 1
        self.name = f"{name}{Stage._n}"

    def __enter__(self):
        self.es = ExitStack()
        self.es.__enter__()
        return self

    def T(self, name, shape, dt=F32):
        return self.es.enter_context(self.K.nc.sbuf_tensor(f"{self.name}_{name}", shape, dt))

    def __exit__(self, *a):
        self.K.P.barrier()
        return self.es.__exit__(*a)


class Kern:
    def __init__(self, cfg):
        self.cfg = cfg

    def build(self):
        nc = bass.Bass("TRN2", target_bir_lowering=False)
        self.nc = nc
        dbg = self.cfg.get('debug', False)

        def din(name, shape, dt=F32):
            return nc.dram_tensor(name, list(shape), dt, kind="ExternalInput").ap()

        def dscr(name, shape, dt=F32):
            kind = "ExternalOutput" if (dbg and name in self.cfg.get('expose', ())) else "Internal"
            return nc.dram_tensor(name, list(shape), dt, kind=kind).ap()

        A = {}
        A['x'] = din('x', [S, D])
        A['c_l'] = din('c_l', [128, 8])
        A['ada_mix_w'] = din('ada_mix_w', [2, D, 3 * D])
        A['ada_ffn_w'] = din('ada_ffn_w', [2, D, 3 * D])
        A['ada_b'] = din('ada_b', [4, 3 * D])
        A['ln_g'] = din('ln_g', [4, D])
        A['ln_b'] = din('ln_b', [4, D])
        A['conv_in_w'] = din('conv_in_w', [D, 2 * D])
        A['conv_in_b_l'] = din('conv_in_b_l', [128, 16])
        A['conv_dw_w_l'] = din('conv_dw_w_l', [128, 8, 31])
        A['conv_vec_l'] = din('conv_vec_l', [128, 3, 8])
        A['conv_out_w'] = din('conv_out_w', [D, D])
        A['conv_out_b'] = din('conv_out_b', [1, D])
        A['attn_in_w'] = din('attn_in_w', [D, 3 * D + 16])
        A['attn_qkb_l'] = din('attn_qkb_l', [128, 16])
        A['attn_vb'] = din('attn_vb', [1, D])
        A['attn_fb'] = din('attn_fb', [16, 1])
        A['attn_out_w'] = din('attn_out_w', [D, D])
        A['attn_out_b'] = din('attn_out_b', [1, D])
        A['peer_query_w'] = din('peer_query_w', [2, D, 2 * D])
        A['peer_skT'] = din('peer_skT', [2, 2, 128, 128])
        A['peer_u'] = din('peer_u', [2, NEXP, D])
        A['peer_v'] = din('peer_v', [2, NEXP, D])
        A['out'] = nc.dram_tensor('out', [S, D], F32, kind="ExternalOutput").ap()
        A['X1'] = dscr('X1', [S, D])
        A['X2'] = dscr('X2', [S, D])
        A['X3'] = dscr('X3', [S, D])
        A['ST'] = dscr('ST', [8, 128, S])
        A['IDX'] = dscr('IDX', [S, 128], I32)
        A['SCR'] = dscr('SCR', [S, 2048])
        A['UVB'] = dscr('UVB', [2 * NEXP, 2 * D], BF16)
        A['H'] = dscr('H', [S, D])
        A['GATE'] = dscr('GATE', [S, 128])
        A['QA'] = dscr('QA', [16, 68, S])
        A['KA'] = dscr('KA', [16, 68, S])
        A['V'] = dscr('V', [S, D])
        A['AOT'] = dscr('AOT', [8, 128, S])
        self.A = A

        with ExitStack() as es:
            self.P = P = Prog(nc, es)
            G = lambda name, shape, dt=F32: es.enter_context(nc.sbuf_tensor(name, shape, dt))
            self.ps = [es.enter_context(nc.psum_tensor(f"ps{i}", [128, 512], F32)) for i in range(8)]
            self.ident = G('ident', [128, 128])
            self.ones = G('ones', [128, 128])
            self.SC = G('SC', [128, 8, 128])
            self.shift_bc = G('shift_bc', [128, D])
            self.scale_bc = G('scale_bc', [128, D])
            self.gate_bc = G('gate_bc', [128, D])
            self.g_bc = G('g_bc', [128, D])
            self.b_bc = G('b_bc', [128, D])
            self.bs = G('bs', [128, 2, 6])
            self.mv = G('mv', [128, 2])
            self.rs = G('rs', [128, 1])
            self.emit_globals()
            self._cast_done = False
            if self.cfg.get('peer_bf16', True) and any(st.startswith('peer') for st in self.cfg.get('stages', ['peer0'])):
                self.emit_cast_tables()
                self._cast_done = True
            order = self.cfg.get('stages', ['conv', 'peer0', 'attn', 'peer1'])
            cur = A['x']
            nxt = {'conv': A['X1'], 'peer0': A['X2'], 'attn': A['X3'], 'peer1': A['out']}
            for i, st in enumerate(order):
                dst = A['out'] if i == len(order) - 1 else nxt[st]
                if st == 'conv':
                    self.emit_adaln(0)
                    self.emit_conv1(cur)
                    self.emit_proj_out(cur, dst, A['conv_out_w'], A['conv_out_b'], src_fm=A['ST'])
                elif st == 'attn':
                    self.emit_adaln(2)
                    self.emit_attn1(cur)
                    self.emit_attn2()
                    self.emit_proj_out(cur, dst, A['attn_out_w'], A['attn_out_b'], src_fm=A['AOT'])
                else:
                    L = int(st[-1])
                    self.emit_adaln(1 + 2 * L)
                    if self.cfg.get('peer_bf16', True):
                        if not self._cast_done:
                            self.emit_cast_tables()
                            self._cast_done = True
                        self.emit_peer1a(cur, L)
                        self.emit_peer2g(cur, dst, L)
                    elif self.cfg.get('peer_fused', True):
                        self.emit_peer1a(cur, L)
                        self.emit_peer2f(cur, dst, L)
                    else:
                        self.emit_peer1(cur, L)
                        self.emit_peer2(cur, dst, L)
                cur = dst
            P.barrier()
            print(f"[kern] ops={P.nops} waits={P.nwaits} sems={P.nsem} seq={P.seq}")
        return nc

    def emit_globals(self):
        P, nc = self.P, self.nc
        ident, ones = self.ident, self.ones
        P.op('pool', lambda e: e.memset(ident[:], 1.0), writes=['ident'])
        P.op('pool', lambda e: e.affine_select(out=ident[:], in_=ident[:], pattern=[[-1, 128]],
                                               compare_op=ALU.is_equal, fill=0.0, base=0, channel_multiplier=1),
             reads=['ident'], writes=['ident'])
        P.op('pool', lambda e: e.memset(ones[:], 1.0), writes=['ones'])
        with Stage(self, 'gl') as st:
            ct = st.T('ct', [128, 8])
            P.op('sp', lambda e: e.dma_start(out=ct[:], in_=self.A['c_l']), writes=['ct'], dma=True)
            P.op('act', lambda e: e.activation(out=ct[:], in_=ct[:], func=AF.Silu), reads=['ct'], writes=['ct'])
            SC = self.SC
            P.op('dve', lambda e: e.tensor_copy(out=SC[:], in_=ct[:].unsqueeze(2).to_broadcast([128, 8, 128])),
                 reads=['ct'], writes=['SC'])

    def modulate(self, xt, ht, rx, rh):
        P = self.P
        P.op('dve', lambda e: e.tensor_tensor(out=ht, in0=xt, in1=self.scale_bc[:], op=ALU.mult),
             reads=[rx, 'scale_bc'], writes=[rh])
        P.op('pool', lambda e: e.tensor_tensor(out=ht, in0=ht, in1=self.shift_bc[:], op=ALU.add),
             reads=[rh, 'shift_bc'], writes=[rh])

    def transpose8(self, src, rsrc, dstT, rdst, col0, pb):
        P = self.P
        ps = self.ps
        for half in range(2):
            bank = ps[pb + half]
            rb = f'ps{pb + half}'
            for kk in range(4):
                k = half * 4 + kk
                P.op('pe', lambda e, k=k, kk=kk, bank=bank: e.transpose(out=bank[:, kk * 128:(kk + 1) * 128],
                                                                      in_=src[:, k * 128:(k + 1) * 128],
                                                                      identity=self.ident[:]),
                     reads=[rsrc, 'ident'], writes=[rb])
            dst = dstT[:, half * 4:half * 4 + 4, col0:col0 + 128]
            srcp = bank[:].rearrange("p (k n) -> p k n", k=4)
            if half == 0:
                P.op('act', lambda e, dst=dst, srcp=srcp: e.copy(out=dst, in_=srcp), reads=[rb], writes=[rdst])
            else:
                P.op('dve', lambda e, dst=dst, srcp=srcp: e.tensor_copy(out=dst, in_=srcp), reads=[rb], writes=[rdst])

    def layernorm_inplace(self, r, rr, gb='pool'):
        P = self.P
        bs, mv, rs = self.bs, self.mv, self.rs
        for c in range(2):
            P.op('dve', lambda e, c=c: e.bn_stats(out=bs[:, c, :], in_=r[:, c * 512:(c + 1) * 512]),
                 reads=[rr], writes=['bs'])
        P.op('dve', lambda e: e.bn_aggr(out=mv[:], in_=bs[:].rearrange("p a b -> p (a b)")), reads=['bs'], writes=['mv'])
        P.op('dve', lambda e: e.tensor_scalar(out=rs[:], in0=mv[:, 1:2], scalar1=EPS, scalar2=None, op0=ALU.add),
             reads=['mv'], writes=['rs'])
        P.op('act', lambda e: e.activation(out=rs[:], in_=rs[:], func=AF.Sqrt), reads=['rs'], writes=['rs'])
        P.op('dve', lambda e: e.reciprocal(out=rs[:], in_=rs[:]), reads=['rs'], writes=['rs'])
        P.op('dve', lambda e: e.tensor_scalar(out=r, in0=r, scalar1=mv[:, 0:1], scalar2=rs[:, 0:1],
                                              op0=ALU.subtract, op1=ALU.mult), reads=[rr, 'mv', 'rs'], writes=[rr])
        P.op(gb, lambda e: e.tensor_tensor(out=r, in0=r, in1=self.g_bc[:], op=ALU.mult), reads=[rr, 'g_bc'], writes=[rr])
        P.op(gb, lambda e: e.tensor_tensor(out=r, in0=r, in1=self.b_bc[:], op=ALU.add), reads=[rr, 'b_bc'], writes=[rr])

    def emit_adaln(self, sub):
        P, nc, A, ps = self.P, self.nc, self.A, self.ps
        L = sub // 2
        wsrc = (A['ada_mix_w'] if sub % 2 == 0 else A['ada_ffn_w'])[L]
        ones = self.ones
        with Stage(self, f'ada{sub}') as st:
            brow = st.T('brow', [1, 3 * D])
            lrow = st.T('lrow', [1, 2 * D])
            wch = [st.T(f'wch{i}', [128, 8, 512]) for i in range(2)]
            P.op('sp', lambda e: e.dma_start(out=brow[:], in_=A['ada_b'][sub:sub + 1, :]), writes=['brow'], dma=True)
            P.op('sp', lambda e: e.dma_start(out=lrow[:, 0:D], in_=A['ln_g'][sub:sub + 1, :]), writes=['lrow'], dma=True)
            P.op('sp', lambda e: e.dma_start(out=lrow[:, D:2 * D], in_=A['ln_b'][sub:sub + 1, :]), writes=['lrow'], dma=True)
            dsts = [self.shift_bc, self.shift_bc, self.scale_bc, self.scale_bc, self.gate_bc, self.gate_bc]
            names = ['shift_bc', 'shift_bc', 'scale_bc', 'scale_bc', 'gate_bc', 'gate_bc']
            for n6 in range(6):
                wb = wch[n6 % 2]
                rw = f'wch{n6 % 2}'
                P.op('sp', lambda e, wb=wb, n6=n6: e.dma_start(
                    out=wb[:], in_=wsrc[:, n6 * 512:(n6 + 1) * 512].rearrange("(k p) n -> p k n", p=128)),
                    writes=[rw], dma=True)
                bank = ps[n6 % 2]
                rb = f'ps{n6 % 2}'
                for k in range(8):
                    P.op('pe', lambda e, k=k, wb=wb, bank=bank: e.matmul(out=bank[:], lhsT=self.SC[:, k, :], rhs=wb[:, k, :],
                                                                        start=(k == 0), stop=False),
                         reads=['SC', rw], writes=[rb])
                P.op('pe', lambda e, bank=bank, n6=n6: e.matmul(out=bank[:], lhsT=ones[0:1, :], rhs=brow[0:1, n6 * 512:(n6 + 1) * 512],
                                                                start=False, stop=True),
                     reads=['ones', 'brow'], writes=[rb])
                dst = dsts[n6][:, (n6 % 2) * 512:(n6 % 2 + 1) * 512]
                if n6 in (2, 3):
                    P.op('dve', lambda e, dst=dst, bank=bank: e.tensor_scalar(out=dst, in0=bank[:], scalar1=1.0, scalar2=None, op0=ALU.add),
                         reads=[rb], writes=[names[n6]])
                else:
                    P.op('dve', lambda e, dst=dst, bank=bank: e.tensor_copy(out=dst, in_=bank[:]), reads=[rb], writes=[names[n6]])
            for j in range(4):
                bank = ps[2 + j % 2]
                rb = f'ps{2 + j % 2}'
                P.op('pe', lambda e, bank=bank, j=j: e.matmul(out=bank[:], lhsT=ones[0:1, :], rhs=lrow[0:1, j * 512:(j + 1) * 512],
                                                              start=True, stop=True), reads=['ones', 'lrow'], writes=[rb])
                dstt = self.g_bc if j < 2 else self.b_bc
                dst = dstt[:, (j % 2) * 512:(j % 2 + 1) * 512]
                P.op('act', lambda e, dst=dst, bank=bank: e.copy(out=dst, in_=bank[:]), reads=[rb],
                     writes=['g_bc' if j < 2 else 'b_bc'])

    def emit_conv1(self, xin):
        P, nc, A, ps = self.P, self.nc, self.A, self.ps
        ones = self.ones
        with Stage(self, 'c1') as st:
            win = st.T('win', [128, 8, 2048])
            cib = st.T('cib', [128, 16])
            dw = st.T('dw', [128, 8, 31])
            cv = st.T('cv', [128, 3, 8])
            xts = [st.T(f'xt{i}', [128, D]) for i in range(2)]
            ht = st.T('ht', [128, D])
            hT = st.T('hT', [128, 8, 512])
            acc = st.T('acc', [128, 8, 512])
            aext = st.T('aext', [128, 8, 542])
            sig = [st.T(f'sig{i}', [128, 512]) for i in range(2)]
            sq = [st.T(f'sq{i}', [128, 512]) for i in range(2)]
            meant = st.T('meant', [128, 512])
            rstd = st.T('rstd', [128, 512])
            tmp = st.T('tmp', [128, 512])
            for q in range(4):
                P.op('sp', lambda e, q=q: e.dma_start(out=win[:, :, q * 512:(q + 1) * 512],
                                                      in_=A['conv_in_w'][:, q * 512:(q + 1) * 512].rearrange("(k p) n -> p k n", p=128)),
                     writes=[('win', q)], dma=True)
            P.op('sp', lambda e: e.dma_start(out=cib[:], in_=A['conv_in_b_l']), writes=['cib'], dma=True)
            P.op('sp', lambda e: e.dma_start(out=dw[:], in_=A['conv_dw_w_l']), writes=['dw'], dma=True)
            P.op('sp', lambda e: e.dma_start(out=cv[:], in_=A['conv_vec_l']), writes=['cv'], dma=True)
            for cc in range(8):
                P.op('pool', lambda e, cc=cc: e.memset(aext[:, cc, 0:30], 0.0), writes=[('aext', cc)])
            ti = 0
            for jb in range(8):
                for tl in range(4):
                    xt = xts[ti % 2]
                    rx = f'xt{ti % 2}'
                    P.op('sp', lambda e, xt=xt, ti=ti: e.dma_start(out=xt[:], in_=xin[ti * 128:(ti + 1) * 128, :]),
                         writes=[rx], dma=True)
                    self.modulate(xt[:], ht[:], rx, 'ht')
                    self.transpose8(ht, 'ht', hT, 'hT', tl * 128, 0)
                    ti += 1
                for cc in range(8):
                    pa, pb = ps[2 + (cc % 2) * 2], ps[3 + (cc % 2) * 2]
                    ra, rb = f'ps{2 + (cc % 2) * 2}', f'ps{3 + (cc % 2) * 2}'
                    for k in range(8):
                        P.op('pe', lambda e, k=k, cc=cc, pa=pa: e.matmul(out=pa[:], lhsT=win[:, k, cc * 128:(cc + 1) * 128], rhs=hT[:, k, :],
                                                                        start=(k == 0), stop=(k == 7)),
                             reads=[('win', cc // 4), 'hT'], writes=[ra])
                    for k in range(8):
                        P.op('pe', lambda e, k=k, cc=cc, pb=pb: e.matmul(out=pb[:], lhsT=win[:, k, D + cc * 128:D + (cc + 1) * 128], rhs=hT[:, k, :],
                                                                        start=(k == 0), stop=(k == 7)),
                             reads=[('win', 2 + cc // 4), 'hT'], writes=[rb])
                    sg = sig[cc % 2]
                    rsg = f'sig{cc % 2}'
                    P.op('act', lambda e, sg=sg, pb=pb, cc=cc: e.activation(out=sg[:], in_=pb[:], func=AF.Sigmoid, bias=cib[:, 8 + cc:9 + cc], scale=1.0),
                         reads=[rb, 'cib'], writes=[rsg])
                    P.op('dve', lambda e, sg=sg, pa=pa, cc=cc: e.scalar_tensor_tensor(out=aext[:, cc, 30:542], in0=pa[:], scalar=cib[:, cc:cc + 1], in1=sg[:],
                                                                                 op0=ALU.add, op1=ALU.mult),
                         reads=[ra, rsg, 'cib'], writes=[('aext', cc)])
                for cc in range(8):
                    P.op('dve', lambda e, cc=cc: e.tensor_scalar(out=acc[:, cc, :], in0=aext[:, cc, 0:512], scalar1=dw[:, cc, 0:1], scalar2=cv[:, 0, cc:cc + 1],
                                                                 op0=ALU.mult, op1=ALU.add),
                         reads=[('aext', cc), 'dw', 'cv'], writes=[('acc', cc)])
                    for w in range(1, 31):
                        P.op('dve', lambda e, cc=cc, w=w: e.scalar_tensor_tensor(out=acc[:, cc, :], in0=aext[:, cc, w:w + 512], scalar=dw[:, cc, w:w + 1],
                                                                                 in1=acc[:, cc, :], op0=ALU.mult, op1=ALU.add),
                             reads=[('aext', cc), 'dw', ('acc', cc)], writes=[('acc', cc)])
                    P.op('act', lambda e, cc=cc: e.copy(out=aext[:, cc, 0:30], in_=aext[:, cc, 512:542]),
                         reads=[('aext', cc)], writes=[('aext', cc)])
                for cc in range(8):
                    s2 = sq[cc % 2]
                    rs2 = f'sq{cc % 2}'
                    P.op('act', lambda e, cc=cc, s2=s2: e.activation(out=s2[:], in_=acc[:, cc, :], func=AF.Square),
                         reads=[('acc', cc)], writes=[rs2])
                    P.op('pe', lambda e, cc=cc: e.matmul(out=ps[6][:], lhsT=ones[:], rhs=acc[:, cc, :], start=(cc == 0), stop=(cc == 7)),
                         reads=['ones', ('acc', cc)], writes=['ps6'])
                    P.op('pe', lambda e, cc=cc, s2=s2: e.matmul(out=ps[7][:], lhsT=ones[:], rhs=s2[:], start=(cc == 0), stop=(cc == 7)),
                         reads=['ones', rs2], writes=['ps7'])
                P.op('act', lambda e: e.activation(out=meant[:], in_=ps[6][:], func=AF.Copy, scale=1.0 / D), reads=['ps6'], writes=['meant'])
                P.op('dve', lambda e: e.tensor_tensor(out=tmp[:], in0=meant[:], in1=meant[:], op=ALU.mult), reads=['meant'], writes=['tmp'])
                P.op('dve', lambda e: e.scalar_tensor_tensor(out=rstd[:], in0=ps[7][:], scalar=1.0 / D, in1=tmp[:], op0=ALU.mult, op1=ALU.subtract),
                     reads=['ps7', 'tmp'], writes=['rstd'])
                P.op('dve', lambda e: e.tensor_scalar(out=rstd[:], in0=rstd[:], scalar1=EPS, scalar2=None, op0=ALU.add), reads=['rstd'], writes=['rstd'])
                P.op('act', lambda e: e.activation(out=rstd[:], in_=rstd[:], func=AF.Sqrt), reads=['rstd'], writes=['rstd'])
                P.op('dve', lambda e: e.reciprocal(out=rstd[:], in_=rstd[:]), reads=['rstd'], writes=['rstd'])
                for cc in range(8):
                    P.op('dve', lambda e, cc=cc: e.tensor_tensor(out=acc[:, cc, :], in0=acc[:, cc, :], in1=meant[:], op=ALU.subtract),
                         reads=[('acc', cc), 'meant'], writes=[('acc', cc)])
                    P.op('pool', lambda e, cc=cc: e.tensor_tensor(out=acc[:, cc, :], in0=acc[:, cc, :], in1=rstd[:], op=ALU.mult),
                         reads=[('acc', cc), 'rstd'], writes=[('acc', cc)])
                    P.op('act', lambda e, cc=cc: e.activation(out=acc[:, cc, :], in_=acc[:, cc, :], func=AF.Silu,
                                                              bias=cv[:, 2, cc:cc + 1], scale=cv[:, 1, cc:cc + 1]),
                         reads=[('acc', cc), 'cv'], writes=[('acc', cc)])
                P.op('sp', lambda e, jb=jb: e.dma_start(out=A['ST'][:, :, jb * 512:(jb + 1) * 512].rearrange("c p t -> p c t"), in_=acc[:]),
                     reads=[('acc', cc) for cc in range(8)], writes=[('ST', jb)], dma=True)

    def emit_proj_out(self, xin, dst, w_ap, b_ap, src_fm=None, src_tm=None):
        P, nc, A, ps = self.P, self.nc, self.A, self.ps
        ones = self.ones
        with Stage(self, 'po') as st:
            wo = st.T('wo', [128, 8, D])
            bo = st.T('bo', [1, D])
            xts = [st.T(f'xt{i}', [128, D]) for i in range(2)]
            rts = [st.T(f'rt{i}', [128, D]) for i in range(2)]
            if src_fm is not None:
                sT = [st.T(f'sT{i}', [128, 8, 512]) for i in range(2)]
            else:
                ao = [st.T(f'ao{i}', [128, D]) for i in range(2)]
                aT = [st.T(f'aT{i}', [128, 8, 128]) for i in range(2)]
            for q in range(2):
                P.op('sp', lambda e, q=q: e.dma_start(out=wo[:, :, q * 512:(q + 1) * 512],
                                                      in_=w_ap[:, q * 512:(q + 1) * 512].rearrange("(k p) n -> p k n", p=128)),
                     writes=[('wo', q)], dma=True)
            P.op('sp', lambda e: e.dma_start(out=bo[:], in_=b_ap), writes=['bo'], dma=True)
            for ti in range(NT):
                xt = xts[ti % 2]
                rx = f'xt{ti % 2}'
                rt = rts[ti % 2]
                rr = f'rt{ti % 2}'
                P.op('sp', lambda e, xt=xt, ti=ti: e.dma_start(out=xt[:], in_=xin[ti * 128:(ti + 1) * 128, :]), writes=[rx], dma=True)
                if src_fm is not None:
                    jb, tl = ti // 4, ti % 4
                    sb = sT[jb % 2]
                    rsb = f'sT{jb % 2}'
                    if tl == 0:
                        P.op('sp', lambda e, sb=sb, jb=jb: e.dma_start(out=sb[:], in_=src_fm[:, :, jb * 512:(jb + 1) * 512].rearrange("c p t -> p c t")),
                             reads=[('ST', jb)], writes=[rsb], dma=True)
                    lhs = lambda k, sb=sb, tl=tl: sb[:, k, tl * 128:(tl + 1) * 128]
                    rl = rsb
                else:
                    a = ao[ti % 2]
                    ra = f'ao{ti % 2}'
                    at = aT[ti % 2]
                    rat = f'aT{ti % 2}'
                    P.op('sp', lambda e, a=a, ti=ti: e.dma_start(out=a[:], in_=src_tm[ti * 128:(ti + 1) * 128, :]), writes=[ra], dma=True)
                    self.transpose8(a, ra, at, rat, 0, 4)
                    lhs = lambda k, at=at: at[:, k, :]
                    rl = rat
                pb = (ti % 2) * 2
                for half in range(2):
                    bank = ps[pb + half]
                    rb = f'ps{pb + half}'
                    for k in range(8):
                        P.op('pe', lambda e, k=k, bank=bank, half=half, lhs=lhs: e.matmul(out=bank[:], lhsT=lhs(k), rhs=wo[:, k, half * 512:(half + 1) * 512],
                                                                                     start=(k == 0), stop=False),
                             reads=[rl, ('wo', half)], writes=[rb])
                    P.op('pe', lambda e, bank=bank, half=half: e.matmul(out=bank[:], lhsT=ones[0:1, :], rhs=bo[0:1, half * 512:(half + 1) * 512],
                                                                        start=False, stop=True), reads=['ones', 'bo'], writes=[rb])
                    P.op('dve', lambda e, bank=bank, half=half, rt=rt: e.tensor_tensor(out=rt[:, half * 512:(half + 1) * 512], in0=bank[:],
                                                                                  in1=self.gate_bc[:, half * 512:(half + 1) * 512], op=ALU.mult),
                         reads=[rb, 'gate_bc'], writes=[rr])
                P.op('dve', lambda e, rt=rt, xt=xt: e.scalar_tensor_tensor(out=rt[:], in0=xt[:], scalar=ALPHA, in1=rt[:], op0=ALU.mult, op1=ALU.add),
                     reads=[rx, rr], writes=[rr])
                self.layernorm_inplace(rt[:], rr)
                P.op('sp', lambda e, rt=rt, ti=ti: e.dma_start(out=dst[ti * 128:(ti + 1) * 128, :], in_=rt[:]), reads=[rr], dma=True)

    def emit_peer1(self, xin, L):
        P, nc, A, ps = self.P, self.nc, self.A, self.ps
        with Stage(self, f'p1{L}') as st:
            wq = st.T('wq', [128, 8, 2048])
            skT = st.T('skT', [128, 2, 128])
            xts = [st.T(f'xt{i}', [128, D]) for i in range(2)]
            ht = st.T('ht', [128, D])
            hT = st.T('hT', [128, 8, 256])
            qT = st.T('qT', [128, 16, 256])
            sc = st.T('sc', [128, 16, 128])
            m = st.T('m', [128, 16, 16])
            ix = st.T('ix', [128, 16, 16], U32)
            ixf = st.T('ixf', [128, 16, 16])
            wk = st.T('wk', [128, 16, 128])
            cand = st.T('cand', [128, 8, 256])
            candi = st.T('candi', [128, 8, 256])
            wk2 = st.T('wk2', [128, 8, 256])
            junk = [st.T(f'junk{i}', [128, 256]) for i in range(2)]
            ts = st.T('ts', [128, 8, 16])
            ef = st.T('ef', [128, 128])
            ei = [st.T(f'ei{i}', [128, 128], I32) for i in range(2)]
            gt = [st.T(f'gt{i}', [128, 8, 16]) for i in range(2)]
            gsum = st.T('gsum', [128, 8])
            for q in range(4):
                P.op('sp', lambda e, q=q: e.dma_start(out=wq[:, :, q * 512:(q + 1) * 512],
                                                      in_=A['peer_query_w'][L][:, q * 512:(q + 1) * 512].rearrange("(k p) n -> p k n", p=128)),
                     writes=[('wq', q)], dma=True)
            P.op('sp', lambda e: e.dma_start(out=skT[:], in_=A['peer_skT'][L].rearrange("h d k -> d h k")), writes=['skT'], dma=True)
            ti = 0
            for jb in range(S // 256):
                for tl in range(2):
                    xt = xts[ti % 2]
                    rx = f'xt{ti % 2}'
                    P.op('sp', lambda e, xt=xt, ti=ti: e.dma_start(out=xt[:], in_=xin[ti * 128:(ti + 1) * 128, :]), writes=[rx], dma=True)
                    self.modulate(xt[:], ht[:], rx, 'ht')
                    self.transpose8(ht, 'ht', hT, 'hT', tl * 128, 0)
                    ti += 1
                for c in range(16):
                    bank = ps[2 + c % 2]
                    rb = f'ps{2 + c % 2}'
                    for k in range(8):
                        P.op('pe', lambda e, k=k, c=c, bank=bank: e.matmul(out=bank[:, 0:256], lhsT=wq[:, k, c * 128:(c + 1) * 128], rhs=hT[:, k, :],
                                                                          start=(k == 0), stop=(k == 7)),
                             reads=[('wq', c // 4), 'hT'], writes=[rb])
                    if c % 2 == 0:
                        P.op('act', lambda e, c=c, bank=bank: e.copy(out=qT[:, c, :], in_=bank[:, 0:256]), reads=[rb], writes=[('qT', c)])
                    else:
                        P.op('dve', lambda e, c=c, bank=bank: e.tensor_copy(out=qT[:, c, :], in_=bank[:, 0:256]), reads=[rb], writes=[('qT', c)])
                for tl in range(2):
                    tix = jb * 2 + tl
                    for c in range(16):
                        bank = ps[4 + c // 4]
                        rb = f'ps{4 + c // 4}'
                        P.op('pe', lambda e, c=c, bank=bank, tl=tl: e.matmul(out=bank[:, (c % 4) * 128:(c % 4 + 1) * 128],
                                                                            lhsT=qT[:, c, tl * 128:(tl + 1) * 128], rhs=skT[:, c % 2, :],
                                                                            start=True, stop=True),
                             reads=[('qT', c), 'skT'], writes=[rb])
                    for g4 in range(4):
                        P.op('act', lambda e, g4=g4: e.copy(out=sc[:, g4 * 4:(g4 + 1) * 4, :], in_=ps[4 + g4][:].rearrange("p (a k) -> p a k", a=4)),
                             reads=[f'ps{4 + g4}'], writes=['sc'])
                    for c in range(16):
                        P.op('dve', lambda e, c=c: e.max(out=m[:, c, 0:8], in_=sc[:, c, :]), reads=['sc'], writes=[('m0', c)])
                    P.fence('dve')
                    for c in range(16):
                        P.op('dve', lambda e, c=c: e.max_index(out=ix[:, c, 0:8], in_max=m[:, c, 0:8], in_values=sc[:, c, :]),
                             reads=['sc', ('m0', c)], writes=[('ix0', c)])
                        P.op('dve', lambda e, c=c: e.match_replace(out=wk[:, c, :], in_to_replace=m[:, c, 0:8], in_values=sc[:, c, :], imm_value=-1e30),
                             reads=['sc', ('m0', c)], writes=[('wk', c)])
                    P.fence('dve')
                    for c in range(16):
                        P.op('dve', lambda e, c=c: e.max(out=m[:, c, 8:16], in_=wk[:, c, :]), reads=[('wk', c)], writes=[('m1', c)])
                    P.fence('dve')
                    for c in range(16):
                        P.op('dve', lambda e, c=c: e.max_index(out=ix[:, c, 8:16], in_max=m[:, c, 8:16], in_values=wk[:, c, :]),
                             reads=[('wk', c), ('m1', c)], writes=[('ix1', c)])
                    P.fence('dve')
                    mres = [('m0', c) for c in range(16)] + [('m1', c) for c in range(16)]
                    ixres = [('ix0', c) for c in range(16)] + [('ix1', c) for c in range(16)]
                    P.op('dve', lambda e: e.tensor_copy(out=ixf[:], in_=ix[:]), reads=ixres, writes=['ixf'])
                    m4 = m[:].rearrange("p (h two) k -> p h two k", two=2)
                    i4 = ixf[:].rearrange("p (h two) k -> p h two k", two=2)
                    c4 = cand[:].rearrange("p h (a b) -> p h a b", a=16)
                    ci4 = candi[:].rearrange("p h (a b) -> p h a b", a=16)
                    P.op('dve', lambda e: e.tensor_tensor(out=c4, in0=m4[:, :, 0, :].unsqueeze(3).to_broadcast([128, 8, 16, 16]),
                                                          in1=m4[:, :, 1, :].unsqueeze(2).to_broadcast([128, 8, 16, 16]), op=ALU.add),
                         reads=mres, writes=['cand'])
                    P.op('dve', lambda e: e.tensor_scalar(out=i4[:, :, 0, :], in0=i4[:, :, 0, :], scalar1=128.0, scalar2=None, op0=ALU.mult),
                         reads=['ixf'], writes=['ixf'])
                    P.op('dve', lambda e: e.tensor_tensor(out=ci4, in0=i4[:, :, 0, :].unsqueeze(3).to_broadcast([128, 8, 16, 16]),
                                                          in1=i4[:, :, 1, :].unsqueeze(2).to_broadcast([128, 8, 16, 16]), op=ALU.add),
                         reads=['ixf'], writes=['candi'])
                    for h in range(8):
                        P.op('dve', lambda e, h=h: e.max(out=ts[:, h, 0:8], in_=cand[:, h, :]), reads=['cand'], writes=[('ts0', h)])
                    P.fence('dve')
                    for h in range(8):
                        P.op('dve', lambda e, h=h: e.match_replace(out=wk2[:, h, :], in_to_replace=ts[:, h, 0:8], in_values=cand[:, h, :], imm_value=-1e30),
                             reads=['cand', ('ts0', h)], writes=[('wk2', h)])
                    P.fence('dve')
                    for h in range(8):
                        P.op('dve', lambda e, h=h: e.max(out=ts[:, h, 8:16], in_=wk2[:, h, :]), reads=[('wk2', h)], writes=[('ts1', h)])
                    P.fence('dve')
                    tsres = [('ts0', h) for h in range(8)] + [('ts1', h) for h in range(8)]
                    for h in range(8):
                        for k in range(16):
                            P.op('dve', lambda e, h=h, k=k: e.scalar_tensor_tensor(out=junk[(h * 16 + k) % 2][:], in0=cand[:, h, :], scalar=ts[:, h, k:k + 1], in1=candi[:, h, :],
                                                                                   op0=ALU.is_equal, op1=ALU.mult, accum_out=ef[:, h * 16 + k:h * 16 + k + 1]),
                                 reads=['cand', 'candi', ('ts0', h), ('ts1', h)], writes=[('ef', h * 16 + k)])
                    P.fence('dve')
                    eib = ei[tix % 2]
                    rei = f'ei{tix % 2}'
                    gtb = gt[tix % 2]
                    rgt = f'gt{tix % 2}'
                    P.op('dve', lambda e: e.tensor_scalar(out=ef[:], in0=ef[:], scalar1=float(NEXP - 1), scalar2=float(L * NEXP), op0=ALU.min, op1=ALU.add),
                         reads=[('ef', q) for q in range(128)], writes=['ef'])
                    P.op('dve', lambda e, eib=eib: e.tensor_copy(out=eib[:], in_=ef[:]), reads=['ef'], writes=[rei])
                    P.op('dve', lambda e, gtb=gtb: e.tensor_tensor(out=gtb[:], in0=ts[:], in1=ts[:, :, 0:1].to_broadcast([128, 8, 16]), op=ALU.subtract),
                         reads=tsres, writes=[rgt])
                    P.op('act', lambda e, gtb=gtb: e.activation(out=gtb[:], in_=gtb[:], func=AF.Exp), reads=[rgt], writes=[rgt])
                    P.op('dve', lambda e, gtb=gtb: e.tensor_reduce(out=gsum[:], in_=gtb[:], axis=AX.X, op=ALU.add), reads=[rgt], writes=['gsum'])
                    P.op('dve', lambda e: e.reciprocal(out=gsum[:], in_=gsum[:]), reads=['gsum'], writes=['gsum'])
                    P.op('dve', lambda e, gtb=gtb: e.tensor_tensor(out=gtb[:], in0=gtb[:], in1=gsum[:].unsqueeze(2).to_broadcast([128, 8, 16]), op=ALU.mult),
                         reads=[rgt, 'gsum'], writes=[rgt])
                    P.op('sp', lambda e, eib=eib, tix=tix: e.dma_start(out=A['IDX'][tix * 128:(tix + 1) * 128, :], in_=eib[:]),
                         reads=[rei], writes=[('IDX', tix)], dma=True)
                    P.op('sp', lambda e, gtb=gtb, tix=tix: e.dma_start(out=A['GATE'][tix * 128:(tix + 1) * 128, :], in_=gtb[:].rearrange("p h k -> p (h k)")),
                         reads=[rgt], writes=[('GATE', tix)], dma=True)

    def emit_peer2(self, xin, dst, L):
        P, nc, A, ps = self.P, self.nc, self.A, self.ps
        NB = self.cfg.get('nb', 14)
        U = A['peer_u'].rearrange("l e d -> (l e) d")
        V = A['peer_v'].rearrange("l e d -> (l e) d")
        with Stage(self, f'p2{L}') as st:
            xts = [st.T(f'xt{i}', [128, D]) for i in range(2)]
            hts = [st.T(f'ht{i}', [128, D]) for i in range(2)]
            eis = [st.T(f'ei{i}', [128, 128], I32) for i in range(2)]
            gts = [st.T(f'gt{i}', [128, 128]) for i in range(2)]
            ub = [st.T(f'ub{i}', [128, D]) for i in range(NB)]
            vb = [st.T(f'vb{i}', [128, D]) for i in range(NB)]
            junk = st.T('junk', [128, D])
            apre = st.T('apre', [128, 128])
            coef = st.T('coef', [128, 128])
            accs = [st.T(f'acc{i}', [128, D]) for i in range(2)]
            nu = nv = 0
            for ti in range(NT):
                b = ti % 2
                xt, ht, eib, gtb, acc = xts[b], hts[b], eis[b], gts[b], accs[b]
                rx, rh, rei, rgt, racc = f'xt{b}', f'ht{b}', f'ei{b}', f'gt{b}', f'acc{b}'
                P.op('sp', lambda e, xt=xt, ti=ti: e.dma_start(out=xt[:], in_=xin[ti * 128:(ti + 1) * 128, :]), writes=[rx], dma=True)
                P.op('sp', lambda e, eib=eib, ti=ti: e.dma_start(out=eib[:], in_=A['IDX'][ti * 128:(ti + 1) * 128, :]),
                     reads=[('IDX', ti)], writes=[rei], dma=True)
                P.op('sp', lambda e, gtb=gtb, ti=ti: e.dma_start(out=gtb[:], in_=A['GATE'][ti * 128:(ti + 1) * 128, :]),
                     reads=[('GATE', ti)], writes=[rgt], dma=True)
                self.modulate(xt[:], ht[:], rx, rh)
                for k in range(128):
                    s = nu % NB
                    nu += 1
                    P.op('pool', lambda e, s=s, k=k, eib=eib: e.indirect_dma_start(
                        out=ub[s][:], out_offset=None, in_=U,
                        in_offset=bass.IndirectOffsetOnAxis(ap=eib[:, k:k + 1], axis=0)),
                        reads=[rei], writes=[('ub', s)], dma=True)
                    P.op('dve', lambda e, s=s, k=k, ht=ht: e.scalar_tensor_tensor(out=junk[:], in0=ub[s][:], scalar=1.0, in1=ht[:], op0=ALU.mult, op1=ALU.mult,
                                                                               accum_out=apre[:, k:k + 1]),
                         reads=[('ub', s), rh], writes=['junk', 'apre'])
                P.op('act', lambda e: e.activation(out=coef[:], in_=apre[:], func=AF.Gelu), reads=['apre'], writes=['coef'])
                P.op('dve', lambda e, gtb=gtb: e.tensor_tensor(out=coef[:], in0=coef[:], in1=gtb[:], op=ALU.mult), reads=['coef', rgt], writes=['coef'])
                for k in range(128):
                    s = nv % NB
                    nv += 1
                    P.op('pool', lambda e, s=s, k=k, eib=eib: e.indirect_dma_start(
                        out=vb[s][:], out_offset=None, in_=V,
                        in_offset=bass.IndirectOffsetOnAxis(ap=eib[:, k:k + 1], axis=0)),
                        reads=[rei], writes=[('vb', s)], dma=True)
                    if k == 0:
                        P.op('dve', lambda e, s=s, acc=acc: e.tensor_scalar(out=acc[:], in0=vb[s][:], scalar1=coef[:, 0:1], scalar2=None, op0=ALU.mult),
                             reads=[('vb', s), 'coef'], writes=[racc])
                    else:
                        P.op('dve', lambda e, s=s, k=k, acc=acc: e.scalar_tensor_tensor(out=acc[:], in0=vb[s][:], scalar=coef[:, k:k + 1], in1=acc[:],
                                                                                      op0=ALU.mult, op1=ALU.add),
                             reads=[('vb', s), 'coef', racc], writes=[racc])
                P.op('pool', lambda e, acc=acc: e.tensor_tensor(out=acc[:], in0=acc[:], in1=self.gate_bc[:], op=ALU.mult), reads=[racc, 'gate_bc'], writes=[racc])
                P.op('dve', lambda e, acc=acc, xt=xt: e.scalar_tensor_tensor(out=acc[:], in0=xt[:], scalar=ALPHA, in1=acc[:], op0=ALU.mult, op1=ALU.add),
                     reads=[rx, racc], writes=[racc])
                self.layernorm_inplace(acc[:], racc)
                P.op('sp', lambda e, acc=acc, ti=ti: e.dma_start(out=dst[ti * 128:(ti + 1) * 128, :], in_=acc[:]), reads=[racc], dma=True)

    def emit_peer1a(self, xin, L):
        P, nc, A, ps = self.P, self.nc, self.A, self.ps
        with Stage(self, f'pa{L}') as st:
            wq = st.T('wq', [128, 8, 2048])
            skT = st.T('skT', [128, 2, 128])
            xts = [st.T(f'xt{i}', [128, D]) for i in range(2)]
            hts = [st.T(f'ht{i}', [128, D]) for i in range(2)]
            hT = st.T('hT', [128, 8, 256])
            qT = st.T('qT', [128, 16, 256])
            sco = [st.T(f'sco{i}', [128, 2048]) for i in range(2)]
            for q in range(4):
                P.op('sp', lambda e, q=q: e.dma_start(out=wq[:, :, q * 512:(q + 1) * 512],
                                                      in_=A['peer_query_w'][L][:, q * 512:(q + 1) * 512].rearrange("(k p) n -> p k n", p=128)),
                     writes=[('wq', q)], dma=True)
            P.op('sp', lambda e: e.dma_start(out=skT[:], in_=A['peer_skT'][L].rearrange("h d k -> d h k")), writes=['skT'], dma=True)
            ti = 0
            for jb in range(S // 256):
                for tl in range(2):
                    xt, ht = xts[ti % 2], hts[ti % 2]
                    rx, rh = f'xt{ti % 2}', f'ht{ti % 2}'
                    P.op('sp', lambda e, xt=xt, ti=ti: e.dma_start(out=xt[:], in_=xin[ti * 128:(ti + 1) * 128, :]), writes=[rx], dma=True)
                    self.modulate(xt[:], ht[:], rx, rh)
                    P.op('sp', lambda e, ht=ht, ti=ti: e.dma_start(out=A['H'][ti * 128:(ti + 1) * 128, :], in_=ht[:]), reads=[rh], writes=[('H', ti)], dma=True)
                    self.transpose8(ht, rh, hT, 'hT', tl * 128, 0)
                    ti += 1
                for c in range(16):
                    bank = ps[2 + c % 2]
                    rb = f'ps{2 + c % 2}'
                    for k in range(8):
                        P.op('pe', lambda e, k=k, c=c, bank=bank: e.matmul(out=bank[:, 0:256], lhsT=wq[:, k, c * 128:(c + 1) * 128], rhs=hT[:, k, :],
                                                                          start=(k == 0), stop=(k == 7)),
                             reads=[('wq', c // 4), 'hT'], writes=[rb])
                    if c % 2 == 0:
                        P.op('act', lambda e, c=c, bank=bank: e.copy(out=qT[:, c, :], in_=bank[:, 0:256]), reads=[rb], writes=[('qT', c)])
                    else:
                        P.op('dve', lambda e, c=c, bank=bank: e.tensor_copy(out=qT[:, c, :], in_=bank[:, 0:256]), reads=[rb], writes=[('qT', c)])
                for tl in range(2):
                    tix = jb * 2 + tl
                    so = sco[tix % 2]
                    rso = f'sco{tix % 2}'
                    for c in range(16):
                        bank = ps[4 + c // 4]
                        rb = f'ps{4 + c // 4}'
                        P.op('pe', lambda e, c=c, bank=bank, tl=tl: e.matmul(out=bank[:, (c % 4) * 128:(c % 4 + 1) * 128],
                                                                            lhsT=qT[:, c, tl * 128:(tl + 1) * 128], rhs=skT[:, c % 2, :],
                                                                            start=True, stop=True),
                             reads=[('qT', c), 'skT'], writes=[rb])
                    for g4 in range(4):
                        eng = ('act', 'dve')[g4 % 2]
                        if eng == 'act':
                            P.op('act', lambda e, g4=g4, so=so: e.copy(out=so[:, g4 * 512:(g4 + 1) * 512], in_=ps[4 + g4][:]), reads=[f'ps{4 + g4}'], writes=[rso])
                        else:
                            P.op('dve', lambda e, g4=g4, so=so: e.tensor_copy(out=so[:, g4 * 512:(g4 + 1) * 512], in_=ps[4 + g4][:]), reads=[f'ps{4 + g4}'], writes=[rso])
                    P.op('sp', lambda e, so=so, tix=tix: e.dma_start(out=A['SCR'][tix * 128:(tix + 1) * 128, :], in_=so[:]), reads=[rso], writes=[('SCR', tix)], dma=True)

    def emit_peer2f(self, xin, dst, L):
        P, nc, A, ps = self.P, self.nc, self.A, self.ps
        NB = self.cfg.get('nb', 11)
        U = A['peer_u'].rearrange("l e d -> (l e) d")
        V = A['peer_v'].rearrange("l e d -> (l e) d")
        with Stage(self, f'pf{L}') as st:
            scs = [st.T(f'sc{i}', [128, 16, 128]) for i in range(2)]
            m = st.T('m', [128, 16, 16])
            ix = st.T('ix', [128, 16, 16], U32)
            ixf = st.T('ixf', [128, 16, 16])
            wk = st.T('wk', [128, 16, 128])
            cand = st.T('cand', [128, 8, 256])
            candi = st.T('candi', [128, 8, 256])
            wk2 = st.T('wk2', [128, 8, 256])
            junk2 = [st.T(f'junk2{i}', [128, 256]) for i in range(2)]
            ts = st.T('ts', [128, 8, 16])
            ef = st.T('ef', [128, 128])
            eis = [st.T(f'ei{i}', [128, 128], I32) for i in range(2)]
            gts = [st.T(f'gt{i}', [128, 8, 16]) for i in range(2)]
            gsum = st.T('gsum', [128, 8])
            xts = [st.T(f'xt{i}', [128, D]) for i in range(2)]
            hts = [st.T(f'ht{i}', [128, D]) for i in range(2)]
            ub = [st.T(f'ub{i}', [128, D]) for i in range(NB)]
            vb = [st.T(f'vb{i}', [128, D]) for i in range(NB)]
            junk = st.T('junk', [128, D])
            apre = st.T('apre', [128, 128])
            coef = st.T('coef', [128, 128])
            accs = [st.T(f'acc{i}', [128, D]) for i in range(2)]

            def load_sc(t):
                P.op('sp', lambda e, t=t: e.dma_start(out=scs[t % 2][:].rearrange("p a k -> p (a k)"), in_=A['SCR'][t * 128:(t + 1) * 128, :]),
                     writes=[f'sc{t % 2}'], dma=True)

            def load_xh(t):
                P.op('sp', lambda e, t=t: e.dma_start(out=xts[t % 2][:], in_=xin[t * 128:(t + 1) * 128, :]), writes=[f'xt{t % 2}'], dma=True)
                P.op('sp', lambda e, t=t: e.dma_start(out=hts[t % 2][:], in_=A['H'][t * 128:(t + 1) * 128, :]), writes=[f'ht{t % 2}'], dma=True)

            def topk_gen(t):
                sc = scs[t % 2]
                rsc = f'sc{t % 2}'
                eib, gtb = eis[t % 2], gts[t % 2]
                rei, rgt = f'ei{t % 2}', f'gt{t % 2}'
                for c in range(16):
                    P.op('dve', lambda e, c=c: e.max(out=m[:, c, 0:8], in_=sc[:, c, :]), reads=[rsc], writes=[('m0', c)])
                    yield
                P.fence('dve')
                for c in range(16):
                    P.op('dve', lambda e, c=c: e.max_index(out=ix[:, c, 0:8], in_max=m[:, c, 0:8], in_values=sc[:, c, :]),
                         reads=[rsc, ('m0', c)], writes=[('ix0', c)])
                    yield
                    P.op('dve', lambda e, c=c: e.match_replace(out=wk[:, c, :], in_to_replace=m[:, c, 0:8], in_values=sc[:, c, :], imm_value=-1e30),
                         reads=[rsc, ('m0', c)], writes=[('wk', c)])
                    yield
                P.fence('dve')
                for c in range(16):
                    P.op('dve', lambda e, c=c: e.max(out=m[:, c, 8:16], in_=wk[:, c, :]), reads=[('wk', c)], writes=[('m1', c)])
                    yield
                P.fence('dve')
                for c in range(16):
                    P.op('dve', lambda e, c=c: e.max_index(out=ix[:, c, 8:16], in_max=m[:, c, 8:16], in_values=wk[:, c, :]),
                         reads=[('wk', c), ('m1', c)], writes=[('ix1', c)])
                    yield
                P.fence('dve')
                mres = [('m0', c) for c in range(16)] + [('m1', c) for c in range(16)]
                ixres = [('ix0', c) for c in range(16)] + [('ix1', c) for c in range(16)]
                P.op('dve', lambda e: e.tensor_copy(out=ixf[:], in_=ix[:]), reads=ixres, writes=['ixf'])
                yield
                m4 = m[:].rearrange("p (h two) k -> p h two k", two=2)
                i4 = ixf[:].rearrange("p (h two) k -> p h two k", two=2)
                c4 = cand[:].rearrange("p h (a b) -> p h a b", a=16)
                ci4 = candi[:].rearrange("p h (a b) -> p h a b", a=16)
                P.op('dve', lambda e: e.tensor_tensor(out=c4, in0=m4[:, :, 0, :].unsqueeze(3).to_broadcast([128, 8, 16, 16]),
                                                      in1=m4[:, :, 1, :].unsqueeze(2).to_broadcast([128, 8, 16, 16]), op=ALU.add),
                     reads=mres, writes=['cand'])
                yield
                P.op('dve', lambda e: e.tensor_scalar(out=i4[:, :, 0, :], in0=i4[:, :, 0, :], scalar1=128.0, scalar2=None, op0=ALU.mult),
                     reads=['ixf'], writes=['ixf'])
                yield
                P.op('dve', lambda e: e.tensor_tensor(out=ci4, in0=i4[:, :, 0, :].unsqueeze(3).to_broadcast([128, 8, 16, 16]),
                                                      in1=i4[:, :, 1, :].unsqueeze(2).to_broadcast([128, 8, 16, 16]), op=ALU.add),
                     reads=['ixf'], writes=['candi'])
                yield
                for h in range(8):
                    P.op('dve', lambda e, h=h: e.max(out=ts[:, h, 0:8], in_=cand[:, h, :]), reads=['cand'], writes=[('ts0', h)])
                    yield
                P.fence('dve')
                for h in range(8):
                    P.op('dve', lambda e, h=h: e.match_replace(out=wk2[:, h, :], in_to_replace=ts[:, h, 0:8], in_values=cand[:, h, :], imm_value=-1e30),
                         reads=['cand', ('ts0', h)], writes=[('wk2', h)])
                    yield
                P.fence('dve')
                for h in range(8):
                    P.op('dve', lambda e, h=h: e.max(out=ts[:, h, 8:16], in_=wk2[:, h, :]), reads=[('wk2', h)], writes=[('ts1', h)])
                    yield
                P.fence('dve')
                tsres = [('ts0', h) for h in range(8)] + [('ts1', h) for h in range(8)]
                for h in range(8):
                    for k in range(16):
                        P.op('dve', lambda e, h=h, k=k: e.scalar_tensor_tensor(out=junk2[(h * 16 + k) % 2][:], in0=cand[:, h, :], scalar=ts[:, h, k:k + 1], in1=candi[:, h, :],
                                                                               op0=ALU.is_equal, op1=ALU.mult, accum_out=ef[:, h * 16 + k:h * 16 + k + 1]),
                             reads=['cand', 'candi', ('ts0', h), ('ts1', h)], writes=[('ef', h * 16 + k)])
                        yield
                P.fence('dve')
                P.op('dve', lambda e: e.tensor_scalar(out=ef[:], in0=ef[:], scalar1=float(NEXP - 1), scalar2=float(L * NEXP), op0=ALU.min, op1=ALU.add),
                     reads=[('ef', q) for q in range(128)], writes=['ef'])
                yield
                P.op('dve', lambda e: e.tensor_copy(out=eib[:], in_=ef[:]), reads=['ef'], writes=[rei])
                yield
                P.op('dve', lambda e: e.tensor_tensor(out=gtb[:], in0=ts[:], in1=ts[:, :, 0:1].to_broadcast([128, 8, 16]), op=ALU.subtract),
                     reads=tsres, writes=[rgt])
                yield
                P.op('act', lambda e: e.activation(out=gtb[:], in_=gtb[:], func=AF.Exp), reads=[rgt], writes=[rgt])
                P.op('dve', lambda e: e.tensor_reduce(out=gsum[:], in_=gtb[:], axis=AX.X, op=ALU.add), reads=[rgt], writes=['gsum'])
                yield
                P.op('dve', lambda e: e.reciprocal(out=gsum[:], in_=gsum[:]), reads=['gsum'], writes=['gsum'])
                yield
                P.op('dve', lambda e: e.tensor_tensor(out=gtb[:], in0=gtb[:], in1=gsum[:].unsqueeze(2).to_broadcast([128, 8, 16]), op=ALU.mult),
                     reads=[rgt, 'gsum'], writes=[rgt])
                yield

            def step(gen, n=1):
                if gen is None:
                    return None
                try:
                    for _ in range(n):
                        next(gen)
                except StopIteration:
                    return None
                return gen

            load_sc(0)
            load_sc(1)
            load_xh(0)
            g0 = topk_gen(0)
            while g0 is not None:
                g0 = step(g0, 64)
            nu = nv = 0
            for ti in range(NT):
                b = ti % 2
                xt, ht, eib, gtb, acc = xts[b], hts[b], eis[b], gts[b], accs[b]
                rx, rh, rei, rgt, racc = f'xt{b}', f'ht{b}', f'ei{b}', f'gt{b}', f'acc{b}'
                gt2 = gtb[:].rearrange("p h k -> p (h k)")
                if ti + 1 < NT:
                    load_xh(ti + 1)
                gen = topk_gen(ti + 1) if ti + 1 < NT else None
                for k in range(128):
                    s_ = nu % NB
                    nu += 1
                    P.op('pool', lambda e, s_=s_, k=k, eib=eib: e.indirect_dma_start(
                        out=ub[s_][:], out_offset=None, in_=U,
                        in_offset=bass.IndirectOffsetOnAxis(ap=eib[:, k:k + 1], axis=0)),
                        reads=[rei], writes=[('ub', s_)], dma=True)
                    P.op('dve', lambda e, s_=s_, k=k, ht=ht: e.scalar_tensor_tensor(out=junk[:], in0=ub[s_][:], scalar=1.0, in1=ht[:], op0=ALU.mult, op1=ALU.mult,
                                                                                 accum_out=apre[:, k:k + 1]),
                         reads=[('ub', s_), rh], writes=[('apre', k)])
                    gen = step(gen)
                P.op('act', lambda e: e.activation(out=coef[:], in_=apre[:], func=AF.Gelu), reads=[('apre', k) for k in range(128)], writes=['coef'])
                P.op('dve', lambda e, gt2=gt2: e.tensor_tensor(out=coef[:], in0=coef[:], in1=gt2, op=ALU.mult), reads=['coef', rgt], writes=['coef'])
                for k in range(128):
                    s_ = nv % NB
                    nv += 1
                    P.op('pool', lambda e, s_=s_, k=k, eib=eib: e.indirect_dma_start(
                        out=vb[s_][:], out_offset=None, in_=V,
                        in_offset=bass.IndirectOffsetOnAxis(ap=eib[:, k:k + 1], axis=0)),
                        reads=[rei], writes=[('vb', s_)], dma=True)
                    if k == 0:
                        P.op('dve', lambda e, s_=s_, acc=acc: e.tensor_scalar(out=acc[:], in0=vb[s_][:], scalar1=coef[:, 0:1], scalar2=None, op0=ALU.mult),
                             reads=[('vb', s_), 'coef'], writes=[racc])
                    else:
                        P.op('dve', lambda e, s_=s_, k=k, acc=acc: e.scalar_tensor_tensor(out=acc[:], in0=vb[s_][:], scalar=coef[:, k:k + 1], in1=acc[:],
                                                                                       op0=ALU.mult, op1=ALU.add),
                             reads=[('vb', s_), 'coef', racc], writes=[racc])
                    gen = step(gen)
                while gen is not None:
                    gen = step(gen, 64)
                if ti + 2 < NT:
                    load_sc(ti + 2)
                P.op('dve', lambda e, acc=acc: e.tensor_tensor(out=acc[:], in0=acc[:], in1=self.gate_bc[:], op=ALU.mult), reads=[racc, 'gate_bc'], writes=[racc])
                P.op('dve', lambda e, acc=acc, xt=xt: e.scalar_tensor_tensor(out=acc[:], in0=xt[:], scalar=ALPHA, in1=acc[:], op0=ALU.mult, op1=ALU.add),
                     reads=[rx, racc], writes=[racc])
                self.layernorm_inplace(acc[:], racc, gb='dve')
                P.op('sp', lambda e, acc=acc, ti=ti: e.dma_start(out=dst[ti * 128:(ti + 1) * 128, :], in_=acc[:]), reads=[racc], dma=True)

    def emit_cast_tables(self):
        P, nc, A = self.P, self.nc, self.A
        CN = 4
        Uv = A['peer_u'].rearrange("l (n p) d -> p (l n) d", p=128)
        Vv = A['peer_v'].rearrange("l (n p) d -> p (l n) d", p=128)
        Ov = A['UVB'].rearrange("(n p) d -> p n d", p=128)
        with Stage(self, 'cast') as st:
            iu = [st.T(f'iu{i}', [128, CN, D]) for i in range(2)]
            iv = [st.T(f'iv{i}', [128, CN, D]) for i in range(2)]
            ob = [st.T(f'ob{i}', [128, CN, 2 * D], BF16) for i in range(2)]
            nchunk = (2 * NEXP // 128) // CN
            for ci in range(nchunk):
                b = ci % 2
                ns = slice(ci * CN, (ci + 1) * CN)
                P.op('sp', lambda e, b=b, ns=ns: e.dma_start(out=iu[b][:], in_=Uv[:, ns, :]), writes=[f'iu{b}'], dma=True)
                P.op('act', lambda e, b=b, ns=ns: e.dma_start(out=iv[b][:], in_=Vv[:, ns, :]), writes=[f'iv{b}'], dma=True)
                P.op('dve', lambda e, b=b: e.tensor_copy(out=ob[b][:, :, 0:D], in_=iu[b][:]), reads=[f'iu{b}'], writes=[(f'ob{b}', 0)])
                P.op('pool', lambda e, b=b: e.tensor_copy(out=ob[b][:, :, D:2 * D], in_=iv[b][:]), reads=[f'iv{b}'], writes=[(f'ob{b}', 1)])
                P.op('sp', lambda e, b=b, ns=ns: e.dma_start(out=Ov[:, ns, :], in_=ob[b][:]), reads=[(f'ob{b}', 0), (f'ob{b}', 1)], writes=[('UVB', ci)], dma=True)

    def emit_peer2g(self, xin, dst, L):
        P, nc, A, ps = self.P, self.nc, self.A, self.ps
        NB = self.cfg.get('nb', 20)
        GS = 16
        UVB = A['UVB']
        with Stage(self, f'pg{L}') as st:
            scs = [st.T(f'sc{i}', [128, 16, 128]) for i in range(2)]
            m = st.T('m', [128, 16, 16])
            ix = st.T('ix', [128, 16, 16], U32)
            ixf = st.T('ixf', [128, 16, 16])
            wk = st.T('wk', [128, 16, 128])
            cand = st.T('cand', [128, 8, 256])
            candi = st.T('candi', [128, 8, 256])
            wk2 = st.T('wk2', [128, 8, 256])
            junk2 = [st.T(f'junk2{i}', [128, 256]) for i in range(2)]
            ts = st.T('ts', [128, 8, 16])
            ef = st.T('ef', [128, 128])
            eis = [st.T(f'ei{i}', [128, 128], I32) for i in range(2)]
            gts = [st.T(f'gt{i}', [128, 8, 16]) for i in range(2)]
            gsum = st.T('gsum', [128, 8])
            xts = [st.T(f'xt{i}', [128, D]) for i in range(2)]
            hts = [st.T(f'ht{i}', [128, D]) for i in range(2)]
            uvb = [st.T(f'uv{i}', [128, 2 * D], BF16) for i in range(NB)]
            junk = st.T('junk', [128, D], BF16)
            apre = st.T('apre', [128, 128])
            ge = st.T('ge', [128, 128])
            cf = st.T('cf', [128, 128])
            dgs = [st.T(f'dg{i}', [128, 128], BF16) for i in range(4)]
            accs = [st.T(f'acc{i}', [128, D]) for i in range(2)]

            def load_sc(t):
                P.op('sp', lambda e, t=t: e.dma_start(out=scs[t % 2][:].rearrange("p a k -> p (a k)"), in_=A['SCR'][t * 128:(t + 1) * 128, :]),
                     writes=[f'sc{t % 2}'], dma=True)

            def load_xh(t):
                P.op('sp', lambda e, t=t: e.dma_start(out=xts[t % 2][:], in_=xin[t * 128:(t + 1) * 128, :]), writes=[f'xt{t % 2}'], dma=True)
                P.op('sp', lambda e, t=t: e.dma_start(out=hts[t % 2][:], in_=A['H'][t * 128:(t + 1) * 128, :]), writes=[f'ht{t % 2}'], dma=True)

            def topk_gen(t):
                sc = scs[t % 2]
                rsc = f'sc{t % 2}'
                eib, gtb = eis[t % 2], gts[t % 2]
                rei, rgt = f'ei{t % 2}', f'gt{t % 2}'
                for c in range(16):
                    P.op('dve', lambda e, c=c: e.max(out=m[:, c, 0:8], in_=sc[:, c, :]), reads=[rsc], writes=[('m0', c)])
                    yield
                P.fence('dve')
                for c in range(16):
                    P.op('dve', lambda e, c=c: e.max_index(out=ix[:, c, 0:8], in_max=m[:, c, 0:8], in_values=sc[:, c, :]),
                         reads=[rsc, ('m0', c)], writes=[('ix0', c)])
                    yield
                    P.op('dve', lambda e, c=c: e.match_replace(out=wk[:, c, :], in_to_replace=m[:, c, 0:8], in_values=sc[:, c, :], imm_value=-1e30),
                         reads=[rsc, ('m0', c)], writes=[('wk', c)])
                    yield
                P.fence('dve')
                for c in range(16):
                    P.op('dve', lambda e, c=c: e.max(out=m[:, c, 8:16], in_=wk[:, c, :]), reads=[('wk', c)], writes=[('m1', c)])
                    yield
                P.fence('dve')
                for c in range(16):
                    P.op('dve', lambda e, c=c: e.max_index(out=ix[:, c, 8:16], in_max=m[:, c, 8:16], in_values=wk[:, c, :]),
                         reads=[('wk', c), ('m1', c)], writes=[('ix1', c)])
                    yield
                P.fence('dve')
                mres = [('m0', c) for c in range(16)] + [('m1', c) for c in range(16)]
                ixres = [('ix0', c) for c in range(16)] + [('ix1', c) for c in range(16)]
                P.op('dve', lambda e: e.tensor_copy(out=ixf[:], in_=ix[:]), reads=ixres, writes=['ixf'])
                yield
                m4 = m[:].rearrange("p (h two) k -> p h two k", two=2)
                i4 = ixf[:].rearrange("p (h two) k -> p h two k", two=2)
                c4 = cand[:].rearrange("p h (a b) -> p h a b", a=16)
                ci4 = candi[:].rearrange("p h (a b) -> p h a b", a=16)
                P.op('dve', lambda e: e.tensor_tensor(out=c4, in0=m4[:, :, 0, :].unsqueeze(3).to_broadcast([128, 8, 16, 16]),
                                                      in1=m4[:, :, 1, :].unsqueeze(2).to_broadcast([128, 8, 16, 16]), op=ALU.add),
                     reads=mres, writes=['cand'])
                yield
                P.op('dve', lambda e: e.tensor_scalar(out=i4[:, :, 0, :], in0=i4[:, :, 0, :], scalar1=128.0, scalar2=None, op0=ALU.mult),
                     reads=['ixf'], writes=['ixf'])
                yield
                P.op('dve', lambda e: e.tensor_tensor(out=ci4, in0=i4[:, :, 0, :].unsqueeze(3).to_broadcast([128, 8, 16, 16]),
                                                      in1=i4[:, :, 1, :].unsqueeze(2).to_broadcast([128, 8, 16, 16]), op=ALU.add),
                     reads=['ixf'], writes=['candi'])
                yield
                for h in range(8):
                    P.op('dve', lambda e, h=h: e.max(out=ts[:, h, 0:8], in_=cand[:, h, :]), reads=['cand'], writes=[('ts0', h)])
                    yield
                P.fence('dve')
                for h in range(8):
                    P.op('dve', lambda e, h=h: e.match_replace(out=wk2[:, h, :], in_to_replace=ts[:, h, 0:8], in_values=cand[:, h, :], imm_value=-1e30),
                         reads=['cand', ('ts0', h)], writes=[('wk2', h)])
                    yield
                P.fence('dve')
                for h in range(8):
                    P.op('dve', lambda e, h=h: e.max(out=ts[:, h, 8:16], in_=wk2[:, h, :]), reads=[('wk2', h)], writes=[('ts1', h)])
                    yield
                P.fence('dve')
                tsres = [('ts0', h) for h in range(8)] + [('ts1', h) for h in range(8)]
                for h in range(8):
                    for k in range(16):
                        P.op('dve', lambda e, h=h, k=k: e.scalar_tensor_tensor(out=junk2[(h * 16 + k) % 2][:], in0=cand[:, h, :], scalar=ts[:, h, k:k + 1], in1=candi[:, h, :],
                                                                               op0=ALU.is_equal, op1=ALU.mult, accum_out=ef[:, h * 16 + k:h * 16 + k + 1]),
                             reads=['cand', 'candi', ('ts0', h), ('ts1', h)], writes=[('ef', h * 16 + k)])
                        yield
                P.fence('dve')
                P.op('dve', lambda e: e.tensor_scalar(out=ef[:], in0=ef[:], scalar1=float(NEXP - 1), scalar2=float(L * NEXP), op0=ALU.min, op1=ALU.add),
                     reads=[('ef', q) for q in range(128)], writes=['ef'])
                yield
                P.op('dve', lambda e: e.tensor_copy(out=eib[:], in_=ef[:]), reads=['ef'], writes=[rei])
                yield
                P.op('dve', lambda e: e.tensor_tensor(out=gtb[:], in0=ts[:], in1=ts[:, :, 0:1].to_broadcast([128, 8, 16]), op=ALU.subtract),
                     reads=tsres, writes=[rgt])
                yield
                P.op('act', lambda e: e.activation(out=gtb[:], in_=gtb[:], func=AF.Exp), reads=[rgt], writes=[rgt])
                P.op('dve', lambda e: e.tensor_reduce(out=gsum[:], in_=gtb[:], axis=AX.X, op=ALU.add), reads=[rgt], writes=['gsum'])
                yield
                P.op('dve', lambda e: e.reciprocal(out=gsum[:], in_=gsum[:]), reads=['gsum'], writes=['gsum'])
                yield
                P.op('dve', lambda e: e.tensor_tensor(out=gtb[:], in0=gtb[:], in1=gsum[:].unsqueeze(2).to_broadcast([128, 8, 16]), op=ALU.mult),
                     reads=[rgt, 'gsum'], writes=[rgt])
                yield

            def step(gen, n=1):
                if gen is None:
                    return None
                try:
                    for _ in range(n):
                        next(gen)
                except StopIteration:
                    return None
                return gen

            load_sc(0)
            load_sc(1)
            load_xh(0)
            g0 = topk_gen(0)
            while g0 is not None:
                g0 = step(g0, 64)
            nu = 0
            ndg = 0
            for ti in range(NT):
                b = ti % 2
                xt, ht, eib, gtb, acc = xts[b], hts[b], eis[b], gts[b], accs[b]
                rx, rh, rei, rgt, racc = f'xt{b}', f'ht{b}', f'ei{b}', f'gt{b}', f'acc{b}'
                gt2 = gtb[:].rearrange("p h k -> p (h k)")
                pa = [ps[(ti % 2) * 2], ps[(ti % 2) * 2 + 1]]
                rpa = [f'ps{(ti % 2) * 2}', f'ps{(ti % 2) * 2 + 1}']
                if ti + 1 < NT:
                    load_xh(ti + 1)
                gen = topk_gen(ti + 1) if ti + 1 < NT else None
                for g in range(128 // GS):
                    slots = []
                    for kk in range(GS):
                        k = g * GS + kk
                        s_ = nu % NB
                        nu += 1
                        slots.append(s_)
                        P.op('pool', lambda e, s_=s_, k=k, eib=eib: e.indirect_dma_start(
                            out=uvb[s_][:], out_offset=None, in_=UVB,
                            in_offset=bass.IndirectOffsetOnAxis(ap=eib[:, k:k + 1], axis=0)),
                            reads=[rei], writes=[('uv', s_)], dma=True)
                        P.op('dve', lambda e, s_=s_, k=k, ht=ht: e.scalar_tensor_tensor(out=junk[:], in0=uvb[s_][:, 0:D], scalar=1.0, in1=ht[:], op0=ALU.mult, op1=ALU.mult,
                                                                                     accum_out=apre[:, k:k + 1]),
                             reads=[('uv', s_), rh], writes=[('apre', k)])
                        gen = step(gen, 2)
                    gsl = slice(g * GS, (g + 1) * GS)
                    P.op('act', lambda e, gsl=gsl: e.activation(out=ge[:, gsl], in_=apre[:, gsl], func=AF.Gelu),
                         reads=[('apre', k) for k in range(g * GS, (g + 1) * GS)], writes=[('ge', g)])
                    P.op('dve', lambda e, gsl=gsl, gt2=gt2: e.tensor_tensor(out=cf[:, gsl], in0=ge[:, gsl], in1=gt2[:, gsl], op=ALU.mult),
                         reads=[('ge', g), rgt], writes=[('cf', g)])
                    for kk in range(GS):
                        k = g * GS + kk
                        s_ = slots[kk]
                        dg = dgs[ndg % 4]
                        rdg = f'dg{ndg % 4}'
                        ndg += 1
                        P.op('act', lambda e, dg=dg, k=k: e.activation(out=dg[:], in_=self.ident[:], func=AF.Copy, scale=cf[:, k:k + 1]),
                             reads=['ident', ('cf', g)], writes=[rdg])
                        for half in range(2):
                            P.op('pe', lambda e, dg=dg, s_=s_, half=half, k=k, pa=pa: e.matmul(out=pa[half][:], lhsT=dg[:], rhs=uvb[s_][:, D + half * 512:D + (half + 1) * 512],
                                                                                           start=(k == 0), stop=(k == 127)),
                                 reads=[rdg, ('uv', s_)], writes=[rpa[half]])
                while gen is not None:
                    gen = step(gen, 64)
                if ti + 2 < NT:
                    load_sc(ti + 2)
                for half in range(2):
                    P.op('dve', lambda e, acc=acc, half=half, pa=pa: e.tensor_tensor(out=acc[:, half * 512:(half + 1) * 512], in0=pa[half][:],
                                                                                   in1=self.gate_bc[:, half * 512:(half + 1) * 512], op=ALU.mult),
                         reads=[rpa[half], 'gate_bc'], writes=[racc])
                P.op('dve', lambda e, acc=acc, xt=xt: e.scalar_tensor_tensor(out=acc[:], in0=xt[:], scalar=ALPHA, in1=acc[:], op0=ALU.mult, op1=ALU.add),
                     reads=[rx, racc], writes=[racc])
                self.layernorm_inplace(acc[:], racc, gb='dve')
                P.op('sp', lambda e, acc=acc, ti=ti: e.dma_start(out=dst[ti * 128:(ti + 1) * 128, :], in_=acc[:]), reads=[racc], dma=True)

    def emit_attn1(self, xin):
        P, nc, A, ps = self.P, self.nc, self.A, self.ps
        ones = self.ones
        NCOL = 3 * D + 16
        with Stage(self, 'a1') as st:
            winr = st.T('winr', [128, 8, 3 * D], F32R)
            wstg = [st.T('wstg0', [128, 8, 512])] * 2
            wf = st.T('wf', [128, 8, 16])
            qkb = st.T('qkb', [128, 16])
            vbr = st.T('vbr', [1, D])
            vb_bc = st.T('vb_bc', [128, D])
            fb = st.T('fb', [16, 1])
            xts = [st.T(f'xt{i}', [128, D]) for i in range(2)]
            ht = st.T('ht', [128, D])
            hT = st.T('hT', [128, 8, 512], F32R)
            qko = [st.T(f'qko{i}', [128, 512]) for i in range(2)]
            vo = [st.T(f'vo{i}', [128, D]) for i in range(2)]
            Fcb = [st.T(f'Fcb{i}', [16, 512]) for i in range(2)]
            Frb = st.T('Frb', [16, 512], F32R)
            Flb = st.T('Flb', [16, 512])
            nFr = st.T('nFr', [16, 512])
            nFl = st.T('nFl', [16, 512])
            spt = st.T('spt', [16, 512])
            o16 = st.T('o16', [16, 512])
            for q in range(6):
                wb = wstg[0]
                rw = 'wstg0'
                P.op('sp', lambda e, q=q, wb=wb: e.dma_start(out=wb[:], in_=A['attn_in_w'][:, q * 512:(q + 1) * 512].rearrange("(k p) n -> p k n", p=128)),
                     writes=[rw], dma=True)
                eng = ('dve', 'pool')[q % 2]
                P.op(eng, lambda e, q=q, wb=wb: e.tensor_copy(out=winr[:, :, q * 512:(q + 1) * 512], in_=wb[:]), reads=[rw], writes=[('win', q)])
            P.op('sp', lambda e: e.dma_start(out=wf[:], in_=A['attn_in_w'][:, 3 * D:NCOL].rearrange("(k p) n -> p k n", p=128)),
                 writes=['wf'], dma=True)
            P.op('sp', lambda e: e.dma_start(out=qkb[:], in_=A['attn_qkb_l']), writes=['qkb'], dma=True)
            P.op('sp', lambda e: e.dma_start(out=vbr[:], in_=A['attn_vb']), writes=['vbr'], dma=True)
            P.op('sp', lambda e: e.dma_start(out=fb[:], in_=A['attn_fb']), writes=['fb'], dma=True)
            P.op('dve', lambda e: e.tensor_scalar(out=qkb[:, 0:8], in0=qkb[:, 0:8], scalar1=0.125, scalar2=None, op0=ALU.mult), reads=['qkb'], writes=['qkb'])
            P.op('dve', lambda e: e.tensor_scalar(out=fb[:], in0=fb[:], scalar1=-1.0, scalar2=None, op0=ALU.mult), reads=['fb'], writes=['fb'])
            P.op('pool', lambda e: e.memset(o16[:], 1.0), writes=['o16'])
            for half in range(2):
                P.op('pe', lambda e, half=half: e.matmul(out=ps[4 + half][:], lhsT=ones[0:1, :], rhs=vbr[0:1, half * 512:(half + 1) * 512], start=True, stop=True),
                     reads=['ones', 'vbr'], writes=[f'ps{4 + half}'])
                P.op('act', lambda e, half=half: e.copy(out=vb_bc[:, half * 512:(half + 1) * 512], in_=ps[4 + half][:]), reads=[f'ps{4 + half}'], writes=['vb_bc'])
            ti = 0
            for jb in range(8):
                cols = slice(jb * 512, (jb + 1) * 512)
                for tl in range(4):
                    xt = xts[ti % 2]
                    rx = f'xt{ti % 2}'
                    P.op('sp', lambda e, xt=xt, ti=ti: e.dma_start(out=xt[:], in_=xin[ti * 128:(ti + 1) * 128, :]), writes=[rx], dma=True)
                    self.modulate(xt[:], ht[:], rx, 'ht')
                    self.transpose8(ht, 'ht', hT, 'hT', tl * 128, 0)
                    ti += 1
                for c in range(16):
                    bank = ps[2 + c % 2]
                    rb = f'ps{2 + c % 2}'
                    for k in range(8):
                        P.op('pe', lambda e, k=k, c=c, bank=bank: e.matmul(out=bank[:], lhsT=winr[:, k, c * 128:(c + 1) * 128], rhs=hT[:, k, :],
                                                                          start=(k == 0), stop=(k == 7)),
                             reads=[('win', c // 4), 'hT'], writes=[rb])
                    ob = qko[c % 2]
                    rob = f'qko{c % 2}'
                    P.op('act', lambda e, c=c, bank=bank, ob=ob: e.activation(out=ob[:], in_=bank[:], func=AF.Identity, bias=qkb[:, c:c + 1],
                                                                             scale=(0.125 if c < 8 else 1.0)),
                         reads=[rb, 'qkb'], writes=[rob])
                    dstt = A['QA'] if c < 8 else A['KA']
                    for hh in range(2):
                        head = (c % 8) * 2 + hh
                        P.op('sp', lambda e, ob=ob, hh=hh, head=head, dstt=dstt: e.dma_start(out=dstt[head, 0:64, cols], in_=ob[hh * 64:(hh + 1) * 64, :]),
                             reads=[rob], writes=[('QK', c, hh)], dma=True)
                for tl in range(4):
                    tix = jb * 4 + tl
                    vt = vo[tix % 2]
                    rv = f'vo{tix % 2}'
                    for half in range(2):
                        bank = ps[4 + half]
                        rb = f'ps{4 + half}'
                        for k in range(8):
                            P.op('pe', lambda e, k=k, bank=bank, half=half, tl=tl: e.matmul(out=bank[:], lhsT=hT[:, k, tl * 128:(tl + 1) * 128],
                                                                                        rhs=winr[:, k, 2 * D + half * 512:2 * D + (half + 1) * 512],
                                                                                        start=(k == 0), stop=(k == 7)),
                                 reads=['hT', ('win', 4 + half)], writes=[rb])
                        P.op('dve', lambda e, vt=vt, bank=bank, half=half: e.tensor_tensor(out=vt[:, half * 512:(half + 1) * 512], in0=bank[:],
                                                                                      in1=vb_bc[:, half * 512:(half + 1) * 512], op=ALU.add),
                             reads=[rb, 'vb_bc'], writes=[rv])
                    P.op('sp', lambda e, vt=vt, tix=tix: e.dma_start(out=A['V'][tix * 128:(tix + 1) * 128, :], in_=vt[:]), reads=[rv], writes=[('V', tix)], dma=True)
                for k in range(8):
                    P.op('pe', lambda e, k=k: e.matmul(out=ps[6][0:16, :], lhsT=wf[:, k, :], rhs=hT[:, k, :].bitcast(F32), start=(k == 0), stop=(k == 7)),
                         reads=['wf', 'hT'], writes=['ps6'])
                P.op('act', lambda e: e.activation(out=spt[:], in_=ps[6][0:16, :], func=AF.Exp, bias=fb[:, 0:1], scale=-1.0), reads=['ps6', 'fb'], writes=['spt'])
                P.op('act', lambda e: e.activation(out=spt[:], in_=spt[:], func=AF.Ln, bias=1.0, scale=1.0), reads=['spt'], writes=['spt'])
                P.op('dve', lambda e: e.tensor_scalar(out=spt[:], in0=spt[:], scalar1=-1.0, scalar2=None, op0=ALU.mult), reads=['spt'], writes=['spt'])
                Fc = Fcb[jb % 2]
                rF = f'Fcb{jb % 2}'
                init = 0.0 if jb == 0 else Fcb[(jb - 1) % 2][:, 511:512]
                P.op('dve', lambda e, init=init, Fc=Fc: e.tensor_tensor_scan(out=Fc[:], data0=o16[:], data1=spt[:], initial=init,
                                                                             op0=ALU.mult, op1=ALU.add),
                     reads=['o16', 'spt', f'Fcb{(jb - 1) % 2}'], writes=[rF])
                P.op('dve', lambda e, Fc=Fc: e.tensor_copy(out=Frb[:], in_=Fc[:]), reads=[rF], writes=['Frb'])
                P.op('dve', lambda e, Fc=Fc: e.tensor_tensor(out=Flb[:], in0=Fc[:], in1=Frb[:].bitcast(F32), op=ALU.subtract), reads=[rF, 'Frb'], writes=['Flb'])
                P.op('dve', lambda e: e.tensor_scalar(out=nFr[:], in0=Frb[:].bitcast(F32), scalar1=-1.0, scalar2=None, op0=ALU.mult), reads=['Frb'], writes=['nFr'])
                P.op('dve', lambda e: e.tensor_scalar(out=nFl[:], in0=Flb[:], scalar1=-1.0, scalar2=None, op0=ALU.mult), reads=['Flb'], writes=['nFl'])
                P.op('sp', lambda e: e.dma_start(out=A['QA'][:, 64, cols], in_=Frb[:].bitcast(F32)), reads=['Frb'], writes=[('QAf', jb)], dma=True)
                P.op('sp', lambda e: e.dma_start(out=A['QA'][:, 65, cols], in_=Flb[:]), reads=['Flb'], writes=[('QAl', jb)], dma=True)
                P.op('sp', lambda e: e.dma_start(out=A['KA'][:, 66, cols], in_=nFr[:]), reads=['nFr'], writes=[('KAf', jb)], dma=True)
                P.op('sp', lambda e: e.dma_start(out=A['KA'][:, 67, cols], in_=nFl[:]), reads=['nFl'], writes=[('KAl', jb)], dma=True)
                for r in (66, 67):
                    P.op('sp', lambda e, r=r: e.dma_start(out=A['QA'][:, r, cols], in_=o16[:]), reads=['o16'], writes=[('QAo', r, jb)], dma=True)
                for r in (64, 65):
                    P.op('sp', lambda e, r=r: e.dma_start(out=A['KA'][:, r, cols], in_=o16[:]), reads=['o16'], writes=[('KAo', r, jb)], dma=True)

    def emit_attn2(self):
        P, nc, A, ps = self.P, self.nc, self.A, self.ps
        NR = 68
        with Stage(self, 'a2') as st:
            qst = st.T('qst', [NR, S])
            kst = st.T('kst', [NR, S])
            vst = st.T('vst', [128, 32, 64])
            QAh = [st.T(f'QAh{i}', [NR, S], F32R) for i in range(2)]
            KAh = [st.T(f'KAh{i}', [NR, S], F32R) for i in range(2)]
            Vh = [st.T(f'Vh{i}', [128, 32, 128], F32R) for i in range(2)]
            ones_r = st.T('ones_r', [128, 128], F32R)
            pt = [st.T(f'pt{i}', [128, 512], F32R) for i in range(3)]
            lm = [st.T(f'lm{i}', [128, 512]) for i in range(2)]
            mask = st.T('mask', [128, 4, 512])
            rzt = st.T('rzt', [64, 512])
            oT = [st.T(f'oT{i}', [64, 512]) for i in range(2)]
            P.op('pool', lambda e: e.memset(mask[:], 0.0), writes=['mask'])
            for i4 in range(4):
                P.op('pool', lambda e, i4=i4: e.affine_select(out=mask[:, i4, :], in_=mask[:, i4, :], pattern=[[1, 512]], compare_op=ALU.is_ge,
                                                              fill=NEG, base=-128 * i4, channel_multiplier=-1), reads=['mask'], writes=['mask'])
            P.op('pool', lambda e: e.tensor_copy(out=ones_r[:], in_=self.ones[:]), reads=['ones'], writes=['ones_r'])
            npt = 0
            nlm = 0
            nS = 0
            nO = 0

            def loads(h):
                b = h % 2
                qa, ka, vh = QAh[b], KAh[b], Vh[b]
                rq, rk, rv = f'QAh{b}', f'KAh{b}', f'Vh{b}'
                for q4 in range(4):
                    cs = slice(q4 * 1024, (q4 + 1) * 1024)
                    P.op('sp', lambda e, h=h, cs=cs: e.dma_start(out=qst[:, cs], in_=A['QA'][h, :, cs]), writes=[('qst', q4)], dma=True)
                    P.op('sp', lambda e, h=h, cs=cs: e.dma_start(out=kst[:, cs], in_=A['KA'][h, :, cs]), writes=[('kst', q4)], dma=True)
                    P.op('sp', lambda e, h=h, q4=q4: e.dma_start(
                        out=vst[:, q4 * 8:(q4 + 1) * 8, :],
                        in_=A['V'][q4 * 1024:(q4 + 1) * 1024, h * 64:(h + 1) * 64].rearrange("(i p) d -> p i d", p=128)),
                        writes=[('vst', q4)], dma=True)
                for q4 in range(4):
                    cs = slice(q4 * 1024, (q4 + 1) * 1024)
                    P.op('pool', lambda e, qa=qa, cs=cs: e.tensor_copy(out=qa[:, cs], in_=qst[:, cs]), reads=[('qst', q4)], writes=[rq])
                    P.op('pool', lambda e, ka=ka, cs=cs: e.tensor_copy(out=ka[:, cs], in_=kst[:, cs]), reads=[('kst', q4)], writes=[rk])
                    for dup in range(2):
                        P.op('pool', lambda e, vh=vh, q4=q4, dup=dup: e.tensor_copy(out=vh[:, q4 * 8:(q4 + 1) * 8, dup * 64:(dup + 1) * 64],
                                                                                  in_=vst[:, q4 * 8:(q4 + 1) * 8, :]),
                             reads=[('vst', q4)], writes=[rv])

            loads(0)
            NH = self.cfg.get('nheads', 16)
            steps = [(h, j, i) for h in range(NH) for j in range(8) for i in range(4 * j + 4)]

            def emit_qk(n):
                h, j, i = steps[n]
                b = h % 2
                sb = ps[n % 3]
                P.op('pe', lambda e, sb=sb, ka=KAh[b], qa=QAh[b], i=i, j=j: e.matmul(out=sb[:], lhsT=ka[:, i * 128:(i + 1) * 128], rhs=qa[:, j * 512:(j + 1) * 512],
                                                                                 start=True, stop=True),
                     reads=[f'KAh{b}', f'QAh{b}'], writes=[f'ps{n % 3}'])

            emit_qk(0)
            for n, (h, j, i) in enumerate(steps):
                b = h % 2
                vh, rv = Vh[b], f'Vh{b}'
                if j == 0 and i == 0 and h + 1 < NH:
                    loads(h + 1)
                if n + 1 < len(steps):
                    emit_qk(n + 1)
                if i == 0:
                    oset = nO % 2
                    nO += 1
                poA, poB = ps[3 + 2 * oset], ps[4 + 2 * oset]
                rA, rB = f'ps{3 + 2 * oset}', f'ps{4 + 2 * oset}'
                ot, rot = oT[oset], f'oT{oset}'
                last = 4 * j + 3
                sb, rsb = ps[n % 3], f'ps{n % 3}'
                p_, rp = pt[n % 3], f'pt{n % 3}'
                if i >= 4 * j:
                    l_ = lm[nlm % 2]
                    rl = f'lm{nlm % 2}'
                    nlm += 1
                    P.op('dve', lambda e, l_=l_, sb=sb, i=i, j=j: e.tensor_tensor(out=l_[:], in0=sb[:], in1=mask[:, i - 4 * j, :], op=ALU.add),
                         reads=[rsb, 'mask'], writes=[rl])
                    P.op('act', lambda e, p_=p_, l_=l_: e.activation(out=p_[:], in_=l_[:], func=AF.Exp), reads=[rl], writes=[rp])
                else:
                    P.op('act', lambda e, p_=p_, sb=sb: e.activation(out=p_[:], in_=sb[:], func=AF.Exp), reads=[rsb], writes=[rp])
                P.op('pe', lambda e, p_=p_, i=i, poA=poA, vh=vh, last=last: e.matmul(out=poA[:], lhsT=vh[:, i, :], rhs=p_[:], start=(i == 0), stop=(i == last)),
                     reads=[rp, rv], writes=[rA])
                P.op('pe', lambda e, p_=p_, i=i, poB=poB, last=last: e.matmul(out=poB[:], lhsT=ones_r[:], rhs=p_[:], start=(i == 0), stop=(i == last)),
                     reads=[rp, 'ones_r'], writes=[rB])
                if i == last:
                    P.op('dve', lambda e, poB=poB: e.reciprocal(out=rzt[:], in_=poB[0:64, :]), reads=[rB], writes=['rzt'])
                    P.op('dve', lambda e, poA=poA, ot=ot: e.tensor_tensor(out=ot[:], in0=poA[0:64, :], in1=rzt[:], op=ALU.mult), reads=[rA, 'rzt'], writes=[rot])
                    P.op('sp', lambda e, ot=ot, h=h, j=j: e.dma_start(out=A['AOT'][h // 2, (h % 2) * 64:(h % 2) * 64 + 64, j * 512:(j + 1) * 512], in_=ot[:]),
                         reads=[rot], writes=[('AOT', h, j)], dma=True)


def make_in_maps(inputs, cores=range(8)):
    f = lambda a: np.ascontiguousarray(np.asarray(a, dtype=np.float32))
    sh = {}
    sh['ada_mix_w'] = f(inputs['ada_mix_w'])
    sh['ada_ffn_w'] = f(inputs['ada_ffn_w'])
    amb, afb = f(inputs['ada_mix_b']), f(inputs['ada_ffn_b'])
    sh['ada_b'] = f(np.stack([amb[0], afb[0], amb[1], afb[1]]))
    g1, g2 = f(inputs['ln_mix_g']), f(inputs['ln_ffn_g'])
    b1, b2 = f(inputs['ln_mix_b']), f(inputs['ln_ffn_b'])
    sh['ln_g'] = f(np.stack([g1[0], g2[0], g1[1], g2[1]]))
    sh['ln_b'] = f(np.stack([b1[0], b2[0], b1[1], b2[1]]))
    sh['conv_in_w'] = f(inputs['conv_in_w'][0])
    sh['conv_in_b_l'] = f(np.asarray(inputs['conv_in_b'][0]).reshape(16, 128).T)
    sh['conv_dw_w_l'] = f(np.asarray(inputs['conv_dw_w'][0]).reshape(31, 8, 128).transpose(2, 1, 0))
    sh['conv_vec_l'] = f(np.stack([np.asarray(inputs[k][0]).reshape(8, 128).T for k in ('conv_dw_b', 'conv_ln_g', 'conv_ln_b')], axis=1))
    sh['conv_out_w'] = f(inputs['conv_out_w'][0])
    sh['conv_out_b'] = f(np.asarray(inputs['conv_out_b'][0]).reshape(1, D))
    sh['attn_in_w'] = f(inputs['attn_in_w'][0])
    ab = np.asarray(inputs['attn_in_b'][0])
    sh['attn_qkb_l'] = f(ab[:2 * D].reshape(16, 128).T)
    sh['attn_vb'] = f(ab[2 * D:3 * D].reshape(1, D))
    sh['attn_fb'] = f(ab[3 * D:].reshape(16, 1))
    sh['attn_out_w'] = f(inputs['attn_out_w'][0])
    sh['attn_out_b'] = f(np.asarray(inputs['attn_out_b'][0]).reshape(1, D))
    sh['peer_query_w'] = f(inputs['peer_query_w'])
    k1, k2 = np.asarray(inputs['peer_sub_keys_1']), np.asarray(inputs['peer_sub_keys_2'])
    sh['peer_skT'] = f(np.stack([np.stack([k1[l].T, k2[l].T]) for l in range(2)]))
    sh['peer_u'] = f(inputs['peer_expert_u'])
    sh['peer_v'] = f(inputs['peer_expert_v'])
    x = np.asarray(inputs['x'])
    c = np.asarray(inputs['c'])
    maps = []
    for b in cores:
        m = dict(sh)
        m['x'] = f(x[b])
        m['c_l'] = f(c[b].reshape(8, 128).T)
        maps.append(m)
    return maps


_NC_CACHE = {}


def kernel(**inputs):
    if 'full' not in _NC_CACHE:
        _NC_CACHE['full'] = Kern({}).build()
    nc = _NC_CACHE['full']
    maps = make_in_maps(inputs)
    res = run_bass_kernel_spmd(nc, maps, core_ids=list(range(8)))
    return np.stack([np.asarray(r['out'], dtype=np.float32) for r in res.results], axis=0)
```

```python
import numpy as np
from contextlib import ExitStack
import concourse.bass as bass
import concourse.mybir as mybir
from concourse.bass_utils import run_bass_kernel_spmd

F32 = mybir.dt.float32
I32 = mybir.dt.int32
U32 = mybir.dt.uint32
F32R = mybir.dt.float32r
BF16 = mybir.dt.bfloat16
ALU = mybir.AluOpType
AF = mybir.ActivationFunctionType
AX = mybir.AxisListType

S = 4096
D = 1024
NT = S // 128
ALPHA = float((2 * 2) ** 0.25)
EPS = 1e-5
NEXP = 16384
MAXV = 30000
NEG = -30000.0


class Prog:
    def __init__(self, nc, es):
        self.nc = nc
        self.es = es
        self.eng = {'pe': nc.tensor, 'dve': nc.vector, 'act': nc.scalar,
                    'pool': nc.gpsimd, 'sp': nc.sync}
        self.seq = {e: 0 for e in self.eng}
        self.csem = {e: [] for e in self.eng}
        self.known = {e: {} for e in self.eng}
        self.snap = {}
        self.last_w = {}
        self.readers = {}
        self.semobj = {}
        self.dma_pool = {}
        self.nsem = 0
        self.nwaits = 0
        self.nops = 0
        for q, n in (('sp', 24), ('pool', 24), ('act', 8)):
            self.dma_pool[q] = {'sems': [self._newsem(f"d{q}{i}") for i in range(n)],
                                'cnt': [0] * n, 'next': 0}

    def _newsem(self, name):
        s = self.es.enter_context(self.nc.semaphore(name))
        self.semobj[name] = s
        self.nsem += 1
        return name

    def _need(self, e, tok, skip_self):
        if tok is None:
            return
        name, val, owner = tok
        if skip_self and owner == e:
            return
        if self.known[e].get(name, 0) >= val:
            return
        self.eng[e].wait_ge(self.semobj[name], val)
        self.nwaits += 1
        k = self.known[e]
        k[name] = val
        sn = self.snap.get((name, val))
        if sn:
            for n2, v2 in sn.items():
                if k.get(n2, 0) < v2:
                    k[n2] = v2

    def op(self, e, fn, reads=(), writes=(), dma=False, skip_self=None):
        if skip_self is None:
            skip_self = (e == 'pe')
        if dma:
            skip_self = False
        for r in reads:
            self._need(e, self.last_w.get(r), skip_self)
        for w in writes:
            self._need(e, self.last_w.get(w), skip_self)
            for t in self.readers.get(w, ()):
                self._need(e, t, skip_self)
        self.nops += 1
        if dma:
            pool = self.dma_pool[e]
            i = pool['next']
            pool['next'] = (i + 1) % len(pool['sems'])
            name = pool['sems'][i]
            if pool['cnt'][i] + 16 > MAXV:
                name = self._newsem(f"{name}r{self.nsem}")
                pool['sems'][i] = name
                pool['cnt'][i] = 0
            prev = pool['cnt'][i]
            if prev > 0:
                self._need(e, (name, prev, e + '_dma'), False)
            ins = fn(self.eng[e])
            pool['cnt'][i] = prev + 16
            ins.then_inc(self.semobj[name], 16)
            tok = (name, prev + 16, e + '_dma')
        else:
            n = self.seq[e]
            ep = n // MAXV
            while len(self.csem[e]) <= ep:
                self.csem[e].append(self._newsem(f"c{e}{len(self.csem[e])}"))
            name = self.csem[e][ep]
            ins = fn(self.eng[e])
            ins.then_inc(self.semobj[name], 1)
            self.seq[e] = n + 1
            tok = (name, n - ep * MAXV + 1, e)
        self.snap[(tok[0], tok[1])] = dict(self.known[e])
        for r in reads:
            self.readers.setdefault(r, []).append(tok)
        for w in writes:
            self.last_w[w] = tok
            self.readers[w] = []
        return tok

    def fence(self, e):
        n = self.seq[e]
        if n > 0:
            ep = (n - 1) // MAXV
            self._need(e, (self.csem[e][ep], n - ep * MAXV, e), False)

    def barrier(self):
        toks = []
        for e in self.eng:
            n = self.seq[e]
            if n > 0:
                ep = (n - 1) // MAXV
                toks.append((self.csem[e][ep], n - ep * MAXV, e))
        for q, pool in self.dma_pool.items():
            for name, c in zip(pool['sems'], pool['cnt']):
                if c > 0:
                    toks.append((name, c, q + '_dma'))
        for e in self.eng:
            for t in toks:
                self._need(e, t, False)
        self.last_w.clear()
        self.readers.clear()
        self.snap.clear()


class Stage:
    _n = 0

    def __init__(self, K, name):
        self.K = K
        Stage._n += 1
        self.name = f"{name}{Stage._n}"

    def __enter__(self):
        self.es = ExitStack()
        self.es.__enter__()
        return self

    def T(self, name, shape, dt=F32):
        return self.es.enter_context(self.K.nc.sbuf_tensor(f"{self.name}_{name}", shape, dt))

    def __exit__(self, *a):
        self.K.P.barrier()
        return self.es.__exit__(*a)


class Kern:
    def __init__(self, cfg):
        self.cfg = cfg

    def build(self):
        nc = bass.Bass("TRN2", target_bir_lowering=False)
        self.nc = nc
        dbg = self.cfg.get('debug', False)

        def din(name, shape, dt=F32):
            return nc.dram_tensor(name, list(shape), dt, kind="ExternalInput").ap()

        def dscr(name, shape, dt=F32):
            kind = "ExternalOutput" if (dbg and name in self.cfg.get('expose', ())) else "Internal"
            return nc.dram_tensor(name, list(shape), dt, kind=kind).ap()

        A = {}
        A['x'] = din('x', [S, D])
        A['c_l'] = din('c_l', [128, 8])
        A['ada_mix_w'] = din('ada_mix_w', [2, D, 3 * D])
        A['ada_ffn_w'] = din('ada_ffn_w', [2, D, 3 * D])
        A['ada_b'] = din('ada_b', [4, 3 * D])
        A['ln_g'] = din('ln_g', [4, D])
        A['ln_b'] = din('ln_b', [4, D])
        A['conv_in_w'] = din('conv_in_w', [D, 2 * D])
        A['conv_in_b_l'] = din('conv_in_b_l', [128, 16])
        A['conv_dw_w_l'] = din('conv_dw_w_l', [128, 8, 31])
        A['conv_vec_l'] = din('conv_vec_l', [128, 3, 8])
        A['conv_out_w'] = din('conv_out_w', [D, D])
        A['conv_out_b'] = din('conv_out_b', [1, D])
        A['attn_in_w'] = din('attn_in_w', [D, 3 * D + 16])
        A['attn_qkb_l'] = din('attn_qkb_l', [128, 16])
        A['attn_vb'] = din('attn_vb', [1, D])
        A['attn_fb'] = din('attn_fb', [16, 1])
        A['attn_out_w'] = din('attn_out_w', [D, D])
        A['attn_out_b'] = din('attn_out_b', [1, D])
        A['peer_query_w'] = din('peer_query_w', [2, D, 2 * D])
        A['peer_skT'] = din('peer_skT', [2, 2, 128, 128])
        A['peer_u'] = din('peer_u', [2, NEXP, D])
        A['peer_v'] = din('peer_v', [2, NEXP, D])
        A['out'] = nc.dram_tensor('out', [S, D], F32, kind="ExternalOutput").ap()
        A['X1'] = dscr('X1', [S, D])
        A['X2'] = dscr('X2', [S, D])
        A['X3'] = dscr('X3', [S, D])
        A['ST'] = dscr('ST', [8, 128, S])
        A['IDX'] = dscr('IDX', [S, 128], I32)
        A['SCR'] = dscr('SCR', [S, 2048])
        A['UVB'] = dscr('UVB', [2 * NEXP, 2 * D], BF16)
        A['H'] = dscr('H', [S, D])
        A['GATE'] = dscr('GATE', [S, 128])
        A['QA'] = dscr('QA', [16, 68, S])
        A['KA'] = dscr('KA', [16, 68, S])
        A['V'] = dscr('V', [S, D])
        A['AOT'] = dscr('AOT', [8, 128, S])
        self.A = A

        with ExitStack() as es:
            self.P = P = Prog(nc, es)
            G = lambda name, shape, dt=F32: es.enter_context(nc.sbuf_tensor(name, shape, dt))
            self.ps = [es.enter_context(nc.psum_tensor(f"ps{i}", [128, 512], F32)) for i in range(8)]
            self.ident = G('ident', [128, 128])
            self.ones = G('ones', [128, 128])
            self.SC = G('SC', [128, 8, 128])
            self.shift_bc = G('shift_bc', [128, D])
            self.scale_bc = G('scale_bc', [128, D])
            self.gate_bc = G('gate_bc', [128, D])
            self.g_bc = G('g_bc', [128, D])
            self.b_bc = G('b_bc', [128, D])
            self.bs = G('bs', [128, 2, 6])
            self.mv = G('mv', [128, 2])
            self.rs = G('rs', [128, 1])
            self.emit_globals()
            self._cast_done = False
            if self.cfg.get('peer_bf16', True) and any(st.startswith('peer') for st in self.cfg.get('stages', ['peer0'])):
                self.emit_cast_tables()
                self._cast_done = True
            order = self.cfg.get('stages', ['conv', 'peer0', 'attn', 'peer1'])
            cur = A['x']
            nxt = {'conv': A['X1'], 'peer0': A['X2'], 'attn': A['X3'], 'peer1': A['out']}
            for i, st in enumerate(order):
                dst = A['out'] if i == len(order) - 1 else nxt[st]
                if st == 'conv':
                    self.emit_adaln(0)
                    self.emit_conv1(cur)
                    self.emit_proj_out(cur, dst, A['conv_out_w'], A['conv_out_b'], src_fm=A['ST'])
                elif st == 'attn':
                    self.emit_adaln(2)
                    self.emit_attn1(cur)
                    self.emit_attn2()
                    self.emit_proj_out(cur, dst, A['attn_out_w'], A['attn_out_b'], src_fm=A['AOT'])
                else:
                    L = int(st[-1])
                    self.emit_adaln(1 + 2 * L)
                    if self.cfg.get('peer_bf16', True):
                        if not self._cast_done:
                            self.emit_cast_tables()
                            self._cast_done = True
                        self.emit_peer1a(cur, L)
                        self.emit_peer2g(cur, dst, L)
                    elif self.cfg.get('peer_fused', True):
                        self.emit_peer1a(cur, L)
                        self.emit_peer2f(cur, dst, L)
                    else:
                        self.emit_peer1(cur, L)
                        self.emit_peer2(cur, dst, L)
                cur = dst
            P.barrier()
            print(f"[kern] ops={P.nops} waits={P.nwaits} sems={P.nsem} seq={P.seq}")
        return nc

    def emit_globals(self):
        P, nc = self.P, self.nc
        ident, ones = self.ident, self.ones
        P.op('pool', lambda e: e.memset(ident[:], 1.0), writes=['ident'])
        P.op('pool', lambda e: e.affine_select(out=ident[:], in_=ident[:], pattern=[[-1, 128]],
                                               compare_op=ALU.is_equal, fill=0.0, base=0, channel_multiplier=1),
             reads=['ident'], writes=['ident'])
        P.op('pool', lambda e: e.memset(ones[:], 1.0), writes=['ones'])
        with Stage(self, 'gl') as st:
            ct = st.T('ct', [128, 8])
            P.op('sp', lambda e: e.dma_start(out=ct[:], in_=self.A['c_l']), writes=['ct'], dma=True)
            P.op('act', lambda e: e.activation(out=ct[:], in_=ct[:], func=AF.Silu), reads=['ct'], writes=['ct'])
            SC = self.SC
            P.op('dve', lambda e: e.tensor_copy(out=SC[:], in_=ct[:].unsqueeze(2).to_broadcast([128, 8, 128])),
                 reads=['ct'], writes=['SC'])

    def modulate(self, xt, ht, rx, rh):
        P = self.P
        P.op('dve', lambda e: e.tensor_tensor(out=ht, in0=xt, in1=self.scale_bc[:], op=ALU.mult),
             reads=[rx, 'scale_bc'], writes=[rh])
        P.op('pool', lambda e: e.tensor_tensor(out=ht, in0=ht, in1=self.shift_bc[:], op=ALU.add),
             reads=[rh, 'shift_bc'], writes=[rh])

    def transpose8(self, src, rsrc, dstT, rdst, col0, pb):
        P = self.P
        ps = self.ps
        for half in range(2):
            bank = ps[pb + half]
            rb = f'ps{pb + half}'
            for kk in range(4):
                k = half * 4 + kk
                P.op('pe', lambda e, k=k, kk=kk, bank=bank: e.transpose(out=bank[:, kk * 128:(kk + 1) * 128],
                                                                      in_=src[:, k * 128:(k + 1) * 128],
                                                                      identity=self.ident[:]),
                     reads=[rsrc, 'ident'], writes=[rb])
            dst = dstT[:, half * 4:half * 4 + 4, col0:col0 + 128]
            srcp = bank[:].rearrange("p (k n) -> p k n", k=4)
            if half == 0:
                P.op('act', lambda e, dst=dst, srcp=srcp: e.copy(out=dst, in_=srcp), reads=[rb], writes=[rdst])
            else:
                P.op('dve', lambda e, dst=dst, srcp=srcp: e.tensor_copy(out=dst, in_=srcp), reads=[rb], writes=[rdst])

    def layernorm_inplace(self, r, rr, gb='pool'):
        P = self.P
        bs, mv, rs = self.bs, self.mv, self.rs
        for c in range(2):
            P.op('dve', lambda e, c=c: e.bn_stats(out=bs[:, c, :], in_=r[:, c * 512:(c + 1) * 512]),
                 reads=[rr], writes=['bs'])
        P.op('dve', lambda e: e.bn_aggr(out=mv[:], in_=bs[:].rearrange("p a b -> p (a b)")), reads=['bs'], writes=['mv'])
        P.op('dve', lambda e: e.tensor_scalar(out=rs[:], in0=mv[:, 1:2], scalar1=EPS, scalar2=None, op0=ALU.add),
             reads=['mv'], writes=['rs'])
        P.op('act', lambda e: e.activation(out=rs[:], in_=rs[:], func=AF.Sqrt), reads=['rs'], writes=['rs'])
        P.op('dve', lambda e: e.reciprocal(out=rs[:], in_=rs[:]), reads=['rs'], writes=['rs'])
        P.op('dve', lambda e: e.tensor_scalar(out=r, in0=r, scalar1=mv[:, 0:1], scalar2=rs[:, 0:1],
                                              op0=ALU.subtract, op1=ALU.mult), reads=[rr, 'mv', 'rs'], writes=[rr])
        P.op(gb, lambda e: e.tensor_tensor(out=r, in0=r, in1=self.g_bc[:], op=ALU.mult), reads=[rr, 'g_bc'], writes=[rr])
        P.op(gb, lambda e: e.tensor_tensor(out=r, in0=r, in1=self.b_bc[:], op=ALU.add), reads=[rr, 'b_bc'], writes=[rr])

    def emit_adaln(self, sub):
        P, nc, A, ps = self.P, self.nc, self.A, self.ps
        L = sub // 2
        wsrc = (A['ada_mix_w'] if sub % 2 == 0 else A['ada_ffn_w'])[L]
        ones = self.ones
        with Stage(self, f'ada{sub}') as st:
            brow = st.T('brow', [1, 3 * D])
            lrow = st.T('lrow', [1, 2 * D])
            wch = [st.T(f'wch{i}', [128, 8, 512]) for i in range(2)]
            P.op('sp', lambda e: e.dma_start(out=brow[:], in_=A['ada_b'][sub:sub + 1, :]), writes=['brow'], dma=True)
            P.op('sp', lambda e: e.dma_start(out=lrow[:, 0:D], in_=A['ln_g'][sub:sub + 1, :]), writes=['lrow'], dma=True)
            P.op('sp', lambda e: e.dma_start(out=lrow[:, D:2 * D], in_=A['ln_b'][sub:sub + 1, :]), writes=['lrow'], dma=True)
            dsts = [self.shift_bc, self.shift_bc, self.scale_bc, self.scale_bc, self.gate_bc, self.gate_bc]
            names = ['shift_bc', 'shift_bc', 'scale_bc', 'scale_bc', 'gate_bc', 'gate_bc']
            for n6 in range(6):
                wb = wch[n6 % 2]
                rw = f'wch{n6 % 2}'
                P.op('sp', lambda e, wb=wb, n6=n6: e.dma_start(
                    out=wb[:], in_=wsrc[:, n6 * 512:(n6 + 1) * 512].rearrange("(k p) n -> p k n", p=128)),
                    writes=[rw], dma=True)
                bank = ps[n6 % 2]
                rb = f'ps{n6 % 2}'
                for k in range(8):
                    P.op('pe', lambda e, k=k, wb=wb, bank=bank: e.matmul(out=bank[:], lhsT=self.SC[:, k, :], rhs=wb[:, k, :],
                                                                        start=(k == 0), stop=False),
                         reads=['SC', rw], writes=[rb])
                P.op('pe', lambda e, bank=bank, n6=n6: e.matmul(out=bank[:], lhsT=ones[0:1, :], rhs=brow[0:1, n6 * 512:(n6 + 1) * 512],
                                                                start=False, stop=True),
                     reads=['ones', 'brow'], writes=[rb])
                dst = dsts[n6][:, (n6 % 2) * 512:(n6 % 2 + 1) * 512]
                if n6 in (2, 3):
                    P.op('dve', lambda e, dst=dst, bank=bank: e.tensor_scalar(out=dst, in0=bank[:], scalar1=1.0, scalar2=None, op0=ALU.add),
                         reads=[rb], writes=[names[n6]])
                else:
                    P.op('dve', lambda e, dst=dst, bank=bank: e.tensor_copy(out=dst, in_=bank[:]), reads=[rb], writes=[names[n6]])
            for j in range(4):
                bank = ps[2 + j % 2]
                rb = f'ps{2 + j % 2}'
                P.op('pe', lambda e, bank=bank, j=j: e.matmul(out=bank[:], lhsT=ones[0:1, :], rhs=lrow[0:1, j * 512:(j + 1) * 512],
                                                              start=True, stop=True), reads=['ones', 'lrow'], writes=[rb])
                dstt = self.g_bc if j < 2 else self.b_bc
                dst = dstt[:, (j % 2) * 512:(j % 2 + 1) * 512]
                P.op('act', lambda e, dst=dst, bank=bank: e.copy(out=dst, in_=bank[:]), reads=[rb],
                     writes=['g_bc' if j < 2 else 'b_bc'])

    def emit_conv1(self, xin):
        P, nc, A, ps = self.P, self.nc, self.A, self.ps
        ones = self.ones
        with Stage(self, 'c1') as st:
            win = st.T('win', [128, 8, 2048])
            cib = st.T('cib', [128, 16])
            dw = st.T('dw', [128, 8, 31])
            cv = st.T('cv', [128, 3, 8])
            xts = [st.T(f'xt{i}', [128, D]) for i in range(2)]
            ht = st.T('ht', [128, D])
            hT = st.T('hT', [128, 8, 512])
            acc = st.T('acc', [128, 8, 512])
            aexts = [st.T(f'aext{i}', [128, 8, 542]) for i in range(2)]
            sig = [st.T(f'sig{i}', [128, 512]) for i in range(2)]
            sq = [st.T(f'sq{i}', [128, 512]) for i in range(2)]
            meant = st.T('meant', [128, 512])
            rstd = st.T('rstd', [128, 512])
            tmp = st.T('tmp', [128, 512])
            for q in range(4):
                P.op('sp', lambda e, q=q: e.dma_start(out=win[:, :, q * 512:(q + 1) * 512],
                                                      in_=A['conv_in_w'][:, q * 512:(q + 1) * 512].rearrange("(k p) n -> p k n", p=128)),
                     writes=[('win', q)], dma=True)
            P.op('sp', lambda e: e.dma_start(out=cib[:], in_=A['conv_in_b_l']), writes=['cib'], dma=True)
            P.op('sp', lambda e: e.dma_start(out=dw[:], in_=A['conv_dw_w_l']), writes=['dw'], dma=True)
            P.op('sp', lambda e: e.dma_start(out=cv[:], in_=A['conv_vec_l']), writes=['cv'], dma=True)
            for cc in range(8):
                P.op('pool', lambda e, cc=cc: e.memset(aexts[0][:, cc, 0:30], 0.0), writes=[('aext0', cc)])
            ti = 0
            for jb in range(8):
                aext = aexts[jb % 2]
                anext = aexts[(jb + 1) % 2]
                AX_ = f'aext{jb % 2}'
                AN_ = f'aext{(jb + 1) % 2}'
                for tl in range(4):
                    xt = xts[ti % 2]
                    rx = f'xt{ti % 2}'
                    P.op('sp', lambda e, xt=xt, ti=ti: e.dma_start(out=xt[:], in_=xin[ti * 128:(ti + 1) * 128, :]),
                         writes=[rx], dma=True)
                    self.modulate(xt[:], ht[:], rx, 'ht')
                    self.transpose8(ht, 'ht', hT, 'hT', tl * 128, 0)
                    ti += 1
                for cc in range(8):
                    pa, pb = ps[2 + (cc % 2) * 2], ps[3 + (cc % 2) * 2]
                    ra, rb = f'ps{2 + (cc % 2) * 2}', f'ps{3 + (cc % 2) * 2}'
                    for k in range(8):
                        P.op('pe', lambda e, k=k, cc=cc, pa=pa: e.matmul(out=pa[:], lhsT=win[:, k, cc * 128:(cc + 1) * 128], rhs=hT[:, k, :],
                                                                        start=(k == 0), stop=(k == 7)),
                             reads=[('win', cc // 4), 'hT'], writes=[ra])
                    for k in range(8):
                        P.op('pe', lambda e, k=k, cc=cc, pb=pb: e.matmul(out=pb[:], lhsT=win[:, k, D + cc * 128:D + (cc + 1) * 128], rhs=hT[:, k, :],
                                                                        start=(k == 0), stop=(k == 7)),
                             reads=[('win', 2 + cc // 4), 'hT'], writes=[rb])
                    sg = sig[cc % 2]
                    rsg = f'sig{cc % 2}'
                    P.op('act', lambda e, sg=sg, pb=pb, cc=cc: e.activation(out=sg[:], in_=pb[:], func=AF.Sigmoid, bias=cib[:, 8 + cc:9 + cc], scale=1.0),
                         reads=[rb, 'cib'], writes=[rsg])
                    P.op('dve', lambda e, sg=sg, pa=pa, cc=cc, aext=aext: e.scalar_tensor_tensor(out=aext[:, cc, 30:542], in0=pa[:], scalar=cib[:, cc:cc + 1], in1=sg[:],
                                                                                 op0=ALU.add, op1=ALU.mult),
                         reads=[ra, rsg, 'cib'], writes=[(AX_, cc)])
                for cc in range(8):
                    P.op('dve', lambda e, cc=cc, aext=aext: e.tensor_scalar(out=acc[:, cc, :], in0=aext[:, cc, 0:512], scalar1=dw[:, cc, 0:1], scalar2=cv[:, 0, cc:cc + 1],
                                                                 op0=ALU.mult, op1=ALU.add),
                         reads=[(AX_, cc), 'dw', 'cv'], writes=[('acc', cc)])
                    for w in range(1, 31):
                        P.op('dve', lambda e, cc=cc, w=w, aext=aext: e.scalar_tensor_tensor(out=acc[:, cc, :], in0=aext[:, cc, w:w + 512], scalar=dw[:, cc, w:w + 1],
                                                                                 in1=acc[:, cc, :], op0=ALU.mult, op1=ALU.add),
                             reads=[(AX_, cc), 'dw', ('acc', cc)], writes=[('acc', cc)])
                    P.op('act', lambda e, cc=cc, aext=aext, anext=anext: e.copy(out=anext[:, cc, 0:30], in_=aext[:, cc, 512:542]),
                         reads=[(AX_, cc)], writes=[(AN_, cc)])
                for cc in range(8):
                    s2 = sq[cc % 2]
                    rs2 = f'sq{cc % 2}'
                    P.op('act', lambda e, cc=cc, s2=s2: e.activation(out=s2[:], in_=acc[:, cc, :], func=AF.Square),
                         reads=[('acc', cc)], writes=[rs2])
                    P.op('pe', lambda e, cc=cc: e.matmul(out=ps[6][:], lhsT=ones[:], rhs=acc[:, cc, :], start=(cc == 0), stop=(cc == 7)),
                         reads=['ones', ('acc', cc)], writes=['ps6'])
                    P.op('pe', lambda e, cc=cc, s2=s2: e.matmul(out=ps[7][:], lhsT=ones[:], rhs=s2[:], start=(cc == 0), stop=(cc == 7)),
                         reads=['ones', rs2], writes=['ps7'])
                P.op('act', lambda e: e.activation(out=meant[:], in_=ps[6][:], func=AF.Copy, scale=1.0 / D), reads=['ps6'], writes=['meant'])
                P.op('dve', lambda e: e.tensor_tensor(out=tmp[:], in0=meant[:], in1=meant[:], op=ALU.mult), reads=['meant'], writes=['tmp'])
                P.op('dve', lambda e: e.scalar_tensor_tensor(out=rstd[:], in0=ps[7][:], scalar=1.0 / D, in1=tmp[:], op0=ALU.mult, op1=ALU.subtract),
                     reads=['ps7', 'tmp'], writes=['rstd'])
                P.op('dve', lambda e: e.tensor_scalar(out=rstd[:], in0=rstd[:], scalar1=EPS, scalar2=None, op0=ALU.add), reads=['rstd'], writes=['rstd'])
                P.op('act', lambda e: e.activation(out=rstd[:], in_=rstd[:], func=AF.Sqrt), reads=['rstd'], writes=['rstd'])
                P.op('dve', lambda e: e.reciprocal(out=rstd[:], in_=rstd[:]), reads=['rstd'], writes=['rstd'])
                for cc in range(8):
                    P.op('dve', lambda e, cc=cc: e.tensor_tensor(out=acc[:, cc, :], in0=acc[:, cc, :], in1=meant[:], op=ALU.subtract),
                         reads=[('acc', cc), 'meant'], writes=[('acc', cc)])
                    P.op('pool', lambda e, cc=cc: e.tensor_tensor(out=acc[:, cc, :], in0=acc[:, cc, :], in1=rstd[:], op=ALU.mult),
                         reads=[('acc', cc), 'rstd'], writes=[('acc', cc)])
                    P.op('act', lambda e, cc=cc: e.activation(out=acc[:, cc, :], in_=acc[:, cc, :], func=AF.Silu,
                                                              bias=cv[:, 2, cc:cc + 1], scale=cv[:, 1, cc:cc + 1]),
                         reads=[('acc', cc), 'cv'], writes=[('acc', cc)])
                P.op('sp', lambda e, jb=jb: e.dma_start(out=A['ST'][:, :, jb * 512:(jb + 1) * 512].rearrange("c p t -> p c t"), in_=acc[:]),
                     reads=[('acc', cc) for cc in range(8)], writes=[('ST', jb)], dma=True)

    def emit_proj_out(self, xin, dst, w_ap, b_ap, src_fm=None, src_tm=None):
        P, nc, A, ps = self.P, self.nc, self.A, self.ps
        ones = self.ones
        with Stage(self, 'po') as st:
            wo = st.T('wo', [128, 8, D])
            bo = st.T('bo', [1, D])
            xts = [st.T(f'xt{i}', [128, D]) for i in range(2)]
            rts = [st.T(f'rt{i}', [128, D]) for i in range(2)]
            if src_fm is not None:
                sT = [st.T(f'sT{i}', [128, 8, 512]) for i in range(2)]
            else:
                ao = [st.T(f'ao{i}', [128, D]) for i in range(2)]
                aT = [st.T(f'aT{i}', [128, 8, 128]) for i in range(2)]
            for q in range(2):
                P.op('sp', lambda e, q=q: e.dma_start(out=wo[:, :, q * 512:(q + 1) * 512],
                                                      in_=w_ap[:, q * 512:(q + 1) * 512].rearrange("(k p) n -> p k n", p=128)),
                     writes=[('wo', q)], dma=True)
            P.op('sp', lambda e: e.dma_start(out=bo[:], in_=b_ap), writes=['bo'], dma=True)
            for ti in range(NT):
                xt = xts[ti % 2]
                rx = f'xt{ti % 2}'
                rt = rts[ti % 2]
                rr = f'rt{ti % 2}'
                P.op('sp', lambda e, xt=xt, ti=ti: e.dma_start(out=xt[:], in_=xin[ti * 128:(ti + 1) * 128, :]), writes=[rx], dma=True)
                if src_fm is not None:
                    jb, tl = ti // 4, ti % 4
                    sb = sT[jb % 2]
                    rsb = f'sT{jb % 2}'
                    if tl == 0:
                        P.op('sp', lambda e, sb=sb, jb=jb: e.dma_start(out=sb[:], in_=src_fm[:, :, jb * 512:(jb + 1) * 512].rearrange("c p t -> p c t")),
                             reads=[('ST', jb)], writes=[rsb], dma=True)
                    lhs = lambda k, sb=sb, tl=tl: sb[:, k, tl * 128:(tl + 1) * 128]
                    rl = rsb
                else:
                    a = ao[ti % 2]
                    ra = f'ao{ti % 2}'
                    at = aT[ti % 2]
                    rat = f'aT{ti % 2}'
                    P.op('sp', lambda e, a=a, ti=ti: e.dma_start(out=a[:], in_=src_tm[ti * 128:(ti + 1) * 128, :]), writes=[ra], dma=True)
                    self.transpose8(a, ra, at, rat, 0, 4)
                    lhs = lambda k, at=at: at[:, k, :]
                    rl = rat
                pb = (ti % 2) * 2
                for half in range(2):
                    bank = ps[pb + half]
                    rb = f'ps{pb + half}'
                    for k in range(8):
                        P.op('pe', lambda e, k=k, bank=bank, half=half, lhs=lhs: e.matmul(out=bank[:], lhsT=lhs(k), rhs=wo[:, k, half * 512:(half + 1) * 512],
                                                                                     start=(k == 0), stop=False),
                             reads=[rl, ('wo', half)], writes=[rb])
                    P.op('pe', lambda e, bank=bank, half=half: e.matmul(out=bank[:], lhsT=ones[0:1, :], rhs=bo[0:1, half * 512:(half + 1) * 512],
                                                                        start=False, stop=True), reads=['ones', 'bo'], writes=[rb])
                    P.op('dve', lambda e, bank=bank, half=half, rt=rt: e.tensor_tensor(out=rt[:, half * 512:(half + 1) * 512], in0=bank[:],
                                                                                  in1=self.gate_bc[:, half * 512:(half + 1) * 512], op=ALU.mult),
                         reads=[rb, 'gate_bc'], writes=[rr])
                P.op('dve', lambda e, rt=rt, xt=xt: e.scalar_tensor_tensor(out=rt[:], in0=xt[:], scalar=ALPHA, in1=rt[:], op0=ALU.mult, op1=ALU.add),
                     reads=[rx, rr], writes=[rr])
                self.layernorm_inplace(rt[:], rr)
                P.op('sp', lambda e, rt=rt, ti=ti: e.dma_start(out=dst[ti * 128:(ti + 1) * 128, :], in_=rt[:]), reads=[rr], dma=True)

    def emit_peer1(self, xin, L):
        P, nc, A, ps = self.P, self.nc, self.A, self.ps
        with Stage(self, f'p1{L}') as st:
            wq = st.T('wq', [128, 8, 2048])
            skT = st.T('skT', [128, 2, 128])
            xts = [st.T(f'xt{i}', [128, D]) for i in range(2)]
            ht = st.T('ht', [128, D])
            hT = st.T('hT', [128, 8, 256])
            qT = st.T('qT', [128, 16, 256])
            sc = st.T('sc', [128, 16, 128])
            m = st.T('m', [128, 16, 16])
            ix = st.T('ix', [128, 16, 16], U32)
            ixf = st.T('ixf', [128, 16, 16])
            wk = st.T('wk', [128, 16, 128])
            cand = st.T('cand', [128, 8, 256])
            candi = st.T('candi', [128, 8, 256])
            wk2 = st.T('wk2', [128, 8, 256])
            junk = [st.T(f'junk{i}', [128, 256]) for i in range(2)]
            ts = st.T('ts', [128, 8, 16])
            ef = st.T('ef', [128, 128])
            ei = [st.T(f'ei{i}', [128, 128], I32) for i in range(2)]
            gt = [st.T(f'gt{i}', [128, 8, 16]) for i in range(2)]
            gsum = st.T('gsum', [128, 8])
            for q in range(4):
                P.op('sp', lambda e, q=q: e.dma_start(out=wq[:, :, q * 512:(q + 1) * 512],
                                                      in_=A['peer_query_w'][L][:, q * 512:(q + 1) * 512].rearrange("(k p) n -> p k n", p=128)),
                     writes=[('wq', q)], dma=True)
            P.op('sp', lambda e: e.dma_start(out=skT[:], in_=A['peer_skT'][L].rearrange("h d k -> d h k")), writes=['skT'], dma=True)
            ti = 0
            for jb in range(S // 256):
                for tl in range(2):
                    xt = xts[ti % 2]
                    rx = f'xt{ti % 2}'
                    P.op('sp', lambda e, xt=xt, ti=ti: e.dma_start(out=xt[:], in_=xin[ti * 128:(ti + 1) * 128, :]), writes=[rx], dma=True)
                    self.modulate(xt[:], ht[:], rx, 'ht')
                    self.transpose8(ht, 'ht', hT, 'hT', tl * 128, 0)
                    ti += 1
                for c in range(16):
                    bank = ps[2 + c % 2]
                    rb = f'ps{2 + c % 2}'
                    for k in range(8):
                        P.op('pe', lambda e, k=k, c=c, bank=bank: e.matmul(out=bank[:, 0:256], lhsT=wq[:, k, c * 128:(c + 1) * 128], rhs=hT[:, k, :],
                                                                          start=(k == 0), stop=(k == 7)),
                             reads=[('wq', c // 4), 'hT'], writes=[rb])
                    if c % 2 == 0:
                        P.op('act', lambda e, c=c, bank=bank: e.copy(out=qT[:, c, :], in_=bank[:, 0:256]), reads=[rb], writes=[('qT', c)])
                    else:
                        P.op('dve', lambda e, c=c, bank=bank: e.tensor_copy(out=qT[:, c, :], in_=bank[:, 0:256]), reads=[rb], writes=[('qT', c)])
                for tl in range(2):
                    tix = jb * 2 + tl
                    for c in range(16):
                        bank = ps[4 + c // 4]
                        rb = f'ps{4 + c // 4}'
                        P.op('pe', lambda e, c=c, bank=bank, tl=tl: e.matmul(out=bank[:, (c % 4) * 128:(c % 4 + 1) * 128],
                                                                            lhsT=qT[:, c, tl * 128:(tl + 1) * 128], rhs=skT[:, c % 2, :],
                                                                            start=True, stop=True),
                             reads=[('qT', c), 'skT'], writes=[rb])
                    for g4 in range(4):
                        P.op('act', lambda e, g4=g4: e.copy(out=sc[:, g4 * 4:(g4 + 1) * 4, :], in_=ps[4 + g4][:].rearrange("p (a k) -> p a k", a=4)),
                             reads=[f'ps{4 + g4}'], writes=['sc'])
                    for c in range(16):
                        P.op('dve', lambda e, c=c: e.max(out=m[:, c, 0:8], in_=sc[:, c, :]), reads=['sc'], writes=[('m0', c)])
                    P.fence('dve')
                    for c in range(16):
                        P.op('dve', lambda e, c=c: e.max_index(out=ix[:, c, 0:8], in_max=m[:, c, 0:8], in_values=sc[:, c, :]),
                             reads=['sc', ('m0', c)], writes=[('ix0', c)])
                        P.op('dve', lambda e, c=c: e.match_replace(out=wk[:, c, :], in_to_replace=m[:, c, 0:8], in_values=sc[:, c, :], imm_value=-1e30),
                             reads=['sc', ('m0', c)], writes=[('wk', c)])
                    P.fence('dve')
                    for c in range(16):
                        P.op('dve', lambda e, c=c: e.max(out=m[:, c, 8:16], in_=wk[:, c, :]), reads=[('wk', c)], writes=[('m1', c)])
                    P.fence('dve')
                    for c in range(16):
                        P.op('dve', lambda e, c=c: e.max_index(out=ix[:, c, 8:16], in_max=m[:, c, 8:16], in_values=wk[:, c, :]),
                             reads=[('wk', c), ('m1', c)], writes=[('ix1', c)])
                    P.fence('dve')
                    mres = [('m0', c) for c in range(16)] + [('m1', c) for c in range(16)]
                    ixres = [('ix0', c) for c in range(16)] + [('ix1', c) for c in range(16)]
                    P.op('dve', lambda e: e.tensor_copy(out=ixf[:], in_=ix[:]), reads=ixres, writes=['ixf'])
                    m4 = m[:].rearrange("p (h two) k -> p h two k", two=2)
                    i4 = ixf[:].rearrange("p (h two) k -> p h two k", two=2)
                    c4 = cand[:].rearrange("p h (a b) -> p h a b", a=16)
                    ci4 = candi[:].rearrange("p h (a b) -> p h a b", a=16)
                    P.op('dve', lambda e: e.tensor_tensor(out=c4, in0=m4[:, :, 0, :].unsqueeze(3).to_broadcast([128, 8, 16, 16]),
                                                          in1=m4[:, :, 1, :].unsqueeze(2).to_broadcast([128, 8, 16, 16]), op=ALU.add),
                         reads=mres, writes=['cand'])
                    P.op('dve', lambda e: e.tensor_scalar(out=i4[:, :, 0, :], in0=i4[:, :, 0, :], scalar1=128.0, scalar2=None, op0=ALU.mult),
                         reads=['ixf'], writes=['ixf'])
                    P.op('dve', lambda e: e.tensor_tensor(out=ci4, in0=i4[:, :, 0, :].unsqueeze(3).to_broadcast([128, 8, 16, 16]),
                                                          in1=i4[:, :, 1, :].unsqueeze(2).to_broadcast([128, 8, 16, 16]), op=ALU.add),
                         reads=['ixf'], writes=['candi'])
                    for h in range(8):
                        P.op('dve', lambda e, h=h: e.max(out=ts[:, h, 0:8], in_=cand[:, h, :]), reads=['cand'], writes=[('ts0', h)])
                    P.fence('dve')
                    for h in range(8):
                        P.op('dve', lambda e, h=h: e.match_replace(out=wk2[:, h, :], in_to_replace=ts[:, h, 0:8], in_values=cand[:, h, :], imm_value=-1e30),
                             reads=['cand', ('ts0', h)], writes=[('wk2', h)])
                    P.fence('dve')
                    for h in range(8):
                        P.op('dve', lambda e, h=h: e.max(out=ts[:, h, 8:16], in_=wk2[:, h, :]), reads=[('wk2', h)], writes=[('ts1', h)])
                    P.fence('dve')
                    tsres = [('ts0', h) for h in range(8)] + [('ts1', h) for h in range(8)]
                    for h in range(8):
                        for k in range(16):
                            P.op('dve', lambda e, h=h, k=k: e.scalar_tensor_tensor(out=junk[(h * 16 + k) % 2][:], in0=cand[:, h, :], scalar=ts[:, h, k:k + 1], in1=candi[:, h, :],
                                                                                   op0=ALU.is_equal, op1=ALU.mult, accum_out=ef[:, h * 16 + k:h * 16 + k + 1]),
                                 reads=['cand', 'candi', ('ts0', h), ('ts1', h)], writes=[('ef', h * 16 + k)])
                    P.fence('dve')
                    eib = ei[tix % 2]
                    rei = f'ei{tix % 2}'
                    gtb = gt[tix % 2]
                    rgt = f'gt{tix % 2}'
                    P.op('dve', lambda e: e.tensor_scalar(out=ef[:], in0=ef[:], scalar1=float(NEXP - 1), scalar2=float(L * NEXP), op0=ALU.min, op1=ALU.add),
                         reads=[('ef', q) for q in range(128)], writes=['ef'])
                    P.op('dve', lambda e, eib=eib: e.tensor_copy(out=eib[:], in_=ef[:]), reads=['ef'], writes=[rei])
                    P.op('dve', lambda e, gtb=gtb: e.tensor_tensor(out=gtb[:], in0=ts[:], in1=ts[:, :, 0:1].to_broadcast([128, 8, 16]), op=ALU.subtract),
                         reads=tsres, writes=[rgt])
                    P.op('act', lambda e, gtb=gtb: e.activation(out=gtb[:], in_=gtb[:], func=AF.Exp), reads=[rgt], writes=[rgt])
                    P.op('dve', lambda e, gtb=gtb: e.tensor_reduce(out=gsum[:], in_=gtb[:], axis=AX.X, op=ALU.add), reads=[rgt], writes=['gsum'])
                    P.op('dve', lambda e: e.reciprocal(out=gsum[:], in_=gsum[:]), reads=['gsum'], writes=['gsum'])
                    P.op('dve', lambda e, gtb=gtb: e.tensor_tensor(out=gtb[:], in0=gtb[:], in1=gsum[:].unsqueeze(2).to_broadcast([128, 8, 16]), op=ALU.mult),
                         reads=[rgt, 'gsum'], writes=[rgt])
                    P.op('sp', lambda e, eib=eib, tix=tix: e.dma_start(out=A['IDX'][tix * 128:(tix + 1) * 128, :], in_=eib[:]),
                         reads=[rei], writes=[('IDX', tix)], dma=True)
                    P.op('sp', lambda e, gtb=gtb, tix=tix: e.dma_start(out=A['GATE'][tix * 128:(tix + 1) * 128, :], in_=gtb[:].rearrange("p h k -> p (h k)")),
                         reads=[rgt], writes=[('GATE', tix)], dma=True)

    def emit_peer2(self, xin, dst, L):
        P, nc, A, ps = self.P, self.nc, self.A, self.ps
        NB = self.cfg.get('nb', 14)
        U = A['peer_u'].rearrange("l e d -> (l e) d")
        V = A['peer_v'].rearrange("l e d -> (l e) d")
        with Stage(self, f'p2{L}') as st:
            xts = [st.T(f'xt{i}', [128, D]) for i in range(2)]
            hts = [st.T(f'ht{i}', [128, D]) for i in range(2)]
            eis = [st.T(f'ei{i}', [128, 128], I32) for i in range(2)]
            gts = [st.T(f'gt{i}', [128, 128]) for i in range(2)]
            ub = [st.T(f'ub{i}', [128, D]) for i in range(NB)]
            vb = [st.T(f'vb{i}', [128, D]) for i in range(NB)]
            junk = st.T('junk', [128, D])
            apre = st.T('apre', [128, 128])
            coef = st.T('coef', [128, 128])
            accs = [st.T(f'acc{i}', [128, D]) for i in range(2)]
            nu = nv = 0
            for ti in range(NT):
                b = ti % 2
                xt, ht, eib, gtb, acc = xts[b], hts[b], eis[b], gts[b], accs[b]
                rx, rh, rei, rgt, racc = f'xt{b}', f'ht{b}', f'ei{b}', f'gt{b}', f'acc{b}'
                P.op('sp', lambda e, xt=xt, ti=ti: e.dma_start(out=xt[:], in_=xin[ti * 128:(ti + 1) * 128, :]), writes=[rx], dma=True)
                P.op('sp', lambda e, eib=eib, ti=ti: e.dma_start(out=eib[:], in_=A['IDX'][ti * 128:(ti + 1) * 128, :]),
                     reads=[('IDX', ti)], writes=[rei], dma=True)
                P.op('sp', lambda e, gtb=gtb, ti=ti: e.dma_start(out=gtb[:], in_=A['GATE'][ti * 128:(ti + 1) * 128, :]),
                     reads=[('GATE', ti)], writes=[rgt], dma=True)
                self.modulate(xt[:], ht[:], rx, rh)
                for k in range(128):
                    s = nu % NB
                    nu += 1
                    P.op('pool', lambda e, s=s, k=k, eib=eib: e.indirect_dma_start(
                        out=ub[s][:], out_offset=None, in_=U,
                        in_offset=bass.IndirectOffsetOnAxis(ap=eib[:, k:k + 1], axis=0)),
                        reads=[rei], writes=[('ub', s)], dma=True)
                    P.op('dve', lambda e, s=s, k=k, ht=ht: e.scalar_tensor_tensor(out=junk[:], in0=ub[s][:], scalar=1.0, in1=ht[:], op0=ALU.mult, op1=ALU.mult,
                                                                               accum_out=apre[:, k:k + 1]),
                         reads=[('ub', s), rh], writes=['junk', 'apre'])
                P.op('act', lambda e: e.activation(out=coef[:], in_=apre[:], func=AF.Gelu), reads=['apre'], writes=['coef'])
                P.op('dve', lambda e, gtb=gtb: e.tensor_tensor(out=coef[:], in0=coef[:], in1=gtb[:], op=ALU.mult), reads=['coef', rgt], writes=['coef'])
                for k in range(128):
                    s = nv % NB
                    nv += 1
                    P.op('pool', lambda e, s=s, k=k, eib=eib: e.indirect_dma_start(
                        out=vb[s][:], out_offset=None, in_=V,
                        in_offset=bass.IndirectOffsetOnAxis(ap=eib[:, k:k + 1], axis=0)),
                        reads=[rei], writes=[('vb', s)], dma=True)
                    if k == 0:
                        P.op('dve', lambda e, s=s, acc=acc: e.tensor_scalar(out=acc[:], in0=vb[s][:], scalar1=coef[:, 0:1], scalar2=None, op0=ALU.mult),
                             reads=[('vb', s), 'coef'], writes=[racc])
                    else:
                        P.op('dve', lambda e, s=s, k=k, acc=acc: e.scalar_tensor_tensor(out=acc[:], in0=vb[s][:], scalar=coef[:, k:k + 1], in1=acc[:],
                                                                                      op0=ALU.mult, op1=ALU.add),
                             reads=[('vb', s), 'coef', racc], writes=[racc])
                P.op('pool', lambda e, acc=acc: e.tensor_tensor(out=acc[:], in0=acc[:], in1=self.gate_bc[:], op=ALU.mult), reads=[racc, 'gate_bc'], writes=[racc])
                P.op('dve', lambda e, acc=acc, xt=xt: e.scalar_tensor_tensor(out=acc[:], in0=xt[:], scalar=ALPHA, in1=acc[:], op0=ALU.mult, op1=ALU.add),
                     reads=[rx, racc], writes=[racc])
                self.layernorm_inplace(acc[:], racc)
                P.op('sp', lambda e, acc=acc, ti=ti: e.dma_start(out=dst[ti * 128:(ti + 1) * 128, :], in_=acc[:]), reads=[racc], dma=True)

    def emit_peer1a(self, xin, L):
        P, nc, A, ps = self.P, self.nc, self.A, self.ps
        with Stage(self, f'pa{L}') as st:
            wq = st.T('wq', [128, 8, 2048])
            skT = st.T('skT', [128, 2, 128])
            xts = [st.T(f'xt{i}', [128, D]) for i in range(2)]
            hts = [st.T(f'ht{i}', [128, D]) for i in range(2)]
            hT = st.T('hT', [128, 8, 256])
            qT = st.T('qT', [128, 16, 256])
            sco = [st.T(f'sco{i}', [128, 2048]) for i in range(2)]
            for q in range(4):
                P.op('sp', lambda e, q=q: e.dma_start(out=wq[:, :, q * 512:(q + 1) * 512],
                                                      in_=A['peer_query_w'][L][:, q * 512:(q + 1) * 512].rearrange("(k p) n -> p k n", p=128)),
                     writes=[('wq', q)], dma=True)
            P.op('sp', lambda e: e.dma_start(out=skT[:], in_=A['peer_skT'][L].rearrange("h d k -> d h k")), writes=['skT'], dma=True)
            ti = 0
            for jb in range(S // 256):
                for tl in range(2):
                    xt, ht = xts[ti % 2], hts[ti % 2]
                    rx, rh = f'xt{ti % 2}', f'ht{ti % 2}'
                    P.op('sp', lambda e, xt=xt, ti=ti: e.dma_start(out=xt[:], in_=xin[ti * 128:(ti + 1) * 128, :]), writes=[rx], dma=True)
                    self.modulate(xt[:], ht[:], rx, rh)
                    P.op('sp', lambda e, ht=ht, ti=ti: e.dma_start(out=A['H'][ti * 128:(ti + 1) * 128, :], in_=ht[:]), reads=[rh], writes=[('H', ti)], dma=True)
                    self.transpose8(ht, rh, hT, 'hT', tl * 128, 0)
                    ti += 1
                for c in range(16):
                    bank = ps[2 + c % 2]
                    rb = f'ps{2 + c % 2}'
                    for k in range(8):
                        P.op('pe', lambda e, k=k, c=c, bank=bank: e.matmul(out=bank[:, 0:256], lhsT=wq[:, k, c * 128:(c + 1) * 128], rhs=hT[:, k, :],
                                                                          start=(k == 0), stop=(k == 7)),
                             reads=[('wq', c // 4), 'hT'], writes=[rb])
                    if c % 2 == 0:
                        P.op('act', lambda e, c=c, bank=bank: e.copy(out=qT[:, c, :], in_=bank[:, 0:256]), reads=[rb], writes=[('qT', c)])
                    else:
                        P.op('dve', lambda e, c=c, bank=bank: e.tensor_copy(out=qT[:, c, :], in_=bank[:, 0:256]), reads=[rb], writes=[('qT', c)])
                for tl in range(2):
                    tix = jb * 2 + tl
                    so = sco[tix % 2]
                    rso = f'sco{tix % 2}'
                    for c in range(16):
                        bank = ps[4 + c // 4]
                        rb = f'ps{4 + c // 4}'
                        P.op('pe', lambda e, c=c, bank=bank, tl=tl: e.matmul(out=bank[:, (c % 4) * 128:(c % 4 + 1) * 128],
                                                                            lhsT=qT[:, c, tl * 128:(tl + 1) * 128], rhs=skT[:, c % 2, :],
                                                                            start=True, stop=True),
                             reads=[('qT', c), 'skT'], writes=[rb])
                    for g4 in range(4):
                        eng = ('act', 'dve')[g4 % 2]
                        if eng == 'act':
                            P.op('act', lambda e, g4=g4, so=so: e.copy(out=so[:, g4 * 512:(g4 + 1) * 512], in_=ps[4 + g4][:]), reads=[f'ps{4 + g4}'], writes=[rso])
                        else:
                            P.op('dve', lambda e, g4=g4, so=so: e.tensor_copy(out=so[:, g4 * 512:(g4 + 1) * 512], in_=ps[4 + g4][:]), reads=[f'ps{4 + g4}'], writes=[rso])
                    P.op('sp', lambda e, so=so, tix=tix: e.dma_start(out=A['SCR'][tix * 128:(tix + 1) * 128, :], in_=so[:]), reads=[rso], writes=[('SCR', tix)], dma=True)

    def emit_peer2f(self, xin, dst, L):
        P, nc, A, ps = self.P, self.nc, self.A, self.ps
        NB = self.cfg.get('nb', 11)
        U = A['peer_u'].rearrange("l e d -> (l e) d")
        V = A['peer_v'].rearrange("l e d -> (l e) d")
        with Stage(self, f'pf{L}') as st:
            scs = [st.T(f'sc{i}', [128, 16, 128]) for i in range(2)]
            m = st.T('m', [128, 16, 16])
            ix = st.T('ix', [128, 16, 16], U32)
            ixf = st.T('ixf', [128, 16, 16])
            wk = st.T('wk', [128, 16, 128])
            cand = st.T('cand', [128, 8, 256])
            candi = st.T('candi', [128, 8, 256])
            wk2 = st.T('wk2', [128, 8, 256])
            junk2 = [st.T(f'junk2{i}', [128, 256]) for i in range(2)]
            ts = st.T('ts', [128, 8, 16])
            ef = st.T('ef', [128, 128])
            eis = [st.T(f'ei{i}', [128, 128], I32) for i in range(2)]
            gts = [st.T(f'gt{i}', [128, 8, 16]) for i in range(2)]
            gsum = st.T('gsum', [128, 8])
            xts = [st.T(f'xt{i}', [128, D]) for i in range(2)]
            hts = [st.T(f'ht{i}', [128, D]) for i in range(2)]
            ub = [st.T(f'ub{i}', [128, D]) for i in range(NB)]
            vb = [st.T(f'vb{i}', [128, D]) for i in range(NB)]
            junk = st.T('junk', [128, D])
            apre = st.T('apre', [128, 128])
            coef = st.T('coef', [128, 128])
            accs = [st.T(f'acc{i}', [128, D]) for i in range(2)]

            def load_sc(t):
                P.op('sp', lambda e, t=t: e.dma_start(out=scs[t % 2][:].rearrange("p a k -> p (a k)"), in_=A['SCR'][t * 128:(t + 1) * 128, :]),
                     writes=[f'sc{t % 2}'], dma=True)

            def load_xh(t):
                P.op('sp', lambda e, t=t: e.dma_start(out=xts[t % 2][:], in_=xin[t * 128:(t + 1) * 128, :]), writes=[f'xt{t % 2}'], dma=True)
                P.op('sp', lambda e, t=t: e.dma_start(out=hts[t % 2][:], in_=A['H'][t * 128:(t + 1) * 128, :]), writes=[f'ht{t % 2}'], dma=True)

            def topk_gen(t):
                sc = scs[t % 2]
                rsc = f'sc{t % 2}'
                eib, gtb = eis[t % 2], gts[t % 2]
                rei, rgt = f'ei{t % 2}', f'gt{t % 2}'
                for c in range(16):
                    P.op('dve', lambda e, c=c: e.max(out=m[:, c, 0:8], in_=sc[:, c, :]), reads=[rsc], writes=[('m0', c)])
                    yield
                P.fence('dve')
                for c in range(16):
                    P.op('dve', lambda e, c=c: e.max_index(out=ix[:, c, 0:8], in_max=m[:, c, 0:8], in_values=sc[:, c, :]),
                         reads=[rsc, ('m0', c)], writes=[('ix0', c)])
                    yield
                    P.op('dve', lambda e, c=c: e.match_replace(out=wk[:, c, :], in_to_replace=m[:, c, 0:8], in_values=sc[:, c, :], imm_value=-1e30),
                         reads=[rsc, ('m0', c)], writes=[('wk', c)])
                    yield
                P.fence('dve')
                for c in range(16):
                    P.op('dve', lambda e, c=c: e.max(out=m[:, c, 8:16], in_=wk[:, c, :]), reads=[('wk', c)], writes=[('m1', c)])
                    yield
                P.fence('dve')
                for c in range(16):
                    P.op('dve', lambda e, c=c: e.max_index(out=ix[:, c, 8:16], in_max=m[:, c, 8:16], in_values=wk[:, c, :]),
                         reads=[('wk', c), ('m1', c)], writes=[('ix1', c)])
                    yield
                P.fence('dve')
                mres = [('m0', c) for c in range(16)] + [('m1', c) for c in range(16)]
                ixres = [('ix0', c) for c in range(16)] + [('ix1', c) for c in range(16)]
                P.op('dve', lambda e: e.tensor_copy(out=ixf[:], in_=ix[:]), reads=ixres, writes=['ixf'])
                yield
                m4 = m[:].rearrange("p (h two) k -> p h two k", two=2)
                i4 = ixf[:].rearrange("p (h two) k -> p h two k", two=2)
                c4 = cand[:].rearrange("p h (a b) -> p h a b", a=16)
                ci4 = candi[:].rearrange("p h (a b) -> p h a b", a=16)
                P.op('dve', lambda e: e.tensor_tensor(out=c4, in0=m4[:, :, 0, :].unsqueeze(3).to_broadcast([128, 8, 16, 16]),
                                                      in1=m4[:, :, 1, :].unsqueeze(2).to_broadcast([128, 8, 16, 16]), op=ALU.add),
                     reads=mres, writes=['cand'])
                yield
                P.op('dve', lambda e: e.tensor_scalar(out=i4[:, :, 0, :], in0=i4[:, :, 0, :], scalar1=128.0, scalar2=None, op0=ALU.mult),
                     reads=['ixf'], writes=['ixf'])
                yield
                P.op('dve', lambda e: e.tensor_tensor(out=ci4, in0=i4[:, :, 0, :].unsqueeze(3).to_broadcast([128, 8, 16, 16]),
                                                      in1=i4[:, :, 1, :].unsqueeze(2).to_broadcast([128, 8, 16, 16]), op=ALU.add),
                     reads=['ixf'], writes=['candi'])
                yield
                for h in range(8):
                    P.op('dve', lambda e, h=h: e.max(out=ts[:, h, 0:8], in_=cand[:, h, :]), reads=['cand'], writes=[('ts0', h)])
                    yield
                P.fence('dve')
                for h in range(8):
                    P.op('dve', lambda e, h=h: e.match_replace(out=wk2[:, h, :], in_to_replace=ts[:, h, 0:8], in_values=cand[:, h, :], imm_value=-1e30),
                         reads=['cand', ('ts0', h)], writes=[('wk2', h)])
                    yield
                P.fence('dve')
                for h in range(8):
                    P.op('dve', lambda e, h=h: e.max(out=ts[:, h, 8:16], in_=wk2[:, h, :]), reads=[('wk2', h)], writes=[('ts1', h)])
                    yield
                P.fence('dve')
                tsres = [('ts0', h) for h in range(8)] + [('ts1', h) for h in range(8)]
                for h in range(8):
                    for k in range(16):
                        P.op('dve', lambda e, h=h, k=k: e.scalar_tensor_tensor(out=junk2[(h * 16 + k) % 2][:], in0=cand[:, h, :], scalar=ts[:, h, k:k + 1], in1=candi[:, h, :],
                                                                               op0=ALU.is_equal, op1=ALU.mult, accum_out=ef[:, h * 16 + k:h * 16 + k + 1]),
                             reads=['cand', 'candi', ('ts0', h), ('ts1', h)], writes=[('ef', h * 16 + k)])
                        yield
                P.fence('dve')
                P.op('dve', lambda e: e.tensor_scalar(out=ef[:], in0=ef[:], scalar1=float(NEXP - 1), scalar2=float(L * NEXP), op0=ALU.min, op1=ALU.add),
                     reads=[('ef', q) for q in range(128)], writes=['ef'])
                yield
                P.op('dve', lambda e: e.tensor_copy(out=eib[:], in_=ef[:]), reads=['ef'], writes=[rei])
                yield
                P.op('dve', lambda e: e.tensor_tensor(out=gtb[:], in0=ts[:], in1=ts[:, :, 0:1].to_broadcast([128, 8, 16]), op=ALU.subtract),
                     reads=tsres, writes=[rgt])
                yield
                P.op('act', lambda e: e.activation(out=gtb[:], in_=gtb[:], func=AF.Exp), reads=[rgt], writes=[rgt])
                P.op('dve', lambda e: e.tensor_reduce(out=gsum[:], in_=gtb[:], axis=AX.X, op=ALU.add), reads=[rgt], writes=['gsum'])
                yield
                P.op('dve', lambda e: e.reciprocal(out=gsum[:], in_=gsum[:]), reads=['gsum'], writes=['gsum'])
                yield
                P.op('dve', lambda e: e.tensor_tensor(out=gtb[:], in0=gtb[:], in1=gsum[:].unsqueeze(2).to_broadcast([128, 8, 16]), op=ALU.mult),
                     reads=[rgt, 'gsum'], writes=[rgt])
                yield

            def step(gen, n=1):
                if gen is None:
                    return None
                try:
                    for _ in range(n):
                        next(gen)
                except StopIteration:
                    return None
                return gen

            load_sc(0)
            load_sc(1)
            load_xh(0)
            g0 = topk_gen(0)
            while g0 is not None:
                g0 = step(g0, 64)
            nu = nv = 0
            for ti in range(NT):
                b = ti % 2
                xt, ht, eib, gtb, acc = xts[b], hts[b], eis[b], gts[b], accs[b]
                rx, rh, rei, rgt, racc = f'xt{b}', f'ht{b}', f'ei{b}', f'gt{b}', f'acc{b}'
                gt2 = gtb[:].rearrange("p h k -> p (h k)")
                if ti + 1 < NT:
                    load_xh(ti + 1)
                gen = topk_gen(ti + 1) if ti + 1 < NT else None
                for k in range(128):
                    s_ = nu % NB
                    nu += 1
                    P.op('pool', lambda e, s_=s_, k=k, eib=eib: e.indirect_dma_start(
                        out=ub[s_][:], out_offset=None, in_=U,
                        in_offset=bass.IndirectOffsetOnAxis(ap=eib[:, k:k + 1], axis=0)),
                        reads=[rei], writes=[('ub', s_)], dma=True)
                    P.op('dve', lambda e, s_=s_, k=k, ht=ht: e.scalar_tensor_tensor(out=junk[:], in0=ub[s_][:], scalar=1.0, in1=ht[:], op0=ALU.mult, op1=ALU.mult,
                                                                                 accum_out=apre[:, k:k + 1]),
                         reads=[('ub', s_), rh], writes=[('apre', k)])
                    gen = step(gen)
                P.op('act', lambda e: e.activation(out=coef[:], in_=apre[:], func=AF.Gelu), reads=[('apre', k) for k in range(128)], writes=['coef'])
                P.op('dve', lambda e, gt2=gt2: e.tensor_tensor(out=coef[:], in0=coef[:], in1=gt2, op=ALU.mult), reads=['coef', rgt], writes=['coef'])
                for k in range(128):
                    s_ = nv % NB
                    nv += 1
                    P.op('pool', lambda e, s_=s_, k=k, eib=eib: e.indirect_dma_start(
                        out=vb[s_][:], out_offset=None, in_=V,
                        in_offset=bass.IndirectOffsetOnAxis(ap=eib[:, k:k + 1], axis=0)),
                        reads=[rei], writes=[('vb', s_)], dma=True)
                    if k == 0:
                        P.op('dve', lambda e, s_=s_, acc=acc: e.tensor_scalar(out=acc[:], in0=vb[s_][:], scalar1=coef[:, 0:1], scalar2=None, op0=ALU.mult),
                             reads=[('vb', s_), 'coef'], writes=[racc])
                    else:
                        P.op('dve', lambda e, s_=s_, k=k, acc=acc: e.scalar_tensor_tensor(out=acc[:], in0=vb[s_][:], scalar=coef[:, k:k + 1], in1=acc[:],
                                                                                       op0=ALU.mult, op1=ALU.add),
                             reads=[('vb', s_), 'coef', racc], writes=[racc])
                    gen = step(gen)
                while gen is not None:
                    gen = step(gen, 64)
                if ti + 2 < NT:
                    load_sc(ti + 2)
                P.op('dve', lambda e, acc=acc: e.tensor_tensor(out=acc[:], in0=acc[:], in1=self.gate_bc[:], op=ALU.mult), reads=[racc, 'gate_bc'], writes=[racc])
                P.op('dve', lambda e, acc=acc, xt=xt: e.scalar_tensor_tensor(out=acc[:], in0=xt[:], scalar=ALPHA, in1=acc[:], op0=ALU.mult, op1=ALU.add),
                     reads=[rx, racc], writes=[racc])
                self.layernorm_inplace(acc[:], racc, gb='dve')
                P.op('sp', lambda e, acc=acc, ti=ti: e.dma_start(out=dst[ti * 128:(ti + 1) * 128, :], in_=acc[:]), reads=[racc], dma=True)

    def emit_cast_tables(self):
        P, nc, A = self.P, self.nc, self.A
        CN = 4
        Uv = A['peer_u'].rearrange("l (n p) d -> p (l n) d", p=128)
        Vv = A['peer_v'].rearrange("l (n p) d -> p (l n) d", p=128)
        Ov = A['UVB'].rearrange("(n p) d -> p n d", p=128)
        with Stage(self, 'cast') as st:
            iu = [st.T(f'iu{i}', [128, CN, D]) for i in range(2)]
            iv = [st.T(f'iv{i}', [128, CN, D]) for i in range(2)]
            ob = [st.T(f'ob{i}', [128, CN, 2 * D], BF16) for i in range(2)]
            nchunk = (2 * NEXP // 128) // CN

            def ld(ci):
                b = ci % 2
                ns = slice(ci * CN, (ci + 1) * CN)
                P.op('sp', lambda e, b=b, ns=ns: e.dma_start(out=iu[b][:], in_=Uv[:, ns, :]), writes=[f'iu{b}'], dma=True)
                P.op('sp', lambda e, b=b, ns=ns: e.dma_start(out=iv[b][:], in_=Vv[:, ns, :]), writes=[f'iv{b}'], dma=True)

            ld(0)
            for ci in range(nchunk):
                b = ci % 2
                ns = slice(ci * CN, (ci + 1) * CN)
                if ci + 1 < nchunk:
                    ld(ci + 1)
                P.op('dve', lambda e, b=b: e.tensor_copy(out=ob[b][:, :, 0:D], in_=iu[b][:]), reads=[f'iu{b}'], writes=[(f'ob{b}', 0)])
                P.op('act', lambda e, b=b: e.copy(out=ob[b][:, 0:2, D:2 * D], in_=iv[b][:, 0:2, :]), reads=[f'iv{b}'], writes=[(f'ob{b}', 1)])
                P.op('pool', lambda e, b=b: e.tensor_copy(out=ob[b][:, 2:4, D:2 * D], in_=iv[b][:, 2:4, :]), reads=[f'iv{b}'], writes=[(f'ob{b}', 2)])
                P.op('act', lambda e, b=b, ns=ns: e.dma_start(out=Ov[:, ns, :], in_=ob[b][:]), reads=[(f'ob{b}', 0), (f'ob{b}', 1), (f'ob{b}', 2)],
                     writes=[('UVB', ci)], dma=True)

    def emit_peer2g(self, xin, dst, L):
        P, nc, A, ps = self.P, self.nc, self.A, self.ps
        NB = self.cfg.get('nb', 20)
        GS = 16
        UVB = A['UVB']
        with Stage(self, f'pg{L}') as st:
            scs = [st.T(f'sc{i}', [128, 16, 128]) for i in range(2)]
            m = st.T('m', [128, 16, 16])
            ix = st.T('ix', [128, 16, 16], U32)
            ixf = st.T('ixf', [128, 16, 16])
            wk = st.T('wk', [128, 16, 128])
            cand = st.T('cand', [128, 8, 256])
            candi = st.T('candi', [128, 8, 256])
            wk2 = st.T('wk2', [128, 8, 256])
            junk2 = [st.T(f'junk2{i}', [128, 256]) for i in range(2)]
            ts = st.T('ts', [128, 8, 16])
            ef = st.T('ef', [128, 128])
            eis = [st.T(f'ei{i}', [128, 128], I32) for i in range(2)]
            gts = [st.T(f'gt{i}', [128, 8, 16]) for i in range(2)]
            gsum = st.T('gsum', [128, 8])
            xts = [st.T(f'xt{i}', [128, D]) for i in range(2)]
            hts = [st.T(f'ht{i}', [128, D]) for i in range(2)]
            uvb = [st.T(f'uv{i}', [128, 2 * D], BF16) for i in range(NB)]
            junk = st.T('junk', [128, D], BF16)
            junka = st.T('junka', [128, D], BF16)
            prods = [st.T(f'prod{i}', [128, D], BF16) for i in range(3)]
            hbs = [st.T(f'hb{i}', [128, D], BF16) for i in range(2)]
            apre = st.T('apre', [128, 128])
            ge = st.T('ge', [128, 128])
            cf = st.T('cf', [128, 128])
            dgs = [st.T(f'dg{i}', [128, 128], BF16) for i in range(4)]
            accs = [st.T(f'acc{i}', [128, D]) for i in range(2)]

            def load_sc(t):
                P.op('sp', lambda e, t=t: e.dma_start(out=scs[t % 2][:].rearrange("p a k -> p (a k)"), in_=A['SCR'][t * 128:(t + 1) * 128, :]),
                     writes=[f'sc{t % 2}'], dma=True)

            def load_xh(t):
                P.op('sp', lambda e, t=t: e.dma_start(out=xts[t % 2][:], in_=xin[t * 128:(t + 1) * 128, :]), writes=[f'xt{t % 2}'], dma=True)
                P.op('sp', lambda e, t=t: e.dma_start(out=hts[t % 2][:], in_=A['H'][t * 128:(t + 1) * 128, :]), writes=[f'ht{t % 2}'], dma=True)

            def topk_gen(t):
                sc = scs[t % 2]
                rsc = f'sc{t % 2}'
                eib, gtb = eis[t % 2], gts[t % 2]
                rei, rgt = f'ei{t % 2}', f'gt{t % 2}'
                for c in range(16):
                    P.op('dve', lambda e, c=c: e.max(out=m[:, c, 0:8], in_=sc[:, c, :]), reads=[rsc], writes=[('m0', c)])
                    yield
                P.fence('dve')
                for c in range(16):
                    P.op('dve', lambda e, c=c: e.max_index(out=ix[:, c, 0:8], in_max=m[:, c, 0:8], in_values=sc[:, c, :]),
                         reads=[rsc, ('m0', c)], writes=[('ix0', c)])
                    yield
                    P.op('dve', lambda e, c=c: e.match_replace(out=wk[:, c, :], in_to_replace=m[:, c, 0:8], in_values=sc[:, c, :], imm_value=-1e30),
                         reads=[rsc, ('m0', c)], writes=[('wk', c)])
                    yield
                P.fence('dve')
                for c in range(16):
                    P.op('dve', lambda e, c=c: e.max(out=m[:, c, 8:16], in_=wk[:, c, :]), reads=[('wk', c)], writes=[('m1', c)])
                    yield
                P.fence('dve')
                for c in range(16):
                    P.op('dve', lambda e, c=c: e.max_index(out=ix[:, c, 8:16], in_max=m[:, c, 8:16], in_values=wk[:, c, :]),
                         reads=[('wk', c), ('m1', c)], writes=[('ix1', c)])
                    yield
                P.fence('dve')
                mres = [('m0', c) for c in range(16)] + [('m1', c) for c in range(16)]
                ixres = [('ix0', c) for c in range(16)] + [('ix1', c) for c in range(16)]
                P.op('dve', lambda e: e.tensor_copy(out=ixf[:], in_=ix[:]), reads=ixres, writes=['ixf'])
                yield
                m4 = m[:].rearrange("p (h two) k -> p h two k", two=2)
                i4 = ixf[:].rearrange("p (h two) k -> p h two k", two=2)
                c4 = cand[:].rearrange("p h (a b) -> p h a b", a=16)
                ci4 = candi[:].rearrange("p h (a b) -> p h a b", a=16)
                P.op('dve', lambda e: e.tensor_tensor(out=c4, in0=m4[:, :, 0, :].unsqueeze(3).to_broadcast([128, 8, 16, 16]),
                                                      in1=m4[:, :, 1, :].unsqueeze(2).to_broadcast([128, 8, 16, 16]), op=ALU.add),
                     reads=mres, writes=['cand'])
                yield
                P.op('dve', lambda e: e.tensor_scalar(out=i4[:, :, 0, :], in0=i4[:, :, 0, :], scalar1=128.0, scalar2=None, op0=ALU.mult),
                     reads=['ixf'], writes=['ixf'])
                yield
                P.op('dve', lambda e: e.tensor_tensor(out=ci4, in0=i4[:, :, 0, :].unsqueeze(3).to_broadcast([128, 8, 16, 16]),
                                                      in1=i4[:, :, 1, :].unsqueeze(2).to_broadcast([128, 8, 16, 16]), op=ALU.add),
                     reads=['ixf'], writes=['candi'])
                yield
                for h in range(8):
                    P.op('dve', lambda e, h=h: e.max(out=ts[:, h, 0:8], in_=cand[:, h, :]), reads=['cand'], writes=[('ts0', h)])
                    yield
                P.fence('dve')
                for h in range(8):
                    P.op('dve', lambda e, h=h: e.match_replace(out=wk2[:, h, :], in_to_replace=ts[:, h, 0:8], in_values=cand[:, h, :], imm_value=-1e30),
                         reads=['cand', ('ts0', h)], writes=[('wk2', h)])
                    yield
                P.fence('dve')
                for h in range(8):
                    P.op('dve', lambda e, h=h: e.max(out=ts[:, h, 8:16], in_=wk2[:, h, :]), reads=[('wk2', h)], writes=[('ts1', h)])
                    yield
                P.fence('dve')
                tsres = [('ts0', h) for h in range(8)] + [('ts1', h) for h in range(8)]
                for h in range(8):
                    for k in range(16):
                        P.op('dve', lambda e, h=h, k=k: e.scalar_tensor_tensor(out=junk2[(h * 16 + k) % 2][:], in0=cand[:, h, :], scalar=ts[:, h, k:k + 1], in1=candi[:, h, :],
                                                                               op0=ALU.is_equal, op1=ALU.mult, accum_out=ef[:, h * 16 + k:h * 16 + k + 1]),
                             reads=['cand', 'candi', ('ts0', h), ('ts1', h)], writes=[('ef', h * 16 + k)])
                        yield
                P.fence('dve')
                P.op('dve', lambda e: e.tensor_scalar(out=ef[:], in0=ef[:], scalar1=float(NEXP - 1), scalar2=float(L * NEXP), op0=ALU.min, op1=ALU.add),
                     reads=[('ef', q) for q in range(128)], writes=['ef'])
                yield
                P.op('dve', lambda e: e.tensor_copy(out=eib[:], in_=ef[:]), reads=['ef'], writes=[rei])
                yield
                P.op('dve', lambda e: e.tensor_tensor(out=gtb[:], in0=ts[:], in1=ts[:, :, 0:1].to_broadcast([128, 8, 16]), op=ALU.subtract),
                     reads=tsres, writes=[rgt])
                yield
                P.op('act', lambda e: e.activation(out=gtb[:], in_=gtb[:], func=AF.Exp), reads=[rgt], writes=[rgt])
                P.op('dve', lambda e: e.tensor_reduce(out=gsum[:], in_=gtb[:], axis=AX.X, op=ALU.add), reads=[rgt], writes=['gsum'])
                yield
                P.op('dve', lambda e: e.reciprocal(out=gsum[:], in_=gsum[:]), reads=['gsum'], writes=['gsum'])
                yield
                P.op('dve', lambda e: e.tensor_tensor(out=gtb[:], in0=gtb[:], in1=gsum[:].unsqueeze(2).to_broadcast([128, 8, 16]), op=ALU.mult),
                     reads=[rgt, 'gsum'], writes=[rgt])
                yield

            def step(gen, n=1):
                if gen is None:
                    return None
                try:
                    for _ in range(n):
                        next(gen)
                except StopIteration:
                    return None
                return gen

            load_sc(0)
            load_sc(1)
            load_xh(0)
            g0 = topk_gen(0)
            while g0 is not None:
                g0 = step(g0, 64)
            nu = 0
            ndg = 0
            npr = 0
            for ti in range(NT):
                b = ti % 2
                xt, ht, eib, gtb, acc = xts[b], hts[b], eis[b], gts[b], accs[b]
                rx, rh, rei, rgt, racc = f'xt{b}', f'ht{b}', f'ei{b}', f'gt{b}', f'acc{b}'
                gt2 = gtb[:].rearrange("p h k -> p (h k)")
                pa = [ps[(ti % 2) * 2], ps[(ti % 2) * 2 + 1]]
                rpa = [f'ps{(ti % 2) * 2}', f'ps{(ti % 2) * 2 + 1}']
                if ti + 1 < NT:
                    load_xh(ti + 1)
                hb, rhb = hbs[b], f'hb{b}'
                P.op('dve', lambda e, hb=hb, ht=ht: e.tensor_copy(out=hb[:], in_=ht[:]), reads=[rh], writes=[rhb])
                gen = topk_gen(ti + 1) if ti + 1 < NT else None
                for g in range(128 // GS):
                    slots = []
                    for kk in range(GS):
                        k = g * GS + kk
                        s_ = nu % NB
                        nu += 1
                        slots.append(s_)
                        P.op('pool', lambda e, s_=s_, k=k, eib=eib: e.indirect_dma_start(
                            out=uvb[s_][:], out_offset=None, in_=UVB,
                            in_offset=bass.IndirectOffsetOnAxis(ap=eib[:, k:k + 1], axis=0)),
                            reads=[rei], writes=[('uv', s_)], dma=True)
                        if k % 2 == 0 or not self.cfg.get('dot_split', True):
                            P.op('dve', lambda e, s_=s_, k=k, ht=ht: e.scalar_tensor_tensor(out=junk[:], in0=uvb[s_][:, 0:D], scalar=1.0, in1=ht[:], op0=ALU.mult, op1=ALU.mult,
                                                                                         accum_out=apre[:, k:k + 1]),
                                 reads=[('uv', s_), rh], writes=[('apre', k)])
                        else:
                            pr = prods[npr % 3]
                            rpr = f'prod{npr % 3}'
                            npr += 1
                            P.op('dve', lambda e, s_=s_, pr=pr, hb=hb: e.tensor_tensor(out=pr[:], in0=uvb[s_][:, 0:D], in1=hb[:], op=ALU.mult),
                                 reads=[('uv', s_), rhb], writes=[rpr])
                            P.op('act', lambda e, pr=pr, k=k: e.activation(out=junka[:], in_=pr[:], func=AF.Copy, accum_out=apre[:, k:k + 1]),
                                 reads=[rpr], writes=[('apre', k)])
                        gen = step(gen, 2)
                    gsl = slice(g * GS, (g + 1) * GS)
                    P.op('act', lambda e, gsl=gsl: e.activation(out=ge[:, gsl], in_=apre[:, gsl], func=AF.Gelu),
                         reads=[('apre', k) for k in range(g * GS, (g + 1) * GS)], writes=[('ge', g)])
                    P.op('dve', lambda e, gsl=gsl, gt2=gt2: e.tensor_tensor(out=cf[:, gsl], in0=ge[:, gsl], in1=gt2[:, gsl], op=ALU.mult),
                         reads=[('ge', g), rgt], writes=[('cf', g)])
                    for kk in range(GS):
                        k = g * GS + kk
                        s_ = slots[kk]
                        dg = dgs[ndg % 4]
                        rdg = f'dg{ndg % 4}'
                        ndg += 1
                        P.op('act', lambda e, dg=dg, k=k: e.activation(out=dg[:], in_=self.ident[:], func=AF.Copy, scale=cf[:, k:k + 1]),
                             reads=['ident', ('cf', g)], writes=[rdg])
                        for half in range(2):
                            P.op('pe', lambda e, dg=dg, s_=s_, half=half, k=k, pa=pa: e.matmul(out=pa[half][:], lhsT=dg[:], rhs=uvb[s_][:, D + half * 512:D + (half + 1) * 512],
                                                                                           start=(k == 0), stop=(k == 127)),
                                 reads=[rdg, ('uv', s_)], writes=[rpa[half]])
                while gen is not None:
                    gen = step(gen, 64)
                if ti + 2 < NT:
                    load_sc(ti + 2)
                for half in range(2):
                    P.op('dve', lambda e, acc=acc, half=half, pa=pa: e.tensor_tensor(out=acc[:, half * 512:(half + 1) * 512], in0=pa[half][:],
                                                                                   in1=self.gate_bc[:, half * 512:(half + 1) * 512], op=ALU.mult),
                         reads=[rpa[half], 'gate_bc'], writes=[racc])
                P.op('dve', lambda e, acc=acc, xt=xt: e.scalar_tensor_tensor(out=acc[:], in0=xt[:], scalar=ALPHA, in1=acc[:], op0=ALU.mult, op1=ALU.add),
                     reads=[rx, racc], writes=[racc])
                self.layernorm_inplace(acc[:], racc, gb='dve')
                P.op('sp', lambda e, acc=acc, ti=ti: e.dma_start(out=dst[ti * 128:(ti + 1) * 128, :], in_=acc[:]), reads=[racc], dma=True)

    def emit_attn1(self, xin):
        P, nc, A, ps = self.P, self.nc, self.A, self.ps
        ones = self.ones
        NCOL = 3 * D + 16
        with Stage(self, 'a1') as st:
            winr = st.T('winr', [128, 8, 3 * D], F32R)
            wstg = [st.T('wstg0', [128, 8, 512])] * 2
            wf = st.T('wf', [128, 8, 16])
            qkb = st.T('qkb', [128, 16])
            vbr = st.T('vbr', [1, D])
            vb_bc = st.T('vb_bc', [128, D])
            fb = st.T('fb', [16, 1])
            xts = [st.T(f'xt{i}', [128, D]) for i in range(2)]
            ht = st.T('ht', [128, D])
            hT = st.T('hT', [128, 8, 512], F32R)
            qko = [st.T(f'qko{i}', [128, 512]) for i in range(2)]
            vo = [st.T(f'vo{i}', [128, D]) for i in range(2)]
            Fcb = [st.T(f'Fcb{i}', [16, 512]) for i in range(2)]
            Frb = st.T('Frb', [16, 512], F32R)
            Flb = st.T('Flb', [16, 512])
            nFr = st.T('nFr', [16, 512])
            nFl = st.T('nFl', [16, 512])
            spt = st.T('spt', [16, 512])
            o16 = st.T('o16', [16, 512])
            for q in range(6):
                wb = wstg[0]
                rw = 'wstg0'
                P.op('sp', lambda e, q=q, wb=wb: e.dma_start(out=wb[:], in_=A['attn_in_w'][:, q * 512:(q + 1) * 512].rearrange("(k p) n -> p k n", p=128)),
                     writes=[rw], dma=True)
                eng = ('dve', 'pool')[q % 2]
                P.op(eng, lambda e, q=q, wb=wb: e.tensor_copy(out=winr[:, :, q * 512:(q + 1) * 512], in_=wb[:]), reads=[rw], writes=[('win', q)])
            P.op('sp', lambda e: e.dma_start(out=wf[:], in_=A['attn_in_w'][:, 3 * D:NCOL].rearrange("(k p) n -> p k n", p=128)),
                 writes=['wf'], dma=True)
            P.op('sp', lambda e: e.dma_start(out=qkb[:], in_=A['attn_qkb_l']), writes=['qkb'], dma=True)
            P.op('sp', lambda e: e.dma_start(out=vbr[:], in_=A['attn_vb']), writes=['vbr'], dma=True)
            P.op('sp', lambda e: e.dma_start(out=fb[:], in_=A['attn_fb']), writes=['fb'], dma=True)
            P.op('dve', lambda e: e.tensor_scalar(out=qkb[:, 0:8], in0=qkb[:, 0:8], scalar1=0.125, scalar2=None, op0=ALU.mult), reads=['qkb'], writes=['qkb'])
            P.op('dve', lambda e: e.tensor_scalar(out=fb[:], in0=fb[:], scalar1=-1.0, scalar2=None, op0=ALU.mult), reads=['fb'], writes=['fb'])
            P.op('pool', lambda e: e.memset(o16[:], 1.0), writes=['o16'])
            for half in range(2):
                P.op('pe', lambda e, half=half: e.matmul(out=ps[4 + half][:], lhsT=ones[0:1, :], rhs=vbr[0:1, half * 512:(half + 1) * 512], start=True, stop=True),
                     reads=['ones', 'vbr'], writes=[f'ps{4 + half}'])
                P.op('act', lambda e, half=half: e.copy(out=vb_bc[:, half * 512:(half + 1) * 512], in_=ps[4 + half][:]), reads=[f'ps{4 + half}'], writes=['vb_bc'])
            ti = 0
            for jb in range(8):
                cols = slice(jb * 512, (jb + 1) * 512)
                for tl in range(4):
                    xt = xts[ti % 2]
                    rx = f'xt{ti % 2}'
                    P.op('sp', lambda e, xt=xt, ti=ti: e.dma_start(out=xt[:], in_=xin[ti * 128:(ti + 1) * 128, :]), writes=[rx], dma=True)
                    self.modulate(xt[:], ht[:], rx, 'ht')
                    self.transpose8(ht, 'ht', hT, 'hT', tl * 128, 0)
                    ti += 1
                for c in range(16):
                    bank = ps[2 + c % 2]
                    rb = f'ps{2 + c % 2}'
                    for k in range(8):
                        P.op('pe', lambda e, k=k, c=c, bank=bank: e.matmul(out=bank[:], lhsT=winr[:, k, c * 128:(c + 1) * 128], rhs=hT[:, k, :],
                                                                          start=(k == 0), stop=(k == 7)),
                             reads=[('win', c // 4), 'hT'], writes=[rb])
                    ob = qko[c % 2]
                    rob = f'qko{c % 2}'
                    P.op('act', lambda e, c=c, bank=bank, ob=ob: e.activation(out=ob[:], in_=bank[:], func=AF.Identity, bias=qkb[:, c:c + 1],
                                                                             scale=(0.125 if c < 8 else 1.0)),
                         reads=[rb, 'qkb'], writes=[rob])
                    dstt = A['QA'] if c < 8 else A['KA']
                    for hh in range(2):
                        head = (c % 8) * 2 + hh
                        P.op('sp', lambda e, ob=ob, hh=hh, head=head, dstt=dstt: e.dma_start(out=dstt[head, 0:64, cols], in_=ob[hh * 64:(hh + 1) * 64, :]),
                             reads=[rob], writes=[('QK', c, hh)], dma=True)
                for tl in range(4):
                    tix = jb * 4 + tl
                    vt = vo[tix % 2]
                    rv = f'vo{tix % 2}'
                    for half in range(2):
                        bank = ps[4 + half]
                        rb = f'ps{4 + half}'
                        for k in range(8):
                            P.op('pe', lambda e, k=k, bank=bank, half=half, tl=tl: e.matmul(out=bank[:], lhsT=hT[:, k, tl * 128:(tl + 1) * 128],
                                                                                        rhs=winr[:, k, 2 * D + half * 512:2 * D + (half + 1) * 512],
                                                                                        start=(k == 0), stop=(k == 7)),
                                 reads=['hT', ('win', 4 + half)], writes=[rb])
                        P.op('dve', lambda e, vt=vt, bank=bank, half=half: e.tensor_tensor(out=vt[:, half * 512:(half + 1) * 512], in0=bank[:],
                                                                                      in1=vb_bc[:, half * 512:(half + 1) * 512], op=ALU.add),
                             reads=[rb, 'vb_bc'], writes=[rv])
                    P.op('sp', lambda e, vt=vt, tix=tix: e.dma_start(out=A['V'][tix * 128:(tix + 1) * 128, :], in_=vt[:]), reads=[rv], writes=[('V', tix)], dma=True)
                for k in range(8):
                    P.op('pe', lambda e, k=k: e.matmul(out=ps[6][0:16, :], lhsT=wf[:, k, :], rhs=hT[:, k, :].bitcast(F32), start=(k == 0), stop=(k == 7)),
                         reads=['wf', 'hT'], writes=['ps6'])
                P.op('act', lambda e: e.activation(out=spt[:], in_=ps[6][0:16, :], func=AF.Exp, bias=fb[:, 0:1], scale=-1.0), reads=['ps6', 'fb'], writes=['spt'])
                P.op('act', lambda e: e.activation(out=spt[:], in_=spt[:], func=AF.Ln, bias=1.0, scale=1.0), reads=['spt'], writes=['spt'])
                P.op('dve', lambda e: e.tensor_scalar(out=spt[:], in0=spt[:], scalar1=-1.0, scalar2=None, op0=ALU.mult), reads=['spt'], writes=['spt'])
                Fc = Fcb[jb % 2]
                rF = f'Fcb{jb % 2}'
                init = 0.0 if jb == 0 else Fcb[(jb - 1) % 2][:, 511:512]
                P.op('dve', lambda e, init=init, Fc=Fc: e.tensor_tensor_scan(out=Fc[:], data0=o16[:], data1=spt[:], initial=init,
                                                                             op0=ALU.mult, op1=ALU.add),
                     reads=['o16', 'spt', f'Fcb{(jb - 1) % 2}'], writes=[rF])
                P.op('dve', lambda e, Fc=Fc: e.tensor_copy(out=Frb[:], in_=Fc[:]), reads=[rF], writes=['Frb'])
                P.op('dve', lambda e, Fc=Fc: e.tensor_tensor(out=Flb[:], in0=Fc[:], in1=Frb[:].bitcast(F32), op=ALU.subtract), reads=[rF, 'Frb'], writes=['Flb'])
                P.op('dve', lambda e: e.tensor_scalar(out=nFr[:], in0=Frb[:].bitcast(F32), scalar1=-1.0, scalar2=None, op0=ALU.mult), reads=['Frb'], writes=['nFr'])
                P.op('dve', lambda e: e.tensor_scalar(out=nFl[:], in0=Flb[:], scalar1=-1.0, scalar2=None, op0=ALU.mult), reads=['Flb'], writes=['nFl'])
                P.op('sp', lambda e: e.dma_start(out=A['QA'][:, 64, cols], in_=Frb[:].bitcast(F32)), reads=['Frb'], writes=[('QAf', jb)], dma=True)
                P.op('sp', lambda e: e.dma_start(out=A['QA'][:, 65, cols], in_=Flb[:]), reads=['Flb'], writes=[('QAl', jb)], dma=True)
                P.op('sp', lambda e: e.dma_start(out=A['KA'][:, 66, cols], in_=nFr[:]), reads=['nFr'], writes=[('KAf', jb)], dma=True)
                P.op('sp', lambda e: e.dma_start(out=A['KA'][:, 67, cols], in_=nFl[:]), reads=['nFl'], writes=[('KAl', jb)], dma=True)
                for r in (66, 67):
                    P.op('sp', lambda e, r=r: e.dma_start(out=A['QA'][:, r, cols], in_=o16[:]), reads=['o16'], writes=[('QAo', r, jb)], dma=True)
                for r in (64, 65):
                    P.op('sp', lambda e, r=r: e.dma_start(out=A['KA'][:, r, cols], in_=o16[:]), reads=['o16'], writes=[('KAo', r, jb)], dma=True)

    def emit_attn2(self):
        P, nc, A, ps = self.P, self.nc, self.A, self.ps
        NR = 68
        with Stage(self, 'a2') as st:
            qst = st.T('qst', [NR, S])
            kst = st.T('kst', [NR, S])
            vst = st.T('vst', [128, 32, 64])
            QAh = [st.T(f'QAh{i}', [NR, S], F32R) for i in range(2)]
            KAh = [st.T(f'KAh{i}', [NR, S], F32R) for i in range(2)]
            Vh = [st.T(f'Vh{i}', [128, 32, 128], F32R) for i in range(2)]
            ones_r = st.T('ones_r', [128, 128], F32R)
            pt = [st.T(f'pt{i}', [128, 512], F32R) for i in range(3)]
            lm = [st.T(f'lm{i}', [128, 512]) for i in range(2)]
            mask = st.T('mask', [128, 4, 512])
            rzt = st.T('rzt', [64, 512])
            oT = [st.T(f'oT{i}', [64, 512]) for i in range(2)]
            P.op('pool', lambda e: e.memset(mask[:], 0.0), writes=['mask'])
            for i4 in range(4):
                P.op('pool', lambda e, i4=i4: e.affine_select(out=mask[:, i4, :], in_=mask[:, i4, :], pattern=[[1, 512]], compare_op=ALU.is_ge,
                                                              fill=NEG, base=-128 * i4, channel_multiplier=-1), reads=['mask'], writes=['mask'])
            P.op('pool', lambda e: e.tensor_copy(out=ones_r[:], in_=self.ones[:]), reads=['ones'], writes=['ones_r'])
            npt = 0
            nlm = 0
            nS = 0
            nO = 0

            def loads(h):
                b = h % 2
                qa, ka, vh = QAh[b], KAh[b], Vh[b]
                rq, rk, rv = f'QAh{b}', f'KAh{b}', f'Vh{b}'
                for q4 in range(4):
                    cs = slice(q4 * 1024, (q4 + 1) * 1024)
                    P.op('sp', lambda e, h=h, cs=cs: e.dma_start(out=qst[:, cs], in_=A['QA'][h, :, cs]), writes=[('qst', q4)], dma=True)
                    P.op('sp', lambda e, h=h, cs=cs: e.dma_start(out=kst[:, cs], in_=A['KA'][h, :, cs]), writes=[('kst', q4)], dma=True)
                    P.op('sp', lambda e, h=h, q4=q4: e.dma_start(
                        out=vst[:, q4 * 8:(q4 + 1) * 8, :],
                        in_=A['V'][q4 * 1024:(q4 + 1) * 1024, h * 64:(h + 1) * 64].rearrange("(i p) d -> p i d", p=128)),
                        writes=[('vst', q4)], dma=True)
                for q4 in range(4):
                    cs = slice(q4 * 1024, (q4 + 1) * 1024)
                    P.op('pool', lambda e, qa=qa, cs=cs: e.tensor_copy(out=qa[:, cs], in_=qst[:, cs]), reads=[('qst', q4)], writes=[rq])
                    P.op('pool', lambda e, ka=ka, cs=cs: e.tensor_copy(out=ka[:, cs], in_=kst[:, cs]), reads=[('kst', q4)], writes=[rk])
                    for dup in range(2):
                        P.op('pool', lambda e, vh=vh, q4=q4, dup=dup: e.tensor_copy(out=vh[:, q4 * 8:(q4 + 1) * 8, dup * 64:(dup + 1) * 64],
                                                                                  in_=vst[:, q4 * 8:(q4 + 1) * 8, :]),
                             reads=[('vst', q4)], writes=[rv])

            loads(0)
            NH = self.cfg.get('nheads', 16)
            steps = [(h, j, i) for h in range(NH) for j in range(8) for i in range(4 * j + 4)]

            def emit_qk(n):
                h, j, i = steps[n]
                b = h % 2
                sb = ps[n % 3]
                P.op('pe', lambda e, sb=sb, ka=KAh[b], qa=QAh[b], i=i, j=j: e.matmul(out=sb[:], lhsT=ka[:, i * 128:(i + 1) * 128], rhs=qa[:, j * 512:(j + 1) * 512],
                                                                                 start=True, stop=True),
                     reads=[f'KAh{b}', f'QAh{b}'], writes=[f'ps{n % 3}'])

            emit_qk(0)
            for n, (h, j, i) in enumerate(steps):
                b = h % 2
                vh, rv = Vh[b], f'Vh{b}'
                if j == 0 and i == 0 and h + 1 < NH:
                    loads(h + 1)
                if n + 1 < len(steps):
                    emit_qk(n + 1)
                if i == 0:
                    oset = nO % 2
                    nO += 1
                poA, poB = ps[3 + 2 * oset], ps[4 + 2 * oset]
                rA, rB = f'ps{3 + 2 * oset}', f'ps{4 + 2 * oset}'
                ot, rot = oT[oset], f'oT{oset}'
                last = 4 * j + 3
                sb, rsb = ps[n % 3], f'ps{n % 3}'
                p_, rp = pt[n % 3], f'pt{n % 3}'
                if i >= 4 * j:
                    l_ = lm[nlm % 2]
                    rl = f'lm{nlm % 2}'
                    nlm += 1
                    P.op('dve', lambda e, l_=l_, sb=sb, i=i, j=j: e.tensor_tensor(out=l_[:], in0=sb[:], in1=mask[:, i - 4 * j, :], op=ALU.add),
                         reads=[rsb, 'mask'], writes=[rl])
                    P.op('act', lambda e, p_=p_, l_=l_: e.activation(out=p_[:], in_=l_[:], func=AF.Exp), reads=[rl], writes=[rp])
                else:
                    P.op('act', lambda e, p_=p_, sb=sb: e.activation(out=p_[:], in_=sb[:], func=AF.Exp), reads=[rsb], writes=[rp])
                P.op('pe', lambda e, p_=p_, i=i, poA=poA, vh=vh, last=last: e.matmul(out=poA[:], lhsT=vh[:, i, :], rhs=p_[:], start=(i == 0), stop=(i == last)),
                     reads=[rp, rv], writes=[rA])
                P.op('pe', lambda e, p_=p_, i=i, poB=poB, last=last: e.matmul(out=poB[:], lhsT=ones_r[:], rhs=p_[:], start=(i == 0), stop=(i == last)),
                     reads=[rp, 'ones_r'], writes=[rB])
                if i == last:
                    P.op('dve', lambda e, poB=poB: e.reciprocal(out=rzt[:], in_=poB[0:64, :]), reads=[rB], writes=['rzt'])
                    P.op('dve', lambda e, poA=poA, ot=ot: e.tensor_tensor(out=ot[:], in0=poA[0:64, :], in1=rzt[:], op=ALU.mult), reads=[rA, 'rzt'], writes=[rot])
                    P.op('sp', lambda e, ot=ot, h=h, j=j: e.dma_start(out=A['AOT'][h // 2, (h % 2) * 64:(h % 2) * 64 + 64, j * 512:(j + 1) * 512], in_=ot[:]),
                         reads=[rot], writes=[('AOT', h, j)], dma=True)


def make_in_maps(inputs, cores=range(8)):
    f = lambda a: np.ascontiguousarray(np.asarray(a, dtype=np.float32))
    sh = {}
    sh['ada_mix_w'] = f(inputs['ada_mix_w'])
    sh['ada_ffn_w'] = f(inputs['ada_ffn_w'])
    amb, afb = f(inputs['ada_mix_b']), f(inputs['ada_ffn_b'])
    sh['ada_b'] = f(np.stack([amb[0], afb[0], amb[1], afb[1]]))
    g1, g2 = f(inputs['ln_mix_g']), f(inputs['ln_ffn_g'])
    b1, b2 = f(inputs['ln_mix_b']), f(inputs['ln_ffn_b'])
    sh['ln_g'] = f(np.stack([g1[0], g2[0], g1[1], g2[1]]))
    sh['ln_b'] = f(np.stack([b1[0], b2[0], b1[1], b2[1]]))
    sh['conv_in_w'] = f(inputs['conv_in_w'][0])
    sh['conv_in_b_l'] = f(np.asarray(inputs['conv_in_b'][0]).reshape(16, 128).T)
    sh['conv_dw_w_l'] = f(np.asarray(inputs['conv_dw_w'][0]).reshape(31, 8, 128).transpose(2, 1, 0))
    sh['conv_vec_l'] = f(np.stack([np.asarray(inputs[k][0]).reshape(8, 128).T for k in ('conv_dw_b', 'conv_ln_g', 'conv_ln_b')], axis=1))
    sh['conv_out_w'] = f(inputs['conv_out_w'][0])
    sh['conv_out_b'] = f(np.asarray(inputs['conv_out_b'][0]).reshape(1, D))
    sh['attn_in_w'] = f(inputs['attn_in_w'][0])
    ab = np.asarray(inputs['attn_in_b'][0])
    sh['attn_qkb_l'] = f(ab[:2 * D].reshape(16, 128).T)
    sh['attn_vb'] = f(ab[2 * D:3 * D].reshape(1, D))
    sh['attn_fb'] = f(ab[3 * D:].reshape(16, 1))
    sh['attn_out_w'] = f(inputs['attn_out_w'][0])
    sh['attn_out_b'] = f(np.asarray(inputs['attn_out_b'][0]).reshape(1, D))
    sh['peer_query_w'] = f(inputs['peer_query_w'])
    k1, k2 = np.asarray(inputs['peer_sub_keys_1']), np.asarray(inputs['peer_sub_keys_2'])
    sh['peer_skT'] = f(np.stack([np.stack([k1[l].T, k2[l].T]) for l in range(2)]))
    sh['peer_u'] = f(inputs['peer_expert_u'])
    sh['peer_v'] = f(inputs['peer_expert_v'])
    x = np.asarray(inputs['x'])
    c = np.asarray(inputs['c'])
    maps = []
    for b in cores:
        m = dict(sh)
        m['x'] = f(x[b])
        m['c_l'] = f(c[b].reshape(8, 128).T)
        maps.append(m)
    return maps


_NC_CACHE = {}


def kernel(**inputs):
    if 'full' not in _NC_CACHE:
        _NC_CACHE['full'] = Kern({}).build()
    nc = _NC_CACHE['full']
    maps = make_in_maps(inputs)
    res = run_bass_kernel_spmd(nc, maps, core_ids=list(range(8)))
    return np.stack([np.asarray(r['out'], dtype=np.float32) for r in res.results], axis=0)
```

```python
import numpy as np
from contextlib import ExitStack
import concourse.bass as bass
import concourse.mybir as mybir
from concourse.bass_utils import run_bass_kernel_spmd

F32 = mybir.dt.float32
I32 = mybir.dt.int32
U32 = mybir.dt.uint32
F32R = mybir.dt.float32r
BF16 = mybir.dt.bfloat16
ALU = mybir.AluOpType
AF = mybir.ActivationFunctionType
AX = mybir.AxisListType

S = 4096
D = 1024
NT = S // 128
ALPHA = float((2 * 2) ** 0.25)
EPS = 1e-5
NEXP = 16384
MAXV = 30000
NEG = -30000.0


class Prog:
    def __init__(self, nc, es):
        self.nc = nc
        self.es = es
        self.eng = {'pe': nc.tensor, 'dve': nc.vector, 'act': nc.scalar,
                    'pool': nc.gpsimd, 'sp': nc.sync}
        self.seq = {e: 0 for e in self.eng}
        self.csem = {e: [] for e in self.eng}
        self.known = {e: {} for e in self.eng}
        self.snap = {}
        self.last_w = {}
        self.readers = {}
        self.semobj = {}
        self.dma_pool = {}
        self.nsem = 0
        self.nwaits = 0
        self.nops = 0
        for q, n in (('sp', 24), ('pool', 24), ('act', 8)):
            self.dma_pool[q] = {'sems': [self._newsem(f"d{q}{i}") for i in range(n)],
                                'cnt': [0] * n, 'next': 0}

    def _newsem(self, name):
        s = self.es.enter_context(self.nc.semaphore(name))
        self.semobj[name] = s
        self.nsem += 1
        return name

    def _need(self, e, tok, skip_self):
        if tok is None:
            return
        name, val, owner = tok
        if skip_self and owner == e:
            return
        if self.known[e].get(name, 0) >= val:
            return
        self.eng[e].wait_ge(self.semobj[name], val)
        self.nwaits += 1
        k = self.known[e]
        k[name] = val
        sn = self.snap.get((name, val))
        if sn:
            for n2, v2 in sn.items():
                if k.get(n2, 0) < v2:
                    k[n2] = v2

    def op(self, e, fn, reads=(), writes=(), dma=False, skip_self=None):
        if skip_self is None:
            skip_self = (e == 'pe')
        if dma:
            skip_self = False
        for r in reads:
            self._need(e, self.last_w.get(r), skip_self)
        for w in writes:
            self._need(e, self.last_w.get(w), skip_self)
            for t in self.readers.get(w, ()):
                self._need(e, t, skip_self)
        self.nops += 1
        if dma:
            pool = self.dma_pool[e]
            i = pool['next']
            pool['next'] = (i + 1) % len(pool['sems'])
            name = pool['sems'][i]
            if pool['cnt'][i] + 16 > MAXV:
                name = self._newsem(f"{name}r{self.nsem}")
                pool['sems'][i] = name
                pool['cnt'][i] = 0
            prev = pool['cnt'][i]
            if prev > 0:
                self._need(e, (name, prev, e + '_dma'), False)
            ins = fn(self.eng[e])
            pool['cnt'][i] = prev + 16
            ins.then_inc(self.semobj[name], 16)
            tok = (name, prev + 16, e + '_dma')
        else:
            n = self.seq[e]
            ep = n // MAXV
            while len(self.csem[e]) <= ep:
                self.csem[e].append(self._newsem(f"c{e}{len(self.csem[e])}"))
            name = self.csem[e][ep]
            ins = fn(self.eng[e])
            ins.then_inc(self.semobj[name], 1)
            self.seq[e] = n + 1
            tok = (name, n - ep * MAXV + 1, e)
        self.snap[(tok[0], tok[1])] = dict(self.known[e])
        for r in reads:
            self.readers.setdefault(r, []).append(tok)
        for w in writes:
            self.last_w[w] = tok
            self.readers[w] = []
        return tok

    def fence(self, e):
        n = self.seq[e]
        if n > 0:
            ep = (n - 1) // MAXV
            self._need(e, (self.csem[e][ep], n - ep * MAXV, e), False)

    def barrier(self):
        toks = []
        for e in self.eng:
            n = self.seq[e]
            if n > 0:
                ep = (n - 1) // MAXV
                toks.append((self.csem[e][ep], n - ep * MAXV, e))
        for q, pool in self.dma_pool.items():
            for name, c in zip(pool['sems'], pool['cnt']):
                if c > 0:
                    toks.append((name, c, q + '_dma'))
        for e in self.eng:
            for t in toks:
                self._need(e, t, False)
        self.last_w.clear()
        self.readers.clear()
        self.snap.clear()


class Stage:
    _n = 0

    def __init__(self, K, name):
        self.K = K
        Stage._n += 1
        self.name = f"{name}{Stage._n}"

    def __enter__(self):
        self.es = ExitStack()
        self.es.__enter__()
        return self

    def T(self, name, shape, dt=F32):
        return self.es.enter_context(self.K.nc.sbuf_tensor(f"{self.name}_{name}", shape, dt))

    def __exit__(self, *a):
        self.K.P.barrier()
        return self.es.__exit__(*a)


class Kern:
    def __init__(self, cfg):
        self.cfg = cfg

    def build(self):
        nc = bass.Bass("TRN2", target_bir_lowering=False)
        self.nc = nc
        dbg = self.cfg.get('debug', False)

        def din(name, shape, dt=F32):
            return nc.dram_tensor(name, list(shape), dt, kind="ExternalInput").ap()

        def dscr(name, shape, dt=F32):
            kind = "ExternalOutput" if (dbg and name in self.cfg.get('expose', ())) else "Internal"
            return nc.dram_tensor(name, list(shape), dt, kind=kind).ap()

        A = {}
        A['x'] = din('x', [S, D])
        A['c_l'] = din('c_l', [128, 8])
        A['ada_mix_w'] = din('ada_mix_w', [2, D, 3 * D])
        A['ada_ffn_w'] = din('ada_ffn_w', [2, D, 3 * D])
        A['ada_b'] = din('ada_b', [4, 3 * D])
        A['ln_g'] = din('ln_g', [4, D])
        A['ln_b'] = din('ln_b', [4, D])
        A['conv_in_w'] = din('conv_in_w', [D, 2 * D])
        A['conv_in_b_l'] = din('conv_in_b_l', [128, 16])
        A['conv_dw_w_l'] = din('conv_dw_w_l', [128, 8, 31])
        A['conv_vec_l'] = din('conv_vec_l', [128, 3, 8])
        A['conv_out_w'] = din('conv_out_w', [D, D])
        A['conv_out_b'] = din('conv_out_b', [1, D])
        A['attn_in_w'] = din('attn_in_w', [D, 3 * D + 16])
        A['attn_qkb_l'] = din('attn_qkb_l', [128, 16])
        A['attn_vb'] = din('attn_vb', [1, D])
        A['attn_fb'] = din('attn_fb', [16, 1])
        A['attn_out_w'] = din('attn_out_w', [D, D])
        A['attn_out_b'] = din('attn_out_b', [1, D])
        A['peer_query_w'] = din('peer_query_w', [2, D, 2 * D])
        A['peer_skT'] = din('peer_skT', [2, 2, 128, 128])
        A['peer_u'] = din('peer_u', [2, NEXP, D])
        A['peer_v'] = din('peer_v', [2, NEXP, D])
        A['out'] = nc.dram_tensor('out', [S, D], F32, kind="ExternalOutput").ap()
        A['X1'] = dscr('X1', [S, D])
        A['X2'] = dscr('X2', [S, D])
        A['X3'] = dscr('X3', [S, D])
        A['ST'] = dscr('ST', [8, 128, S])
        A['IDX'] = dscr('IDX', [S, 128], I32)
        A['SCR'] = dscr('SCR', [S, 2048])
        A['UVB'] = dscr('UVB', [2 * NEXP, 2 * D], BF16)
        A['H'] = dscr('H', [S, D])
        A['GATE'] = dscr('GATE', [S, 128])
        A['QA'] = dscr('QA', [16, 68, S])
        A['KA'] = dscr('KA', [16, 68, S])
        A['V'] = dscr('V', [S, D])
        A['AOT'] = dscr('AOT', [8, 128, S])
        self.A = A

        with ExitStack() as es:
            self.P = P = Prog(nc, es)
            G = lambda name, shape, dt=F32: es.enter_context(nc.sbuf_tensor(name, shape, dt))
            self.ps = [es.enter_context(nc.psum_tensor(f"ps{i}", [128, 512], F32)) for i in range(8)]
            self.ident = G('ident', [128, 128])
            self.ones = G('ones', [128, 128])
            self.SC = G('SC', [128, 8, 128])
            self.shift_bc = G('shift_bc', [128, D])
            self.scale_bc = G('scale_bc', [128, D])
            self.gate_bc = G('gate_bc', [128, D])
            self.g_bc = G('g_bc', [128, D])
            self.b_bc = G('b_bc', [128, D])
            self.bs = G('bs', [128, 2, 6])
            self.mv = G('mv', [128, 2])
            self.rs = G('rs', [128, 1])
            self.emit_globals()
            self._cast_done = False
            if self.cfg.get('peer_bf16', True) and any(st.startswith('peer') for st in self.cfg.get('stages', ['peer0'])):
                self.emit_cast_tables()
                self._cast_done = True
            order = self.cfg.get('stages', ['conv', 'peer0', 'attn', 'peer1'])
            cur = A['x']
            nxt = {'conv': A['X1'], 'peer0': A['X2'], 'attn': A['X3'], 'peer1': A['out']}
            for i, st in enumerate(order):
                dst = A['out'] if i == len(order) - 1 else nxt[st]
                if st == 'conv':
                    self.emit_adaln(0)
                    self.emit_conv1(cur)
                    self.emit_proj_out(cur, dst, A['conv_out_w'], A['conv_out_b'], src_fm=A['ST'])
                elif st == 'attn':
                    self.emit_adaln(2)
                    self.emit_attn1(cur)
                    self.emit_attn2()
                    self.emit_proj_out(cur, dst, A['attn_out_w'], A['attn_out_b'], src_fm=A['AOT'])
                else:
                    L = int(st[-1])
                    self.emit_adaln(1 + 2 * L)
                    if self.cfg.get('peer_bf16', True):
                        if not self._cast_done:
                            self.emit_cast_tables()
                            self._cast_done = True
                        self.emit_peer1a(cur, L)
                        self.emit_peer2g(cur, dst, L)
                    elif self.cfg.get('peer_fused', True):
                        self.emit_peer1a(cur, L)
                        self.emit_peer2f(cur, dst, L)
                    else:
                        self.emit_peer1(cur, L)
                        self.emit_peer2(cur, dst, L)
                cur = dst
            P.barrier()
            print(f"[kern] ops={P.nops} waits={P.nwaits} sems={P.nsem} seq={P.seq}")
        return nc

    def emit_globals(self):
        P, nc = self.P, self.nc
        ident, ones = self.ident, self.ones
        P.op('pool', lambda e: e.memset(ident[:], 1.0), writes=['ident'])
        P.op('pool', lambda e: e.affine_select(out=ident[:], in_=ident[:], pattern=[[-1, 128]],
                                               compare_op=ALU.is_equal, fill=0.0, base=0, channel_multiplier=1),
             reads=['ident'], writes=['ident'])
        P.op('pool', lambda e: e.memset(ones[:], 1.0), writes=['ones'])
        with Stage(self, 'gl') as st:
            ct = st.T('ct', [128, 8])
            P.op('sp', lambda e: e.dma_start(out=ct[:], in_=self.A['c_l']), writes=['ct'], dma=True)
            P.op('act', lambda e: e.activation(out=ct[:], in_=ct[:], func=AF.Silu), reads=['ct'], writes=['ct'])
            SC = self.SC
            P.op('dve', lambda e: e.tensor_copy(out=SC[:], in_=ct[:].unsqueeze(2).to_broadcast([128, 8, 128])),
                 reads=['ct'], writes=['SC'])

    def modulate(self, xt, ht, rx, rh):
        P = self.P
        P.op('dve', lambda e: e.tensor_tensor(out=ht, in0=xt, in1=self.scale_bc[:], op=ALU.mult),
             reads=[rx, 'scale_bc'], writes=[rh])
        P.op('pool', lambda e: e.tensor_tensor(out=ht, in0=ht, in1=self.shift_bc[:], op=ALU.add),
             reads=[rh, 'shift_bc'], writes=[rh])

    def transpose8(self, src, rsrc, dstT, rdst, col0, pb):
        P = self.P
        ps = self.ps
        for half in range(2):
            bank = ps[pb + half]
            rb = f'ps{pb + half}'
            for kk in range(4):
                k = half * 4 + kk
                P.op('pe', lambda e, k=k, kk=kk, bank=bank: e.transpose(out=bank[:, kk * 128:(kk + 1) * 128],
                                                                      in_=src[:, k * 128:(k + 1) * 128],
                                                                      identity=self.ident[:]),
                     reads=[rsrc, 'ident'], writes=[rb])
            dst = dstT[:, half * 4:half * 4 + 4, col0:col0 + 128]
            srcp = bank[:].rearrange("p (k n) -> p k n", k=4)
            if half == 0:
                P.op('act', lambda e, dst=dst, srcp=srcp: e.copy(out=dst, in_=srcp), reads=[rb], writes=[rdst])
            else:
                P.op('dve', lambda e, dst=dst, srcp=srcp: e.tensor_copy(out=dst, in_=srcp), reads=[rb], writes=[rdst])

    def layernorm_inplace(self, r, rr, gb='pool'):
        P = self.P
        bs, mv, rs = self.bs, self.mv, self.rs
        for c in range(2):
            P.op('dve', lambda e, c=c: e.bn_stats(out=bs[:, c, :], in_=r[:, c * 512:(c + 1) * 512]),
                 reads=[rr], writes=['bs'])
        P.op('dve', lambda e: e.bn_aggr(out=mv[:], in_=bs[:].rearrange("p a b -> p (a b)")), reads=['bs'], writes=['mv'])
        P.op('dve', lambda e: e.tensor_scalar(out=rs[:], in0=mv[:, 1:2], scalar1=EPS, scalar2=None, op0=ALU.add),
             reads=['mv'], writes=['rs'])
        P.op('act', lambda e: e.activation(out=rs[:], in_=rs[:], func=AF.Sqrt), reads=['rs'], writes=['rs'])
        P.op('dve', lambda e: e.reciprocal(out=rs[:], in_=rs[:]), reads=['rs'], writes=['rs'])
        P.op('dve', lambda e: e.tensor_scalar(out=r, in0=r, scalar1=mv[:, 0:1], scalar2=rs[:, 0:1],
                                              op0=ALU.subtract, op1=ALU.mult), reads=[rr, 'mv', 'rs'], writes=[rr])
        P.op(gb, lambda e: e.tensor_tensor(out=r, in0=r, in1=self.g_bc[:], op=ALU.mult), reads=[rr, 'g_bc'], writes=[rr])
        P.op(gb, lambda e: e.tensor_tensor(out=r, in0=r, in1=self.b_bc[:], op=ALU.add), reads=[rr, 'b_bc'], writes=[rr])

    def emit_adaln(self, sub):
        P, nc, A, ps = self.P, self.nc, self.A, self.ps
        L = sub // 2
        wsrc = (A['ada_mix_w'] if sub % 2 == 0 else A['ada_ffn_w'])[L]
        ones = self.ones
        with Stage(self, f'ada{sub}') as st:
            brow = st.T('brow', [1, 3 * D])
            lrow = st.T('lrow', [1, 2 * D])
            wch = [st.T(f'wch{i}', [128, 8, 512]) for i in range(2)]
            P.op('sp', lambda e: e.dma_start(out=brow[:], in_=A['ada_b'][sub:sub + 1, :]), writes=['brow'], dma=True)
            P.op('sp', lambda e: e.dma_start(out=lrow[:, 0:D], in_=A['ln_g'][sub:sub + 1, :]), writes=['lrow'], dma=True)
            P.op('sp', lambda e: e.dma_start(out=lrow[:, D:2 * D], in_=A['ln_b'][sub:sub + 1, :]), writes=['lrow'], dma=True)
            dsts = [self.shift_bc, self.shift_bc, self.scale_bc, self.scale_bc, self.gate_bc, self.gate_bc]
            names = ['shift_bc', 'shift_bc', 'scale_bc', 'scale_bc', 'gate_bc', 'gate_bc']
            for n6 in range(6):
                wb = wch[n6 % 2]
                rw = f'wch{n6 % 2}'
                P.op('sp', lambda e, wb=wb, n6=n6: e.dma_start(
                    out=wb[:], in_=wsrc[:, n6 * 512:(n6 + 1) * 512].rearrange("(k p) n -> p k n", p=128)),
                    writes=[rw], dma=True)
                bank = ps[n6 % 2]
                rb = f'ps{n6 % 2}'
                for k in range(8):
                    P.op('pe', lambda e, k=k, wb=wb, bank=bank: e.matmul(out=bank[:], lhsT=self.SC[:, k, :], rhs=wb[:, k, :],
                                                                        start=(k == 0), stop=False),
                         reads=['SC', rw], writes=[rb])
                P.op('pe', lambda e, bank=bank, n6=n6: e.matmul(out=bank[:], lhsT=ones[0:1, :], rhs=brow[0:1, n6 * 512:(n6 + 1) * 512],
                                                                start=False, stop=True),
                     reads=['ones', 'brow'], writes=[rb])
                dst = dsts[n6][:, (n6 % 2) * 512:(n6 % 2 + 1) * 512]
                if n6 in (2, 3):
                    P.op('dve', lambda e, dst=dst, bank=bank: e.tensor_scalar(out=dst, in0=bank[:], scalar1=1.0, scalar2=None, op0=ALU.add),
                         reads=[rb], writes=[names[n6]])
                else:
                    P.op('dve', lambda e, dst=dst, bank=bank: e.tensor_copy(out=dst, in_=bank[:]), reads=[rb], writes=[names[n6]])
            for j in range(4):
                bank = ps[2 + j % 2]
                rb = f'ps{2 + j % 2}'
                P.op('pe', lambda e, bank=bank, j=j: e.matmul(out=bank[:], lhsT=ones[0:1, :], rhs=lrow[0:1, j * 512:(j + 1) * 512],
                                                              start=True, stop=True), reads=['ones', 'lrow'], writes=[rb])
                dstt = self.g_bc if j < 2 else self.b_bc
                dst = dstt[:, (j % 2) * 512:(j % 2 + 1) * 512]
                P.op('act', lambda e, dst=dst, bank=bank: e.copy(out=dst, in_=bank[:]), reads=[rb],
                     writes=['g_bc' if j < 2 else 'b_bc'])

    def emit_conv1(self, xin):
        P, nc, A, ps = self.P, self.nc, self.A, self.ps
        ones = self.ones
        with Stage(self, 'c1') as st:
            win = st.T('win', [128, 8, 2048])
            cib = st.T('cib', [128, 16])
            dw = st.T('dw', [128, 8, 31])
            cv = st.T('cv', [128, 3, 8])
            xts = [st.T(f'xt{i}', [128, D]) for i in range(2)]
            ht = st.T('ht', [128, D])
            hT = st.T('hT', [128, 8, 512])
            acc = st.T('acc', [128, 8, 512])
            aexts = [st.T(f'aext{i}', [128, 8, 542]) for i in range(2)]
            sig = [st.T(f'sig{i}', [128, 512]) for i in range(2)]
            sq = [st.T(f'sq{i}', [128, 512]) for i in range(2)]
            meant = st.T('meant', [128, 512])
            rstd = st.T('rstd', [128, 512])
            tmp = st.T('tmp', [128, 512])
            for q in range(4):
                P.op('sp', lambda e, q=q: e.dma_start(out=win[:, :, q * 512:(q + 1) * 512],
                                                      in_=A['conv_in_w'][:, q * 512:(q + 1) * 512].rearrange("(k p) n -> p k n", p=128)),
                     writes=[('win', q)], dma=True)
            P.op('sp', lambda e: e.dma_start(out=cib[:], in_=A['conv_in_b_l']), writes=['cib'], dma=True)
            P.op('sp', lambda e: e.dma_start(out=dw[:], in_=A['conv_dw_w_l']), writes=['dw'], dma=True)
            P.op('sp', lambda e: e.dma_start(out=cv[:], in_=A['conv_vec_l']), writes=['cv'], dma=True)
            for cc in range(8):
                P.op('pool', lambda e, cc=cc: e.memset(aexts[0][:, cc, 0:30], 0.0), writes=[('aext0', cc)])
            accs_ = [acc, st.T('acc2', [128, 8, 512])]

            def load_T(jb):
                for tl in range(4):
                    ti = jb * 4 + tl
                    xt = xts[ti % 2]
                    rx = f'xt{ti % 2}'
                    P.op('sp', lambda e, xt=xt, ti=ti: e.dma_start(out=xt[:], in_=xin[ti * 128:(ti + 1) * 128, :]),
                         writes=[rx], dma=True)
                    self.modulate(xt[:], ht[:], rx, 'ht')
                    self.transpose8(ht, 'ht', hT, 'hT', tl * 128, 0)

            def glu_chunk(jb, cc):
                aext = aexts[jb % 2]
                AX_ = f'aext{jb % 2}'
                pa, pb = ps[2 + (cc % 2) * 2], ps[3 + (cc % 2) * 2]
                ra, rb = f'ps{2 + (cc % 2) * 2}', f'ps{3 + (cc % 2) * 2}'
                for k in range(8):
                    P.op('pe', lambda e, k=k: e.matmul(out=pa[:], lhsT=win[:, k, cc * 128:(cc + 1) * 128], rhs=hT[:, k, :],
                                                       start=(k == 0), stop=(k == 7)),
                         reads=[('win', cc // 4), 'hT'], writes=[ra])
                for k in range(8):
                    P.op('pe', lambda e, k=k: e.matmul(out=pb[:], lhsT=win[:, k, D + cc * 128:D + (cc + 1) * 128], rhs=hT[:, k, :],
                                                       start=(k == 0), stop=(k == 7)),
                         reads=[('win', 2 + cc // 4), 'hT'], writes=[rb])
                sg = sig[cc % 2]
                rsg = f'sig{cc % 2}'
                P.op('act', lambda e: e.activation(out=sg[:], in_=pb[:], func=AF.Sigmoid, bias=cib[:, 8 + cc:9 + cc], scale=1.0),
                     reads=[rb, 'cib'], writes=[rsg])
                P.op('dve', lambda e: e.scalar_tensor_tensor(out=aext[:, cc, 30:542], in0=pa[:], scalar=cib[:, cc:cc + 1], in1=sg[:],
                                                             op0=ALU.add, op1=ALU.mult),
                     reads=[ra, rsg, 'cib'], writes=[(AX_, cc)])

            def conv_chunk(jb, cc):
                aext, anext = aexts[jb % 2], aexts[(jb + 1) % 2]
                AX_, AN_ = f'aext{jb % 2}', f'aext{(jb + 1) % 2}'
                ac = accs_[jb % 2]
                RA = f'acc{jb % 2}'
                P.op('dve', lambda e: e.tensor_scalar(out=ac[:, cc, :], in0=aext[:, cc, 0:512], scalar1=dw[:, cc, 0:1], scalar2=cv[:, 0, cc:cc + 1],
                                                      op0=ALU.mult, op1=ALU.add),
                     reads=[(AX_, cc), 'dw', 'cv'], writes=[(RA, cc)])
                for w in range(1, 31):
                    P.op('dve', lambda e, w=w: e.scalar_tensor_tensor(out=ac[:, cc, :], in0=aext[:, cc, w:w + 512], scalar=dw[:, cc, w:w + 1],
                                                                      in1=ac[:, cc, :], op0=ALU.mult, op1=ALU.add),
                         reads=[(AX_, cc), 'dw', (RA, cc)], writes=[(RA, cc)])
                P.op('act', lambda e: e.copy(out=anext[:, cc, 0:30], in_=aext[:, cc, 512:542]),
                     reads=[(AX_, cc)], writes=[(AN_, cc)])

            def stats_pe(jb):
                ac, RA = accs_[jb % 2], f'acc{jb % 2}'
                for cc in range(8):
                    s2 = sq[cc % 2]
                    rs2 = f'sq{cc % 2}'
                    P.op('act', lambda e, cc=cc, s2=s2: e.activation(out=s2[:], in_=ac[:, cc, :], func=AF.Square),
                         reads=[(RA, cc)], writes=[rs2])
                    P.op('pe', lambda e, cc=cc: e.matmul(out=ps[6][:], lhsT=ones[:], rhs=ac[:, cc, :], start=(cc == 0), stop=(cc == 7)),
                         reads=['ones', (RA, cc)], writes=['ps6'])
                    P.op('pe', lambda e, cc=cc, s2=s2: e.matmul(out=ps[7][:], lhsT=ones[:], rhs=s2[:], start=(cc == 0), stop=(cc == 7)),
                         reads=['ones', rs2], writes=['ps7'])
                P.op('act', lambda e: e.activation(out=meant[:], in_=ps[6][:], func=AF.Copy, scale=1.0 / D), reads=['ps6'], writes=['meant'])

            def stats_dve(jb):
                P.op('dve', lambda e: e.tensor_tensor(out=tmp[:], in0=meant[:], in1=meant[:], op=ALU.mult), reads=['meant'], writes=['tmp'])
                P.op('dve', lambda e: e.scalar_tensor_tensor(out=rstd[:], in0=ps[7][:], scalar=1.0 / D, in1=tmp[:], op0=ALU.mult, op1=ALU.subtract),
                     reads=['ps7', 'tmp'], writes=['rstd'])
                P.op('dve', lambda e: e.tensor_scalar(out=rstd[:], in0=rstd[:], scalar1=EPS, scalar2=None, op0=ALU.add), reads=['rstd'], writes=['rstd'])
                P.op('act', lambda e: e.activation(out=rstd[:], in_=rstd[:], func=AF.Sqrt), reads=['rstd'], writes=['rstd'])
                P.op('dve', lambda e: e.reciprocal(out=rstd[:], in_=rstd[:]), reads=['rstd'], writes=['rstd'])

            def norm_chunk(jb, cc):
                ac, RA = accs_[jb % 2], f'acc{jb % 2}'
                P.op('dve', lambda e: e.tensor_tensor(out=ac[:, cc, :], in0=ac[:, cc, :], in1=meant[:], op=ALU.subtract),
                     reads=[(RA, cc), 'meant'], writes=[(RA, cc)])
                P.op('pool', lambda e: e.tensor_tensor(out=ac[:, cc, :], in0=ac[:, cc, :], in1=rstd[:], op=ALU.mult),
                     reads=[(RA, cc), 'rstd'], writes=[(RA, cc)])
                P.op('act', lambda e: e.activation(out=ac[:, cc, :], in_=ac[:, cc, :], func=AF.Silu,
                                                   bias=cv[:, 2, cc:cc + 1], scale=cv[:, 1, cc:cc + 1]),
                     reads=[(RA, cc), 'cv'], writes=[(RA, cc)])

            def store_block(jb):
                ac, RA = accs_[jb % 2], f'acc{jb % 2}'
                P.op('sp', lambda e: e.dma_start(out=A['ST'][:, :, jb * 512:(jb + 1) * 512].rearrange("c p t -> p c t"), in_=ac[:]),
                     reads=[(RA, cc) for cc in range(8)], writes=[('ST', jb)], dma=True)

            load_T(0)
            for cc in range(8):
                glu_chunk(0, cc)
            for jb in range(8):
                if jb + 1 < 8:
                    load_T(jb + 1)
                for cc in range(8):
                    if jb + 1 < 8:
                        glu_chunk(jb + 1, cc)
                    conv_chunk(jb, cc)
                stats_pe(jb)
                stats_dve(jb)
                for cc in range(8):
                    norm_chunk(jb, cc)
                store_block(jb)

    def emit_proj_out(self, xin, dst, w_ap, b_ap, src_fm=None, src_tm=None):
        P, nc, A, ps = self.P, self.nc, self.A, self.ps
        ones = self.ones
        with Stage(self, 'po') as st:
            wo = st.T('wo', [128, 8, D], F32R)
            wstg = st.T('wstg', [128, 8, 512])
            bo = st.T('bo', [1, D])
            xts = [st.T(f'xt{i}', [128, D]) for i in range(4)]
            rts = [st.T(f'rt{i}', [128, D]) for i in range(4)]
            if src_fm is not None:
                sT = [st.T(f'sT{i}', [128, 8, 512], F32R) for i in range(2)]
                sstg = st.T('sstg', [128, 8, 512])
            else:
                ao = [st.T(f'ao{i}', [128, D]) for i in range(2)]
                aT = [st.T(f'aT{i}', [128, 8, 128]) for i in range(2)]
            for q in range(2):
                P.op('sp', lambda e, q=q: e.dma_start(out=wstg[:], in_=w_ap[:, q * 512:(q + 1) * 512].rearrange("(k p) n -> p k n", p=128)),
                     writes=['wstg'], dma=True)
                P.op('pool', lambda e, q=q: e.tensor_copy(out=wo[:, :, q * 512:(q + 1) * 512], in_=wstg[:]), reads=['wstg'], writes=[('wo', q)])
            P.op('sp', lambda e: e.dma_start(out=bo[:], in_=b_ap), writes=['bo'], dma=True)
            bo_bc = st.T('bo_bc', [128, D])
            for half in range(2):
                P.op('pe', lambda e, half=half: e.matmul(out=ps[half][:], lhsT=ones[0:1, :], rhs=bo[0:1, half * 512:(half + 1) * 512], start=True, stop=True),
                     reads=['ones', 'bo'], writes=[f'ps{half}'])
                P.op('act', lambda e, half=half: e.copy(out=bo_bc[:, half * 512:(half + 1) * 512], in_=ps[half][:]), reads=[f'ps{half}'], writes=['bo_bc'])
            for ti in range(NT):
                xt = xts[ti % 4]
                rx = f'xt{ti % 4}'
                rt = rts[ti % 4]
                rr = f'rt{ti % 4}'
                P.op('sp', lambda e, xt=xt, ti=ti: e.dma_start(out=xt[:], in_=xin[ti * 128:(ti + 1) * 128, :]), writes=[rx], dma=True)
                if src_fm is not None:
                    jb, tl = ti // 4, ti % 4
                    sb = sT[jb % 2]
                    rsb = f'sT{jb % 2}'
                    if tl == 0:
                        P.op('sp', lambda e, jb=jb: e.dma_start(out=sstg[:], in_=src_fm[:, :, jb * 512:(jb + 1) * 512].rearrange("c p t -> p c t")),
                             reads=[('ST', jb)], writes=['sstg'], dma=True)
                        P.op('pool', lambda e, sb=sb: e.tensor_copy(out=sb[:], in_=sstg[:]), reads=['sstg'], writes=[rsb])
                    lhs = lambda k, sb=sb, tl=tl: sb[:, k, tl * 128:(tl + 1) * 128]
                    rl = rsb
                else:
                    a = ao[ti % 2]
                    ra = f'ao{ti % 2}'
                    at = aT[ti % 2]
                    rat = f'aT{ti % 2}'
                    P.op('sp', lambda e, a=a, ti=ti: e.dma_start(out=a[:], in_=src_tm[ti * 128:(ti + 1) * 128, :]), writes=[ra], dma=True)
                    self.transpose8(a, ra, at, rat, 0, 4)
                    lhs = lambda k, at=at: at[:, k, :]
                    rl = rat
                pb = (ti % 4) * 2
                for half in range(2):
                    bank = ps[pb + half]
                    rb = f'ps{pb + half}'
                    for k in range(8):
                        P.op('pe', lambda e, k=k, bank=bank, half=half, lhs=lhs: e.matmul(out=bank[:], lhsT=lhs(k), rhs=wo[:, k, half * 512:(half + 1) * 512],
                                                                                     start=(k == 0), stop=(k == 7)),
                             reads=[rl, ('wo', half)], writes=[rb])
                    P.op('dve', lambda e, bank=bank, half=half, rt=rt: e.tensor_tensor(out=rt[:, half * 512:(half + 1) * 512], in0=bank[:],
                                                                                  in1=bo_bc[:, half * 512:(half + 1) * 512], op=ALU.add),
                         reads=[rb, 'bo_bc'], writes=[rr])
                    P.op('pool', lambda e, half=half, rt=rt: e.tensor_tensor(out=rt[:, half * 512:(half + 1) * 512], in0=rt[:, half * 512:(half + 1) * 512],
                                                                           in1=self.gate_bc[:, half * 512:(half + 1) * 512], op=ALU.mult),
                         reads=[rr, 'gate_bc'], writes=[rr])
                P.op('dve', lambda e, rt=rt, xt=xt: e.scalar_tensor_tensor(out=rt[:], in0=xt[:], scalar=ALPHA, in1=rt[:], op0=ALU.mult, op1=ALU.add),
                     reads=[rx, rr], writes=[rr])
                self.layernorm_inplace(rt[:], rr)
                P.op('sp', lambda e, rt=rt, ti=ti: e.dma_start(out=dst[ti * 128:(ti + 1) * 128, :], in_=rt[:]), reads=[rr], dma=True)

    def emit_peer1(self, xin, L):
        P, nc, A, ps = self.P, self.nc, self.A, self.ps
        with Stage(self, f'p1{L}') as st:
            wq = st.T('wq', [128, 8, 2048])
            skT = st.T('skT', [128, 2, 128])
            xts = [st.T(f'xt{i}', [128, D]) for i in range(2)]
            ht = st.T('ht', [128, D])
            hT = st.T('hT', [128, 8, 256])
            qT = st.T('qT', [128, 16, 256])
            sc = st.T('sc', [128, 16, 128])
            m = st.T('m', [128, 16, 16])
            ix = st.T('ix', [128, 16, 16], U32)
            ixf = st.T('ixf', [128, 16, 16])
            wk = st.T('wk', [128, 16, 128])
            cand = st.T('cand', [128, 8, 256])
            candi = st.T('candi', [128, 8, 256])
            wk2 = st.T('wk2', [128, 8, 256])
            junk = [st.T(f'junk{i}', [128, 256]) for i in range(2)]
            ts = st.T('ts', [128, 8, 16])
            ef = st.T('ef', [128, 128])
            ei = [st.T(f'ei{i}', [128, 128], I32) for i in range(2)]
            gt = [st.T(f'gt{i}', [128, 8, 16]) for i in range(2)]
            gsum = st.T('gsum', [128, 8])
            for q in range(4):
                P.op('sp', lambda e, q=q: e.dma_start(out=wq[:, :, q * 512:(q + 1) * 512],
                                                      in_=A['peer_query_w'][L][:, q * 512:(q + 1) * 512].rearrange("(k p) n -> p k n", p=128)),
                     writes=[('wq', q)], dma=True)
            P.op('sp', lambda e: e.dma_start(out=skT[:], in_=A['peer_skT'][L].rearrange("h d k -> d h k")), writes=['skT'], dma=True)
            ti = 0
            for jb in range(S // 256):
                for tl in range(2):
                    xt = xts[ti % 2]
                    rx = f'xt{ti % 2}'
                    P.op('sp', lambda e, xt=xt, ti=ti: e.dma_start(out=xt[:], in_=xin[ti * 128:(ti + 1) * 128, :]), writes=[rx], dma=True)
                    self.modulate(xt[:], ht[:], rx, 'ht')
                    self.transpose8(ht, 'ht', hT, 'hT', tl * 128, 0)
                    ti += 1
                for c in range(16):
                    bank = ps[2 + c % 2]
                    rb = f'ps{2 + c % 2}'
                    for k in range(8):
                        P.op('pe', lambda e, k=k, c=c, bank=bank: e.matmul(out=bank[:, 0:256], lhsT=wq[:, k, c * 128:(c + 1) * 128], rhs=hT[:, k, :],
                                                                          start=(k == 0), stop=(k == 7)),
                             reads=[('wq', c // 4), 'hT'], writes=[rb])
                    if c % 2 == 0:
                        P.op('act', lambda e, c=c, bank=bank: e.copy(out=qT[:, c, :], in_=bank[:, 0:256]), reads=[rb], writes=[('qT', c)])
                    else:
                        P.op('dve', lambda e, c=c, bank=bank: e.tensor_copy(out=qT[:, c, :], in_=bank[:, 0:256]), reads=[rb], writes=[('qT', c)])
                for tl in range(2):
                    tix = jb * 2 + tl
                    for c in range(16):
                        bank = ps[4 + c // 4]
                        rb = f'ps{4 + c // 4}'
                        P.op('pe', lambda e, c=c, bank=bank, tl=tl: e.matmul(out=bank[:, (c % 4) * 128:(c % 4 + 1) * 128],
                                                                            lhsT=qT[:, c, tl * 128:(tl + 1) * 128], rhs=skT[:, c % 2, :],
                                                                            start=True, stop=True),
                             reads=[('qT', c), 'skT'], writes=[rb])
                    for g4 in range(4):
                        P.op('act', lambda e, g4=g4: e.copy(out=sc[:, g4 * 4:(g4 + 1) * 4, :], in_=ps[4 + g4][:].rearrange("p (a k) -> p a k", a=4)),
                             reads=[f'ps{4 + g4}'], writes=['sc'])
                    for c in range(16):
                        P.op('dve', lambda e, c=c: e.max(out=m[:, c, 0:8], in_=sc[:, c, :]), reads=['sc'], writes=[('m0', c)])
                    P.fence('dve')
                    for c in range(16):
                        P.op('dve', lambda e, c=c: e.max_index(out=ix[:, c, 0:8], in_max=m[:, c, 0:8], in_values=sc[:, c, :]),
                             reads=['sc', ('m0', c)], writes=[('ix0', c)])
                        P.op('dve', lambda e, c=c: e.match_replace(out=wk[:, c, :], in_to_replace=m[:, c, 0:8], in_values=sc[:, c, :], imm_value=-1e30),
                             reads=['sc', ('m0', c)], writes=[('wk', c)])
                    P.fence('dve')
                    for c in range(16):
                        P.op('dve', lambda e, c=c: e.max(out=m[:, c, 8:16], in_=wk[:, c, :]), reads=[('wk', c)], writes=[('m1', c)])
                    P.fence('dve')
                    for c in range(16):
                        P.op('dve', lambda e, c=c: e.max_index(out=ix[:, c, 8:16], in_max=m[:, c, 8:16], in_values=wk[:, c, :]),
                             reads=[('wk', c), ('m1', c)], writes=[('ix1', c)])
                    P.fence('dve')
                    mres = [('m0', c) for c in range(16)] + [('m1', c) for c in range(16)]
                    ixres = [('ix0', c) for c in range(16)] + [('ix1', c) for c in range(16)]
                    P.op('dve', lambda e: e.tensor_copy(out=ixf[:], in_=ix[:]), reads=ixres, writes=['ixf'])
                    m4 = m[:].rearrange("p (h two) k -> p h two k", two=2)
                    i4 = ixf[:].rearrange("p (h two) k -> p h two k", two=2)
                    c4 = cand[:].rearrange("p h (a b) -> p h a b", a=16)
                    ci4 = candi[:].rearrange("p h (a b) -> p h a b", a=16)
                    P.op('dve', lambda e: e.tensor_tensor(out=c4, in0=m4[:, :, 0, :].unsqueeze(3).to_broadcast([128, 8, 16, 16]),
                                                          in1=m4[:, :, 1, :].unsqueeze(2).to_broadcast([128, 8, 16, 16]), op=ALU.add),
                         reads=mres, writes=['cand'])
                    P.op('dve', lambda e: e.tensor_scalar(out=i4[:, :, 0, :], in0=i4[:, :, 0, :], scalar1=128.0, scalar2=None, op0=ALU.mult),
                         reads=['ixf'], writes=['ixf'])
                    P.op('dve', lambda e: e.tensor_tensor(out=ci4, in0=i4[:, :, 0, :].unsqueeze(3).to_broadcast([128, 8, 16, 16]),
                                                          in1=i4[:, :, 1, :].unsqueeze(2).to_broadcast([128, 8, 16, 16]), op=ALU.add),
                         reads=['ixf'], writes=['candi'])
                    for h in range(8):
                        P.op('dve', lambda e, h=h: e.max(out=ts[:, h, 0:8], in_=cand[:, h, :]), reads=['cand'], writes=[('ts0', h)])
                    P.fence('dve')
                    for h in range(8):
                        P.op('dve', lambda e, h=h: e.match_replace(out=wk2[:, h, :], in_to_replace=ts[:, h, 0:8], in_values=cand[:, h, :], imm_value=-1e30),
                             reads=['cand', ('ts0', h)], writes=[('wk2', h)])
                    P.fence('dve')
                    for h in range(8):
                        P.op('dve', lambda e, h=h: e.max(out=ts[:, h, 8:16], in_=wk2[:, h, :]), reads=[('wk2', h)], writes=[('ts1', h)])
                    P.fence('dve')
                    tsres = [('ts0', h) for h in range(8)] + [('ts1', h) for h in range(8)]
                    for h in range(8):
                        for k in range(16):
                            P.op('dve', lambda e, h=h, k=k: e.scalar_tensor_tensor(out=junk[(h * 16 + k) % 2][:], in0=cand[:, h, :], scalar=ts[:, h, k:k + 1], in1=candi[:, h, :],
                                                                                   op0=ALU.is_equal, op1=ALU.mult, accum_out=ef[:, h * 16 + k:h * 16 + k + 1]),
                                 reads=['cand', 'candi', ('ts0', h), ('ts1', h)], writes=[('ef', h * 16 + k)])
                    P.fence('dve')
                    eib = ei[tix % 2]
                    rei = f'ei{tix % 2}'
                    gtb = gt[tix % 2]
                    rgt = f'gt{tix % 2}'
                    P.op('dve', lambda e: e.tensor_scalar(out=ef[:], in0=ef[:], scalar1=float(NEXP - 1), scalar2=float(L * NEXP), op0=ALU.min, op1=ALU.add),
                         reads=[('ef', q) for q in range(128)], writes=['ef'])
                    P.op('dve', lambda e, eib=eib: e.tensor_copy(out=eib[:], in_=ef[:]), reads=['ef'], writes=[rei])
                    P.op('dve', lambda e, gtb=gtb: e.tensor_tensor(out=gtb[:], in0=ts[:], in1=ts[:, :, 0:1].to_broadcast([128, 8, 16]), op=ALU.subtract),
                         reads=tsres, writes=[rgt])
                    P.op('act', lambda e, gtb=gtb: e.activation(out=gtb[:], in_=gtb[:], func=AF.Exp), reads=[rgt], writes=[rgt])
                    P.op('dve', lambda e, gtb=gtb: e.tensor_reduce(out=gsum[:], in_=gtb[:], axis=AX.X, op=ALU.add), reads=[rgt], writes=['gsum'])
                    P.op('dve', lambda e: e.reciprocal(out=gsum[:], in_=gsum[:]), reads=['gsum'], writes=['gsum'])
                    P.op('dve', lambda e, gtb=gtb: e.tensor_tensor(out=gtb[:], in0=gtb[:], in1=gsum[:].unsqueeze(2).to_broadcast([128, 8, 16]), op=ALU.mult),
                         reads=[rgt, 'gsum'], writes=[rgt])
                    P.op('sp', lambda e, eib=eib, tix=tix: e.dma_start(out=A['IDX'][tix * 128:(tix + 1) * 128, :], in_=eib[:]),
                         reads=[rei], writes=[('IDX', tix)], dma=True)
                    P.op('sp', lambda e, gtb=gtb, tix=tix: e.dma_start(out=A['GATE'][tix * 128:(tix + 1) * 128, :], in_=gtb[:].rearrange("p h k -> p (h k)")),
                         reads=[rgt], writes=[('GATE', tix)], dma=True)

    def emit_peer2(self, xin, dst, L):
        P, nc, A, ps = self.P, self.nc, self.A, self.ps
        NB = self.cfg.get('nb', 14)
        U = A['peer_u'].rearrange("l e d -> (l e) d")
        V = A['peer_v'].rearrange("l e d -> (l e) d")
        with Stage(self, f'p2{L}') as st:
            xts = [st.T(f'xt{i}', [128, D]) for i in range(2)]
            hts = [st.T(f'ht{i}', [128, D]) for i in range(2)]
            eis = [st.T(f'ei{i}', [128, 128], I32) for i in range(2)]
            gts = [st.T(f'gt{i}', [128, 128]) for i in range(2)]
            ub = [st.T(f'ub{i}', [128, D]) for i in range(NB)]
            vb = [st.T(f'vb{i}', [128, D]) for i in range(NB)]
            junk = st.T('junk', [128, D])
            apre = st.T('apre', [128, 128])
            coef = st.T('coef', [128, 128])
            accs = [st.T(f'acc{i}', [128, D]) for i in range(2)]
            nu = nv = 0
            for ti in range(NT):
                b = ti % 2
                xt, ht, eib, gtb, acc = xts[b], hts[b], eis[b], gts[b], accs[b]
                rx, rh, rei, rgt, racc = f'xt{b}', f'ht{b}', f'ei{b}', f'gt{b}', f'acc{b}'
                P.op('sp', lambda e, xt=xt, ti=ti: e.dma_start(out=xt[:], in_=xin[ti * 128:(ti + 1) * 128, :]), writes=[rx], dma=True)
                P.op('sp', lambda e, eib=eib, ti=ti: e.dma_start(out=eib[:], in_=A['IDX'][ti * 128:(ti + 1) * 128, :]),
                     reads=[('IDX', ti)], writes=[rei], dma=True)
                P.op('sp', lambda e, gtb=gtb, ti=ti: e.dma_start(out=gtb[:], in_=A['GATE'][ti * 128:(ti + 1) * 128, :]),
                     reads=[('GATE', ti)], writes=[rgt], dma=True)
                self.modulate(xt[:], ht[:], rx, rh)
                for k in range(128):
                    s = nu % NB
                    nu += 1
                    P.op('pool', lambda e, s=s, k=k, eib=eib: e.indirect_dma_start(
                        out=ub[s][:], out_offset=None, in_=U,
                        in_offset=bass.IndirectOffsetOnAxis(ap=eib[:, k:k + 1], axis=0)),
                        reads=[rei], writes=[('ub', s)], dma=True)
                    P.op('dve', lambda e, s=s, k=k, ht=ht: e.scalar_tensor_tensor(out=junk[:], in0=ub[s][:], scalar=1.0, in1=ht[:], op0=ALU.mult, op1=ALU.mult,
                                                                               accum_out=apre[:, k:k + 1]),
                         reads=[('ub', s), rh], writes=['junk', 'apre'])
                P.op('act', lambda e: e.activation(out=coef[:], in_=apre[:], func=AF.Gelu), reads=['apre'], writes=['coef'])
                P.op('dve', lambda e, gtb=gtb: e.tensor_tensor(out=coef[:], in0=coef[:], in1=gtb[:], op=ALU.mult), reads=['coef', rgt], writes=['coef'])
                for k in range(128):
                    s = nv % NB
                    nv += 1
                    P.op('pool', lambda e, s=s, k=k, eib=eib: e.indirect_dma_start(
                        out=vb[s][:], out_offset=None, in_=V,
                        in_offset=bass.IndirectOffsetOnAxis(ap=eib[:, k:k + 1], axis=0)),
                        reads=[rei], writes=[('vb', s)], dma=True)
                    if k == 0:
                        P.op('dve', lambda e, s=s, acc=acc: e.tensor_scalar(out=acc[:], in0=vb[s][:], scalar1=coef[:, 0:1], scalar2=None, op0=ALU.mult),
                             reads=[('vb', s), 'coef'], writes=[racc])
                    else:
                        P.op('dve', lambda e, s=s, k=k, acc=acc: e.scalar_tensor_tensor(out=acc[:], in0=vb[s][:], scalar=coef[:, k:k + 1], in1=acc[:],
                                                                                      op0=ALU.mult, op1=ALU.add),
                             reads=[('vb', s), 'coef', racc], writes=[racc])
                P.op('pool', lambda e, acc=acc: e.tensor_tensor(out=acc[:], in0=acc[:], in1=self.gate_bc[:], op=ALU.mult), reads=[racc, 'gate_bc'], writes=[racc])
                P.op('dve', lambda e, acc=acc, xt=xt: e.scalar_tensor_tensor(out=acc[:], in0=xt[:], scalar=ALPHA, in1=acc[:], op0=ALU.mult, op1=ALU.add),
                     reads=[rx, racc], writes=[racc])
                self.layernorm_inplace(acc[:], racc)
                P.op('sp', lambda e, acc=acc, ti=ti: e.dma_start(out=dst[ti * 128:(ti + 1) * 128, :], in_=acc[:]), reads=[racc], dma=True)

    def emit_peer1a(self, xin, L):
        P, nc, A, ps = self.P, self.nc, self.A, self.ps
        with Stage(self, f'pa{L}') as st:
            wq = st.T('wq', [128, 8, 2048])
            skT = st.T('skT', [128, 2, 128])
            xts = [st.T(f'xt{i}', [128, D]) for i in range(2)]
            hts = [st.T(f'ht{i}', [128, D]) for i in range(2)]
            hT = st.T('hT', [128, 8, 256])
            qT = st.T('qT', [128, 16, 256])
            sco = [st.T(f'sco{i}', [128, 2048]) for i in range(2)]
            for q in range(4):
                P.op('sp', lambda e, q=q: e.dma_start(out=wq[:, :, q * 512:(q + 1) * 512],
                                                      in_=A['peer_query_w'][L][:, q * 512:(q + 1) * 512].rearrange("(k p) n -> p k n", p=128)),
                     writes=[('wq', q)], dma=True)
            P.op('sp', lambda e: e.dma_start(out=skT[:], in_=A['peer_skT'][L].rearrange("h d k -> d h k")), writes=['skT'], dma=True)
            ti = 0
            for jb in range(S // 256):
                for tl in range(2):
                    xt, ht = xts[ti % 2], hts[ti % 2]
                    rx, rh = f'xt{ti % 2}', f'ht{ti % 2}'
                    P.op('sp', lambda e, xt=xt, ti=ti: e.dma_start(out=xt[:], in_=xin[ti * 128:(ti + 1) * 128, :]), writes=[rx], dma=True)
                    self.modulate(xt[:], ht[:], rx, rh)
                    P.op('sp', lambda e, ht=ht, ti=ti: e.dma_start(out=A['H'][ti * 128:(ti + 1) * 128, :], in_=ht[:]), reads=[rh], writes=[('H', ti)], dma=True)
                    self.transpose8(ht, rh, hT, 'hT', tl * 128, 0)
                    ti += 1
                for c in range(16):
                    bank = ps[2 + c % 2]
                    rb = f'ps{2 + c % 2}'
                    for k in range(8):
                        P.op('pe', lambda e, k=k, c=c, bank=bank: e.matmul(out=bank[:, 0:256], lhsT=wq[:, k, c * 128:(c + 1) * 128], rhs=hT[:, k, :],
                                                                          start=(k == 0), stop=(k == 7)),
                             reads=[('wq', c // 4), 'hT'], writes=[rb])
                    if c % 2 == 0:
                        P.op('act', lambda e, c=c, bank=bank: e.copy(out=qT[:, c, :], in_=bank[:, 0:256]), reads=[rb], writes=[('qT', c)])
                    else:
                        P.op('dve', lambda e, c=c, bank=bank: e.tensor_copy(out=qT[:, c, :], in_=bank[:, 0:256]), reads=[rb], writes=[('qT', c)])
                for tl in range(2):
                    tix = jb * 2 + tl
                    so = sco[tix % 2]
                    rso = f'sco{tix % 2}'
                    for c in range(16):
                        bank = ps[4 + c // 4]
                        rb = f'ps{4 + c // 4}'
                        P.op('pe', lambda e, c=c, bank=bank, tl=tl: e.matmul(out=bank[:, (c % 4) * 128:(c % 4 + 1) * 128],
                                                                            lhsT=qT[:, c, tl * 128:(tl + 1) * 128], rhs=skT[:, c % 2, :],
                                                                            start=True, stop=True),
                             reads=[('qT', c), 'skT'], writes=[rb])
                    for g4 in range(4):
                        eng = ('act', 'dve')[g4 % 2]
                        if eng == 'act':
                            P.op('act', lambda e, g4=g4, so=so: e.copy(out=so[:, g4 * 512:(g4 + 1) * 512], in_=ps[4 + g4][:]), reads=[f'ps{4 + g4}'], writes=[rso])
                        else:
                            P.op('dve', lambda e, g4=g4, so=so: e.tensor_copy(out=so[:, g4 * 512:(g4 + 1) * 512], in_=ps[4 + g4][:]), reads=[f'ps{4 + g4}'], writes=[rso])
                    P.op('sp', lambda e, so=so, tix=tix: e.dma_start(out=A['SCR'][tix * 128:(tix + 1) * 128, :], in_=so[:]), reads=[rso], writes=[('SCR', tix)], dma=True)

    def emit_peer2f(self, xin, dst, L):
        P, nc, A, ps = self.P, self.nc, self.A, self.ps
        NB = self.cfg.get('nb', 11)
        U = A['peer_u'].rearrange("l e d -> (l e) d")
        V = A['peer_v'].rearrange("l e d -> (l e) d")
        with Stage(self, f'pf{L}') as st:
            scs = [st.T(f'sc{i}', [128, 16, 128]) for i in range(2)]
            m = st.T('m', [128, 16, 16])
            ix = st.T('ix', [128, 16, 16], U32)
            ixf = st.T('ixf', [128, 16, 16])
            wk = st.T('wk', [128, 16, 128])
            cand = st.T('cand', [128, 8, 256])
            candi = st.T('candi', [128, 8, 256])
            wk2 = st.T('wk2', [128, 8, 256])
            junk2 = [st.T(f'junk2{i}', [128, 256]) for i in range(2)]
            ts = st.T('ts', [128, 8, 16])
            ef = st.T('ef', [128, 128])
            eis = [st.T(f'ei{i}', [128, 128], I32) for i in range(2)]
            gts = [st.T(f'gt{i}', [128, 8, 16]) for i in range(2)]
            gsum = st.T('gsum', [128, 8])
            xts = [st.T(f'xt{i}', [128, D]) for i in range(2)]
            hts = [st.T(f'ht{i}', [128, D]) for i in range(2)]
            ub = [st.T(f'ub{i}', [128, D]) for i in range(NB)]
            vb = [st.T(f'vb{i}', [128, D]) for i in range(NB)]
            junk = st.T('junk', [128, D])
            apre = st.T('apre', [128, 128])
            coef = st.T('coef', [128, 128])
            accs = [st.T(f'acc{i}', [128, D]) for i in range(2)]

            def load_sc(t):
                P.op('sp', lambda e, t=t: e.dma_start(out=scs[t % 2][:].rearrange("p a k -> p (a k)"), in_=A['SCR'][t * 128:(t + 1) * 128, :]),
                     writes=[f'sc{t % 2}'], dma=True)

            def load_xh(t):
                P.op('sp', lambda e, t=t: e.dma_start(out=xts[t % 2][:], in_=xin[t * 128:(t + 1) * 128, :]), writes=[f'xt{t % 2}'], dma=True)
                P.op('sp', lambda e, t=t: e.dma_start(out=hts[t % 2][:], in_=A['H'][t * 128:(t + 1) * 128, :]), writes=[f'ht{t % 2}'], dma=True)

            def topk_gen(t):
                sc = scs[t % 2]
                rsc = f'sc{t % 2}'
                eib, gtb = eis[t % 2], gts[t % 2]
                rei, rgt = f'ei{t % 2}', f'gt{t % 2}'
                for c in range(16):
                    P.op('dve', lambda e, c=c: e.max(out=m[:, c, 0:8], in_=sc[:, c, :]), reads=[rsc], writes=[('m0', c)])
                    yield
                P.fence('dve')
                for c in range(16):
                    P.op('dve', lambda e, c=c: e.max_index(out=ix[:, c, 0:8], in_max=m[:, c, 0:8], in_values=sc[:, c, :]),
                         reads=[rsc, ('m0', c)], writes=[('ix0', c)])
                    yield
                    P.op('dve', lambda e, c=c: e.match_replace(out=wk[:, c, :], in_to_replace=m[:, c, 0:8], in_values=sc[:, c, :], imm_value=-1e30),
                         reads=[rsc, ('m0', c)], writes=[('wk', c)])
                    yield
                P.fence('dve')
                for c in range(16):
                    P.op('dve', lambda e, c=c: e.max(out=m[:, c, 8:16], in_=wk[:, c, :]), reads=[('wk', c)], writes=[('m1', c)])
                    yield
                P.fence('dve')
                for c in range(16):
                    P.op('dve', lambda e, c=c: e.max_index(out=ix[:, c, 8:16], in_max=m[:, c, 8:16], in_values=wk[:, c, :]),
                         reads=[('wk', c), ('m1', c)], writes=[('ix1', c)])
                    yield
                P.fence('dve')
                mres = [('m0', c) for c in range(16)] + [('m1', c) for c in range(16)]
                ixres = [('ix0', c) for c in range(16)] + [('ix1', c) for c in range(16)]
                P.op('dve', lambda e: e.tensor_copy(out=ixf[:], in_=ix[:]), reads=ixres, writes=['ixf'])
                yield
                m4 = m[:].rearrange("p (h two) k -> p h two k", two=2)
                i4 = ixf[:].rearrange("p (h two) k -> p h two k", two=2)
                c4 = cand[:].rearrange("p h (a b) -> p h a b", a=16)
                ci4 = candi[:].rearrange("p h (a b) -> p h a b", a=16)
                P.op('dve', lambda e: e.tensor_tensor(out=c4, in0=m4[:, :, 0, :].unsqueeze(3).to_broadcast([128, 8, 16, 16]),
                                                      in1=m4[:, :, 1, :].unsqueeze(2).to_broadcast([128, 8, 16, 16]), op=ALU.add),
                     reads=mres, writes=['cand'])
                yield
                P.op('dve', lambda e: e.tensor_scalar(out=i4[:, :, 0, :], in0=i4[:, :, 0, :], scalar1=128.0, scalar2=None, op0=ALU.mult),
                     reads=['ixf'], writes=['ixf'])
                yield
                P.op('dve', lambda e: e.tensor_tensor(out=ci4, in0=i4[:, :, 0, :].unsqueeze(3).to_broadcast([128, 8, 16, 16]),
                                                      in1=i4[:, :, 1, :].unsqueeze(2).to_broadcast([128, 8, 16, 16]), op=ALU.add),
                     reads=['ixf'], writes=['candi'])
                yield
                for h in range(8):
                    P.op('dve', lambda e, h=h: e.max(out=ts[:, h, 0:8], in_=cand[:, h, :]), reads=['cand'], writes=[('ts0', h)])
                    yield
                P.fence('dve')
                for h in range(8):
                    P.op('dve', lambda e, h=h: e.match_replace(out=wk2[:, h, :], in_to_replace=ts[:, h, 0:8], in_values=cand[:, h, :], imm_value=-1e30),
                         reads=['cand', ('ts0', h)], writes=[('wk2', h)])
                    yield
                P.fence('dve')
                for h in range(8):
                    P.op('dve', lambda e, h=h: e.max(out=ts[:, h, 8:16], in_=wk2[:, h, :]), reads=[('wk2', h)], writes=[('ts1', h)])
                    yield
                P.fence('dve')
                tsres = [('ts0', h) for h in range(8)] + [('ts1', h) for h in range(8)]
                for h in range(8):
                    for k in range(16):
                        P.op('dve', lambda e, h=h, k=k: e.scalar_tensor_tensor(out=junk2[(h * 16 + k) % 2][:], in0=cand[:, h, :], scalar=ts[:, h, k:k + 1], in1=candi[:, h, :],
                                                                               op0=ALU.is_equal, op1=ALU.mult, accum_out=ef[:, h * 16 + k:h * 16 + k + 1]),
                             reads=['cand', 'candi', ('ts0', h), ('ts1', h)], writes=[('ef', h * 16 + k)])
                        yield
                P.fence('dve')
                P.op('dve', lambda e: e.tensor_scalar(out=ef[:], in0=ef[:], scalar1=float(NEXP - 1), scalar2=float(L * NEXP), op0=ALU.min, op1=ALU.add),
                     reads=[('ef', q) for q in range(128)], writes=['ef'])
                yield
                P.op('dve', lambda e: e.tensor_copy(out=eib[:], in_=ef[:]), reads=['ef'], writes=[rei])
                yield
                P.op('dve', lambda e: e.tensor_tensor(out=gtb[:], in0=ts[:], in1=ts[:, :, 0:1].to_broadcast([128, 8, 16]), op=ALU.subtract),
                     reads=tsres, writes=[rgt])
                yield
                P.op('act', lambda e: e.activation(out=gtb[:], in_=gtb[:], func=AF.Exp), reads=[rgt], writes=[rgt])
                P.op('dve', lambda e: e.tensor_reduce(out=gsum[:], in_=gtb[:], axis=AX.X, op=ALU.add), reads=[rgt], writes=['gsum'])
                yield
                P.op('dve', lambda e: e.reciprocal(out=gsum[:], in_=gsum[:]), reads=['gsum'], writes=['gsum'])
                yield
                P.op('dve', lambda e: e.tensor_tensor(out=gtb[:], in0=gtb[:], in1=gsum[:].unsqueeze(2).to_broadcast([128, 8, 16]), op=ALU.mult),
                     reads=[rgt, 'gsum'], writes=[rgt])
                yield

            def step(gen, n=1):
                if gen is None:
                    return None
                try:
                    for _ in range(n):
                        next(gen)
                except StopIteration:
                    return None
                return gen

            load_sc(0)
            load_sc(1)
            load_xh(0)
            g0 = topk_gen(0)
            while g0 is not None:
                g0 = step(g0, 64)
            nu = nv = 0
            for ti in range(NT):
                b = ti % 2
                xt, ht, eib, gtb, acc = xts[b], hts[b], eis[b], gts[b], accs[b]
                rx, rh, rei, rgt, racc = f'xt{b}', f'ht{b}', f'ei{b}', f'gt{b}', f'acc{b}'
                gt2 = gtb[:].rearrange("p h k -> p (h k)")
                if ti + 1 < NT:
                    load_xh(ti + 1)
                gen = topk_gen(ti + 1) if ti + 1 < NT else None
                for k in range(128):
                    s_ = nu % NB
                    nu += 1
                    P.op('pool', lambda e, s_=s_, k=k, eib=eib: e.indirect_dma_start(
                        out=ub[s_][:], out_offset=None, in_=U,
                        in_offset=bass.IndirectOffsetOnAxis(ap=eib[:, k:k + 1], axis=0)),
                        reads=[rei], writes=[('ub', s_)], dma=True)
                    P.op('dve', lambda e, s_=s_, k=k, ht=ht: e.scalar_tensor_tensor(out=junk[:], in0=ub[s_][:], scalar=1.0, in1=ht[:], op0=ALU.mult, op1=ALU.mult,
                                                                                 accum_out=apre[:, k:k + 1]),
                         reads=[('ub', s_), rh], writes=[('apre', k)])
                    gen = step(gen)
                P.op('act', lambda e: e.activation(out=coef[:], in_=apre[:], func=AF.Gelu), reads=[('apre', k) for k in range(128)], writes=['coef'])
                P.op('dve', lambda e, gt2=gt2: e.tensor_tensor(out=coef[:], in0=coef[:], in1=gt2, op=ALU.mult), reads=['coef', rgt], writes=['coef'])
                for k in range(128):
                    s_ = nv % NB
                    nv += 1
                    P.op('pool', lambda e, s_=s_, k=k, eib=eib: e.indirect_dma_start(
                        out=vb[s_][:], out_offset=None, in_=V,
                        in_offset=bass.IndirectOffsetOnAxis(ap=eib[:, k:k + 1], axis=0)),
                        reads=[rei], writes=[('vb', s_)], dma=True)
                    if k == 0:
                        P.op('dve', lambda e, s_=s_, acc=acc: e.tensor_scalar(out=acc[:], in0=vb[s_][:], scalar1=coef[:, 0:1], scalar2=None, op0=ALU.mult),
                             reads=[('vb', s_), 'coef'], writes=[racc])
                    else:
                        P.op('dve', lambda e, s_=s_, k=k, acc=acc: e.scalar_tensor_tensor(out=acc[:], in0=vb[s_][:], scalar=coef[:, k:k + 1], in1=acc[:],
                                                                                       op0=ALU.mult, op1=ALU.add),
                             reads=[('vb', s_), 'coef', racc], writes=[racc])
                    gen = step(gen)
                while gen is not None:
                    gen = step(gen, 64)
                if ti + 2 < NT:
                    load_sc(ti + 2)
                P.op('dve', lambda e, acc=acc: e.tensor_tensor(out=acc[:], in0=acc[:], in1=self.gate_bc[:], op=ALU.mult), reads=[racc, 'gate_bc'], writes=[racc])
                P.op('dve', lambda e, acc=acc, xt=xt: e.scalar_tensor_tensor(out=acc[:], in0=xt[:], scalar=ALPHA, in1=acc[:], op0=ALU.mult, op1=ALU.add),
                     reads=[rx, racc], writes=[racc])
                self.layernorm_inplace(acc[:], racc, gb='dve')
                P.op('sp', lambda e, acc=acc, ti=ti: e.dma_start(out=dst[ti * 128:(ti + 1) * 128, :], in_=acc[:]), reads=[racc], dma=True)

    def emit_cast_tables(self):
        P, nc, A = self.P, self.nc, self.A
        CN = 4
        Uv = A['peer_u'].rearrange("l (n p) d -> p (l n) d", p=128)
        Vv = A['peer_v'].rearrange("l (n p) d -> p (l n) d", p=128)
        Ov = A['UVB'].rearrange("(n p) d -> p n d", p=128)
        with Stage(self, 'cast') as st:
            iu = [st.T(f'iu{i}', [128, CN, D]) for i in range(2)]
            iv = [st.T(f'iv{i}', [128, CN, D]) for i in range(2)]
            ob = [st.T(f'ob{i}', [128, CN, 2 * D], BF16) for i in range(2)]
            nchunk = (2 * NEXP // 128) // CN

            def ld(ci):
                b = ci % 2
                ns = slice(ci * CN, (ci + 1) * CN)
                P.op('sp', lambda e, b=b, ns=ns: e.dma_start(out=iu[b][:], in_=Uv[:, ns, :]), writes=[f'iu{b}'], dma=True)
                P.op('sp', lambda e, b=b, ns=ns: e.dma_start(out=iv[b][:], in_=Vv[:, ns, :]), writes=[f'iv{b}'], dma=True)

            ld(0)
            for ci in range(nchunk):
                b = ci % 2
                ns = slice(ci * CN, (ci + 1) * CN)
                if ci + 1 < nchunk:
                    ld(ci + 1)
                P.op('dve', lambda e, b=b: e.tensor_copy(out=ob[b][:, :, 0:D], in_=iu[b][:]), reads=[f'iu{b}'], writes=[(f'ob{b}', 0)])
                P.op('act', lambda e, b=b: e.copy(out=ob[b][:, 0:2, D:2 * D], in_=iv[b][:, 0:2, :]), reads=[f'iv{b}'], writes=[(f'ob{b}', 1)])
                P.op('pool', lambda e, b=b: e.tensor_copy(out=ob[b][:, 2:4, D:2 * D], in_=iv[b][:, 2:4, :]), reads=[f'iv{b}'], writes=[(f'ob{b}', 2)])
                P.op('act', lambda e, b=b, ns=ns: e.dma_start(out=Ov[:, ns, :], in_=ob[b][:]), reads=[(f'ob{b}', 0), (f'ob{b}', 1), (f'ob{b}', 2)],
                     writes=[('UVB', ci)], dma=True)

    def emit_peer2g(self, xin, dst, L):
        P, nc, A, ps = self.P, self.nc, self.A, self.ps
        NB = self.cfg.get('nb', 24)
        GS = self.cfg.get('gs', 8)
        UVB = A['UVB']
        with Stage(self, f'pg{L}') as st:
            scs = [st.T('sc0', [128, 16, 128])] * 2
            m = st.T('m', [128, 16, 16])
            ix = st.T('ix', [128, 16, 16], U32)
            ixf = st.T('ixf', [128, 16, 16])
            wk = st.T('wk', [128, 16, 128])
            cand = st.T('cand', [128, 8, 256])
            candi = st.T('candi', [128, 8, 256])
            wk2 = wk[:].rearrange("p a k -> p (a k)").rearrange("p (h c) -> p h c", h=8)
            junk2 = [st.T(f'junk2{i}', [128, 256]) for i in range(2)]
            ts = st.T('ts', [128, 8, 16])
            ef = st.T('ef', [128, 128])
            eis = [st.T(f'ei{i}', [128, 128], I32) for i in range(2)]
            gts = [st.T(f'gt{i}', [128, 8, 16]) for i in range(2)]
            gsum = st.T('gsum', [128, 8])
            xts = [st.T(f'xt{i}', [128, D]) for i in range(2)]
            hts = [st.T(f'ht{i}', [128, D]) for i in range(2)]
            uvb = [st.T(f'uv{i}', [128, 2 * D], BF16) for i in range(NB)]
            junk = st.T('junk', [128, D], BF16)
            junka = st.T('junka', [128, D], BF16)
            prods = [st.T(f'prod{i}', [128, D], BF16) for i in range(3)]
            hbs = [st.T(f'hb{i}', [128, D], BF16) for i in range(2)]
            apre = st.T('apre', [128, 128])
            ge = st.T('ge', [128, 128])
            cf = st.T('cf', [128, 128])
            dgs = [st.T(f'dg{i}', [128, 128], BF16) for i in range(4)]
            accs = [st.T(f'acc{i}', [128, D]) for i in range(2)]

            def load_sc(t):
                P.op('sp', lambda e, t=t: e.dma_start(out=scs[t % 2][:].rearrange("p a k -> p (a k)"), in_=A['SCR'][t * 128:(t + 1) * 128, :]),
                     writes=['sc0'], dma=True)

            def load_xh(t):
                P.op('sp', lambda e, t=t: e.dma_start(out=xts[t % 2][:], in_=xin[t * 128:(t + 1) * 128, :]), writes=[f'xt{t % 2}'], dma=True)
                P.op('sp', lambda e, t=t: e.dma_start(out=hts[t % 2][:], in_=A['H'][t * 128:(t + 1) * 128, :]), writes=[f'ht{t % 2}'], dma=True)

            def topk_gen(t):
                sc = scs[t % 2]
                rsc = 'sc0'
                eib, gtb = eis[t % 2], gts[t % 2]
                rei, rgt = f'ei{t % 2}', f'gt{t % 2}'
                for c in range(16):
                    P.op('dve', lambda e, c=c: e.max(out=m[:, c, 0:8], in_=sc[:, c, :]), reads=[rsc], writes=[('m0', c)])
                    yield
                P.fence('dve')
                for c in range(16):
                    P.op('dve', lambda e, c=c: e.max_index(out=ix[:, c, 0:8], in_max=m[:, c, 0:8], in_values=sc[:, c, :]),
                         reads=[rsc, ('m0', c)], writes=[('ix0', c)])
                    yield
                    P.op('dve', lambda e, c=c: e.match_replace(out=wk[:, c, :], in_to_replace=m[:, c, 0:8], in_values=sc[:, c, :], imm_value=-1e30),
                         reads=[rsc, ('m0', c)], writes=[('wk', c)])
                    yield
                P.fence('dve')
                for c in range(16):
                    P.op('dve', lambda e, c=c: e.max(out=m[:, c, 8:16], in_=wk[:, c, :]), reads=[('wk', c)], writes=[('m1', c)])
                    yield
                P.fence('dve')
                for c in range(16):
                    P.op('dve', lambda e, c=c: e.max_index(out=ix[:, c, 8:16], in_max=m[:, c, 8:16], in_values=wk[:, c, :]),
                         reads=[('wk', c), ('m1', c)], writes=[('ix1', c)])
                    yield
                P.fence('dve')
                mres = [('m0', c) for c in range(16)] + [('m1', c) for c in range(16)]
                ixres = [('ix0', c) for c in range(16)] + [('ix1', c) for c in range(16)]
                P.op('dve', lambda e: e.tensor_copy(out=ixf[:], in_=ix[:]), reads=ixres, writes=['ixf'])
                yield
                m4 = m[:].rearrange("p (h two) k -> p h two k", two=2)
                i4 = ixf[:].rearrange("p (h two) k -> p h two k", two=2)
                c4 = cand[:].rearrange("p h (a b) -> p h a b", a=16)
                ci4 = candi[:].rearrange("p h (a b) -> p h a b", a=16)
                P.op('dve', lambda e: e.tensor_tensor(out=c4, in0=m4[:, :, 0, :].unsqueeze(3).to_broadcast([128, 8, 16, 16]),
                                                      in1=m4[:, :, 1, :].unsqueeze(2).to_broadcast([128, 8, 16, 16]), op=ALU.add),
                     reads=mres, writes=['cand'])
                yield
                P.op('dve', lambda e: e.tensor_scalar(out=i4[:, :, 0, :], in0=i4[:, :, 0, :], scalar1=128.0, scalar2=None, op0=ALU.mult),
                     reads=['ixf'], writes=['ixf'])
                yield
                P.op('dve', lambda e: e.tensor_tensor(out=ci4, in0=i4[:, :, 0, :].unsqueeze(3).to_broadcast([128, 8, 16, 16]),
                                                      in1=i4[:, :, 1, :].unsqueeze(2).to_broadcast([128, 8, 16, 16]), op=ALU.add),
                     reads=['ixf'], writes=['candi'])
                yield
                for h in range(8):
                    P.op('dve', lambda e, h=h: e.max(out=ts[:, h, 0:8], in_=cand[:, h, :]), reads=['cand'], writes=[('ts0', h)])
                    yield
                P.fence('dve')
                for h in range(8):
                    P.op('dve', lambda e, h=h: e.match_replace(out=wk2[:, h, :], in_to_replace=ts[:, h, 0:8], in_values=cand[:, h, :], imm_value=-1e30),
                         reads=['cand', ('ts0', h)], writes=[('wk2', h)])
                    yield
                P.fence('dve')
                for h in range(8):
                    P.op('dve', lambda e, h=h: e.max(out=ts[:, h, 8:16], in_=wk2[:, h, :]), reads=[('wk2', h)], writes=[('ts1', h)])
                    yield
                P.fence('dve')
                tsres = [('ts0', h) for h in range(8)] + [('ts1', h) for h in range(8)]
                for h in range(8):
                    for k in range(16):
                        P.op('dve', lambda e, h=h, k=k: e.scalar_tensor_tensor(out=junk2[(h * 16 + k) % 2][:], in0=cand[:, h, :], scalar=ts[:, h, k:k + 1], in1=candi[:, h, :],
                                                                               op0=ALU.is_equal, op1=ALU.mult, accum_out=ef[:, h * 16 + k:h * 16 + k + 1]),
                             reads=['cand', 'candi', ('ts0', h), ('ts1', h)], writes=[('ef', h * 16 + k)])
                        yield
                P.fence('dve')
                P.op('dve', lambda e: e.tensor_scalar(out=ef[:], in0=ef[:], scalar1=float(NEXP - 1), scalar2=float(L * NEXP), op0=ALU.min, op1=ALU.add),
                     reads=[('ef', q) for q in range(128)], writes=['ef'])
                yield
                P.op('dve', lambda e: e.tensor_copy(out=eib[:], in_=ef[:]), reads=['ef'], writes=[rei])
                yield
                P.op('dve', lambda e: e.tensor_tensor(out=gtb[:], in0=ts[:], in1=ts[:, :, 0:1].to_broadcast([128, 8, 16]), op=ALU.subtract),
                     reads=tsres, writes=[rgt])
                yield
                P.op('act', lambda e: e.activation(out=gtb[:], in_=gtb[:], func=AF.Exp), reads=[rgt], writes=[rgt])
                P.op('dve', lambda e: e.tensor_reduce(out=gsum[:], in_=gtb[:], axis=AX.X, op=ALU.add), reads=[rgt], writes=['gsum'])
                yield
                P.op('dve', lambda e: e.reciprocal(out=gsum[:], in_=gsum[:]), reads=['gsum'], writes=['gsum'])
                yield
                P.op('dve', lambda e: e.tensor_tensor(out=gtb[:], in0=gtb[:], in1=gsum[:].unsqueeze(2).to_broadcast([128, 8, 16]), op=ALU.mult),
                     reads=[rgt, 'gsum'], writes=[rgt])
                yield

            def step(gen, n=1):
                if gen is None:
                    return None
                try:
                    for _ in range(n):
                        next(gen)
                except StopIteration:
                    return None
                return gen

            load_sc(0)
            load_xh(0)
            g0 = topk_gen(0)
            while g0 is not None:
                g0 = step(g0, 64)
            load_sc(1)
            nu = 0
            ndg = 0
            npr = 0
            for ti in range(NT):
                b = ti % 2
                xt, ht, eib, gtb, acc = xts[b], hts[b], eis[b], gts[b], accs[b]
                rx, rh, rei, rgt, racc = f'xt{b}', f'ht{b}', f'ei{b}', f'gt{b}', f'acc{b}'
                gt2 = gtb[:].rearrange("p h k -> p (h k)")
                pa = [ps[(ti % 2) * 2], ps[(ti % 2) * 2 + 1]]
                rpa = [f'ps{(ti % 2) * 2}', f'ps{(ti % 2) * 2 + 1}']
                if ti + 1 < NT:
                    load_xh(ti + 1)
                hb, rhb = hbs[b], f'hb{b}'
                P.op('dve', lambda e, hb=hb, ht=ht: e.tensor_copy(out=hb[:], in_=ht[:]), reads=[rh], writes=[rhb])
                gen = topk_gen(ti + 1) if ti + 1 < NT else None
                for g in range(128 // GS):
                    slots = []
                    for kk in range(GS):
                        k = g * GS + kk
                        s_ = nu % NB
                        nu += 1
                        slots.append(s_)
                        P.op('pool', lambda e, s_=s_, k=k, eib=eib: e.indirect_dma_start(
                            out=uvb[s_][:], out_offset=None, in_=UVB,
                            in_offset=bass.IndirectOffsetOnAxis(ap=eib[:, k:k + 1], axis=0)),
                            reads=[rei], writes=[('uv', s_)], dma=True)
                        if (k % self.cfg.get('split_den', 3)) < self.cfg.get('split_num', 1) or not self.cfg.get('dot_split', True):
                            P.op('dve', lambda e, s_=s_, k=k, ht=ht: e.scalar_tensor_tensor(out=junk[:], in0=uvb[s_][:, 0:D], scalar=1.0, in1=ht[:], op0=ALU.mult, op1=ALU.mult,
                                                                                         accum_out=apre[:, k:k + 1]),
                                 reads=[('uv', s_), rh], writes=[('apre', k)])
                        else:
                            pr = prods[npr % 3]
                            rpr = f'prod{npr % 3}'
                            npr += 1
                            P.op('dve', lambda e, s_=s_, pr=pr, hb=hb: e.tensor_tensor(out=pr[:], in0=uvb[s_][:, 0:D], in1=hb[:], op=ALU.mult),
                                 reads=[('uv', s_), rhb], writes=[rpr])
                            P.op('act', lambda e, pr=pr, k=k: e.activation(out=junka[:], in_=pr[:], func=AF.Copy, accum_out=apre[:, k:k + 1]),
                                 reads=[rpr], writes=[('apre', k)])
                        gen = step(gen, 2)
                    gsl = slice(g * GS, (g + 1) * GS)
                    P.op('act', lambda e, gsl=gsl: e.activation(out=ge[:, gsl], in_=apre[:, gsl], func=AF.Gelu),
                         reads=[('apre', k) for k in range(g * GS, (g + 1) * GS)], writes=[('ge', g)])
                    P.op('dve', lambda e, gsl=gsl, gt2=gt2: e.tensor_tensor(out=cf[:, gsl], in0=ge[:, gsl], in1=gt2[:, gsl], op=ALU.mult),
                         reads=[('ge', g), rgt], writes=[('cf', g)])
                    for kk in range(GS):
                        k = g * GS + kk
                        s_ = slots[kk]
                        dg = dgs[ndg % 4]
                        rdg = f'dg{ndg % 4}'
                        ndg += 1
                        P.op('act', lambda e, dg=dg, k=k: e.activation(out=dg[:], in_=self.ident[:], func=AF.Copy, scale=cf[:, k:k + 1]),
                             reads=['ident', ('cf', g)], writes=[rdg])
                        for half in range(2):
                            P.op('pe', lambda e, dg=dg, s_=s_, half=half, k=k, pa=pa: e.matmul(out=pa[half][:], lhsT=dg[:], rhs=uvb[s_][:, D + half * 512:D + (half + 1) * 512],
                                                                                           start=(k == 0), stop=(k == 127)),
                                 reads=[rdg, ('uv', s_)], writes=[rpa[half]])
                while gen is not None:
                    gen = step(gen, 64)
                if ti + 2 < NT:
                    load_sc(ti + 2)
                for half in range(2):
                    P.op('dve', lambda e, acc=acc, half=half, pa=pa: e.tensor_tensor(out=acc[:, half * 512:(half + 1) * 512], in0=pa[half][:],
                                                                                   in1=self.gate_bc[:, half * 512:(half + 1) * 512], op=ALU.mult),
                         reads=[rpa[half], 'gate_bc'], writes=[racc])
                P.op('dve', lambda e, acc=acc, xt=xt: e.scalar_tensor_tensor(out=acc[:], in0=xt[:], scalar=ALPHA, in1=acc[:], op0=ALU.mult, op1=ALU.add),
                     reads=[rx, racc], writes=[racc])
                self.layernorm_inplace(acc[:], racc, gb='dve')
                P.op('sp', lambda e, acc=acc, ti=ti: e.dma_start(out=dst[ti * 128:(ti + 1) * 128, :], in_=acc[:]), reads=[racc], dma=True)

    def emit_attn1(self, xin):
        P, nc, A, ps = self.P, self.nc, self.A, self.ps
        ones = self.ones
        NCOL = 3 * D + 16
        with Stage(self, 'a1') as st:
            winr = st.T('winr', [128, 8, 3 * D], F32R)
            wstg = [st.T('wstg0', [128, 8, 512])] * 2
            wf = st.T('wf', [128, 8, 16])
            qkb = st.T('qkb', [128, 16])
            vbr = st.T('vbr', [1, D])
            vb_bc = st.T('vb_bc', [128, D])
            fb = st.T('fb', [16, 1])
            xts = [st.T(f'xt{i}', [128, D]) for i in range(2)]
            ht = st.T('ht', [128, D])
            hT = st.T('hT', [128, 8, 512], F32R)
            qko = [st.T(f'qko{i}', [128, 512]) for i in range(2)]
            vo = [st.T(f'vo{i}', [128, D]) for i in range(2)]
            Fcb = [st.T(f'Fcb{i}', [16, 512]) for i in range(2)]
            Frb = st.T('Frb', [16, 512], F32R)
            Flb = st.T('Flb', [16, 512])
            nFr = st.T('nFr', [16, 512])
            nFl = st.T('nFl', [16, 512])
            spt = st.T('spt', [16, 512])
            o16 = st.T('o16', [16, 512])
            for q in range(6):
                wb = wstg[0]
                rw = 'wstg0'
                P.op('sp', lambda e, q=q, wb=wb: e.dma_start(out=wb[:], in_=A['attn_in_w'][:, q * 512:(q + 1) * 512].rearrange("(k p) n -> p k n", p=128)),
                     writes=[rw], dma=True)
                eng = ('dve', 'pool')[q % 2]
                P.op(eng, lambda e, q=q, wb=wb: e.tensor_copy(out=winr[:, :, q * 512:(q + 1) * 512], in_=wb[:]), reads=[rw], writes=[('win', q)])
            P.op('sp', lambda e: e.dma_start(out=wf[:], in_=A['attn_in_w'][:, 3 * D:NCOL].rearrange("(k p) n -> p k n", p=128)),
                 writes=['wf'], dma=True)
            P.op('sp', lambda e: e.dma_start(out=qkb[:], in_=A['attn_qkb_l']), writes=['qkb'], dma=True)
            P.op('sp', lambda e: e.dma_start(out=vbr[:], in_=A['attn_vb']), writes=['vbr'], dma=True)
            P.op('sp', lambda e: e.dma_start(out=fb[:], in_=A['attn_fb']), writes=['fb'], dma=True)
            P.op('dve', lambda e: e.tensor_scalar(out=qkb[:, 0:8], in0=qkb[:, 0:8], scalar1=0.125, scalar2=None, op0=ALU.mult), reads=['qkb'], writes=['qkb'])
            P.op('dve', lambda e: e.tensor_scalar(out=fb[:], in0=fb[:], scalar1=-1.0, scalar2=None, op0=ALU.mult), reads=['fb'], writes=['fb'])
            P.op('pool', lambda e: e.memset(o16[:], 1.0), writes=['o16'])
            for half in range(2):
                P.op('pe', lambda e, half=half: e.matmul(out=ps[4 + half][:], lhsT=ones[0:1, :], rhs=vbr[0:1, half * 512:(half + 1) * 512], start=True, stop=True),
                     reads=['ones', 'vbr'], writes=[f'ps{4 + half}'])
                P.op('act', lambda e, half=half: e.copy(out=vb_bc[:, half * 512:(half + 1) * 512], in_=ps[4 + half][:]), reads=[f'ps{4 + half}'], writes=['vb_bc'])
            ti = 0
            for jb in range(8):
                cols = slice(jb * 512, (jb + 1) * 512)
                for tl in range(4):
                    xt = xts[ti % 2]
                    rx = f'xt{ti % 2}'
                    P.op('sp', lambda e, xt=xt, ti=ti: e.dma_start(out=xt[:], in_=xin[ti * 128:(ti + 1) * 128, :]), writes=[rx], dma=True)
                    self.modulate(xt[:], ht[:], rx, 'ht')
                    self.transpose8(ht, 'ht', hT, 'hT', tl * 128, 0)
                    ti += 1
                for c in range(16):
                    bank = ps[2 + c % 2]
                    rb = f'ps{2 + c % 2}'
                    for k in range(8):
                        P.op('pe', lambda e, k=k, c=c, bank=bank: e.matmul(out=bank[:], lhsT=winr[:, k, c * 128:(c + 1) * 128], rhs=hT[:, k, :],
                                                                          start=(k == 0), stop=(k == 7)),
                             reads=[('win', c // 4), 'hT'], writes=[rb])
                    ob = qko[c % 2]
                    rob = f'qko{c % 2}'
                    P.op('act', lambda e, c=c, bank=bank, ob=ob: e.activation(out=ob[:], in_=bank[:], func=AF.Identity, bias=qkb[:, c:c + 1],
                                                                             scale=(0.125 if c < 8 else 1.0)),
                         reads=[rb, 'qkb'], writes=[rob])
                    dstt = A['QA'] if c < 8 else A['KA']
                    for hh in range(2):
                        head = (c % 8) * 2 + hh
                        P.op('sp', lambda e, ob=ob, hh=hh, head=head, dstt=dstt: e.dma_start(out=dstt[head, 0:64, cols], in_=ob[hh * 64:(hh + 1) * 64, :]),
                             reads=[rob], writes=[('QK', c, hh)], dma=True)
                for tl in range(4):
                    tix = jb * 4 + tl
                    vt = vo[tix % 2]
                    rv = f'vo{tix % 2}'
                    for half in range(2):
                        bank = ps[4 + half]
                        rb = f'ps{4 + half}'
                        for k in range(8):
                            P.op('pe', lambda e, k=k, bank=bank, half=half, tl=tl: e.matmul(out=bank[:], lhsT=hT[:, k, tl * 128:(tl + 1) * 128],
                                                                                        rhs=winr[:, k, 2 * D + half * 512:2 * D + (half + 1) * 512],
                                                                                        start=(k == 0), stop=(k == 7)),
                                 reads=['hT', ('win', 4 + half)], writes=[rb])
                        P.op('dve', lambda e, vt=vt, bank=bank, half=half: e.tensor_tensor(out=vt[:, half * 512:(half + 1) * 512], in0=bank[:],
                                                                                      in1=vb_bc[:, half * 512:(half + 1) * 512], op=ALU.add),
                             reads=[rb, 'vb_bc'], writes=[rv])
                    P.op('sp', lambda e, vt=vt, tix=tix: e.dma_start(out=A['V'][tix * 128:(tix + 1) * 128, :], in_=vt[:]), reads=[rv], writes=[('V', tix)], dma=True)
                for k in range(8):
                    P.op('pe', lambda e, k=k: e.matmul(out=ps[6][0:16, :], lhsT=wf[:, k, :], rhs=hT[:, k, :].bitcast(F32), start=(k == 0), stop=(k == 7)),
                         reads=['wf', 'hT'], writes=['ps6'])
                P.op('act', lambda e: e.activation(out=spt[:], in_=ps[6][0:16, :], func=AF.Exp, bias=fb[:, 0:1], scale=-1.0), reads=['ps6', 'fb'], writes=['spt'])
                P.op('act', lambda e: e.activation(out=spt[:], in_=spt[:], func=AF.Ln, bias=1.0, scale=1.0), reads=['spt'], writes=['spt'])
                P.op('dve', lambda e: e.tensor_scalar(out=spt[:], in0=spt[:], scalar1=-1.0, scalar2=None, op0=ALU.mult), reads=['spt'], writes=['spt'])
                Fc = Fcb[jb % 2]
                rF = f'Fcb{jb % 2}'
                init = 0.0 if jb == 0 else Fcb[(jb - 1) % 2][:, 511:512]
                P.op('dve', lambda e, init=init, Fc=Fc: e.tensor_tensor_scan(out=Fc[:], data0=o16[:], data1=spt[:], initial=init,
                                                                             op0=ALU.mult, op1=ALU.add),
                     reads=['o16', 'spt', f'Fcb{(jb - 1) % 2}'], writes=[rF])
                P.op('dve', lambda e, Fc=Fc: e.tensor_copy(out=Frb[:], in_=Fc[:]), reads=[rF], writes=['Frb'])
                P.op('dve', lambda e, Fc=Fc: e.tensor_tensor(out=Flb[:], in0=Fc[:], in1=Frb[:].bitcast(F32), op=ALU.subtract), reads=[rF, 'Frb'], writes=['Flb'])
                P.op('dve', lambda e: e.tensor_scalar(out=nFr[:], in0=Frb[:].bitcast(F32), scalar1=-1.0, scalar2=None, op0=ALU.mult), reads=['Frb'], writes=['nFr'])
                P.op('dve', lambda e: e.tensor_scalar(out=nFl[:], in0=Flb[:], scalar1=-1.0, scalar2=None, op0=ALU.mult), reads=['Flb'], writes=['nFl'])
                P.op('sp', lambda e: e.dma_start(out=A['QA'][:, 64, cols], in_=Frb[:].bitcast(F32)), reads=['Frb'], writes=[('QAf', jb)], dma=True)
                P.op('sp', lambda e: e.dma_start(out=A['QA'][:, 65, cols], in_=Flb[:]), reads=['Flb'], writes=[('QAl', jb)], dma=True)
                P.op('sp', lambda e: e.dma_start(out=A['KA'][:, 66, cols], in_=nFr[:]), reads=['nFr'], writes=[('KAf', jb)], dma=True)
                P.op('sp', lambda e: e.dma_start(out=A['KA'][:, 67, cols], in_=nFl[:]), reads=['nFl'], writes=[('KAl', jb)], dma=True)
                for r in (66, 67):
                    P.op('sp', lambda e, r=r: e.dma_start(out=A['QA'][:, r, cols], in_=o16[:]), reads=['o16'], writes=[('QAo', r, jb)], dma=True)
                for r in (64, 65):
                    P.op('sp', lambda e, r=r: e.dma_start(out=A['KA'][:, r, cols], in_=o16[:]), reads=['o16'], writes=[('KAo', r, jb)], dma=True)

    def emit_attn2(self):
        P, nc, A, ps = self.P, self.nc, self.A, self.ps
        NR = 68
        with Stage(self, 'a2') as st:
            qst = st.T('qst', [NR, S])
            kst = st.T('kst', [NR, S])
            vst = st.T('vst', [128, 32, 64])
            QAh = [st.T(f'QAh{i}', [NR, S], F32R) for i in range(2)]
            KAh = [st.T(f'KAh{i}', [NR, S], F32R) for i in range(2)]
            Vh = [st.T(f'Vh{i}', [128, 32, 128], F32R) for i in range(2)]
            ones_r = st.T('ones_r', [128, 128], F32R)
            pt = [st.T(f'pt{i}', [128, 512], F32R) for i in range(3)]
            lm = [st.T(f'lm{i}', [128, 512]) for i in range(2)]
            mask = st.T('mask', [128, 4, 512])
            rzt = st.T('rzt', [64, 512])
            oT = [st.T(f'oT{i}', [64, 512]) for i in range(2)]
            P.op('pool', lambda e: e.memset(mask[:], 0.0), writes=['mask'])
            for i4 in range(4):
                P.op('pool', lambda e, i4=i4: e.affine_select(out=mask[:, i4, :], in_=mask[:, i4, :], pattern=[[1, 512]], compare_op=ALU.is_ge,
                                                              fill=NEG, base=-128 * i4, channel_multiplier=-1), reads=['mask'], writes=['mask'])
            P.op('pool', lambda e: e.tensor_copy(out=ones_r[:], in_=self.ones[:]), reads=['ones'], writes=['ones_r'])
            npt = 0
            nlm = 0
            nS = 0
            nO = 0

            def loads(h):
                b = h % 2
                qa, ka, vh = QAh[b], KAh[b], Vh[b]
                rq, rk, rv = f'QAh{b}', f'KAh{b}', f'Vh{b}'
                for q4 in range(4):
                    cs = slice(q4 * 1024, (q4 + 1) * 1024)
                    P.op('sp', lambda e, h=h, cs=cs: e.dma_start(out=qst[:, cs], in_=A['QA'][h, :, cs]), writes=[('qst', q4)], dma=True)
                    P.op('sp', lambda e, h=h, cs=cs: e.dma_start(out=kst[:, cs], in_=A['KA'][h, :, cs]), writes=[('kst', q4)], dma=True)
                    P.op('sp', lambda e, h=h, q4=q4: e.dma_start(
                        out=vst[:, q4 * 8:(q4 + 1) * 8, :],
                        in_=A['V'][q4 * 1024:(q4 + 1) * 1024, h * 64:(h + 1) * 64].rearrange("(i p) d -> p i d", p=128)),
                        writes=[('vst', q4)], dma=True)
                for q4 in range(4):
                    cs = slice(q4 * 1024, (q4 + 1) * 1024)
                    P.op('pool', lambda e, qa=qa, cs=cs: e.tensor_copy(out=qa[:, cs], in_=qst[:, cs]), reads=[('qst', q4)], writes=[rq])
                    P.op('pool', lambda e, ka=ka, cs=cs: e.tensor_copy(out=ka[:, cs], in_=kst[:, cs]), reads=[('kst', q4)], writes=[rk])
                    for dup in range(2):
                        P.op('pool', lambda e, vh=vh, q4=q4, dup=dup: e.tensor_copy(out=vh[:, q4 * 8:(q4 + 1) * 8, dup * 64:(dup + 1) * 64],
                                                                                  in_=vst[:, q4 * 8:(q4 + 1) * 8, :]),
                             reads=[('vst', q4)], writes=[rv])

            loads(0)
            NH = self.cfg.get('nheads', 16)
            steps = [(h, j, i) for h in range(NH) for j in range(8) for i in range(4 * j + 4)]

            def emit_qk(n):
                h, j, i = steps[n]
                b = h % 2
                sb = ps[n % 3]
                P.op('pe', lambda e, sb=sb, ka=KAh[b], qa=QAh[b], i=i, j=j: e.matmul(out=sb[:], lhsT=ka[:, i * 128:(i + 1) * 128], rhs=qa[:, j * 512:(j + 1) * 512],
                                                                                 start=True, stop=True),
                     reads=[f'KAh{b}', f'QAh{b}'], writes=[f'ps{n % 3}'])

            emit_qk(0)
            for n, (h, j, i) in enumerate(steps):
                b = h % 2
                vh, rv = Vh[b], f'Vh{b}'
                if j == 0 and i == 0 and h + 1 < NH:
                    loads(h + 1)
                if n + 1 < len(steps):
                    emit_qk(n + 1)
                if i == 0:
                    oset = nO % 2
                    nO += 1
                poA, poB = ps[3 + 2 * oset], ps[4 + 2 * oset]
                rA, rB = f'ps{3 + 2 * oset}', f'ps{4 + 2 * oset}'
                ot, rot = oT[oset], f'oT{oset}'
                last = 4 * j + 3
                sb, rsb = ps[n % 3], f'ps{n % 3}'
                p_, rp = pt[n % 3], f'pt{n % 3}'
                if i >= 4 * j:
                    l_ = lm[nlm % 2]
                    rl = f'lm{nlm % 2}'
                    nlm += 1
                    P.op('dve', lambda e, l_=l_, sb=sb, i=i, j=j: e.tensor_tensor(out=l_[:], in0=sb[:], in1=mask[:, i - 4 * j, :], op=ALU.add),
                         reads=[rsb, 'mask'], writes=[rl])
                    P.op('act', lambda e, p_=p_, l_=l_: e.activation(out=p_[:], in_=l_[:], func=AF.Exp), reads=[rl], writes=[rp])
                else:
                    P.op('act', lambda e, p_=p_, sb=sb: e.activation(out=p_[:], in_=sb[:], func=AF.Exp), reads=[rsb], writes=[rp])
                P.op('pe', lambda e, p_=p_, i=i, poA=poA, vh=vh, last=last: e.matmul(out=poA[:], lhsT=vh[:, i, :], rhs=p_[:], start=(i == 0), stop=(i == last)),
                     reads=[rp, rv], writes=[rA])
                P.op('pe', lambda e, p_=p_, i=i, poB=poB, last=last: e.matmul(out=poB[:], lhsT=ones_r[:], rhs=p_[:], start=(i == 0), stop=(i == last)),
                     reads=[rp, 'ones_r'], writes=[rB])
                if i == last:
                    P.op('dve', lambda e, poB=poB: e.reciprocal(out=rzt[:], in_=poB[0:64, :]), reads=[rB], writes=['rzt'])
                    P.op('dve', lambda e, poA=poA, ot=ot: e.tensor_tensor(out=ot[:], in0=poA[0:64, :], in1=rzt[:], op=ALU.mult), reads=[rA, 'rzt'], writes=[rot])
                    P.op('sp', lambda e, ot=ot, h=h, j=j: e.dma_start(out=A['AOT'][h // 2, (h % 2) * 64:(h % 2) * 64 + 64, j * 512:(j + 1) * 512], in_=ot[:]),
                         reads=[rot], writes=[('AOT', h, j)], dma=True)


def make_in_maps(inputs, cores=range(8)):
    f = lambda a: np.ascontiguousarray(np.asarray(a, dtype=np.float32))
    sh = {}
    sh['ada_mix_w'] = f(inputs['ada_mix_w'])
    sh['ada_ffn_w'] = f(inputs['ada_ffn_w'])
    amb, afb = f(inputs['ada_mix_b']), f(inputs['ada_ffn_b'])
    sh['ada_b'] = f(np.stack([amb[0], afb[0], amb[1], afb[1]]))
    g1, g2 = f(inputs['ln_mix_g']), f(inputs['ln_ffn_g'])
    b1, b2 = f(inputs['ln_mix_b']), f(inputs['ln_ffn_b'])
    sh['ln_g'] = f(np.stack([g1[0], g2[0], g1[1], g2[1]]))
    sh['ln_b'] = f(np.stack([b1[0], b2[0], b1[1], b2[1]]))
    sh['conv_in_w'] = f(inputs['conv_in_w'][0])
    sh['conv_in_b_l'] = f(np.asarray(inputs['conv_in_b'][0]).reshape(16, 128).T)
    sh['conv_dw_w_l'] = f(np.asarray(inputs['conv_dw_w'][0]).reshape(31, 8, 128).transpose(2, 1, 0))
    sh['conv_vec_l'] = f(np.stack([np.asarray(inputs[k][0]).reshape(8, 128).T for k in ('conv_dw_b', 'conv_ln_g', 'conv_ln_b')], axis=1))
    sh['conv_out_w'] = f(inputs['conv_out_w'][0])
    sh['conv_out_b'] = f(np.asarray(inputs['conv_out_b'][0]).reshape(1, D))
    sh['attn_in_w'] = f(inputs['attn_in_w'][0])
    ab = np.asarray(inputs['attn_in_b'][0])
    sh['attn_qkb_l'] = f(ab[:2 * D].reshape(16, 128).T)
    sh['attn_vb'] = f(ab[2 * D:3 * D].reshape(1, D))
    sh['attn_fb'] = f(ab[3 * D:].reshape(16, 1))
    sh['attn_out_w'] = f(inputs['attn_out_w'][0])
    sh['attn_out_b'] = f(np.asarray(inputs['attn_out_b'][0]).reshape(1, D))
    sh['peer_query_w'] = f(inputs['peer_query_w'])
    k1, k2 = np.asarray(inputs['peer_sub_keys_1']), np.asarray(inputs['peer_sub_keys_2'])
    sh['peer_skT'] = f(np.stack([np.stack([k1[l].T, k2[l].T]) for l in range(2)]))
    sh['peer_u'] = f(inputs['peer_expert_u'])
    sh['peer_v'] = f(inputs['peer_expert_v'])
    x = np.asarray(inputs['x'])
    c = np.asarray(inputs['c'])
    maps = []
    for b in cores:
        m = dict(sh)
        m['x'] = f(x[b])
        m['c_l'] = f(c[b].reshape(8, 128).T)
        maps.append(m)
    return maps


_NC_CACHE = {}


def kernel(**inputs):
    if 'full' not in _NC_CACHE:
        _NC_CACHE['full'] = Kern({}).build()
    nc = _NC_CACHE['full']
    maps = make_in_maps(inputs)
    res = run_bass_kernel_spmd(nc, maps, core_ids=list(range(8)))
    return np.stack([np.asarray(r['out'], dtype=np.float32) for r in res.results], axis=0)
```

```python
import numpy as np
from contextlib import ExitStack
import concourse.bass as bass
import concourse.mybir as mybir
from concourse.bass_utils import run_bass_kernel_spmd

F32 = mybir.dt.float32
I32 = mybir.dt.int32
U32 = mybir.dt.uint32
F32R = mybir.dt.float32r
BF16 = mybir.dt.bfloat16
ALU = mybir.AluOpType
AF = mybir.ActivationFunctionType
AX = mybir.AxisListType

S = 4096
D = 1024
NT = S // 128
ALPHA = float((2 * 2) ** 0.25)
EPS = 1e-5
NEXP = 16384
MAXV = 30000
NEG = -30000.0


class Prog:
    def __init__(self, nc, es):
        self.nc = nc
        self.es = es
        self.eng = {'pe': nc.tensor, 'dve': nc.vector, 'act': nc.scalar,
                    'pool': nc.gpsimd, 'sp': nc.sync}
        self.seq = {e: 0 for e in self.eng}
        self.csem = {e: [] for e in self.eng}
        self.known = {e: {} for e in self.eng}
        self.snap = {}
        self.last_w = {}
        self.readers = {}
        self.semobj = {}
        self.dma_pool = {}
        self.nsem = 0
        self.nwaits = 0
        self.nops = 0
        for q, n in (('sp', 24), ('pool', 24), ('act', 8)):
            self.dma_pool[q] = {'sems': [self._newsem(f"d{q}{i}") for i in range(n)],
                                'cnt': [0] * n, 'next': 0}

    def _newsem(self, name):
        s = self.es.enter_context(self.nc.semaphore(name))
        self.semobj[name] = s
        self.nsem += 1
        return name

    def _need(self, e, tok, skip_self):
        if tok is None:
            return
        name, val, owner = tok
        if skip_self and owner == e:
            return
        if self.known[e].get(name, 0) >= val:
            return
        self.eng[e].wait_ge(self.semobj[name], val)
        self.nwaits += 1
        k = self.known[e]
        k[name] = val
        sn = self.snap.get((name, val))
        if sn:
            for n2, v2 in sn.items():
                if k.get(n2, 0) < v2:
                    k[n2] = v2

    def op(self, e, fn, reads=(), writes=(), dma=False, skip_self=None):
        if skip_self is None:
            skip_self = (e == 'pe')
        if dma:
            skip_self = False
        for r in reads:
            self._need(e, self.last_w.get(r), skip_self)
        for w in writes:
            self._need(e, self.last_w.get(w), skip_self)
            for t in self.readers.get(w, ()):
                self._need(e, t, skip_self)
        self.nops += 1
        if dma:
            pool = self.dma_pool[e]
            i = pool['next']
            pool['next'] = (i + 1) % len(pool['sems'])
            name = pool['sems'][i]
            if pool['cnt'][i] + 16 > MAXV:
                name = self._newsem(f"{name}r{self.nsem}")
                pool['sems'][i] = name
                pool['cnt'][i] = 0
            prev = pool['cnt'][i]
            if prev > 0:
                self._need(e, (name, prev, e + '_dma'), False)
            ins = fn(self.eng[e])
            pool['cnt'][i] = prev + 16
            ins.then_inc(self.semobj[name], 16)
            tok = (name, prev + 16, e + '_dma')
        else:
            n = self.seq[e]
            ep = n // MAXV
            while len(self.csem[e]) <= ep:
                self.csem[e].append(self._newsem(f"c{e}{len(self.csem[e])}"))
            name = self.csem[e][ep]
            ins = fn(self.eng[e])
            ins.then_inc(self.semobj[name], 1)
            self.seq[e] = n + 1
            tok = (name, n - ep * MAXV + 1, e)
        self.snap[(tok[0], tok[1])] = dict(self.known[e])
        for r in reads:
            self.readers.setdefault(r, []).append(tok)
        for w in writes:
            self.last_w[w] = tok
            self.readers[w] = []
        return tok

    def fence(self, e):
        n = self.seq[e]
        if n > 0:
            ep = (n - 1) // MAXV
            self._need(e, (self.csem[e][ep], n - ep * MAXV, e), False)

    def barrier(self):
        toks = []
        for e in self.eng:
            n = self.seq[e]
            if n > 0:
                ep = (n - 1) // MAXV
                toks.append((self.csem[e][ep], n - ep * MAXV, e))
        for q, pool in self.dma_pool.items():
            for name, c in zip(pool['sems'], pool['cnt']):
                if c > 0:
                    toks.append((name, c, q + '_dma'))
        for e in self.eng:
            for t in toks:
                self._need(e, t, False)
        self.last_w.clear()
        self.readers.clear()
        self.snap.clear()


class Stage:
    _n = 0

    def __init__(self, K, name):
        self.K = K
        Stage._n += 1
        self.name = f"{name}{Stage._n}"

    def __enter__(self):
        self.es = ExitStack()
        self.es.__enter__()
        return self

    def T(self, name, shape, dt=F32):
        return self.es.enter_context(self.K.nc.sbuf_tensor(f"{self.name}_{name}", shape, dt))

    def __exit__(self, *a):
        self.K.P.barrier()
        return self.es.__exit__(*a)


class Kern:
    def __init__(self, cfg):
        self.cfg = cfg

    def build(self):
        nc = bass.Bass("TRN2", target_bir_lowering=False)
        self.nc = nc
        dbg = self.cfg.get('debug', False)

        def din(name, shape, dt=F32):
            return nc.dram_tensor(name, list(shape), dt, kind="ExternalInput").ap()

        def dscr(name, shape, dt=F32):
            kind = "ExternalOutput" if (dbg and name in self.cfg.get('expose', ())) else "Internal"
            return nc.dram_tensor(name, list(shape), dt, kind=kind).ap()

        A = {}
        A['x'] = din('x', [S, D])
        A['c_l'] = din('c_l', [128, 8])
        A['ada_mix_w'] = din('ada_mix_w', [2, D, 3 * D])
        A['ada_ffn_w'] = din('ada_ffn_w', [2, D, 3 * D])
        A['ada_b'] = din('ada_b', [4, 3 * D])
        A['ln_g'] = din('ln_g', [4, D])
        A['ln_b'] = din('ln_b', [4, D])
        A['conv_in_w'] = din('conv_in_w', [D, 2 * D])
        A['conv_in_b_l'] = din('conv_in_b_l', [128, 16])
        A['conv_dw_w_l'] = din('conv_dw_w_l', [128, 8, 31])
        A['conv_vec_l'] = din('conv_vec_l', [128, 3, 8])
        A['conv_out_w'] = din('conv_out_w', [D, D])
        A['conv_out_b'] = din('conv_out_b', [1, D])
        A['attn_in_w'] = din('attn_in_w', [D, 3 * D + 16])
        A['attn_qkb_l'] = din('attn_qkb_l', [128, 16])
        A['attn_vb'] = din('attn_vb', [1, D])
        A['attn_fb'] = din('attn_fb', [16, 1])
        A['attn_out_w'] = din('attn_out_w', [D, D])
        A['attn_out_b'] = din('attn_out_b', [1, D])
        A['peer_query_w'] = din('peer_query_w', [2, D, 2 * D])
        A['peer_skT'] = din('peer_skT', [2, 2, 128, 128])
        A['peer_u'] = din('peer_u', [2, NEXP, D])
        A['peer_v'] = din('peer_v', [2, NEXP, D])
        A['out'] = nc.dram_tensor('out', [S, D], F32, kind="ExternalOutput").ap()
        A['X1'] = dscr('X1', [S, D])
        A['X2'] = dscr('X2', [S, D])
        A['X3'] = dscr('X3', [S, D])
        A['ST'] = dscr('ST', [8, 128, S])
        A['IDX'] = dscr('IDX', [S, 128], I32)
        A['SCR'] = dscr('SCR', [S, 2048])
        A['UVB'] = dscr('UVB', [2 * NEXP, 2 * D], BF16)
        A['H'] = dscr('H', [S, D])
        A['GATE'] = dscr('GATE', [S, 128])
        A['QA'] = dscr('QA', [16, 68, S])
        A['KA'] = dscr('KA', [16, 68, S])
        A['V'] = dscr('V', [S, D])
        A['AOT'] = dscr('AOT', [8, 128, S])
        self.A = A

        with ExitStack() as es:
            self.P = P = Prog(nc, es)
            G = lambda name, shape, dt=F32: es.enter_context(nc.sbuf_tensor(name, shape, dt))
            self.ps = [es.enter_context(nc.psum_tensor(f"ps{i}", [128, 512], F32)) for i in range(8)]
            self.ident = G('ident', [128, 128])
            self.ones = G('ones', [128, 128])
            self.SC = G('SC', [128, 8, 128])
            self.shift_bc = G('shift_bc', [128, D])
            self.scale_bc = G('scale_bc', [128, D])
            self.gate_bc = G('gate_bc', [128, D])
            self.g_bc = G('g_bc', [128, D])
            self.b_bc = G('b_bc', [128, D])
            self.bs = G('bs', [128, 2, 6])
            self.mv = G('mv', [128, 2])
            self.rs = G('rs', [128, 1])
            self.emit_globals()
            self._cast_done = False
            if self.cfg.get('peer_bf16', True) and self.cfg.get('cast_stage', False):
                self.emit_cast_tables()
                self._cast_done = True
            order = self.cfg.get('stages', ['conv', 'peer0', 'attn', 'peer1'])
            cur = A['x']
            nxt = {'conv': A['X1'], 'peer0': A['X2'], 'attn': A['X3'], 'peer1': A['out']}
            for i, st in enumerate(order):
                dst = A['out'] if i == len(order) - 1 else nxt[st]
                if st == 'conv':
                    self.emit_adaln(0)
                    self.emit_conv1(cur)
                    self.emit_proj_out(cur, dst, A['conv_out_w'], A['conv_out_b'], src_fm=A['ST'])
                elif st == 'attn':
                    self.emit_adaln(2)
                    self.emit_attn1(cur)
                    self.emit_attn2()
                    self.emit_proj_out(cur, dst, A['attn_out_w'], A['attn_out_b'], src_fm=A['AOT'])
                else:
                    L = int(st[-1])
                    self.emit_adaln(1 + 2 * L)
                    if self.cfg.get('peer_bf16', True):
                        self.emit_peer1a(cur, L, cast=not self._cast_done)
                        self.emit_peer2g(cur, dst, L)
                    elif self.cfg.get('peer_fused', True):
                        self.emit_peer1a(cur, L)
                        self.emit_peer2f(cur, dst, L)
                    else:
                        self.emit_peer1(cur, L)
                        self.emit_peer2(cur, dst, L)
                cur = dst
            P.barrier()
            print(f"[kern] ops={P.nops} waits={P.nwaits} sems={P.nsem} seq={P.seq}")
        return nc

    def emit_globals(self):
        P, nc = self.P, self.nc
        ident, ones = self.ident, self.ones
        P.op('pool', lambda e: e.memset(ident[:], 1.0), writes=['ident'])
        P.op('pool', lambda e: e.affine_select(out=ident[:], in_=ident[:], pattern=[[-1, 128]],
                                               compare_op=ALU.is_equal, fill=0.0, base=0, channel_multiplier=1),
             reads=['ident'], writes=['ident'])
        P.op('pool', lambda e: e.memset(ones[:], 1.0), writes=['ones'])
        with Stage(self, 'gl') as st:
            ct = st.T('ct', [128, 8])
            P.op('sp', lambda e: e.dma_start(out=ct[:], in_=self.A['c_l']), writes=['ct'], dma=True)
            P.op('act', lambda e: e.activation(out=ct[:], in_=ct[:], func=AF.Silu), reads=['ct'], writes=['ct'])
            SC = self.SC
            P.op('dve', lambda e: e.tensor_copy(out=SC[:], in_=ct[:].unsqueeze(2).to_broadcast([128, 8, 128])),
                 reads=['ct'], writes=['SC'])

    def modulate(self, xt, ht, rx, rh):
        P = self.P
        P.op('dve', lambda e: e.tensor_tensor(out=ht, in0=xt, in1=self.scale_bc[:], op=ALU.mult),
             reads=[rx, 'scale_bc'], writes=[rh])
        P.op('pool', lambda e: e.tensor_tensor(out=ht, in0=ht, in1=self.shift_bc[:], op=ALU.add),
             reads=[rh, 'shift_bc'], writes=[rh])

    def transpose8(self, src, rsrc, dstT, rdst, col0, pb):
        P = self.P
        ps = self.ps
        for half in range(2):
            bank = ps[pb + half]
            rb = f'ps{pb + half}'
            for kk in range(4):
                k = half * 4 + kk
                P.op('pe', lambda e, k=k, kk=kk, bank=bank: e.transpose(out=bank[:, kk * 128:(kk + 1) * 128],
                                                                      in_=src[:, k * 128:(k + 1) * 128],
                                                                      identity=self.ident[:]),
                     reads=[rsrc, 'ident'], writes=[rb])
            dst = dstT[:, half * 4:half * 4 + 4, col0:col0 + 128]
            srcp = bank[:].rearrange("p (k n) -> p k n", k=4)
            if half == 0:
                P.op('act', lambda e, dst=dst, srcp=srcp: e.copy(out=dst, in_=srcp), reads=[rb], writes=[rdst])
            else:
                P.op('dve', lambda e, dst=dst, srcp=srcp: e.tensor_copy(out=dst, in_=srcp), reads=[rb], writes=[rdst])

    def layernorm_inplace(self, r, rr, gb='pool'):
        P = self.P
        bs, mv, rs = self.bs, self.mv, self.rs
        for c in range(2):
            P.op('dve', lambda e, c=c: e.bn_stats(out=bs[:, c, :], in_=r[:, c * 512:(c + 1) * 512]),
                 reads=[rr], writes=['bs'])
        P.op('dve', lambda e: e.bn_aggr(out=mv[:], in_=bs[:].rearrange("p a b -> p (a b)")), reads=['bs'], writes=['mv'])
        P.op('dve', lambda e: e.tensor_scalar(out=rs[:], in0=mv[:, 1:2], scalar1=EPS, scalar2=None, op0=ALU.add),
             reads=['mv'], writes=['rs'])
        P.op('act', lambda e: e.activation(out=rs[:], in_=rs[:], func=AF.Sqrt), reads=['rs'], writes=['rs'])
        P.op('dve', lambda e: e.reciprocal(out=rs[:], in_=rs[:]), reads=['rs'], writes=['rs'])
        P.op('dve', lambda e: e.tensor_scalar(out=r, in0=r, scalar1=mv[:, 0:1], scalar2=rs[:, 0:1],
                                              op0=ALU.subtract, op1=ALU.mult), reads=[rr, 'mv', 'rs'], writes=[rr])
        P.op(gb, lambda e: e.tensor_tensor(out=r, in0=r, in1=self.g_bc[:], op=ALU.mult), reads=[rr, 'g_bc'], writes=[rr])
        P.op(gb, lambda e: e.tensor_tensor(out=r, in0=r, in1=self.b_bc[:], op=ALU.add), reads=[rr, 'b_bc'], writes=[rr])

    def emit_adaln(self, sub):
        P, nc, A, ps = self.P, self.nc, self.A, self.ps
        L = sub // 2
        wsrc = (A['ada_mix_w'] if sub % 2 == 0 else A['ada_ffn_w'])[L]
        ones = self.ones
        with Stage(self, f'ada{sub}') as st:
            brow = st.T('brow', [1, 3 * D])
            lrow = st.T('lrow', [1, 2 * D])
            wch = [st.T(f'wch{i}', [128, 8, 512]) for i in range(2)]
            P.op('sp', lambda e: e.dma_start(out=brow[:], in_=A['ada_b'][sub:sub + 1, :]), writes=['brow'], dma=True)
            P.op('sp', lambda e: e.dma_start(out=lrow[:, 0:D], in_=A['ln_g'][sub:sub + 1, :]), writes=['lrow'], dma=True)
            P.op('sp', lambda e: e.dma_start(out=lrow[:, D:2 * D], in_=A['ln_b'][sub:sub + 1, :]), writes=['lrow'], dma=True)
            dsts = [self.shift_bc, self.shift_bc, self.scale_bc, self.scale_bc, self.gate_bc, self.gate_bc]
            names = ['shift_bc', 'shift_bc', 'scale_bc', 'scale_bc', 'gate_bc', 'gate_bc']
            for n6 in range(6):
                wb = wch[n6 % 2]
                rw = f'wch{n6 % 2}'
                P.op('sp', lambda e, wb=wb, n6=n6: e.dma_start(
                    out=wb[:], in_=wsrc[:, n6 * 512:(n6 + 1) * 512].rearrange("(k p) n -> p k n", p=128)),
                    writes=[rw], dma=True)
                bank = ps[n6 % 2]
                rb = f'ps{n6 % 2}'
                for k in range(8):
                    P.op('pe', lambda e, k=k, wb=wb, bank=bank: e.matmul(out=bank[:], lhsT=self.SC[:, k, :], rhs=wb[:, k, :],
                                                                        start=(k == 0), stop=False),
                         reads=['SC', rw], writes=[rb])
                P.op('pe', lambda e, bank=bank, n6=n6: e.matmul(out=bank[:], lhsT=ones[0:1, :], rhs=brow[0:1, n6 * 512:(n6 + 1) * 512],
                                                                start=False, stop=True),
                     reads=['ones', 'brow'], writes=[rb])
                dst = dsts[n6][:, (n6 % 2) * 512:(n6 % 2 + 1) * 512]
                if n6 in (2, 3):
                    P.op('dve', lambda e, dst=dst, bank=bank: e.tensor_scalar(out=dst, in0=bank[:], scalar1=1.0, scalar2=None, op0=ALU.add),
                         reads=[rb], writes=[names[n6]])
                else:
                    P.op('dve', lambda e, dst=dst, bank=bank: e.tensor_copy(out=dst, in_=bank[:]), reads=[rb], writes=[names[n6]])
            for j in range(4):
                bank = ps[2 + j % 2]
                rb = f'ps{2 + j % 2}'
                P.op('pe', lambda e, bank=bank, j=j: e.matmul(out=bank[:], lhsT=ones[0:1, :], rhs=lrow[0:1, j * 512:(j + 1) * 512],
                                                              start=True, stop=True), reads=['ones', 'lrow'], writes=[rb])
                dstt = self.g_bc if j < 2 else self.b_bc
                dst = dstt[:, (j % 2) * 512:(j % 2 + 1) * 512]
                P.op('act', lambda e, dst=dst, bank=bank: e.copy(out=dst, in_=bank[:]), reads=[rb],
                     writes=['g_bc' if j < 2 else 'b_bc'])

    def emit_conv1(self, xin):
        P, nc, A, ps = self.P, self.nc, self.A, self.ps
        ones = self.ones
        with Stage(self, 'c1') as st:
            win = st.T('win', [128, 8, 2048])
            cib = st.T('cib', [128, 16])
            dw = st.T('dw', [128, 8, 31])
            cv = st.T('cv', [128, 3, 8])
            xts = [st.T(f'xt{i}', [128, D]) for i in range(2)]
            ht = st.T('ht', [128, D])
            hT = st.T('hT', [128, 8, 512])
            acc = st.T('acc', [128, 8, 512])
            aexts = [st.T(f'aext{i}', [128, 8, 542]) for i in range(2)]
            sig = [st.T(f'sig{i}', [128, 512]) for i in range(2)]
            sq = [st.T(f'sq{i}', [128, 512]) for i in range(2)]
            meant = st.T('meant', [128, 512])
            rstd = st.T('rstd', [128, 512])
            tmp = st.T('tmp', [128, 512])
            for q in range(4):
                P.op('sp', lambda e, q=q: e.dma_start(out=win[:, :, q * 512:(q + 1) * 512],
                                                      in_=A['conv_in_w'][:, q * 512:(q + 1) * 512].rearrange("(k p) n -> p k n", p=128)),
                     writes=[('win', q)], dma=True)
            P.op('sp', lambda e: e.dma_start(out=cib[:], in_=A['conv_in_b_l']), writes=['cib'], dma=True)
            P.op('sp', lambda e: e.dma_start(out=dw[:], in_=A['conv_dw_w_l']), writes=['dw'], dma=True)
            P.op('sp', lambda e: e.dma_start(out=cv[:], in_=A['conv_vec_l']), writes=['cv'], dma=True)
            for cc in range(8):
                P.op('pool', lambda e, cc=cc: e.memset(aexts[0][:, cc, 0:30], 0.0), writes=[('aext0', cc)])
            accs_ = [acc, st.T('acc2', [128, 8, 512])]

            def load_T(jb):
                for tl in range(4):
                    ti = jb * 4 + tl
                    xt = xts[ti % 2]
                    rx = f'xt{ti % 2}'
                    P.op('sp', lambda e, xt=xt, ti=ti: e.dma_start(out=xt[:], in_=xin[ti * 128:(ti + 1) * 128, :]),
                         writes=[rx], dma=True)
                    self.modulate(xt[:], ht[:], rx, 'ht')
                    self.transpose8(ht, 'ht', hT, 'hT', tl * 128, 0)

            def glu_chunk(jb, cc):
                aext = aexts[jb % 2]
                AX_ = f'aext{jb % 2}'
                pa, pb = ps[2 + (cc % 2) * 2], ps[3 + (cc % 2) * 2]
                ra, rb = f'ps{2 + (cc % 2) * 2}', f'ps{3 + (cc % 2) * 2}'
                for k in range(8):
                    P.op('pe', lambda e, k=k: e.matmul(out=pa[:], lhsT=win[:, k, cc * 128:(cc + 1) * 128], rhs=hT[:, k, :],
                                                       start=(k == 0), stop=(k == 7)),
                         reads=[('win', cc // 4), 'hT'], writes=[ra])
                for k in range(8):
                    P.op('pe', lambda e, k=k: e.matmul(out=pb[:], lhsT=win[:, k, D + cc * 128:D + (cc + 1) * 128], rhs=hT[:, k, :],
                                                       start=(k == 0), stop=(k == 7)),
                         reads=[('win', 2 + cc // 4), 'hT'], writes=[rb])
                sg = sig[cc % 2]
                rsg = f'sig{cc % 2}'
                P.op('act', lambda e: e.activation(out=sg[:], in_=pb[:], func=AF.Sigmoid, bias=cib[:, 8 + cc:9 + cc], scale=1.0),
                     reads=[rb, 'cib'], writes=[rsg])
                P.op('dve', lambda e: e.scalar_tensor_tensor(out=aext[:, cc, 30:542], in0=pa[:], scalar=cib[:, cc:cc + 1], in1=sg[:],
                                                             op0=ALU.add, op1=ALU.mult),
                     reads=[ra, rsg, 'cib'], writes=[(AX_, cc)])

            def conv_chunk(jb, cc):
                aext, anext = aexts[jb % 2], aexts[(jb + 1) % 2]
                AX_, AN_ = f'aext{jb % 2}', f'aext{(jb + 1) % 2}'
                ac = accs_[jb % 2]
                RA = f'acc{jb % 2}'
                P.op('dve', lambda e: e.tensor_scalar(out=ac[:, cc, :], in0=aext[:, cc, 0:512], scalar1=dw[:, cc, 0:1], scalar2=cv[:, 0, cc:cc + 1],
                                                      op0=ALU.mult, op1=ALU.add),
                     reads=[(AX_, cc), 'dw', 'cv'], writes=[(RA, cc)])
                for w in range(1, 31):
                    P.op('dve', lambda e, w=w: e.scalar_tensor_tensor(out=ac[:, cc, :], in0=aext[:, cc, w:w + 512], scalar=dw[:, cc, w:w + 1],
                                                                      in1=ac[:, cc, :], op0=ALU.mult, op1=ALU.add),
                         reads=[(AX_, cc), 'dw', (RA, cc)], writes=[(RA, cc)])
                P.op('act', lambda e: e.copy(out=anext[:, cc, 0:30], in_=aext[:, cc, 512:542]),
                     reads=[(AX_, cc)], writes=[(AN_, cc)])

            def stats_pe(jb):
                ac, RA = accs_[jb % 2], f'acc{jb % 2}'
                for cc in range(8):
                    s2 = sq[cc % 2]
                    rs2 = f'sq{cc % 2}'
                    P.op('act', lambda e, cc=cc, s2=s2: e.activation(out=s2[:], in_=ac[:, cc, :], func=AF.Square),
                         reads=[(RA, cc)], writes=[rs2])
                    P.op('pe', lambda e, cc=cc: e.matmul(out=ps[6][:], lhsT=ones[:], rhs=ac[:, cc, :], start=(cc == 0), stop=(cc == 7)),
                         reads=['ones', (RA, cc)], writes=['ps6'])
                    P.op('pe', lambda e, cc=cc, s2=s2: e.matmul(out=ps[7][:], lhsT=ones[:], rhs=s2[:], start=(cc == 0), stop=(cc == 7)),
                         reads=['ones', rs2], writes=['ps7'])
                P.op('act', lambda e: e.activation(out=meant[:], in_=ps[6][:], func=AF.Copy, scale=1.0 / D), reads=['ps6'], writes=['meant'])

            def stats_dve(jb):
                P.op('dve', lambda e: e.tensor_tensor(out=tmp[:], in0=meant[:], in1=meant[:], op=ALU.mult), reads=['meant'], writes=['tmp'])
                P.op('dve', lambda e: e.scalar_tensor_tensor(out=rstd[:], in0=ps[7][:], scalar=1.0 / D, in1=tmp[:], op0=ALU.mult, op1=ALU.subtract),
                     reads=['ps7', 'tmp'], writes=['rstd'])
                P.op('dve', lambda e: e.tensor_scalar(out=rstd[:], in0=rstd[:], scalar1=EPS, scalar2=None, op0=ALU.add), reads=['rstd'], writes=['rstd'])
                P.op('act', lambda e: e.activation(out=rstd[:], in_=rstd[:], func=AF.Sqrt), reads=['rstd'], writes=['rstd'])
                P.op('dve', lambda e: e.reciprocal(out=rstd[:], in_=rstd[:]), reads=['rstd'], writes=['rstd'])

            def norm_chunk(jb, cc):
                ac, RA = accs_[jb % 2], f'acc{jb % 2}'
                P.op('dve', lambda e: e.tensor_tensor(out=ac[:, cc, :], in0=ac[:, cc, :], in1=meant[:], op=ALU.subtract),
                     reads=[(RA, cc), 'meant'], writes=[(RA, cc)])
                P.op('pool', lambda e: e.tensor_tensor(out=ac[:, cc, :], in0=ac[:, cc, :], in1=rstd[:], op=ALU.mult),
                     reads=[(RA, cc), 'rstd'], writes=[(RA, cc)])
                P.op('act', lambda e: e.activation(out=ac[:, cc, :], in_=ac[:, cc, :], func=AF.Silu,
                                                   bias=cv[:, 2, cc:cc + 1], scale=cv[:, 1, cc:cc + 1]),
                     reads=[(RA, cc), 'cv'], writes=[(RA, cc)])

            def store_block(jb):
                ac, RA = accs_[jb % 2], f'acc{jb % 2}'
                P.op('sp', lambda e: e.dma_start(out=A['ST'][:, :, jb * 512:(jb + 1) * 512].rearrange("c p t -> p c t"), in_=ac[:]),
                     reads=[(RA, cc) for cc in range(8)], writes=[('ST', jb)], dma=True)

            load_T(0)
            for cc in range(8):
                glu_chunk(0, cc)
            for jb in range(8):
                if jb + 1 < 8:
                    load_T(jb + 1)
                for cc in range(8):
                    if jb + 1 < 8:
                        glu_chunk(jb + 1, cc)
                    conv_chunk(jb, cc)
                stats_pe(jb)
                stats_dve(jb)
                for cc in range(8):
                    norm_chunk(jb, cc)
                store_block(jb)

    def emit_proj_out(self, xin, dst, w_ap, b_ap, src_fm=None, src_tm=None):
        P, nc, A, ps = self.P, self.nc, self.A, self.ps
        ones = self.ones
        with Stage(self, 'po') as st:
            wo = st.T('wo', [128, 8, D], F32R)
            wstg = st.T('wstg', [128, 8, 512])
            bo = st.T('bo', [1, D])
            xts = [st.T(f'xt{i}', [128, D]) for i in range(4)]
            rts = [st.T(f'rt{i}', [128, D]) for i in range(4)]
            if src_fm is not None:
                sT = [st.T(f'sT{i}', [128, 8, 512], F32R) for i in range(2)]
                sstg = st.T('sstg', [128, 8, 512])
            else:
                ao = [st.T(f'ao{i}', [128, D]) for i in range(2)]
                aT = [st.T(f'aT{i}', [128, 8, 128]) for i in range(2)]
            for q in range(2):
                P.op('sp', lambda e, q=q: e.dma_start(out=wstg[:], in_=w_ap[:, q * 512:(q + 1) * 512].rearrange("(k p) n -> p k n", p=128)),
                     writes=['wstg'], dma=True)
                P.op('pool', lambda e, q=q: e.tensor_copy(out=wo[:, :, q * 512:(q + 1) * 512], in_=wstg[:]), reads=['wstg'], writes=[('wo', q)])
            P.op('sp', lambda e: e.dma_start(out=bo[:], in_=b_ap), writes=['bo'], dma=True)
            bo_bc = st.T('bo_bc', [128, D])
            for half in range(2):
                P.op('pe', lambda e, half=half: e.matmul(out=ps[half][:], lhsT=ones[0:1, :], rhs=bo[0:1, half * 512:(half + 1) * 512], start=True, stop=True),
                     reads=['ones', 'bo'], writes=[f'ps{half}'])
                P.op('act', lambda e, half=half: e.copy(out=bo_bc[:, half * 512:(half + 1) * 512], in_=ps[half][:]), reads=[f'ps{half}'], writes=['bo_bc'])
            for ti in range(NT):
                xt = xts[ti % 4]
                rx = f'xt{ti % 4}'
                rt = rts[ti % 4]
                rr = f'rt{ti % 4}'
                P.op('sp', lambda e, xt=xt, ti=ti: e.dma_start(out=xt[:], in_=xin[ti * 128:(ti + 1) * 128, :]), writes=[rx], dma=True)
                if src_fm is not None:
                    jb, tl = ti // 4, ti % 4
                    sb = sT[jb % 2]
                    rsb = f'sT{jb % 2}'
                    if tl == 0:
                        P.op('sp', lambda e, jb=jb: e.dma_start(out=sstg[:], in_=src_fm[:, :, jb * 512:(jb + 1) * 512].rearrange("c p t -> p c t")),
                             reads=[('ST', jb)], writes=['sstg'], dma=True)
                        P.op('pool', lambda e, sb=sb: e.tensor_copy(out=sb[:], in_=sstg[:]), reads=['sstg'], writes=[rsb])
                    lhs = lambda k, sb=sb, tl=tl: sb[:, k, tl * 128:(tl + 1) * 128]
                    rl = rsb
                else:
                    a = ao[ti % 2]
                    ra = f'ao{ti % 2}'
                    at = aT[ti % 2]
                    rat = f'aT{ti % 2}'
                    P.op('sp', lambda e, a=a, ti=ti: e.dma_start(out=a[:], in_=src_tm[ti * 128:(ti + 1) * 128, :]), writes=[ra], dma=True)
                    self.transpose8(a, ra, at, rat, 0, 4)
                    lhs = lambda k, at=at: at[:, k, :]
                    rl = rat
                pb = (ti % 4) * 2
                for half in range(2):
                    bank = ps[pb + half]
                    rb = f'ps{pb + half}'
                    for k in range(8):
                        P.op('pe', lambda e, k=k, bank=bank, half=half, lhs=lhs: e.matmul(out=bank[:], lhsT=lhs(k), rhs=wo[:, k, half * 512:(half + 1) * 512],
                                                                                     start=(k == 0), stop=(k == 7)),
                             reads=[rl, ('wo', half)], writes=[rb])
                    P.op('dve', lambda e, bank=bank, half=half, rt=rt: e.tensor_tensor(out=rt[:, half * 512:(half + 1) * 512], in0=bank[:],
                                                                                  in1=bo_bc[:, half * 512:(half + 1) * 512], op=ALU.add),
                         reads=[rb, 'bo_bc'], writes=[rr])
                    P.op('pool', lambda e, half=half, rt=rt: e.tensor_tensor(out=rt[:, half * 512:(half + 1) * 512], in0=rt[:, half * 512:(half + 1) * 512],
                                                                           in1=self.gate_bc[:, half * 512:(half + 1) * 512], op=ALU.mult),
                         reads=[rr, 'gate_bc'], writes=[rr])
                P.op('dve', lambda e, rt=rt, xt=xt: e.scalar_tensor_tensor(out=rt[:], in0=xt[:], scalar=ALPHA, in1=rt[:], op0=ALU.mult, op1=ALU.add),
                     reads=[rx, rr], writes=[rr])
                self.layernorm_inplace(rt[:], rr)
                P.op('sp', lambda e, rt=rt, ti=ti: e.dma_start(out=dst[ti * 128:(ti + 1) * 128, :], in_=rt[:]), reads=[rr], dma=True)

    def emit_peer1(self, xin, L):
        P, nc, A, ps = self.P, self.nc, self.A, self.ps
        with Stage(self, f'p1{L}') as st:
            wq = st.T('wq', [128, 8, 2048])
            skT = st.T('skT', [128, 2, 128])
            xts = [st.T(f'xt{i}', [128, D]) for i in range(2)]
            ht = st.T('ht', [128, D])
            hT = st.T('hT', [128, 8, 256])
            qT = st.T('qT', [128, 16, 256])
            sc = st.T('sc', [128, 16, 128])
            m = st.T('m', [128, 16, 16])
            ix = st.T('ix', [128, 16, 16], U32)
            ixf = st.T('ixf', [128, 16, 16])
            wk = st.T('wk', [128, 16, 128])
            cand = st.T('cand', [128, 8, 256])
            candi = st.T('candi', [128, 8, 256])
            wk2 = st.T('wk2', [128, 8, 256])
            junk = [st.T(f'junk{i}', [128, 256]) for i in range(2)]
            ts = st.T('ts', [128, 8, 16])
            ef = st.T('ef', [128, 128])
            ei = [st.T(f'ei{i}', [128, 128], I32) for i in range(2)]
            gt = [st.T(f'gt{i}', [128, 8, 16]) for i in range(2)]
            gsum = st.T('gsum', [128, 8])
            for q in range(4):
                P.op('sp', lambda e, q=q: e.dma_start(out=wq[:, :, q * 512:(q + 1) * 512],
                                                      in_=A['peer_query_w'][L][:, q * 512:(q + 1) * 512].rearrange("(k p) n -> p k n", p=128)),
                     writes=[('wq', q)], dma=True)
            P.op('sp', lambda e: e.dma_start(out=skT[:], in_=A['peer_skT'][L].rearrange("h d k -> d h k")), writes=['skT'], dma=True)
            ti = 0
            for jb in range(S // 256):
                for tl in range(2):
                    xt = xts[ti % 2]
                    rx = f'xt{ti % 2}'
                    P.op('sp', lambda e, xt=xt, ti=ti: e.dma_start(out=xt[:], in_=xin[ti * 128:(ti + 1) * 128, :]), writes=[rx], dma=True)
                    self.modulate(xt[:], ht[:], rx, 'ht')
                    self.transpose8(ht, 'ht', hT, 'hT', tl * 128, 0)
                    ti += 1
                for c in range(16):
                    bank = ps[2 + c % 2]
                    rb = f'ps{2 + c % 2}'
                    for k in range(8):
                        P.op('pe', lambda e, k=k, c=c, bank=bank: e.matmul(out=bank[:, 0:256], lhsT=wq[:, k, c * 128:(c + 1) * 128], rhs=hT[:, k, :],
                                                                          start=(k == 0), stop=(k == 7)),
                             reads=[('wq', c // 4), 'hT'], writes=[rb])
                    if c % 2 == 0:
                        P.op('act', lambda e, c=c, bank=bank: e.copy(out=qT[:, c, :], in_=bank[:, 0:256]), reads=[rb], writes=[('qT', c)])
                    else:
                        P.op('dve', lambda e, c=c, bank=bank: e.tensor_copy(out=qT[:, c, :], in_=bank[:, 0:256]), reads=[rb], writes=[('qT', c)])
                for tl in range(2):
                    tix = jb * 2 + tl
                    for c in range(16):
                        bank = ps[4 + c // 4]
                        rb = f'ps{4 + c // 4}'
                        P.op('pe', lambda e, c=c, bank=bank, tl=tl: e.matmul(out=bank[:, (c % 4) * 128:(c % 4 + 1) * 128],
                                                                            lhsT=qT[:, c, tl * 128:(tl + 1) * 128], rhs=skT[:, c % 2, :],
                                                                            start=True, stop=True),
                             reads=[('qT', c), 'skT'], writes=[rb])
                    for g4 in range(4):
                        P.op('act', lambda e, g4=g4: e.copy(out=sc[:, g4 * 4:(g4 + 1) * 4, :], in_=ps[4 + g4][:].rearrange("p (a k) -> p a k", a=4)),
                             reads=[f'ps{4 + g4}'], writes=['sc'])
                    for c in range(16):
                        P.op('dve', lambda e, c=c: e.max(out=m[:, c, 0:8], in_=sc[:, c, :]), reads=['sc'], writes=[('m0', c)])
                    P.fence('dve')
                    for c in range(16):
                        P.op('dve', lambda e, c=c: e.max_index(out=ix[:, c, 0:8], in_max=m[:, c, 0:8], in_values=sc[:, c, :]),
                             reads=['sc', ('m0', c)], writes=[('ix0', c)])
                        P.op('dve', lambda e, c=c: e.match_replace(out=wk[:, c, :], in_to_replace=m[:, c, 0:8], in_values=sc[:, c, :], imm_value=-1e30),
                             reads=['sc', ('m0', c)], writes=[('wk', c)])
                    P.fence('dve')
                    for c in range(16):
                        P.op('dve', lambda e, c=c: e.max(out=m[:, c, 8:16], in_=wk[:, c, :]), reads=[('wk', c)], writes=[('m1', c)])
                    P.fence('dve')
                    for c in range(16):
                        P.op('dve', lambda e, c=c: e.max_index(out=ix[:, c, 8:16], in_max=m[:, c, 8:16], in_values=wk[:, c, :]),
                             reads=[('wk', c), ('m1', c)], writes=[('ix1', c)])
                    P.fence('dve')
                    mres = [('m0', c) for c in range(16)] + [('m1', c) for c in range(16)]
                    ixres = [('ix0', c) for c in range(16)] + [('ix1', c) for c in range(16)]
                    P.op('dve', lambda e: e.tensor_copy(out=ixf[:], in_=ix[:]), reads=ixres, writes=['ixf'])
                    m4 = m[:].rearrange("p (h two) k -> p h two k", two=2)
                    i4 = ixf[:].rearrange("p (h two) k -> p h two k", two=2)
                    c4 = cand[:].rearrange("p h (a b) -> p h a b", a=16)
                    ci4 = candi[:].rearrange("p h (a b) -> p h a b", a=16)
                    P.op('dve', lambda e: e.tensor_tensor(out=c4, in0=m4[:, :, 0, :].unsqueeze(3).to_broadcast([128, 8, 16, 16]),
                                                          in1=m4[:, :, 1, :].unsqueeze(2).to_broadcast([128, 8, 16, 16]), op=ALU.add),
                         reads=mres, writes=['cand'])
                    P.op('dve', lambda e: e.tensor_scalar(out=i4[:, :, 0, :], in0=i4[:, :, 0, :], scalar1=128.0, scalar2=None, op0=ALU.mult),
                         reads=['ixf'], writes=['ixf'])
                    P.op('dve', lambda e: e.tensor_tensor(out=ci4, in0=i4[:, :, 0, :].unsqueeze(3).to_broadcast([128, 8, 16, 16]),
                                                          in1=i4[:, :, 1, :].unsqueeze(2).to_broadcast([128, 8, 16, 16]), op=ALU.add),
                         reads=['ixf'], writes=['candi'])
                    for h in range(8):
                        P.op('dve', lambda e, h=h: e.max(out=ts[:, h, 0:8], in_=cand[:, h, :]), reads=['cand'], writes=[('ts0', h)])
                    P.fence('dve')
                    for h in range(8):
                        P.op('dve', lambda e, h=h: e.match_replace(out=wk2[:, h, :], in_to_replace=ts[:, h, 0:8], in_values=cand[:, h, :], imm_value=-1e30),
                             reads=['cand', ('ts0', h)], writes=[('wk2', h)])
                    P.fence('dve')
                    for h in range(8):
                        P.op('dve', lambda e, h=h: e.max(out=ts[:, h, 8:16], in_=wk2[:, h, :]), reads=[('wk2', h)], writes=[('ts1', h)])
                    P.fence('dve')
                    tsres = [('ts0', h) for h in range(8)] + [('ts1', h) for h in range(8)]
                    for h in range(8):
                        for k in range(16):
                            P.op('dve', lambda e, h=h, k=k: e.scalar_tensor_tensor(out=junk[(h * 16 + k) % 2][:], in0=cand[:, h, :], scalar=ts[:, h, k:k + 1], in1=candi[:, h, :],
                                                                                   op0=ALU.is_equal, op1=ALU.mult, accum_out=ef[:, h * 16 + k:h * 16 + k + 1]),
                                 reads=['cand', 'candi', ('ts0', h), ('ts1', h)], writes=[('ef', h * 16 + k)])
                    P.fence('dve')
                    eib = ei[tix % 2]
                    rei = f'ei{tix % 2}'
                    gtb = gt[tix % 2]
                    rgt = f'gt{tix % 2}'
                    P.op('dve', lambda e: e.tensor_scalar(out=ef[:], in0=ef[:], scalar1=float(NEXP - 1), scalar2=float(L * NEXP), op0=ALU.min, op1=ALU.add),
                         reads=[('ef', q) for q in range(128)], writes=['ef'])
                    P.op('dve', lambda e, eib=eib: e.tensor_copy(out=eib[:], in_=ef[:]), reads=['ef'], writes=[rei])
                    P.op('dve', lambda e, gtb=gtb: e.tensor_tensor(out=gtb[:], in0=ts[:], in1=ts[:, :, 0:1].to_broadcast([128, 8, 16]), op=ALU.subtract),
                         reads=tsres, writes=[rgt])
                    P.op('act', lambda e, gtb=gtb: e.activation(out=gtb[:], in_=gtb[:], func=AF.Exp), reads=[rgt], writes=[rgt])
                    P.op('dve', lambda e, gtb=gtb: e.tensor_reduce(out=gsum[:], in_=gtb[:], axis=AX.X, op=ALU.add), reads=[rgt], writes=['gsum'])
                    P.op('dve', lambda e: e.reciprocal(out=gsum[:], in_=gsum[:]), reads=['gsum'], writes=['gsum'])
                    P.op('dve', lambda e, gtb=gtb: e.tensor_tensor(out=gtb[:], in0=gtb[:], in1=gsum[:].unsqueeze(2).to_broadcast([128, 8, 16]), op=ALU.mult),
                         reads=[rgt, 'gsum'], writes=[rgt])
                    P.op('sp', lambda e, eib=eib, tix=tix: e.dma_start(out=A['IDX'][tix * 128:(tix + 1) * 128, :], in_=eib[:]),
                         reads=[rei], writes=[('IDX', tix)], dma=True)
                    P.op('sp', lambda e, gtb=gtb, tix=tix: e.dma_start(out=A['GATE'][tix * 128:(tix + 1) * 128, :], in_=gtb[:].rearrange("p h k -> p (h k)")),
                         reads=[rgt], writes=[('GATE', tix)], dma=True)

    def emit_peer2(self, xin, dst, L):
        P, nc, A, ps = self.P, self.nc, self.A, self.ps
        NB = self.cfg.get('nb', 14)
        U = A['peer_u'].rearrange("l e d -> (l e) d")
        V = A['peer_v'].rearrange("l e d -> (l e) d")
        with Stage(self, f'p2{L}') as st:
            xts = [st.T(f'xt{i}', [128, D]) for i in range(2)]
            hts = [st.T(f'ht{i}', [128, D]) for i in range(2)]
            eis = [st.T(f'ei{i}', [128, 128], I32) for i in range(2)]
            gts = [st.T(f'gt{i}', [128, 128]) for i in range(2)]
            ub = [st.T(f'ub{i}', [128, D]) for i in range(NB)]
            vb = [st.T(f'vb{i}', [128, D]) for i in range(NB)]
            junk = st.T('junk', [128, D])
            apre = st.T('apre', [128, 128])
            coef = st.T('coef', [128, 128])
            accs = [st.T(f'acc{i}', [128, D]) for i in range(2)]
            nu = nv = 0
            for ti in range(NT):
                b = ti % 2
                xt, ht, eib, gtb, acc = xts[b], hts[b], eis[b], gts[b], accs[b]
                rx, rh, rei, rgt, racc = f'xt{b}', f'ht{b}', f'ei{b}', f'gt{b}', f'acc{b}'
                P.op('sp', lambda e, xt=xt, ti=ti: e.dma_start(out=xt[:], in_=xin[ti * 128:(ti + 1) * 128, :]), writes=[rx], dma=True)
                P.op('sp', lambda e, eib=eib, ti=ti: e.dma_start(out=eib[:], in_=A['IDX'][ti * 128:(ti + 1) * 128, :]),
                     reads=[('IDX', ti)], writes=[rei], dma=True)
                P.op('sp', lambda e, gtb=gtb, ti=ti: e.dma_start(out=gtb[:], in_=A['GATE'][ti * 128:(ti + 1) * 128, :]),
                     reads=[('GATE', ti)], writes=[rgt], dma=True)
                self.modulate(xt[:], ht[:], rx, rh)
                for k in range(128):
                    s = nu % NB
                    nu += 1
                    P.op('pool', lambda e, s=s, k=k, eib=eib: e.indirect_dma_start(
                        out=ub[s][:], out_offset=None, in_=U,
                        in_offset=bass.IndirectOffsetOnAxis(ap=eib[:, k:k + 1], axis=0)),
                        reads=[rei], writes=[('ub', s)], dma=True)
                    P.op('dve', lambda e, s=s, k=k, ht=ht: e.scalar_tensor_tensor(out=junk[:], in0=ub[s][:], scalar=1.0, in1=ht[:], op0=ALU.mult, op1=ALU.mult,
                                                                               accum_out=apre[:, k:k + 1]),
                         reads=[('ub', s), rh], writes=['junk', 'apre'])
                P.op('act', lambda e: e.activation(out=coef[:], in_=apre[:], func=AF.Gelu), reads=['apre'], writes=['coef'])
                P.op('dve', lambda e, gtb=gtb: e.tensor_tensor(out=coef[:], in0=coef[:], in1=gtb[:], op=ALU.mult), reads=['coef', rgt], writes=['coef'])
                for k in range(128):
                    s = nv % NB
                    nv += 1
                    P.op('pool', lambda e, s=s, k=k, eib=eib: e.indirect_dma_start(
                        out=vb[s][:], out_offset=None, in_=V,
                        in_offset=bass.IndirectOffsetOnAxis(ap=eib[:, k:k + 1], axis=0)),
                        reads=[rei], writes=[('vb', s)], dma=True)
                    if k == 0:
                        P.op('dve', lambda e, s=s, acc=acc: e.tensor_scalar(out=acc[:], in0=vb[s][:], scalar1=coef[:, 0:1], scalar2=None, op0=ALU.mult),
                             reads=[('vb', s), 'coef'], writes=[racc])
                    else:
                        P.op('dve', lambda e, s=s, k=k, acc=acc: e.scalar_tensor_tensor(out=acc[:], in0=vb[s][:], scalar=coef[:, k:k + 1], in1=acc[:],
                                                                                      op0=ALU.mult, op1=ALU.add),
                             reads=[('vb', s), 'coef', racc], writes=[racc])
                P.op('pool', lambda e, acc=acc: e.tensor_tensor(out=acc[:], in0=acc[:], in1=self.gate_bc[:], op=ALU.mult), reads=[racc, 'gate_bc'], writes=[racc])
                P.op('dve', lambda e, acc=acc, xt=xt: e.scalar_tensor_tensor(out=acc[:], in0=xt[:], scalar=ALPHA, in1=acc[:], op0=ALU.mult, op1=ALU.add),
                     reads=[rx, racc], writes=[racc])
                self.layernorm_inplace(acc[:], racc)
                P.op('sp', lambda e, acc=acc, ti=ti: e.dma_start(out=dst[ti * 128:(ti + 1) * 128, :], in_=acc[:]), reads=[racc], dma=True)

    def emit_peer1a(self, xin, L, cast=False):
        P, nc, A, ps = self.P, self.nc, self.A, self.ps
        with Stage(self, f'pa{L}') as st:
            cg = self.cast_gen(st, L) if cast else None
            wq = st.T('wq', [128, 8, 2048])
            skT = st.T('skT', [128, 2, 128])
            xts = [st.T(f'xt{i}', [128, D]) for i in range(2)]
            hts = [st.T(f'ht{i}', [128, D]) for i in range(2)]
            hT = st.T('hT', [128, 8, 256])
            qT = st.T('qT', [128, 16, 256])
            sco = [st.T(f'sco{i}', [128, 2048]) for i in range(2)]
            for q in range(4):
                P.op('sp', lambda e, q=q: e.dma_start(out=wq[:, :, q * 512:(q + 1) * 512],
                                                      in_=A['peer_query_w'][L][:, q * 512:(q + 1) * 512].rearrange("(k p) n -> p k n", p=128)),
                     writes=[('wq', q)], dma=True)
            P.op('sp', lambda e: e.dma_start(out=skT[:], in_=A['peer_skT'][L].rearrange("h d k -> d h k")), writes=['skT'], dma=True)
            ti = 0
            for jb in range(S // 256):
                for tl in range(2):
                    xt, ht = xts[ti % 2], hts[ti % 2]
                    rx, rh = f'xt{ti % 2}', f'ht{ti % 2}'
                    P.op('sp', lambda e, xt=xt, ti=ti: e.dma_start(out=xt[:], in_=xin[ti * 128:(ti + 1) * 128, :]), writes=[rx], dma=True)
                    self.modulate(xt[:], ht[:], rx, rh)
                    P.op('sp', lambda e, ht=ht, ti=ti: e.dma_start(out=A['H'][ti * 128:(ti + 1) * 128, :], in_=ht[:]), reads=[rh], writes=[('H', ti)], dma=True)
                    self.transpose8(ht, rh, hT, 'hT', tl * 128, 0)
                    ti += 1
                for c in range(16):
                    bank = ps[2 + c % 2]
                    rb = f'ps{2 + c % 2}'
                    for k in range(8):
                        P.op('pe', lambda e, k=k, c=c, bank=bank: e.matmul(out=bank[:, 0:256], lhsT=wq[:, k, c * 128:(c + 1) * 128], rhs=hT[:, k, :],
                                                                          start=(k == 0), stop=(k == 7)),
                             reads=[('wq', c // 4), 'hT'], writes=[rb])
                    if c % 2 == 0:
                        P.op('act', lambda e, c=c, bank=bank: e.copy(out=qT[:, c, :], in_=bank[:, 0:256]), reads=[rb], writes=[('qT', c)])
                    else:
                        P.op('dve', lambda e, c=c, bank=bank: e.tensor_copy(out=qT[:, c, :], in_=bank[:, 0:256]), reads=[rb], writes=[('qT', c)])
                for tl in range(2):
                    tix = jb * 2 + tl
                    so = sco[tix % 2]
                    rso = f'sco{tix % 2}'
                    for c in range(16):
                        bank = ps[4 + c // 4]
                        rb = f'ps{4 + c // 4}'
                        P.op('pe', lambda e, c=c, bank=bank, tl=tl: e.matmul(out=bank[:, (c % 4) * 128:(c % 4 + 1) * 128],
                                                                            lhsT=qT[:, c, tl * 128:(tl + 1) * 128], rhs=skT[:, c % 2, :],
                                                                            start=True, stop=True),
                             reads=[('qT', c), 'skT'], writes=[rb])
                    for g4 in range(4):
                        eng = ('act', 'dve')[g4 % 2]
                        if eng == 'act':
                            P.op('act', lambda e, g4=g4, so=so: e.copy(out=so[:, g4 * 512:(g4 + 1) * 512], in_=ps[4 + g4][:]), reads=[f'ps{4 + g4}'], writes=[rso])
                        else:
                            P.op('dve', lambda e, g4=g4, so=so: e.tensor_copy(out=so[:, g4 * 512:(g4 + 1) * 512], in_=ps[4 + g4][:]), reads=[f'ps{4 + g4}'], writes=[rso])
                    P.op('sp', lambda e, so=so, tix=tix: e.dma_start(out=A['SCR'][tix * 128:(tix + 1) * 128, :], in_=so[:]), reads=[rso], writes=[('SCR', tix)], dma=True)
                    if cg is not None:
                        for _ in range(2):
                            try:
                                next(cg)
                            except StopIteration:
                                cg = None
                                break
            while cg is not None:
                try:
                    next(cg)
                except StopIteration:
                    cg = None

    def emit_peer2f(self, xin, dst, L):
        P, nc, A, ps = self.P, self.nc, self.A, self.ps
        NB = self.cfg.get('nb', 11)
        U = A['peer_u'].rearrange("l e d -> (l e) d")
        V = A['peer_v'].rearrange("l e d -> (l e) d")
        with Stage(self, f'pf{L}') as st:
            scs = [st.T(f'sc{i}', [128, 16, 128]) for i in range(2)]
            m = st.T('m', [128, 16, 16])
            ix = st.T('ix', [128, 16, 16], U32)
            ixf = st.T('ixf', [128, 16, 16])
            wk = st.T('wk', [128, 16, 128])
            cand = st.T('cand', [128, 8, 256])
            candi = st.T('candi', [128, 8, 256])
            wk2 = st.T('wk2', [128, 8, 256])
            junk2 = [st.T(f'junk2{i}', [128, 256]) for i in range(2)]
            ts = st.T('ts', [128, 8, 16])
            ef = st.T('ef', [128, 128])
            eis = [st.T(f'ei{i}', [128, 128], I32) for i in range(2)]
            gts = [st.T(f'gt{i}', [128, 8, 16]) for i in range(2)]
            gsum = st.T('gsum', [128, 8])
            xts = [st.T(f'xt{i}', [128, D]) for i in range(2)]
            hts = [st.T(f'ht{i}', [128, D]) for i in range(2)]
            ub = [st.T(f'ub{i}', [128, D]) for i in range(NB)]
            vb = [st.T(f'vb{i}', [128, D]) for i in range(NB)]
            junk = st.T('junk', [128, D])
            apre = st.T('apre', [128, 128])
            coef = st.T('coef', [128, 128])
            accs = [st.T(f'acc{i}', [128, D]) for i in range(2)]

            def load_sc(t):
                P.op('sp', lambda e, t=t: e.dma_start(out=scs[t % 2][:].rearrange("p a k -> p (a k)"), in_=A['SCR'][t * 128:(t + 1) * 128, :]),
                     writes=[f'sc{t % 2}'], dma=True)

            def load_xh(t):
                P.op('sp', lambda e, t=t: e.dma_start(out=xts[t % 2][:], in_=xin[t * 128:(t + 1) * 128, :]), writes=[f'xt{t % 2}'], dma=True)
                P.op('sp', lambda e, t=t: e.dma_start(out=hts[t % 2][:], in_=A['H'][t * 128:(t + 1) * 128, :]), writes=[f'ht{t % 2}'], dma=True)

            def topk_gen(t):
                sc = scs[t % 2]
                rsc = f'sc{t % 2}'
                eib, gtb = eis[t % 2], gts[t % 2]
                rei, rgt = f'ei{t % 2}', f'gt{t % 2}'
                for c in range(16):
                    P.op('dve', lambda e, c=c: e.max(out=m[:, c, 0:8], in_=sc[:, c, :]), reads=[rsc], writes=[('m0', c)])
                    yield
                P.fence('dve')
                for c in range(16):
                    P.op('dve', lambda e, c=c: e.max_index(out=ix[:, c, 0:8], in_max=m[:, c, 0:8], in_values=sc[:, c, :]),
                         reads=[rsc, ('m0', c)], writes=[('ix0', c)])
                    yield
                    P.op('dve', lambda e, c=c: e.match_replace(out=wk[:, c, :], in_to_replace=m[:, c, 0:8], in_values=sc[:, c, :], imm_value=-1e30),
                         reads=[rsc, ('m0', c)], writes=[('wk', c)])
                    yield
                P.fence('dve')
                for c in range(16):
                    P.op('dve', lambda e, c=c: e.max(out=m[:, c, 8:16], in_=wk[:, c, :]), reads=[('wk', c)], writes=[('m1', c)])
                    yield
                P.fence('dve')
                for c in range(16):
                    P.op('dve', lambda e, c=c: e.max_index(out=ix[:, c, 8:16], in_max=m[:, c, 8:16], in_values=wk[:, c, :]),
                         reads=[('wk', c), ('m1', c)], writes=[('ix1', c)])
                    yield
                P.fence('dve')
                mres = [('m0', c) for c in range(16)] + [('m1', c) for c in range(16)]
                ixres = [('ix0', c) for c in range(16)] + [('ix1', c) for c in range(16)]
                P.op('dve', lambda e: e.tensor_copy(out=ixf[:], in_=ix[:]), reads=ixres, writes=['ixf'])
                yield
                m4 = m[:].rearrange("p (h two) k -> p h two k", two=2)
                i4 = ixf[:].rearrange("p (h two) k -> p h two k", two=2)
                c4 = cand[:].rearrange("p h (a b) -> p h a b", a=16)
                ci4 = candi[:].rearrange("p h (a b) -> p h a b", a=16)
                P.op('dve', lambda e: e.tensor_tensor(out=c4, in0=m4[:, :, 0, :].unsqueeze(3).to_broadcast([128, 8, 16, 16]),
                                                      in1=m4[:, :, 1, :].unsqueeze(2).to_broadcast([128, 8, 16, 16]), op=ALU.add),
                     reads=mres, writes=['cand'])
                yield
                P.op('dve', lambda e: e.tensor_scalar(out=i4[:, :, 0, :], in0=i4[:, :, 0, :], scalar1=128.0, scalar2=None, op0=ALU.mult),
                     reads=['ixf'], writes=['ixf'])
                yield
                P.op('dve', lambda e: e.tensor_tensor(out=ci4, in0=i4[:, :, 0, :].unsqueeze(3).to_broadcast([128, 8, 16, 16]),
                                                      in1=i4[:, :, 1, :].unsqueeze(2).to_broadcast([128, 8, 16, 16]), op=ALU.add),
                     reads=['ixf'], writes=['candi'])
                yield
                for h in range(8):
                    P.op('dve', lambda e, h=h: e.max(out=ts[:, h, 0:8], in_=cand[:, h, :]), reads=['cand'], writes=[('ts0', h)])
                    yield
                P.fence('dve')
                for h in range(8):
                    P.op('dve', lambda e, h=h: e.match_replace(out=wk2[:, h, :], in_to_replace=ts[:, h, 0:8], in_values=cand[:, h, :], imm_value=-1e30),
                         reads=['cand', ('ts0', h)], writes=[('wk2', h)])
                    yield
                P.fence('dve')
                for h in range(8):
                    P.op('dve', lambda e, h=h: e.max(out=ts[:, h, 8:16], in_=wk2[:, h, :]), reads=[('wk2', h)], writes=[('ts1', h)])
                    yield
                P.fence('dve')
                tsres = [('ts0', h) for h in range(8)] + [('ts1', h) for h in range(8)]
                for h in range(8):
                    for k in range(16):
                        P.op('dve', lambda e, h=h, k=k: e.scalar_tensor_tensor(out=junk2[(h * 16 + k) % 2][:], in0=cand[:, h, :], scalar=ts[:, h, k:k + 1], in1=candi[:, h, :],
                                                                               op0=ALU.is_equal, op1=ALU.mult, accum_out=ef[:, h * 16 + k:h * 16 + k + 1]),
                             reads=['cand', 'candi', ('ts0', h), ('ts1', h)], writes=[('ef', h * 16 + k)])
                        yield
                P.fence('dve')
                P.op('dve', lambda e: e.tensor_scalar(out=ef[:], in0=ef[:], scalar1=float(NEXP - 1), scalar2=float(L * NEXP), op0=ALU.min, op1=ALU.add),
                     reads=[('ef', q) for q in range(128)], writes=['ef'])
                yield
                P.op('dve', lambda e: e.tensor_copy(out=eib[:], in_=ef[:]), reads=['ef'], writes=[rei])
                yield
                P.op('dve', lambda e: e.tensor_tensor(out=gtb[:], in0=ts[:], in1=ts[:, :, 0:1].to_broadcast([128, 8, 16]), op=ALU.subtract),
                     reads=tsres, writes=[rgt])
                yield
                P.op('act', lambda e: e.activation(out=gtb[:], in_=gtb[:], func=AF.Exp), reads=[rgt], writes=[rgt])
                P.op('dve', lambda e: e.tensor_reduce(out=gsum[:], in_=gtb[:], axis=AX.X, op=ALU.add), reads=[rgt], writes=['gsum'])
                yield
                P.op('dve', lambda e: e.reciprocal(out=gsum[:], in_=gsum[:]), reads=['gsum'], writes=['gsum'])
                yield
                P.op('dve', lambda e: e.tensor_tensor(out=gtb[:], in0=gtb[:], in1=gsum[:].unsqueeze(2).to_broadcast([128, 8, 16]), op=ALU.mult),
                     reads=[rgt, 'gsum'], writes=[rgt])
                yield

            def step(gen, n=1):
                if gen is None:
                    return None
                try:
                    for _ in range(n):
                        next(gen)
                except StopIteration:
                    return None
                return gen

            load_sc(0)
            load_sc(1)
            load_xh(0)
            g0 = topk_gen(0)
            while g0 is not None:
                g0 = step(g0, 64)
            nu = nv = 0
            for ti in range(NT):
                b = ti % 2
                xt, ht, eib, gtb, acc = xts[b], hts[b], eis[b], gts[b], accs[b]
                rx, rh, rei, rgt, racc = f'xt{b}', f'ht{b}', f'ei{b}', f'gt{b}', f'acc{b}'
                gt2 = gtb[:].rearrange("p h k -> p (h k)")
                if ti + 1 < NT:
                    load_xh(ti + 1)
                gen = topk_gen(ti + 1) if ti + 1 < NT else None
                for k in range(128):
                    s_ = nu % NB
                    nu += 1
                    P.op('pool', lambda e, s_=s_, k=k, eib=eib: e.indirect_dma_start(
                        out=ub[s_][:], out_offset=None, in_=U,
                        in_offset=bass.IndirectOffsetOnAxis(ap=eib[:, k:k + 1], axis=0)),
                        reads=[rei], writes=[('ub', s_)], dma=True)
                    P.op('dve', lambda e, s_=s_, k=k, ht=ht: e.scalar_tensor_tensor(out=junk[:], in0=ub[s_][:], scalar=1.0, in1=ht[:], op0=ALU.mult, op1=ALU.mult,
                                                                                 accum_out=apre[:, k:k + 1]),
                         reads=[('ub', s_), rh], writes=[('apre', k)])
                    gen = step(gen)
                P.op('act', lambda e: e.activation(out=coef[:], in_=apre[:], func=AF.Gelu), reads=[('apre', k) for k in range(128)], writes=['coef'])
                P.op('dve', lambda e, gt2=gt2: e.tensor_tensor(out=coef[:], in0=coef[:], in1=gt2, op=ALU.mult), reads=['coef', rgt], writes=['coef'])
                for k in range(128):
                    s_ = nv % NB
                    nv += 1
                    P.op('pool', lambda e, s_=s_, k=k, eib=eib: e.indirect_dma_start(
                        out=vb[s_][:], out_offset=None, in_=V,
                        in_offset=bass.IndirectOffsetOnAxis(ap=eib[:, k:k + 1], axis=0)),
                        reads=[rei], writes=[('vb', s_)], dma=True)
                    if k == 0:
                        P.op('dve', lambda e, s_=s_, acc=acc: e.tensor_scalar(out=acc[:], in0=vb[s_][:], scalar1=coef[:, 0:1], scalar2=None, op0=ALU.mult),
                             reads=[('vb', s_), 'coef'], writes=[racc])
                    else:
                        P.op('dve', lambda e, s_=s_, k=k, acc=acc: e.scalar_tensor_tensor(out=acc[:], in0=vb[s_][:], scalar=coef[:, k:k + 1], in1=acc[:],
                                                                                       op0=ALU.mult, op1=ALU.add),
                             reads=[('vb', s_), 'coef', racc], writes=[racc])
                    gen = step(gen)
                while gen is not None:
                    gen = step(gen, 64)
                if ti + 2 < NT:
                    load_sc(ti + 2)
                P.op('dve', lambda e, acc=acc: e.tensor_tensor(out=acc[:], in0=acc[:], in1=self.gate_bc[:], op=ALU.mult), reads=[racc, 'gate_bc'], writes=[racc])
                P.op('dve', lambda e, acc=acc, xt=xt: e.scalar_tensor_tensor(out=acc[:], in0=xt[:], scalar=ALPHA, in1=acc[:], op0=ALU.mult, op1=ALU.add),
                     reads=[rx, racc], writes=[racc])
                self.layernorm_inplace(acc[:], racc, gb='dve')
                P.op('sp', lambda e, acc=acc, ti=ti: e.dma_start(out=dst[ti * 128:(ti + 1) * 128, :], in_=acc[:]), reads=[racc], dma=True)

    def cast_gen(self, st, L, CN=2):
        P, nc, A = self.P, self.nc, self.A
        Uv = A['peer_u'][L].rearrange("(n p) d -> p n d", p=128)
        Vv = A['peer_v'][L].rearrange("(n p) d -> p n d", p=128)
        Ov = A['UVB'][L * NEXP:(L + 1) * NEXP, :].rearrange("(n p) d -> p n d", p=128)
        iu = [st.T(f'ciu{i}', [128, CN, D]) for i in range(2)]
        iv = [st.T(f'civ{i}', [128, CN, D]) for i in range(2)]
        ob = [st.T(f'cob{i}', [128, CN, 2 * D], BF16) for i in range(2)]
        nchunk = (NEXP // 128) // CN

        def ld(ci):
            b = ci % 2
            ns = slice(ci * CN, (ci + 1) * CN)
            P.op(self.cfg.get('cast_q', 'pool'), lambda e: e.dma_start(out=iu[b][:], in_=Uv[:, ns, :]), writes=[f'ciu{b}'], dma=True)
            P.op(self.cfg.get('cast_q', 'pool'), lambda e: e.dma_start(out=iv[b][:], in_=Vv[:, ns, :]), writes=[f'civ{b}'], dma=True)

        ld(0)
        for ci in range(nchunk):
            b = ci % 2
            ns = slice(ci * CN, (ci + 1) * CN)
            if ci + 1 < nchunk:
                ld(ci + 1)
            P.op('pool', lambda e: e.tensor_copy(out=ob[b][:, :, 0:D], in_=iu[b][:]), reads=[f'ciu{b}'], writes=[(f'cob{b}', 0)])
            P.op('pool', lambda e: e.tensor_copy(out=ob[b][:, :, D:2 * D], in_=iv[b][:]), reads=[f'civ{b}'], writes=[(f'cob{b}', 1)])
            P.op(self.cfg.get('cast_q', 'pool'), lambda e: e.dma_start(out=Ov[:, ns, :], in_=ob[b][:]), reads=[(f'cob{b}', 0), (f'cob{b}', 1)],
                 writes=[('UVB', L, ci)], dma=True)
            yield

    def emit_cast_tables(self):
        P, nc, A = self.P, self.nc, self.A
        CN = 4
        Uv = A['peer_u'].rearrange("l (n p) d -> p (l n) d", p=128)
        Vv = A['peer_v'].rearrange("l (n p) d -> p (l n) d", p=128)
        Ov = A['UVB'].rearrange("(n p) d -> p n d", p=128)
        with Stage(self, 'cast') as st:
            iu = [st.T(f'iu{i}', [128, CN, D]) for i in range(2)]
            iv = [st.T(f'iv{i}', [128, CN, D]) for i in range(2)]
            ob = [st.T(f'ob{i}', [128, CN, 2 * D], BF16) for i in range(2)]
            nchunk = (2 * NEXP // 128) // CN

            def ld(ci):
                b = ci % 2
                ns = slice(ci * CN, (ci + 1) * CN)
                P.op('sp', lambda e, b=b, ns=ns: e.dma_start(out=iu[b][:], in_=Uv[:, ns, :]), writes=[f'iu{b}'], dma=True)
                P.op('sp', lambda e, b=b, ns=ns: e.dma_start(out=iv[b][:], in_=Vv[:, ns, :]), writes=[f'iv{b}'], dma=True)

            ld(0)
            for ci in range(nchunk):
                b = ci % 2
                ns = slice(ci * CN, (ci + 1) * CN)
                if ci + 1 < nchunk:
                    ld(ci + 1)
                P.op('dve', lambda e, b=b: e.tensor_copy(out=ob[b][:, :, 0:D], in_=iu[b][:]), reads=[f'iu{b}'], writes=[(f'ob{b}', 0)])
                P.op('act', lambda e, b=b: e.copy(out=ob[b][:, 0:2, D:2 * D], in_=iv[b][:, 0:2, :]), reads=[f'iv{b}'], writes=[(f'ob{b}', 1)])
                P.op('pool', lambda e, b=b: e.tensor_copy(out=ob[b][:, 2:4, D:2 * D], in_=iv[b][:, 2:4, :]), reads=[f'iv{b}'], writes=[(f'ob{b}', 2)])
                P.op('act', lambda e, b=b, ns=ns: e.dma_start(out=Ov[:, ns, :], in_=ob[b][:]), reads=[(f'ob{b}', 0), (f'ob{b}', 1), (f'ob{b}', 2)],
                     writes=[('UVB', ci)], dma=True)

    def emit_peer2g(self, xin, dst, L):
        P, nc, A, ps = self.P, self.nc, self.A, self.ps
        NB = self.cfg.get('nb', 24)
        GS = self.cfg.get('gs', 8)
        UVB = A['UVB']
        with Stage(self, f'pg{L}') as st:
            scs = [st.T('sc0', [128, 16, 128])] * 2
            m = st.T('m', [128, 16, 16])
            ix = st.T('ix', [128, 16, 16], U32)
            ixf = st.T('ixf', [128, 16, 16])
            wk = st.T('wk', [128, 16, 128])
            cand = st.T('cand', [128, 8, 256])
            candi = st.T('candi', [128, 8, 256])
            wk2 = wk[:].rearrange("p a k -> p (a k)").rearrange("p (h c) -> p h c", h=8)
            junk2 = [st.T(f'junk2{i}', [128, 256]) for i in range(2)]
            ts = st.T('ts', [128, 8, 16])
            ef = st.T('ef', [128, 128])
            eis = [st.T(f'ei{i}', [128, 128], I32) for i in range(2)]
            gts = [st.T(f'gt{i}', [128, 8, 16]) for i in range(2)]
            gsum = st.T('gsum', [128, 8])
            xts = [st.T(f'xt{i}', [128, D]) for i in range(2)]
            hts = [st.T(f'ht{i}', [128, D]) for i in range(2)]
            uvb = [st.T(f'uv{i}', [128, 2 * D], BF16) for i in range(NB)]
            junk = st.T('junk', [128, D], BF16)
            junka = st.T('junka', [128, D], BF16)
            prods = [st.T(f'prod{i}', [128, D], BF16) for i in range(3)]
            hbs = [st.T(f'hb{i}', [128, D], BF16) for i in range(2)]
            apre = st.T('apre', [128, 128])
            ge = st.T('ge', [128, 128])
            cf = st.T('cf', [128, 128])
            dgs = [st.T(f'dg{i}', [128, 128], BF16) for i in range(4)]
            accs = [st.T(f'acc{i}', [128, D]) for i in range(2)]

            def load_sc(t):
                P.op('sp', lambda e, t=t: e.dma_start(out=scs[t % 2][:].rearrange("p a k -> p (a k)"), in_=A['SCR'][t * 128:(t + 1) * 128, :]),
                     writes=['sc0'], dma=True)

            def load_xh(t):
                P.op('sp', lambda e, t=t: e.dma_start(out=xts[t % 2][:], in_=xin[t * 128:(t + 1) * 128, :]), writes=[f'xt{t % 2}'], dma=True)
                P.op('sp', lambda e, t=t: e.dma_start(out=hts[t % 2][:], in_=A['H'][t * 128:(t + 1) * 128, :]), writes=[f'ht{t % 2}'], dma=True)

            def topk_gen(t):
                sc = scs[t % 2]
                rsc = 'sc0'
                eib, gtb = eis[t % 2], gts[t % 2]
                rei, rgt = f'ei{t % 2}', f'gt{t % 2}'
                for c in range(16):
                    P.op('dve', lambda e, c=c: e.max(out=m[:, c, 0:8], in_=sc[:, c, :]), reads=[rsc], writes=[('m0', c)])
                    yield
                P.fence('dve')
                for c in range(16):
                    P.op('dve', lambda e, c=c: e.max_index(out=ix[:, c, 0:8], in_max=m[:, c, 0:8], in_values=sc[:, c, :]),
                         reads=[rsc, ('m0', c)], writes=[('ix0', c)])
                    yield
                    P.op('dve', lambda e, c=c: e.match_replace(out=wk[:, c, :], in_to_replace=m[:, c, 0:8], in_values=sc[:, c, :], imm_value=-1e30),
                         reads=[rsc, ('m0', c)], writes=[('wk', c)])
                    yield
                P.fence('dve')
                for c in range(16):
                    P.op('dve', lambda e, c=c: e.max(out=m[:, c, 8:16], in_=wk[:, c, :]), reads=[('wk', c)], writes=[('m1', c)])
                    yield
                P.fence('dve')
                for c in range(16):
                    P.op('dve', lambda e, c=c: e.max_index(out=ix[:, c, 8:16], in_max=m[:, c, 8:16], in_values=wk[:, c, :]),
                         reads=[('wk', c), ('m1', c)], writes=[('ix1', c)])
                    yield
                P.fence('dve')
                mres = [('m0', c) for c in range(16)] + [('m1', c) for c in range(16)]
                ixres = [('ix0', c) for c in range(16)] + [('ix1', c) for c in range(16)]
                P.op('dve', lambda e: e.tensor_copy(out=ixf[:], in_=ix[:]), reads=ixres, writes=['ixf'])
                yield
                m4 = m[:].rearrange("p (h two) k -> p h two k", two=2)
                i4 = ixf[:].rearrange("p (h two) k -> p h two k", two=2)
                c4 = cand[:].rearrange("p h (a b) -> p h a b", a=16)
                ci4 = candi[:].rearrange("p h (a b) -> p h a b", a=16)
                P.op('dve', lambda e: e.tensor_tensor(out=c4, in0=m4[:, :, 0, :].unsqueeze(3).to_broadcast([128, 8, 16, 16]),
                                                      in1=m4[:, :, 1, :].unsqueeze(2).to_broadcast([128, 8, 16, 16]), op=ALU.add),
                     reads=mres, writes=['cand'])
                yield
                P.op('dve', lambda e: e.tensor_scalar(out=i4[:, :, 0, :], in0=i4[:, :, 0, :], scalar1=128.0, scalar2=None, op0=ALU.mult),
                     reads=['ixf'], writes=['ixf'])
                yield
                P.op('dve', lambda e: e.tensor_tensor(out=ci4, in0=i4[:, :, 0, :].unsqueeze(3).to_broadcast([128, 8, 16, 16]),
                                                      in1=i4[:, :, 1, :].unsqueeze(2).to_broadcast([128, 8, 16, 16]), op=ALU.add),
                     reads=['ixf'], writes=['candi'])
                yield
                for h in range(8):
                    P.op('dve', lambda e, h=h: e.max(out=ts[:, h, 0:8], in_=cand[:, h, :]), reads=['cand'], writes=[('ts0', h)])
                    yield
                P.fence('dve')
                for h in range(8):
                    P.op('dve', lambda e, h=h: e.match_replace(out=wk2[:, h, :], in_to_replace=ts[:, h, 0:8], in_values=cand[:, h, :], imm_value=-1e30),
                         reads=['cand', ('ts0', h)], writes=[('wk2', h)])
                    yield
                P.fence('dve')
                for h in range(8):
                    P.op('dve', lambda e, h=h: e.max(out=ts[:, h, 8:16], in_=wk2[:, h, :]), reads=[('wk2', h)], writes=[('ts1', h)])
                    yield
                P.fence('dve')
                tsres = [('ts0', h) for h in range(8)] + [('ts1', h) for h in range(8)]
                for h in range(8):
                    for k in range(16):
                        P.op('dve', lambda e, h=h, k=k: e.scalar_tensor_tensor(out=junk2[(h * 16 + k) % 2][:], in0=cand[:, h, :], scalar=ts[:, h, k:k + 1], in1=candi[:, h, :],
                                                                               op0=ALU.is_equal, op1=ALU.mult, accum_out=ef[:, h * 16 + k:h * 16 + k + 1]),
                             reads=['cand', 'candi', ('ts0', h), ('ts1', h)], writes=[('ef', h * 16 + k)])
                        yield
                P.fence('dve')
                P.op('dve', lambda e: e.tensor_scalar(out=ef[:], in0=ef[:], scalar1=float(NEXP - 1), scalar2=float(L * NEXP), op0=ALU.min, op1=ALU.add),
                     reads=[('ef', q) for q in range(128)], writes=['ef'])
                yield
                P.op('dve', lambda e: e.tensor_copy(out=eib[:], in_=ef[:]), reads=['ef'], writes=[rei])
                yield
                P.op('dve', lambda e: e.tensor_tensor(out=gtb[:], in0=ts[:], in1=ts[:, :, 0:1].to_broadcast([128, 8, 16]), op=ALU.subtract),
                     reads=tsres, writes=[rgt])
                yield
                P.op('act', lambda e: e.activation(out=gtb[:], in_=gtb[:], func=AF.Exp), reads=[rgt], writes=[rgt])
                P.op('dve', lambda e: e.tensor_reduce(out=gsum[:], in_=gtb[:], axis=AX.X, op=ALU.add), reads=[rgt], writes=['gsum'])
                yield
                P.op('dve', lambda e: e.reciprocal(out=gsum[:], in_=gsum[:]), reads=['gsum'], writes=['gsum'])
                yield
                P.op('dve', lambda e: e.tensor_tensor(out=gtb[:], in0=gtb[:], in1=gsum[:].unsqueeze(2).to_broadcast([128, 8, 16]), op=ALU.mult),
                     reads=[rgt, 'gsum'], writes=[rgt])
                yield

            def step(gen, n=1):
                if gen is None:
                    return None
                try:
                    for _ in range(n):
                        next(gen)
                except StopIteration:
                    return None
                return gen

            load_sc(0)
            load_xh(0)
            g0 = topk_gen(0)
            while g0 is not None:
                g0 = step(g0, 64)
            load_sc(1)
            nu = 0
            ndg = 0
            npr = 0
            for ti in range(NT):
                b = ti % 2
                xt, ht, eib, gtb, acc = xts[b], hts[b], eis[b], gts[b], accs[b]
                rx, rh, rei, rgt, racc = f'xt{b}', f'ht{b}', f'ei{b}', f'gt{b}', f'acc{b}'
                gt2 = gtb[:].rearrange("p h k -> p (h k)")
                pa = [ps[(ti % 2) * 2], ps[(ti % 2) * 2 + 1]]
                rpa = [f'ps{(ti % 2) * 2}', f'ps{(ti % 2) * 2 + 1}']
                if ti + 1 < NT:
                    load_xh(ti + 1)
                hb, rhb = hbs[b], f'hb{b}'
                P.op('dve', lambda e, hb=hb, ht=ht: e.tensor_copy(out=hb[:], in_=ht[:]), reads=[rh], writes=[rhb])
                gen = topk_gen(ti + 1) if ti + 1 < NT else None
                for g in range(128 // GS):
                    slots = []
                    for kk in range(GS):
                        k = g * GS + kk
                        s_ = nu % NB
                        nu += 1
                        slots.append(s_)
                        P.op('pool', lambda e, s_=s_, k=k, eib=eib: e.indirect_dma_start(
                            out=uvb[s_][:], out_offset=None, in_=UVB,
                            in_offset=bass.IndirectOffsetOnAxis(ap=eib[:, k:k + 1], axis=0)),
                            reads=[rei], writes=[('uv', s_)], dma=True)
                        if (k % self.cfg.get('split_den', 3)) < self.cfg.get('split_num', 1) or not self.cfg.get('dot_split', True):
                            P.op('dve', lambda e, s_=s_, k=k, ht=ht: e.scalar_tensor_tensor(out=junk[:], in0=uvb[s_][:, 0:D], scalar=1.0, in1=ht[:], op0=ALU.mult, op1=ALU.mult,
                                                                                         accum_out=apre[:, k:k + 1]),
                                 reads=[('uv', s_), rh], writes=[('apre', k)])
                        else:
                            pr = prods[npr % 3]
                            rpr = f'prod{npr % 3}'
                            npr += 1
                            P.op('dve', lambda e, s_=s_, pr=pr, hb=hb: e.tensor_tensor(out=pr[:], in0=uvb[s_][:, 0:D], in1=hb[:], op=ALU.mult),
                                 reads=[('uv', s_), rhb], writes=[rpr])
                            P.op('act', lambda e, pr=pr, k=k: e.activation(out=junka[:], in_=pr[:], func=AF.Copy, accum_out=apre[:, k:k + 1]),
                                 reads=[rpr], writes=[('apre', k)])
                        gen = step(gen, 2)
                    gsl = slice(g * GS, (g + 1) * GS)
                    P.op('act', lambda e, gsl=gsl: e.activation(out=ge[:, gsl], in_=apre[:, gsl], func=AF.Gelu),
                         reads=[('apre', k) for k in range(g * GS, (g + 1) * GS)], writes=[('ge', g)])
                    P.op('dve', lambda e, gsl=gsl, gt2=gt2: e.tensor_tensor(out=cf[:, gsl], in0=ge[:, gsl], in1=gt2[:, gsl], op=ALU.mult),
                         reads=[('ge', g), rgt], writes=[('cf', g)])
                    for kk in range(GS):
                        k = g * GS + kk
                        s_ = slots[kk]
                        dg = dgs[ndg % 4]
                        rdg = f'dg{ndg % 4}'
                        ndg += 1
                        P.op('act', lambda e, dg=dg, k=k: e.activation(out=dg[:], in_=self.ident[:], func=AF.Copy, scale=cf[:, k:k + 1]),
                             reads=['ident', ('cf', g)], writes=[rdg])
                        for half in range(2):
                            P.op('pe', lambda e, dg=dg, s_=s_, half=half, k=k, pa=pa: e.matmul(out=pa[half][:], lhsT=dg[:], rhs=uvb[s_][:, D + half * 512:D + (half + 1) * 512],
                                                                                           start=(k == 0), stop=(k == 127)),
                                 reads=[rdg, ('uv', s_)], writes=[rpa[half]])
                while gen is not None:
                    gen = step(gen, 64)
                if ti + 2 < NT:
                    load_sc(ti + 2)
                for half in range(2):
                    P.op('dve', lambda e, acc=acc, half=half, pa=pa: e.tensor_tensor(out=acc[:, half * 512:(half + 1) * 512], in0=pa[half][:],
                                                                                   in1=self.gate_bc[:, half * 512:(half + 1) * 512], op=ALU.mult),
                         reads=[rpa[half], 'gate_bc'], writes=[racc])
                P.op('dve', lambda e, acc=acc, xt=xt: e.scalar_tensor_tensor(out=acc[:], in0=xt[:], scalar=ALPHA, in1=acc[:], op0=ALU.mult, op1=ALU.add),
                     reads=[rx, racc], writes=[racc])
                self.layernorm_inplace(acc[:], racc, gb='dve')
                P.op('sp', lambda e, acc=acc, ti=ti: e.dma_start(out=dst[ti * 128:(ti + 1) * 128, :], in_=acc[:]), reads=[racc], dma=True)

    def emit_attn1(self, xin):
        P, nc, A, ps = self.P, self.nc, self.A, self.ps
        ones = self.ones
        NCOL = 3 * D + 16
        with Stage(self, 'a1') as st:
            winr = st.T('winr', [128, 8, 3 * D], F32R)
            wstg = [st.T('wstg0', [128, 8, 512])] * 2
            wf = st.T('wf', [128, 8, 16])
            qkb = st.T('qkb', [128, 16])
            vbr = st.T('vbr', [1, D])
            vb_bc = st.T('vb_bc', [128, D])
            fb = st.T('fb', [16, 1])
            xts = [st.T(f'xt{i}', [128, D]) for i in range(2)]
            ht = st.T('ht', [128, D])
            hT = st.T('hT', [128, 8, 512], F32R)
            qko = [st.T(f'qko{i}', [128, 512]) for i in range(2)]
            vo = [st.T(f'vo{i}', [128, D]) for i in range(2)]
            Fcb = [st.T(f'Fcb{i}', [16, 512]) for i in range(2)]
            Frb = st.T('Frb', [16, 512], F32R)
            Flb = st.T('Flb', [16, 512])
            nFr = st.T('nFr', [16, 512])
            nFl = st.T('nFl', [16, 512])
            spt = st.T('spt', [16, 512])
            o16 = st.T('o16', [16, 512])
            for q in range(6):
                wb = wstg[0]
                rw = 'wstg0'
                P.op('sp', lambda e, q=q, wb=wb: e.dma_start(out=wb[:], in_=A['attn_in_w'][:, q * 512:(q + 1) * 512].rearrange("(k p) n -> p k n", p=128)),
                     writes=[rw], dma=True)
                eng = ('dve', 'pool')[q % 2]
                P.op(eng, lambda e, q=q, wb=wb: e.tensor_copy(out=winr[:, :, q * 512:(q + 1) * 512], in_=wb[:]), reads=[rw], writes=[('win', q)])
            P.op('sp', lambda e: e.dma_start(out=wf[:], in_=A['attn_in_w'][:, 3 * D:NCOL].rearrange("(k p) n -> p k n", p=128)),
                 writes=['wf'], dma=True)
            P.op('sp', lambda e: e.dma_start(out=qkb[:], in_=A['attn_qkb_l']), writes=['qkb'], dma=True)
            P.op('sp', lambda e: e.dma_start(out=vbr[:], in_=A['attn_vb']), writes=['vbr'], dma=True)
            P.op('sp', lambda e: e.dma_start(out=fb[:], in_=A['attn_fb']), writes=['fb'], dma=True)
            P.op('dve', lambda e: e.tensor_scalar(out=qkb[:, 0:8], in0=qkb[:, 0:8], scalar1=0.125, scalar2=None, op0=ALU.mult), reads=['qkb'], writes=['qkb'])
            P.op('dve', lambda e: e.tensor_scalar(out=fb[:], in0=fb[:], scalar1=-1.0, scalar2=None, op0=ALU.mult), reads=['fb'], writes=['fb'])
            P.op('pool', lambda e: e.memset(o16[:], 1.0), writes=['o16'])
            for half in range(2):
                P.op('pe', lambda e, half=half: e.matmul(out=ps[4 + half][:], lhsT=ones[0:1, :], rhs=vbr[0:1, half * 512:(half + 1) * 512], start=True, stop=True),
                     reads=['ones', 'vbr'], writes=[f'ps{4 + half}'])
                P.op('act', lambda e, half=half: e.copy(out=vb_bc[:, half * 512:(half + 1) * 512], in_=ps[4 + half][:]), reads=[f'ps{4 + half}'], writes=['vb_bc'])
            ti = 0
            for jb in range(8):
                cols = slice(jb * 512, (jb + 1) * 512)
                for tl in range(4):
                    xt = xts[ti % 2]
                    rx = f'xt{ti % 2}'
                    P.op('sp', lambda e, xt=xt, ti=ti: e.dma_start(out=xt[:], in_=xin[ti * 128:(ti + 1) * 128, :]), writes=[rx], dma=True)
                    self.modulate(xt[:], ht[:], rx, 'ht')
                    self.transpose8(ht, 'ht', hT, 'hT', tl * 128, 0)
                    ti += 1
                for c in range(16):
                    bank = ps[2 + c % 2]
                    rb = f'ps{2 + c % 2}'
                    for k in range(8):
                        P.op('pe', lambda e, k=k, c=c, bank=bank: e.matmul(out=bank[:], lhsT=winr[:, k, c * 128:(c + 1) * 128], rhs=hT[:, k, :],
                                                                          start=(k == 0), stop=(k == 7)),
                             reads=[('win', c // 4), 'hT'], writes=[rb])
                    ob = qko[c % 2]
                    rob = f'qko{c % 2}'
                    P.op('act', lambda e, c=c, bank=bank, ob=ob: e.activation(out=ob[:], in_=bank[:], func=AF.Identity, bias=qkb[:, c:c + 1],
                                                                             scale=(0.125 if c < 8 else 1.0)),
                         reads=[rb, 'qkb'], writes=[rob])
                    dstt = A['QA'] if c < 8 else A['KA']
                    for hh in range(2):
                        head = (c % 8) * 2 + hh
                        P.op('sp', lambda e, ob=ob, hh=hh, head=head, dstt=dstt: e.dma_start(out=dstt[head, 0:64, cols], in_=ob[hh * 64:(hh + 1) * 64, :]),
                             reads=[rob], writes=[('QK', c, hh)], dma=True)
                for tl in range(4):
                    tix = jb * 4 + tl
                    vt = vo[tix % 2]
                    rv = f'vo{tix % 2}'
                    for half in range(2):
                        bank = ps[4 + half]
                        rb = f'ps{4 + half}'
                        for k in range(8):
                            P.op('pe', lambda e, k=k, bank=bank, half=half, tl=tl: e.matmul(out=bank[:], lhsT=hT[:, k, tl * 128:(tl + 1) * 128],
                                                                                        rhs=winr[:, k, 2 * D + half * 512:2 * D + (half + 1) * 512],
                                                                                        start=(k == 0), stop=(k == 7)),
                                 reads=['hT', ('win', 4 + half)], writes=[rb])
                        P.op('dve', lambda e, vt=vt, bank=bank, half=half: e.tensor_tensor(out=vt[:, half * 512:(half + 1) * 512], in0=bank[:],
                                                                                      in1=vb_bc[:, half * 512:(half + 1) * 512], op=ALU.add),
                             reads=[rb, 'vb_bc'], writes=[rv])
                    P.op('sp', lambda e, vt=vt, tix=tix: e.dma_start(out=A['V'][tix * 128:(tix + 1) * 128, :], in_=vt[:]), reads=[rv], writes=[('V', tix)], dma=True)
                for k in range(8):
                    P.op('pe', lambda e, k=k: e.matmul(out=ps[6][0:16, :], lhsT=wf[:, k, :], rhs=hT[:, k, :].bitcast(F32), start=(k == 0), stop=(k == 7)),
                         reads=['wf', 'hT'], writes=['ps6'])
                P.op('act', lambda e: e.activation(out=spt[:], in_=ps[6][0:16, :], func=AF.Exp, bias=fb[:, 0:1], scale=-1.0), reads=['ps6', 'fb'], writes=['spt'])
                P.op('act', lambda e: e.activation(out=spt[:], in_=spt[:], func=AF.Ln, bias=1.0, scale=1.0), reads=['spt'], writes=['spt'])
                P.op('dve', lambda e: e.tensor_scalar(out=spt[:], in0=spt[:], scalar1=-1.0, scalar2=None, op0=ALU.mult), reads=['spt'], writes=['spt'])
                Fc = Fcb[jb % 2]
                rF = f'Fcb{jb % 2}'
                init = 0.0 if jb == 0 else Fcb[(jb - 1) % 2][:, 511:512]
                P.op('dve', lambda e, init=init, Fc=Fc: e.tensor_tensor_scan(out=Fc[:], data0=o16[:], data1=spt[:], initial=init,
                                                                             op0=ALU.mult, op1=ALU.add),
                     reads=['o16', 'spt', f'Fcb{(jb - 1) % 2}'], writes=[rF])
                P.op('dve', lambda e, Fc=Fc: e.tensor_copy(out=Frb[:], in_=Fc[:]), reads=[rF], writes=['Frb'])
                P.op('dve', lambda e, Fc=Fc: e.tensor_tensor(out=Flb[:], in0=Fc[:], in1=Frb[:].bitcast(F32), op=ALU.subtract), reads=[rF, 'Frb'], writes=['Flb'])
                P.op('dve', lambda e: e.tensor_scalar(out=nFr[:], in0=Frb[:].bitcast(F32), scalar1=-1.0, scalar2=None, op0=ALU.mult), reads=['Frb'], writes=['nFr'])
                P.op('dve', lambda e: e.tensor_scalar(out=nFl[:], in0=Flb[:], scalar1=-1.0, scalar2=None, op0=ALU.mult), reads=['Flb'], writes=['nFl'])
                P.op('sp', lambda e: e.dma_start(out=A['QA'][:, 64, cols], in_=Frb[:].bitcast(F32)), reads=['Frb'], writes=[('QAf', jb)], dma=True)
                P.op('sp', lambda e: e.dma_start(out=A['QA'][:, 65, cols], in_=Flb[:]), reads=['Flb'], writes=[('QAl', jb)], dma=True)
                P.op('sp', lambda e: e.dma_start(out=A['KA'][:, 66, cols], in_=nFr[:]), reads=['nFr'], writes=[('KAf', jb)], dma=True)
                P.op('sp', lambda e: e.dma_start(out=A['KA'][:, 67, cols], in_=nFl[:]), reads=['nFl'], writes=[('KAl', jb)], dma=True)
                for r in (66, 67):
                    P.op('sp', lambda e, r=r: e.dma_start(out=A['QA'][:, r, cols], in_=o16[:]), reads=['o16'], writes=[('QAo', r, jb)], dma=True)
                for r in (64, 65):
                    P.op('sp', lambda e, r=r: e.dma_start(out=A['KA'][:, r, cols], in_=o16[:]), reads=['o16'], writes=[('KAo', r, jb)], dma=True)

    def emit_attn2(self):
        P, nc, A, ps = self.P, self.nc, self.A, self.ps
        NR = 68
        with Stage(self, 'a2') as st:
            qst = st.T('qst', [NR, S])
            kst = st.T('kst', [NR, S])
            vst = st.T('vst', [128, 32, 64])
            QAh = [st.T(f'QAh{i}', [NR, S], F32R) for i in range(2)]
            KAh = [st.T(f'KAh{i}', [NR, S], F32R) for i in range(2)]
            Vh = [st.T(f'Vh{i}', [128, 32, 128], F32R) for i in range(2)]
            ones_r = st.T('ones_r', [128, 128], F32R)
            pt = [st.T(f'pt{i}', [128, 512], F32R) for i in range(3)]
            lm = [st.T(f'lm{i}', [128, 512]) for i in range(2)]
            mask = st.T('mask', [128, 4, 512])
            rzt = st.T('rzt', [64, 512])
            oT = [st.T(f'oT{i}', [64, 512]) for i in range(2)]
            P.op('pool', lambda e: e.memset(mask[:], 0.0), writes=['mask'])
            for i4 in range(4):
                P.op('pool', lambda e, i4=i4: e.affine_select(out=mask[:, i4, :], in_=mask[:, i4, :], pattern=[[1, 512]], compare_op=ALU.is_ge,
                                                              fill=NEG, base=-128 * i4, channel_multiplier=-1), reads=['mask'], writes=['mask'])
            P.op('pool', lambda e: e.tensor_copy(out=ones_r[:], in_=self.ones[:]), reads=['ones'], writes=['ones_r'])
            npt = 0
            nlm = 0
            nS = 0
            nO = 0

            def loads(h):
                b = h % 2
                qa, ka, vh = QAh[b], KAh[b], Vh[b]
                rq, rk, rv = f'QAh{b}', f'KAh{b}', f'Vh{b}'
                for q4 in range(4):
                    cs = slice(q4 * 1024, (q4 + 1) * 1024)
                    P.op('sp', lambda e, h=h, cs=cs: e.dma_start(out=qst[:, cs], in_=A['QA'][h, :, cs]), writes=[('qst', q4)], dma=True)
                    P.op('sp', lambda e, h=h, cs=cs: e.dma_start(out=kst[:, cs], in_=A['KA'][h, :, cs]), writes=[('kst', q4)], dma=True)
                    P.op('sp', lambda e, h=h, q4=q4: e.dma_start(
                        out=vst[:, q4 * 8:(q4 + 1) * 8, :],
                        in_=A['V'][q4 * 1024:(q4 + 1) * 1024, h * 64:(h + 1) * 64].rearrange("(i p) d -> p i d", p=128)),
                        writes=[('vst', q4)], dma=True)
                for q4 in range(4):
                    cs = slice(q4 * 1024, (q4 + 1) * 1024)
                    P.op('pool', lambda e, qa=qa, cs=cs: e.tensor_copy(out=qa[:, cs], in_=qst[:, cs]), reads=[('qst', q4)], writes=[rq])
                    P.op('pool', lambda e, ka=ka, cs=cs: e.tensor_copy(out=ka[:, cs], in_=kst[:, cs]), reads=[('kst', q4)], writes=[rk])
                    for dup in range(2):
                        P.op('pool', lambda e, vh=vh, q4=q4, dup=dup: e.tensor_copy(out=vh[:, q4 * 8:(q4 + 1) * 8, dup * 64:(dup + 1) * 64],
                                                                                  in_=vst[:, q4 * 8:(q4 + 1) * 8, :]),
                             reads=[('vst', q4)], writes=[rv])

            loads(0)
            NH = self.cfg.get('nheads', 16)
            steps = [(h, j, i) for h in range(NH) for j in range(8) for i in range(4 * j + 4)]

            def emit_qk(n):
                h, j, i = steps[n]
                b = h % 2
                sb = ps[n % 3]
                P.op('pe', lambda e, sb=sb, ka=KAh[b], qa=QAh[b], i=i, j=j: e.matmul(out=sb[:], lhsT=ka[:, i * 128:(i + 1) * 128], rhs=qa[:, j * 512:(j + 1) * 512],
                                                                                 start=True, stop=True),
                     reads=[f'KAh{b}', f'QAh{b}'], writes=[f'ps{n % 3}'])

            emit_qk(0)
            for n, (h, j, i) in enumerate(steps):
                b = h % 2
                vh, rv = Vh[b], f'Vh{b}'
                if j == 0 and i == 0 and h + 1 < NH:
                    loads(h + 1)
                if n + 1 < len(steps):
                    emit_qk(n + 1)
                if i == 0:
                    oset = nO % 2
                    nO += 1
                poA, poB = ps[3 + 2 * oset], ps[4 + 2 * oset]
                rA, rB = f'ps{3 + 2 * oset}', f'ps{4 + 2 * oset}'
                ot, rot = oT[oset], f'oT{oset}'
                last = 4 * j + 3
                sb, rsb = ps[n % 3], f'ps{n % 3}'
                p_, rp = pt[n % 3], f'pt{n % 3}'
                if i >= 4 * j:
                    l_ = lm[nlm % 2]
                    rl = f'lm{nlm % 2}'
                    nlm += 1
                    P.op('dve', lambda e, l_=l_, sb=sb, i=i, j=j: e.tensor_tensor(out=l_[:], in0=sb[:], in1=mask[:, i - 4 * j, :], op=ALU.add),
                         reads=[rsb, 'mask'], writes=[rl])
                    P.op('act', lambda e, p_=p_, l_=l_: e.activation(out=p_[:], in_=l_[:], func=AF.Exp), reads=[rl], writes=[rp])
                else:
                    P.op('act', lambda e, p_=p_, sb=sb: e.activation(out=p_[:], in_=sb[:], func=AF.Exp), reads=[rsb], writes=[rp])
                P.op('pe', lambda e, p_=p_, i=i, poA=poA, vh=vh, last=last: e.matmul(out=poA[:], lhsT=vh[:, i, :], rhs=p_[:], start=(i == 0), stop=(i == last)),
                     reads=[rp, rv], writes=[rA])
                P.op('pe', lambda e, p_=p_, i=i, poB=poB, last=last: e.matmul(out=poB[:], lhsT=ones_r[:], rhs=p_[:], start=(i == 0), stop=(i == last)),
                     reads=[rp, 'ones_r'], writes=[rB])
                if i == last:
                    P.op('dve', lambda e, poB=poB: e.reciprocal(out=rzt[:], in_=poB[0:64, :]), reads=[rB], writes=['rzt'])
                    P.op('dve', lambda e, poA=poA, ot=ot: e.tensor_tensor(out=ot[:], in0=poA[0:64, :], in1=rzt[:], op=ALU.mult), reads=[rA, 'rzt'], writes=[rot])
                    P.op('sp', lambda e, ot=ot, h=h, j=j: e.dma_start(out=A['AOT'][h // 2, (h % 2) * 64:(h % 2) * 64 + 64, j * 512:(j + 1) * 512], in_=ot[:]),
                         reads=[rot], writes=[('AOT', h, j)], dma=True)


def make_in_maps(inputs, cores=range(8)):
    f = lambda a: np.ascontiguousarray(np.asarray(a, dtype=np.float32))
    sh = {}
    sh['ada_mix_w'] = f(inputs['ada_mix_w'])
    sh['ada_ffn_w'] = f(inputs['ada_ffn_w'])
    amb, afb = f(inputs['ada_mix_b']), f(inputs['ada_ffn_b'])
    sh['ada_b'] = f(np.stack([amb[0], afb[0], amb[1], afb[1]]))
    g1, g2 = f(inputs['ln_mix_g']), f(inputs['ln_ffn_g'])
    b1, b2 = f(inputs['ln_mix_b']), f(inputs['ln_ffn_b'])
    sh['ln_g'] = f(np.stack([g1[0], g2[0], g1[1], g2[1]]))
    sh['ln_b'] = f(np.stack([b1[0], b2[0], b1[1], b2[1]]))
    sh['conv_in_w'] = f(inputs['conv_in_w'][0])
    sh['conv_in_b_l'] = f(np.asarray(inputs['conv_in_b'][0]).reshape(16, 128).T)
    sh['conv_dw_w_l'] = f(np.asarray(inputs['conv_dw_w'][0]).reshape(31, 8, 128).transpose(2, 1, 0))
    sh['conv_vec_l'] = f(np.stack([np.asarray(inputs[k][0]).reshape(8, 128).T for k in ('conv_dw_b', 'conv_ln_g', 'conv_ln_b')], axis=1))
    sh['conv_out_w'] = f(inputs['conv_out_w'][0])
    sh['conv_out_b'] = f(np.asarray(inputs['conv_out_b'][0]).reshape(1, D))
    sh['attn_in_w'] = f(inputs['attn_in_w'][0])
    ab = np.asarray(inputs['attn_in_b'][0])
    sh['attn_qkb_l'] = f(ab[:2 * D].reshape(16, 128).T)
    sh['attn_vb'] = f(ab[2 * D:3 * D].reshape(1, D))
    sh['attn_fb'] = f(ab[3 * D:].reshape(16, 1))
    sh['attn_out_w'] = f(inputs['attn_out_w'][0])
    sh['attn_out_b'] = f(np.asarray(inputs['attn_out_b'][0]).reshape(1, D))
    sh['peer_query_w'] = f(inputs['peer_query_w'])
    k1, k2 = np.asarray(inputs['peer_sub_keys_1']), np.asarray(inputs['peer_sub_keys_2'])
    sh['peer_skT'] = f(np.stack([np.stack([k1[l].T, k2[l].T]) for l in range(2)]))
    sh['peer_u'] = f(inputs['peer_expert_u'])
    sh['peer_v'] = f(inputs['peer_expert_v'])
    x = np.asarray(inputs['x'])
    c = np.asarray(inputs['c'])
    maps = []
    for b in cores:
        m = dict(sh)
        m['x'] = f(x[b])
        m['c_l'] = f(c[b].reshape(8, 128).T)
        maps.append(m)
    return maps


_NC_CACHE = {}


def kernel(**inputs):
    if 'full' not in _NC_CACHE:
        _NC_CACHE['full'] = Kern({}).build()
    nc = _NC_CACHE['full']
    maps = make_in_maps(inputs)
    res = run_bass_kernel_spmd(nc, maps, core_ids=list(range(8)))
    return np.stack([np.asarray(r['out'], dtype=np.float32) for r in res.results], axis=0)
```

```python
import numpy as np
from contextlib import ExitStack
import concourse.bass as bass
import concourse.mybir as mybir
from concourse.bass_utils import run_bass_kernel_spmd

F32 = mybir.dt.float32
I32 = mybir.dt.int32
U32 = mybir.dt.uint32
F32R = mybir.dt.float32r
BF16 = mybir.dt.bfloat16
ALU = mybir.AluOpType
AF = mybir.ActivationFunctionType
AX = mybir.AxisListType

S = 4096
D = 1024
NT = S // 128
ALPHA = float((2 * 2) ** 0.25)
EPS = 1e-5
NEXP = 16384
MAXV = 30000
NEG = -30000.0


class Prog:
    def __init__(self, nc, es):
        self.nc = nc
        self.es = es
        self.eng = {'pe': nc.tensor, 'dve': nc.vector, 'act': nc.scalar,
                    'pool': nc.gpsimd, 'sp': nc.sync}
        self.seq = {e: 0 for e in self.eng}
        self.csem = {e: [] for e in self.eng}
        self.known = {e: {} for e in self.eng}
        self.snap = {}
        self.last_w = {}
        self.readers = {}
        self.semobj = {}
        self.dma_pool = {}
        self.nsem = 0
        self.nwaits = 0
        self.nops = 0
        for q, n in (('sp', 24), ('pool', 24), ('act', 8)):
            self.dma_pool[q] = {'sems': [self._newsem(f"d{q}{i}") for i in range(n)],
                                'cnt': [0] * n, 'next': 0}

    def _newsem(self, name):
        s = self.es.enter_context(self.nc.semaphore(name))
        self.semobj[name] = s
        self.nsem += 1
        return name

    def _need(self, e, tok, skip_self):
        if tok is None:
            return
        name, val, owner = tok
        if skip_self and owner == e:
            return
        if self.known[e].get(name, 0) >= val:
            return
        self.eng[e].wait_ge(self.semobj[name], val)
        self.nwaits += 1
        k = self.known[e]
        k[name] = val
        sn = self.snap.get((name, val))
        if sn:
            for n2, v2 in sn.items():
                if k.get(n2, 0) < v2:
                    k[n2] = v2

    def op(self, e, fn, reads=(), writes=(), dma=False, skip_self=None):
        if skip_self is None:
            skip_self = (e == 'pe')
        if dma:
            skip_self = False
        for r in reads:
            self._need(e, self.last_w.get(r), skip_self)
        for w in writes:
            self._need(e, self.last_w.get(w), skip_self)
            for t in self.readers.get(w, ()):
                self._need(e, t, skip_self)
        self.nops += 1
        if dma:
            pool = self.dma_pool[e]
            i = pool['next']
            pool['next'] = (i + 1) % len(pool['sems'])
            name = pool['sems'][i]
            if pool['cnt'][i] + 16 > MAXV:
                name = self._newsem(f"{name}r{self.nsem}")
                pool['sems'][i] = name
                pool['cnt'][i] = 0
            prev = pool['cnt'][i]
            if prev > 0:
                self._need(e, (name, prev, e + '_dma'), False)
            ins = fn(self.eng[e])
            pool['cnt'][i] = prev + 16
            ins.then_inc(self.semobj[name], 16)
            tok = (name, prev + 16, e + '_dma')
        else:
            n = self.seq[e]
            ep = n // MAXV
            while len(self.csem[e]) <= ep:
                self.csem[e].append(self._newsem(f"c{e}{len(self.csem[e])}"))
            name = self.csem[e][ep]
            ins = fn(self.eng[e])
            ins.then_inc(self.semobj[name], 1)
            self.seq[e] = n + 1
            tok = (name, n - ep * MAXV + 1, e)
        self.snap[(tok[0], tok[1])] = dict(self.known[e])
        for r in reads:
            self.readers.setdefault(r, []).append(tok)
        for w in writes:
            self.last_w[w] = tok
            self.readers[w] = []
        return tok

    def fence(self, e):
        n = self.seq[e]
        if n > 0:
            ep = (n - 1) // MAXV
            self._need(e, (self.csem[e][ep], n - ep * MAXV, e), False)

    def barrier(self):
        toks = []
        for e in self.eng:
            n = self.seq[e]
            if n > 0:
                ep = (n - 1) // MAXV
                toks.append((self.csem[e][ep], n - ep * MAXV, e))
        for q, pool in self.dma_pool.items():
            for name, c in zip(pool['sems'], pool['cnt']):
                if c > 0:
                    toks.append((name, c, q + '_dma'))
        for e in self.eng:
            for t in toks:
                self._need(e, t, False)
        self.last_w.clear()
        self.readers.clear()
        self.snap.clear()


class Stage:
    _n = 0

    def __init__(self, K, name):
        self.K = K
        Stage._n += 1
        self.name = f"{name}{Stage._n}"

    def __enter__(self):
        self.es = ExitStack()
        self.es.__enter__()
        return self

    def T(self, name, shape, dt=F32):
        return self.es.enter_context(self.K.nc.sbuf_tensor(f"{self.name}_{name}", shape, dt))

    def __exit__(self, *a):
        self.K.P.barrier()
        return self.es.__exit__(*a)


class Kern:
    def __init__(self, cfg):
        self.cfg = cfg

    def build(self):
        nc = bass.Bass("TRN2", target_bir_lowering=False)
        self.nc = nc
        dbg = self.cfg.get('debug', False)

        def din(name, shape, dt=F32):
            return nc.dram_tensor(name, list(shape), dt, kind="ExternalInput").ap()

        def dscr(name, shape, dt=F32):
            kind = "ExternalOutput" if (dbg and name in self.cfg.get('expose', ())) else "Internal"
            return nc.dram_tensor(name, list(shape), dt, kind=kind).ap()

        A = {}
        A['x'] = din('x', [S, D])
        A['c_l'] = din('c_l', [128, 8])
        A['ada_mix_w'] = din('ada_mix_w', [2, D, 3 * D])
        A['ada_ffn_w'] = din('ada_ffn_w', [2, D, 3 * D])
        A['ada_b'] = din('ada_b', [4, 3 * D])
        A['ln_g'] = din('ln_g', [4, D])
        A['ln_b'] = din('ln_b', [4, D])
        A['conv_in_w'] = din('conv_in_w', [D, 2 * D])
        A['conv_in_b_l'] = din('conv_in_b_l', [128, 16])
        A['conv_dw_w_l'] = din('conv_dw_w_l', [128, 8, 31])
        A['conv_vec_l'] = din('conv_vec_l', [128, 3, 8])
        A['conv_out_w'] = din('conv_out_w', [D, D])
        A['conv_out_b'] = din('conv_out_b', [1, D])
        A['attn_in_w'] = din('attn_in_w', [D, 3 * D + 16])
        A['attn_qkb_l'] = din('attn_qkb_l', [128, 16])
        A['attn_vb'] = din('attn_vb', [1, D])
        A['attn_fb'] = din('attn_fb', [16, 1])
        A['attn_out_w'] = din('attn_out_w', [D, D])
        A['attn_out_b'] = din('attn_out_b', [1, D])
        A['peer_query_w'] = din('peer_query_w', [2, D, 2 * D])
        A['peer_skT'] = din('peer_skT', [2, 2, 128, 128])
        A['peer_u'] = din('peer_u', [2, NEXP, D])
        A['peer_v'] = din('peer_v', [2, NEXP, D])
        A['out'] = nc.dram_tensor('out', [S, D], F32, kind="ExternalOutput").ap()
        A['X1'] = dscr('X1', [S, D])
        A['X2'] = dscr('X2', [S, D])
        A['X3'] = dscr('X3', [S, D])
        A['ST'] = dscr('ST', [8, 128, S])
        A['IDX'] = dscr('IDX', [S, 128], I32)
        A['SCR'] = dscr('SCR', [S, 2048])
        A['UVB'] = dscr('UVB', [2 * NEXP, 2 * D], BF16)
        A['H'] = dscr('H', [S, D])
        A['GATE'] = dscr('GATE', [S, 128])
        A['QA'] = dscr('QA', [16, 68, S])
        A['KA'] = dscr('KA', [16, 68, S])
        A['V'] = dscr('V', [S, D])
        A['AOT'] = dscr('AOT', [8, 128, S])
        self.A = A

        with ExitStack() as es:
            self.P = P = Prog(nc, es)
            G = lambda name, shape, dt=F32: es.enter_context(nc.sbuf_tensor(name, shape, dt))
            self.ps = [es.enter_context(nc.psum_tensor(f"ps{i}", [128, 512], F32)) for i in range(8)]
            self.ident = G('ident', [128, 128])
            self.ones = G('ones', [128, 128])
            self.SC = G('SC', [128, 8, 128])
            self.shift_bc = G('shift_bc', [128, D])
            self.scale_bc = G('scale_bc', [128, D])
            self.gate_bc = G('gate_bc', [128, D])
            self.g_bc = G('g_bc', [128, D])
            self.b_bc = G('b_bc', [128, D])
            self.bs = G('bs', [128, 2, 6])
            self.mv = G('mv', [128, 2])
            self.rs = G('rs', [128, 1])
            self.emit_globals()
            self._cast_done = False
            if self.cfg.get('peer_bf16', True) and self.cfg.get('cast_stage', False):
                self.emit_cast_tables()
                self._cast_done = True
            order = self.cfg.get('stages', ['conv', 'peer0', 'attn', 'peer1'])
            cur = A['x']
            nxt = {'conv': A['X1'], 'peer0': A['X2'], 'attn': A['X3'], 'peer1': A['out']}
            for i, st in enumerate(order):
                dst = A['out'] if i == len(order) - 1 else nxt[st]
                if st == 'conv':
                    self.emit_adaln(0)
                    self.emit_conv1(cur)
                    self.emit_proj_out(cur, dst, A['conv_out_w'], A['conv_out_b'], src_fm=A['ST'])
                elif st == 'attn':
                    self.emit_adaln(2)
                    self.emit_attn1(cur)
                    self.emit_attn2()
                    self.emit_proj_out(cur, dst, A['attn_out_w'], A['attn_out_b'], src_fm=A['AOT'])
                else:
                    L = int(st[-1])
                    self.emit_adaln(1 + 2 * L)
                    if self.cfg.get('peer_bf16', True):
                        self.emit_peer1a(cur, L, cast=not self._cast_done)
                        self.emit_peer2g(cur, dst, L)
                    elif self.cfg.get('peer_fused', True):
                        self.emit_peer1a(cur, L)
                        self.emit_peer2f(cur, dst, L)
                    else:
                        self.emit_peer1(cur, L)
                        self.emit_peer2(cur, dst, L)
                cur = dst
            P.barrier()
            print(f"[kern] ops={P.nops} waits={P.nwaits} sems={P.nsem} seq={P.seq}")
        return nc

    def emit_globals(self):
        P, nc = self.P, self.nc
        ident, ones = self.ident, self.ones
        P.op('pool', lambda e: e.memset(ident[:], 1.0), writes=['ident'])
        P.op('pool', lambda e: e.affine_select(out=ident[:], in_=ident[:], pattern=[[-1, 128]],
                                               compare_op=ALU.is_equal, fill=0.0, base=0, channel_multiplier=1),
             reads=['ident'], writes=['ident'])
        P.op('pool', lambda e: e.memset(ones[:], 1.0), writes=['ones'])
        with Stage(self, 'gl') as st:
            ct = st.T('ct', [128, 8])
            P.op('sp', lambda e: e.dma_start(out=ct[:], in_=self.A['c_l']), writes=['ct'], dma=True)
            P.op('act', lambda e: e.activation(out=ct[:], in_=ct[:], func=AF.Silu), reads=['ct'], writes=['ct'])
            SC = self.SC
            P.op('dve', lambda e: e.tensor_copy(out=SC[:], in_=ct[:].unsqueeze(2).to_broadcast([128, 8, 128])),
                 reads=['ct'], writes=['SC'])

    def modulate(self, xt, ht, rx, rh, eng0='dve'):
        P = self.P
        P.op(eng0, lambda e: e.tensor_tensor(out=ht, in0=xt, in1=self.scale_bc[:], op=ALU.mult),
             reads=[rx, 'scale_bc'], writes=[rh])
        P.op('pool', lambda e: e.tensor_tensor(out=ht, in0=ht, in1=self.shift_bc[:], op=ALU.add),
             reads=[rh, 'shift_bc'], writes=[rh])

    def transpose8(self, src, rsrc, dstT, rdst, col0, pb, evac1='dve'):
        P = self.P
        ps = self.ps
        for half in range(2):
            bank = ps[pb + half]
            rb = f'ps{pb + half}'
            for kk in range(4):
                k = half * 4 + kk
                P.op('pe', lambda e, k=k, kk=kk, bank=bank: e.transpose(out=bank[:, kk * 128:(kk + 1) * 128],
                                                                      in_=src[:, k * 128:(k + 1) * 128],
                                                                      identity=self.ident[:]),
                     reads=[rsrc, 'ident'], writes=[rb])
            dst = dstT[:, half * 4:half * 4 + 4, col0:col0 + 128]
            srcp = bank[:].rearrange("p (k n) -> p k n", k=4)
            if half == 0 or evac1 == 'act':
                P.op('act', lambda e, dst=dst, srcp=srcp: e.copy(out=dst, in_=srcp), reads=[rb], writes=[rdst])
            else:
                P.op('dve', lambda e, dst=dst, srcp=srcp: e.tensor_copy(out=dst, in_=srcp), reads=[rb], writes=[rdst])

    def layernorm_inplace(self, r, rr, gb='pool'):
        P = self.P
        bs, mv, rs = self.bs, self.mv, self.rs
        for c in range(2):
            P.op('dve', lambda e, c=c: e.bn_stats(out=bs[:, c, :], in_=r[:, c * 512:(c + 1) * 512]),
                 reads=[rr], writes=['bs'])
        P.op('dve', lambda e: e.bn_aggr(out=mv[:], in_=bs[:].rearrange("p a b -> p (a b)")), reads=['bs'], writes=['mv'])
        P.op('dve', lambda e: e.tensor_scalar(out=rs[:], in0=mv[:, 1:2], scalar1=EPS, scalar2=None, op0=ALU.add),
             reads=['mv'], writes=['rs'])
        P.op('act', lambda e: e.activation(out=rs[:], in_=rs[:], func=AF.Sqrt), reads=['rs'], writes=['rs'])
        P.op('dve', lambda e: e.reciprocal(out=rs[:], in_=rs[:]), reads=['rs'], writes=['rs'])
        P.op('dve', lambda e: e.tensor_scalar(out=r, in0=r, scalar1=mv[:, 0:1], scalar2=rs[:, 0:1],
                                              op0=ALU.subtract, op1=ALU.mult), reads=[rr, 'mv', 'rs'], writes=[rr])
        P.op(gb, lambda e: e.tensor_tensor(out=r, in0=r, in1=self.g_bc[:], op=ALU.mult), reads=[rr, 'g_bc'], writes=[rr])
        P.op(gb, lambda e: e.tensor_tensor(out=r, in0=r, in1=self.b_bc[:], op=ALU.add), reads=[rr, 'b_bc'], writes=[rr])

    def emit_adaln(self, sub):
        P, nc, A, ps = self.P, self.nc, self.A, self.ps
        L = sub // 2
        wsrc = (A['ada_mix_w'] if sub % 2 == 0 else A['ada_ffn_w'])[L]
        ones = self.ones
        with Stage(self, f'ada{sub}') as st:
            brow = st.T('brow', [1, 3 * D])
            lrow = st.T('lrow', [1, 2 * D])
            wch = [st.T(f'wch{i}', [128, 8, 512]) for i in range(2)]
            P.op('sp', lambda e: e.dma_start(out=brow[:], in_=A['ada_b'][sub:sub + 1, :]), writes=['brow'], dma=True)
            P.op('sp', lambda e: e.dma_start(out=lrow[:, 0:D], in_=A['ln_g'][sub:sub + 1, :]), writes=['lrow'], dma=True)
            P.op('sp', lambda e: e.dma_start(out=lrow[:, D:2 * D], in_=A['ln_b'][sub:sub + 1, :]), writes=['lrow'], dma=True)
            dsts = [self.shift_bc, self.shift_bc, self.scale_bc, self.scale_bc, self.gate_bc, self.gate_bc]
            names = ['shift_bc', 'shift_bc', 'scale_bc', 'scale_bc', 'gate_bc', 'gate_bc']
            for n6 in range(6):
                wb = wch[n6 % 2]
                rw = f'wch{n6 % 2}'
                P.op('sp', lambda e, wb=wb, n6=n6: e.dma_start(
                    out=wb[:], in_=wsrc[:, n6 * 512:(n6 + 1) * 512].rearrange("(k p) n -> p k n", p=128)),
                    writes=[rw], dma=True)
                bank = ps[n6 % 2]
                rb = f'ps{n6 % 2}'
                for k in range(8):
                    P.op('pe', lambda e, k=k, wb=wb, bank=bank: e.matmul(out=bank[:], lhsT=self.SC[:, k, :], rhs=wb[:, k, :],
                                                                        start=(k == 0), stop=False),
                         reads=['SC', rw], writes=[rb])
                P.op('pe', lambda e, bank=bank, n6=n6: e.matmul(out=bank[:], lhsT=ones[0:1, :], rhs=brow[0:1, n6 * 512:(n6 + 1) * 512],
                                                                start=False, stop=True),
                     reads=['ones', 'brow'], writes=[rb])
                dst = dsts[n6][:, (n6 % 2) * 512:(n6 % 2 + 1) * 512]
                if n6 in (2, 3):
                    P.op('dve', lambda e, dst=dst, bank=bank: e.tensor_scalar(out=dst, in0=bank[:], scalar1=1.0, scalar2=None, op0=ALU.add),
                         reads=[rb], writes=[names[n6]])
                else:
                    P.op('dve', lambda e, dst=dst, bank=bank: e.tensor_copy(out=dst, in_=bank[:]), reads=[rb], writes=[names[n6]])
            for j in range(4):
                bank = ps[2 + j % 2]
                rb = f'ps{2 + j % 2}'
                P.op('pe', lambda e, bank=bank, j=j: e.matmul(out=bank[:], lhsT=ones[0:1, :], rhs=lrow[0:1, j * 512:(j + 1) * 512],
                                                              start=True, stop=True), reads=['ones', 'lrow'], writes=[rb])
                dstt = self.g_bc if j < 2 else self.b_bc
                dst = dstt[:, (j % 2) * 512:(j % 2 + 1) * 512]
                P.op('act', lambda e, dst=dst, bank=bank: e.copy(out=dst, in_=bank[:]), reads=[rb],
                     writes=['g_bc' if j < 2 else 'b_bc'])

    def emit_conv1(self, xin):
        P, nc, A, ps = self.P, self.nc, self.A, self.ps
        ones = self.ones
        with Stage(self, 'c1') as st:
            win = st.T('win', [128, 8, 2048])
            cib = st.T('cib', [128, 16])
            dw = st.T('dw', [128, 8, 31])
            cv = st.T('cv', [128, 3, 8])
            xts = [st.T(f'xt{i}', [128, D]) for i in range(2)]
            ht = st.T('ht', [128, D])
            hT = st.T('hT', [128, 8, 512])
            acc = st.T('acc', [128, 8, 512])
            aexts = [st.T(f'aext{i}', [128, 8, 542]) for i in range(2)]
            sig = [st.T(f'sig{i}', [128, 512]) for i in range(2)]
            sq = [st.T(f'sq{i}', [128, 512]) for i in range(2)]
            meant = st.T('meant', [128, 512])
            rstd = st.T('rstd', [128, 512])
            tmp = st.T('tmp', [128, 512])
            for q in range(4):
                P.op('sp', lambda e, q=q: e.dma_start(out=win[:, :, q * 512:(q + 1) * 512],
                                                      in_=A['conv_in_w'][:, q * 512:(q + 1) * 512].rearrange("(k p) n -> p k n", p=128)),
                     writes=[('win', q)], dma=True)
            P.op('sp', lambda e: e.dma_start(out=cib[:], in_=A['conv_in_b_l']), writes=['cib'], dma=True)
            P.op('sp', lambda e: e.dma_start(out=dw[:], in_=A['conv_dw_w_l']), writes=['dw'], dma=True)
            P.op('sp', lambda e: e.dma_start(out=cv[:], in_=A['conv_vec_l']), writes=['cv'], dma=True)
            for cc in range(8):
                P.op('pool', lambda e, cc=cc: e.memset(aexts[0][:, cc, 0:30], 0.0), writes=[('aext0', cc)])
            accs_ = [acc, st.T('acc2', [128, 8, 512])]

            def load_T(jb):
                for tl in range(4):
                    ti = jb * 4 + tl
                    xt = xts[ti % 2]
                    rx = f'xt{ti % 2}'
                    P.op('sp', lambda e, xt=xt, ti=ti: e.dma_start(out=xt[:], in_=xin[ti * 128:(ti + 1) * 128, :]),
                         writes=[rx], dma=True)
                    self.modulate(xt[:], ht[:], rx, 'ht', eng0='pool')
                    self.transpose8(ht, 'ht', hT, 'hT', tl * 128, 0, evac1='act')

            def glu_chunk(jb, cc):
                aext = aexts[jb % 2]
                AX_ = f'aext{jb % 2}'
                pa, pb = ps[2 + (cc % 2) * 2], ps[3 + (cc % 2) * 2]
                ra, rb = f'ps{2 + (cc % 2) * 2}', f'ps{3 + (cc % 2) * 2}'
                for k in range(8):
                    P.op('pe', lambda e, k=k: e.matmul(out=pa[:], lhsT=win[:, k, cc * 128:(cc + 1) * 128], rhs=hT[:, k, :],
                                                       start=(k == 0), stop=(k == 7)),
                         reads=[('win', cc // 4), 'hT'], writes=[ra])
                for k in range(8):
                    P.op('pe', lambda e, k=k: e.matmul(out=pb[:], lhsT=win[:, k, D + cc * 128:D + (cc + 1) * 128], rhs=hT[:, k, :],
                                                       start=(k == 0), stop=(k == 7)),
                         reads=[('win', 2 + cc // 4), 'hT'], writes=[rb])
                sg = sig[cc % 2]
                rsg = f'sig{cc % 2}'
                P.op('act', lambda e: e.activation(out=sg[:], in_=pb[:], func=AF.Sigmoid, bias=cib[:, 8 + cc:9 + cc], scale=1.0),
                     reads=[rb, 'cib'], writes=[rsg])
                P.op('dve', lambda e: e.scalar_tensor_tensor(out=aext[:, cc, 30:542], in0=pa[:], scalar=cib[:, cc:cc + 1], in1=sg[:],
                                                             op0=ALU.add, op1=ALU.mult),
                     reads=[ra, rsg, 'cib'], writes=[(AX_, cc)])

            def conv_pair(jb, c0):
                aext, anext = aexts[jb % 2], aexts[(jb + 1) % 2]
                AX_, AN_ = f'aext{jb % 2}', f'aext{(jb + 1) % 2}'
                ac = accs_[jb % 2]
                RA = f'acc{jb % 2}'
                for cc in (c0, c0 + 1):
                    P.op('dve', lambda e, cc=cc: e.tensor_scalar(out=ac[:, cc, :], in0=aext[:, cc, 0:512], scalar1=dw[:, cc, 0:1], scalar2=cv[:, 0, cc:cc + 1],
                                                                 op0=ALU.mult, op1=ALU.add),
                         reads=[(AX_, cc), 'dw', 'cv'], writes=[(RA, cc)])
                for w in range(1, 31):
                    for cc in (c0, c0 + 1):
                        P.op('dve', lambda e, w=w, cc=cc: e.scalar_tensor_tensor(out=ac[:, cc, :], in0=aext[:, cc, w:w + 512], scalar=dw[:, cc, w:w + 1],
                                                                                 in1=ac[:, cc, :], op0=ALU.mult, op1=ALU.add),
                             reads=[(AX_, cc), 'dw', (RA, cc)], writes=[(RA, cc)])
                for cc in (c0, c0 + 1):
                    P.op('act', lambda e, cc=cc: e.copy(out=anext[:, cc, 0:30], in_=aext[:, cc, 512:542]),
                         reads=[(AX_, cc)], writes=[(AN_, cc)])

            def stats_pe(jb):
                ac, RA = accs_[jb % 2], f'acc{jb % 2}'
                for cc in range(8):
                    s2 = sq[cc % 2]
                    rs2 = f'sq{cc % 2}'
                    P.op('act', lambda e, cc=cc, s2=s2: e.activation(out=s2[:], in_=ac[:, cc, :], func=AF.Square),
                         reads=[(RA, cc)], writes=[rs2])
                    P.op('pe', lambda e, cc=cc: e.matmul(out=ps[6][:], lhsT=ones[:], rhs=ac[:, cc, :], start=(cc == 0), stop=(cc == 7)),
                         reads=['ones', (RA, cc)], writes=['ps6'])
                    P.op('pe', lambda e, cc=cc, s2=s2: e.matmul(out=ps[7][:], lhsT=ones[:], rhs=s2[:], start=(cc == 0), stop=(cc == 7)),
                         reads=['ones', rs2], writes=['ps7'])
                P.op('act', lambda e: e.activation(out=meant[:], in_=ps[6][:], func=AF.Copy, scale=1.0 / D), reads=['ps6'], writes=['meant'])

            def stats_dve(jb):
                P.op('dve', lambda e: e.tensor_tensor(out=tmp[:], in0=meant[:], in1=meant[:], op=ALU.mult), reads=['meant'], writes=['tmp'])
                P.op('dve', lambda e: e.scalar_tensor_tensor(out=rstd[:], in0=ps[7][:], scalar=1.0 / D, in1=tmp[:], op0=ALU.mult, op1=ALU.subtract),
                     reads=['ps7', 'tmp'], writes=['rstd'])
                P.op('dve', lambda e: e.tensor_scalar(out=rstd[:], in0=rstd[:], scalar1=EPS, scalar2=None, op0=ALU.add), reads=['rstd'], writes=['rstd'])
                P.op('act', lambda e: e.activation(out=rstd[:], in_=rstd[:], func=AF.Sqrt), reads=['rstd'], writes=['rstd'])
                P.op('dve', lambda e: e.reciprocal(out=rstd[:], in_=rstd[:]), reads=['rstd'], writes=['rstd'])

            def norm_chunk(jb, cc):
                ac, RA = accs_[jb % 2], f'acc{jb % 2}'
                P.op('pool', lambda e: e.tensor_tensor(out=ac[:, cc, :], in0=ac[:, cc, :], in1=meant[:], op=ALU.subtract),
                     reads=[(RA, cc), 'meant'], writes=[(RA, cc)])
                P.op('pool', lambda e: e.tensor_tensor(out=ac[:, cc, :], in0=ac[:, cc, :], in1=rstd[:], op=ALU.mult),
                     reads=[(RA, cc), 'rstd'], writes=[(RA, cc)])
                P.op('act', lambda e: e.activation(out=ac[:, cc, :], in_=ac[:, cc, :], func=AF.Silu,
                                                   bias=cv[:, 2, cc:cc + 1], scale=cv[:, 1, cc:cc + 1]),
                     reads=[(RA, cc), 'cv'], writes=[(RA, cc)])

            def store_block(jb):
                ac, RA = accs_[jb % 2], f'acc{jb % 2}'
                P.op('act', lambda e: e.dma_start(out=A['ST'][:, :, jb * 512:(jb + 1) * 512].rearrange("c p t -> p c t"), in_=ac[:]),
                     reads=[(RA, cc) for cc in range(8)], writes=[('ST', jb)], dma=True)

            load_T(0)
            for cc in range(8):
                glu_chunk(0, cc)
            for jb in range(8):
                if jb + 1 < 8:
                    load_T(jb + 1)
                for c0 in range(0, 8, 2):
                    conv_pair(jb, c0)
                    if jb + 1 < 8:
                        glu_chunk(jb + 1, c0)
                        glu_chunk(jb + 1, c0 + 1)
                stats_pe(jb)
                stats_dve(jb)
                for cc in range(8):
                    norm_chunk(jb, cc)
                store_block(jb)

    def emit_proj_out(self, xin, dst, w_ap, b_ap, src_fm=None, src_tm=None):
        P, nc, A, ps = self.P, self.nc, self.A, self.ps
        ones = self.ones
        NBUF = 4
        with Stage(self, 'po') as st:
            wo = st.T('wo', [128, 8, D], F32R)
            wstg = st.T('wstg', [128, 8, 512])
            bo = st.T('bo', [1, D])
            bo_bc = st.T('bo_bc', [128, D])
            xts = [st.T(f'xt{i}', [128, D]) for i in range(NBUF)]
            rts = [st.T(f'rt{i}', [128, D]) for i in range(NBUF)]
            bss = [st.T(f'bs{i}', [128, 2, 6]) for i in range(NBUF)]
            mvs = [st.T(f'mv{i}', [128, 2]) for i in range(NBUF)]
            rss = [st.T(f'rs{i}', [128, 1]) for i in range(NBUF)]
            sT = [st.T(f'sT{i}', [128, 8, 512], F32R) for i in range(2)]
            sstg = st.T('sstg', [128, 8, 512])
            for q in range(2):
                P.op('sp', lambda e, q=q: e.dma_start(out=wstg[:], in_=w_ap[:, q * 512:(q + 1) * 512].rearrange("(k p) n -> p k n", p=128)),
                     writes=['wstg'], dma=True)
                P.op('pool', lambda e, q=q: e.tensor_copy(out=wo[:, :, q * 512:(q + 1) * 512], in_=wstg[:]), reads=['wstg'], writes=[('wo', q)])
            P.op('sp', lambda e: e.dma_start(out=bo[:], in_=b_ap), writes=['bo'], dma=True)
            for half in range(2):
                P.op('pe', lambda e, half=half: e.matmul(out=ps[half][:], lhsT=ones[0:1, :], rhs=bo[0:1, half * 512:(half + 1) * 512], start=True, stop=True),
                     reads=['ones', 'bo'], writes=[f'ps{half}'])
                P.op('act', lambda e, half=half: e.copy(out=bo_bc[:, half * 512:(half + 1) * 512], in_=ps[half][:]), reads=[f'ps{half}'], writes=['bo_bc'])

            def phaseA(ti):
                b = ti % NBUF
                xt, rt = xts[b], rts[b]
                rx, rr = f'xt{b}', f'rt{b}'
                P.op('sp', lambda e: e.dma_start(out=xt[:], in_=xin[ti * 128:(ti + 1) * 128, :]), writes=[rx], dma=True)
                jb, tl = ti // 4, ti % 4
                sb = sT[jb % 2]
                rsb = f'sT{jb % 2}'
                if tl == 0:
                    P.op('sp', lambda e: e.dma_start(out=sstg[:], in_=src_fm[:, :, jb * 512:(jb + 1) * 512].rearrange("c p t -> p c t")),
                         reads=[('ST', jb)], writes=['sstg'], dma=True)
                    P.op('act', lambda e: e.copy(out=sb[:], in_=sstg[:]), reads=['sstg'], writes=[rsb])
                pb = (ti % 4) * 2
                for half in range(2):
                    bank = ps[pb + half]
                    rb = f'ps{pb + half}'
                    for k in range(8):
                        P.op('pe', lambda e, k=k, bank=bank, half=half: e.matmul(out=bank[:], lhsT=sb[:, k, tl * 128:(tl + 1) * 128],
                                                                                rhs=wo[:, k, half * 512:(half + 1) * 512], start=(k == 0), stop=(k == 7)),
                             reads=[rsb, ('wo', half)], writes=[rb])
                    hs = slice(half * 512, (half + 1) * 512)
                    P.op('dve', lambda e, bank=bank, hs=hs: e.tensor_tensor(out=rt[:, hs], in0=bank[:], in1=bo_bc[:, hs], op=ALU.add),
                         reads=[rb, 'bo_bc'], writes=[(rr, half)])
                    P.op('pool', lambda e, hs=hs: e.tensor_tensor(out=rt[:, hs], in0=rt[:, hs], in1=self.gate_bc[:, hs], op=ALU.mult),
                         reads=[(rr, half), 'gate_bc'], writes=[(rr, half)])

            def phaseB(ti):
                b = ti % NBUF
                xt, rt, bs, mv, rs = xts[b], rts[b], bss[b], mvs[b], rss[b]
                rx, rr = f'xt{b}', f'rt{b}'
                P.op('dve', lambda e: e.scalar_tensor_tensor(out=rt[:], in0=xt[:], scalar=ALPHA, in1=rt[:], op0=ALU.mult, op1=ALU.add),
                     reads=[rx, (rr, 0), (rr, 1)], writes=[rr])
                for c in range(2):
                    P.op('dve', lambda e, c=c: e.bn_stats(out=bs[:, c, :], in_=rt[:, c * 512:(c + 1) * 512]), reads=[rr], writes=[(f'bs{b}', c)])
                P.op('dve', lambda e: e.bn_aggr(out=mv[:], in_=bs[:].rearrange("p a b -> p (a b)")), reads=[(f'bs{b}', 0), (f'bs{b}', 1)], writes=[f'mv{b}'])
                P.op('dve', lambda e: e.tensor_scalar(out=rs[:], in0=mv[:, 1:2], scalar1=EPS, scalar2=None, op0=ALU.add), reads=[f'mv{b}'], writes=[f'rs{b}'])
                P.op('act', lambda e: e.activation(out=rs[:], in_=rs[:], func=AF.Sqrt), reads=[f'rs{b}'], writes=[f'rs{b}'])

            def phaseC(ti):
                b = ti % NBUF
                rt, mv, rs = rts[b], mvs[b], rss[b]
                rr = f'rt{b}'
                P.op('dve', lambda e: e.reciprocal(out=rs[:], in_=rs[:]), reads=[f'rs{b}'], writes=[f'rs{b}'])
                P.op('dve', lambda e: e.tensor_scalar(out=rt[:], in0=rt[:], scalar1=mv[:, 0:1], scalar2=rs[:, 0:1],
                                                      op0=ALU.subtract, op1=ALU.mult), reads=[rr, f'mv{b}', f'rs{b}'], writes=[rr])
                P.op('pool', lambda e: e.tensor_tensor(out=rt[:], in0=rt[:], in1=self.g_bc[:], op=ALU.mult), reads=[rr, 'g_bc'], writes=[rr])
                P.op('pool', lambda e: e.tensor_tensor(out=rt[:], in0=rt[:], in1=self.b_bc[:], op=ALU.add), reads=[rr, 'b_bc'], writes=[rr])
                P.op('act', lambda e: e.dma_start(out=dst[ti * 128:(ti + 1) * 128, :], in_=rt[:]), reads=[rr], writes=[(rr, 0), (rr, 1)], dma=True)

            for i in range(NT + 2):
                if i < NT:
                    phaseA(i)
                if 0 <= i - 1 < NT:
                    phaseB(i - 1)
                if 0 <= i - 2 < NT:
                    phaseC(i - 2)

    def emit_peer1(self, xin, L):
        P, nc, A, ps = self.P, self.nc, self.A, self.ps
        with Stage(self, f'p1{L}') as st:
            wq = st.T('wq', [128, 8, 2048])
            skT = st.T('skT', [128, 2, 128])
            xts = [st.T(f'xt{i}', [128, D]) for i in range(2)]
            ht = st.T('ht', [128, D])
            hT = st.T('hT', [128, 8, 256])
            qT = st.T('qT', [128, 16, 256])
            sc = st.T('sc', [128, 16, 128])
            m = st.T('m', [128, 16, 16])
            ix = st.T('ix', [128, 16, 16], U32)
            ixf = st.T('ixf', [128, 16, 16])
            wk = st.T('wk', [128, 16, 128])
            cand = st.T('cand', [128, 8, 256])
            candi = st.T('candi', [128, 8, 256])
            wk2 = st.T('wk2', [128, 8, 256])
            junk = [st.T(f'junk{i}', [128, 256]) for i in range(2)]
            ts = st.T('ts', [128, 8, 16])
            ef = st.T('ef', [128, 128])
            ei = [st.T(f'ei{i}', [128, 128], I32) for i in range(2)]
            gt = [st.T(f'gt{i}', [128, 8, 16]) for i in range(2)]
            gsum = st.T('gsum', [128, 8])
            for q in range(4):
                P.op('sp', lambda e, q=q: e.dma_start(out=wq[:, :, q * 512:(q + 1) * 512],
                                                      in_=A['peer_query_w'][L][:, q * 512:(q + 1) * 512].rearrange("(k p) n -> p k n", p=128)),
                     writes=[('wq', q)], dma=True)
            P.op('sp', lambda e: e.dma_start(out=skT[:], in_=A['peer_skT'][L].rearrange("h d k -> d h k")), writes=['skT'], dma=True)
            ti = 0
            for jb in range(S // 256):
                for tl in range(2):
                    xt = xts[ti % 2]
                    rx = f'xt{ti % 2}'
                    P.op('sp', lambda e, xt=xt, ti=ti: e.dma_start(out=xt[:], in_=xin[ti * 128:(ti + 1) * 128, :]), writes=[rx], dma=True)
                    self.modulate(xt[:], ht[:], rx, 'ht')
                    self.transpose8(ht, 'ht', hT, 'hT', tl * 128, 0)
                    ti += 1
                for c in range(16):
                    bank = ps[2 + c % 2]
                    rb = f'ps{2 + c % 2}'
                    for k in range(8):
                        P.op('pe', lambda e, k=k, c=c, bank=bank: e.matmul(out=bank[:, 0:256], lhsT=wq[:, k, c * 128:(c + 1) * 128], rhs=hT[:, k, :],
                                                                          start=(k == 0), stop=(k == 7)),
                             reads=[('wq', c // 4), 'hT'], writes=[rb])
                    if c % 2 == 0:
                        P.op('act', lambda e, c=c, bank=bank: e.copy(out=qT[:, c, :], in_=bank[:, 0:256]), reads=[rb], writes=[('qT', c)])
                    else:
                        P.op('dve', lambda e, c=c, bank=bank: e.tensor_copy(out=qT[:, c, :], in_=bank[:, 0:256]), reads=[rb], writes=[('qT', c)])
                for tl in range(2):
                    tix = jb * 2 + tl
                    for c in range(16):
                        bank = ps[4 + c // 4]
                        rb = f'ps{4 + c // 4}'
                        P.op('pe', lambda e, c=c, bank=bank, tl=tl: e.matmul(out=bank[:, (c % 4) * 128:(c % 4 + 1) * 128],
                                                                            lhsT=qT[:, c, tl * 128:(tl + 1) * 128], rhs=skT[:, c % 2, :],
                                                                            start=True, stop=True),
                             reads=[('qT', c), 'skT'], writes=[rb])
                    for g4 in range(4):
                        P.op('act', lambda e, g4=g4: e.copy(out=sc[:, g4 * 4:(g4 + 1) * 4, :], in_=ps[4 + g4][:].rearrange("p (a k) -> p a k", a=4)),
                             reads=[f'ps{4 + g4}'], writes=['sc'])
                    for c in range(16):
                        P.op('dve', lambda e, c=c: e.max(out=m[:, c, 0:8], in_=sc[:, c, :]), reads=['sc'], writes=[('m0', c)])
                    P.fence('dve')
                    for c in range(16):
                        P.op('dve', lambda e, c=c: e.max_index(out=ix[:, c, 0:8], in_max=m[:, c, 0:8], in_values=sc[:, c, :]),
                             reads=['sc', ('m0', c)], writes=[('ix0', c)])
                        P.op('dve', lambda e, c=c: e.match_replace(out=wk[:, c, :], in_to_replace=m[:, c, 0:8], in_values=sc[:, c, :], imm_value=-1e30),
                             reads=['sc', ('m0', c)], writes=[('wk', c)])
                    P.fence('dve')
                    for c in range(16):
                        P.op('dve', lambda e, c=c: e.max(out=m[:, c, 8:16], in_=wk[:, c, :]), reads=[('wk', c)], writes=[('m1', c)])
                    P.fence('dve')
                    for c in range(16):
                        P.op('dve', lambda e, c=c: e.max_index(out=ix[:, c, 8:16], in_max=m[:, c, 8:16], in_values=wk[:, c, :]),
                             reads=[('wk', c), ('m1', c)], writes=[('ix1', c)])
                    P.fence('dve')
                    mres = [('m0', c) for c in range(16)] + [('m1', c) for c in range(16)]
                    ixres = [('ix0', c) for c in range(16)] + [('ix1', c) for c in range(16)]
                    P.op('dve', lambda e: e.tensor_copy(out=ixf[:], in_=ix[:]), reads=ixres, writes=['ixf'])
                    m4 = m[:].rearrange("p (h two) k -> p h two k", two=2)
                    i4 = ixf[:].rearrange("p (h two) k -> p h two k", two=2)
                    c4 = cand[:].rearrange("p h (a b) -> p h a b", a=16)
                    ci4 = candi[:].rearrange("p h (a b) -> p h a b", a=16)
                    P.op('dve', lambda e: e.tensor_tensor(out=c4, in0=m4[:, :, 0, :].unsqueeze(3).to_broadcast([128, 8, 16, 16]),
                                                          in1=m4[:, :, 1, :].unsqueeze(2).to_broadcast([128, 8, 16, 16]), op=ALU.add),
                         reads=mres, writes=['cand'])
                    P.op('dve', lambda e: e.tensor_scalar(out=i4[:, :, 0, :], in0=i4[:, :, 0, :], scalar1=128.0, scalar2=None, op0=ALU.mult),
                         reads=['ixf'], writes=['ixf'])
                    P.op('dve', lambda e: e.tensor_tensor(out=ci4, in0=i4[:, :, 0, :].unsqueeze(3).to_broadcast([128, 8, 16, 16]),
                                                          in1=i4[:, :, 1, :].unsqueeze(2).to_broadcast([128, 8, 16, 16]), op=ALU.add),
                         reads=['ixf'], writes=['candi'])
                    for h in range(8):
                        P.op('dve', lambda e, h=h: e.max(out=ts[:, h, 0:8], in_=cand[:, h, :]), reads=['cand'], writes=[('ts0', h)])
                    P.fence('dve')
                    for h in range(8):
                        P.op('dve', lambda e, h=h: e.match_replace(out=wk2[:, h, :], in_to_replace=ts[:, h, 0:8], in_values=cand[:, h, :], imm_value=-1e30),
                             reads=['cand', ('ts0', h)], writes=[('wk2', h)])
                    P.fence('dve')
                    for h in range(8):
                        P.op('dve', lambda e, h=h: e.max(out=ts[:, h, 8:16], in_=wk2[:, h, :]), reads=[('wk2', h)], writes=[('ts1', h)])
                    P.fence('dve')
                    tsres = [('ts0', h) for h in range(8)] + [('ts1', h) for h in range(8)]
                    for h in range(8):
                        for k in range(16):
                            P.op('dve', lambda e, h=h, k=k: e.scalar_tensor_tensor(out=junk[(h * 16 + k) % 2][:], in0=cand[:, h, :], scalar=ts[:, h, k:k + 1], in1=candi[:, h, :],
                                                                                   op0=ALU.is_equal, op1=ALU.mult, accum_out=ef[:, h * 16 + k:h * 16 + k + 1]),
                                 reads=['cand', 'candi', ('ts0', h), ('ts1', h)], writes=[('ef', h * 16 + k)])
                    P.fence('dve')
                    eib = ei[tix % 2]
                    rei = f'ei{tix % 2}'
                    gtb = gt[tix % 2]
                    rgt = f'gt{tix % 2}'
                    P.op('dve', lambda e: e.tensor_scalar(out=ef[:], in0=ef[:], scalar1=float(NEXP - 1), scalar2=float(L * NEXP), op0=ALU.min, op1=ALU.add),
                         reads=[('ef', q) for q in range(128)], writes=['ef'])
                    P.op('dve', lambda e, eib=eib: e.tensor_copy(out=eib[:], in_=ef[:]), reads=['ef'], writes=[rei])
                    P.op('dve', lambda e, gtb=gtb: e.tensor_tensor(out=gtb[:], in0=ts[:], in1=ts[:, :, 0:1].to_broadcast([128, 8, 16]), op=ALU.subtract),
                         reads=tsres, writes=[rgt])
                    P.op('act', lambda e, gtb=gtb: e.activation(out=gtb[:], in_=gtb[:], func=AF.Exp), reads=[rgt], writes=[rgt])
                    P.op('dve', lambda e, gtb=gtb: e.tensor_reduce(out=gsum[:], in_=gtb[:], axis=AX.X, op=ALU.add), reads=[rgt], writes=['gsum'])
                    P.op('dve', lambda e: e.reciprocal(out=gsum[:], in_=gsum[:]), reads=['gsum'], writes=['gsum'])
                    P.op('dve', lambda e, gtb=gtb: e.tensor_tensor(out=gtb[:], in0=gtb[:], in1=gsum[:].unsqueeze(2).to_broadcast([128, 8, 16]), op=ALU.mult),
                         reads=[rgt, 'gsum'], writes=[rgt])
                    P.op('sp', lambda e, eib=eib, tix=tix: e.dma_start(out=A['IDX'][tix * 128:(tix + 1) * 128, :], in_=eib[:]),
                         reads=[rei], writes=[('IDX', tix)], dma=True)
                    P.op('sp', lambda e, gtb=gtb, tix=tix: e.dma_start(out=A['GATE'][tix * 128:(tix + 1) * 128, :], in_=gtb[:].rearrange("p h k -> p (h k)")),
                         reads=[rgt], writes=[('GATE', tix)], dma=True)

    def emit_peer2(self, xin, dst, L):
        P, nc, A, ps = self.P, self.nc, self.A, self.ps
        NB = self.cfg.get('nb', 14)
        U = A['peer_u'].rearrange("l e d -> (l e) d")
        V = A['peer_v'].rearrange("l e d -> (l e) d")
        with Stage(self, f'p2{L}') as st:
            xts = [st.T(f'xt{i}', [128, D]) for i in range(2)]
            hts = [st.T(f'ht{i}', [128, D]) for i in range(2)]
            eis = [st.T(f'ei{i}', [128, 128], I32) for i in range(2)]
            gts = [st.T(f'gt{i}', [128, 128]) for i in range(2)]
            ub = [st.T(f'ub{i}', [128, D]) for i in range(NB)]
            vb = [st.T(f'vb{i}', [128, D]) for i in range(NB)]
            junk = st.T('junk', [128, D])
            apre = st.T('apre', [128, 128])
            coef = st.T('coef', [128, 128])
            accs = [st.T(f'acc{i}', [128, D]) for i in range(2)]
            nu = nv = 0
            for ti in range(NT):
                b = ti % 2
                xt, ht, eib, gtb, acc = xts[b], hts[b], eis[b], gts[b], accs[b]
                rx, rh, rei, rgt, racc = f'xt{b}', f'ht{b}', f'ei{b}', f'gt{b}', f'acc{b}'
                P.op('sp', lambda e, xt=xt, ti=ti: e.dma_start(out=xt[:], in_=xin[ti * 128:(ti + 1) * 128, :]), writes=[rx], dma=True)
                P.op('sp', lambda e, eib=eib, ti=ti: e.dma_start(out=eib[:], in_=A['IDX'][ti * 128:(ti + 1) * 128, :]),
                     reads=[('IDX', ti)], writes=[rei], dma=True)
                P.op('sp', lambda e, gtb=gtb, ti=ti: e.dma_start(out=gtb[:], in_=A['GATE'][ti * 128:(ti + 1) * 128, :]),
                     reads=[('GATE', ti)], writes=[rgt], dma=True)
                self.modulate(xt[:], ht[:], rx, rh)
                for k in range(128):
                    s = nu % NB
                    nu += 1
                    P.op('pool', lambda e, s=s, k=k, eib=eib: e.indirect_dma_start(
                        out=ub[s][:], out_offset=None, in_=U,
                        in_offset=bass.IndirectOffsetOnAxis(ap=eib[:, k:k + 1], axis=0)),
                        reads=[rei], writes=[('ub', s)], dma=True)
                    P.op('dve', lambda e, s=s, k=k, ht=ht: e.scalar_tensor_tensor(out=junk[:], in0=ub[s][:], scalar=1.0, in1=ht[:], op0=ALU.mult, op1=ALU.mult,
                                                                               accum_out=apre[:, k:k + 1]),
                         reads=[('ub', s), rh], writes=['junk', 'apre'])
                P.op('act', lambda e: e.activation(out=coef[:], in_=apre[:], func=AF.Gelu), reads=['apre'], writes=['coef'])
                P.op('dve', lambda e, gtb=gtb: e.tensor_tensor(out=coef[:], in0=coef[:], in1=gtb[:], op=ALU.mult), reads=['coef', rgt], writes=['coef'])
                for k in range(128):
                    s = nv % NB
                    nv += 1
                    P.op('pool', lambda e, s=s, k=k, eib=eib: e.indirect_dma_start(
                        out=vb[s][:], out_offset=None, in_=V,
                        in_offset=bass.IndirectOffsetOnAxis(ap=eib[:, k:k + 1], axis=0)),
                        reads=[rei], writes=[('vb', s)], dma=True)
                    if k == 0:
                        P.op('dve', lambda e, s=s, acc=acc: e.tensor_scalar(out=acc[:], in0=vb[s][:], scalar1=coef[:, 0:1], scalar2=None, op0=ALU.mult),
                             reads=[('vb', s), 'coef'], writes=[racc])
                    else:
                        P.op('dve', lambda e, s=s, k=k, acc=acc: e.scalar_tensor_tensor(out=acc[:], in0=vb[s][:], scalar=coef[:, k:k + 1], in1=acc[:],
                                                                                      op0=ALU.mult, op1=ALU.add),
                             reads=[('vb', s), 'coef', racc], writes=[racc])
                P.op('pool', lambda e, acc=acc: e.tensor_tensor(out=acc[:], in0=acc[:], in1=self.gate_bc[:], op=ALU.mult), reads=[racc, 'gate_bc'], writes=[racc])
                P.op('dve', lambda e, acc=acc, xt=xt: e.scalar_tensor_tensor(out=acc[:], in0=xt[:], scalar=ALPHA, in1=acc[:], op0=ALU.mult, op1=ALU.add),
                     reads=[rx, racc], writes=[racc])
                self.layernorm_inplace(acc[:], racc)
                P.op('sp', lambda e, acc=acc, ti=ti: e.dma_start(out=dst[ti * 128:(ti + 1) * 128, :], in_=acc[:]), reads=[racc], dma=True)

    def emit_peer1a(self, xin, L, cast=False):
        P, nc, A, ps = self.P, self.nc, self.A, self.ps
        with Stage(self, f'pa{L}') as st:
            cg = self.cast_gen(st, L) if cast else None
            wq = st.T('wq', [128, 8, 2048])
            skT = st.T('skT', [128, 2, 128])
            xts = [st.T(f'xt{i}', [128, D]) for i in range(2)]
            hts = [st.T(f'ht{i}', [128, D]) for i in range(2)]
            hT = st.T('hT', [128, 8, 256])
            qT = st.T('qT', [128, 16, 256])
            sco = [st.T(f'sco{i}', [128, 2048]) for i in range(2)]
            for q in range(4):
                P.op('sp', lambda e, q=q: e.dma_start(out=wq[:, :, q * 512:(q + 1) * 512],
                                                      in_=A['peer_query_w'][L][:, q * 512:(q + 1) * 512].rearrange("(k p) n -> p k n", p=128)),
                     writes=[('wq', q)], dma=True)
            P.op('sp', lambda e: e.dma_start(out=skT[:], in_=A['peer_skT'][L].rearrange("h d k -> d h k")), writes=['skT'], dma=True)
            ti = 0
            for jb in range(S // 256):
                for tl in range(2):
                    xt, ht = xts[ti % 2], hts[ti % 2]
                    rx, rh = f'xt{ti % 2}', f'ht{ti % 2}'
                    P.op('sp', lambda e, xt=xt, ti=ti: e.dma_start(out=xt[:], in_=xin[ti * 128:(ti + 1) * 128, :]), writes=[rx], dma=True)
                    self.modulate(xt[:], ht[:], rx, rh)
                    P.op('sp', lambda e, ht=ht, ti=ti: e.dma_start(out=A['H'][ti * 128:(ti + 1) * 128, :], in_=ht[:]), reads=[rh], writes=[('H', ti)], dma=True)
                    self.transpose8(ht, rh, hT, 'hT', tl * 128, 0)
                    ti += 1
                for c in range(16):
                    bank = ps[2 + c % 2]
                    rb = f'ps{2 + c % 2}'
                    for k in range(8):
                        P.op('pe', lambda e, k=k, c=c, bank=bank: e.matmul(out=bank[:, 0:256], lhsT=wq[:, k, c * 128:(c + 1) * 128], rhs=hT[:, k, :],
                                                                          start=(k == 0), stop=(k == 7)),
                             reads=[('wq', c // 4), 'hT'], writes=[rb])
                    if c % 2 == 0:
                        P.op('act', lambda e, c=c, bank=bank: e.copy(out=qT[:, c, :], in_=bank[:, 0:256]), reads=[rb], writes=[('qT', c)])
                    else:
                        P.op('dve', lambda e, c=c, bank=bank: e.tensor_copy(out=qT[:, c, :], in_=bank[:, 0:256]), reads=[rb], writes=[('qT', c)])
                for tl in range(2):
                    tix = jb * 2 + tl
                    so = sco[tix % 2]
                    rso = f'sco{tix % 2}'
                    for c in range(16):
                        bank = ps[4 + c // 4]
                        rb = f'ps{4 + c // 4}'
                        P.op('pe', lambda e, c=c, bank=bank, tl=tl: e.matmul(out=bank[:, (c % 4) * 128:(c % 4 + 1) * 128],
                                                                            lhsT=qT[:, c, tl * 128:(tl + 1) * 128], rhs=skT[:, c % 2, :],
                                                                            start=True, stop=True),
                             reads=[('qT', c), 'skT'], writes=[rb])
                    for g4 in range(4):
                        eng = ('act', 'dve')[g4 % 2]
                        if eng == 'act':
                            P.op('act', lambda e, g4=g4, so=so: e.copy(out=so[:, g4 * 512:(g4 + 1) * 512], in_=ps[4 + g4][:]), reads=[f'ps{4 + g4}'], writes=[rso])
                        else:
                            P.op('dve', lambda e, g4=g4, so=so: e.tensor_copy(out=so[:, g4 * 512:(g4 + 1) * 512], in_=ps[4 + g4][:]), reads=[f'ps{4 + g4}'], writes=[rso])
                    P.op('sp', lambda e, so=so, tix=tix: e.dma_start(out=A['SCR'][tix * 128:(tix + 1) * 128, :], in_=so[:]), reads=[rso], writes=[('SCR', tix)], dma=True)
                    if cg is not None:
                        for _ in range(2):
                            try:
                                next(cg)
                            except StopIteration:
                                cg = None
                                break
            while cg is not None:
                try:
                    next(cg)
                except StopIteration:
                    cg = None

    def emit_peer2f(self, xin, dst, L):
        P, nc, A, ps = self.P, self.nc, self.A, self.ps
        NB = self.cfg.get('nb', 11)
        U = A['peer_u'].rearrange("l e d -> (l e) d")
        V = A['peer_v'].rearrange("l e d -> (l e) d")
        with Stage(self, f'pf{L}') as st:
            scs = [st.T(f'sc{i}', [128, 16, 128]) for i in range(2)]
            m = st.T('m', [128, 16, 16])
            ix = st.T('ix', [128, 16, 16], U32)
            ixf = st.T('ixf', [128, 16, 16])
            wk = st.T('wk', [128, 16, 128])
            cand = st.T('cand', [128, 8, 256])
            candi = st.T('candi', [128, 8, 256])
            wk2 = st.T('wk2', [128, 8, 256])
            junk2 = [st.T(f'junk2{i}', [128, 256]) for i in range(2)]
            ts = st.T('ts', [128, 8, 16])
            ef = st.T('ef', [128, 128])
            eis = [st.T(f'ei{i}', [128, 128], I32) for i in range(2)]
            gts = [st.T(f'gt{i}', [128, 8, 16]) for i in range(2)]
            gsum = st.T('gsum', [128, 8])
            xts = [st.T(f'xt{i}', [128, D]) for i in range(2)]
            hts = [st.T(f'ht{i}', [128, D]) for i in range(2)]
            ub = [st.T(f'ub{i}', [128, D]) for i in range(NB)]
            vb = [st.T(f'vb{i}', [128, D]) for i in range(NB)]
            junk = st.T('junk', [128, D])
            apre = st.T('apre', [128, 128])
            coef = st.T('coef', [128, 128])
            accs = [st.T(f'acc{i}', [128, D]) for i in range(2)]

            def load_sc(t):
                P.op('sp', lambda e, t=t: e.dma_start(out=scs[t % 2][:].rearrange("p a k -> p (a k)"), in_=A['SCR'][t * 128:(t + 1) * 128, :]),
                     writes=[f'sc{t % 2}'], dma=True)

            def load_xh(t):
                P.op('sp', lambda e, t=t: e.dma_start(out=xts[t % 2][:], in_=xin[t * 128:(t + 1) * 128, :]), writes=[f'xt{t % 2}'], dma=True)
                P.op('sp', lambda e, t=t: e.dma_start(out=hts[t % 2][:], in_=A['H'][t * 128:(t + 1) * 128, :]), writes=[f'ht{t % 2}'], dma=True)

            def topk_gen(t):
                sc = scs[t % 2]
                rsc = f'sc{t % 2}'
                eib, gtb = eis[t % 2], gts[t % 2]
                rei, rgt = f'ei{t % 2}', f'gt{t % 2}'
                for c in range(16):
                    P.op('dve', lambda e, c=c: e.max(out=m[:, c, 0:8], in_=sc[:, c, :]), reads=[rsc], writes=[('m0', c)])
                    yield
                P.fence('dve')
                for c in range(16):
                    P.op('dve', lambda e, c=c: e.max_index(out=ix[:, c, 0:8], in_max=m[:, c, 0:8], in_values=sc[:, c, :]),
                         reads=[rsc, ('m0', c)], writes=[('ix0', c)])
                    yield
                    P.op('dve', lambda e, c=c: e.match_replace(out=wk[:, c, :], in_to_replace=m[:, c, 0:8], in_values=sc[:, c, :], imm_value=-1e30),
                         reads=[rsc, ('m0', c)], writes=[('wk', c)])
                    yield
                P.fence('dve')
                for c in range(16):
                    P.op('dve', lambda e, c=c: e.max(out=m[:, c, 8:16], in_=wk[:, c, :]), reads=[('wk', c)], writes=[('m1', c)])
                    yield
                P.fence('dve')
                for c in range(16):
                    P.op('dve', lambda e, c=c: e.max_index(out=ix[:, c, 8:16], in_max=m[:, c, 8:16], in_values=wk[:, c, :]),
                         reads=[('wk', c), ('m1', c)], writes=[('ix1', c)])
                    yield
                P.fence('dve')
                mres = [('m0', c) for c in range(16)] + [('m1', c) for c in range(16)]
                ixres = [('ix0', c) for c in range(16)] + [('ix1', c) for c in range(16)]
                P.op('dve', lambda e: e.tensor_copy(out=ixf[:], in_=ix[:]), reads=ixres, writes=['ixf'])
                yield
                m4 = m[:].rearrange("p (h two) k -> p h two k", two=2)
                i4 = ixf[:].rearrange("p (h two) k -> p h two k", two=2)
                c4 = cand[:].rearrange("p h (a b) -> p h a b", a=16)
                ci4 = candi[:].rearrange("p h (a b) -> p h a b", a=16)
                P.op('dve', lambda e: e.tensor_tensor(out=c4, in0=m4[:, :, 0, :].unsqueeze(3).to_broadcast([128, 8, 16, 16]),
                                                      in1=m4[:, :, 1, :].unsqueeze(2).to_broadcast([128, 8, 16, 16]), op=ALU.add),
                     reads=mres, writes=['cand'])
                yield
                P.op('dve', lambda e: e.tensor_scalar(out=i4[:, :, 0, :], in0=i4[:, :, 0, :], scalar1=128.0, scalar2=None, op0=ALU.mult),
                     reads=['ixf'], writes=['ixf'])
                yield
                P.op('dve', lambda e: e.tensor_tensor(out=ci4, in0=i4[:, :, 0, :].unsqueeze(3).to_broadcast([128, 8, 16, 16]),
                                                      in1=i4[:, :, 1, :].unsqueeze(2).to_broadcast([128, 8, 16, 16]), op=ALU.add),
                     reads=['ixf'], writes=['candi'])
                yield
                for h in range(8):
                    P.op('dve', lambda e, h=h: e.max(out=ts[:, h, 0:8], in_=cand[:, h, :]), reads=['cand'], writes=[('ts0', h)])
                    yield
                P.fence('dve')
                for h in range(8):
                    P.op('dve', lambda e, h=h: e.match_replace(out=wk2[:, h, :], in_to_replace=ts[:, h, 0:8], in_values=cand[:, h, :], imm_value=-1e30),
                         reads=['cand', ('ts0', h)], writes=[('wk2', h)])
                    yield
                P.fence('dve')
                for h in range(8):
                    P.op('dve', lambda e, h=h: e.max(out=ts[:, h, 8:16], in_=wk2[:, h, :]), reads=[('wk2', h)], writes=[('ts1', h)])
                    yield
                P.fence('dve')
                tsres = [('ts0', h) for h in range(8)] + [('ts1', h) for h in range(8)]
                for h in range(8):
                    for k in range(16):
                        P.op('dve', lambda e, h=h, k=k: e.scalar_tensor_tensor(out=junk2[(h * 16 + k) % 2][:], in0=cand[:, h, :], scalar=ts[:, h, k:k + 1], in1=candi[:, h, :],
                                                                               op0=ALU.is_equal, op1=ALU.mult, accum_out=ef[:, h * 16 + k:h * 16 + k + 1]),
                             reads=['cand', 'candi', ('ts0', h), ('ts1', h)], writes=[('ef', h * 16 + k)])
                        yield
                P.fence('dve')
                P.op('dve', lambda e: e.tensor_scalar(out=ef[:], in0=ef[:], scalar1=float(NEXP - 1), scalar2=float(L * NEXP), op0=ALU.min, op1=ALU.add),
                     reads=[('ef', q) for q in range(128)], writes=['ef'])
                yield
                P.op('dve', lambda e: e.tensor_copy(out=eib[:], in_=ef[:]), reads=['ef'], writes=[rei])
                yield
                P.op('dve', lambda e: e.tensor_tensor(out=gtb[:], in0=ts[:], in1=ts[:, :, 0:1].to_broadcast([128, 8, 16]), op=ALU.subtract),
                     reads=tsres, writes=[rgt])
                yield
                P.op('act', lambda e: e.activation(out=gtb[:], in_=gtb[:], func=AF.Exp), reads=[rgt], writes=[rgt])
                P.op('dve', lambda e: e.tensor_reduce(out=gsum[:], in_=gtb[:], axis=AX.X, op=ALU.add), reads=[rgt], writes=['gsum'])
                yield
                P.op('dve', lambda e: e.reciprocal(out=gsum[:], in_=gsum[:]), reads=['gsum'], writes=['gsum'])
                yield
                P.op('dve', lambda e: e.tensor_tensor(out=gtb[:], in0=gtb[:], in1=gsum[:].unsqueeze(2).to_broadcast([128, 8, 16]), op=ALU.mult),
                     reads=[rgt, 'gsum'], writes=[rgt])
                yield

            def step(gen, n=1):
                if gen is None:
                    return None
                try:
                    for _ in range(n):
                        next(gen)
                except StopIteration:
                    return None
                return gen

            load_sc(0)
            load_sc(1)
            load_xh(0)
            g0 = topk_gen(0)
            while g0 is not None:
                g0 = step(g0, 64)
            nu = nv = 0
            for ti in range(NT):
                b = ti % 2
                xt, ht, eib, gtb, acc = xts[b], hts[b], eis[b], gts[b], accs[b]
                rx, rh, rei, rgt, racc = f'xt{b}', f'ht{b}', f'ei{b}', f'gt{b}', f'acc{b}'
                gt2 = gtb[:].rearrange("p h k -> p (h k)")
                if ti + 1 < NT:
                    load_xh(ti + 1)
                gen = topk_gen(ti + 1) if ti + 1 < NT else None
                for k in range(128):
                    s_ = nu % NB
                    nu += 1
                    P.op('pool', lambda e, s_=s_, k=k, eib=eib: e.indirect_dma_start(
                        out=ub[s_][:], out_offset=None, in_=U,
                        in_offset=bass.IndirectOffsetOnAxis(ap=eib[:, k:k + 1], axis=0)),
                        reads=[rei], writes=[('ub', s_)], dma=True)
                    P.op('dve', lambda e, s_=s_, k=k, ht=ht: e.scalar_tensor_tensor(out=junk[:], in0=ub[s_][:], scalar=1.0, in1=ht[:], op0=ALU.mult, op1=ALU.mult,
                                                                                 accum_out=apre[:, k:k + 1]),
                         reads=[('ub', s_), rh], writes=[('apre', k)])
                    gen = step(gen)
                P.op('act', lambda e: e.activation(out=coef[:], in_=apre[:], func=AF.Gelu), reads=[('apre', k) for k in range(128)], writes=['coef'])
                P.op('dve', lambda e, gt2=gt2: e.tensor_tensor(out=coef[:], in0=coef[:], in1=gt2, op=ALU.mult), reads=['coef', rgt], writes=['coef'])
                for k in range(128):
                    s_ = nv % NB
                    nv += 1
                    P.op('pool', lambda e, s_=s_, k=k, eib=eib: e.indirect_dma_start(
                        out=vb[s_][:], out_offset=None, in_=V,
                        in_offset=bass.IndirectOffsetOnAxis(ap=eib[:, k:k + 1], axis=0)),
                        reads=[rei], writes=[('vb', s_)], dma=True)
                    if k == 0:
                        P.op('dve', lambda e, s_=s_, acc=acc: e.tensor_scalar(out=acc[:], in0=vb[s_][:], scalar1=coef[:, 0:1], scalar2=None, op0=ALU.mult),
                             reads=[('vb', s_), 'coef'], writes=[racc])
                    else:
                        P.op('dve', lambda e, s_=s_, k=k, acc=acc: e.scalar_tensor_tensor(out=acc[:], in0=vb[s_][:], scalar=coef[:, k:k + 1], in1=acc[:],
                                                                                       op0=ALU.mult, op1=ALU.add),
                             reads=[('vb', s_), 'coef', racc], writes=[racc])
                    gen = step(gen)
                while gen is not None:
                    gen = step(gen, 64)
                if ti + 2 < NT:
                    load_sc(ti + 2)
                P.op('dve', lambda e, acc=acc: e.tensor_tensor(out=acc[:], in0=acc[:], in1=self.gate_bc[:], op=ALU.mult), reads=[racc, 'gate_bc'], writes=[racc])
                P.op('dve', lambda e, acc=acc, xt=xt: e.scalar_tensor_tensor(out=acc[:], in0=xt[:], scalar=ALPHA, in1=acc[:], op0=ALU.mult, op1=ALU.add),
                     reads=[rx, racc], writes=[racc])
                self.layernorm_inplace(acc[:], racc, gb='dve')
                P.op('sp', lambda e, acc=acc, ti=ti: e.dma_start(out=dst[ti * 128:(ti + 1) * 128, :], in_=acc[:]), reads=[racc], dma=True)

    def cast_gen(self, st, L, CN=2):
        P, nc, A = self.P, self.nc, self.A
        Uv = A['peer_u'][L].rearrange("(n p) d -> p n d", p=128)
        Vv = A['peer_v'][L].rearrange("(n p) d -> p n d", p=128)
        Ov = A['UVB'][L * NEXP:(L + 1) * NEXP, :].rearrange("(n p) d -> p n d", p=128)
        iu = [st.T(f'ciu{i}', [128, CN, D]) for i in range(2)]
        iv = [st.T(f'civ{i}', [128, CN, D]) for i in range(2)]
        ob = [st.T(f'cob{i}', [128, CN, 2 * D], BF16) for i in range(2)]
        nchunk = (NEXP // 128) // CN

        def ld(ci):
            b = ci % 2
            ns = slice(ci * CN, (ci + 1) * CN)
            P.op(self.cfg.get('cast_q', 'pool'), lambda e: e.dma_start(out=iu[b][:], in_=Uv[:, ns, :]), writes=[f'ciu{b}'], dma=True)
            P.op(self.cfg.get('cast_q', 'pool'), lambda e: e.dma_start(out=iv[b][:], in_=Vv[:, ns, :]), writes=[f'civ{b}'], dma=True)

        ld(0)
        for ci in range(nchunk):
            b = ci % 2
            ns = slice(ci * CN, (ci + 1) * CN)
            if ci + 1 < nchunk:
                ld(ci + 1)
            P.op('pool', lambda e: e.tensor_copy(out=ob[b][:, :, 0:D], in_=iu[b][:]), reads=[f'ciu{b}'], writes=[(f'cob{b}', 0)])
            P.op('pool', lambda e: e.tensor_copy(out=ob[b][:, :, D:2 * D], in_=iv[b][:]), reads=[f'civ{b}'], writes=[(f'cob{b}', 1)])
            P.op(self.cfg.get('cast_q', 'pool'), lambda e: e.dma_start(out=Ov[:, ns, :], in_=ob[b][:]), reads=[(f'cob{b}', 0), (f'cob{b}', 1)],
                 writes=[('UVB', L, ci)], dma=True)
            yield

    def emit_cast_tables(self):
        P, nc, A = self.P, self.nc, self.A
        CN = 4
        Uv = A['peer_u'].rearrange("l (n p) d -> p (l n) d", p=128)
        Vv = A['peer_v'].rearrange("l (n p) d -> p (l n) d", p=128)
        Ov = A['UVB'].rearrange("(n p) d -> p n d", p=128)
        with Stage(self, 'cast') as st:
            iu = [st.T(f'iu{i}', [128, CN, D]) for i in range(2)]
            iv = [st.T(f'iv{i}', [128, CN, D]) for i in range(2)]
            ob = [st.T(f'ob{i}', [128, CN, 2 * D], BF16) for i in range(2)]
            nchunk = (2 * NEXP // 128) // CN

            def ld(ci):
                b = ci % 2
                ns = slice(ci * CN, (ci + 1) * CN)
                P.op('sp', lambda e, b=b, ns=ns: e.dma_start(out=iu[b][:], in_=Uv[:, ns, :]), writes=[f'iu{b}'], dma=True)
                P.op('sp', lambda e, b=b, ns=ns: e.dma_start(out=iv[b][:], in_=Vv[:, ns, :]), writes=[f'iv{b}'], dma=True)

            ld(0)
            for ci in range(nchunk):
                b = ci % 2
                ns = slice(ci * CN, (ci + 1) * CN)
                if ci + 1 < nchunk:
                    ld(ci + 1)
                P.op('dve', lambda e, b=b: e.tensor_copy(out=ob[b][:, :, 0:D], in_=iu[b][:]), reads=[f'iu{b}'], writes=[(f'ob{b}', 0)])
                P.op('act', lambda e, b=b: e.copy(out=ob[b][:, 0:2, D:2 * D], in_=iv[b][:, 0:2, :]), reads=[f'iv{b}'], writes=[(f'ob{b}', 1)])
                P.op('pool', lambda e, b=b: e.tensor_copy(out=ob[b][:, 2:4, D:2 * D], in_=iv[b][:, 2:4, :]), reads=[f'iv{b}'], writes=[(f'ob{b}', 2)])
                P.op('act', lambda e, b=b, ns=ns: e.dma_start(out=Ov[:, ns, :], in_=ob[b][:]), reads=[(f'ob{b}', 0), (f'ob{b}', 1), (f'ob{b}', 2)],
                     writes=[('UVB', ci)], dma=True)

    def emit_peer2g(self, xin, dst, L):
        P, nc, A, ps = self.P, self.nc, self.A, self.ps
        NB = self.cfg.get('nb', 24)
        GS = self.cfg.get('gs', 8)
        UVB = A['UVB']
        with Stage(self, f'pg{L}') as st:
            scs = [st.T('sc0', [128, 16, 128])] * 2
            m = st.T('m', [128, 16, 16])
            ix = st.T('ix', [128, 16, 16], U32)
            ixf = st.T('ixf', [128, 16, 16])
            wk = st.T('wk', [128, 16, 128])
            cand = st.T('cand', [128, 8, 256])
            candi = st.T('candi', [128, 8, 256])
            wk2 = wk[:].rearrange("p a k -> p (a k)").rearrange("p (h c) -> p h c", h=8)
            junk2 = [st.T(f'junk2{i}', [128, 256]) for i in range(2)]
            ts = st.T('ts', [128, 8, 16])
            ef = st.T('ef', [128, 128])
            eis = [st.T(f'ei{i}', [128, 128], I32) for i in range(2)]
            gts = [st.T(f'gt{i}', [128, 8, 16]) for i in range(2)]
            gsum = st.T('gsum', [128, 8])
            xts = [st.T(f'xt{i}', [128, D]) for i in range(2)]
            hts = [st.T(f'ht{i}', [128, D]) for i in range(2)]
            uvb = [st.T(f'uv{i}', [128, 2 * D], BF16) for i in range(NB)]
            junk = st.T('junk', [128, D], BF16)
            junka = st.T('junka', [128, D], BF16)
            prods = [st.T(f'prod{i}', [128, D], BF16) for i in range(3)]
            hbs = [st.T(f'hb{i}', [128, D], BF16) for i in range(2)]
            apre = st.T('apre', [128, 128])
            ge = st.T('ge', [128, 128])
            cf = st.T('cf', [128, 128])
            dgs = [st.T(f'dg{i}', [128, 128], BF16) for i in range(4)]
            accs = [st.T(f'acc{i}', [128, D]) for i in range(2)]

            def load_sc(t):
                P.op('sp', lambda e, t=t: e.dma_start(out=scs[t % 2][:].rearrange("p a k -> p (a k)"), in_=A['SCR'][t * 128:(t + 1) * 128, :]),
                     writes=['sc0'], dma=True)

            def load_xh(t):
                P.op('sp', lambda e, t=t: e.dma_start(out=xts[t % 2][:], in_=xin[t * 128:(t + 1) * 128, :]), writes=[f'xt{t % 2}'], dma=True)
                P.op('sp', lambda e, t=t: e.dma_start(out=hts[t % 2][:], in_=A['H'][t * 128:(t + 1) * 128, :]), writes=[f'ht{t % 2}'], dma=True)

            def topk_gen(t):
                sc = scs[t % 2]
                rsc = 'sc0'
                eib, gtb = eis[t % 2], gts[t % 2]
                rei, rgt = f'ei{t % 2}', f'gt{t % 2}'
                for c in range(16):
                    P.op('dve', lambda e, c=c: e.max(out=m[:, c, 0:8], in_=sc[:, c, :]), reads=[rsc], writes=[('m0', c)])
                    yield
                P.fence('dve')
                for c in range(16):
                    P.op('dve', lambda e, c=c: e.max_index(out=ix[:, c, 0:8], in_max=m[:, c, 0:8], in_values=sc[:, c, :]),
                         reads=[rsc, ('m0', c)], writes=[('ix0', c)])
                    yield
                    P.op('dve', lambda e, c=c: e.match_replace(out=wk[:, c, :], in_to_replace=m[:, c, 0:8], in_values=sc[:, c, :], imm_value=-1e30),
                         reads=[rsc, ('m0', c)], writes=[('wk', c)])
                    yield
                P.fence('dve')
                for c in range(16):
                    P.op('dve', lambda e, c=c: e.max(out=m[:, c, 8:16], in_=wk[:, c, :]), reads=[('wk', c)], writes=[('m1', c)])
                    yield
                P.fence('dve')
                for c in range(16):
                    P.op('dve', lambda e, c=c: e.max_index(out=ix[:, c, 8:16], in_max=m[:, c, 8:16], in_values=wk[:, c, :]),
                         reads=[('wk', c), ('m1', c)], writes=[('ix1', c)])
                    yield
                P.fence('dve')
                mres = [('m0', c) for c in range(16)] + [('m1', c) for c in range(16)]
                ixres = [('ix0', c) for c in range(16)] + [('ix1', c) for c in range(16)]
                P.op('dve', lambda e: e.tensor_copy(out=ixf[:], in_=ix[:]), reads=ixres, writes=['ixf'])
                yield
                m4 = m[:].rearrange("p (h two) k -> p h two k", two=2)
                i4 = ixf[:].rearrange("p (h two) k -> p h two k", two=2)
                c4 = cand[:].rearrange("p h (a b) -> p h a b", a=16)
                ci4 = candi[:].rearrange("p h (a b) -> p h a b", a=16)
                P.op('dve', lambda e: e.tensor_tensor(out=c4, in0=m4[:, :, 0, :].unsqueeze(3).to_broadcast([128, 8, 16, 16]),
                                                      in1=m4[:, :, 1, :].unsqueeze(2).to_broadcast([128, 8, 16, 16]), op=ALU.add),
                     reads=mres, writes=['cand'])
                yield
                P.op('dve', lambda e: e.tensor_scalar(out=i4[:, :, 0, :], in0=i4[:, :, 0, :], scalar1=128.0, scalar2=None, op0=ALU.mult),
                     reads=['ixf'], writes=['ixf'])
                yield
                P.op('dve', lambda e: e.tensor_tensor(out=ci4, in0=i4[:, :, 0, :].unsqueeze(3).to_broadcast([128, 8, 16, 16]),
                                                      in1=i4[:, :, 1, :].unsqueeze(2).to_broadcast([128, 8, 16, 16]), op=ALU.add),
                     reads=['ixf'], writes=['candi'])
                yield
                for h in range(8):
                    P.op('dve', lambda e, h=h: e.max(out=ts[:, h, 0:8], in_=cand[:, h, :]), reads=['cand'], writes=[('ts0', h)])
                    yield
                P.fence('dve')
                for h in range(8):
                    P.op('dve', lambda e, h=h: e.match_replace(out=wk2[:, h, :], in_to_replace=ts[:, h, 0:8], in_values=cand[:, h, :], imm_value=-1e30),
                         reads=['cand', ('ts0', h)], writes=[('wk2', h)])
                    yield
                P.fence('dve')
                for h in range(8):
                    P.op('dve', lambda e, h=h: e.max(out=ts[:, h, 8:16], in_=wk2[:, h, :]), reads=[('wk2', h)], writes=[('ts1', h)])
                    yield
                P.fence('dve')
                tsres = [('ts0', h) for h in range(8)] + [('ts1', h) for h in range(8)]
                for h in range(8):
                    for k in range(16):
                        P.op('dve', lambda e, h=h, k=k: e.scalar_tensor_tensor(out=junk2[(h * 16 + k) % 2][:], in0=cand[:, h, :], scalar=ts[:, h, k:k + 1], in1=candi[:, h, :],
                                                                               op0=ALU.is_equal, op1=ALU.mult, accum_out=ef[:, h * 16 + k:h * 16 + k + 1]),
                             reads=['cand', 'candi', ('ts0', h), ('ts1', h)], writes=[('ef', h * 16 + k)])
                        yield
                P.fence('dve')
                P.op('dve', lambda e: e.tensor_scalar(out=ef[:], in0=ef[:], scalar1=float(NEXP - 1), scalar2=float(L * NEXP), op0=ALU.min, op1=ALU.add),
                     reads=[('ef', q) for q in range(128)], writes=['ef'])
                yield
                P.op('dve', lambda e: e.tensor_copy(out=eib[:], in_=ef[:]), reads=['ef'], writes=[rei])
                yield
                P.op('dve', lambda e: e.tensor_tensor(out=gtb[:], in0=ts[:], in1=ts[:, :, 0:1].to_broadcast([128, 8, 16]), op=ALU.subtract),
                     reads=tsres, writes=[rgt])
                yield
                P.op('act', lambda e: e.activation(out=gtb[:], in_=gtb[:], func=AF.Exp), reads=[rgt], writes=[rgt])
                P.op('dve', lambda e: e.tensor_reduce(out=gsum[:], in_=gtb[:], axis=AX.X, op=ALU.add), reads=[rgt], writes=['gsum'])
                yield
                P.op('dve', lambda e: e.reciprocal(out=gsum[:], in_=gsum[:]), reads=['gsum'], writes=['gsum'])
                yield
                P.op('dve', lambda e: e.tensor_tensor(out=gtb[:], in0=gtb[:], in1=gsum[:].unsqueeze(2).to_broadcast([128, 8, 16]), op=ALU.mult),
                     reads=[rgt, 'gsum'], writes=[rgt])
                yield

            def step(gen, n=1):
                if gen is None:
                    return None
                try:
                    for _ in range(n):
                        next(gen)
                except StopIteration:
                    return None
                return gen

            load_sc(0)
            load_xh(0)
            g0 = topk_gen(0)
            while g0 is not None:
                g0 = step(g0, 64)
            load_sc(1)
            nu = 0
            ndg = 0
            npr = 0
            for ti in range(NT):
                b = ti % 2
                xt, ht, eib, gtb, acc = xts[b], hts[b], eis[b], gts[b], accs[b]
                rx, rh, rei, rgt, racc = f'xt{b}', f'ht{b}', f'ei{b}', f'gt{b}', f'acc{b}'
                gt2 = gtb[:].rearrange("p h k -> p (h k)")
                pa = [ps[(ti % 2) * 2], ps[(ti % 2) * 2 + 1]]
                rpa = [f'ps{(ti % 2) * 2}', f'ps{(ti % 2) * 2 + 1}']
                if ti + 1 < NT:
                    load_xh(ti + 1)
                hb, rhb = hbs[b], f'hb{b}'
                P.op('dve', lambda e, hb=hb, ht=ht: e.tensor_copy(out=hb[:], in_=ht[:]), reads=[rh], writes=[rhb])
                gen = topk_gen(ti + 1) if ti + 1 < NT else None
                for g in range(128 // GS):
                    slots = []
                    for kk in range(GS):
                        k = g * GS + kk
                        s_ = nu % NB
                        nu += 1
                        slots.append(s_)
                        P.op('pool', lambda e, s_=s_, k=k, eib=eib: e.indirect_dma_start(
                            out=uvb[s_][:], out_offset=None, in_=UVB,
                            in_offset=bass.IndirectOffsetOnAxis(ap=eib[:, k:k + 1], axis=0)),
                            reads=[rei], writes=[('uv', s_)], dma=True)
                        if (k % self.cfg.get('split_den', 3)) < self.cfg.get('split_num', 1) or not self.cfg.get('dot_split', True):
                            P.op('dve', lambda e, s_=s_, k=k, ht=ht: e.scalar_tensor_tensor(out=junk[:], in0=uvb[s_][:, 0:D], scalar=1.0, in1=ht[:], op0=ALU.mult, op1=ALU.mult,
                                                                                         accum_out=apre[:, k:k + 1]),
                                 reads=[('uv', s_), rh], writes=[('apre', k)])
                        else:
                            pr = prods[npr % 3]
                            rpr = f'prod{npr % 3}'
                            npr += 1
                            P.op('dve', lambda e, s_=s_, pr=pr, hb=hb: e.tensor_tensor(out=pr[:], in0=uvb[s_][:, 0:D], in1=hb[:], op=ALU.mult),
                                 reads=[('uv', s_), rhb], writes=[rpr])
                            P.op('act', lambda e, pr=pr, k=k: e.activation(out=junka[:], in_=pr[:], func=AF.Copy, accum_out=apre[:, k:k + 1]),
                                 reads=[rpr], writes=[('apre', k)])
                        gen = step(gen, 2)
                    gsl = slice(g * GS, (g + 1) * GS)
                    P.op('act', lambda e, gsl=gsl: e.activation(out=ge[:, gsl], in_=apre[:, gsl], func=AF.Gelu),
                         reads=[('apre', k) for k in range(g * GS, (g + 1) * GS)], writes=[('ge', g)])
                    P.op('dve', lambda e, gsl=gsl, gt2=gt2: e.tensor_tensor(out=cf[:, gsl], in0=ge[:, gsl], in1=gt2[:, gsl], op=ALU.mult),
                         reads=[('ge', g), rgt], writes=[('cf', g)])
                    for kk in range(GS):
                        k = g * GS + kk
                        s_ = slots[kk]
                        dg = dgs[ndg % 4]
                        rdg = f'dg{ndg % 4}'
                        ndg += 1
                        P.op('act', lambda e, dg=dg, k=k: e.activation(out=dg[:], in_=self.ident[:], func=AF.Copy, scale=cf[:, k:k + 1]),
                             reads=['ident', ('cf', g)], writes=[rdg])
                        for half in range(2):
                            P.op('pe', lambda e, dg=dg, s_=s_, half=half, k=k, pa=pa: e.matmul(out=pa[half][:], lhsT=dg[:], rhs=uvb[s_][:, D + half * 512:D + (half + 1) * 512],
                                                                                           start=(k == 0), stop=(k == 127)),
                                 reads=[rdg, ('uv', s_)], writes=[rpa[half]])
                while gen is not None:
                    gen = step(gen, 64)
                if ti + 2 < NT:
                    load_sc(ti + 2)
                for half in range(2):
                    P.op('dve', lambda e, acc=acc, half=half, pa=pa: e.tensor_tensor(out=acc[:, half * 512:(half + 1) * 512], in0=pa[half][:],
                                                                                   in1=self.gate_bc[:, half * 512:(half + 1) * 512], op=ALU.mult),
                         reads=[rpa[half], 'gate_bc'], writes=[racc])
                P.op('dve', lambda e, acc=acc, xt=xt: e.scalar_tensor_tensor(out=acc[:], in0=xt[:], scalar=ALPHA, in1=acc[:], op0=ALU.mult, op1=ALU.add),
                     reads=[rx, racc], writes=[racc])
                self.layernorm_inplace(acc[:], racc, gb='dve')
                P.op('sp', lambda e, acc=acc, ti=ti: e.dma_start(out=dst[ti * 128:(ti + 1) * 128, :], in_=acc[:]), reads=[racc], dma=True)

    def emit_attn1(self, xin):
        P, nc, A, ps = self.P, self.nc, self.A, self.ps
        ones = self.ones
        NCOL = 3 * D + 16
        with Stage(self, 'a1') as st:
            winr = st.T('winr', [128, 8, 3 * D], F32R)
            wstg = [st.T('wstg0', [128, 8, 512])] * 2
            wf = st.T('wf', [128, 8, 16])
            qkb = st.T('qkb', [128, 16])
            vbr = st.T('vbr', [1, D])
            vb_bc = st.T('vb_bc', [128, D])
            fb = st.T('fb', [16, 1])
            xts = [st.T(f'xt{i}', [128, D]) for i in range(2)]
            ht = st.T('ht', [128, D])
            hT = st.T('hT', [128, 8, 512], F32R)
            qko = [st.T(f'qko{i}', [128, 512]) for i in range(2)]
            vo = [st.T(f'vo{i}', [128, D]) for i in range(2)]
            Fcb = [st.T(f'Fcb{i}', [16, 512]) for i in range(2)]
            Frb = st.T('Frb', [16, 512], F32R)
            Flb = st.T('Flb', [16, 512])
            nFr = st.T('nFr', [16, 512])
            nFl = st.T('nFl', [16, 512])
            spt = st.T('spt', [16, 512])
            o16 = st.T('o16', [16, 512])
            for q in range(6):
                wb = wstg[0]
                rw = 'wstg0'
                P.op('sp', lambda e, q=q, wb=wb: e.dma_start(out=wb[:], in_=A['attn_in_w'][:, q * 512:(q + 1) * 512].rearrange("(k p) n -> p k n", p=128)),
                     writes=[rw], dma=True)
                eng = ('dve', 'pool')[q % 2]
                P.op(eng, lambda e, q=q, wb=wb: e.tensor_copy(out=winr[:, :, q * 512:(q + 1) * 512], in_=wb[:]), reads=[rw], writes=[('win', q)])
            P.op('sp', lambda e: e.dma_start(out=wf[:], in_=A['attn_in_w'][:, 3 * D:NCOL].rearrange("(k p) n -> p k n", p=128)),
                 writes=['wf'], dma=True)
            P.op('sp', lambda e: e.dma_start(out=qkb[:], in_=A['attn_qkb_l']), writes=['qkb'], dma=True)
            P.op('sp', lambda e: e.dma_start(out=vbr[:], in_=A['attn_vb']), writes=['vbr'], dma=True)
            P.op('sp', lambda e: e.dma_start(out=fb[:], in_=A['attn_fb']), writes=['fb'], dma=True)
            P.op('dve', lambda e: e.tensor_scalar(out=qkb[:, 0:8], in0=qkb[:, 0:8], scalar1=0.125, scalar2=None, op0=ALU.mult), reads=['qkb'], writes=['qkb'])
            P.op('dve', lambda e: e.tensor_scalar(out=fb[:], in0=fb[:], scalar1=-1.0, scalar2=None, op0=ALU.mult), reads=['fb'], writes=['fb'])
            P.op('pool', lambda e: e.memset(o16[:], 1.0), writes=['o16'])
            for half in range(2):
                P.op('pe', lambda e, half=half: e.matmul(out=ps[4 + half][:], lhsT=ones[0:1, :], rhs=vbr[0:1, half * 512:(half + 1) * 512], start=True, stop=True),
                     reads=['ones', 'vbr'], writes=[f'ps{4 + half}'])
                P.op('act', lambda e, half=half: e.copy(out=vb_bc[:, half * 512:(half + 1) * 512], in_=ps[4 + half][:]), reads=[f'ps{4 + half}'], writes=['vb_bc'])
            ti = 0
            for jb in range(8):
                cols = slice(jb * 512, (jb + 1) * 512)
                for tl in range(4):
                    xt = xts[ti % 2]
                    rx = f'xt{ti % 2}'
                    P.op('sp', lambda e, xt=xt, ti=ti: e.dma_start(out=xt[:], in_=xin[ti * 128:(ti + 1) * 128, :]), writes=[rx], dma=True)
                    self.modulate(xt[:], ht[:], rx, 'ht')
                    self.transpose8(ht, 'ht', hT, 'hT', tl * 128, 0)
                    ti += 1
                for c in range(16):
                    bank = ps[2 + c % 2]
                    rb = f'ps{2 + c % 2}'
                    for k in range(8):
                        P.op('pe', lambda e, k=k, c=c, bank=bank: e.matmul(out=bank[:], lhsT=winr[:, k, c * 128:(c + 1) * 128], rhs=hT[:, k, :],
                                                                          start=(k == 0), stop=(k == 7)),
                             reads=[('win', c // 4), 'hT'], writes=[rb])
                    ob = qko[c % 2]
                    rob = f'qko{c % 2}'
                    P.op('act', lambda e, c=c, bank=bank, ob=ob: e.activation(out=ob[:], in_=bank[:], func=AF.Identity, bias=qkb[:, c:c + 1],
                                                                             scale=(0.125 if c < 8 else 1.0)),
                         reads=[rb, 'qkb'], writes=[rob])
                    dstt = A['QA'] if c < 8 else A['KA']
                    for hh in range(2):
                        head = (c % 8) * 2 + hh
                        P.op('sp', lambda e, ob=ob, hh=hh, head=head, dstt=dstt: e.dma_start(out=dstt[head, 0:64, cols], in_=ob[hh * 64:(hh + 1) * 64, :]),
                             reads=[rob], writes=[('QK', c, hh)], dma=True)
                for tl in range(4):
                    tix = jb * 4 + tl
                    vt = vo[tix % 2]
                    rv = f'vo{tix % 2}'
                    for half in range(2):
                        bank = ps[4 + half]
                        rb = f'ps{4 + half}'
                        for k in range(8):
                            P.op('pe', lambda e, k=k, bank=bank, half=half, tl=tl: e.matmul(out=bank[:], lhsT=hT[:, k, tl * 128:(tl + 1) * 128],
                                                                                        rhs=winr[:, k, 2 * D + half * 512:2 * D + (half + 1) * 512],
                                                                                        start=(k == 0), stop=(k == 7)),
                                 reads=['hT', ('win', 4 + half)], writes=[rb])
                        P.op('dve', lambda e, vt=vt, bank=bank, half=half: e.tensor_tensor(out=vt[:, half * 512:(half + 1) * 512], in0=bank[:],
                                                                                      in1=vb_bc[:, half * 512:(half + 1) * 512], op=ALU.add),
                             reads=[rb, 'vb_bc'], writes=[rv])
                    P.op('sp', lambda e, vt=vt, tix=tix: e.dma_start(out=A['V'][tix * 128:(tix + 1) * 128, :], in_=vt[:]), reads=[rv], writes=[('V', tix)], dma=True)
                for k in range(8):
                    P.op('pe', lambda e, k=k: e.matmul(out=ps[6][0:16, :], lhsT=wf[:, k, :], rhs=hT[:, k, :].bitcast(F32), start=(k == 0), stop=(k == 7)),
                         reads=['wf', 'hT'], writes=['ps6'])
                P.op('act', lambda e: e.activation(out=spt[:], in_=ps[6][0:16, :], func=AF.Exp, bias=fb[:, 0:1], scale=-1.0), reads=['ps6', 'fb'], writes=['spt'])
                P.op('act', lambda e: e.activation(out=spt[:], in_=spt[:], func=AF.Ln, bias=1.0, scale=1.0), reads=['spt'], writes=['spt'])
                P.op('dve', lambda e: e.tensor_scalar(out=spt[:], in0=spt[:], scalar1=-1.0, scalar2=None, op0=ALU.mult), reads=['spt'], writes=['spt'])
                Fc = Fcb[jb % 2]
                rF = f'Fcb{jb % 2}'
                init = 0.0 if jb == 0 else Fcb[(jb - 1) % 2][:, 511:512]
                P.op('dve', lambda e, init=init, Fc=Fc: e.tensor_tensor_scan(out=Fc[:], data0=o16[:], data1=spt[:], initial=init,
                                                                             op0=ALU.mult, op1=ALU.add),
                     reads=['o16', 'spt', f'Fcb{(jb - 1) % 2}'], writes=[rF])
                P.op('dve', lambda e, Fc=Fc: e.tensor_copy(out=Frb[:], in_=Fc[:]), reads=[rF], writes=['Frb'])
                P.op('dve', lambda e, Fc=Fc: e.tensor_tensor(out=Flb[:], in0=Fc[:], in1=Frb[:].bitcast(F32), op=ALU.subtract), reads=[rF, 'Frb'], writes=['Flb'])
                P.op('dve', lambda e: e.tensor_scalar(out=nFr[:], in0=Frb[:].bitcast(F32), scalar1=-1.0, scalar2=None, op0=ALU.mult), reads=['Frb'], writes=['nFr'])
                P.op('dve', lambda e: e.tensor_scalar(out=nFl[:], in0=Flb[:], scalar1=-1.0, scalar2=None, op0=ALU.mult), reads=['Flb'], writes=['nFl'])
                P.op('sp', lambda e: e.dma_start(out=A['QA'][:, 64, cols], in_=Frb[:].bitcast(F32)), reads=['Frb'], writes=[('QAf', jb)], dma=True)
                P.op('sp', lambda e: e.dma_start(out=A['QA'][:, 65, cols], in_=Flb[:]), reads=['Flb'], writes=[('QAl', jb)], dma=True)
                P.op('sp', lambda e: e.dma_start(out=A['KA'][:, 66, cols], in_=nFr[:]), reads=['nFr'], writes=[('KAf', jb)], dma=True)
                P.op('sp', lambda e: e.dma_start(out=A['KA'][:, 67, cols], in_=nFl[:]), reads=['nFl'], writes=[('KAl', jb)], dma=True)
                for r in (66, 67):
                    P.op('sp', lambda e, r=r: e.dma_start(out=A['QA'][:, r, cols], in_=o16[:]), reads=['o16'], writes=[('QAo', r, jb)], dma=True)
                for r in (64, 65):
                    P.op('sp', lambda e, r=r: e.dma_start(out=A['KA'][:, r, cols], in_=o16[:]), reads=['o16'], writes=[('KAo', r, jb)], dma=True)

    def emit_attn2(self):
        P, nc, A, ps = self.P, self.nc, self.A, self.ps
        NR = 68
        with Stage(self, 'a2') as st:
            qst = st.T('qst', [NR, S])
            kst = st.T('kst', [NR, S])
            vst = st.T('vst', [128, 32, 64])
            QAh = [st.T(f'QAh{i}', [NR, S], F32R) for i in range(2)]
            KAh = [st.T(f'KAh{i}', [NR, S], F32R) for i in range(2)]
            Vh = [st.T(f'Vh{i}', [128, 32, 128], F32R) for i in range(2)]
            ones_r = st.T('ones_r', [128, 128], F32R)
            pt = [st.T(f'pt{i}', [128, 512], F32R) for i in range(3)]
            lm = [st.T(f'lm{i}', [128, 512]) for i in range(2)]
            mask = st.T('mask', [128, 4, 512])
            rzt = st.T('rzt', [64, 512])
            oT = [st.T(f'oT{i}', [64, 512]) for i in range(2)]
            P.op('pool', lambda e: e.memset(mask[:], 0.0), writes=['mask'])
            for i4 in range(4):
                P.op('pool', lambda e, i4=i4: e.affine_select(out=mask[:, i4, :], in_=mask[:, i4, :], pattern=[[1, 512]], compare_op=ALU.is_ge,
                                                              fill=NEG, base=-128 * i4, channel_multiplier=-1), reads=['mask'], writes=['mask'])
            P.op('pool', lambda e: e.tensor_copy(out=ones_r[:], in_=self.ones[:]), reads=['ones'], writes=['ones_r'])
            npt = 0
            nlm = 0
            nS = 0
            nO = 0

            def loads(h):
                b = h % 2
                qa, ka, vh = QAh[b], KAh[b], Vh[b]
                rq, rk, rv = f'QAh{b}', f'KAh{b}', f'Vh{b}'
                for q4 in range(4):
                    cs = slice(q4 * 1024, (q4 + 1) * 1024)
                    P.op('sp', lambda e, h=h, cs=cs: e.dma_start(out=qst[:, cs], in_=A['QA'][h, :, cs]), writes=[('qst', q4)], dma=True)
                    P.op('sp', lambda e, h=h, cs=cs: e.dma_start(out=kst[:, cs], in_=A['KA'][h, :, cs]), writes=[('kst', q4)], dma=True)
                    P.op('sp', lambda e, h=h, q4=q4: e.dma_start(
                        out=vst[:, q4 * 8:(q4 + 1) * 8, :],
                        in_=A['V'][q4 * 1024:(q4 + 1) * 1024, h * 64:(h + 1) * 64].rearrange("(i p) d -> p i d", p=128)),
                        writes=[('vst', q4)], dma=True)
                for q4 in range(4):
                    cs = slice(q4 * 1024, (q4 + 1) * 1024)
                    P.op('pool', lambda e, qa=qa, cs=cs: e.tensor_copy(out=qa[:, cs], in_=qst[:, cs]), reads=[('qst', q4)], writes=[rq])
                    P.op('pool', lambda e, ka=ka, cs=cs: e.tensor_copy(out=ka[:, cs], in_=kst[:, cs]), reads=[('kst', q4)], writes=[rk])
                    for dup in range(2):
                        P.op('pool', lambda e, vh=vh, q4=q4, dup=dup: e.tensor_copy(out=vh[:, q4 * 8:(q4 + 1) * 8, dup * 64:(dup + 1) * 64],
                                                                                  in_=vst[:, q4 * 8:(q4 + 1) * 8, :]),
                             reads=[('vst', q4)], writes=[rv])

            loads(0)
            NH = self.cfg.get('nheads', 16)
            steps = [(h, j, i) for h in range(NH) for j in range(8) for i in range(4 * j + 4)]

            def emit_qk(n):
                h, j, i = steps[n]
                b = h % 2
                sb = ps[n % 3]
                P.op('pe', lambda e, sb=sb, ka=KAh[b], qa=QAh[b], i=i, j=j: e.matmul(out=sb[:], lhsT=ka[:, i * 128:(i + 1) * 128], rhs=qa[:, j * 512:(j + 1) * 512],
                                                                                 start=True, stop=True),
                     reads=[f'KAh{b}', f'QAh{b}'], writes=[f'ps{n % 3}'])

            emit_qk(0)
            for n, (h, j, i) in enumerate(steps):
                b = h % 2
                vh, rv = Vh[b], f'Vh{b}'
                if j == 0 and i == 0 and h + 1 < NH:
                    loads(h + 1)
                if n + 1 < len(steps):
                    emit_qk(n + 1)
                if i == 0:
                    oset = nO % 2
                    nO += 1
                poA, poB = ps[3 + 2 * oset], ps[4 + 2 * oset]
                rA, rB = f'ps{3 + 2 * oset}', f'ps{4 + 2 * oset}'
                ot, rot = oT[oset], f'oT{oset}'
                last = 4 * j + 3
                sb, rsb = ps[n % 3], f'ps{n % 3}'
                p_, rp = pt[n % 3], f'pt{n % 3}'
                if i >= 4 * j:
                    l_ = lm[nlm % 2]
                    rl = f'lm{nlm % 2}'
                    nlm += 1
                    P.op('dve', lambda e, l_=l_, sb=sb, i=i, j=j: e.tensor_tensor(out=l_[:], in0=sb[:], in1=mask[:, i - 4 * j, :], op=ALU.add),
                         reads=[rsb, 'mask'], writes=[rl])
                    P.op('act', lambda e, p_=p_, l_=l_: e.activation(out=p_[:], in_=l_[:], func=AF.Exp), reads=[rl], writes=[rp])
                else:
                    P.op('act', lambda e, p_=p_, sb=sb: e.activation(out=p_[:], in_=sb[:], func=AF.Exp), reads=[rsb], writes=[rp])
                P.op('pe', lambda e, p_=p_, i=i, poA=poA, vh=vh, last=last: e.matmul(out=poA[:], lhsT=vh[:, i, :], rhs=p_[:], start=(i == 0), stop=(i == last)),
                     reads=[rp, rv], writes=[rA])
                P.op('pe', lambda e, p_=p_, i=i, poB=poB, last=last: e.matmul(out=poB[:], lhsT=ones_r[:], rhs=p_[:], start=(i == 0), stop=(i == last)),
                     reads=[rp, 'ones_r'], writes=[rB])
                if i == last:
                    P.op('dve', lambda e, poB=poB: e.reciprocal(out=rzt[:], in_=poB[0:64, :]), reads=[rB], writes=['rzt'])
                    P.op('dve', lambda e, poA=poA, ot=ot: e.tensor_tensor(out=ot[:], in0=poA[0:64, :], in1=rzt[:], op=ALU.mult), reads=[rA, 'rzt'], writes=[rot])
                    P.op('sp', lambda e, ot=ot, h=h, j=j: e.dma_start(out=A['AOT'][h // 2, (h % 2) * 64:(h % 2) * 64 + 64, j * 512:(j + 1) * 512], in_=ot[:]),
                         reads=[rot], writes=[('AOT', h, j)], dma=True)


def make_in_maps(inputs, cores=range(8)):
    f = lambda a: np.ascontiguousarray(np.asarray(a, dtype=np.float32))
    sh = {}
    sh['ada_mix_w'] = f(inputs['ada_mix_w'])
    sh['ada_ffn_w'] = f(inputs['ada_ffn_w'])
    amb, afb = f(inputs['ada_mix_b']), f(inputs['ada_ffn_b'])
    sh['ada_b'] = f(np.stack([amb[0], afb[0], amb[1], afb[1]]))
    g1, g2 = f(inputs['ln_mix_g']), f(inputs['ln_ffn_g'])
    b1, b2 = f(inputs['ln_mix_b']), f(inputs['ln_ffn_b'])
    sh['ln_g'] = f(np.stack([g1[0], g2[0], g1[1], g2[1]]))
    sh['ln_b'] = f(np.stack([b1[0], b2[0], b1[1], b2[1]]))
    sh['conv_in_w'] = f(inputs['conv_in_w'][0])
    sh['conv_in_b_l'] = f(np.asarray(inputs['conv_in_b'][0]).reshape(16, 128).T)
    sh['conv_dw_w_l'] = f(np.asarray(inputs['conv_dw_w'][0]).reshape(31, 8, 128).transpose(2, 1, 0))
    sh['conv_vec_l'] = f(np.stack([np.asarray(inputs[k][0]).reshape(8, 128).T for k in ('conv_dw_b', 'conv_ln_g', 'conv_ln_b')], axis=1))
    sh['conv_out_w'] = f(inputs['conv_out_w'][0])
    sh['conv_out_b'] = f(np.asarray(inputs['conv_out_b'][0]).reshape(1, D))
    sh['attn_in_w'] = f(inputs['attn_in_w'][0])
    ab = np.asarray(inputs['attn_in_b'][0])
    sh['attn_qkb_l'] = f(ab[:2 * D].reshape(16, 128).T)
    sh['attn_vb'] = f(ab[2 * D:3 * D].reshape(1, D))
    sh['attn_fb'] = f(ab[3 * D:].reshape(16, 1))
    sh['attn_out_w'] = f(inputs['attn_out_w'][0])
    sh['attn_out_b'] = f(np.asarray(inputs['attn_out_b'][0]).reshape(1, D))
    sh['peer_query_w'] = f(inputs['peer_query_w'])
    k1, k2 = np.asarray(inputs['peer_sub_keys_1']), np.asarray(inputs['peer_sub_keys_2'])
    sh['peer_skT'] = f(np.stack([np.stack([k1[l].T, k2[l].T]) for l in range(2)]))
    sh['peer_u'] = f(inputs['peer_expert_u'])
    sh['peer_v'] = f(inputs['peer_expert_v'])
    x = np.asarray(inputs['x'])
    c = np.asarray(inputs['c'])
    maps = []
    for b in cores:
        m = dict(sh)
        m['x'] = f(x[b])
        m['c_l'] = f(c[b].reshape(8, 128).T)
        maps.append(m)
    return maps


_NC_CACHE = {}


def kernel(**inputs):
    if 'full' not in _NC_CACHE:
        _NC_CACHE['full'] = Kern({}).build()
    nc = _NC_CACHE['full']
    maps = make_in_maps(inputs)
    res = run_bass_kernel_spmd(nc, maps, core_ids=list(range(8)))
    return np.stack([np.asarray(r['out'], dtype=np.float32) for r in res.results], axis=0)
```

```python
import numpy as np
from contextlib import ExitStack
import concourse.bass as bass
import concourse.mybir as mybir
from concourse.bass_utils import run_bass_kernel_spmd

F32 = mybir.dt.float32
I32 = mybir.dt.int32
U32 = mybir.dt.uint32
F32R = mybir.dt.float32r
BF16 = mybir.dt.bfloat16
ALU = mybir.AluOpType
AF = mybir.ActivationFunctionType
AX = mybir.AxisListType

S = 4096
D = 1024
NT = S // 128
ALPHA = float((2 * 2) ** 0.25)
EPS = 1e-5
NEXP = 16384
MAXV = 30000
NEG = -30000.0


class Prog:
    def __init__(self, nc, es):
        self.nc = nc
        self.es = es
        self.eng = {'pe': nc.tensor, 'dve': nc.vector, 'act': nc.scalar,
                    'pool': nc.gpsimd, 'sp': nc.sync}
        self.seq = {e: 0 for e in self.eng}
        self.csem = {e: [] for e in self.eng}
        self.known = {e: {} for e in self.eng}
        self.snap = {}
        self.last_w = {}
        self.readers = {}
        self.semobj = {}
        self.dma_pool = {}
        self.nsem = 0
        self.nwaits = 0
        self.nops = 0
        for q, n in (('sp', 24), ('pool', 24), ('act', 8)):
            self.dma_pool[q] = {'sems': [self._newsem(f"d{q}{i}") for i in range(n)],
                                'cnt': [0] * n, 'next': 0}

    def _newsem(self, name):
        s = self.es.enter_context(self.nc.semaphore(name))
        self.semobj[name] = s
        self.nsem += 1
        return name

    def _need(self, e, tok, skip_self):
        if tok is None:
            return
        name, val, owner = tok
        if skip_self and owner == e:
            return
        if self.known[e].get(name, 0) >= val:
            return
        self.eng[e].wait_ge(self.semobj[name], val)
        self.nwaits += 1
        k = self.known[e]
        k[name] = val
        sn = self.snap.get((name, val))
        if sn:
            for n2, v2 in sn.items():
                if k.get(n2, 0) < v2:
                    k[n2] = v2

    def op(self, e, fn, reads=(), writes=(), dma=False, skip_self=None):
        if skip_self is None:
            skip_self = (e == 'pe')
        if dma:
            skip_self = False
        for r in reads:
            self._need(e, self.last_w.get(r), skip_self)
        for w in writes:
            self._need(e, self.last_w.get(w), skip_self)
            for t in self.readers.get(w, ()):
                self._need(e, t, skip_self)
        self.nops += 1
        if dma:
            pool = self.dma_pool[e]
            i = pool['next']
            pool['next'] = (i + 1) % len(pool['sems'])
            name = pool['sems'][i]
            if pool['cnt'][i] + 16 > MAXV:
                name = self._newsem(f"{name}r{self.nsem}")
                pool['sems'][i] = name
                pool['cnt'][i] = 0
            prev = pool['cnt'][i]
            if prev > 0:
                self._need(e, (name, prev, e + '_dma'), False)
            ins = fn(self.eng[e])
            pool['cnt'][i] = prev + 16
            ins.then_inc(self.semobj[name], 16)
            tok = (name, prev + 16, e + '_dma')
        else:
            n = self.seq[e]
            ep = n // MAXV
            while len(self.csem[e]) <= ep:
                self.csem[e].append(self._newsem(f"c{e}{len(self.csem[e])}"))
            name = self.csem[e][ep]
            ins = fn(self.eng[e])
            ins.then_inc(self.semobj[name], 1)
            self.seq[e] = n + 1
            tok = (name, n - ep * MAXV + 1, e)
        self.snap[(tok[0], tok[1])] = dict(self.known[e])
        for r in reads:
            self.readers.setdefault(r, []).append(tok)
        for w in writes:
            self.last_w[w] = tok
            self.readers[w] = []
        return tok

    def fence(self, e):
        n = self.seq[e]
        if n > 0:
            ep = (n - 1) // MAXV
            self._need(e, (self.csem[e][ep], n - ep * MAXV, e), False)

    def barrier(self):
        toks = []
        for e in self.eng:
            n = self.seq[e]
            if n > 0:
                ep = (n - 1) // MAXV
                toks.append((self.csem[e][ep], n - ep * MAXV, e))
        for q, pool in self.dma_pool.items():
            for name, c in zip(pool['sems'], pool['cnt']):
                if c > 0:
                    toks.append((name, c, q + '_dma'))
        for e in self.eng:
            for t in toks:
                self._need(e, t, False)
        self.last_w.clear()
        self.readers.clear()
        self.snap.clear()


class Stage:
    _n = 0

    def __init__(self, K, name):
        self.K = K
        Stage._n += 1
        self.name = f"{name}{Stage._n}"

    def __enter__(self):
        self.es = ExitStack()
        self.es.__enter__()
        return self

    def T(self, name, shape, dt=F32):
        return self.es.enter_context(self.K.nc.sbuf_tensor(f"{self.name}_{name}", shape, dt))

    def __exit__(self, *a):
        self.K.P.barrier()
        return self.es.__exit__(*a)


class Kern:
    def __init__(self, cfg):
        self.cfg = cfg

    def build(self):
        nc = bass.Bass("TRN2", target_bir_lowering=False)
        self.nc = nc
        dbg = self.cfg.get('debug', False)

        def din(name, shape, dt=F32):
            return nc.dram_tensor(name, list(shape), dt, kind="ExternalInput").ap()

        def dscr(name, shape, dt=F32):
            kind = "ExternalOutput" if (dbg and name in self.cfg.get('expose', ())) else "Internal"
            return nc.dram_tensor(name, list(shape), dt, kind=kind).ap()

        A = {}
        A['x'] = din('x', [S, D])
        A['c_l'] = din('c_l', [128, 8])
        A['ada_mix_w'] = din('ada_mix_w', [2, D, 3 * D])
        A['ada_ffn_w'] = din('ada_ffn_w', [2, D, 3 * D])
        A['ada_b'] = din('ada_b', [4, 3 * D])
        A['ln_g'] = din('ln_g', [4, D])
        A['ln_b'] = din('ln_b', [4, D])
        A['conv_in_w'] = din('conv_in_w', [D, 2 * D])
        A['conv_in_b_l'] = din('conv_in_b_l', [128, 16])
        A['conv_dw_w_l'] = din('conv_dw_w_l', [128, 8, 31])
        A['conv_vec_l'] = din('conv_vec_l', [128, 3, 8])
        A['conv_out_w'] = din('conv_out_w', [D, D])
        A['conv_out_b'] = din('conv_out_b', [1, D])
        A['attn_in_w'] = din('attn_in_w', [D, 3 * D + 16])
        A['attn_qkb_l'] = din('attn_qkb_l', [128, 16])
        A['attn_vb'] = din('attn_vb', [1, D])
        A['attn_fb'] = din('attn_fb', [16, 1])
        A['attn_out_w'] = din('attn_out_w', [D, D])
        A['attn_out_b'] = din('attn_out_b', [1, D])
        A['peer_query_w'] = din('peer_query_w', [2, D, 2 * D])
        A['peer_skT'] = din('peer_skT', [2, 2, 128, 128])
        A['peer_u'] = din('peer_u', [2, NEXP, D])
        A['peer_v'] = din('peer_v', [2, NEXP, D])
        A['out'] = nc.dram_tensor('out', [S, D], F32, kind="ExternalOutput").ap()
        A['X1'] = dscr('X1', [S, D])
        A['X2'] = dscr('X2', [S, D])
        A['X3'] = dscr('X3', [S, D])
        A['ST'] = dscr('ST', [8, 128, S])
        A['IDX'] = dscr('IDX', [S, 128], I32)
        A['SCR'] = dscr('SCR', [S, 2048])
        A['UVB'] = dscr('UVB', [2 * NEXP, 2 * D], BF16)
        A['H'] = dscr('H', [S, D])
        A['GATE'] = dscr('GATE', [S, 128])
        A['QA'] = dscr('QA', [16, 68, S])
        A['KA'] = dscr('KA', [16, 68, S])
        A['V'] = dscr('V', [S, D])
        A['AOT'] = dscr('AOT', [8, 128, S])
        self.A = A

        with ExitStack() as es:
            self.P = P = Prog(nc, es)
            G = lambda name, shape, dt=F32: es.enter_context(nc.sbuf_tensor(name, shape, dt))
            self.ps = [es.enter_context(nc.psum_tensor(f"ps{i}", [128, 512], F32)) for i in range(8)]
            self.ident = G('ident', [128, 128])
            self.ones = G('ones', [128, 128])
            self.SC = G('SC', [128, 8, 128])
            self.shift_bc = G('shift_bc', [128, D])
            self.scale_bc = G('scale_bc', [128, D])
            self.gate_bc = G('gate_bc', [128, D])
            self.g_bc = G('g_bc', [128, D])
            self.b_bc = G('b_bc', [128, D])
            self.bs = G('bs', [128, 2, 6])
            self.mv = G('mv', [128, 2])
            self.rs = G('rs', [128, 1])
            self.emit_globals()
            self._cast_done = False
            if self.cfg.get('peer_bf16', True) and self.cfg.get('cast_stage', False):
                self.emit_cast_tables()
                self._cast_done = True
            order = self.cfg.get('stages', ['conv', 'peer0', 'attn', 'peer1'])
            cur = A['x']
            nxt = {'conv': A['X1'], 'peer0': A['X2'], 'attn': A['X3'], 'peer1': A['out']}
            for i, st in enumerate(order):
                dst = A['out'] if i == len(order) - 1 else nxt[st]
                if st == 'conv':
                    self.emit_adaln(0)
                    self.emit_conv1(cur)
                    self.emit_proj_out(cur, dst, A['conv_out_w'], A['conv_out_b'], src_fm=A['ST'])
                elif st == 'attn':
                    self.emit_adaln(2)
                    self.emit_attn1(cur)
                    self.emit_attn2()
                    self.emit_proj_out(cur, dst, A['attn_out_w'], A['attn_out_b'], src_fm=A['AOT'])
                else:
                    L = int(st[-1])
                    self.emit_adaln(1 + 2 * L)
                    if self.cfg.get('peer_bf16', True):
                        self.emit_peer1a(cur, L, cast=not self._cast_done)
                        self.emit_peer2g(cur, dst, L)
                    elif self.cfg.get('peer_fused', True):
                        self.emit_peer1a(cur, L)
                        self.emit_peer2f(cur, dst, L)
                    else:
                        self.emit_peer1(cur, L)
                        self.emit_peer2(cur, dst, L)
                cur = dst
            P.barrier()
            print(f"[kern] ops={P.nops} waits={P.nwaits} sems={P.nsem} seq={P.seq}")
        return nc

    def emit_globals(self):
        P, nc = self.P, self.nc
        ident, ones = self.ident, self.ones
        P.op('pool', lambda e: e.memset(ident[:], 1.0), writes=['ident'])
        P.op('pool', lambda e: e.affine_select(out=ident[:], in_=ident[:], pattern=[[-1, 128]],
                                               compare_op=ALU.is_equal, fill=0.0, base=0, channel_multiplier=1),
             reads=['ident'], writes=['ident'])
        P.op('pool', lambda e: e.memset(ones[:], 1.0), writes=['ones'])
        with Stage(self, 'gl') as st:
            ct = st.T('ct', [128, 8])
            P.op('sp', lambda e: e.dma_start(out=ct[:], in_=self.A['c_l']), writes=['ct'], dma=True)
            P.op('act', lambda e: e.activation(out=ct[:], in_=ct[:], func=AF.Silu), reads=['ct'], writes=['ct'])
            SC = self.SC
            P.op('dve', lambda e: e.tensor_copy(out=SC[:], in_=ct[:].unsqueeze(2).to_broadcast([128, 8, 128])),
                 reads=['ct'], writes=['SC'])

    def modulate(self, xt, ht, rx, rh, eng0='dve'):
        P = self.P
        P.op(eng0, lambda e: e.tensor_tensor(out=ht, in0=xt, in1=self.scale_bc[:], op=ALU.mult),
             reads=[rx, 'scale_bc'], writes=[rh])
        P.op('pool', lambda e: e.tensor_tensor(out=ht, in0=ht, in1=self.shift_bc[:], op=ALU.add),
             reads=[rh, 'shift_bc'], writes=[rh])

    def transpose8(self, src, rsrc, dstT, rdst, col0, pb, evac1='dve'):
        P = self.P
        ps = self.ps
        for half in range(2):
            bank = ps[pb + half]
            rb = f'ps{pb + half}'
            for kk in range(4):
                k = half * 4 + kk
                P.op('pe', lambda e, k=k, kk=kk, bank=bank: e.transpose(out=bank[:, kk * 128:(kk + 1) * 128],
                                                                      in_=src[:, k * 128:(k + 1) * 128],
                                                                      identity=self.ident[:]),
                     reads=[rsrc, 'ident'], writes=[rb])
            dst = dstT[:, half * 4:half * 4 + 4, col0:col0 + 128]
            srcp = bank[:].rearrange("p (k n) -> p k n", k=4)
            if half == 0 or evac1 == 'act':
                P.op('act', lambda e, dst=dst, srcp=srcp: e.copy(out=dst, in_=srcp), reads=[rb], writes=[rdst])
            else:
                P.op('dve', lambda e, dst=dst, srcp=srcp: e.tensor_copy(out=dst, in_=srcp), reads=[rb], writes=[rdst])

    def layernorm_inplace(self, r, rr, gb='pool'):
        P = self.P
        bs, mv, rs = self.bs, self.mv, self.rs
        for c in range(2):
            P.op('dve', lambda e, c=c: e.bn_stats(out=bs[:, c, :], in_=r[:, c * 512:(c + 1) * 512]),
                 reads=[rr], writes=['bs'])
        P.op('dve', lambda e: e.bn_aggr(out=mv[:], in_=bs[:].rearrange("p a b -> p (a b)")), reads=['bs'], writes=['mv'])
        P.op('dve', lambda e: e.tensor_scalar(out=rs[:], in0=mv[:, 1:2], scalar1=EPS, scalar2=None, op0=ALU.add),
             reads=['mv'], writes=['rs'])
        P.op('act', lambda e: e.activation(out=rs[:], in_=rs[:], func=AF.Sqrt), reads=['rs'], writes=['rs'])
        P.op('dve', lambda e: e.reciprocal(out=rs[:], in_=rs[:]), reads=['rs'], writes=['rs'])
        P.op('dve', lambda e: e.tensor_scalar(out=r, in0=r, scalar1=mv[:, 0:1], scalar2=rs[:, 0:1],
                                              op0=ALU.subtract, op1=ALU.mult), reads=[rr, 'mv', 'rs'], writes=[rr])
        P.op(gb, lambda e: e.tensor_tensor(out=r, in0=r, in1=self.g_bc[:], op=ALU.mult), reads=[rr, 'g_bc'], writes=[rr])
        P.op(gb, lambda e: e.tensor_tensor(out=r, in0=r, in1=self.b_bc[:], op=ALU.add), reads=[rr, 'b_bc'], writes=[rr])

    def emit_adaln(self, sub):
        P, nc, A, ps = self.P, self.nc, self.A, self.ps
        L = sub // 2
        wsrc = (A['ada_mix_w'] if sub % 2 == 0 else A['ada_ffn_w'])[L]
        ones = self.ones
        with Stage(self, f'ada{sub}') as st:
            brow = st.T('brow', [1, 3 * D])
            lrow = st.T('lrow', [1, 2 * D])
            wch = [st.T(f'wch{i}', [128, 8, 512]) for i in range(2)]
            P.op('sp', lambda e: e.dma_start(out=brow[:], in_=A['ada_b'][sub:sub + 1, :]), writes=['brow'], dma=True)
            P.op('sp', lambda e: e.dma_start(out=lrow[:, 0:D], in_=A['ln_g'][sub:sub + 1, :]), writes=['lrow'], dma=True)
            P.op('sp', lambda e: e.dma_start(out=lrow[:, D:2 * D], in_=A['ln_b'][sub:sub + 1, :]), writes=['lrow'], dma=True)
            dsts = [self.shift_bc, self.shift_bc, self.scale_bc, self.scale_bc, self.gate_bc, self.gate_bc]
            names = ['shift_bc', 'shift_bc', 'scale_bc', 'scale_bc', 'gate_bc', 'gate_bc']
            for n6 in range(6):
                wb = wch[n6 % 2]
                rw = f'wch{n6 % 2}'
                P.op('sp', lambda e, wb=wb, n6=n6: e.dma_start(
                    out=wb[:], in_=wsrc[:, n6 * 512:(n6 + 1) * 512].rearrange("(k p) n -> p k n", p=128)),
                    writes=[rw], dma=True)
                bank = ps[n6 % 2]
                rb = f'ps{n6 % 2}'
                for k in range(8):
                    P.op('pe', lambda e, k=k, wb=wb, bank=bank: e.matmul(out=bank[:], lhsT=self.SC[:, k, :], rhs=wb[:, k, :],
                                                                        start=(k == 0), stop=False),
                         reads=['SC', rw], writes=[rb])
                P.op('pe', lambda e, bank=bank, n6=n6: e.matmul(out=bank[:], lhsT=ones[0:1, :], rhs=brow[0:1, n6 * 512:(n6 + 1) * 512],
                                                                start=False, stop=True),
                     reads=['ones', 'brow'], writes=[rb])
                dst = dsts[n6][:, (n6 % 2) * 512:(n6 % 2 + 1) * 512]
                if n6 in (2, 3):
                    P.op('dve', lambda e, dst=dst, bank=bank: e.tensor_scalar(out=dst, in0=bank[:], scalar1=1.0, scalar2=None, op0=ALU.add),
                         reads=[rb], writes=[names[n6]])
                else:
                    P.op('dve', lambda e, dst=dst, bank=bank: e.tensor_copy(out=dst, in_=bank[:]), reads=[rb], writes=[names[n6]])
            for j in range(4):
                bank = ps[2 + j % 2]
                rb = f'ps{2 + j % 2}'
                P.op('pe', lambda e, bank=bank, j=j: e.matmul(out=bank[:], lhsT=ones[0:1, :], rhs=lrow[0:1, j * 512:(j + 1) * 512],
                                                              start=True, stop=True), reads=['ones', 'lrow'], writes=[rb])
                dstt = self.g_bc if j < 2 else self.b_bc
                dst = dstt[:, (j % 2) * 512:(j % 2 + 1) * 512]
                P.op('act', lambda e, dst=dst, bank=bank: e.copy(out=dst, in_=bank[:]), reads=[rb],
                     writes=['g_bc' if j < 2 else 'b_bc'])

    def emit_conv1(self, xin):
        P, nc, A, ps = self.P, self.nc, self.A, self.ps
        ones = self.ones
        with Stage(self, 'c1') as st:
            win = st.T('win', [128, 8, 2048])
            cib = st.T('cib', [128, 16])
            dw = st.T('dw', [128, 8, 31])
            cv = st.T('cv', [128, 3, 8])
            xts = [st.T(f'xt{i}', [128, D]) for i in range(2)]
            ht = st.T('ht', [128, D])
            hT = st.T('hT', [128, 8, 512])
            acc = st.T('acc', [128, 8, 512])
            aexts = [st.T(f'aext{i}', [128, 8, 542]) for i in range(2)]
            sig = [st.T(f'sig{i}', [128, 512]) for i in range(2)]
            sq = [st.T(f'sq{i}', [128, 512]) for i in range(2)]
            meant = st.T('meant', [128, 512])
            rstd = st.T('rstd', [128, 512])
            tmp = st.T('tmp', [128, 512])
            for q in range(4):
                P.op('sp', lambda e, q=q: e.dma_start(out=win[:, :, q * 512:(q + 1) * 512],
                                                      in_=A['conv_in_w'][:, q * 512:(q + 1) * 512].rearrange("(k p) n -> p k n", p=128)),
                     writes=[('win', q)], dma=True)
            P.op('sp', lambda e: e.dma_start(out=cib[:], in_=A['conv_in_b_l']), writes=['cib'], dma=True)
            P.op('sp', lambda e: e.dma_start(out=dw[:], in_=A['conv_dw_w_l']), writes=['dw'], dma=True)
            P.op('sp', lambda e: e.dma_start(out=cv[:], in_=A['conv_vec_l']), writes=['cv'], dma=True)
            for cc in range(8):
                P.op('pool', lambda e, cc=cc: e.memset(aexts[0][:, cc, 0:30], 0.0), writes=[('aext0', cc)])
            accs_ = [acc, st.T('acc2', [128, 8, 512])]

            def load_T(jb):
                for tl in range(4):
                    ti = jb * 4 + tl
                    xt = xts[ti % 2]
                    rx = f'xt{ti % 2}'
                    P.op('sp', lambda e, xt=xt, ti=ti: e.dma_start(out=xt[:], in_=xin[ti * 128:(ti + 1) * 128, :]),
                         writes=[rx], dma=True)
                    self.modulate(xt[:], ht[:], rx, 'ht', eng0='pool')
                    self.transpose8(ht, 'ht', hT, 'hT', tl * 128, 0, evac1='act')

            def glu_chunk(jb, cc):
                aext = aexts[jb % 2]
                AX_ = f'aext{jb % 2}'
                pa, pb = ps[2 + (cc % 2) * 2], ps[3 + (cc % 2) * 2]
                ra, rb = f'ps{2 + (cc % 2) * 2}', f'ps{3 + (cc % 2) * 2}'
                for k in range(8):
                    P.op('pe', lambda e, k=k: e.matmul(out=pa[:], lhsT=win[:, k, cc * 128:(cc + 1) * 128], rhs=hT[:, k, :],
                                                       start=(k == 0), stop=(k == 7)),
                         reads=[('win', cc // 4), 'hT'], writes=[ra])
                for k in range(8):
                    P.op('pe', lambda e, k=k: e.matmul(out=pb[:], lhsT=win[:, k, D + cc * 128:D + (cc + 1) * 128], rhs=hT[:, k, :],
                                                       start=(k == 0), stop=(k == 7)),
                         reads=[('win', 2 + cc // 4), 'hT'], writes=[rb])
                sg = sig[cc % 2]
                rsg = f'sig{cc % 2}'
                P.op('act', lambda e: e.activation(out=sg[:], in_=pb[:], func=AF.Sigmoid, bias=cib[:, 8 + cc:9 + cc], scale=1.0),
                     reads=[rb, 'cib'], writes=[rsg])
                P.op('dve', lambda e: e.scalar_tensor_tensor(out=aext[:, cc, 30:542], in0=pa[:], scalar=cib[:, cc:cc + 1], in1=sg[:],
                                                             op0=ALU.add, op1=ALU.mult),
                     reads=[ra, rsg, 'cib'], writes=[(AX_, cc)])

            def conv_pair(jb, c0):
                aext, anext = aexts[jb % 2], aexts[(jb + 1) % 2]
                AX_, AN_ = f'aext{jb % 2}', f'aext{(jb + 1) % 2}'
                ac = accs_[jb % 2]
                RA = f'acc{jb % 2}'
                for cc in (c0, c0 + 1):
                    P.op('dve', lambda e, cc=cc: e.tensor_scalar(out=ac[:, cc, :], in0=aext[:, cc, 0:512], scalar1=dw[:, cc, 0:1], scalar2=cv[:, 0, cc:cc + 1],
                                                                 op0=ALU.mult, op1=ALU.add),
                         reads=[(AX_, cc), 'dw', 'cv'], writes=[(RA, cc)])
                for w in range(1, 31):
                    for cc in (c0, c0 + 1):
                        P.op('dve', lambda e, w=w, cc=cc: e.scalar_tensor_tensor(out=ac[:, cc, :], in0=aext[:, cc, w:w + 512], scalar=dw[:, cc, w:w + 1],
                                                                                 in1=ac[:, cc, :], op0=ALU.mult, op1=ALU.add),
                             reads=[(AX_, cc), 'dw', (RA, cc)], writes=[(RA, cc)])
                for cc in (c0, c0 + 1):
                    P.op('act', lambda e, cc=cc: e.copy(out=anext[:, cc, 0:30], in_=aext[:, cc, 512:542]),
                         reads=[(AX_, cc)], writes=[(AN_, cc)])

            def stats_pe(jb):
                ac, RA = accs_[jb % 2], f'acc{jb % 2}'
                for cc in range(8):
                    s2 = sq[cc % 2]
                    rs2 = f'sq{cc % 2}'
                    P.op('act', lambda e, cc=cc, s2=s2: e.activation(out=s2[:], in_=ac[:, cc, :], func=AF.Square),
                         reads=[(RA, cc)], writes=[rs2])
                    P.op('pe', lambda e, cc=cc: e.matmul(out=ps[6][:], lhsT=ones[:], rhs=ac[:, cc, :], start=(cc == 0), stop=(cc == 7)),
                         reads=['ones', (RA, cc)], writes=['ps6'])
                    P.op('pe', lambda e, cc=cc, s2=s2: e.matmul(out=ps[7][:], lhsT=ones[:], rhs=s2[:], start=(cc == 0), stop=(cc == 7)),
                         reads=['ones', rs2], writes=['ps7'])
                P.op('act', lambda e: e.activation(out=meant[:], in_=ps[6][:], func=AF.Copy, scale=1.0 / D), reads=['ps6'], writes=['meant'])

            def stats_dve(jb):
                P.op('dve', lambda e: e.tensor_tensor(out=tmp[:], in0=meant[:], in1=meant[:], op=ALU.mult), reads=['meant'], writes=['tmp'])
                P.op('dve', lambda e: e.scalar_tensor_tensor(out=rstd[:], in0=ps[7][:], scalar=1.0 / D, in1=tmp[:], op0=ALU.mult, op1=ALU.subtract),
                     reads=['ps7', 'tmp'], writes=['rstd'])
                P.op('dve', lambda e: e.tensor_scalar(out=rstd[:], in0=rstd[:], scalar1=EPS, scalar2=None, op0=ALU.add), reads=['rstd'], writes=['rstd'])
                P.op('act', lambda e: e.activation(out=rstd[:], in_=rstd[:], func=AF.Sqrt), reads=['rstd'], writes=['rstd'])
                P.op('dve', lambda e: e.reciprocal(out=rstd[:], in_=rstd[:]), reads=['rstd'], writes=['rstd'])

            def norm_chunk(jb, cc):
                ac, RA = accs_[jb % 2], f'acc{jb % 2}'
                P.op('pool', lambda e: e.tensor_tensor(out=ac[:, cc, :], in0=ac[:, cc, :], in1=meant[:], op=ALU.subtract),
                     reads=[(RA, cc), 'meant'], writes=[(RA, cc)])
                P.op('pool', lambda e: e.tensor_tensor(out=ac[:, cc, :], in0=ac[:, cc, :], in1=rstd[:], op=ALU.mult),
                     reads=[(RA, cc), 'rstd'], writes=[(RA, cc)])
                P.op('act', lambda e: e.activation(out=ac[:, cc, :], in_=ac[:, cc, :], func=AF.Silu,
                                                   bias=cv[:, 2, cc:cc + 1], scale=cv[:, 1, cc:cc + 1]),
                     reads=[(RA, cc), 'cv'], writes=[(RA, cc)])

            def store_block(jb):
                ac, RA = accs_[jb % 2], f'acc{jb % 2}'
                P.op('act', lambda e: e.dma_start(out=A['ST'][:, :, jb * 512:(jb + 1) * 512].rearrange("c p t -> p c t"), in_=ac[:]),
                     reads=[(RA, cc) for cc in range(8)], writes=[('ST', jb)], dma=True)

            load_T(0)
            for cc in range(8):
                glu_chunk(0, cc)
            for jb in range(8):
                if jb + 1 < 8:
                    load_T(jb + 1)
                for c0 in range(0, 8, 2):
                    conv_pair(jb, c0)
                    if jb + 1 < 8:
                        glu_chunk(jb + 1, c0)
                        glu_chunk(jb + 1, c0 + 1)
                stats_pe(jb)
                stats_dve(jb)
                for cc in range(8):
                    norm_chunk(jb, cc)
                store_block(jb)

    def emit_proj_out(self, xin, dst, w_ap, b_ap, src_fm=None, src_tm=None):
        P, nc, A, ps = self.P, self.nc, self.A, self.ps
        ones = self.ones
        NBUF = 4
        with Stage(self, 'po') as st:
            wo = st.T('wo', [128, 8, D], F32R)
            wstg = st.T('wstg', [128, 8, 512])
            bo = st.T('bo', [1, D])
            bo_bc = st.T('bo_bc', [128, D])
            xts = [st.T(f'xt{i}', [128, D]) for i in range(NBUF)]
            rts = [st.T(f'rt{i}', [128, D]) for i in range(NBUF)]
            bss = [st.T(f'bs{i}', [128, 2, 6]) for i in range(NBUF)]
            mvs = [st.T(f'mv{i}', [128, 2]) for i in range(NBUF)]
            rss = [st.T(f'rs{i}', [128, 1]) for i in range(NBUF)]
            sT = [st.T(f'sT{i}', [128, 8, 512], F32R) for i in range(2)]
            sstg = st.T('sstg', [128, 8, 512])
            for q in range(2):
                P.op('sp', lambda e, q=q: e.dma_start(out=wstg[:], in_=w_ap[:, q * 512:(q + 1) * 512].rearrange("(k p) n -> p k n", p=128)),
                     writes=['wstg'], dma=True)
                P.op('pool', lambda e, q=q: e.tensor_copy(out=wo[:, :, q * 512:(q + 1) * 512], in_=wstg[:]), reads=['wstg'], writes=[('wo', q)])
            P.op('sp', lambda e: e.dma_start(out=bo[:], in_=b_ap), writes=['bo'], dma=True)
            for half in range(2):
                P.op('pe', lambda e, half=half: e.matmul(out=ps[half][:], lhsT=ones[0:1, :], rhs=bo[0:1, half * 512:(half + 1) * 512], start=True, stop=True),
                     reads=['ones', 'bo'], writes=[f'ps{half}'])
                P.op('act', lambda e, half=half: e.copy(out=bo_bc[:, half * 512:(half + 1) * 512], in_=ps[half][:]), reads=[f'ps{half}'], writes=['bo_bc'])

            def phaseA(ti):
                b = ti % NBUF
                xt, rt = xts[b], rts[b]
                rx, rr = f'xt{b}', f'rt{b}'
                P.op('sp', lambda e: e.dma_start(out=xt[:], in_=xin[ti * 128:(ti + 1) * 128, :]), writes=[rx], dma=True)
                jb, tl = ti // 4, ti % 4
                sb = sT[jb % 2]
                rsb = f'sT{jb % 2}'
                if tl == 0:
                    P.op('sp', lambda e: e.dma_start(out=sstg[:], in_=src_fm[:, :, jb * 512:(jb + 1) * 512].rearrange("c p t -> p c t")),
                         reads=[('ST', jb)], writes=['sstg'], dma=True)
                    P.op('act', lambda e: e.copy(out=sb[:], in_=sstg[:]), reads=['sstg'], writes=[rsb])
                pb = (ti % 4) * 2
                for half in range(2):
                    bank = ps[pb + half]
                    rb = f'ps{pb + half}'
                    for k in range(8):
                        P.op('pe', lambda e, k=k, bank=bank, half=half: e.matmul(out=bank[:], lhsT=sb[:, k, tl * 128:(tl + 1) * 128],
                                                                                rhs=wo[:, k, half * 512:(half + 1) * 512], start=(k == 0), stop=(k == 7)),
                             reads=[rsb, ('wo', half)], writes=[rb])
                    hs = slice(half * 512, (half + 1) * 512)
                    P.op('dve', lambda e, bank=bank, hs=hs: e.tensor_tensor(out=rt[:, hs], in0=bank[:], in1=bo_bc[:, hs], op=ALU.add),
                         reads=[rb, 'bo_bc'], writes=[(rr, half)])
                    P.op('pool', lambda e, hs=hs: e.tensor_tensor(out=rt[:, hs], in0=rt[:, hs], in1=self.gate_bc[:, hs], op=ALU.mult),
                         reads=[(rr, half), 'gate_bc'], writes=[(rr, half)])

            def phaseB(ti):
                b = ti % NBUF
                xt, rt, bs, mv, rs = xts[b], rts[b], bss[b], mvs[b], rss[b]
                rx, rr = f'xt{b}', f'rt{b}'
                P.op('dve', lambda e: e.scalar_tensor_tensor(out=rt[:], in0=xt[:], scalar=ALPHA, in1=rt[:], op0=ALU.mult, op1=ALU.add),
                     reads=[rx, (rr, 0), (rr, 1)], writes=[rr])
                for c in range(2):
                    P.op('dve', lambda e, c=c: e.bn_stats(out=bs[:, c, :], in_=rt[:, c * 512:(c + 1) * 512]), reads=[rr], writes=[(f'bs{b}', c)])
                P.op('dve', lambda e: e.bn_aggr(out=mv[:], in_=bs[:].rearrange("p a b -> p (a b)")), reads=[(f'bs{b}', 0), (f'bs{b}', 1)], writes=[f'mv{b}'])
                P.op('dve', lambda e: e.tensor_scalar(out=rs[:], in0=mv[:, 1:2], scalar1=EPS, scalar2=None, op0=ALU.add), reads=[f'mv{b}'], writes=[f'rs{b}'])
                P.op('act', lambda e: e.activation(out=rs[:], in_=rs[:], func=AF.Sqrt), reads=[f'rs{b}'], writes=[f'rs{b}'])

            def phaseC(ti):
                b = ti % NBUF
                rt, mv, rs = rts[b], mvs[b], rss[b]
                rr = f'rt{b}'
                P.op('dve', lambda e: e.reciprocal(out=rs[:], in_=rs[:]), reads=[f'rs{b}'], writes=[f'rs{b}'])
                P.op('dve', lambda e: e.tensor_scalar(out=rt[:], in0=rt[:], scalar1=mv[:, 0:1], scalar2=rs[:, 0:1],
                                                      op0=ALU.subtract, op1=ALU.mult), reads=[rr, f'mv{b}', f'rs{b}'], writes=[rr])
                P.op('pool', lambda e: e.tensor_tensor(out=rt[:], in0=rt[:], in1=self.g_bc[:], op=ALU.mult), reads=[rr, 'g_bc'], writes=[rr])
                P.op('pool', lambda e: e.tensor_tensor(out=rt[:], in0=rt[:], in1=self.b_bc[:], op=ALU.add), reads=[rr, 'b_bc'], writes=[rr])
                P.op('act', lambda e: e.dma_start(out=dst[ti * 128:(ti + 1) * 128, :], in_=rt[:]), reads=[rr], writes=[(rr, 0), (rr, 1)], dma=True)

            for i in range(NT + 2):
                if i < NT:
                    phaseA(i)
                if 0 <= i - 1 < NT:
                    phaseB(i - 1)
                if 0 <= i - 2 < NT:
                    phaseC(i - 2)

    def emit_peer1(self, xin, L):
        P, nc, A, ps = self.P, self.nc, self.A, self.ps
        with Stage(self, f'p1{L}') as st:
            wq = st.T('wq', [128, 8, 2048])
            skT = st.T('skT', [128, 2, 128])
            xts = [st.T(f'xt{i}', [128, D]) for i in range(2)]
            ht = st.T('ht', [128, D])
            hT = st.T('hT', [128, 8, 256])
            qT = st.T('qT', [128, 16, 256])
            sc = st.T('sc', [128, 16, 128])
            m = st.T('m', [128, 16, 16])
            ix = st.T('ix', [128, 16, 16], U32)
            ixf = st.T('ixf', [128, 16, 16])
            wk = st.T('wk', [128, 16, 128])
            cand = st.T('cand', [128, 8, 256])
            candi = st.T('candi', [128, 8, 256])
            wk2 = st.T('wk2', [128, 8, 256])
            junk = [st.T(f'junk{i}', [128, 256]) for i in range(2)]
            ts = st.T('ts', [128, 8, 16])
            ef = st.T('ef', [128, 128])
            ei = [st.T(f'ei{i}', [128, 128], I32) for i in range(2)]
            gt = [st.T(f'gt{i}', [128, 8, 16]) for i in range(2)]
            gsum = st.T('gsum', [128, 8])
            for q in range(4):
                P.op('sp', lambda e, q=q: e.dma_start(out=wq[:, :, q * 512:(q + 1) * 512],
                                                      in_=A['peer_query_w'][L][:, q * 512:(q + 1) * 512].rearrange("(k p) n -> p k n", p=128)),
                     writes=[('wq', q)], dma=True)
            P.op('sp', lambda e: e.dma_start(out=skT[:], in_=A['peer_skT'][L].rearrange("h d k -> d h k")), writes=['skT'], dma=True)
            ti = 0
            for jb in range(S // 256):
                for tl in range(2):
                    xt = xts[ti % 2]
                    rx = f'xt{ti % 2}'
                    P.op('sp', lambda e, xt=xt, ti=ti: e.dma_start(out=xt[:], in_=xin[ti * 128:(ti + 1) * 128, :]), writes=[rx], dma=True)
                    self.modulate(xt[:], ht[:], rx, 'ht')
                    self.transpose8(ht, 'ht', hT, 'hT', tl * 128, 0)
                    ti += 1
                for c in range(16):
                    bank = ps[2 + c % 2]
                    rb = f'ps{2 + c % 2}'
                    for k in range(8):
                        P.op('pe', lambda e, k=k, c=c, bank=bank: e.matmul(out=bank[:, 0:256], lhsT=wq[:, k, c * 128:(c + 1) * 128], rhs=hT[:, k, :],
                                                                          start=(k == 0), stop=(k == 7)),
                             reads=[('wq', c // 4), 'hT'], writes=[rb])
                    if c % 2 == 0:
                        P.op('act', lambda e, c=c, bank=bank: e.copy(out=qT[:, c, :], in_=bank[:, 0:256]), reads=[rb], writes=[('qT', c)])
                    else:
                        P.op('dve', lambda e, c=c, bank=bank: e.tensor_copy(out=qT[:, c, :], in_=bank[:, 0:256]), reads=[rb], writes=[('qT', c)])
                for tl in range(2):
                    tix = jb * 2 + tl
                    for c in range(16):
                        bank = ps[4 + c // 4]
                        rb = f'ps{4 + c // 4}'
                        P.op('pe', lambda e, c=c, bank=bank, tl=tl: e.matmul(out=bank[:, (c % 4) * 128:(c % 4 + 1) * 128],
                                                                            lhsT=qT[:, c, tl * 128:(tl + 1) * 128], rhs=skT[:, c % 2, :],
                                                                            start=True, stop=True),
                             reads=[('qT', c), 'skT'], writes=[rb])
                    for g4 in range(4):
                        P.op('act', lambda e, g4=g4: e.copy(out=sc[:, g4 * 4:(g4 + 1) * 4, :], in_=ps[4 + g4][:].rearrange("p (a k) -> p a k", a=4)),
                             reads=[f'ps{4 + g4}'], writes=['sc'])
                    for c in range(16):
                        P.op('dve', lambda e, c=c: e.max(out=m[:, c, 0:8], in_=sc[:, c, :]), reads=['sc'], writes=[('m0', c)])
                    P.fence('dve')
                    for c in range(16):
                        P.op('dve', lambda e, c=c: e.max_index(out=ix[:, c, 0:8], in_max=m[:, c, 0:8], in_values=sc[:, c, :]),
                             reads=['sc', ('m0', c)], writes=[('ix0', c)])
                        P.op('dve', lambda e, c=c: e.match_replace(out=wk[:, c, :], in_to_replace=m[:, c, 0:8], in_values=sc[:, c, :], imm_value=-1e30),
                             reads=['sc', ('m0', c)], writes=[('wk', c)])
                    P.fence('dve')
                    for c in range(16):
                        P.op('dve', lambda e, c=c: e.max(out=m[:, c, 8:16], in_=wk[:, c, :]), reads=[('wk', c)], writes=[('m1', c)])
                    P.fence('dve')
                    for c in range(16):
                        P.op('dve', lambda e, c=c: e.max_index(out=ix[:, c, 8:16], in_max=m[:, c, 8:16], in_values=wk[:, c, :]),
                             reads=[('wk', c), ('m1', c)], writes=[('ix1', c)])
                    P.fence('dve')
                    mres = [('m0', c) for c in range(16)] + [('m1', c) for c in range(16)]
                    ixres = [('ix0', c) for c in range(16)] + [('ix1', c) for c in range(16)]
                    P.op('dve', lambda e: e.tensor_copy(out=ixf[:], in_=ix[:]), reads=ixres, writes=['ixf'])
                    m4 = m[:].rearrange("p (h two) k -> p h two k", two=2)
                    i4 = ixf[:].rearrange("p (h two) k -> p h two k", two=2)
                    c4 = cand[:].rearrange("p h (a b) -> p h a b", a=16)
                    ci4 = candi[:].rearrange("p h (a b) -> p h a b", a=16)
                    P.op('dve', lambda e: e.tensor_tensor(out=c4, in0=m4[:, :, 0, :].unsqueeze(3).to_broadcast([128, 8, 16, 16]),
                                                          in1=m4[:, :, 1, :].unsqueeze(2).to_broadcast([128, 8, 16, 16]), op=ALU.add),
                         reads=mres, writes=['cand'])
                    P.op('dve', lambda e: e.tensor_scalar(out=i4[:, :, 0, :], in0=i4[:, :, 0, :], scalar1=128.0, scalar2=None, op0=ALU.mult),
                         reads=['ixf'], writes=['ixf'])
                    P.op('dve', lambda e: e.tensor_tensor(out=ci4, in0=i4[:, :, 0, :].unsqueeze(3).to_broadcast([128, 8, 16, 16]),
                                                          in1=i4[:, :, 1, :].unsqueeze(2).to_broadcast([128, 8, 16, 16]), op=ALU.add),
                         reads=['ixf'], writes=['candi'])
                    for h in range(8):
                        P.op('dve', lambda e, h=h: e.max(out=ts[:, h, 0:8], in_=cand[:, h, :]), reads=['cand'], writes=[('ts0', h)])
                    P.fence('dve')
                    for h in range(8):
                        P.op('dve', lambda e, h=h: e.match_replace(out=wk2[:, h, :], in_to_replace=ts[:, h, 0:8], in_values=cand[:, h, :], imm_value=-1e30),
                             reads=['cand', ('ts0', h)], writes=[('wk2', h)])
                    P.fence('dve')
                    for h in range(8):
                        P.op('dve', lambda e, h=h: e.max(out=ts[:, h, 8:16], in_=wk2[:, h, :]), reads=[('wk2', h)], writes=[('ts1', h)])
                    P.fence('dve')
                    tsres = [('ts0', h) for h in range(8)] + [('ts1', h) for h in range(8)]
                    for h in range(8):
                        for k in range(16):
                            P.op('dve', lambda e, h=h, k=k: e.scalar_tensor_tensor(out=junk[(h * 16 + k) % 2][:], in0=cand[:, h, :], scalar=ts[:, h, k:k + 1], in1=candi[:, h, :],
                                                                                   op0=ALU.is_equal, op1=ALU.mult, accum_out=ef[:, h * 16 + k:h * 16 + k + 1]),
                                 reads=['cand', 'candi', ('ts0', h), ('ts1', h)], writes=[('ef', h * 16 + k)])
                    P.fence('dve')
                    eib = ei[tix % 2]
                    rei = f'ei{tix % 2}'
                    gtb = gt[tix % 2]
                    rgt = f'gt{tix % 2}'
                    P.op('dve', lambda e: e.tensor_scalar(out=ef[:], in0=ef[:], scalar1=float(NEXP - 1), scalar2=float(L * NEXP), op0=ALU.min, op1=ALU.add),
                         reads=[('ef', q) for q in range(128)], writes=['ef'])
                    P.op('dve', lambda e, eib=eib: e.tensor_copy(out=eib[:], in_=ef[:]), reads=['ef'], writes=[rei])
                    P.op('dve', lambda e, gtb=gtb: e.tensor_tensor(out=gtb[:], in0=ts[:], in1=ts[:, :, 0:1].to_broadcast([128, 8, 16]), op=ALU.subtract),
                         reads=tsres, writes=[rgt])
                    P.op('act', lambda e, gtb=gtb: e.activation(out=gtb[:], in_=gtb[:], func=AF.Exp), reads=[rgt], writes=[rgt])
                    P.op('dve', lambda e, gtb=gtb: e.tensor_reduce(out=gsum[:], in_=gtb[:], axis=AX.X, op=ALU.add), reads=[rgt], writes=['gsum'])
                    P.op('dve', lambda e: e.reciprocal(out=gsum[:], in_=gsum[:]), reads=['gsum'], writes=['gsum'])
                    P.op('dve', lambda e, gtb=gtb: e.tensor_tensor(out=gtb[:], in0=gtb[:], in1=gsum[:].unsqueeze(2).to_broadcast([128, 8, 16]), op=ALU.mult),
                         reads=[rgt, 'gsum'], writes=[rgt])
                    P.op('sp', lambda e, eib=eib, tix=tix: e.dma_start(out=A['IDX'][tix * 128:(tix + 1) * 128, :], in_=eib[:]),
                         reads=[rei], writes=[('IDX', tix)], dma=True)
                    P.op('sp', lambda e, gtb=gtb, tix=tix: e.dma_start(out=A['GATE'][tix * 128:(tix + 1) * 128, :], in_=gtb[:].rearrange("p h k -> p (h k)")),
                         reads=[rgt], writes=[('GATE', tix)], dma=True)

    def emit_peer2(self, xin, dst, L):
        P, nc, A, ps = self.P, self.nc, self.A, self.ps
        NB = self.cfg.get('nb', 14)
        U = A['peer_u'].rearrange("l e d -> (l e) d")
        V = A['peer_v'].rearrange("l e d -> (l e) d")
        with Stage(self, f'p2{L}') as st:
            xts = [st.T(f'xt{i}', [128, D]) for i in range(2)]
            hts = [st.T(f'ht{i}', [128, D]) for i in range(2)]
            eis = [st.T(f'ei{i}', [128, 128], I32) for i in range(2)]
            gts = [st.T(f'gt{i}', [128, 128]) for i in range(2)]
            ub = [st.T(f'ub{i}', [128, D]) for i in range(NB)]
            vb = [st.T(f'vb{i}', [128, D]) for i in range(NB)]
            junk = st.T('junk', [128, D])
            apre = st.T('apre', [128, 128])
            coef = st.T('coef', [128, 128])
            accs = [st.T(f'acc{i}', [128, D]) for i in range(2)]
            nu = nv = 0
            for ti in range(NT):
                b = ti % 2
                xt, ht, eib, gtb, acc = xts[b], hts[b], eis[b], gts[b], accs[b]
                rx, rh, rei, rgt, racc = f'xt{b}', f'ht{b}', f'ei{b}', f'gt{b}', f'acc{b}'
                P.op('sp', lambda e, xt=xt, ti=ti: e.dma_start(out=xt[:], in_=xin[ti * 128:(ti + 1) * 128, :]), writes=[rx], dma=True)
                P.op('sp', lambda e, eib=eib, ti=ti: e.dma_start(out=eib[:], in_=A['IDX'][ti * 128:(ti + 1) * 128, :]),
                     reads=[('IDX', ti)], writes=[rei], dma=True)
                P.op('sp', lambda e, gtb=gtb, ti=ti: e.dma_start(out=gtb[:], in_=A['GATE'][ti * 128:(ti + 1) * 128, :]),
                     reads=[('GATE', ti)], writes=[rgt], dma=True)
                self.modulate(xt[:], ht[:], rx, rh)
                for k in range(128):
                    s = nu % NB
                    nu += 1
                    P.op('pool', lambda e, s=s, k=k, eib=eib: e.indirect_dma_start(
                        out=ub[s][:], out_offset=None, in_=U,
                        in_offset=bass.IndirectOffsetOnAxis(ap=eib[:, k:k + 1], axis=0)),
                        reads=[rei], writes=[('ub', s)], dma=True)
                    P.op('dve', lambda e, s=s, k=k, ht=ht: e.scalar_tensor_tensor(out=junk[:], in0=ub[s][:], scalar=1.0, in1=ht[:], op0=ALU.mult, op1=ALU.mult,
                                                                               accum_out=apre[:, k:k + 1]),
                         reads=[('ub', s), rh], writes=['junk', 'apre'])
                P.op('act', lambda e: e.activation(out=coef[:], in_=apre[:], func=AF.Gelu), reads=['apre'], writes=['coef'])
                P.op('dve', lambda e, gtb=gtb: e.tensor_tensor(out=coef[:], in0=coef[:], in1=gtb[:], op=ALU.mult), reads=['coef', rgt], writes=['coef'])
                for k in range(128):
                    s = nv % NB
                    nv += 1
                    P.op('pool', lambda e, s=s, k=k, eib=eib: e.indirect_dma_start(
                        out=vb[s][:], out_offset=None, in_=V,
                        in_offset=bass.IndirectOffsetOnAxis(ap=eib[:, k:k + 1], axis=0)),
                        reads=[rei], writes=[('vb', s)], dma=True)
                    if k == 0:
                        P.op('dve', lambda e, s=s, acc=acc: e.tensor_scalar(out=acc[:], in0=vb[s][:], scalar1=coef[:, 0:1], scalar2=None, op0=ALU.mult),
                             reads=[('vb', s), 'coef'], writes=[racc])
                    else:
                        P.op('dve', lambda e, s=s, k=k, acc=acc: e.scalar_tensor_tensor(out=acc[:], in0=vb[s][:], scalar=coef[:, k:k + 1], in1=acc[:],
                                                                                      op0=ALU.mult, op1=ALU.add),
                             reads=[('vb', s), 'coef', racc], writes=[racc])
                P.op('pool', lambda e, acc=acc: e.tensor_tensor(out=acc[:], in0=acc[:], in1=self.gate_bc[:], op=ALU.mult), reads=[racc, 'gate_bc'], writes=[racc])
                P.op('dve', lambda e, acc=acc, xt=xt: e.scalar_tensor_tensor(out=acc[:], in0=xt[:], scalar=ALPHA, in1=acc[:], op0=ALU.mult, op1=ALU.add),
                     reads=[rx, racc], writes=[racc])
                self.layernorm_inplace(acc[:], racc)
                P.op('sp', lambda e, acc=acc, ti=ti: e.dma_start(out=dst[ti * 128:(ti + 1) * 128, :], in_=acc[:]), reads=[racc], dma=True)

    def emit_peer1a(self, xin, L, cast=False):
        P, nc, A, ps = self.P, self.nc, self.A, self.ps
        with Stage(self, f'pa{L}') as st:
            cg = self.cast_gen(st, L) if cast else None
            wq = st.T('wq', [128, 8, 2048])
            skT = st.T('skT', [128, 2, 128])
            xts = [st.T(f'xt{i}', [128, D]) for i in range(2)]
            hts = [st.T(f'ht{i}', [128, D]) for i in range(2)]
            hT = st.T('hT', [128, 8, 256])
            qT = st.T('qT', [128, 16, 256])
            sco = [st.T(f'sco{i}', [128, 2048]) for i in range(2)]
            for q in range(4):
                P.op('sp', lambda e, q=q: e.dma_start(out=wq[:, :, q * 512:(q + 1) * 512],
                                                      in_=A['peer_query_w'][L][:, q * 512:(q + 1) * 512].rearrange("(k p) n -> p k n", p=128)),
                     writes=[('wq', q)], dma=True)
            P.op('sp', lambda e: e.dma_start(out=skT[:], in_=A['peer_skT'][L].rearrange("h d k -> d h k")), writes=['skT'], dma=True)
            ti = 0
            for jb in range(S // 256):
                for tl in range(2):
                    xt, ht = xts[ti % 2], hts[ti % 2]
                    rx, rh = f'xt{ti % 2}', f'ht{ti % 2}'
                    P.op('sp', lambda e, xt=xt, ti=ti: e.dma_start(out=xt[:], in_=xin[ti * 128:(ti + 1) * 128, :]), writes=[rx], dma=True)
                    self.modulate(xt[:], ht[:], rx, rh)
                    P.op('sp', lambda e, ht=ht, ti=ti: e.dma_start(out=A['H'][ti * 128:(ti + 1) * 128, :], in_=ht[:]), reads=[rh], writes=[('H', ti)], dma=True)
                    self.transpose8(ht, rh, hT, 'hT', tl * 128, 0)
                    ti += 1
                for c in range(16):
                    bank = ps[2 + c % 2]
                    rb = f'ps{2 + c % 2}'
                    for k in range(8):
                        P.op('pe', lambda e, k=k, c=c, bank=bank: e.matmul(out=bank[:, 0:256], lhsT=wq[:, k, c * 128:(c + 1) * 128], rhs=hT[:, k, :],
                                                                          start=(k == 0), stop=(k == 7)),
                             reads=[('wq', c // 4), 'hT'], writes=[rb])
                    if c % 2 == 0:
                        P.op('act', lambda e, c=c, bank=bank: e.copy(out=qT[:, c, :], in_=bank[:, 0:256]), reads=[rb], writes=[('qT', c)])
                    else:
                        P.op('dve', lambda e, c=c, bank=bank: e.tensor_copy(out=qT[:, c, :], in_=bank[:, 0:256]), reads=[rb], writes=[('qT', c)])
                for tl in range(2):
                    tix = jb * 2 + tl
                    so = sco[tix % 2]
                    rso = f'sco{tix % 2}'
                    for c in range(16):
                        bank = ps[4 + c // 4]
                        rb = f'ps{4 + c // 4}'
                        P.op('pe', lambda e, c=c, bank=bank, tl=tl: e.matmul(out=bank[:, (c % 4) * 128:(c % 4 + 1) * 128],
                                                                            lhsT=qT[:, c, tl * 128:(tl + 1) * 128], rhs=skT[:, c % 2, :],
                                                                            start=True, stop=True),
                             reads=[('qT', c), 'skT'], writes=[rb])
                    for g4 in range(4):
                        eng = ('act', 'dve')[g4 % 2]
                        if eng == 'act':
                            P.op('act', lambda e, g4=g4, so=so: e.copy(out=so[:, g4 * 512:(g4 + 1) * 512], in_=ps[4 + g4][:]), reads=[f'ps{4 + g4}'], writes=[rso])
                        else:
                            P.op('dve', lambda e, g4=g4, so=so: e.tensor_copy(out=so[:, g4 * 512:(g4 + 1) * 512], in_=ps[4 + g4][:]), reads=[f'ps{4 + g4}'], writes=[rso])
                    P.op('sp', lambda e, so=so, tix=tix: e.dma_start(out=A['SCR'][tix * 128:(tix + 1) * 128, :], in_=so[:]), reads=[rso], writes=[('SCR', tix)], dma=True)
                    if cg is not None:
                        for _ in range(2):
                            try:
                                next(cg)
                            except StopIteration:
                                cg = None
                                break
            while cg is not None:
                try:
                    next(cg)
                except StopIteration:
                    cg = None

    def emit_peer2f(self, xin, dst, L):
        P, nc, A, ps = self.P, self.nc, self.A, self.ps
        NB = self.cfg.get('nb', 11)
        U = A['peer_u'].rearrange("l e d -> (l e) d")
        V = A['peer_v'].rearrange("l e d -> (l e) d")
        with Stage(self, f'pf{L}') as st:
            scs = [st.T(f'sc{i}', [128, 16, 128]) for i in range(2)]
            m = st.T('m', [128, 16, 16])
            ix = st.T('ix', [128, 16, 16], U32)
            ixf = st.T('ixf', [128, 16, 16])
            wk = st.T('wk', [128, 16, 128])
            cand = st.T('cand', [128, 8, 256])
            candi = st.T('candi', [128, 8, 256])
            wk2 = st.T('wk2', [128, 8, 256])
            junk2 = [st.T(f'junk2{i}', [128, 256]) for i in range(2)]
            ts = st.T('ts', [128, 8, 16])
            ef = st.T('ef', [128, 128])
            eis = [st.T(f'ei{i}', [128, 128], I32) for i in range(2)]
            gts = [st.T(f'gt{i}', [128, 8, 16]) for i in range(2)]
            gsum = st.T('gsum', [128, 8])
            xts = [st.T(f'xt{i}', [128, D]) for i in range(2)]
            hts = [st.T(f'ht{i}', [128, D]) for i in range(2)]
            ub = [st.T(f'ub{i}', [128, D]) for i in range(NB)]
            vb = [st.T(f'vb{i}', [128, D]) for i in range(NB)]
            junk = st.T('junk', [128, D])
            apre = st.T('apre', [128, 128])
            coef = st.T('coef', [128, 128])
            accs = [st.T(f'acc{i}', [128, D]) for i in range(2)]

            def load_sc(t):
                P.op('sp', lambda e, t=t: e.dma_start(out=scs[t % 2][:].rearrange("p a k -> p (a k)"), in_=A['SCR'][t * 128:(t + 1) * 128, :]),
                     writes=[f'sc{t % 2}'], dma=True)

            def load_xh(t):
                P.op('sp', lambda e, t=t: e.dma_start(out=xts[t % 2][:], in_=xin[t * 128:(t + 1) * 128, :]), writes=[f'xt{t % 2}'], dma=True)
                P.op('sp', lambda e, t=t: e.dma_start(out=hts[t % 2][:], in_=A['H'][t * 128:(t + 1) * 128, :]), writes=[f'ht{t % 2}'], dma=True)

            def topk_gen(t):
                sc = scs[t % 2]
                rsc = f'sc{t % 2}'
                eib, gtb = eis[t % 2], gts[t % 2]
                rei, rgt = f'ei{t % 2}', f'gt{t % 2}'
                for c in range(16):
                    P.op('dve', lambda e, c=c: e.max(out=m[:, c, 0:8], in_=sc[:, c, :]), reads=[rsc], writes=[('m0', c)])
                    yield
                P.fence('dve')
                for c in range(16):
                    P.op('dve', lambda e, c=c: e.max_index(out=ix[:, c, 0:8], in_max=m[:, c, 0:8], in_values=sc[:, c, :]),
                         reads=[rsc, ('m0', c)], writes=[('ix0', c)])
                    yield
                    P.op('dve', lambda e, c=c: e.match_replace(out=wk[:, c, :], in_to_replace=m[:, c, 0:8], in_values=sc[:, c, :], imm_value=-1e30),
                         reads=[rsc, ('m0', c)], writes=[('wk', c)])
                    yield
                P.fence('dve')
                for c in range(16):
                    P.op('dve', lambda e, c=c: e.max(out=m[:, c, 8:16], in_=wk[:, c, :]), reads=[('wk', c)], writes=[('m1', c)])
                    yield
                P.fence('dve')
                for c in range(16):
                    P.op('dve', lambda e, c=c: e.max_index(out=ix[:, c, 8:16], in_max=m[:, c, 8:16], in_values=wk[:, c, :]),
                         reads=[('wk', c), ('m1', c)], writes=[('ix1', c)])
                    yield
                P.fence('dve')
                mres = [('m0', c) for c in range(16)] + [('m1', c) for c in range(16)]
                ixres = [('ix0', c) for c in range(16)] + [('ix1', c) for c in range(16)]
                P.op('dve', lambda e: e.tensor_copy(out=ixf[:], in_=ix[:]), reads=ixres, writes=['ixf'])
                yield
                m4 = m[:].rearrange("p (h two) k -> p h two k", two=2)
                i4 = ixf[:].rearrange("p (h two) k -> p h two k", two=2)
                c4 = cand[:].rearrange("p h (a b) -> p h a b", a=16)
                ci4 = candi[:].rearrange("p h (a b) -> p h a b", a=16)
                P.op('dve', lambda e: e.tensor_tensor(out=c4, in0=m4[:, :, 0, :].unsqueeze(3).to_broadcast([128, 8, 16, 16]),
                                                      in1=m4[:, :, 1, :].unsqueeze(2).to_broadcast([128, 8, 16, 16]), op=ALU.add),
                     reads=mres, writes=['cand'])
                yield
                P.op('dve', lambda e: e.tensor_scalar(out=i4[:, :, 0, :], in0=i4[:, :, 0, :], scalar1=128.0, scalar2=None, op0=ALU.mult),
                     reads=['ixf'], writes=['ixf'])
                yield
                P.op('dve', lambda e: e.tensor_tensor(out=ci4, in0=i4[:, :, 0, :].unsqueeze(3).to_broadcast([128, 8, 16, 16]),
                                                      in1=i4[:, :, 1, :].unsqueeze(2).to_broadcast([128, 8, 16, 16]), op=ALU.add),
                     reads=['ixf'], writes=['candi'])
                yield
                for h in range(8):
                    P.op('dve', lambda e, h=h: e.max(out=ts[:, h, 0:8], in_=cand[:, h, :]), reads=['cand'], writes=[('ts0', h)])
                    yield
                P.fence('dve')
                for h in range(8):
                    P.op('dve', lambda e, h=h: e.match_replace(out=wk2[:, h, :], in_to_replace=ts[:, h, 0:8], in_values=cand[:, h, :], imm_value=-1e30),
                         reads=['cand', ('ts0', h)], writes=[('wk2', h)])
                    yield
                P.fence('dve')
                for h in range(8):
                    P.op('dve', lambda e, h=h: e.max(out=ts[:, h, 8:16], in_=wk2[:, h, :]), reads=[('wk2', h)], writes=[('ts1', h)])
                    yield
                P.fence('dve')
                tsres = [('ts0', h) for h in range(8)] + [('ts1', h) for h in range(8)]
                for h in range(8):
                    for k in range(16):
                        P.op('dve', lambda e, h=h, k=k: e.scalar_tensor_tensor(out=junk2[(h * 16 + k) % 2][:], in0=cand[:, h, :], scalar=ts[:, h, k:k + 1], in1=candi[:, h, :],
                                                                               op0=ALU.is_equal, op1=ALU.mult, accum_out=ef[:, h * 16 + k:h * 16 + k + 1]),
                             reads=['cand', 'candi', ('ts0', h), ('ts1', h)], writes=[('ef', h * 16 + k)])
                        yield
                P.fence('dve')
                P.op('dve', lambda e: e.tensor_scalar(out=ef[:], in0=ef[:], scalar1=float(NEXP - 1), scalar2=float(L * NEXP), op0=ALU.min, op1=ALU.add),
                     reads=[('ef', q) for q in range(128)], writes=['ef'])
                yield
                P.op('dve', lambda e: e.tensor_copy(out=eib[:], in_=ef[:]), reads=['ef'], writes=[rei])
                yield
                P.op('dve', lambda e: e.tensor_tensor(out=gtb[:], in0=ts[:], in1=ts[:, :, 0:1].to_broadcast([128, 8, 16]), op=ALU.subtract),
                     reads=tsres, writes=[rgt])
                yield
                P.op('act', lambda e: e.activation(out=gtb[:], in_=gtb[:], func=AF.Exp), reads=[rgt], writes=[rgt])
                P.op('dve', lambda e: e.tensor_reduce(out=gsum[:], in_=gtb[:], axis=AX.X, op=ALU.add), reads=[rgt], writes=['gsum'])
                yield
                P.op('dve', lambda e: e.reciprocal(out=gsum[:], in_=gsum[:]), reads=['gsum'], writes=['gsum'])
                yield
                P.op('dve', lambda e: e.tensor_tensor(out=gtb[:], in0=gtb[:], in1=gsum[:].unsqueeze(2).to_broadcast([128, 8, 16]), op=ALU.mult),
                     reads=[rgt, 'gsum'], writes=[rgt])
                yield

            def step(gen, n=1):
                if gen is None:
                    return None
                try:
                    for _ in range(n):
                        next(gen)
                except StopIteration:
                    return None
                return gen

            load_sc(0)
            load_sc(1)
            load_xh(0)
            g0 = topk_gen(0)
            while g0 is not None:
                g0 = step(g0, 64)
            nu = nv = 0
            for ti in range(NT):
                b = ti % 2
                xt, ht, eib, gtb, acc = xts[b], hts[b], eis[b], gts[b], accs[b]
                rx, rh, rei, rgt, racc = f'xt{b}', f'ht{b}', f'ei{b}', f'gt{b}', f'acc{b}'
                gt2 = gtb[:].rearrange("p h k -> p (h k)")
                if ti + 1 < NT:
                    load_xh(ti + 1)
                gen = topk_gen(ti + 1) if ti + 1 < NT else None
                for k in range(128):
                    s_ = nu % NB
                    nu += 1
                    P.op('pool', lambda e, s_=s_, k=k, eib=eib: e.indirect_dma_start(
                        out=ub[s_][:], out_offset=None, in_=U,
                        in_offset=bass.IndirectOffsetOnAxis(ap=eib[:, k:k + 1], axis=0)),
                        reads=[rei], writes=[('ub', s_)], dma=True)
                    P.op('dve', lambda e, s_=s_, k=k, ht=ht: e.scalar_tensor_tensor(out=junk[:], in0=ub[s_][:], scalar=1.0, in1=ht[:], op0=ALU.mult, op1=ALU.mult,
                                                                                 accum_out=apre[:, k:k + 1]),
                         reads=[('ub', s_), rh], writes=[('apre', k)])
                    gen = step(gen)
                P.op('act', lambda e: e.activation(out=coef[:], in_=apre[:], func=AF.Gelu), reads=[('apre', k) for k in range(128)], writes=['coef'])
                P.op('dve', lambda e, gt2=gt2: e.tensor_tensor(out=coef[:], in0=coef[:], in1=gt2, op=ALU.mult), reads=['coef', rgt], writes=['coef'])
                for k in range(128):
                    s_ = nv % NB
                    nv += 1
                    P.op('pool', lambda e, s_=s_, k=k, eib=eib: e.indirect_dma_start(
                        out=vb[s_][:], out_offset=None, in_=V,
                        in_offset=bass.IndirectOffsetOnAxis(ap=eib[:, k:k + 1], axis=0)),
                        reads=[rei], writes=[('vb', s_)], dma=True)
                    if k == 0:
                        P.op('dve', lambda e, s_=s_, acc=acc: e.tensor_scalar(out=acc[:], in0=vb[s_][:], scalar1=coef[:, 0:1], scalar2=None, op0=ALU.mult),
                             reads=[('vb', s_), 'coef'], writes=[racc])
                    else:
                        P.op('dve', lambda e, s_=s_, k=k, acc=acc: e.scalar_tensor_tensor(out=acc[:], in0=vb[s_][:], scalar=coef[:, k:k + 1], in1=acc[:],
                                                                                       op0=ALU.mult, op1=ALU.add),
                             reads=[('vb', s_), 'coef', racc], writes=[racc])
                    gen = step(gen)
                while gen is not None:
                    gen = step(gen, 64)
                if ti + 2 < NT:
                    load_sc(ti + 2)
                P.op('dve', lambda e, acc=acc: e.tensor_tensor(out=acc[:], in0=acc[:], in1=self.gate_bc[:], op=ALU.mult), reads=[racc, 'gate_bc'], writes=[racc])
                P.op('dve', lambda e, acc=acc, xt=xt: e.scalar_tensor_tensor(out=acc[:], in0=xt[:], scalar=ALPHA, in1=acc[:], op0=ALU.mult, op1=ALU.add),
                     reads=[rx, racc], writes=[racc])
                self.layernorm_inplace(acc[:], racc, gb='dve')
                P.op('sp', lambda e, acc=acc, ti=ti: e.dma_start(out=dst[ti * 128:(ti + 1) * 128, :], in_=acc[:]), reads=[racc], dma=True)

    def cast_gen(self, st, L, CN=2):
        P, nc, A = self.P, self.nc, self.A
        Uv = A['peer_u'][L].rearrange("(n p) d -> p n d", p=128)
        Vv = A['peer_v'][L].rearrange("(n p) d -> p n d", p=128)
        Ov = A['UVB'][L * NEXP:(L + 1) * NEXP, :].rearrange("(n p) d -> p n d", p=128)
        iu = [st.T(f'ciu{i}', [128, CN, D]) for i in range(2)]
        iv = [st.T(f'civ{i}', [128, CN, D]) for i in range(2)]
        ob = [st.T(f'cob{i}', [128, CN, 2 * D], BF16) for i in range(2)]
        nchunk = (NEXP // 128) // CN

        def ld(ci):
            b = ci % 2
            ns = slice(ci * CN, (ci + 1) * CN)
            P.op(self.cfg.get('cast_q', 'pool'), lambda e: e.dma_start(out=iu[b][:], in_=Uv[:, ns, :]), writes=[f'ciu{b}'], dma=True)
            P.op(self.cfg.get('cast_q', 'pool'), lambda e: e.dma_start(out=iv[b][:], in_=Vv[:, ns, :]), writes=[f'civ{b}'], dma=True)

        ld(0)
        for ci in range(nchunk):
            b = ci % 2
            ns = slice(ci * CN, (ci + 1) * CN)
            if ci + 1 < nchunk:
                ld(ci + 1)
            P.op('pool', lambda e: e.tensor_copy(out=ob[b][:, :, 0:D], in_=iu[b][:]), reads=[f'ciu{b}'], writes=[(f'cob{b}', 0)])
            P.op('pool', lambda e: e.tensor_copy(out=ob[b][:, :, D:2 * D], in_=iv[b][:]), reads=[f'civ{b}'], writes=[(f'cob{b}', 1)])
            P.op(self.cfg.get('cast_q', 'pool'), lambda e: e.dma_start(out=Ov[:, ns, :], in_=ob[b][:]), reads=[(f'cob{b}', 0), (f'cob{b}', 1)],
                 writes=[('UVB', L, ci)], dma=True)
            yield

    def emit_cast_tables(self):
        P, nc, A = self.P, self.nc, self.A
        CN = 4
        Uv = A['peer_u'].rearrange("l (n p) d -> p (l n) d", p=128)
        Vv = A['peer_v'].rearrange("l (n p) d -> p (l n) d", p=128)
        Ov = A['UVB'].rearrange("(n p) d -> p n d", p=128)
        with Stage(self, 'cast') as st:
            iu = [st.T(f'iu{i}', [128, CN, D]) for i in range(2)]
            iv = [st.T(f'iv{i}', [128, CN, D]) for i in range(2)]
            ob = [st.T(f'ob{i}', [128, CN, 2 * D], BF16) for i in range(2)]
            nchunk = (2 * NEXP // 128) // CN

            def ld(ci):
                b = ci % 2
                ns = slice(ci * CN, (ci + 1) * CN)
                P.op('sp', lambda e, b=b, ns=ns: e.dma_start(out=iu[b][:], in_=Uv[:, ns, :]), writes=[f'iu{b}'], dma=True)
                P.op('sp', lambda e, b=b, ns=ns: e.dma_start(out=iv[b][:], in_=Vv[:, ns, :]), writes=[f'iv{b}'], dma=True)

            ld(0)
            for ci in range(nchunk):
                b = ci % 2
                ns = slice(ci * CN, (ci + 1) * CN)
                if ci + 1 < nchunk:
                    ld(ci + 1)
                P.op('dve', lambda e, b=b: e.tensor_copy(out=ob[b][:, :, 0:D], in_=iu[b][:]), reads=[f'iu{b}'], writes=[(f'ob{b}', 0)])
                P.op('act', lambda e, b=b: e.copy(out=ob[b][:, 0:2, D:2 * D], in_=iv[b][:, 0:2, :]), reads=[f'iv{b}'], writes=[(f'ob{b}', 1)])
                P.op('pool', lambda e, b=b: e.tensor_copy(out=ob[b][:, 2:4, D:2 * D], in_=iv[b][:, 2:4, :]), reads=[f'iv{b}'], writes=[(f'ob{b}', 2)])
                P.op('act', lambda e, b=b, ns=ns: e.dma_start(out=Ov[:, ns, :], in_=ob[b][:]), reads=[(f'ob{b}', 0), (f'ob{b}', 1), (f'ob{b}', 2)],
                     writes=[('UVB', ci)], dma=True)

    def emit_peer2g(self, xin, dst, L):
        P, nc, A, ps = self.P, self.nc, self.A, self.ps
        NB = self.cfg.get('nb', 24)
        GS = self.cfg.get('gs', 8)
        UVB = A['UVB']
        with Stage(self, f'pg{L}') as st:
            scs = [st.T('sc0', [128, 16, 128])] * 2
            m = st.T('m', [128, 16, 16])
            ix = st.T('ix', [128, 16, 16], U32)
            ixf = st.T('ixf', [128, 16, 16])
            wk = st.T('wk', [128, 16, 128])
            cand = st.T('cand', [128, 8, 256])
            candi = st.T('candi', [128, 8, 256])
            wk2 = wk[:].rearrange("p a k -> p (a k)").rearrange("p (h c) -> p h c", h=8)
            junk2 = [st.T(f'junk2{i}', [128, 256]) for i in range(2)]
            ts = st.T('ts', [128, 8, 16])
            ef = st.T('ef', [128, 128])
            eis = [st.T(f'ei{i}', [128, 128], I32) for i in range(2)]
            gts = [st.T(f'gt{i}', [128, 8, 16]) for i in range(2)]
            gsum = st.T('gsum', [128, 8])
            xts = [st.T(f'xt{i}', [128, D]) for i in range(2)]
            hts = [st.T(f'ht{i}', [128, D]) for i in range(2)]
            uvb = [st.T(f'uv{i}', [128, 2 * D], BF16) for i in range(NB)]
            junk = st.T('junk', [128, D], BF16)
            junka = st.T('junka', [128, D], BF16)
            prods = [st.T(f'prod{i}', [128, D], BF16) for i in range(3)]
            hbs = [st.T(f'hb{i}', [128, D], BF16) for i in range(2)]
            apre = st.T('apre', [128, 128])
            ge = st.T('ge', [128, 128])
            cf = st.T('cf', [128, 128])
            dgs = [st.T(f'dg{i}', [128, 128], BF16) for i in range(4)]
            accs = [st.T(f'acc{i}', [128, D]) for i in range(2)]

            def load_sc(t):
                P.op('sp', lambda e, t=t: e.dma_start(out=scs[t % 2][:].rearrange("p a k -> p (a k)"), in_=A['SCR'][t * 128:(t + 1) * 128, :]),
                     writes=['sc0'], dma=True)

            def load_xh(t):
                P.op('sp', lambda e, t=t: e.dma_start(out=xts[t % 2][:], in_=xin[t * 128:(t + 1) * 128, :]), writes=[f'xt{t % 2}'], dma=True)
                P.op('sp', lambda e, t=t: e.dma_start(out=hts[t % 2][:], in_=A['H'][t * 128:(t + 1) * 128, :]), writes=[f'ht{t % 2}'], dma=True)

            def topk_gen(t):
                sc = scs[t % 2]
                rsc = 'sc0'
                eib, gtb = eis[t % 2], gts[t % 2]
                rei, rgt = f'ei{t % 2}', f'gt{t % 2}'
                for c in range(16):
                    P.op('dve', lambda e, c=c: e.max(out=m[:, c, 0:8], in_=sc[:, c, :]), reads=[rsc], writes=[('m0', c)])
                    yield
                P.fence('dve')
                for c in range(16):
                    P.op('dve', lambda e, c=c: e.max_index(out=ix[:, c, 0:8], in_max=m[:, c, 0:8], in_values=sc[:, c, :]),
                         reads=[rsc, ('m0', c)], writes=[('ix0', c)])
                    yield
                    P.op('dve', lambda e, c=c: e.match_replace(out=wk[:, c, :], in_to_replace=m[:, c, 0:8], in_values=sc[:, c, :], imm_value=-1e30),
                         reads=[rsc, ('m0', c)], writes=[('wk', c)])
                    yield
                P.fence('dve')
                for c in range(16):
                    P.op('dve', lambda e, c=c: e.max(out=m[:, c, 8:16], in_=wk[:, c, :]), reads=[('wk', c)], writes=[('m1', c)])
                    yield
                P.fence('dve')
                for c in range(16):
                    P.op('dve', lambda e, c=c: e.max_index(out=ix[:, c, 8:16], in_max=m[:, c, 8:16], in_values=wk[:, c, :]),
                         reads=[('wk', c), ('m1', c)], writes=[('ix1', c)])
                    yield
                P.fence('dve')
                mres = [('m0', c) for c in range(16)] + [('m1', c) for c in range(16)]
                ixres = [('ix0', c) for c in range(16)] + [('ix1', c) for c in range(16)]
                P.op('dve', lambda e: e.tensor_copy(out=ixf[:], in_=ix[:]), reads=ixres, writes=['ixf'])
                yield
                m4 = m[:].rearrange("p (h two) k -> p h two k", two=2)
                i4 = ixf[:].rearrange("p (h two) k -> p h two k", two=2)
                c4 = cand[:].rearrange("p h (a b) -> p h a b", a=16)
                ci4 = candi[:].rearrange("p h (a b) -> p h a b", a=16)
                P.op('dve', lambda e: e.tensor_tensor(out=c4, in0=m4[:, :, 0, :].unsqueeze(3).to_broadcast([128, 8, 16, 16]),
                                                      in1=m4[:, :, 1, :].unsqueeze(2).to_broadcast([128, 8, 16, 16]), op=ALU.add),
                     reads=mres, writes=['cand'])
                yield
                P.op('dve', lambda e: e.tensor_scalar(out=i4[:, :, 0, :], in0=i4[:, :, 0, :], scalar1=128.0, scalar2=None, op0=ALU.mult),
                     reads=['ixf'], writes=['ixf'])
                yield
                P.op('dve', lambda e: e.tensor_tensor(out=ci4, in0=i4[:, :, 0, :].unsqueeze(3).to_broadcast([128, 8, 16, 16]),
                                                      in1=i4[:, :, 1, :].unsqueeze(2).to_broadcast([128, 8, 16, 16]), op=ALU.add),
                     reads=['ixf'], writes=['candi'])
                yield
                for h in range(8):
                    P.op('dve', lambda e, h=h: e.max(out=ts[:, h, 0:8], in_=cand[:, h, :]), reads=['cand'], writes=[('ts0', h)])
                    yield
                P.fence('dve')
                for h in range(8):
                    P.op('dve', lambda e, h=h: e.match_replace(out=wk2[:, h, :], in_to_replace=ts[:, h, 0:8], in_values=cand[:, h, :], imm_value=-1e30),
                         reads=['cand', ('ts0', h)], writes=[('wk2', h)])
                    yield
                P.fence('dve')
                for h in range(8):
                    P.op('dve', lambda e, h=h: e.max(out=ts[:, h, 8:16], in_=wk2[:, h, :]), reads=[('wk2', h)], writes=[('ts1', h)])
                    yield
                P.fence('dve')
                tsres = [('ts0', h) for h in range(8)] + [('ts1', h) for h in range(8)]
                for h in range(8):
                    for k in range(16):
                        P.op('dve', lambda e, h=h, k=k: e.scalar_tensor_tensor(out=junk2[(h * 16 + k) % 2][:], in0=cand[:, h, :], scalar=ts[:, h, k:k + 1], in1=candi[:, h, :],
                                                                               op0=ALU.is_equal, op1=ALU.mult, accum_out=ef[:, h * 16 + k:h * 16 + k + 1]),
                             reads=['cand', 'candi', ('ts0', h), ('ts1', h)], writes=[('ef', h * 16 + k)])
                        yield
                P.fence('dve')
                P.op('dve', lambda e: e.tensor_scalar(out=ef[:], in0=ef[:], scalar1=float(NEXP - 1), scalar2=float(L * NEXP), op0=ALU.min, op1=ALU.add),
                     reads=[('ef', q) for q in range(128)], writes=['ef'])
                yield
                P.op('dve', lambda e: e.tensor_copy(out=eib[:], in_=ef[:]), reads=['ef'], writes=[rei])
                yield
                P.op('dve', lambda e: e.tensor_tensor(out=gtb[:], in0=ts[:], in1=ts[:, :, 0:1].to_broadcast([128, 8, 16]), op=ALU.subtract),
                     reads=tsres, writes=[rgt])
                yield
                P.op('act', lambda e: e.activation(out=gtb[:], in_=gtb[:], func=AF.Exp), reads=[rgt], writes=[rgt])
                P.op('dve', lambda e: e.tensor_reduce(out=gsum[:], in_=gtb[:], axis=AX.X, op=ALU.add), reads=[rgt], writes=['gsum'])
                yield
                P.op('dve', lambda e: e.reciprocal(out=gsum[:], in_=gsum[:]), reads=['gsum'], writes=['gsum'])
                yield
                P.op('dve', lambda e: e.tensor_tensor(out=gtb[:], in0=gtb[:], in1=gsum[:].unsqueeze(2).to_broadcast([128, 8, 16]), op=ALU.mult),
                     reads=[rgt, 'gsum'], writes=[rgt])
                yield

            def step(gen, n=1):
                if gen is None:
                    return None
                try:
                    for _ in range(n):
                        next(gen)
                except StopIteration:
                    return None
                return gen

            load_sc(0)
            load_xh(0)
            g0 = topk_gen(0)
            while g0 is not None:
                g0 = step(g0, 64)
            load_sc(1)
            nu = 0
            ndg = 0
            npr = 0
            for ti in range(NT):
                b = ti % 2
                xt, ht, eib, gtb, acc = xts[b], hts[b], eis[b], gts[b], accs[b]
                rx, rh, rei, rgt, racc = f'xt{b}', f'ht{b}', f'ei{b}', f'gt{b}', f'acc{b}'
                gt2 = gtb[:].rearrange("p h k -> p (h k)")
                pa = [ps[(ti % 2) * 2], ps[(ti % 2) * 2 + 1]]
                rpa = [f'ps{(ti % 2) * 2}', f'ps{(ti % 2) * 2 + 1}']
                if ti + 1 < NT:
                    load_xh(ti + 1)
                hb, rhb = hbs[b], f'hb{b}'
                P.op('dve', lambda e, hb=hb, ht=ht: e.tensor_copy(out=hb[:], in_=ht[:]), reads=[rh], writes=[rhb])
                gen = topk_gen(ti + 1) if ti + 1 < NT else None
                for g in range(128 // GS):
                    slots = []
                    for kk in range(GS):
                        k = g * GS + kk
                        s_ = nu % NB
                        nu += 1
                        slots.append(s_)
                        P.op('pool', lambda e, s_=s_, k=k, eib=eib: e.indirect_dma_start(
                            out=uvb[s_][:], out_offset=None, in_=UVB,
                            in_offset=bass.IndirectOffsetOnAxis(ap=eib[:, k:k + 1], axis=0)),
                            reads=[rei], writes=[('uv', s_)], dma=True)
                        if (k % self.cfg.get('split_den', 3)) < self.cfg.get('split_num', 1) or not self.cfg.get('dot_split', True):
                            P.op('dve', lambda e, s_=s_, k=k, ht=ht: e.scalar_tensor_tensor(out=junk[:], in0=uvb[s_][:, 0:D], scalar=1.0, in1=ht[:], op0=ALU.mult, op1=ALU.mult,
                                                                                         accum_out=apre[:, k:k + 1]),
                                 reads=[('uv', s_), rh], writes=[('apre', k)])
                        else:
                            pr = prods[npr % 3]
                            rpr = f'prod{npr % 3}'
                            npr += 1
                            P.op('dve', lambda e, s_=s_, pr=pr, hb=hb: e.tensor_tensor(out=pr[:], in0=uvb[s_][:, 0:D], in1=hb[:], op=ALU.mult),
                                 reads=[('uv', s_), rhb], writes=[rpr])
                            P.op('act', lambda e, pr=pr, k=k: e.activation(out=junka[:], in_=pr[:], func=AF.Copy, accum_out=apre[:, k:k + 1]),
                                 reads=[rpr], writes=[('apre', k)])
                        gen = step(gen, 2)
                    gsl = slice(g * GS, (g + 1) * GS)
                    P.op('act', lambda e, gsl=gsl: e.activation(out=ge[:, gsl], in_=apre[:, gsl], func=AF.Gelu),
                         reads=[('apre', k) for k in range(g * GS, (g + 1) * GS)], writes=[('ge', g)])
                    P.op('dve', lambda e, gsl=gsl, gt2=gt2: e.tensor_tensor(out=cf[:, gsl], in0=ge[:, gsl], in1=gt2[:, gsl], op=ALU.mult),
                         reads=[('ge', g), rgt], writes=[('cf', g)])
                    for kk in range(GS):
                        k = g * GS + kk
                        s_ = slots[kk]
                        dg = dgs[ndg % 4]
                        rdg = f'dg{ndg % 4}'
                        ndg += 1
                        P.op('act', lambda e, dg=dg, k=k: e.activation(out=dg[:], in_=self.ident[:], func=AF.Copy, scale=cf[:, k:k + 1]),
                             reads=['ident', ('cf', g)], writes=[rdg])
                        for half in range(2):
                            P.op('pe', lambda e, dg=dg, s_=s_, half=half, k=k, pa=pa: e.matmul(out=pa[half][:], lhsT=dg[:], rhs=uvb[s_][:, D + half * 512:D + (half + 1) * 512],
                                                                                           start=(k == 0), stop=(k == 127)),
                                 reads=[rdg, ('uv', s_)], writes=[rpa[half]])
                while gen is not None:
                    gen = step(gen, 64)
                if ti + 2 < NT:
                    load_sc(ti + 2)
                for half in range(2):
                    P.op('dve', lambda e, acc=acc, half=half, pa=pa: e.tensor_tensor(out=acc[:, half * 512:(half + 1) * 512], in0=pa[half][:],
                                                                                   in1=self.gate_bc[:, half * 512:(half + 1) * 512], op=ALU.mult),
                         reads=[rpa[half], 'gate_bc'], writes=[racc])
                P.op('dve', lambda e, acc=acc, xt=xt: e.scalar_tensor_tensor(out=acc[:], in0=xt[:], scalar=ALPHA, in1=acc[:], op0=ALU.mult, op1=ALU.add),
                     reads=[rx, racc], writes=[racc])
                self.layernorm_inplace(acc[:], racc, gb='dve')
                P.op('sp', lambda e, acc=acc, ti=ti: e.dma_start(out=dst[ti * 128:(ti + 1) * 128, :], in_=acc[:]), reads=[racc], dma=True)

    def emit_attn1(self, xin):
        P, nc, A, ps = self.P, self.nc, self.A, self.ps
        ones = self.ones
        NCOL = 3 * D + 16
        with Stage(self, 'a1') as st:
            winr = st.T('winr', [128, 8, 3 * D], F32R)
            wstg = [st.T('wstg0', [128, 8, 512])] * 2
            wf = st.T('wf', [128, 8, 16])
            qkb = st.T('qkb', [128, 16])
            vbr = st.T('vbr', [1, D])
            vb_bc = st.T('vb_bc', [128, D])
            fb = st.T('fb', [16, 1])
            xts = [st.T(f'xt{i}', [128, D]) for i in range(2)]
            ht = st.T('ht', [128, D])
            hT = st.T('hT', [128, 8, 512], F32R)
            qko = [st.T(f'qko{i}', [128, 512]) for i in range(2)]
            vo = [st.T(f'vo{i}', [128, D]) for i in range(2)]
            Fcb = [st.T(f'Fcb{i}', [16, 512]) for i in range(2)]
            Frb = st.T('Frb', [16, 512], F32R)
            Flb = st.T('Flb', [16, 512])
            nFr = st.T('nFr', [16, 512])
            nFl = st.T('nFl', [16, 512])
            spt = st.T('spt', [16, 512])
            o16 = st.T('o16', [16, 512])
            for q in range(6):
                wb = wstg[0]
                rw = 'wstg0'
                P.op('sp', lambda e, q=q, wb=wb: e.dma_start(out=wb[:], in_=A['attn_in_w'][:, q * 512:(q + 1) * 512].rearrange("(k p) n -> p k n", p=128)),
                     writes=[rw], dma=True)
                eng = ('dve', 'pool')[q % 2]
                P.op(eng, lambda e, q=q, wb=wb: e.tensor_copy(out=winr[:, :, q * 512:(q + 1) * 512], in_=wb[:]), reads=[rw], writes=[('win', q)])
            P.op('sp', lambda e: e.dma_start(out=wf[:], in_=A['attn_in_w'][:, 3 * D:NCOL].rearrange("(k p) n -> p k n", p=128)),
                 writes=['wf'], dma=True)
            P.op('sp', lambda e: e.dma_start(out=qkb[:], in_=A['attn_qkb_l']), writes=['qkb'], dma=True)
            P.op('sp', lambda e: e.dma_start(out=vbr[:], in_=A['attn_vb']), writes=['vbr'], dma=True)
            P.op('sp', lambda e: e.dma_start(out=fb[:], in_=A['attn_fb']), writes=['fb'], dma=True)
            P.op('dve', lambda e: e.tensor_scalar(out=qkb[:, 0:8], in0=qkb[:, 0:8], scalar1=0.125, scalar2=None, op0=ALU.mult), reads=['qkb'], writes=['qkb'])
            P.op('dve', lambda e: e.tensor_scalar(out=fb[:], in0=fb[:], scalar1=-1.0, scalar2=None, op0=ALU.mult), reads=['fb'], writes=['fb'])
            P.op('pool', lambda e: e.memset(o16[:], 1.0), writes=['o16'])
            for half in range(2):
                P.op('pe', lambda e, half=half: e.matmul(out=ps[4 + half][:], lhsT=ones[0:1, :], rhs=vbr[0:1, half * 512:(half + 1) * 512], start=True, stop=True),
                     reads=['ones', 'vbr'], writes=[f'ps{4 + half}'])
                P.op('act', lambda e, half=half: e.copy(out=vb_bc[:, half * 512:(half + 1) * 512], in_=ps[4 + half][:]), reads=[f'ps{4 + half}'], writes=['vb_bc'])
            ti = 0
            for jb in range(8):
                cols = slice(jb * 512, (jb + 1) * 512)
                for tl in range(4):
                    xt = xts[ti % 2]
                    rx = f'xt{ti % 2}'
                    P.op('sp', lambda e, xt=xt, ti=ti: e.dma_start(out=xt[:], in_=xin[ti * 128:(ti + 1) * 128, :]), writes=[rx], dma=True)
                    self.modulate(xt[:], ht[:], rx, 'ht')
                    self.transpose8(ht, 'ht', hT, 'hT', tl * 128, 0)
                    ti += 1
                for c in range(16):
                    bank = ps[2 + c % 2]
                    rb = f'ps{2 + c % 2}'
                    for k in range(8):
                        P.op('pe', lambda e, k=k, c=c, bank=bank: e.matmul(out=bank[:], lhsT=winr[:, k, c * 128:(c + 1) * 128], rhs=hT[:, k, :],
                                                                          start=(k == 0), stop=(k == 7)),
                             reads=[('win', c // 4), 'hT'], writes=[rb])
                    ob = qko[c % 2]
                    rob = f'qko{c % 2}'
                    P.op('act', lambda e, c=c, bank=bank, ob=ob: e.activation(out=ob[:], in_=bank[:], func=AF.Identity, bias=qkb[:, c:c + 1],
                                                                             scale=(0.125 if c < 8 else 1.0)),
                         reads=[rb, 'qkb'], writes=[rob])
                    dstt = A['QA'] if c < 8 else A['KA']
                    for hh in range(2):
                        head = (c % 8) * 2 + hh
                        P.op('sp', lambda e, ob=ob, hh=hh, head=head, dstt=dstt: e.dma_start(out=dstt[head, 0:64, cols], in_=ob[hh * 64:(hh + 1) * 64, :]),
                             reads=[rob], writes=[('QK', c, hh)], dma=True)
                for tl in range(4):
                    tix = jb * 4 + tl
                    vt = vo[tix % 2]
                    rv = f'vo{tix % 2}'
                    for half in range(2):
                        bank = ps[4 + half]
                        rb = f'ps{4 + half}'
                        for k in range(8):
                            P.op('pe', lambda e, k=k, bank=bank, half=half, tl=tl: e.matmul(out=bank[:], lhsT=hT[:, k, tl * 128:(tl + 1) * 128],
                                                                                        rhs=winr[:, k, 2 * D + half * 512:2 * D + (half + 1) * 512],
                                                                                        start=(k == 0), stop=(k == 7)),
                                 reads=['hT', ('win', 4 + half)], writes=[rb])
                        P.op('dve', lambda e, vt=vt, bank=bank, half=half: e.tensor_tensor(out=vt[:, half * 512:(half + 1) * 512], in0=bank[:],
                                                                                      in1=vb_bc[:, half * 512:(half + 1) * 512], op=ALU.add),
                             reads=[rb, 'vb_bc'], writes=[rv])
                    P.op('sp', lambda e, vt=vt, tix=tix: e.dma_start(out=A['V'][tix * 128:(tix + 1) * 128, :], in_=vt[:]), reads=[rv], writes=[('V', tix)], dma=True)
                for k in range(8):
                    P.op('pe', lambda e, k=k: e.matmul(out=ps[6][0:16, :], lhsT=wf[:, k, :], rhs=hT[:, k, :].bitcast(F32), start=(k == 0), stop=(k == 7)),
                         reads=['wf', 'hT'], writes=['ps6'])
                P.op('act', lambda e: e.activation(out=spt[:], in_=ps[6][0:16, :], func=AF.Exp, bias=fb[:, 0:1], scale=-1.0), reads=['ps6', 'fb'], writes=['spt'])
                P.op('act', lambda e: e.activation(out=spt[:], in_=spt[:], func=AF.Ln, bias=1.0, scale=1.0), reads=['spt'], writes=['spt'])
                P.op('dve', lambda e: e.tensor_scalar(out=spt[:], in0=spt[:], scalar1=-1.0, scalar2=None, op0=ALU.mult), reads=['spt'], writes=['spt'])
                Fc = Fcb[jb % 2]
                rF = f'Fcb{jb % 2}'
                init = 0.0 if jb == 0 else Fcb[(jb - 1) % 2][:, 511:512]
                P.op('dve', lambda e, init=init, Fc=Fc: e.tensor_tensor_scan(out=Fc[:], data0=o16[:], data1=spt[:], initial=init,
                                                                             op0=ALU.mult, op1=ALU.add),
                     reads=['o16', 'spt', f'Fcb{(jb - 1) % 2}'], writes=[rF])
                P.op('dve', lambda e, Fc=Fc: e.tensor_copy(out=Frb[:], in_=Fc[:]), reads=[rF], writes=['Frb'])
                P.op('dve', lambda e, Fc=Fc: e.tensor_tensor(out=Flb[:], in0=Fc[:], in1=Frb[:].bitcast(F32), op=ALU.subtract), reads=[rF, 'Frb'], writes=['Flb'])
                P.op('dve', lambda e: e.tensor_scalar(out=nFr[:], in0=Frb[:].bitcast(F32), scalar1=-1.0, scalar2=None, op0=ALU.mult), reads=['Frb'], writes=['nFr'])
                P.op('dve', lambda e: e.tensor_scalar(out=nFl[:], in0=Flb[:], scalar1=-1.0, scalar2=None, op0=ALU.mult), reads=['Flb'], writes=['nFl'])
                P.op('sp', lambda e: e.dma_start(out=A['QA'][:, 64, cols], in_=Frb[:].bitcast(F32)), reads=['Frb'], writes=[('QAf', jb)], dma=True)
                P.op('sp', lambda e: e.dma_start(out=A['QA'][:, 65, cols], in_=Flb[:]), reads=['Flb'], writes=[('QAl', jb)], dma=True)
                P.op('sp', lambda e: e.dma_start(out=A['KA'][:, 66, cols], in_=nFr[:]), reads=['nFr'], writes=[('KAf', jb)], dma=True)
                P.op('sp', lambda e: e.dma_start(out=A['KA'][:, 67, cols], in_=nFl[:]), reads=['nFl'], writes=[('KAl', jb)], dma=True)
                for r in (66, 67):
                    P.op('sp', lambda e, r=r: e.dma_start(out=A['QA'][:, r, cols], in_=o16[:]), reads=['o16'], writes=[('QAo', r, jb)], dma=True)
                for r in (64, 65):
                    P.op('sp', lambda e, r=r: e.dma_start(out=A['KA'][:, r, cols], in_=o16[:]), reads=['o16'], writes=[('KAo', r, jb)], dma=True)

    def emit_attn2(self):
        P, nc, A, ps = self.P, self.nc, self.A, self.ps
        NR = 68
        with Stage(self, 'a2') as st:
            qst = st.T('qst', [NR, S])
            kst = st.T('kst', [NR, S])
            vst = st.T('vst', [128, 32, 64])
            QAh = [st.T(f'QAh{i}', [NR, S], F32R) for i in range(2)]
            KAh = [st.T(f'KAh{i}', [NR, S], F32R) for i in range(2)]
            Vh = [st.T(f'Vh{i}', [128, 32, 128], F32R) for i in range(2)]
            ones_r = st.T('ones_r', [128, 128], F32R)
            pt = [st.T(f'pt{i}', [128, 512], F32R) for i in range(3)]
            lm = [st.T(f'lm{i}', [128, 512]) for i in range(2)]
            mask = st.T('mask', [128, 4, 512])
            rzt = st.T('rzt', [64, 512])
            oT = [st.T(f'oT{i}', [64, 512]) for i in range(2)]
            P.op('pool', lambda e: e.memset(mask[:], 0.0), writes=['mask'])
            for i4 in range(4):
                P.op('pool', lambda e, i4=i4: e.affine_select(out=mask[:, i4, :], in_=mask[:, i4, :], pattern=[[1, 512]], compare_op=ALU.is_ge,
                                                              fill=NEG, base=-128 * i4, channel_multiplier=-1), reads=['mask'], writes=['mask'])
            P.op('pool', lambda e: e.tensor_copy(out=ones_r[:], in_=self.ones[:]), reads=['ones'], writes=['ones_r'])
            npt = 0
            nlm = 0
            nS = 0
            nO = 0

            def loads(h):
                b = h % 2
                qa, ka, vh = QAh[b], KAh[b], Vh[b]
                rq, rk, rv = f'QAh{b}', f'KAh{b}', f'Vh{b}'
                for q4 in range(4):
                    cs = slice(q4 * 1024, (q4 + 1) * 1024)
                    P.op('sp', lambda e, h=h, cs=cs: e.dma_start(out=qst[:, cs], in_=A['QA'][h, :, cs]), writes=[('qst', q4)], dma=True)
                    P.op('sp', lambda e, h=h, cs=cs: e.dma_start(out=kst[:, cs], in_=A['KA'][h, :, cs]), writes=[('kst', q4)], dma=True)
                    P.op('sp', lambda e, h=h, q4=q4: e.dma_start(
                        out=vst[:, q4 * 8:(q4 + 1) * 8, :],
                        in_=A['V'][q4 * 1024:(q4 + 1) * 1024, h * 64:(h + 1) * 64].rearrange("(i p) d -> p i d", p=128)),
                        writes=[('vst', q4)], dma=True)
                for q4 in range(4):
                    cs = slice(q4 * 1024, (q4 + 1) * 1024)
                    P.op('pool', lambda e, qa=qa, cs=cs: e.tensor_copy(out=qa[:, cs], in_=qst[:, cs]), reads=[('qst', q4)], writes=[rq])
                    P.op('pool', lambda e, ka=ka, cs=cs: e.tensor_copy(out=ka[:, cs], in_=kst[:, cs]), reads=[('kst', q4)], writes=[rk])
                    for dup in range(2):
                        P.op('pool', lambda e, vh=vh, q4=q4, dup=dup: e.tensor_copy(out=vh[:, q4 * 8:(q4 + 1) * 8, dup * 64:(dup + 1) * 64],
                                                                                  in_=vst[:, q4 * 8:(q4 + 1) * 8, :]),
                             reads=[('vst', q4)], writes=[rv])

            loads(0)
            NH = self.cfg.get('nheads', 16)
            steps = [(h, j, i) for h in range(NH) for j in range(8) for i in range(4 * j + 4)]

            def emit_qk(n):
                h, j, i = steps[n]
                b = h % 2
                sb = ps[n % 3]
                c0 = max(0, i - 4 * j) * 128
                P.op('pe', lambda e, sb=sb, ka=KAh[b], qa=QAh[b], i=i, j=j, c0=c0: e.matmul(out=sb[:, c0:512], lhsT=ka[:, i * 128:(i + 1) * 128],
                                                                                        rhs=qa[:, j * 512 + c0:(j + 1) * 512], start=True, stop=True),
                     reads=[f'KAh{b}', f'QAh{b}'], writes=[f'ps{n % 3}'])

            emit_qk(0)
            for n, (h, j, i) in enumerate(steps):
                b = h % 2
                vh, rv = Vh[b], f'Vh{b}'
                if j == 0 and i == 0 and h + 1 < NH:
                    loads(h + 1)
                if n + 1 < len(steps):
                    emit_qk(n + 1)
                if i == 0:
                    oset = nO % 2
                    nO += 1
                poA, poB = ps[3 + 2 * oset], ps[4 + 2 * oset]
                rA, rB = f'ps{3 + 2 * oset}', f'ps{4 + 2 * oset}'
                ot, rot = oT[oset], f'oT{oset}'
                last = 4 * j + 3
                sb, rsb = ps[n % 3], f'ps{n % 3}'
                p_, rp = pt[n % 3], f'pt{n % 3}'
                c0 = max(0, i - 4 * j) * 128
                if i >= 4 * j:
                    l_ = lm[nlm % 2]
                    rl = f'lm{nlm % 2}'
                    nlm += 1
                    P.op('dve', lambda e, l_=l_, sb=sb, i=i, j=j, c0=c0: e.tensor_tensor(out=l_[:, c0:512], in0=sb[:, c0:512], in1=mask[:, i - 4 * j, c0:512], op=ALU.add),
                         reads=[rsb, 'mask'], writes=[rl])
                    P.op('act', lambda e, p_=p_, l_=l_, c0=c0: e.activation(out=p_[:, c0:512], in_=l_[:, c0:512], func=AF.Exp), reads=[rl], writes=[rp])
                else:
                    P.op('act', lambda e, p_=p_, sb=sb: e.activation(out=p_[:], in_=sb[:], func=AF.Exp), reads=[rsb], writes=[rp])
                P.op('pe', lambda e, p_=p_, i=i, poA=poA, vh=vh, last=last, c0=c0: e.matmul(out=poA[:, c0:512], lhsT=vh[:, i, :], rhs=p_[:, c0:512],
                                                                                     start=(i == 0), stop=(i == last)),
                     reads=[rp, rv], writes=[rA])
                P.op('pe', lambda e, p_=p_, i=i, poB=poB, last=last, c0=c0: e.matmul(out=poB[:, c0:512], lhsT=ones_r[:], rhs=p_[:, c0:512],
                                                                              start=(i == 0), stop=(i == last)),
                     reads=[rp, 'ones_r'], writes=[rB])
                if i == last:
                    P.op('dve', lambda e, poB=poB: e.reciprocal(out=rzt[:], in_=poB[0:64, :]), reads=[rB], writes=['rzt'])
                    P.op('dve', lambda e, poA=poA, ot=ot: e.tensor_tensor(out=ot[:], in0=poA[0:64, :], in1=rzt[:], op=ALU.mult), reads=[rA, 'rzt'], writes=[rot])
                    P.op('sp', lambda e, ot=ot, h=h, j=j: e.dma_start(out=A['AOT'][h // 2, (h % 2) * 64:(h % 2) * 64 + 64, j * 512:(j + 1) * 512], in_=ot[:]),
                         reads=[rot], writes=[('AOT', h, j)], dma=True)


def make_in_maps(inputs, cores=range(8)):
    f = lambda a: np.ascontiguousarray(np.asarray(a, dtype=np.float32))
    sh = {}
    sh['ada_mix_w'] = f(inputs['ada_mix_w'])
    sh['ada_ffn_w'] = f(inputs['ada_ffn_w'])
    amb, afb = f(inputs['ada_mix_b']), f(inputs['ada_ffn_b'])
    sh['ada_b'] = f(np.stack([amb[0], afb[0], amb[1], afb[1]]))
    g1, g2 = f(inputs['ln_mix_g']), f(inputs['ln_ffn_g'])
    b1, b2 = f(inputs['ln_mix_b']), f(inputs['ln_ffn_b'])
    sh['ln_g'] = f(np.stack([g1[0], g2[0], g1[1], g2[1]]))
    sh['ln_b'] = f(np.stack([b1[0], b2[0], b1[1], b2[1]]))
    sh['conv_in_w'] = f(inputs['conv_in_w'][0])
    sh['conv_in_b_l'] = f(np.asarray(inputs['conv_in_b'][0]).reshape(16, 128).T)
    sh['conv_dw_w_l'] = f(np.asarray(inputs['conv_dw_w'][0]).reshape(31, 8, 128).transpose(2, 1, 0))
    sh['conv_vec_l'] = f(np.stack([np.asarray(inputs[k][0]).reshape(8, 128).T for k in ('conv_dw_b', 'conv_ln_g', 'conv_ln_b')], axis=1))
    sh['conv_out_w'] = f(inputs['conv_out_w'][0])
    sh['conv_out_b'] = f(np.asarray(inputs['conv_out_b'][0]).reshape(1, D))
    sh['attn_in_w'] = f(inputs['attn_in_w'][0])
    ab = np.asarray(inputs['attn_in_b'][0])
    sh['attn_qkb_l'] = f(ab[:2 * D].reshape(16, 128).T)
    sh['attn_vb'] = f(ab[2 * D:3 * D].reshape(1, D))
    sh['attn_fb'] = f(ab[3 * D:].reshape(16, 1))
    sh['attn_out_w'] = f(inputs['attn_out_w'][0])
    sh['attn_out_b'] = f(np.asarray(inputs['attn_out_b'][0]).reshape(1, D))
    sh['peer_query_w'] = f(inputs['peer_query_w'])
    k1, k2 = np.asarray(inputs['peer_sub_keys_1']), np.asarray(inputs['peer_sub_keys_2'])
    sh['peer_skT'] = f(np.stack([np.stack([k1[l].T, k2[l].T]) for l in range(2)]))
    sh['peer_u'] = f(inputs['peer_expert_u'])
    sh['peer_v'] = f(inputs['peer_expert_v'])
    x = np.asarray(inputs['x'])
    c = np.asarray(inputs['c'])
    maps = []
    for b in cores:
        m = dict(sh)
        m['x'] = f(x[b])
        m['c_l'] = f(c[b].reshape(8, 128).T)
        maps.append(m)
    return maps


_NC_CACHE = {}


def kernel(**inputs):
    if 'full' not in _NC_CACHE:
        _NC_CACHE['full'] = Kern({}).build()
    nc = _NC_CACHE['full']
    maps = make_in_maps(inputs)
    res = run_bass_kernel_spmd(nc, maps, core_ids=list(range(8)))
    return np.stack([np.asarray(r['out'], dtype=np.float32) for r in res.results], axis=0)
```
